# Optimizing a Trainium2 kernel written in Bass

```python
import math
import jax
import jax.numpy as jnp
from jax import lax
import numpy as np

D_MODEL = 1024
BATCH = 4
SEQ = 8192
DEPTH = 4

GRID_W = 64
CTX_LEN = 256
Q_BLOCK = 128
ROPE_BASE = 10000.0
LN_EPS = 1e-5
RMS_EPS = 1e-6
DEEPNORM_ALPHA = (2.0 * DEPTH) ** 0.25
DEEPNORM_BETA = (8.0 * DEPTH) ** -0.25

DA_HEADS = 4
DA_QK = 64
DA_V = 128
DA_WIDTH = DA_HEADS * DA_V
DA_SCALE = DA_QK ** -0.5
MLA_HEADS = 8
MLA_Q_RANK = 256
MLA_KV_RANK = 128
MLA_NOPE = 64
MLA_ROPE = 32
MLA_V = 64
MLA_WIDTH = MLA_HEADS * MLA_V
MLA_SCALE = (MLA_NOPE + MLA_ROPE) ** -0.5
ML_HEADS = 4
ML_DH = 128
ML_WIDTH = ML_HEADS * ML_DH
ML_CONV = 5
ML_CHUNK = 128
NA_HEADS = 8
NA_DH = 64
NA_WIDTH = NA_HEADS * NA_DH
NA_ROWS = 8
NA_COLS = 16
NA_SCALE = NA_DH ** -0.5

EVEN_WIDTH = DA_WIDTH + MLA_WIDTH
ODD_WIDTH = ML_WIDTH + NA_WIDTH
EVEN_SPLIT = [DA_HEADS * DA_QK] * 4 + [DA_WIDTH, MLA_Q_RANK, MLA_KV_RANK, MLA_ROPE, EVEN_WIDTH]
ODD_SPLIT = [2 * ML_WIDTH, ML_WIDTH, ML_WIDTH, 4 * ML_HEADS, NA_WIDTH, NA_WIDTH, NA_WIDTH, ODD_WIDTH]
EVEN_IN = sum(EVEN_SPLIT)
ODD_IN = sum(ODD_SPLIT)
N_EVEN = (DEPTH + 1) // 2
N_ODD = DEPTH // 2
F32 = jnp.float32

kernel_name = 'hybrid_diffattn_mla_mlstm_natten_prefix'


def split_cols(z, sizes):
    return jnp.split(z, np.cumsum(sizes)[:-1].tolist(), axis=-1)


def to_heads(t, n):
    b, s, e = t.shape
    return t.reshape(b, s, n, e // n).transpose(0, 2, 1, 3)


def from_heads(t):
    b, n, s, d = t.shape
    return t.transpose(0, 2, 1, 3).reshape(b, s, n * d)


def layer_norm(x, g, b):
    xf = x.astype(F32)
    mu = jnp.mean(xf, -1, keepdims=True)
    var = jnp.mean(jnp.square(xf - mu), -1, keepdims=True)
    return ((xf - mu) * lax.rsqrt(var + LN_EPS) * g.astype(F32) + b.astype(F32)).astype(x.dtype)


def head_layer_norm(x):
    xf = x.astype(F32)
    mu = jnp.mean(xf, -1, keepdims=True)
    var = jnp.mean(jnp.square(xf - mu), -1, keepdims=True)
    return ((xf - mu) * lax.rsqrt(var + LN_EPS)).astype(x.dtype)


def rms_norm(x, g):
    xf = x.astype(F32)
    return (xf * lax.rsqrt(jnp.mean(xf * xf, -1, keepdims=True) + RMS_EPS) * g.astype(F32)).astype(x.dtype)


def adaln(cond, w, b):
    m = jnp.einsum('...d,de->...e', jax.nn.silu(cond), w) + b
    return jnp.split(m, 3, axis=-1)


def axial_rope(n_tokens, dim):
    t = jnp.arange(n_tokens)
    row = (t // GRID_W).astype(F32)
    col = (t % GRID_W).astype(F32)
    n_freq = dim // 4
    freqs = ROPE_BASE ** (-jnp.arange(n_freq, dtype=F32) / n_freq)
    ang = jnp.concatenate([row[:, None] * freqs, col[:, None] * freqs], axis=-1)
    return jnp.cos(ang), jnp.sin(ang)


def apply_rope(x, cos, sin):
    half = x.shape[-1] // 2
    xf = x.astype(F32)
    x1, x2 = xf[..., :half], xf[..., half:]
    return jnp.concatenate([x1 * cos - x2 * sin, x2 * cos + x1 * sin], axis=-1).astype(x.dtype)


def _to_blocks(q):
    b, h, s, d = q.shape
    return q.reshape(b, h, s // Q_BLOCK, Q_BLOCK, d).transpose(2, 0, 1, 3, 4)


def _from_blocks(o):
    nb, b, h, qb, d = o.shape
    return o.transpose(1, 2, 0, 3, 4).reshape(b, h, nb * qb, d)


def blocked_queries(fn, *qs):
    out = lax.map(lambda blk: fn(*blk), tuple(_to_blocks(q) for q in qs))
    return _from_blocks(out)


def softmax_attention(q, k, v, scale):
    def one(qb):
        s = jnp.einsum('bhqd,bhkd->bhqk', qb, k).astype(F32) * scale
        p = jax.nn.softmax(s, axis=-1).astype(v.dtype)
        return jnp.einsum('bhqk,bhkd->bhqd', p, v)
    return blocked_queries(one, q)


def diff_attention(q1, q2, k1, k2, v, lam):
    def one(a, b_):
        s1 = jnp.einsum('bhqd,bhkd->bhqk', a, k1).astype(F32) * DA_SCALE
        s2 = jnp.einsum('bhqd,bhkd->bhqk', b_, k2).astype(F32) * DA_SCALE
        p = jax.nn.softmax(s1, axis=-1) - lam * jax.nn.softmax(s2, axis=-1)
        return jnp.einsum('bhqk,bhkd->bhqd', p.astype(v.dtype), v)
    return blocked_queries(one, q1, q2)


def even_project(h, w_in, b_in, q_norm_g, kv_norm_g, w_uq, w_ukv, rope_da, rope_mla):
    z = jnp.einsum('bsd,de->bse', h, w_in) + b_in
    q1, q2, k1, k2, v, cq, ckv, k_rope, gate = split_cols(z, EVEN_SPLIT)
    q1, q2, k1, k2 = (to_heads(t, DA_HEADS) for t in (q1, q2, k1, k2))
    v = to_heads(v, DA_HEADS)
    qm = to_heads(jnp.einsum('bsr,re->bse', rms_norm(cq, q_norm_g), w_uq), MLA_HEADS)
    kvm = to_heads(jnp.einsum('bsr,re->bse', rms_norm(ckv, kv_norm_g), w_ukv), MLA_HEADS)
    q_nope, q_rope = qm[..., :MLA_NOPE], qm[..., MLA_NOPE:]
    k_nope, vm = kvm[..., :MLA_NOPE], kvm[..., MLA_NOPE:]
    k_rope = k_rope[:, None]
    if rope_da is not None:
        q1, q2, k1, k2 = (apply_rope(t, *rope_da) for t in (q1, q2, k1, k2))
        q_rope = apply_rope(q_rope, *rope_mla)
        k_rope = apply_rope(k_rope, *rope_mla)
    qm = jnp.concatenate([q_nope, q_rope], axis=-1)
    km = jnp.concatenate([k_nope, jnp.broadcast_to(k_rope, k_nope.shape[:-1] + (MLA_ROPE,))], axis=-1)
    return (q1, q2, qm, gate), (k1, k2, v, km, vm)


def even_mix(queries, keys, lam, lam_init, subln_g, w_out):
    q1, q2, qm, gate = queries
    k1, k2, vd, km, vm = keys
    oa = rms_norm(diff_attention(q1, q2, k1, k2, vd, lam), subln_g) * (1.0 - lam_init)
    ob = softmax_attention(qm, km, vm, MLA_SCALE)
    y = jnp.concatenate([from_heads(oa), from_heads(ob)], axis=-1) * jax.nn.silu(gate)
    return jnp.einsum('bse,ed->bsd', y, w_out)


def centred_depthwise_conv(x, w, b):
    y = lax.conv_general_dilated(x, w[:, None, :].astype(x.dtype), window_strides=(1,),
                                 padding=[(ML_CONV // 2, ML_CONV // 2)],
                                 dimension_numbers=('NWC', 'WIO', 'NWC'),
                                 feature_group_count=x.shape[-1])
    return y + b


def odd_project(h, w_in, b_in, conv_w, conv_b, f_bias):
    b, s, _ = h.shape
    z = jnp.einsum('bsd,de->bse', h, w_in) + b_in
    qk, v, o, gates, qn, kn, vn, gate = split_cols(z, ODD_SPLIT)
    qk = jax.nn.silu(centred_depthwise_conv(qk, conv_w, conv_b))
    q = to_heads(qk[..., :ML_WIDTH], ML_HEADS)
    k = to_heads(qk[..., ML_WIDTH:], ML_HEADS) * (ML_DH ** -0.5)
    v = to_heads(v, ML_HEADS)
    g = gates.astype(F32).reshape(b, s, 4, ML_HEADS).transpose(2, 0, 3, 1)
    fb = f_bias.astype(F32)
    fwd = (g[0], jax.nn.log_sigmoid(g[1] + fb[0][:, None]))
    bwd = (g[2], jax.nn.log_sigmoid(g[3] + fb[1][:, None]))
    na = tuple(t.reshape(b, s, NA_HEADS, NA_DH) for t in (qn, kn, vn))
    return (q, k, v, jax.nn.sigmoid(o), fwd, bwd), na, gate


def zero_state(b):
    return (jnp.zeros((b, ML_HEADS, ML_DH, ML_DH), F32), jnp.zeros((b, ML_HEADS, ML_DH), F32),
            jnp.zeros((b, ML_HEADS), F32))


def mlstm_chunkwise(q, k, v, log_i, log_f, state, with_output=True):
    b, nh, s, d = q.shape
    nc = s // ML_CHUNK

    def chunks(t):
        return jnp.moveaxis(t.reshape(t.shape[:2] + (nc, ML_CHUNK) + t.shape[3:]), 2, 0)

    lower = jnp.tril(jnp.ones((ML_CHUNK, ML_CHUNK), dtype=bool))

    def step(carry, inp):
        c_mat, n_vec, m_sc = carry
        qb, kb, vb, li, lf = inp
        qb, kb, vb = qb.astype(F32), kb.astype(F32), vb.astype(F32)
        bcum = jnp.cumsum(lf, axis=-1)
        btot = bcum[..., -1]
        g = btot[..., None] - bcum + li
        m_new = jnp.maximum(btot + m_sc, jnp.max(g, axis=-1))
        w_s = jnp.exp(g - m_new[..., None])
        decay = jnp.exp(btot + m_sc - m_new)
        c_new = decay[..., None, None] * c_mat + jnp.einsum('bhs,bhsd,bhse->bhde', w_s, kb, vb)
        n_new = decay[..., None] * n_vec + jnp.einsum('bhs,bhsd->bhd', w_s, kb)
        if not with_output:
            return (c_new, n_new, m_new), None
        dmat = jnp.where(lower, bcum[..., :, None] - bcum[..., None, :] + li[..., None, :], -jnp.inf)
        inter = bcum + m_sc[..., None]
        m_t = jnp.maximum(inter, jnp.max(dmat, axis=-1))
        w_ts = jnp.exp(dmat - m_t[..., None]) * jnp.einsum('bhtd,bhsd->bhts', qb, kb)
        w_c = jnp.exp(inter - m_t)
        num = jnp.einsum('bhts,bhsd->bhtd', w_ts, vb) + w_c[..., None] * jnp.einsum('bhtd,bhde->bhte', qb, c_mat)
        den = jnp.sum(w_ts, axis=-1) + w_c * jnp.einsum('bhtd,bhd->bht', qb, n_vec)
        h_t = num / jnp.maximum(jnp.abs(den), jnp.exp(-m_t))[..., None]
        return (c_new, n_new, m_new), h_t.astype(v.dtype)

    state, hs = lax.scan(step, state, tuple(chunks(t) for t in (q, k, v, log_i, log_f)))
    if not with_output:
        return None, state
    return jnp.moveaxis(hs, 0, 2).reshape(b, nh, s, d), state


def flip_seq(t):
    return jnp.flip(t, axis=2)


def mlstm_bidirectional(ml, state_f, state_b, with_output=True):
    q, k, v, _, (li_f, lf_f), (li_b, lf_b) = ml
    h_f, st_f = mlstm_chunkwise(q, k, v, li_f, lf_f, state_f, with_output)
    h_b, st_b = mlstm_chunkwise(flip_seq(q), flip_seq(k), flip_seq(v), flip_seq(li_b), flip_seq(lf_b),
                                state_b, with_output)
    h = h_f + flip_seq(h_b) if with_output else None
    return h, st_f, st_b


def neighbourhood_attention(q, k, v, k_ctx, v_ctx, rpb, rows):
    b, s, h, d = q.shape
    wr = min(NA_ROWS, rows)
    n_nb = wr * NA_COLS
    qg, kg, vg = (t.reshape(b, rows, GRID_W, h, d) for t in (q, k, v))
    col = jnp.arange(GRID_W)
    col_start = jnp.clip(col - NA_COLS // 2, 0, GRID_W - NA_COLS)
    col_idx = col_start[:, None] + jnp.arange(NA_COLS)[None, :]
    col_off = col_idx - col[:, None] + (NA_COLS - 1)

    def one_row(r):
        rs = jnp.clip(r - wr // 2, 0, rows - wr)
        k_nb = lax.dynamic_slice_in_dim(kg, rs, wr, axis=1)[:, :, col_idx]
        v_nb = lax.dynamic_slice_in_dim(vg, rs, wr, axis=1)[:, :, col_idx]
        q_r = lax.dynamic_index_in_dim(qg, r, axis=1, keepdims=False)
        row_off = rs + jnp.arange(wr) - r + (NA_ROWS - 1)
        bias = rpb[:, row_off][:, :, col_off].transpose(0, 2, 1, 3).astype(F32)
        s_nb = jnp.einsum('bqhd,brqjhd->bhqrj', q_r, k_nb).astype(F32) * NA_SCALE + bias
        s_cx = jnp.einsum('bqhd,bthd->bhqt', q_r, k_ctx).astype(F32) * NA_SCALE
        p = jax.nn.softmax(jnp.concatenate([s_nb.reshape(b, h, GRID_W, n_nb), s_cx], axis=-1), axis=-1)
        p = p.astype(v.dtype)
        p_nb = p[..., :n_nb].reshape(b, h, GRID_W, wr, NA_COLS)
        return (jnp.einsum('bhqrj,brqjhd->bqhd', p_nb, v_nb)
                + jnp.einsum('bhqt,bthd->bqhd', p[..., n_nb:], v_ctx))

    o = lax.map(one_row, jnp.arange(rows))
    return o.transpose(1, 0, 2, 3, 4).reshape(b, s, h, d)


def odd_mix(h_ml, o_gate, na_out, gate, norm_g, w_out):
    b, s = na_out.shape[:2]
    y_ml = from_heads(head_layer_norm(h_ml)) * norm_g * o_gate
    y = jnp.concatenate([y_ml, na_out.reshape(b, s, NA_WIDTH)], axis=-1) * jax.nn.silu(gate)
    return jnp.einsum('bse,ed->bsd', y, w_out)


def setup_inputs(seed: int = 0) -> dict:
    key = jax.random.key(seed)
    ks = iter(jax.random.split(key, 32))

    def nrm(shape, scale):
        return jax.random.normal(next(ks), shape, F32) * scale

    d = D_MODEL
    return {
        'x': nrm((BATCH, SEQ, d), 1.0),
        'c': nrm((BATCH, d), 1.0),
        'ctx': nrm((BATCH, CTX_LEN, d), 1.0),
        'c_ctx': nrm((d,), 1.0),
        'ada_w': nrm((DEPTH, d, 3 * d), d ** -0.5),
        'ada_b': nrm((DEPTH, 3 * d), 0.02),
        'ln_g': 1.0 + nrm((DEPTH, d), 0.01),
        'ln_b': nrm((DEPTH, d), 0.01),
        'ev_w_in': nrm((N_EVEN, d, EVEN_IN), d ** -0.5),
        'ev_b_in': nrm((N_EVEN, EVEN_IN), 0.01),
        'da_lambda': nrm((N_EVEN, 4, DA_QK), 0.1),
        'da_subln_g': 1.0 + nrm((N_EVEN, DA_V), 0.01),
        'mla_q_norm_g': 1.0 + nrm((N_EVEN, MLA_Q_RANK), 0.01),
        'mla_kv_norm_g': 1.0 + nrm((N_EVEN, MLA_KV_RANK), 0.01),
        'mla_w_uq': nrm((N_EVEN, MLA_Q_RANK, MLA_HEADS * (MLA_NOPE + MLA_ROPE)), MLA_Q_RANK ** -0.5),
        'mla_w_ukv': nrm((N_EVEN, MLA_KV_RANK, MLA_HEADS * (MLA_NOPE + MLA_V)), MLA_KV_RANK ** -0.5),
        'ev_w_out': nrm((N_EVEN, EVEN_WIDTH, d), EVEN_WIDTH ** -0.5 * DEEPNORM_BETA),
        'od_w_in': nrm((N_ODD, d, ODD_IN), d ** -0.5),
        'od_b_in': nrm((N_ODD, ODD_IN), 0.01),
        'ml_conv_w': nrm((N_ODD, ML_CONV, 2 * ML_WIDTH), ML_CONV ** -0.5),
        'ml_conv_b': nrm((N_ODD, 2 * ML_WIDTH), 0.01),
        'ml_f_bias': jnp.linspace(3.0, 6.0, ML_HEADS, dtype=F32) + nrm((N_ODD, 2, ML_HEADS), 0.01),
        'ml_norm_g': 1.0 + nrm((N_ODD, ML_WIDTH), 0.01),
        'na_rpb': nrm((N_ODD, NA_HEADS, 2 * NA_ROWS - 1, 2 * NA_COLS - 1), 0.02),
        'od_w_out': nrm((N_ODD, ODD_WIDTH, d), ODD_WIDTH ** -0.5 * DEEPNORM_BETA),
    }


def reference(x, c, ctx, c_ctx, ada_w, ada_b, ln_g, ln_b, ev_w_in, ev_b_in, da_lambda, da_subln_g,
              mla_q_norm_g, mla_kv_norm_g, mla_w_uq, mla_w_ukv, ev_w_out, od_w_in, od_b_in,
              ml_conv_w, ml_conv_b, ml_f_bias, ml_norm_g, na_rpb, od_w_out):
    n_lat = x.shape[1]
    rows = n_lat // GRID_W
    rope_da = axial_rope(n_lat, DA_QK)
    rope_mla = axial_rope(n_lat, MLA_ROPE)
    xl, xc = x, ctx
    for l in range(DEPTH):
        update_ctx = l < DEPTH - 1
        sh_l, sc_l, g_l = adaln(c, ada_w[l], ada_b[l])
        sh_c, sc_c, g_c = adaln(c_ctx, ada_w[l], ada_b[l])
        hl = xl * (1.0 + sc_l[:, None]) + sh_l[:, None]
        hc = xc * (1.0 + sc_c) + sh_c
        i = l // 2
        yc = None
        if l % 2 == 0:
            lam_init = 0.8 - 0.6 * math.exp(-0.3 * l)
            lq1, lk1, lq2, lk2 = da_lambda[i].astype(F32)
            lam = jnp.exp(jnp.sum(lq1 * lk1)) - jnp.exp(jnp.sum(lq2 * lk2)) + lam_init
            ql, kl = even_project(hl, ev_w_in[i], ev_b_in[i], mla_q_norm_g[i], mla_kv_norm_g[i],
                                  mla_w_uq[i], mla_w_ukv[i], rope_da, rope_mla)
            qc, kc = even_project(hc, ev_w_in[i], ev_b_in[i], mla_q_norm_g[i], mla_kv_norm_g[i],
                                  mla_w_uq[i], mla_w_ukv[i], None, None)
            keys_l = tuple(jnp.concatenate([a, b_], axis=2) for a, b_ in zip(kl, kc))
            yl = even_mix(ql, keys_l, lam, lam_init, da_subln_g[i], ev_w_out[i])
            if update_ctx:
                yc = even_mix(qc, kc, lam, lam_init, da_subln_g[i], ev_w_out[i])
        else:
            ml_l, na_l, gate_l = odd_project(hl, od_w_in[i], od_b_in[i], ml_conv_w[i], ml_conv_b[i], ml_f_bias[i])
            ml_c, na_c, gate_c = odd_project(hc, od_w_in[i], od_b_in[i], ml_conv_w[i], ml_conv_b[i], ml_f_bias[i])
            z0 = zero_state(xc.shape[0])
            h_c, st_f, st_b = mlstm_bidirectional(ml_c, z0, z0, update_ctx)
            h_l, _, _ = mlstm_bidirectional(ml_l, st_f, st_b)
            na_out_l = neighbourhood_attention(na_l[0], na_l[1], na_l[2], na_c[1], na_c[2], na_rpb[i], rows)
            yl = odd_mix(h_l, ml_l[3], na_out_l, gate_l, ml_norm_g[i], od_w_out[i])
            if update_ctx:
                qn, kn, vn = (t.transpose(0, 2, 1, 3) for t in na_c)
                na_out_c = softmax_attention(qn, kn, vn, NA_SCALE).transpose(0, 2, 1, 3)
                yc = odd_mix(h_c, ml_c[3], na_out_c, gate_c, ml_norm_g[i], od_w_out[i])
        xl = layer_norm(DEEPNORM_ALPHA * xl + g_l[:, None] * yl, ln_g[l], ln_b[l])
        if update_ctx:
            xc = layer_norm(DEEPNORM_ALPHA * xc + g_c * yc, ln_g[l], ln_b[l])
    return xl
```

```python
import contextlib
import math
import numpy as np
import concourse.bass as bass
import concourse.mybir as mybir
from concourse.bass_utils import run_bass_kernel_spmd

F32 = mybir.dt.float32
BF16 = mybir.dt.bfloat16
AF = mybir.ActivationFunctionType
ALU = mybir.AluOpType
AX = mybir.AxisListType

D = 1024
LAT = 8192
CTX = 256
NTOK = LAT + CTX
NT = NTOK // 128
NT_LAT = LAT // 128
DEPTH = 4
GRID_W = 64
ALPHA = (2.0 * DEPTH) ** 0.25
LN_EPS = 1e-5
RMS_EPS = 1e-6
DA_SCALE = 64 ** -0.5
MLA_SCALE = 96 ** -0.5
NA_SCALE = 64 ** -0.5
EV_COLS = 4032
EQ, EK, EKR, EV_, EC, EG = 0, 1024, 2048, 2112, 2624, 3008
OD_COLS = 4624
OQK, OQN, OKN, OV, OO, OGT, OVN, OG = 0, 1024, 1536, 2048, 2560, 3072, 3088, 3600
NA_ROWS, NA_COLS, GRID_H = 8, 16, 128
NEG = -30000.0


def na_plan():
    variants = {}
    plan = []
    for j in range(NT_LAT):
        r0, r1 = 2 * j, 2 * j + 1
        rs0 = min(max(r0 - NA_ROWS // 2, 0), GRID_H - NA_ROWS)
        rs1 = min(max(r1 - NA_ROWS // 2, 0), GRID_H - NA_ROWS)
        lst = []
        for kt in range(rs0 // 2, (rs1 + NA_ROWS - 1) // 2 + 1):
            key = (kt - j, rs0 - r0, rs1 - r1)
            if key not in variants:
                variants[key] = len(variants)
            lst.append((kt, variants[key]))
        plan.append(lst)
    return plan, variants


NA_PLAN, NA_VARIANTS = na_plan()
NA_NVAR = len(NA_VARIANTS)


class Tok:
    __slots__ = ("sem", "key", "val", "eng")

    def __init__(self, sem, key, val, eng):
        self.sem, self.key, self.val, self.eng = sem, key, val, eng


class Buf:
    __slots__ = ("name", "last_w", "readers", "sem", "key", "cnt")

    def __init__(self, name):
        self.name = name
        self.last_w = None
        self.readers = {}
        self.sem = None
        self.key = None
        self.cnt = 0


class Eng:
    def __init__(self, name, h, sem):
        self.name, self.h, self.sem = name, h, sem
        self.key = "E_" + name
        self.cnt = 0
        self.waited = {}


class Sync:
    def __init__(self, nc, stack):
        self.nc = nc
        self.stack = stack
        self.E = {}
        for name, h in (("pe", nc.tensor), ("act", nc.scalar), ("dve", nc.vector),
                        ("pool", nc.gpsimd), ("sp", nc.sync)):
            sem = stack.enter_context(nc.semaphore("s_" + name))
            self.E[name] = Eng(name, h, sem)
        self.dma_bufs = []
        self.free_sems = []
        self.nsem = 0
        self.nwait = 0
        self.nins = 0

    def buf(self, name):
        return Buf(name)

    def bufs(self, name, n):
        return [Buf(f"{name}{i}") for i in range(n)]

    def _deps(self, eng, r, w):
        raw = []
        oth = []
        for b in r:
            if b.last_w is not None:
                raw.append(b.last_w)
        for b in w:
            if b.last_w is not None:
                oth.append(b.last_w)
            oth.extend(b.readers.values())
        toks = []
        for t in raw:
            if t.eng is eng and eng.name == "pe":
                continue
            toks.append(t)
        for t in oth:
            if t.eng is eng:
                continue
            toks.append(t)
        return toks

    def _wait(self, eng, toks):
        for t in toks:
            if eng.waited.get(t.key, 0) >= t.val:
                continue
            eng.h.wait_ge(t.sem, t.val)
            eng.waited[t.key] = t.val
            self.nwait += 1

    def op(self, en, fn, r=(), w=()):
        eng = self.E[en]
        self._wait(eng, self._deps(eng, r, w))
        ins = fn(eng.h)
        ins.then_inc(eng.sem, 1)
        eng.cnt += 1
        self.nins += 1
        tok = Tok(eng.sem, eng.key, eng.cnt, eng)
        for b in r:
            b.readers[tok.key] = tok
        for b in w:
            b.last_w = tok
            b.readers = {}
        return tok

    def dma(self, q, out, in_, r=(), w=(), sb=None):
        eng = self.E[q]
        self._wait(eng, self._deps(eng, r, w))
        if sb is None:
            sb = w[0] if w else r[0]
        if sb.sem is None:
            if self.free_sems:
                sb.sem, sb.key, sb.cnt = self.free_sems.pop()
            else:
                sb.sem = self.stack.enter_context(self.nc.semaphore(f"d{self.nsem}"))
                sb.key = f"D{self.nsem}"
                sb.cnt = 0
                self.nsem += 1
            self.dma_bufs.append(sb)
        ins = eng.h.dma_start(out=out, in_=in_)
        ins.then_inc(sb.sem, 16)
        sb.cnt += 16
        self.nins += 1
        tok = Tok(sb.sem, sb.key, sb.cnt, None)
        for b in r:
            b.readers[tok.key] = tok
        for b in w:
            b.last_w = tok
            b.readers = {}
        return tok

    def barrier(self, engines=("pe", "act", "dve", "pool", "sp")):
        toks = [Tok(e.sem, e.key, e.cnt, e) for e in self.E.values() if e.cnt > 0]
        toks += [Tok(b.sem, b.key, b.cnt, None) for b in self.dma_bufs if b.cnt > 0]
        for en in engines:
            eng = self.E[en]
            self._wait(eng, [t for t in toks if t.eng is not eng])
        for b in self.dma_bufs:
            self.free_sems.append((b.sem, b.key, b.cnt))
            b.sem = None
            b.last_w = None
            b.readers = {}
        self.dma_bufs = []


class Prog:
    def __init__(self, layers=(0, 1, 2, 3), final_out=True, qb_filter=None, tile_filter=None):
        self.layers = tuple(layers)
        self.qb_filter = qb_filter
        self.tile_filter = tile_filter
        self.nc = bass.Bass("TRN2", target_bir_lowering=False)
        self.final_out = final_out

    def din(self, name, shape, dt=F32):
        return self.nc.dram_tensor(name, list(shape), dt, kind="ExternalInput").ap()

    def dout(self, name, shape, dt=F32):
        return self.nc.dram_tensor(name, list(shape), dt, kind="ExternalOutput").ap()

    def dscr(self, name, shape, dt=F32):
        return self.nc.dram_tensor(name, list(shape), dt, kind="Internal").ap()

    def sb(self, st, name, shape, dt=F32):
        self._uid = getattr(self, "_uid", 0) + 1
        return st.enter_context(self.nc.sbuf_tensor(f"s{self._uid}_{name}", list(shape), dt))

    def ps(self, st, name, shape, dt=F32):
        self._uid = getattr(self, "_uid", 0) + 1
        return st.enter_context(self.nc.psum_tensor(f"p{self._uid}_{name}", list(shape), dt))

    def build(self):
        nc = self.nc
        with contextlib.ExitStack() as st:
            self.S = Sync(nc, st)
            self.declare()
            self.setup_consts(st)
            nl = len(self.layers)
            for li, l in enumerate(self.layers):
                src = self.x_in if li == 0 else self.xbuf[(li - 1) % 2]
                last = li == nl - 1
                dst = self.y_out if last else self.xbuf[li % 2]
                if l % 2 == 0:
                    self.even_layer(l, src, dst, last)
                else:
                    self.odd_layer(l, src, dst, last)
            self.S.barrier()
        return nc

    def declare(self):
        self.x_in = self.din("x_in", [NTOK, D])
        self.cc = self.din("cc", [128, 8, 2])
        self.ada_w = self.din("ada_w", [DEPTH, D, 3 * D])
        self.ada_b = self.din("ada_b", [DEPTH, 3 * D])
        self.ln_g = self.din("ln_g", [DEPTH, D])
        self.ln_b = self.din("ln_b", [DEPTH, D])
        self.ident_in = self.din("ident", [128, 128])
        self.sel_in = self.din("sel", [2, 256])
        self.ev_w = self.din("ev_w", [2, D, EV_COLS])
        self.ev_bcol = self.din("ev_bcol", [2, 128, 18])
        self.ev_brow = self.din("ev_brow", [2, 1, 1920])
        self.ev_lam = self.din("ev_lam", [2, 1, 256])
        self.ev_subg = self.din("ev_subg", [2, 1, 128])
        self.ev_qg = self.din("ev_qg", [2, 128, 2])
        self.ev_kvg = self.din("ev_kvg", [2, 128, 1])
        self.ev_wuq = self.din("ev_wuq", [2, 256, 1536])
        self.ev_wukv = self.din("ev_wukv", [2, 128, 1024])
        self.ev_wo = self.din("ev_wo", [2, D, D])
        self.rope_da = self.din("rope_da", [2, 128, NTOK])
        self.rope_ml = self.din("rope_ml", [2, 128, NTOK])
        self.od_w = self.din("od_w", [2, D, OD_COLS])
        self.od_bcol = self.din("od_bcol", [2, 128, 16])
        self.od_brow = self.din("od_brow", [2, 1, 2576])
        self.od_convw = self.din("od_convw", [2, 128, 8, 5])
        self.od_convb = self.din("od_convb", [2, 128, 8])
        self.od_fb = self.din("od_fb", [2, 1, 8])
        self.od_ng = self.din("od_ng", [2, 1, 512])
        self.od_nat = self.din("od_nat", [2, 8, 128, NA_NVAR, 128])
        self.od_wo = self.din("od_wo", [2, D, D])
        self.tri = self.din("tri", [128, 3, 128])
        self.qkpre = self.dscr("qkpre", [8, 128, NTOK])
        self.mq = self.dscr("mq", [4, 128, NTOK], BF16)
        self.mk = self.dscr("mk", [4, 128, NTOK], BF16)
        self.og = self.dscr("og", [NTOK, 512])
        self.gates = self.dscr("gates", [128, NT, 16])
        self.hfb = self.dscr("hfb", [2, NTOK, 512])
        n_last_tok = LAT if self.final_out else NTOK
        self.y_out = self.dout("y", [n_last_tok, D])
        self.xbuf = [self.dscr("xbuf0", [NTOK, D]), self.dscr("xbuf1", [NTOK, D])]
        self.qda = self.dscr("qda", [4, 128, NTOK], BF16)
        self.kda = self.dscr("kda", [4, 128, NTOK], BF16)
        self.vda = self.dscr("vda", [4, 128, NT, 129], BF16)
        self.qm = self.dscr("qm", [8, 96, NTOK], BF16)
        self.kmn = self.dscr("kmn", [8, 64, NTOK], BF16)
        self.krt = self.dscr("krt", [32, NTOK], BF16)
        self.vm = self.dscr("vm", [8, 128, NT, 65], BF16)
        self.gate = self.dscr("gate", [NTOK, D])
        self.ybuf = self.dscr("ybuf", [NTOK, D])

    def setup_consts(self, st):
        S = self.S
        self.ident_f = self.sb(st, "ident_f", [128, 128])
        self.ident_b = self.sb(st, "ident_b", [128, 128], BF16)
        self.sel = self.sb(st, "sel", [2, 256])
        self.ccs = self.sb(st, "ccs", [128, 8, 2])
        self.zeros_b = self.sb(st, "zeros_b", [128, 128], BF16)
        self.ones_b = self.sb(st, "ones_b", [1, 128], BF16)
        self.zeros_w = self.sb(st, "zeros_w", [128, 512], BF16)
        self.B_const = S.buf("consts")
        b = self.B_const
        S.dma("sp", self.ident_f[:], self.ident_in[:, :], w=[b])
        S.dma("sp", self.sel[:], self.sel_in[:, :], w=[b])
        S.dma("sp", self.ccs[:], self.cc[:, :, :], w=[b])
        S.op("dve", lambda e: e.tensor_copy(self.ident_b[:], self.ident_f[:]), r=[b], w=[b])
        S.op("dve", lambda e: e.memset(self.zeros_b[:], 0.0), w=[b])
        S.op("dve", lambda e: e.memset(self.ones_b[:], 1.0), w=[b])
        S.op("dve", lambda e: e.memset(self.zeros_w[:], 0.0), w=[b])
        S.op("act", lambda e: e.activation(out=self.ccs[:], in_=self.ccs[:], func=AF.Silu), r=[b], w=[b])
        self.sc1 = self.sb(st, "sc1", [128, 8, 2])
        self.sh = self.sb(st, "sh", [128, 8, 2])
        self.g_l = self.sb(st, "g_l", [128, D])
        self.g_c = self.sb(st, "g_c", [128, D])
        self.lng = self.sb(st, "lng", [128, D])
        self.lnb = self.sb(st, "lnb", [128, D])
        self.B_mod = S.buf("mod")

    def adaln(self, l):
        S, nc = self.S, self.nc
        S.barrier()
        with contextlib.ExitStack() as st:
            wt = [self.sb(st, f"adaw{i}", [128, 3 * D]) for i in range(2)]
            Bw = S.bufs("adaw", 2)
            mrow = self.sb(st, "mrow", [2, 3 * D])
            brow = self.sb(st, "adab", [2, 3 * D])
            Bm = S.buf("mrow")
            Bb = S.buf("adab")
            pm = [self.ps(st, f"pm{i}", [128, 512]) for i in range(8)]
            Bp = S.bufs("pm", 8)
            S.dma("sp", brow[:], self.ada_b[l:l + 1, :].to_broadcast([2, 3 * D]), w=[Bb])
            S.dma("sp", self.lng[:], self.ln_g[l:l + 1, :].to_broadcast([128, D]), w=[self.B_mod])
            S.dma("sp", self.lnb[:], self.ln_b[l:l + 1, :].to_broadcast([128, D]), w=[self.B_mod])
            for k in range(8):
                S.dma("sp", wt[k % 2][:], self.ada_w[l, k * 128:(k + 1) * 128, :], w=[Bw[k % 2]])
                for n in range(6):
                    S.op("pe", lambda e, k=k, n=n: e.matmul(
                        pm[n][0:2, :], self.ccs[:, k, :], wt[k % 2][:, n * 512:(n + 1) * 512],
                        start=(k == 0), stop=(k == 7)), r=[Bw[k % 2], self.B_const], w=[Bp[n]])
            for n in range(6):
                S.op("dve", lambda e, n=n: e.tensor_tensor(
                    out=mrow[:, n * 512:(n + 1) * 512], in0=pm[n][0:2, :], in1=brow[:, n * 512:(n + 1) * 512],
                    op=ALU.add), r=[Bp[n], Bb], w=[Bm])
            for j in range(8):
                S.op("pe", lambda e, j=j: e.transpose(
                    pm[6][:, 2 * j:2 * j + 2], mrow[0:2, j * 128:(j + 1) * 128], self.ident_f[0:2, 0:2]),
                    r=[Bm, self.B_const], w=[Bp[6]])
                S.op("pe", lambda e, j=j: e.transpose(
                    pm[7][:, 2 * j:2 * j + 2], mrow[0:2, D + j * 128:D + (j + 1) * 128], self.ident_f[0:2, 0:2]),
                    r=[Bm, self.B_const], w=[Bp[7]])
            S.op("dve", lambda e: e.tensor_copy(self.sh[:].rearrange("p a b -> p (a b)"), pm[6][:, 0:16]),
                 r=[Bp[6]], w=[self.B_mod])
            S.op("dve", lambda e: e.tensor_scalar(
                out=self.sc1[:].rearrange("p a b -> p (a b)"), in0=pm[7][:, 0:16], scalar1=1.0, scalar2=None,
                op0=ALU.add), r=[Bp[7]], w=[self.B_mod])
            for n in range(2):
                S.op("pe", lambda e, n=n: e.matmul(
                    pm[n][:, :], self.sel[:, 0:128], mrow[0:2, 2 * D + n * 512:2 * D + (n + 1) * 512],
                    start=True, stop=True), r=[Bm, self.B_const], w=[Bp[n]])
                S.op("pe", lambda e, n=n: e.matmul(
                    pm[2 + n][:, :], self.sel[:, 128:256], mrow[0:2, 2 * D + n * 512:2 * D + (n + 1) * 512],
                    start=True, stop=True), r=[Bm, self.B_const], w=[Bp[2 + n]])
                S.op("dve", lambda e, n=n: e.tensor_copy(self.g_l[:, n * 512:(n + 1) * 512], pm[n][:, :]),
                     r=[Bp[n]], w=[self.B_mod])
                S.op("dve", lambda e, n=n: e.tensor_copy(self.g_c[:, n * 512:(n + 1) * 512], pm[2 + n][:, :]),
                     r=[Bp[2 + n]], w=[self.B_mod])
            S.barrier()

    def load_cast(self, st_w, dst_fn, src_fn, nparts, ncols, pieces, Bdst, scale_fn=None, name="lc"):
        S = self.S
        with contextlib.ExitStack() as st:
            stg = [self.sb(st, f"{name}_stg{i}", [nparts, ncols]) for i in range(2)]
            Bs = S.bufs(name + "_stg", 2)
            for i, (dst, src, sc) in enumerate(pieces):
                j = i % 2
                n = src.shape[-1]
                S.dma("sp", stg[j][:, 0:n], src, w=[Bs[j]])
                en = "dve" if i % 2 == 0 else "pool"
                if sc is None:
                    S.op(en, lambda e, dst=dst, j=j, n=n: e.tensor_copy(dst, stg[j][:, 0:n]), r=[Bs[j]], w=[Bdst])
                else:
                    S.op(en, lambda e, dst=dst, j=j, n=n, sc=sc: e.tensor_scalar(
                        out=dst, in0=stg[j][:, 0:n], scalar1=sc, scalar2=None, op0=ALU.mult),
                        r=[Bs[j], Bdst], w=[Bdst])
            S.barrier()

    def even_layer(self, l, src, dst, last):
        i = l // 2
        self.adaln(l)
        self.even_project(i, src)
        self.da_attention(i, l)
        self.mla_attention(i)
        self.out_stage(self.ev_wo[i], src, dst, last)

    def even_project(self, i, src):
        S, nc = self.S, self.nc
        with contextlib.ExitStack() as st:
            wb = self.sb(st, "ev_wb", [128, 8, EV_COLS], BF16)
            wuq = self.sb(st, "ev_wuqb", [128, 2, 1536], BF16)
            wukv = self.sb(st, "ev_wukvb", [128, 1024], BF16)
            bcol = self.sb(st, "ev_bcol", [128, 18])
            brow_f = self.sb(st, "ev_brow_f", [1, 1920])
            brow = self.sb(st, "ev_brow", [1, 1920], BF16)
            qg = self.sb(st, "ev_qg", [128, 2])
            kvg = self.sb(st, "ev_kvg", [128, 1])
            Bw = S.buf("ev_w")
            S.dma("sp", bcol[:], self.ev_bcol[i, :, :], w=[Bw])
            S.dma("sp", brow_f[:], self.ev_brow[i, :, :], w=[Bw])
            S.dma("sp", qg[:], self.ev_qg[i, :, :], w=[Bw])
            S.dma("sp", kvg[:], self.ev_kvg[i, :, :], w=[Bw])
            S.op("dve", lambda e: e.tensor_copy(brow[:], brow_f[:]), r=[Bw], w=[Bw])
            pieces = []
            for k in range(8):
                for c in range(4):
                    pieces.append((wb[:, k, c * 1008:(c + 1) * 1008],
                                   self.ev_w[i, k * 128:(k + 1) * 128, c * 1008:(c + 1) * 1008], None))
            self.load_cast(st, None, None, 128, 1536, pieces, Bw, name="evw")
            pieces = [(wuq[:, rc, :], self.ev_wuq[i, rc * 128:(rc + 1) * 128, :], qg[:, rc:rc + 1]) for rc in range(2)]
            pieces.append((wukv[:, :], self.ev_wukv[i, :, :], kvg[:, 0:1]))
            self.load_cast(st, None, None, 128, 1536, pieces, Bw, name="evw2")

            NXS = 3
            xt = [self.sb(st, f"xt{j}", [128, D]) for j in range(NXS)]
            Bx = S.bufs("xt", NXS)
            hT = [self.sb(st, f"hT{j}", [128, 8, 512], BF16) for j in range(2)]
            Bh = S.bufs("hT", 2)
            cosd = [self.sb(st, f"cosd{j}", [128, 512]) for j in range(2)]
            sind = [self.sb(st, f"sind{j}", [128, 512]) for j in range(2)]
            cosm = [self.sb(st, f"cosm{j}", [128, 512]) for j in range(2)]
            sinm = [self.sb(st, f"sinm{j}", [128, 512]) for j in range(2)]
            Brt = S.bufs("ropet", 2)
            t1 = [self.sb(st, f"t1_{j}", [128, 512]) for j in range(2)]
            t2 = [self.sb(st, f"t2_{j}", [128, 512]) for j in range(2)]
            Bt1 = S.bufs("t1", 2)
            Bt2 = S.bufs("t2", 2)
            fo = [self.sb(st, f"fo{j}", [128, 512], BF16) for j in range(4)]
            Bfo = S.bufs("fo", 4)
            va = [self.sb(st, f"va{j}", [128, 4, 129], BF16) for j in range(2)]
            Bva = S.bufs("va", 2)
            vma = [self.sb(st, f"vma{j}", [128, 8, 65], BF16) for j in range(2)]
            Bvma = S.bufs("vma", 2)
            cq = [self.sb(st, f"cq{j}", [128, 384]) for j in range(2)]
            Bcq = S.bufs("cq", 2)
            cqn = [self.sb(st, f"cqn{j}", [128, 384], BF16) for j in range(2)]
            Bcqn = S.bufs("cqn", 2)
            stat = [self.sb(st, f"stat{j}", [128, 8]) for j in range(2)]
            Bstat = S.bufs("stat", 2)
            junk = self.sb(st, "junk", [128, 256])
            Bjunk = S.buf("junk")
            cT = [self.sb(st, f"cT{j}", [128, 3, 512], BF16) for j in range(2)]
            BcT = S.bufs("cT", 2)
            gt = [self.sb(st, f"gt{j}", [128, D]) for j in range(2)]
            Bgt = S.bufs("gt", 2)
            pp = [self.ps(st, f"pp{j}", [128, 512]) for j in range(7)]
            ppb = self.ps(st, "ppb", [128, 1024], BF16)
            Bpp = S.bufs("pp", 7)
            Bppb = S.buf("ppb")
            for j in range(2):
                S.op("pool", lambda e, j=j: e.memset(va[j][:], 1.0), w=[Bva[j]])
                S.op("pool", lambda e, j=j: e.memset(vma[j][:], 1.0), w=[Bvma[j]])
            eps_t = self.sb(st, "eps_t", [128, 1])
            S.op("pool", lambda e: e.memset(eps_t[:], RMS_EPS), w=[Bw])

            self._pp_i = 0

            def next_pp():
                j = self._pp_i % 7
                self._pp_i += 1
                return pp[j], Bpp[j]

            nblk = (NTOK + 511) // 512
            xi = 0
            foi = 0
            evac = 0
            for blk in range(nblk):
                t0 = blk * 512
                ntok = min(512, NTOK - t0)
                nsub = ntok // 128
                m = 0 if t0 < LAT else 1
                hb = blk % 2
                S.dma("sp", cosd[hb][:, :ntok], self.rope_da[0, :, t0:t0 + ntok], w=[Brt[hb]])
                S.dma("sp", sind[hb][:, :ntok], self.rope_da[1, :, t0:t0 + ntok], w=[Brt[hb]])
                S.dma("sp", cosm[hb][:, :ntok], self.rope_ml[0, :, t0:t0 + ntok], w=[Brt[hb]])
                S.dma("sp", sinm[hb][:, :ntok], self.rope_ml[1, :, t0:t0 + ntok], w=[Brt[hb]])
                for s in range(nsub):
                    xj = xi % NXS
                    xi += 1
                    S.dma("sp", xt[xj][:], src[t0 + s * 128:t0 + (s + 1) * 128, :], w=[Bx[xj]])
                    for half in range(2):
                        p, Bp = next_pp()
                        for q in range(4):
                            dc = half * 4 + q
                            S.op("pe", lambda e, p=p, q=q, dc=dc, xj=xj: e.transpose(
                                p[:, q * 128:(q + 1) * 128], xt[xj][:, dc * 128:(dc + 1) * 128], self.ident_f[:]),
                                r=[Bx[xj], self.B_const], w=[Bp])
                        for q in range(4):
                            dc = half * 4 + q
                            if evac % 2 == 0:
                                S.op("act", lambda e, p=p, q=q, dc=dc, s=s, m=m: e.activation(
                                    out=hT[hb][:, dc, s * 128:(s + 1) * 128], in_=p[:, q * 128:(q + 1) * 128],
                                    func=AF.Identity, scale=self.sc1[:, dc, m:m + 1], bias=self.sh[:, dc, m:m + 1]),
                                    r=[Bp, self.B_mod], w=[Bh[hb]])
                            else:
                                S.op("dve", lambda e, p=p, q=q, dc=dc, s=s, m=m: e.tensor_scalar(
                                    out=hT[hb][:, dc, s * 128:(s + 1) * 128], in0=p[:, q * 128:(q + 1) * 128],
                                    scalar1=self.sc1[:, dc, m:m + 1], scalar2=self.sh[:, dc, m:m + 1],
                                    op0=ALU.mult, op1=ALU.add), r=[Bp, self.B_mod], w=[Bh[hb]])
                            evac += 1
                for grp, base, dstT, bofs in ((0, EQ, self.qda, 0), (1, EK, self.kda, 8)):
                    for h in range(4):
                        pa, Ba = next_pp()
                        pb, Bb = next_pp()
                        for k in range(8):
                            S.op("pe", lambda e, pa=pa, k=k, c0=base + h * 128: e.matmul(
                                pa[:, :ntok], wb[:, k, c0:c0 + 128], hT[hb][:, k, :ntok], start=(k == 0), stop=(k == 7)),
                                r=[Bw, Bh[hb]], w=[Ba])
                        for k in range(8):
                            S.op("pe", lambda e, pb=pb, k=k, c0=base + 512 + h * 128: e.matmul(
                                pb[:, :ntok], wb[:, k, c0:c0 + 128], hT[hb][:, k, :ntok], start=(k == 0), stop=(k == 7)),
                                r=[Bw, Bh[hb]], w=[Bb])
                        tj = (grp * 4 + h) % 2
                        S.op("dve", lambda e, pa=pa, tj=tj, c=bofs + h: e.scalar_tensor_tensor(
                            out=t1[tj][:, :ntok], in0=pa[:, :ntok], scalar=bcol[:, c:c + 1], in1=cosd[hb][:, :ntok],
                            op0=ALU.add, op1=ALU.mult), r=[Ba, Brt[hb], Bw], w=[Bt1[tj]])
                        S.op("dve", lambda e, pb=pb, tj=tj, c=bofs + 4 + h: e.scalar_tensor_tensor(
                            out=t2[tj][:, :ntok], in0=pb[:, :ntok], scalar=bcol[:, c:c + 1], in1=sind[hb][:, :ntok],
                            op0=ALU.add, op1=ALU.mult), r=[Bb, Brt[hb], Bw], w=[Bt2[tj]])
                        fj = foi % 4
                        foi += 1
                        S.op("pool", lambda e, tj=tj, fj=fj: e.tensor_tensor(
                            out=fo[fj][:, :ntok], in0=t1[tj][:, :ntok], in1=t2[tj][:, :ntok], op=ALU.add),
                            r=[Bt1[tj], Bt2[tj]], w=[Bfo[fj]])
                        S.dma("pool", dstT[h, :, t0:t0 + ntok], fo[fj][:, :ntok], r=[Bfo[fj]])
                pa, Ba = next_pp()
                pb, Bb = next_pp()
                for k in range(8):
                    S.op("pe", lambda e, pa=pa, k=k: e.matmul(
                        pa[0:32, :ntok], wb[:, k, EKR:EKR + 32], hT[hb][:, k, :ntok], start=(k == 0), stop=(k == 7)),
                        r=[Bw, Bh[hb]], w=[Ba])
                for k in range(8):
                    S.op("pe", lambda e, pb=pb, k=k: e.matmul(
                        pb[0:32, :ntok], wb[:, k, EKR + 32:EKR + 64], hT[hb][:, k, :ntok], start=(k == 0), stop=(k == 7)),
                        r=[Bw, Bh[hb]], w=[Bb])
                tj = 0
                S.op("dve", lambda e, pa=pa: e.scalar_tensor_tensor(
                    out=t1[tj][0:32, :ntok], in0=pa[0:32, :ntok], scalar=bcol[0:32, 16:17], in1=cosm[hb][0:32, :ntok],
                    op0=ALU.add, op1=ALU.mult), r=[Ba, Brt[hb], Bw], w=[Bt1[tj]])
                S.op("dve", lambda e, pb=pb: e.scalar_tensor_tensor(
                    out=t2[tj][0:32, :ntok], in0=pb[0:32, :ntok], scalar=bcol[0:32, 17:18], in1=sinm[hb][0:32, :ntok],
                    op0=ALU.add, op1=ALU.mult), r=[Bb, Brt[hb], Bw], w=[Bt2[tj]])
                fj = foi % 4
                foi += 1
                S.op("pool", lambda e, fj=fj: e.tensor_tensor(
                    out=fo[fj][0:32, :ntok], in0=t1[tj][0:32, :ntok], in1=t2[tj][0:32, :ntok], op=ALU.add),
                    r=[Bt1[tj], Bt2[tj]], w=[Bfo[fj]])
                S.dma("pool", self.krt[:, t0:t0 + ntok], fo[fj][0:32, :ntok], r=[Bfo[fj]])
                cb = blk % 2
                for s in range(nsub):
                    tt = (t0 // 128) + s
                    tok0 = t0 + s * 128
                    p, Bp = next_pp()
                    for k in range(8):
                        S.op("pe", lambda e, p=p, k=k, s=s: e.matmul(
                            p[:, :], hT[hb][:, k, s * 128:(s + 1) * 128], wb[:, k, EV_:EV_ + 512], start=(k == 0), stop=False),
                            r=[Bw, Bh[hb]], w=[Bp])
                    S.op("pe", lambda e, p=p: e.matmul(p[:, :], self.ones_b[0:1, :], brow[0:1, 0:512], start=False, stop=True),
                         r=[Bw, self.B_const], w=[Bp])
                    vj = tt % 2
                    S.op("act", lambda e, p=p, vj=vj: e.activation(
                        out=va[vj][:, :, 0:128], in_=p[:, :].rearrange("p (h c) -> p h c", h=4), func=AF.Copy),
                        r=[Bp], w=[Bva[vj]])
                    S.dma("pool", self.vda[:, :, tt, :].rearrange("h p c -> p h c"), va[vj][:], r=[Bva[vj]])
                    p, Bp = next_pp()
                    for k in range(8):
                        S.op("pe", lambda e, p=p, k=k, s=s: e.matmul(
                            p[:, 0:384], hT[hb][:, k, s * 128:(s + 1) * 128], wb[:, k, EC:EC + 384], start=(k == 0), stop=False),
                            r=[Bw, Bh[hb]], w=[Bp])
                    S.op("pe", lambda e, p=p: e.matmul(p[:, 0:384], self.ones_b[0:1, :], brow[0:1, 512:896], start=False, stop=True),
                         r=[Bw, self.B_const], w=[Bp])
                    cj = tt % 2
                    S.op("act", lambda e, p=p, cj=cj: e.activation(out=cq[cj][:], in_=p[:, 0:384], func=AF.Copy),
                         r=[Bp], w=[Bcq[cj]])
                    S.op("act", lambda e, cj=cj: e.activation(
                        out=junk[:, 0:256], in_=cq[cj][:, 0:256], func=AF.Square, accum_out=stat[cj][:, 0:1]),
                        r=[Bcq[cj]], w=[Bjunk, Bstat[cj]])
                    S.op("act", lambda e, cj=cj: e.activation(
                        out=junk[:, 0:128], in_=cq[cj][:, 256:384], func=AF.Square, accum_out=stat[cj][:, 1:2]),
                        r=[Bcq[cj]], w=[Bjunk, Bstat[cj]])
                    S.op("act", lambda e, cj=cj: e.activation(out=stat[cj][:, 2:3], in_=stat[cj][:, 0:1], func=AF.Sqrt,
                                                              scale=1.0 / 256, bias=eps_t[:, 0:1]), r=[Bstat[cj], Bw], w=[Bstat[cj]])
                    S.op("act", lambda e, cj=cj: e.activation(out=stat[cj][:, 3:4], in_=stat[cj][:, 1:2], func=AF.Sqrt,
                                                              scale=1.0 / 128, bias=eps_t[:, 0:1]), r=[Bstat[cj], Bw], w=[Bstat[cj]])
                    S.op("dve", lambda e, cj=cj: e.reciprocal(stat[cj][:, 4:6], stat[cj][:, 2:4]), r=[Bstat[cj]], w=[Bstat[cj]])
                    S.op("dve", lambda e, cj=cj: e.tensor_scalar(
                        out=cqn[cj][:, 0:256], in0=cq[cj][:, 0:256], scalar1=stat[cj][:, 4:5], scalar2=None, op0=ALU.mult),
                        r=[Bcq[cj], Bstat[cj]], w=[Bcqn[cj]])
                    S.op("pool", lambda e, cj=cj: e.tensor_scalar(
                        out=cqn[cj][:, 256:384], in0=cq[cj][:, 256:384], scalar1=stat[cj][:, 5:6], scalar2=None, op0=ALU.mult),
                        r=[Bcq[cj], Bstat[cj]], w=[Bcqn[cj]])
                    for rc in range(3):
                        S.op("pe", lambda e, rc=rc, cj=cj: e.transpose(
                            ppb[:, rc * 128:(rc + 1) * 128], cqn[cj][:, rc * 128:(rc + 1) * 128], self.ident_b[:]),
                            r=[Bcqn[cj], self.B_const], w=[Bppb])
                    S.op("dve", lambda e, s=s: e.tensor_copy(
                        cT[cb][:, :, s * 128:(s + 1) * 128], ppb[:, 0:384].rearrange("p (r t) -> p r t", r=3)),
                        r=[Bppb], w=[BcT[cb]])
                    gj = tt % 2
                    for n in range(2):
                        p, Bp = next_pp()
                        for k in range(8):
                            S.op("pe", lambda e, p=p, k=k, s=s, n=n: e.matmul(
                                p[:, :], hT[hb][:, k, s * 128:(s + 1) * 128], wb[:, k, EG + n * 512:EG + (n + 1) * 512],
                                start=(k == 0), stop=False), r=[Bw, Bh[hb]], w=[Bp])
                        S.op("pe", lambda e, p=p, n=n: e.matmul(
                            p[:, :], self.ones_b[0:1, :], brow[0:1, 896 + n * 512:896 + (n + 1) * 512], start=False, stop=True),
                            r=[Bw, self.B_const], w=[Bp])
                        S.op("act", lambda e, p=p, n=n, gj=gj: e.activation(
                            out=gt[gj][:, n * 512:(n + 1) * 512], in_=p[:, :], func=AF.Silu), r=[Bp], w=[Bgt[gj]])
                    S.dma("pool", self.gate[tok0:tok0 + 128, :], gt[gj][:], r=[Bgt[gj]])
                for h in range(8):
                    pa, Ba = next_pp()
                    pb, Bb = next_pp()
                    for rc in range(2):
                        S.op("pe", lambda e, pa=pa, rc=rc, h=h: e.matmul(
                            pa[0:96, :ntok], wuq[:, rc, h * 96:(h + 1) * 96], cT[cb][:, rc, :ntok], start=(rc == 0), stop=(rc == 1)),
                            r=[Bw, BcT[cb]], w=[Ba])
                    for rc in range(2):
                        S.op("pe", lambda e, pb=pb, rc=rc, h=h: e.matmul(
                            pb[0:96, :ntok], wuq[:, rc, 768 + h * 96:768 + (h + 1) * 96], cT[cb][:, rc, :ntok],
                            start=(rc == 0), stop=(rc == 1)), r=[Bw, BcT[cb]], w=[Bb])
                    tj = h % 2
                    fj = foi % 4
                    foi += 1
                    S.op("dve", lambda e, pa=pa, tj=tj: e.tensor_tensor(
                        out=t1[tj][64:96, :ntok], in0=pa[64:96, :ntok], in1=cosm[hb][64:96, :ntok], op=ALU.mult),
                        r=[Ba, Brt[hb]], w=[Bt1[tj]])
                    S.op("dve", lambda e, pb=pb, tj=tj: e.tensor_tensor(
                        out=t2[tj][64:96, :ntok], in0=pb[64:96, :ntok], in1=sinm[hb][64:96, :ntok], op=ALU.mult),
                        r=[Bb, Brt[hb]], w=[Bt2[tj]])
                    S.op("act", lambda e, pa=pa, fj=fj: e.activation(out=fo[fj][0:64, :ntok], in_=pa[0:64, :ntok], func=AF.Copy),
                         r=[Ba], w=[Bfo[fj]])
                    S.op("pool", lambda e, tj=tj, fj=fj: e.tensor_tensor(
                        out=fo[fj][64:96, :ntok], in0=t1[tj][64:96, :ntok], in1=t2[tj][64:96, :ntok], op=ALU.add),
                        r=[Bt1[tj], Bt2[tj]], w=[Bfo[fj]])
                    S.dma("pool", self.qm[h, :, t0:t0 + ntok], fo[fj][0:96, :ntok], r=[Bfo[fj]])
                for hp in range(4):
                    p, Bp = next_pp()
                    S.op("pe", lambda e, p=p, hp=hp: e.matmul(
                        p[:, :ntok], wukv[:, hp * 128:(hp + 1) * 128], cT[cb][:, 2, :ntok], start=True, stop=True),
                        r=[Bw, BcT[cb]], w=[Bp])
                    fj = foi % 4
                    foi += 1
                    S.op("act", lambda e, p=p, fj=fj: e.activation(out=fo[fj][:, :ntok], in_=p[:, :ntok], func=AF.Copy),
                         r=[Bp], w=[Bfo[fj]])
                    S.dma("pool", self.kmn[2 * hp, :, t0:t0 + ntok], fo[fj][0:64, :ntok], r=[Bfo[fj]])
                    S.dma("pool", self.kmn[2 * hp + 1, :, t0:t0 + ntok], fo[fj][64:128, :ntok], r=[Bfo[fj]])
                for s in range(nsub):
                    tt = (t0 // 128) + s
                    p, Bp = next_pp()
                    S.op("pe", lambda e, p=p, s=s: e.matmul(
                        p[:, :], cT[cb][:, 2, s * 128:(s + 1) * 128], wukv[:, 512:1024], start=True, stop=True),
                        r=[Bw, BcT[cb]], w=[Bp])
                    vj = tt % 2
                    S.op("act", lambda e, p=p, vj=vj: e.activation(
                        out=vma[vj][:, :, 0:64], in_=p[:, :].rearrange("p (h c) -> p h c", h=8), func=AF.Copy),
                        r=[Bp], w=[Bvma[vj]])
                    S.dma("pool", self.vm[:, :, tt, :].rearrange("h p c -> p h c"), vma[vj][:], r=[Bvma[vj]])
            S.barrier()

    def attn_core(self, st, name, KT, QT, VA, Bkv, kparts, nv, scale, qblocks, finalize, kslices=None):
        S = self.S
        nmap = len(kparts)
        nacc_per_bank = 512 // nv
        n_acc = nmap * 4
        n_acc_banks = (n_acc + nacc_per_bank - 1) // nacc_per_bank
        accb = [self.ps(st, f"{name}_acc{j}", [128, 512]) for j in range(n_acc_banks)]
        Bacc = S.bufs(name + "_acc", n_acc_banks)
        n_sc = 8 - n_acc_banks
        n_sc = min(n_sc, 4)
        scb = [self.ps(st, f"{name}_sc{j}", [128, 512]) for j in range(n_sc)]
        Bsc = S.bufs(name + "_sc", n_sc)
        NP = 4
        pt = [self.sb(st, f"{name}_pt{j}", [128, 512], BF16) for j in range(NP)]
        Bpt = S.bufs(name + "_pt", NP)

        def acc_ap(mi, sub):
            idx = mi * 4 + sub
            b = idx // nacc_per_bank
            o = (idx % nacc_per_bank) * nv
            return accb[b][:, o:o + nv], Bacc[b]

        sci = 0
        pti = 0
        for qbi, (q0, nq, ktiles) in enumerate(qblocks):
            nsub = nq // 128
            for b in range(n_acc_banks):
                S.op("pe", lambda e, b=b: e.matmul(accb[b][:, :], self.zeros_b[:, :], self.zeros_w[:, :], start=True, stop=False,
                                                   skip_group_check=True), r=[self.B_const], w=[Bacc[b]])
            units = [(kt, mi) for kt in ktiles for mi in range(nmap)]
            pend = []

            def emit_score(u):
                nonlocal sci
                kt, mi = u
                lo, hi = kparts[mi]
                j = sci % n_sc
                sci += 1
                S.op("pe", lambda e, j=j, lo=lo, hi=hi, kt=kt: e.matmul(
                    scb[j][:, :nq], KT[lo:hi, kt * 128:(kt + 1) * 128], QT[lo:hi, q0:q0 + nq], start=True, stop=True),
                    r=[Bkv], w=[Bsc[j]])
                return j

            def emit_exp_pv(u, j, lastflag):
                nonlocal pti
                kt, mi = u
                pj = pti % NP
                pti += 1
                S.op("act", lambda e, j=j, pj=pj: e.activation(out=pt[pj][:, :nq], in_=scb[j][:, :nq], func=AF.Exp, scale=scale),
                     r=[Bsc[j]], w=[Bpt[pj]])
                for sub in range(nsub):
                    ap, Ba = acc_ap(mi, sub)
                    S.op("pe", lambda e, ap=ap, pj=pj, sub=sub, kt=kt: e.matmul(
                        ap, pt[pj][:, sub * 128:(sub + 1) * 128], VA[:, kt, :], start=False, stop=lastflag,
                        skip_group_check=True), r=[Bpt[pj], Bkv], w=[Ba])

            LOOK = min(n_sc - 1, 2)
            q = []
            for ui, u in enumerate(units):
                q.append((u, emit_score(u)))
                if len(q) > LOOK:
                    u0, j0 = q.pop(0)
                    emit_exp_pv(u0, j0, False)
            while q:
                u0, j0 = q.pop(0)
                emit_exp_pv(u0, j0, u0[0] == ktiles[-1])
            for sub in range(nsub):
                accs = [acc_ap(mi, sub) for mi in range(nmap)]
                finalize(qbi, q0, sub, accs)

    def qblocks_all(self):
        qb = []
        lat_k = list(range(NT))
        for b in range(LAT // 512):
            qb.append((b * 512, 512, lat_k))
        qb.append((LAT, CTX, [NT_LAT, NT_LAT + 1]))
        if self.qb_filter is not None:
            qb = [q for i, q in enumerate(qb) if i in self.qb_filter]
        return qb

    def da_attention(self, i, l):
        S = self.S
        lam_init = 0.8 - 0.6 * math.exp(-0.3 * l)
        with contextlib.ExitStack() as st:
            KT = self.sb(st, "da_KT", [128, NTOK], BF16)
            QT = self.sb(st, "da_QT", [128, NTOK], BF16)
            VA = self.sb(st, "da_VA", [128, NT, 129], BF16)
            Bkv = S.buf("da_kv")
            lamt = self.sb(st, "lamt", [128, 256])
            lamw = self.sb(st, "lamw", [128, 8])
            subg = self.sb(st, "subg", [128, 128])
            Bl = S.buf("lam")
            eps_t = self.sb(st, "da_eps", [128, 1])
            S.op("pool", lambda e: e.memset(eps_t[:], RMS_EPS), w=[Bl])
            S.dma("sp", lamt[:], self.ev_lam[i, :, :].to_broadcast([128, 256]), w=[Bl])
            S.dma("sp", subg[:], self.ev_subg[i, :, :].to_broadcast([128, 128]), w=[Bl])
            junk = self.sb(st, "da_junk", [128, 128])
            Bj = S.buf("da_junk")
            S.op("dve", lambda e: e.tensor_tensor(out=junk[:, 0:64], in0=lamt[:, 0:64], in1=lamt[:, 64:128], op=ALU.mult),
                 r=[Bl], w=[Bj])
            S.op("dve", lambda e: e.reduce_sum(out=lamw[:, 0:1], in_=junk[:, 0:64], axis=AX.X), r=[Bj], w=[Bl])
            S.op("dve", lambda e: e.tensor_tensor(out=junk[:, 64:128], in0=lamt[:, 128:192], in1=lamt[:, 192:256], op=ALU.mult),
                 r=[Bl], w=[Bj])
            S.op("dve", lambda e: e.reduce_sum(out=lamw[:, 1:2], in_=junk[:, 64:128], axis=AX.X), r=[Bj], w=[Bl])
            S.op("act", lambda e: e.activation(out=lamw[:, 2:4], in_=lamw[:, 0:2], func=AF.Exp), r=[Bl], w=[Bl])
            S.op("dve", lambda e: e.tensor_tensor(out=lamw[:, 4:5], in0=lamw[:, 3:4], in1=lamw[:, 2:3], op=ALU.subtract),
                 r=[Bl], w=[Bl])
            S.op("dve", lambda e: e.tensor_scalar(out=lamw[:, 5:6], in0=lamw[:, 4:5], scalar1=-lam_init, scalar2=None, op0=ALU.add),
                 r=[Bl], w=[Bl])
            NF = 3
            rr = [self.sb(st, f"da_rr{j}", [128, 8]) for j in range(NF)]
            ta = [self.sb(st, f"da_ta{j}", [128, 128]) for j in range(NF)]
            td = [self.sb(st, f"da_td{j}", [128, 128]) for j in range(NF)]
            to = [self.sb(st, f"da_to{j}", [128, 128]) for j in range(NF)]
            Bf = S.bufs("da_fin", NF)
            Bto = S.bufs("da_to", NF)
            self._fi = 0
            for h in range(4):
                S.dma("sp", KT[:], self.kda[h, :, :], w=[Bkv])
                S.dma("sp", QT[:], self.qda[h, :, :], w=[Bkv])
                S.dma("sp", VA[:], self.vda[h, :, :, :], w=[Bkv])

                def fin(qbi, q0, sub, accs, h=h):
                    j = self._fi % NF
                    self._fi += 1
                    (a1, B1), (a2, B2) = accs
                    S.op("dve", lambda e: e.reciprocal(rr[j][:, 0:1], a1[:, 128:129]), r=[B1], w=[Bf[j]])
                    S.op("dve", lambda e: e.reciprocal(rr[j][:, 1:2], a2[:, 128:129]), r=[B2], w=[Bf[j]])
                    S.op("dve", lambda e: e.tensor_tensor(out=rr[j][:, 2:3], in0=rr[j][:, 1:2], in1=lamw[:, 5:6], op=ALU.mult),
                         r=[Bf[j], Bl], w=[Bf[j]])
                    S.op("dve", lambda e: e.tensor_scalar(out=ta[j][:], in0=a1[:, 0:128], scalar1=rr[j][:, 0:1], scalar2=None,
                                                          op0=ALU.mult), r=[B1, Bf[j]], w=[Bf[j]])
                    S.op("dve", lambda e: e.scalar_tensor_tensor(out=td[j][:], in0=a2[:, 0:128], scalar=rr[j][:, 2:3], in1=ta[j][:],
                                                                 op0=ALU.mult, op1=ALU.add), r=[B2, Bf[j]], w=[Bf[j]])
                    S.op("act", lambda e: e.activation(out=ta[j][:], in_=td[j][:], func=AF.Square, accum_out=rr[j][:, 3:4]),
                         r=[Bf[j]], w=[Bf[j]])
                    S.op("act", lambda e: e.activation(out=rr[j][:, 4:5], in_=rr[j][:, 3:4], func=AF.Sqrt, scale=1.0 / 128,
                                                       bias=eps_t[:, 0:1]), r=[Bf[j], Bl], w=[Bf[j]])
                    S.op("dve", lambda e: e.reciprocal(rr[j][:, 5:6], rr[j][:, 4:5]), r=[Bf[j]], w=[Bf[j]])
                    S.op("pool", lambda e: e.tensor_scalar(out=td[j][:], in0=td[j][:], scalar1=rr[j][:, 5:6], scalar2=(1.0 - lam_init),
                                                           op0=ALU.mult, op1=ALU.mult), r=[Bf[j]], w=[Bf[j]])
                    S.op("pool", lambda e: e.tensor_tensor(out=to[j][:], in0=td[j][:], in1=subg[:], op=ALU.mult),
                         r=[Bf[j], Bl], w=[Bto[j]])
                    tok0 = q0 + sub * 128
                    S.dma("pool", self.ybuf[tok0:tok0 + 128, h * 128:(h + 1) * 128], to[j][:], r=[Bto[j]])

                with contextlib.ExitStack() as st2:
                    self.attn_core(st2, f"da{h}", KT, QT, VA, Bkv, [(0, 64), (64, 128)], 129, DA_SCALE,
                                   self.qblocks_all(), fin)
                    S.barrier()
            S.barrier()

    def mla_attention(self, i):
        S = self.S
        with contextlib.ExitStack() as st:
            KT = self.sb(st, "ml_KT", [128, NTOK], BF16)
            QT = self.sb(st, "ml_QT", [128, NTOK], BF16)
            VA = self.sb(st, "ml_VA", [128, NT, 65], BF16)
            Bkv = S.buf("ml_kv")
            NF = 3
            rr = [self.sb(st, f"ml_rr{j}", [128, 2]) for j in range(NF)]
            to = [self.sb(st, f"ml_to{j}", [128, 64]) for j in range(NF)]
            Bto = S.bufs("ml_to", NF)
            self._fi = 0
            for h in range(8):
                S.dma("sp", KT[0:64, :], self.kmn[h, :, :], w=[Bkv])
                S.dma("sp", KT[64:96, :], self.krt[:, :], w=[Bkv])
                S.dma("sp", QT[0:96, :], self.qm[h, :, :], w=[Bkv])
                S.dma("sp", VA[:], self.vm[h, :, :, :], w=[Bkv])

                def fin(qbi, q0, sub, accs, h=h):
                    j = self._fi % NF
                    self._fi += 1
                    (a1, B1), = accs
                    S.op("dve", lambda e: e.reciprocal(rr[j][:, 0:1], a1[:, 64:65]), r=[B1], w=[Bto[j]])
                    S.op("dve", lambda e: e.tensor_scalar(out=to[j][:], in0=a1[:, 0:64], scalar1=rr[j][:, 0:1], scalar2=None,
                                                          op0=ALU.mult), r=[B1, Bto[j]], w=[Bto[j]])
                    tok0 = q0 + sub * 128
                    S.dma("pool", self.ybuf[tok0:tok0 + 128, 512 + h * 64:512 + (h + 1) * 64], to[j][:], r=[Bto[j]])

                with contextlib.ExitStack() as st2:
                    self.attn_core(st2, f"ml{h}", KT, QT, VA, Bkv, [(0, 96)], 65, MLA_SCALE, self.qblocks_all(), fin)
                    S.barrier()
            S.barrier()

    def out_stage(self, wo_dram, src, dst, last):
        S = self.S
        with contextlib.ExitStack() as st:
            wo = self.sb(st, "wo", [128, 8, D], BF16)
            Bw = S.buf("wo")
            pieces = [(wo[:, k, :], wo_dram[k * 128:(k + 1) * 128, :], None) for k in range(8)]
            self.load_cast(st, None, None, 128, D, pieces, Bw, name="wo")
            NB = 2
            yt = [self.sb(st, f"o_yt{j}", [128, D]) for j in range(NB)]
            gt = [self.sb(st, f"o_gt{j}", [128, D]) for j in range(NB)]
            xt = [self.sb(st, f"o_xt{j}", [128, D]) for j in range(NB)]
            yg = [self.sb(st, f"o_yg{j}", [128, D], BF16) for j in range(NB)]
            ygT = [self.sb(st, f"o_ygT{j}", [128, D], BF16) for j in range(NB)]
            rt = [self.sb(st, f"o_rt{j}", [128, D]) for j in range(NB)]
            ot = [self.sb(st, f"o_ot{j}", [128, D]) for j in range(NB)]
            stt = [self.sb(st, f"o_st{j}", [128, 16]) for j in range(NB)]
            By, Bg, Bx, Byg, BygT, Br, Bo, Bs = (S.bufs(n, NB) for n in ("o_yt", "o_gt", "o_xt", "o_yg", "o_ygT", "o_rt", "o_ot", "o_st"))
            ptr = [self.ps(st, f"o_ptr{j}", [128, D], BF16) for j in range(2)]
            Bptr = S.bufs("o_ptr", 2)
            pout = [self.ps(st, f"o_po{j}", [128, 512]) for j in range(4)]
            Bpo = S.bufs("o_po", 4)
            eps_t = self.sb(st, "o_eps", [128, 1])
            S.op("pool", lambda e: e.memset(eps_t[:], LN_EPS), w=[Bw])
            ntiles = NT_LAT if (last and self.final_out) else NT
            for t in range(ntiles):
                if self.tile_filter is not None and t not in self.tile_filter:
                    continue
                j = t % NB
                tok0 = t * 128
                isctx = t >= NT_LAT
                G = self.g_c if isctx else self.g_l
                S.dma("sp", yt[j][:], self.ybuf[tok0:tok0 + 128, :], w=[By[j]])
                S.dma("sp", gt[j][:], self.gate[tok0:tok0 + 128, :], w=[Bg[j]])
                S.dma("sp", xt[j][:], src[tok0:tok0 + 128, :], w=[Bx[j]])
                S.op("pool", lambda e, j=j: e.tensor_tensor(out=yg[j][:], in0=yt[j][:], in1=gt[j][:], op=ALU.mult),
                     r=[By[j], Bg[j]], w=[Byg[j]])
                pj = t % 2
                for ec in range(8):
                    S.op("pe", lambda e, ec=ec, j=j, pj=pj: e.transpose(
                        ptr[pj][:, ec * 128:(ec + 1) * 128], yg[j][:, ec * 128:(ec + 1) * 128], self.ident_b[:]),
                        r=[Byg[j], self.B_const], w=[Bptr[pj]])
                S.op("act", lambda e, j=j, pj=pj: e.activation(out=ygT[j][:], in_=ptr[pj][:], func=AF.Copy),
                     r=[Bptr[pj]], w=[BygT[j]])
                for n in range(2):
                    pn = (t % 2) * 2 + n
                    for ec in range(8):
                        S.op("pe", lambda e, ec=ec, n=n, pn=pn, j=j: e.matmul(
                            pout[pn][:, :], ygT[j][:, ec * 128:(ec + 1) * 128], wo[:, ec, n * 512:(n + 1) * 512],
                            start=(ec == 0), stop=(ec == 7)), r=[BygT[j], Bw], w=[Bpo[pn]])
                    S.op("dve", lambda e, n=n, pn=pn, j=j, G=G: e.tensor_tensor(
                        out=rt[j][:, n * 512:(n + 1) * 512], in0=pout[pn][:, :], in1=G[:, n * 512:(n + 1) * 512], op=ALU.mult),
                        r=[Bpo[pn], self.B_mod], w=[Br[j]])
                S.op("dve", lambda e, j=j: e.scalar_tensor_tensor(
                    out=rt[j][:], in0=xt[j][:], scalar=ALPHA, in1=rt[j][:], op0=ALU.mult, op1=ALU.add),
                    r=[Bx[j], Br[j]], w=[Br[j]])
                for n in range(2):
                    S.op("dve", lambda e, n=n, j=j: e.bn_stats(stt[j][:, n * 6:(n + 1) * 6], rt[j][:, n * 512:(n + 1) * 512]),
                         r=[Br[j]], w=[Bs[j]])
                S.op("dve", lambda e, j=j: e.bn_aggr(stt[j][:, 12:14], stt[j][:, 0:12]), r=[Bs[j]], w=[Bs[j]])
                S.op("act", lambda e, j=j: e.activation(out=stt[j][:, 14:15], in_=stt[j][:, 13:14], func=AF.Sqrt, scale=1.0,
                                                        bias=eps_t[:, 0:1]), r=[Bs[j], Bw], w=[Bs[j]])
                S.op("dve", lambda e, j=j: e.reciprocal(stt[j][:, 15:16], stt[j][:, 14:15]), r=[Bs[j]], w=[Bs[j]])
                S.op("dve", lambda e, j=j: e.tensor_scalar(
                    out=ot[j][:], in0=rt[j][:], scalar1=stt[j][:, 12:13], scalar2=stt[j][:, 15:16], op0=ALU.subtract, op1=ALU.mult),
                    r=[Br[j], Bs[j]], w=[Bo[j]])
                S.op("pool", lambda e, j=j: e.tensor_tensor(out=ot[j][:], in0=ot[j][:], in1=self.lng[:], op=ALU.mult),
                     r=[Bo[j], self.B_mod], w=[Bo[j]])
                S.op("pool", lambda e, j=j: e.tensor_tensor(out=ot[j][:], in0=ot[j][:], in1=self.lnb[:], op=ALU.add),
                     r=[Bo[j], self.B_mod], w=[Bo[j]])
                S.dma("sp", dst[tok0:tok0 + 128, :], ot[j][:], r=[Bo[j]])
            S.barrier()

    def odd_layer(self, l, src, dst, last):
        i = l // 2
        self.adaln(l)
        self.odd_project(i, src)
        self.odd_conv(i)
        self.mlstm(i)
        self.mlstm_post(i)
        self.na_attention(i, last)
        self.out_stage(self.od_wo[i], src, dst, last)

    def odd_project(self, i, src):
        S = self.S
        with contextlib.ExitStack() as st:
            wb = self.sb(st, "od_wb", [128, 8, OD_COLS], BF16)
            bcol = self.sb(st, "od_bcol", [128, 16])
            brow_f = self.sb(st, "od_brow_f", [1, 2576])
            brow = self.sb(st, "od_brow", [1, 2576], BF16)
            fb = self.sb(st, "od_fb", [128, 8])
            Bw = S.buf("od_w")
            S.dma("sp", bcol[:], self.od_bcol[i, :, :], w=[Bw])
            S.dma("sp", brow_f[:], self.od_brow[i, :, :], w=[Bw])
            S.dma("sp", fb[:], self.od_fb[i, :, :].to_broadcast([128, 8]), w=[Bw])
            S.op("dve", lambda e: e.tensor_copy(brow[:], brow_f[:]), r=[Bw], w=[Bw])
            pieces = []
            for k in range(8):
                for c in range(4):
                    pieces.append((wb[:, k, c * 1156:(c + 1) * 1156],
                                   self.od_w[i, k * 128:(k + 1) * 128, c * 1156:(c + 1) * 1156], None))
            self.load_cast(st, None, None, 128, 1156, pieces, Bw, name="odw")
            NXS = 3
            xt = [self.sb(st, f"xt{j}", [128, D]) for j in range(NXS)]
            Bx = S.bufs("xt", NXS)
            hT = [self.sb(st, f"hT{j}", [128, 8, 512], BF16) for j in range(2)]
            Bh = S.bufs("hT", 2)
            ff = [self.sb(st, f"ff{j}", [128, 512]) for j in range(3)]
            Bff = S.bufs("ff", 3)
            fo = [self.sb(st, f"fo{j}", [128, 512], BF16) for j in range(3)]
            Bfo = S.bufs("fo", 3)
            va = [self.sb(st, f"va{j}", [128, 4, 129], BF16) for j in range(2)]
            Bva = S.bufs("va", 2)
            vna = [self.sb(st, f"vna{j}", [128, 8, 65], BF16) for j in range(2)]
            Bvna = S.bufs("vna", 2)
            ot = [self.sb(st, f"ot{j}", [128, 512]) for j in range(2)]
            Bot = S.bufs("ot", 2)
            gt = [self.sb(st, f"gt{j}", [128, D]) for j in range(2)]
            Bgt = S.bufs("gt", 2)
            gs = [self.sb(st, f"gs{j}", [128, 4, 16]) for j in range(2)]
            gtmp = [self.sb(st, f"gtmp{j}", [128, 4, 8]) for j in range(2)]
            Bgs = S.bufs("gs", 2)
            pp = [self.ps(st, f"pp{j}", [128, 512]) for j in range(7)]
            pg = self.ps(st, "pg", [128, 512])
            Bpp = S.bufs("pp", 7)
            Bpg = S.buf("pg")
            for j in range(2):
                S.op("pool", lambda e, j=j: e.memset(va[j][:], 1.0), w=[Bva[j]])
                S.op("pool", lambda e, j=j: e.memset(vna[j][:], 1.0), w=[Bvna[j]])
            self._pp_i = 0

            def next_pp():
                j = self._pp_i % 7
                self._pp_i += 1
                return pp[j], Bpp[j]

            nblk = (NTOK + 511) // 512
            xi = 0
            evac = 0
            ffi = 0
            foi = 0
            for blk in range(nblk):
                t0 = blk * 512
                ntok = min(512, NTOK - t0)
                nsub = ntok // 128
                m = 0 if t0 < LAT else 1
                hb = blk % 2
                for s in range(nsub):
                    xj = xi % NXS
                    xi += 1
                    S.dma("sp", xt[xj][:], src[t0 + s * 128:t0 + (s + 1) * 128, :], w=[Bx[xj]])
                    for half in range(2):
                        p, Bp = next_pp()
                        for q in range(4):
                            dc = half * 4 + q
                            S.op("pe", lambda e, p=p, q=q, dc=dc, xj=xj: e.transpose(
                                p[:, q * 128:(q + 1) * 128], xt[xj][:, dc * 128:(dc + 1) * 128], self.ident_f[:]),
                                r=[Bx[xj], self.B_const], w=[Bp])
                        for q in range(4):
                            dc = half * 4 + q
                            if evac % 2 == 0:
                                S.op("act", lambda e, p=p, q=q, dc=dc, s=s, m=m: e.activation(
                                    out=hT[hb][:, dc, s * 128:(s + 1) * 128], in_=p[:, q * 128:(q + 1) * 128],
                                    func=AF.Identity, scale=self.sc1[:, dc, m:m + 1], bias=self.sh[:, dc, m:m + 1]),
                                    r=[Bp, self.B_mod], w=[Bh[hb]])
                            else:
                                S.op("dve", lambda e, p=p, q=q, dc=dc, s=s, m=m: e.tensor_scalar(
                                    out=hT[hb][:, dc, s * 128:(s + 1) * 128], in0=p[:, q * 128:(q + 1) * 128],
                                    scalar1=self.sc1[:, dc, m:m + 1], scalar2=self.sh[:, dc, m:m + 1],
                                    op0=ALU.mult, op1=ALU.add), r=[Bp, self.B_mod], w=[Bh[hb]])
                            evac += 1
                for c in range(16):
                    p, Bp = next_pp()
                    for k in range(8):
                        S.op("pe", lambda e, p=p, k=k, c=c: e.matmul(
                            p[:, :ntok], wb[:, k, c * 128:(c + 1) * 128], hT[hb][:, k, :ntok], start=(k == 0), stop=(k == 7)),
                            r=[Bw, Bh[hb]], w=[Bp])
                    if c < 8:
                        fj = ffi % 3
                        ffi += 1
                        if c % 2 == 0:
                            S.op("act", lambda e, p=p, c=c, fj=fj: e.activation(
                                out=ff[fj][:, :ntok], in_=p[:, :ntok], func=AF.Identity, bias=bcol[:, c:c + 1]),
                                r=[Bp, Bw], w=[Bff[fj]])
                        else:
                            S.op("dve", lambda e, p=p, c=c, fj=fj: e.tensor_scalar(
                                out=ff[fj][:, :ntok], in0=p[:, :ntok], scalar1=bcol[:, c:c + 1], scalar2=None, op0=ALU.add),
                                r=[Bp, Bw], w=[Bff[fj]])
                        S.dma("pool", self.qkpre[c, :, t0:t0 + ntok], ff[fj][:, :ntok], r=[Bff[fj]])
                    else:
                        fj = foi % 3
                        foi += 1
                        if c % 2 == 0:
                            S.op("act", lambda e, p=p, c=c, fj=fj: e.activation(
                                out=fo[fj][:, :ntok], in_=p[:, :ntok], func=AF.Identity, bias=bcol[:, c:c + 1]),
                                r=[Bp, Bw], w=[Bfo[fj]])
                        else:
                            S.op("dve", lambda e, p=p, c=c, fj=fj: e.tensor_scalar(
                                out=fo[fj][:, :ntok], in0=p[:, :ntok], scalar1=bcol[:, c:c + 1], scalar2=None, op0=ALU.add),
                                r=[Bp, Bw], w=[Bfo[fj]])
                        dd = self.qda if c < 12 else self.kda
                        S.dma("pool", dd[(c - 8) % 4, :, t0:t0 + ntok], fo[fj][:, :ntok], r=[Bfo[fj]])
                gb = blk % 2
                for s in range(nsub):
                    tt = (t0 // 128) + s
                    tok0 = t0 + s * 128

                    def tm(p, c0, n, b0, s=s):
                        for k in range(8):
                            S.op("pe", lambda e, k=k: e.matmul(
                                p[:, 0:n], hT[hb][:, k, s * 128:(s + 1) * 128], wb[:, k, c0:c0 + n], start=(k == 0), stop=False),
                                r=[Bw, Bh[hb]], w=[Bp])
                        S.op("pe", lambda e: e.matmul(p[:, 0:n], self.ones_b[0:1, :], brow[0:1, b0:b0 + n], start=False, stop=True),
                             r=[Bw, self.B_const], w=[Bp])
                    p, Bp = next_pp()
                    tm(p, OV, 512, 0)
                    vj = tt % 2
                    S.op("act", lambda e, p=p, vj=vj: e.activation(
                        out=va[vj][:, :, 0:128], in_=p[:, :].rearrange("p (h c) -> p h c", h=4), func=AF.Copy),
                        r=[Bp], w=[Bva[vj]])
                    S.dma("pool", self.vda[:, :, tt, :].rearrange("h p c -> p h c"), va[vj][:], r=[Bva[vj]])
                    p, Bp = next_pp()
                    tm(p, OO, 512, 512)
                    oj = tt % 2
                    S.op("act", lambda e, p=p, oj=oj: e.activation(out=ot[oj][:], in_=p[:, :], func=AF.Sigmoid), r=[Bp], w=[Bot[oj]])
                    S.dma("pool", self.og[tok0:tok0 + 128, :], ot[oj][:], r=[Bot[oj]])
                    Bp = Bpg
                    for k in range(8):
                        S.op("pe", lambda e, k=k, s=s: e.matmul(
                            pg[:, s * 16:(s + 1) * 16], hT[hb][:, k, s * 128:(s + 1) * 128], wb[:, k, OGT:OGT + 16],
                            start=(k == 0), stop=False), r=[Bw, Bh[hb]], w=[Bpg])
                    S.op("pe", lambda e, s=s: e.matmul(pg[:, s * 16:(s + 1) * 16], self.ones_b[0:1, :], brow[0:1, 1024:1040],
                                                       start=False, stop=True), r=[Bw, self.B_const], w=[Bpg])
                    p, Bp = next_pp()
                    tm(p, OVN, 512, 1040)
                    vj = tt % 2
                    S.op("act", lambda e, p=p, vj=vj: e.activation(
                        out=vna[vj][:, :, 0:64], in_=p[:, :].rearrange("p (h c) -> p h c", h=8), func=AF.Copy),
                        r=[Bp], w=[Bvna[vj]])
                    S.dma("pool", self.vm[:, :, tt, :].rearrange("h p c -> p h c"), vna[vj][:], r=[Bvna[vj]])
                    gj = tt % 2
                    for n in range(2):
                        p, Bp = next_pp()
                        tm(p, OG + n * 512, 512, 1552 + n * 512)
                        S.op("act", lambda e, p=p, n=n, gj=gj: e.activation(
                            out=gt[gj][:, n * 512:(n + 1) * 512], in_=p[:, :], func=AF.Silu), r=[Bp], w=[Bgt[gj]])
                    S.dma("pool", self.gate[tok0:tok0 + 128, :], gt[gj][:], r=[Bgt[gj]])
                pgv = pg[:, 0:nsub * 16].rearrange("p (s c) -> p s c", c=16)
                S.op("dve", lambda e, pgv=pgv: e.tensor_copy(gs[gb][:, 0:nsub, 0:8], pgv[:, :, 0:8]), r=[Bpg], w=[Bgs[gb]])
                for s in range(nsub):
                    S.op("dve", lambda e, s=s: e.tensor_tensor(out=gtmp[gb][:, s, :], in0=pg[:, s * 16 + 8:s * 16 + 16], in1=fb[:, :],
                                                               op=ALU.add), r=[Bpg, Bw], w=[Bgs[gb]])
                S.op("act", lambda e: e.activation(out=gtmp[gb][:, 0:nsub, :], in_=gtmp[gb][:, 0:nsub, :], func=AF.Exp, scale=-1.0),
                     r=[Bgs[gb]], w=[Bgs[gb]])
                S.op("act", lambda e: e.activation(out=gtmp[gb][:, 0:nsub, :], in_=gtmp[gb][:, 0:nsub, :], func=AF.Ln, bias=1.0),
                     r=[Bgs[gb]], w=[Bgs[gb]])
                S.op("dve", lambda e: e.tensor_scalar(out=gs[gb][:, 0:nsub, 8:16], in0=gtmp[gb][:, 0:nsub, :], scalar1=-1.0,
                                                      scalar2=None, op0=ALU.mult), r=[Bgs[gb]], w=[Bgs[gb]])
                tt0 = t0 // 128
                S.dma("pool", self.gates[:, tt0:tt0 + nsub, :], gs[gb][:, 0:nsub, :], r=[Bgs[gb]])
            S.barrier()

    def odd_conv(self, i):
        S = self.S
        SEG = 2048
        with contextlib.ExitStack() as st:
            cw = self.sb(st, "cv_w", [128, 8, 5])
            cb = self.sb(st, "cv_b", [128, 8])
            Bw = S.buf("cv_w")
            S.dma("sp", cw[:], self.od_convw[i, :, :, :], w=[Bw])
            S.dma("sp", cb[:], self.od_convb[i, :, :], w=[Bw])
            xin = [self.sb(st, f"cv_x{j}", [128, SEG + 4]) for j in range(2)]
            Bxin = S.bufs("cv_x", 2)
            acc = [self.sb(st, f"cv_a{j}", [128, SEG]) for j in range(2)]
            Bacc = S.bufs("cv_a", 2)
            tmp = [self.sb(st, f"cv_t{j}", [128, SEG]) for j in range(2)]
            Btmp = S.bufs("cv_t", 2)
            outb = [self.sb(st, f"cv_o{j}", [128, SEG], BF16) for j in range(2)]
            Bout = S.bufs("cv_o", 2)
            segs = [(a, SEG, 0, LAT) for a in range(0, LAT, SEG)] + [(LAT, CTX, LAT, LAT + CTX)]
            it = 0
            for c in range(8):
                for (t0, n, lo, hi) in segs:
                    j = it % 2
                    it += 1
                    a = max(t0 - 2, lo)
                    b = min(t0 + n + 2, hi)
                    if a > t0 - 2:
                        S.op("pool", lambda e, j=j: e.memset(xin[j][:, 0:2], 0.0), w=[Bxin[j]])
                    if b < t0 + n + 2:
                        S.op("pool", lambda e, j=j, n=n: e.memset(xin[j][:, n + 2:n + 4], 0.0), w=[Bxin[j]])
                    S.dma("sp", xin[j][:, a - (t0 - 2):b - (t0 - 2)], self.qkpre[c, :, a:b], w=[Bxin[j]])
                    S.op("pool", lambda e, j=j, n=n, c=c: e.tensor_scalar(
                        out=acc[j][:, 0:n], in0=xin[j][:, 0:n], scalar1=cw[:, c, 0:1], scalar2=None, op0=ALU.mult),
                        r=[Bxin[j], Bw], w=[Bacc[j]])
                    for k in range(1, 5):
                        S.op("dve", lambda e, j=j, n=n, c=c, k=k: e.scalar_tensor_tensor(
                            out=acc[j][:, 0:n], in0=xin[j][:, k:k + n], scalar=cw[:, c, k:k + 1], in1=acc[j][:, 0:n],
                            op0=ALU.mult, op1=ALU.add), r=[Bxin[j], Bw, Bacc[j]], w=[Bacc[j]])
                    if c < 4:
                        S.op("act", lambda e, j=j, n=n, c=c: e.activation(
                            out=outb[j][:, 0:n], in_=acc[j][:, 0:n], func=AF.Silu, bias=cb[:, c:c + 1]),
                            r=[Bacc[j], Bw], w=[Bout[j]])
                        S.dma("sp", self.mq[c, :, t0:t0 + n], outb[j][:, 0:n], r=[Bout[j]])
                    else:
                        S.op("act", lambda e, j=j, n=n, c=c: e.activation(
                            out=tmp[j][:, 0:n], in_=acc[j][:, 0:n], func=AF.Silu, bias=cb[:, c:c + 1]),
                            r=[Bacc[j], Bw], w=[Btmp[j]])
                        S.op("pool", lambda e, j=j, n=n: e.tensor_scalar(
                            out=outb[j][:, 0:n], in0=tmp[j][:, 0:n], scalar1=128 ** -0.5, scalar2=None, op0=ALU.mult),
                            r=[Btmp[j]], w=[Bout[j]])
                        S.dma("sp", self.mk[c - 4, :, t0:t0 + n], outb[j][:, 0:n], r=[Bout[j]])
            S.barrier()

    def mlstm(self, i):
        S = self.S
        with contextlib.ExitStack() as st:
            tri = self.sb(st, "ml_tri", [128, 3, 128])
            G = self.sb(st, "ml_G", [128, NT, 16])
            A = self.sb(st, "ml_A", [128, NT, 8])
            A2 = self.sb(st, "ml_A2", [128, NT, 8])
            Bq = self.sb(st, "ml_Bq", [128, NT, 8])
            EB = self.sb(st, "ml_EB", [128, NT, 8])
            Bg = S.buf("ml_g")
            S.dma("sp", tri[:], self.tri[:, :, :], w=[Bg])
            S.dma("sp", G[:], self.gates[:, :, :], w=[Bg])
            with contextlib.ExitStack() as st1:
                pg = [self.ps(st1, f"ml_pg{j}", [128, 512]) for j in range(3)]
                Bpg = S.bufs("ml_pg", 3)
                tmpg = self.sb(st1, "ml_tmpg", [128, 32, 8])
                Btg = S.buf("ml_tmpg")
                grp = 0
                for g0 in range(0, NT, 32):
                    g1 = min(g0 + 32, NT)
                    pj = grp % 3
                    grp += 1
                    for t in range(g0, g1):
                        o = (t - g0) * 16
                        S.op("pe", lambda e, t=t, o=o, pj=pj: e.matmul(pg[pj][:, o:o + 4], tri[:, 0, :], G[:, t, 8:12], start=True, stop=True),
                             r=[Bg], w=[Bpg[pj]])
                        S.op("pe", lambda e, t=t, o=o, pj=pj: e.matmul(pg[pj][:, o + 4:o + 8], tri[:, 1, :], G[:, t, 12:16], start=True, stop=True),
                             r=[Bg], w=[Bpg[pj]])
                        S.op("pe", lambda e, t=t, o=o, pj=pj: e.matmul(pg[pj][:, o + 8:o + 16], tri[:, 2, :], G[:, t, 8:16], start=True, stop=True),
                             r=[Bg], w=[Bpg[pj]])
                    n = g1 - g0
                    pv = pg[pj][:, 0:n * 16].rearrange("p (t c) -> p t c", c=16)
                    S.op("dve", lambda e, pv=pv, g0=g0, g1=g1, n=n: e.tensor_tensor(
                        out=tmpg[:, 0:n, :], in0=G[:, g0:g1, 0:8], in1=pv[:, :, 0:8], op=ALU.subtract), r=[Bg, Bpg[pj]], w=[Btg])
                    S.op("act", lambda e, g0=g0, g1=g1, n=n: e.activation(out=A[:, g0:g1, :], in_=tmpg[:, 0:n, :], func=AF.Exp),
                         r=[Btg], w=[Bg])
                    S.op("act", lambda e, pv=pv, g0=g0, g1=g1: e.activation(out=Bq[:, g0:g1, :], in_=pv[:, :, 0:8], func=AF.Exp),
                         r=[Bpg[pj]], w=[Bg])
                    S.op("act", lambda e, pv=pv, g0=g0, g1=g1: e.activation(out=EB[:, g0:g1, :], in_=pv[:, :, 8:16], func=AF.Exp),
                         r=[Bpg[pj]], w=[Bg])
                    S.op("dve", lambda e, g0=g0, g1=g1: e.tensor_tensor(out=A2[:, g0:g1, :], in0=A[:, g0:g1, :], in1=EB[:, g0:g1, :],
                                                                        op=ALU.mult), r=[Bg], w=[Bg])
                S.barrier()
            qT = self.sb(st, "ml_qT", [128, NTOK], BF16)
            kT = self.sb(st, "ml_kT", [128, NTOK], BF16)
            V = self.sb(st, "ml_V", [128, NT, 129], BF16)
            KTOK = self.sb(st, "ml_KTOK", [128, NT, 128], BF16)
            Bin = S.buf("ml_in")
            Bkt = S.buf("ml_ktok")
            Cn = [self.sb(st, f"ml_Cn{d}", [128, 129]) for d in range(2)]
            Cnb = [[self.sb(st, f"ml_Cnb{d}{j}", [128, 129], BF16) for j in range(2)] for d in range(2)]
            BCn = S.bufs("ml_Cn", 2)
            BCnb = [S.bufs(f"ml_Cnb{d}", 2) for d in range(2)]
            NW = 3
            W = [self.sb(st, f"ml_W{j}", [128, 128], BF16) for j in range(NW)]
            BW = S.bufs("ml_W", NW)
            v2 = [self.sb(st, f"ml_v2{j}", [128, 129], BF16) for j in range(NW)]
            Bv2 = S.bufs("ml_v2", NW)
            ho = [self.sb(st, f"ml_ho{j}", [128, 128]) for j in range(NW)]
            Bho = S.bufs("ml_ho", NW)
            sm = [self.sb(st, f"ml_sm{j}", [128, 4]) for j in range(NW)]
            Bsm = S.bufs("ml_sm", NW)
            ps_s = [self.ps(st, f"ml_pss{j}", [128, 512]) for j in range(2)]
            ps_kv = [self.ps(st, f"ml_pkv{j}", [128, 512]) for j in range(2)]
            ps_n = [self.ps(st, f"ml_pn{j}", [128, 512]) for j in range(2)]
            ppb = self.ps(st, "ml_ppb", [128, 1024], BF16)
            Bps_s, Bps_kv, Bps_n = S.bufs("ml_pss", 2), S.bufs("ml_pkv", 2), S.bufs("ml_pn", 2)
            Bppb = S.buf("ml_ppb")
            order = [[NT_LAT, NT_LAT + 1] + list(range(NT_LAT)), [NT_LAT + 1, NT_LAT] + list(range(NT_LAT - 1, -1, -1))]
            wi = 0
            for h in range(4):
                S.dma("sp", qT[:], self.mq[h, :, :], w=[Bin])
                S.dma("sp", kT[:], self.mk[h, :, :], w=[Bin])
                S.dma("sp", V[:], self.vda[h, :, :, :], w=[Bin])
                for t0 in range(0, NT, 8):
                    n = min(8, NT - t0)
                    for q in range(n):
                        t = t0 + q
                        S.op("pe", lambda e, t=t, q=q: e.transpose(ppb[:, q * 128:(q + 1) * 128], kT[:, t * 128:(t + 1) * 128], self.ident_b[:]),
                             r=[Bin, self.B_const], w=[Bppb])
                    S.op("act", lambda e, t0=t0, n=n: e.activation(
                        out=KTOK[:, t0:t0 + n, :], in_=ppb[:, 0:n * 128].rearrange("p (t c) -> p t c", c=128), func=AF.Copy),
                        r=[Bppb], w=[Bkt])
                for d in range(2):
                    S.op("pool", lambda e, d=d: e.memset(Cn[d][:], 0.0), w=[BCn[d]])
                    S.op("pool", lambda e, d=d: e.memset(Cnb[d][0][:], 0.0), w=[BCnb[d][0]])
                for step in range(NT):
                    for d in range(2):
                        t = order[d][step]
                        hd = d * 4 + h
                        cur, nxt = step % 2, (step + 1) % 2
                        j = wi % NW
                        pj = wi % 2
                        wi += 1
                        tsl = slice(t * 128, (t + 1) * 128)
                        S.op("pe", lambda e, pj=pj, tsl=tsl: e.matmul(ps_s[pj][:, 0:128], kT[:, tsl], qT[:, tsl], start=True, stop=True),
                             r=[Bin], w=[Bps_s[pj]])
                        S.op("dve", lambda e, pj=pj, j=j, t=t, hd=hd, d=d: e.scalar_tensor_tensor(
                            out=W[j][:], in0=ps_s[pj][:, 0:128], scalar=A[:, t, hd:hd + 1], in1=tri[:, d, :],
                            op0=ALU.mult, op1=ALU.mult), r=[Bps_s[pj], Bg], w=[BW[j]])
                        S.op("pool", lambda e, j=j, t=t, hd=hd: e.tensor_scalar(
                            out=v2[j][:], in0=V[:, t, :], scalar1=A2[:, t, hd:hd + 1], scalar2=None, op0=ALU.mult),
                            r=[Bin, Bg], w=[Bv2[j]])
                        S.op("pe", lambda e, pj=pj, j=j, t=t: e.matmul(ps_kv[pj][:, 0:129], KTOK[:, t, :], v2[j][:], start=True, stop=True),
                             r=[Bkt, Bv2[j]], w=[Bps_kv[pj]])
                        S.op("pe", lambda e, pj=pj, j=j, t=t: e.matmul(ps_n[pj][:, 0:129], W[j][:], V[:, t, :], start=True, stop=False),
                             r=[BW[j], Bin], w=[Bps_n[pj]])
                        S.op("pe", lambda e, pj=pj, tsl=tsl, d=d, cur=cur: e.matmul(ps_n[pj][:, 0:129], qT[:, tsl], Cnb[d][cur][:], start=False, stop=True),
                             r=[Bin, BCnb[d][cur]], w=[Bps_n[pj]])
                        S.op("dve", lambda e, pj=pj, d=d, t=t, hd=hd: e.scalar_tensor_tensor(
                            out=Cn[d][:], in0=Cn[d][:], scalar=EB[:, t, hd:hd + 1], in1=ps_kv[pj][:, 0:129],
                            op0=ALU.mult, op1=ALU.add), r=[BCn[d], Bps_kv[pj], Bg], w=[BCn[d]])
                        S.op("act", lambda e, d=d, nxt=nxt: e.activation(out=Cnb[d][nxt][:], in_=Cn[d][:], func=AF.Copy),
                             r=[BCn[d]], w=[BCnb[d][nxt]])
                        S.op("dve", lambda e, pj=pj, j=j, t=t, hd=hd: e.tensor_tensor(
                            out=sm[j][:, 0:1], in0=ps_n[pj][:, 128:129], in1=Bq[:, t, hd:hd + 1], op=ALU.mult),
                            r=[Bps_n[pj], Bg], w=[Bsm[j]])
                        S.op("dve", lambda e, j=j: e.scalar_tensor_tensor(out=sm[j][:, 1:2], in0=sm[j][:, 0:1], scalar=-1.0,
                                                                          in1=sm[j][:, 0:1], op0=ALU.mult, op1=ALU.max),
                             r=[Bsm[j]], w=[Bsm[j]])
                        S.op("dve", lambda e, j=j: e.tensor_scalar(out=sm[j][:, 1:2], in0=sm[j][:, 1:2], scalar1=1.0, scalar2=None,
                                                                   op0=ALU.max), r=[Bsm[j]], w=[Bsm[j]])
                        S.op("dve", lambda e, j=j: e.reciprocal(sm[j][:, 2:3], sm[j][:, 1:2]), r=[Bsm[j]], w=[Bsm[j]])
                        S.op("dve", lambda e, j=j, t=t, hd=hd: e.tensor_tensor(
                            out=sm[j][:, 3:4], in0=sm[j][:, 2:3], in1=Bq[:, t, hd:hd + 1], op=ALU.mult), r=[Bsm[j], Bg], w=[Bsm[j]])
                        S.op("dve", lambda e, pj=pj, j=j: e.tensor_scalar(
                            out=ho[j][:], in0=ps_n[pj][:, 0:128], scalar1=sm[j][:, 3:4], scalar2=None, op0=ALU.mult),
                            r=[Bps_n[pj], Bsm[j]], w=[Bho[j]])
                        S.dma("sp", self.hfb[d, t * 128:(t + 1) * 128, h * 128:(h + 1) * 128], ho[j][:], r=[Bho[j]])
                S.barrier()
            S.barrier()

    def mlstm_post(self, i):
        S = self.S
        with contextlib.ExitStack() as st:
            ng = self.sb(st, "mp_ng", [128, 512])
            Bw = S.buf("mp_w")
            S.dma("sp", ng[:], self.od_ng[i, :, :].to_broadcast([128, 512]), w=[Bw])
            eps_t = self.sb(st, "mp_eps", [128, 1])
            S.op("pool", lambda e: e.memset(eps_t[:], LN_EPS), w=[Bw])
            NB = 2
            hf = [self.sb(st, f"mp_hf{j}", [128, 512]) for j in range(NB)]
            hb = [self.sb(st, f"mp_hb{j}", [128, 512]) for j in range(NB)]
            og = [self.sb(st, f"mp_og{j}", [128, 512]) for j in range(NB)]
            hs = [self.sb(st, f"mp_hs{j}", [128, 512]) for j in range(NB)]
            yo = [self.sb(st, f"mp_yo{j}", [128, 512]) for j in range(NB)]
            stt = [self.sb(st, f"mp_st{j}", [128, 4, 12]) for j in range(NB)]
            Bhf, Bhb, Bog, Bhs, Byo, Bst = (S.bufs(n, NB) for n in ("mp_hf", "mp_hb", "mp_og", "mp_hs", "mp_yo", "mp_st"))
            for t in range(NT):
                j = t % NB
                tok0 = t * 128
                S.dma("sp", hf[j][:], self.hfb[0, tok0:tok0 + 128, :], w=[Bhf[j]])
                S.dma("sp", hb[j][:], self.hfb[1, tok0:tok0 + 128, :], w=[Bhb[j]])
                S.dma("sp", og[j][:], self.og[tok0:tok0 + 128, :], w=[Bog[j]])
                S.op("pool", lambda e, j=j: e.tensor_tensor(out=hs[j][:], in0=hf[j][:], in1=hb[j][:], op=ALU.add),
                     r=[Bhf[j], Bhb[j]], w=[Bhs[j]])
                S.op("pool", lambda e, j=j: e.tensor_tensor(out=og[j][:], in0=og[j][:], in1=ng[:], op=ALU.mult),
                     r=[Bog[j], Bw], w=[Bog[j]])
                for h in range(4):
                    S.op("dve", lambda e, j=j, h=h: e.bn_stats(stt[j][:, h, 0:6], hs[j][:, h * 128:(h + 1) * 128]),
                         r=[Bhs[j]], w=[Bst[j]])
                    S.op("dve", lambda e, j=j, h=h: e.bn_aggr(stt[j][:, h, 6:8], stt[j][:, h, 0:6]), r=[Bst[j]], w=[Bst[j]])
                S.op("act", lambda e, j=j: e.activation(out=stt[j][:, :, 8:9], in_=stt[j][:, :, 7:8], func=AF.Sqrt, scale=1.0,
                                                        bias=eps_t[:, 0:1]), r=[Bst[j], Bw], w=[Bst[j]])
                S.op("dve", lambda e, j=j: e.reciprocal(stt[j][:, :, 9:10], stt[j][:, :, 8:9]), r=[Bst[j]], w=[Bst[j]])
                for h in range(4):
                    S.op("dve", lambda e, j=j, h=h: e.tensor_scalar(
                        out=yo[j][:, h * 128:(h + 1) * 128], in0=hs[j][:, h * 128:(h + 1) * 128], scalar1=stt[j][:, h, 6:7],
                        scalar2=stt[j][:, h, 9:10], op0=ALU.subtract, op1=ALU.mult), r=[Bhs[j], Bst[j]], w=[Byo[j]])
                S.op("pool", lambda e, j=j: e.tensor_tensor(out=yo[j][:], in0=yo[j][:], in1=og[j][:], op=ALU.mult),
                     r=[Byo[j], Bog[j]], w=[Byo[j]])
                S.dma("sp", self.ybuf[tok0:tok0 + 128, 0:512], yo[j][:], r=[Byo[j]])
            S.barrier()

    def na_attention(self, i, last):
        S = self.S
        with contextlib.ExitStack() as st:
            QN = self.sb(st, "na_Q", [128, NTOK], BF16)
            KN = self.sb(st, "na_K", [128, NTOK], BF16)
            VN = self.sb(st, "na_V", [128, NT, 65], BF16)
            MB = self.sb(st, "na_MB", [128, NA_NVAR, 128], BF16)
            Bqk = S.buf("na_qk")
            Bv = S.buf("na_v")
            Bmb = S.buf("na_mb")
            stg = [self.sb(st, f"na_stg{j}", [128, 7, 128]) for j in range(2)]
            Bstg = S.bufs("na_stg", 2)
            NP = 3
            ptA = [self.sb(st, f"na_ptA{j}", [128, 512], BF16) for j in range(NP)]
            ptB = [self.sb(st, f"na_ptB{j}", [128, 384], BF16) for j in range(NP)]
            BptA, BptB = S.bufs("na_ptA", NP), S.bufs("na_ptB", NP)
            rr = [self.sb(st, f"na_rr{j}", [128, 2]) for j in range(NP)]
            to = [self.sb(st, f"na_to{j}", [128, 64]) for j in range(NP)]
            Bto = S.bufs("na_to", NP)
            psA = [self.ps(st, f"na_psA{j}", [128, 512]) for j in range(2)]
            psB = [self.ps(st, f"na_psB{j}", [128, 512]) for j in range(2)]
            acc = [self.ps(st, f"na_acc{j}", [128, 512]) for j in range(2)]
            BpsA, BpsB, Bacc = S.bufs("na_psA", 2), S.bufs("na_psB", 2), S.bufs("na_acc", 2)
            it = 0
            qtiles = list(range(NT_LAT)) + ([] if (last and self.final_out) else [NT_LAT, NT_LAT + 1])
            if self.tile_filter is not None:
                qtiles = [t for t in qtiles if t in self.tile_filter]
            for h in range(8):
                lo = (h % 2) * 64
                if h % 2 == 0:
                    S.dma("sp", QN[:], self.qda[h // 2, :, :], w=[Bqk])
                    S.dma("sp", KN[:], self.kda[h // 2, :, :], w=[Bqk])
                S.dma("sp", VN[:], self.vm[h, :, :, :], w=[Bv])
                for v0 in range(0, NA_NVAR, 7):
                    n = min(7, NA_NVAR - v0)
                    sj = (v0 // 7) % 2
                    S.dma("sp", stg[sj][:, 0:n, :], self.od_nat[i, h, :, v0:v0 + n, :], w=[Bstg[sj]])
                    S.op("pool", lambda e, sj=sj, n=n, v0=v0: e.tensor_scalar(
                        out=MB[:, v0:v0 + n, :], in0=stg[sj][:, 0:n, :], scalar1=1.0 / NA_SCALE, scalar2=None, op0=ALU.mult),
                        r=[Bstg[sj]], w=[Bmb])
                for j in qtiles:
                    pj = it % 2
                    tj = it % NP
                    it += 1
                    qsl = slice(j * 128, (j + 1) * 128)
                    if j < NT_LAT:
                        slots = [(kt, var) for (kt, var) in NA_PLAN[j]] + [(NT_LAT, None), (NT_LAT + 1, None)]
                    else:
                        slots = [(NT_LAT, None), (NT_LAT + 1, None)]
                    nA = min(4, len(slots))
                    nB = len(slots) - nA
                    for si, (kt, var) in enumerate(slots):
                        if si < 4:
                            dst, Bd = psA[pj][:, si * 128:(si + 1) * 128], BpsA[pj]
                        else:
                            dst, Bd = psB[pj][:, (si - 4) * 128:(si - 3) * 128], BpsB[pj]
                        S.op("pe", lambda e, dst=dst, kt=kt, var=var: e.matmul(
                            dst, KN[lo:lo + 64, kt * 128:(kt + 1) * 128], QN[lo:lo + 64, qsl], start=True, stop=(var is None)),
                            r=[Bqk], w=[Bd])
                        if var is not None:
                            S.op("pe", lambda e, dst=dst, var=var: e.matmul(dst, self.ident_b[:], MB[:, var, :], start=False, stop=True),
                                 r=[Bmb, self.B_const], w=[Bd])
                    S.op("act", lambda e, pj=pj, tj=tj, nA=nA: e.activation(out=ptA[tj][:, 0:nA * 128], in_=psA[pj][:, 0:nA * 128],
                                                                            func=AF.Exp, scale=NA_SCALE), r=[BpsA[pj]], w=[BptA[tj]])
                    if nB > 0:
                        S.op("act", lambda e, pj=pj, tj=tj, nB=nB: e.activation(out=ptB[tj][:, 0:nB * 128], in_=psB[pj][:, 0:nB * 128],
                                                                                func=AF.Exp, scale=NA_SCALE), r=[BpsB[pj]], w=[BptB[tj]])
                    for si, (kt, var) in enumerate(slots):
                        if si < 4:
                            lhs, Bl = ptA[tj][:, si * 128:(si + 1) * 128], BptA[tj]
                        else:
                            lhs, Bl = ptB[tj][:, (si - 4) * 128:(si - 3) * 128], BptB[tj]
                        S.op("pe", lambda e, lhs=lhs, kt=kt, si=si, pj=pj: e.matmul(
                            acc[pj][:, 0:65], lhs, VN[:, kt, :], start=(si == 0), stop=(si == len(slots) - 1)),
                            r=[Bl, Bv], w=[Bacc[pj]])
                    S.op("dve", lambda e, pj=pj, tj=tj: e.reciprocal(rr[tj][:, 0:1], acc[pj][:, 64:65]), r=[Bacc[pj]], w=[Bto[tj]])
                    S.op("dve", lambda e, pj=pj, tj=tj: e.tensor_scalar(out=to[tj][:], in0=acc[pj][:, 0:64], scalar1=rr[tj][:, 0:1],
                                                                        scalar2=None, op0=ALU.mult), r=[Bacc[pj], Bto[tj]], w=[Bto[tj]])
                    S.dma("pool", self.ybuf[j * 128:(j + 1) * 128, 512 + h * 64:512 + (h + 1) * 64], to[tj][:], r=[Bto[tj]])
            S.barrier()


def _swap64(cols):
    return np.concatenate([cols[32:64], cols[0:32]])


def _rope_tables():
    t = np.arange(LAT)
    row = (t // GRID_W).astype(np.float32)
    col = (t % GRID_W).astype(np.float32)

    def tab(dim, nrows_pattern):
        n_freq = dim // 4
        freqs = (10000.0 ** (-np.arange(n_freq, dtype=np.float32) / n_freq)).astype(np.float32)
        ang = np.concatenate([row[:, None] * freqs, col[:, None] * freqs], axis=-1).astype(np.float32)
        cos = np.cos(ang).astype(np.float32).T
        sin = np.sin(ang).astype(np.float32).T
        half = dim // 2
        c = np.concatenate([cos, cos], 0)
        s = np.concatenate([-sin, sin], 0)
        c = np.concatenate([c, np.ones((dim, CTX), np.float32)], 1)
        s = np.concatenate([s, np.zeros((dim, CTX), np.float32)], 1)
        return c, s

    c64, s64 = tab(64, None)
    c32, s32 = tab(32, None)
    rope_da = np.stack([np.concatenate([c64, c64], 0), np.concatenate([s64, s64], 0)]).astype(np.float32)
    ml_c = np.ones((128, NTOK), np.float32)
    ml_s = np.zeros((128, NTOK), np.float32)
    ml_c[0:32] = c32
    ml_s[0:32] = s32
    ml_c[64:96] = c32
    ml_s[64:96] = s32
    rope_ml = np.stack([ml_c, ml_s]).astype(np.float32)
    return rope_da, rope_ml


def _even_layout(inp):
    ev_w_in, ev_b_in = inp["ev_w_in"], inp["ev_b_in"]
    o_q1, o_q2, o_k1, o_k2, o_v, o_cq, o_ckv, o_kr, o_g = 0, 256, 512, 768, 1024, 1536, 1792, 1920, 1952
    cols = []
    for (a, b) in ((o_q1, o_q2), (o_k1, o_k2)):
        main = []
        sw = []
        for h in range(4):
            c1 = np.arange(a + h * 64, a + (h + 1) * 64)
            c2 = np.arange(b + h * 64, b + (h + 1) * 64)
            main += [c1, c2]
            sw += [_swap64(c1), _swap64(c2)]
        cols += main + sw
    kr = np.arange(o_kr, o_kr + 32)
    cols += [kr, np.concatenate([kr[16:], kr[:16]])]
    cols += [np.arange(o_v, o_v + 512), np.arange(o_cq, o_cq + 384), np.arange(o_g, o_g + 1024)]
    perm = np.concatenate(cols)
    assert perm.shape[0] == EV_COLS
    ev_w = np.ascontiguousarray(ev_w_in[:, :, perm])
    bp = ev_b_in[:, perm]
    bcol = np.zeros((2, 128, 18), np.float32)
    for g in range(16):
        bcol[:, :, g] = bp[:, g * 128:(g + 1) * 128]
    bcol[:, 0:32, 16] = bp[:, EKR:EKR + 32]
    bcol[:, 0:32, 17] = bp[:, EKR + 32:EKR + 64]
    brow = np.ascontiguousarray(bp[:, EV_:EV_ + 1920])[:, None, :]
    wuq = inp["mla_w_uq"]
    swc = []
    for h in range(8):
        base = h * 96
        c = np.arange(base, base + 96)
        r = c[64:96]
        swc.append(np.concatenate([c[0:64], r[16:], r[:16]]))
    swc = np.concatenate(swc)
    ev_wuq = np.ascontiguousarray(np.concatenate([wuq, wuq[:, :, swc]], axis=2))
    wukv = inp["mla_w_ukv"]
    nope = np.concatenate([np.arange(h * 128, h * 128 + 64) for h in range(8)])
    vv = np.concatenate([np.arange(h * 128 + 64, h * 128 + 128) for h in range(8)])
    ev_wukv = np.ascontiguousarray(wukv[:, :, np.concatenate([nope, vv])])
    return dict(
        ev_w=ev_w, ev_bcol=bcol, ev_brow=np.ascontiguousarray(brow),
        ev_lam=np.ascontiguousarray(inp["da_lambda"].reshape(2, 1, 256)),
        ev_subg=np.ascontiguousarray(inp["da_subln_g"].reshape(2, 1, 128)),
        ev_qg=np.ascontiguousarray(inp["mla_q_norm_g"].reshape(2, 2, 128).transpose(0, 2, 1)),
        ev_kvg=np.ascontiguousarray(inp["mla_kv_norm_g"].reshape(2, 128, 1)),
        ev_wuq=ev_wuq, ev_wukv=ev_wukv, ev_wo=np.ascontiguousarray(inp["ev_w_out"]),
    )


def _odd_layout(inp):
    w, b = inp["od_w_in"], inp["od_b_in"]
    g0 = 2048
    gates = np.concatenate([g0 + j * 4 + np.arange(4) for j in (0, 2, 1, 3)])
    perm = np.concatenate([np.arange(0, 1024), np.arange(2064, 2576), np.arange(2576, 3088), np.arange(1024, 1536),
                           np.arange(1536, 2048), gates, np.arange(3088, 3600), np.arange(3600, 4624)])
    assert perm.shape[0] == OD_COLS
    od_w = np.ascontiguousarray(w[:, :, perm])
    bp = b[:, perm]
    bcol = np.ascontiguousarray(bp[:, 0:2048].reshape(2, 16, 128).transpose(0, 2, 1))
    brow = np.ascontiguousarray(bp[:, 2048:])[:, None, :]
    convw = np.ascontiguousarray(inp["ml_conv_w"].reshape(2, 5, 8, 128).transpose(0, 3, 2, 1))
    convb = np.ascontiguousarray(inp["ml_conv_b"].reshape(2, 8, 128).transpose(0, 2, 1))
    fb = np.ascontiguousarray(inp["ml_f_bias"].reshape(2, 1, 8))
    ng = np.ascontiguousarray(inp["ml_norm_g"].reshape(2, 1, 512))
    rpb = inp["na_rpb"]
    nat = np.full((2, 8, 128, NA_NVAR, 128), NEG, np.float32)
    k = np.arange(128)
    krl, kc = k // 64, k % 64
    q = np.arange(128)
    qrl, qc = q // 64, q % 64
    cs = np.clip(qc - NA_COLS // 2, 0, GRID_W - NA_COLS)
    for (dk, o0, o1), vid in NA_VARIANTS.items():
        rel = (2 * dk + krl[:, None]) - qrl[None, :]
        off = np.where(qrl[None, :] == 0, o0, o1)
        valid_r = (rel >= off) & (rel <= off + NA_ROWS - 1)
        valid_c = (kc[:, None] >= cs[None, :]) & (kc[:, None] <= cs[None, :] + NA_COLS - 1)
        valid = valid_r & valid_c
        ridx = np.clip(rel + NA_ROWS - 1, 0, 2 * NA_ROWS - 2)
        cidx = np.clip(kc[:, None] - qc[None, :] + NA_COLS - 1, 0, 2 * NA_COLS - 2)
        tab = rpb[:, :, ridx, cidx]
        nat[:, :, :, vid, :] = np.where(valid[None, None], tab, np.float32(NEG))
    tri = np.stack([np.triu(np.ones((128, 128), np.float32)), np.tril(np.ones((128, 128), np.float32)),
                    np.ones((128, 128), np.float32)], 1)
    return dict(od_w=od_w, od_bcol=bcol, od_brow=np.ascontiguousarray(brow), od_convw=convw, od_convb=convb, od_fb=fb,
                od_ng=ng, od_nat=nat, od_wo=np.ascontiguousarray(inp["od_w_out"]), tri=np.ascontiguousarray(tri))


def make_in_maps(inp, batches):
    inp = {k: np.asarray(v, dtype=np.float32) for k, v in inp.items()}
    rope_da, rope_ml = _rope_tables()
    common = dict(
        ada_w=inp["ada_w"], ada_b=inp["ada_b"], ln_g=inp["ln_g"], ln_b=inp["ln_b"],
        ident=np.eye(128, dtype=np.float32),
        sel=np.concatenate([np.stack([np.ones(128), np.zeros(128)]), np.stack([np.zeros(128), np.ones(128)])], 1).astype(np.float32),
        rope_da=rope_da, rope_ml=rope_ml,
    )
    common.update(_even_layout(inp))
    common.update(_odd_layout(inp))
    maps = []
    for b in batches:
        m = dict(common)
        m["x_in"] = np.ascontiguousarray(np.concatenate([inp["x"][b], inp["ctx"][b]], 0))
        cc = np.stack([inp["c"][b], inp["c_ctx"]], -1)
        m["cc"] = np.ascontiguousarray(cc.reshape(8, 128, 2).transpose(1, 0, 2))
        maps.append(m)
    return maps


_PROG_CACHE = {}


def kernel(**inputs):
    batches = [c // 2 for c in range(8)]
    in_maps = make_in_maps(inputs, batches)
    prog = Prog()
    nc = prog.build()
    res = run_bass_kernel_spmd(nc, in_maps, core_ids=list(range(8)))
    out = np.stack([res.results[2 * b]["y"] for b in range(4)], 0)
    return out.astype(np.float32)
```

```python
import contextlib
import math
import numpy as np
import concourse.bass as bass
import concourse.mybir as mybir
from concourse.bass_utils import run_bass_kernel_spmd

F32 = mybir.dt.float32
BF16 = mybir.dt.bfloat16
AF = mybir.ActivationFunctionType
ALU = mybir.AluOpType
AX = mybir.AxisListType

D = 1024
LAT = 8192
CTX = 256
NTOK = LAT + CTX
NT = NTOK // 128
NT_LAT = LAT // 128
DEPTH = 4
GRID_W = 64
ALPHA = (2.0 * DEPTH) ** 0.25
LN_EPS = 1e-5
RMS_EPS = 1e-6
DA_SCALE = 64 ** -0.5
MLA_SCALE = 96 ** -0.5
NA_SCALE = 64 ** -0.5
EV_COLS = 4032
EQ, EK, EKR, EV_, EC, EG = 0, 1024, 2048, 2112, 2624, 3008
OD_COLS = 4624
OQK, OQN, OKN, OV, OO, OGT, OVN, OG = 0, 1024, 1536, 2048, 2560, 3072, 3088, 3600
NA_ROWS, NA_COLS, GRID_H = 8, 16, 128
NEG = -30000.0


def na_plan():
    variants = {}
    plan = []
    for j in range(NT_LAT):
        r0, r1 = 2 * j, 2 * j + 1
        rs0 = min(max(r0 - NA_ROWS // 2, 0), GRID_H - NA_ROWS)
        rs1 = min(max(r1 - NA_ROWS // 2, 0), GRID_H - NA_ROWS)
        lst = []
        for kt in range(rs0 // 2, (rs1 + NA_ROWS - 1) // 2 + 1):
            key = (kt - j, rs0 - r0, rs1 - r1)
            if key not in variants:
                variants[key] = len(variants)
            lst.append((kt, variants[key]))
        plan.append(lst)
    return plan, variants


NA_PLAN, NA_VARIANTS = na_plan()
NA_NVAR = len(NA_VARIANTS)


class Tok:
    __slots__ = ("sem", "key", "val", "eng")

    def __init__(self, sem, key, val, eng):
        self.sem, self.key, self.val, self.eng = sem, key, val, eng


class Buf:
    __slots__ = ("name", "last_w", "readers", "sem", "key", "cnt")

    def __init__(self, name):
        self.name = name
        self.last_w = None
        self.readers = {}
        self.sem = None
        self.key = None
        self.cnt = 0


class Eng:
    def __init__(self, name, h, sem):
        self.name, self.h, self.sem = name, h, sem
        self.key = "E_" + name
        self.cnt = 0
        self.waited = {}


class Sync:
    def __init__(self, nc, stack):
        self.nc = nc
        self.stack = stack
        self.E = {}
        for name, h in (("pe", nc.tensor), ("act", nc.scalar), ("dve", nc.vector),
                        ("pool", nc.gpsimd), ("sp", nc.sync)):
            sem = stack.enter_context(nc.semaphore("s_" + name))
            self.E[name] = Eng(name, h, sem)
        self.dma_bufs = []
        self.free_sems = []
        self.nsem = 0
        self.nwait = 0
        self.nins = 0

    def buf(self, name):
        return Buf(name)

    def bufs(self, name, n):
        return [Buf(f"{name}{i}") for i in range(n)]

    def _deps(self, eng, r, w):
        raw = []
        oth = []
        for b in r:
            if b.last_w is not None:
                raw.append(b.last_w)
        for b in w:
            if b.last_w is not None:
                oth.append(b.last_w)
            oth.extend(b.readers.values())
        toks = []
        for t in raw:
            if t.eng is eng and eng.name == "pe":
                continue
            toks.append(t)
        for t in oth:
            if t.eng is eng:
                continue
            toks.append(t)
        return toks

    def _wait(self, eng, toks):
        for t in toks:
            if eng.waited.get(t.key, 0) >= t.val:
                continue
            eng.h.wait_ge(t.sem, t.val)
            eng.waited[t.key] = t.val
            self.nwait += 1

    def op(self, en, fn, r=(), w=()):
        eng = self.E[en]
        self._wait(eng, self._deps(eng, r, w))
        ins = fn(eng.h)
        ins.then_inc(eng.sem, 1)
        eng.cnt += 1
        self.nins += 1
        tok = Tok(eng.sem, eng.key, eng.cnt, eng)
        for b in r:
            b.readers[tok.key] = tok
        for b in w:
            b.last_w = tok
            b.readers = {}
        return tok

    def dma(self, q, out, in_, r=(), w=(), sb=None):
        eng = self.E[q]
        self._wait(eng, self._deps(eng, r, w))
        if sb is None:
            sb = w[0] if w else r[0]
        if sb.sem is None:
            if self.free_sems:
                sb.sem, sb.key, sb.cnt = self.free_sems.pop()
            else:
                sb.sem = self.stack.enter_context(self.nc.semaphore(f"d{self.nsem}"))
                sb.key = f"D{self.nsem}"
                sb.cnt = 0
                self.nsem += 1
            self.dma_bufs.append(sb)
        ins = eng.h.dma_start(out=out, in_=in_)
        ins.then_inc(sb.sem, 16)
        sb.cnt += 16
        self.nins += 1
        tok = Tok(sb.sem, sb.key, sb.cnt, None)
        for b in r:
            b.readers[tok.key] = tok
        for b in w:
            b.last_w = tok
            b.readers = {}
        return tok

    def barrier(self, engines=("pe", "act", "dve", "pool", "sp")):
        toks = [Tok(e.sem, e.key, e.cnt, e) for e in self.E.values() if e.cnt > 0]
        toks += [Tok(b.sem, b.key, b.cnt, None) for b in self.dma_bufs if b.cnt > 0]
        for en in engines:
            eng = self.E[en]
            self._wait(eng, [t for t in toks if t.eng is not eng])
        for b in self.dma_bufs:
            self.free_sems.append((b.sem, b.key, b.cnt))
            b.sem = None
            b.last_w = None
            b.readers = {}
        self.dma_bufs = []


class Prog:
    def __init__(self, layers=(0, 1, 2, 3), final_out=True, qb_filter=None, tile_filter=None):
        self.layers = tuple(layers)
        self.qb_filter = qb_filter
        self.tile_filter = tile_filter
        self.nc = bass.Bass("TRN2", target_bir_lowering=False)
        self.final_out = final_out

    def din(self, name, shape, dt=F32):
        return self.nc.dram_tensor(name, list(shape), dt, kind="ExternalInput").ap()

    def dout(self, name, shape, dt=F32):
        return self.nc.dram_tensor(name, list(shape), dt, kind="ExternalOutput").ap()

    def dscr(self, name, shape, dt=F32):
        return self.nc.dram_tensor(name, list(shape), dt, kind="Internal").ap()

    def sb(self, st, name, shape, dt=F32):
        self._uid = getattr(self, "_uid", 0) + 1
        return st.enter_context(self.nc.sbuf_tensor(f"s{self._uid}_{name}", list(shape), dt))

    def ps(self, st, name, shape, dt=F32):
        self._uid = getattr(self, "_uid", 0) + 1
        return st.enter_context(self.nc.psum_tensor(f"p{self._uid}_{name}", list(shape), dt))

    def build(self):
        nc = self.nc
        with contextlib.ExitStack() as st:
            self.S = Sync(nc, st)
            self.declare()
            self.setup_consts(st)
            nl = len(self.layers)
            for li, l in enumerate(self.layers):
                src = self.x_in if li == 0 else self.xbuf[(li - 1) % 2]
                last = li == nl - 1
                dst = self.y_out if last else self.xbuf[li % 2]
                if l % 2 == 0:
                    self.even_layer(l, src, dst, last)
                else:
                    self.odd_layer(l, src, dst, last)
            self.S.barrier()
        return nc

    def declare(self):
        self.x_in = self.din("x_in", [NTOK, D])
        self.cc = self.din("cc", [128, 8, 2])
        self.ada_w = self.din("ada_w", [DEPTH, D, 3 * D])
        self.ada_b = self.din("ada_b", [DEPTH, 3 * D])
        self.ln_g = self.din("ln_g", [DEPTH, D])
        self.ln_b = self.din("ln_b", [DEPTH, D])
        self.ident_in = self.din("ident", [128, 128])
        self.sel_in = self.din("sel", [2, 256])
        self.ev_w = self.din("ev_w", [2, D, EV_COLS])
        self.ev_bcol = self.din("ev_bcol", [2, 128, 18])
        self.ev_brow = self.din("ev_brow", [2, 1, 1920])
        self.ev_lam = self.din("ev_lam", [2, 1, 256])
        self.ev_subg = self.din("ev_subg", [2, 1, 128])
        self.ev_qg = self.din("ev_qg", [2, 128, 2])
        self.ev_kvg = self.din("ev_kvg", [2, 128, 1])
        self.ev_wuq = self.din("ev_wuq", [2, 256, 1536])
        self.ev_wukv = self.din("ev_wukv", [2, 128, 1024])
        self.ev_wo = self.din("ev_wo", [2, D, D])
        self.rope_da = self.din("rope_da", [2, 128, NTOK])
        self.rope_ml = self.din("rope_ml", [2, 128, NTOK])
        self.od_w = self.din("od_w", [2, D, OD_COLS])
        self.od_bcol = self.din("od_bcol", [2, 128, 16])
        self.od_brow = self.din("od_brow", [2, 1, 2576])
        self.od_convw = self.din("od_convw", [2, 128, 8, 5])
        self.od_convb = self.din("od_convb", [2, 128, 8])
        self.od_fb = self.din("od_fb", [2, 1, 8])
        self.od_ng = self.din("od_ng", [2, 1, 512])
        self.od_nat = self.din("od_nat", [2, 8, 128, NA_NVAR, 128])
        self.od_wo = self.din("od_wo", [2, D, D])
        self.tri = self.din("tri", [128, 3, 128])
        self.qkpre = self.dscr("qkpre", [8, 128, NTOK])
        self.mq = self.dscr("mq", [4, 128, NTOK], BF16)
        self.mk = self.dscr("mk", [4, 128, NTOK], BF16)
        self.og = self.dscr("og", [NTOK, 512])
        self.gates = self.dscr("gates", [128, NT, 16])
        self.hfb = self.dscr("hfb", [2, NTOK, 512])
        n_last_tok = LAT if self.final_out else NTOK
        self.y_out = self.dout("y", [n_last_tok, D])
        self.xbuf = [self.dscr("xbuf0", [NTOK, D]), self.dscr("xbuf1", [NTOK, D])]
        self.qda = self.dscr("qda", [4, 128, NTOK], BF16)
        self.kda = self.dscr("kda", [4, 128, NTOK], BF16)
        self.vda = self.dscr("vda", [4, 128, NT, 129], BF16)
        self.qm = self.dscr("qm", [8, 96, NTOK], BF16)
        self.kmn = self.dscr("kmn", [8, 64, NTOK], BF16)
        self.krt = self.dscr("krt", [32, NTOK], BF16)
        self.vm = self.dscr("vm", [8, 128, NT, 65], BF16)
        self.gate = self.dscr("gate", [NTOK, D], BF16)
        self.ybuf = self.dscr("ybuf", [NTOK, D], BF16)

    def setup_consts(self, st):
        S = self.S
        self.ident_f = self.sb(st, "ident_f", [128, 128])
        self.ident_b = self.sb(st, "ident_b", [128, 128], BF16)
        self.sel = self.sb(st, "sel", [2, 256])
        self.ccs = self.sb(st, "ccs", [128, 8, 2])
        self.zeros_b = self.sb(st, "zeros_b", [128, 128], BF16)
        self.ones_b = self.sb(st, "ones_b", [1, 128], BF16)
        self.zeros_w = self.sb(st, "zeros_w", [128, 512], BF16)
        self.B_const = S.buf("consts")
        b = self.B_const
        S.dma("sp", self.ident_f[:], self.ident_in[:, :], w=[b])
        S.dma("sp", self.sel[:], self.sel_in[:, :], w=[b])
        S.dma("sp", self.ccs[:], self.cc[:, :, :], w=[b])
        S.op("dve", lambda e: e.tensor_copy(self.ident_b[:], self.ident_f[:]), r=[b], w=[b])
        S.op("dve", lambda e: e.memset(self.zeros_b[:], 0.0), w=[b])
        S.op("dve", lambda e: e.memset(self.ones_b[:], 1.0), w=[b])
        S.op("dve", lambda e: e.memset(self.zeros_w[:], 0.0), w=[b])
        S.op("act", lambda e: e.activation(out=self.ccs[:], in_=self.ccs[:], func=AF.Silu), r=[b], w=[b])
        self.sc1 = self.sb(st, "sc1", [128, 8, 2])
        self.sh = self.sb(st, "sh", [128, 8, 2])
        self.g_l = self.sb(st, "g_l", [128, D])
        self.g_c = self.sb(st, "g_c", [128, D])
        self.lng = self.sb(st, "lng", [128, D])
        self.lnb = self.sb(st, "lnb", [128, D])
        self.B_mod = S.buf("mod")

    def adaln(self, l):
        S, nc = self.S, self.nc
        S.barrier()
        with contextlib.ExitStack() as st:
            wt = [self.sb(st, f"adaw{i}", [128, 3 * D]) for i in range(2)]
            Bw = S.bufs("adaw", 2)
            mrow = self.sb(st, "mrow", [2, 3 * D])
            brow = self.sb(st, "adab", [2, 3 * D])
            Bm = S.buf("mrow")
            Bb = S.buf("adab")
            pm = [self.ps(st, f"pm{i}", [128, 512]) for i in range(8)]
            Bp = S.bufs("pm", 8)
            S.dma("sp", brow[:], self.ada_b[l:l + 1, :].to_broadcast([2, 3 * D]), w=[Bb])
            S.dma("sp", self.lng[:], self.ln_g[l:l + 1, :].to_broadcast([128, D]), w=[self.B_mod])
            S.dma("sp", self.lnb[:], self.ln_b[l:l + 1, :].to_broadcast([128, D]), w=[self.B_mod])
            for k in range(8):
                S.dma("sp", wt[k % 2][:], self.ada_w[l, k * 128:(k + 1) * 128, :], w=[Bw[k % 2]])
                for n in range(6):
                    S.op("pe", lambda e, k=k, n=n: e.matmul(
                        pm[n][0:2, :], self.ccs[:, k, :], wt[k % 2][:, n * 512:(n + 1) * 512],
                        start=(k == 0), stop=(k == 7)), r=[Bw[k % 2], self.B_const], w=[Bp[n]])
            for n in range(6):
                S.op("dve", lambda e, n=n: e.tensor_tensor(
                    out=mrow[:, n * 512:(n + 1) * 512], in0=pm[n][0:2, :], in1=brow[:, n * 512:(n + 1) * 512],
                    op=ALU.add), r=[Bp[n], Bb], w=[Bm])
            for j in range(8):
                S.op("pe", lambda e, j=j: e.transpose(
                    pm[6][:, 2 * j:2 * j + 2], mrow[0:2, j * 128:(j + 1) * 128], self.ident_f[0:2, 0:2]),
                    r=[Bm, self.B_const], w=[Bp[6]])
                S.op("pe", lambda e, j=j: e.transpose(
                    pm[7][:, 2 * j:2 * j + 2], mrow[0:2, D + j * 128:D + (j + 1) * 128], self.ident_f[0:2, 0:2]),
                    r=[Bm, self.B_const], w=[Bp[7]])
            S.op("dve", lambda e: e.tensor_copy(self.sh[:].rearrange("p a b -> p (a b)"), pm[6][:, 0:16]),
                 r=[Bp[6]], w=[self.B_mod])
            S.op("dve", lambda e: e.tensor_scalar(
                out=self.sc1[:].rearrange("p a b -> p (a b)"), in0=pm[7][:, 0:16], scalar1=1.0, scalar2=None,
                op0=ALU.add), r=[Bp[7]], w=[self.B_mod])
            for n in range(2):
                S.op("pe", lambda e, n=n: e.matmul(
                    pm[n][:, :], self.sel[:, 0:128], mrow[0:2, 2 * D + n * 512:2 * D + (n + 1) * 512],
                    start=True, stop=True), r=[Bm, self.B_const], w=[Bp[n]])
                S.op("pe", lambda e, n=n: e.matmul(
                    pm[2 + n][:, :], self.sel[:, 128:256], mrow[0:2, 2 * D + n * 512:2 * D + (n + 1) * 512],
                    start=True, stop=True), r=[Bm, self.B_const], w=[Bp[2 + n]])
                S.op("dve", lambda e, n=n: e.tensor_copy(self.g_l[:, n * 512:(n + 1) * 512], pm[n][:, :]),
                     r=[Bp[n]], w=[self.B_mod])
                S.op("dve", lambda e, n=n: e.tensor_copy(self.g_c[:, n * 512:(n + 1) * 512], pm[2 + n][:, :]),
                     r=[Bp[2 + n]], w=[self.B_mod])
            S.barrier()

    def load_cast(self, st_w, dst_fn, src_fn, nparts, ncols, pieces, Bdst, scale_fn=None, name="lc"):
        S = self.S
        with contextlib.ExitStack() as st:
            stg = [self.sb(st, f"{name}_stg{i}", [nparts, ncols]) for i in range(2)]
            Bs = S.bufs(name + "_stg", 2)
            for i, (dst, src, sc) in enumerate(pieces):
                j = i % 2
                n = src.shape[-1]
                S.dma("sp", stg[j][:, 0:n], src, w=[Bs[j]])
                en = "dve" if i % 2 == 0 else "pool"
                if sc is None:
                    S.op(en, lambda e, dst=dst, j=j, n=n: e.tensor_copy(dst, stg[j][:, 0:n]), r=[Bs[j]], w=[Bdst])
                else:
                    S.op(en, lambda e, dst=dst, j=j, n=n, sc=sc: e.tensor_scalar(
                        out=dst, in0=stg[j][:, 0:n], scalar1=sc, scalar2=None, op0=ALU.mult),
                        r=[Bs[j], Bdst], w=[Bdst])
            S.barrier()

    def even_layer(self, l, src, dst, last):
        i = l // 2
        self.adaln(l)
        self.even_project(i, src)
        self.da_attention(i, l)
        self.mla_attention(i)
        self.out_stage(self.ev_wo[i], src, dst, last)

    def even_project(self, i, src):
        S, nc = self.S, self.nc
        with contextlib.ExitStack() as st:
            wb = self.sb(st, "ev_wb", [128, 8, EV_COLS], BF16)
            wuq = self.sb(st, "ev_wuqb", [128, 2, 1536], BF16)
            wukv = self.sb(st, "ev_wukvb", [128, 1024], BF16)
            bcol = self.sb(st, "ev_bcol", [128, 18])
            brow_f = self.sb(st, "ev_brow_f", [1, 1920])
            brow = self.sb(st, "ev_brow", [1, 1920], BF16)
            qg = self.sb(st, "ev_qg", [128, 2])
            kvg = self.sb(st, "ev_kvg", [128, 1])
            Bw = S.buf("ev_w")
            S.dma("sp", bcol[:], self.ev_bcol[i, :, :], w=[Bw])
            S.dma("sp", brow_f[:], self.ev_brow[i, :, :], w=[Bw])
            S.dma("sp", qg[:], self.ev_qg[i, :, :], w=[Bw])
            S.dma("sp", kvg[:], self.ev_kvg[i, :, :], w=[Bw])
            S.op("dve", lambda e: e.tensor_copy(brow[:], brow_f[:]), r=[Bw], w=[Bw])
            pieces = []
            for k in range(8):
                for c in range(4):
                    pieces.append((wb[:, k, c * 1008:(c + 1) * 1008],
                                   self.ev_w[i, k * 128:(k + 1) * 128, c * 1008:(c + 1) * 1008], None))
            self.load_cast(st, None, None, 128, 1536, pieces, Bw, name="evw")
            pieces = [(wuq[:, rc, :], self.ev_wuq[i, rc * 128:(rc + 1) * 128, :], qg[:, rc:rc + 1]) for rc in range(2)]
            pieces.append((wukv[:, :], self.ev_wukv[i, :, :], kvg[:, 0:1]))
            self.load_cast(st, None, None, 128, 1536, pieces, Bw, name="evw2")

            NXS = 3
            xt = [self.sb(st, f"xt{j}", [128, D]) for j in range(NXS)]
            Bx = S.bufs("xt", NXS)
            hT = [self.sb(st, f"hT{j}", [128, 8, 512], BF16) for j in range(2)]
            Bh = S.bufs("hT", 2)
            cosd = [self.sb(st, f"cosd{j}", [128, 512]) for j in range(2)]
            sind = [self.sb(st, f"sind{j}", [128, 512]) for j in range(2)]
            cosm = [self.sb(st, f"cosm{j}", [128, 512]) for j in range(2)]
            sinm = [self.sb(st, f"sinm{j}", [128, 512]) for j in range(2)]
            Brt = S.bufs("ropet", 2)
            t1 = [self.sb(st, f"t1_{j}", [128, 512]) for j in range(2)]
            t2 = [self.sb(st, f"t2_{j}", [128, 512]) for j in range(2)]
            Bt1 = S.bufs("t1", 2)
            Bt2 = S.bufs("t2", 2)
            fo = [self.sb(st, f"fo{j}", [128, 512], BF16) for j in range(4)]
            Bfo = S.bufs("fo", 4)
            va = [self.sb(st, f"va{j}", [128, 4, 129], BF16) for j in range(2)]
            Bva = S.bufs("va", 2)
            vma = [self.sb(st, f"vma{j}", [128, 8, 65], BF16) for j in range(2)]
            Bvma = S.bufs("vma", 2)
            cq = [self.sb(st, f"cq{j}", [128, 384]) for j in range(2)]
            Bcq = S.bufs("cq", 2)
            cqn = [self.sb(st, f"cqn{j}", [128, 384], BF16) for j in range(2)]
            Bcqn = S.bufs("cqn", 2)
            stat = [self.sb(st, f"stat{j}", [128, 8]) for j in range(2)]
            Bstat = S.bufs("stat", 2)
            junk = self.sb(st, "junk", [128, 256])
            Bjunk = S.buf("junk")
            cT = [self.sb(st, f"cT{j}", [128, 3, 512], BF16) for j in range(2)]
            BcT = S.bufs("cT", 2)
            gt = [self.sb(st, f"gt{j}", [128, D], BF16) for j in range(2)]
            Bgt = S.bufs("gt", 2)
            pp = [self.ps(st, f"pp{j}", [128, 512]) for j in range(7)]
            ppb = self.ps(st, "ppb", [128, 1024], BF16)
            Bpp = S.bufs("pp", 7)
            Bppb = S.buf("ppb")
            for j in range(2):
                S.op("pool", lambda e, j=j: e.memset(va[j][:], 1.0), w=[Bva[j]])
                S.op("pool", lambda e, j=j: e.memset(vma[j][:], 1.0), w=[Bvma[j]])
            eps_t = self.sb(st, "eps_t", [128, 1])
            S.op("pool", lambda e: e.memset(eps_t[:], RMS_EPS), w=[Bw])

            self._pp_i = 0

            def next_pp():
                j = self._pp_i % 7
                self._pp_i += 1
                return pp[j], Bpp[j]

            nblk = (NTOK + 511) // 512
            xi = 0
            foi = 0
            evac = 0
            for blk in range(nblk):
                t0 = blk * 512
                ntok = min(512, NTOK - t0)
                nsub = ntok // 128
                m = 0 if t0 < LAT else 1
                hb = blk % 2
                S.dma("sp", cosd[hb][:, :ntok], self.rope_da[0, :, t0:t0 + ntok], w=[Brt[hb]])
                S.dma("sp", sind[hb][:, :ntok], self.rope_da[1, :, t0:t0 + ntok], w=[Brt[hb]])
                S.dma("sp", cosm[hb][:, :ntok], self.rope_ml[0, :, t0:t0 + ntok], w=[Brt[hb]])
                S.dma("sp", sinm[hb][:, :ntok], self.rope_ml[1, :, t0:t0 + ntok], w=[Brt[hb]])
                for s in range(nsub):
                    xj = xi % NXS
                    xi += 1
                    S.dma("sp", xt[xj][:], src[t0 + s * 128:t0 + (s + 1) * 128, :], w=[Bx[xj]])
                    for half in range(2):
                        p, Bp = next_pp()
                        for q in range(4):
                            dc = half * 4 + q
                            S.op("pe", lambda e, p=p, q=q, dc=dc, xj=xj: e.transpose(
                                p[:, q * 128:(q + 1) * 128], xt[xj][:, dc * 128:(dc + 1) * 128], self.ident_f[:]),
                                r=[Bx[xj], self.B_const], w=[Bp])
                        for q in range(4):
                            dc = half * 4 + q
                            if evac % 2 == 0:
                                S.op("act", lambda e, p=p, q=q, dc=dc, s=s, m=m: e.activation(
                                    out=hT[hb][:, dc, s * 128:(s + 1) * 128], in_=p[:, q * 128:(q + 1) * 128],
                                    func=AF.Identity, scale=self.sc1[:, dc, m:m + 1], bias=self.sh[:, dc, m:m + 1]),
                                    r=[Bp, self.B_mod], w=[Bh[hb]])
                            else:
                                S.op("dve", lambda e, p=p, q=q, dc=dc, s=s, m=m: e.tensor_scalar(
                                    out=hT[hb][:, dc, s * 128:(s + 1) * 128], in0=p[:, q * 128:(q + 1) * 128],
                                    scalar1=self.sc1[:, dc, m:m + 1], scalar2=self.sh[:, dc, m:m + 1],
                                    op0=ALU.mult, op1=ALU.add), r=[Bp, self.B_mod], w=[Bh[hb]])
                            evac += 1
                for grp, base, dstT, bofs in ((0, EQ, self.qda, 0), (1, EK, self.kda, 8)):
                    for h in range(4):
                        pa, Ba = next_pp()
                        pb, Bb = next_pp()
                        for k in range(8):
                            S.op("pe", lambda e, pa=pa, k=k, c0=base + h * 128: e.matmul(
                                pa[:, :ntok], wb[:, k, c0:c0 + 128], hT[hb][:, k, :ntok], start=(k == 0), stop=(k == 7)),
                                r=[Bw, Bh[hb]], w=[Ba])
                        for k in range(8):
                            S.op("pe", lambda e, pb=pb, k=k, c0=base + 512 + h * 128: e.matmul(
                                pb[:, :ntok], wb[:, k, c0:c0 + 128], hT[hb][:, k, :ntok], start=(k == 0), stop=(k == 7)),
                                r=[Bw, Bh[hb]], w=[Bb])
                        tj = (grp * 4 + h) % 2
                        S.op("dve", lambda e, pa=pa, tj=tj, c=bofs + h: e.scalar_tensor_tensor(
                            out=t1[tj][:, :ntok], in0=pa[:, :ntok], scalar=bcol[:, c:c + 1], in1=cosd[hb][:, :ntok],
                            op0=ALU.add, op1=ALU.mult), r=[Ba, Brt[hb], Bw], w=[Bt1[tj]])
                        S.op("dve", lambda e, pb=pb, tj=tj, c=bofs + 4 + h: e.scalar_tensor_tensor(
                            out=t2[tj][:, :ntok], in0=pb[:, :ntok], scalar=bcol[:, c:c + 1], in1=sind[hb][:, :ntok],
                            op0=ALU.add, op1=ALU.mult), r=[Bb, Brt[hb], Bw], w=[Bt2[tj]])
                        fj = foi % 4
                        foi += 1
                        S.op("pool", lambda e, tj=tj, fj=fj: e.tensor_tensor(
                            out=fo[fj][:, :ntok], in0=t1[tj][:, :ntok], in1=t2[tj][:, :ntok], op=ALU.add),
                            r=[Bt1[tj], Bt2[tj]], w=[Bfo[fj]])
                        S.dma("pool", dstT[h, :, t0:t0 + ntok], fo[fj][:, :ntok], r=[Bfo[fj]])
                pa, Ba = next_pp()
                pb, Bb = next_pp()
                for k in range(8):
                    S.op("pe", lambda e, pa=pa, k=k: e.matmul(
                        pa[0:32, :ntok], wb[:, k, EKR:EKR + 32], hT[hb][:, k, :ntok], start=(k == 0), stop=(k == 7)),
                        r=[Bw, Bh[hb]], w=[Ba])
                for k in range(8):
                    S.op("pe", lambda e, pb=pb, k=k: e.matmul(
                        pb[0:32, :ntok], wb[:, k, EKR + 32:EKR + 64], hT[hb][:, k, :ntok], start=(k == 0), stop=(k == 7)),
                        r=[Bw, Bh[hb]], w=[Bb])
                tj = 0
                S.op("dve", lambda e, pa=pa: e.scalar_tensor_tensor(
                    out=t1[tj][0:32, :ntok], in0=pa[0:32, :ntok], scalar=bcol[0:32, 16:17], in1=cosm[hb][0:32, :ntok],
                    op0=ALU.add, op1=ALU.mult), r=[Ba, Brt[hb], Bw], w=[Bt1[tj]])
                S.op("dve", lambda e, pb=pb: e.scalar_tensor_tensor(
                    out=t2[tj][0:32, :ntok], in0=pb[0:32, :ntok], scalar=bcol[0:32, 17:18], in1=sinm[hb][0:32, :ntok],
                    op0=ALU.add, op1=ALU.mult), r=[Bb, Brt[hb], Bw], w=[Bt2[tj]])
                fj = foi % 4
                foi += 1
                S.op("pool", lambda e, fj=fj: e.tensor_tensor(
                    out=fo[fj][0:32, :ntok], in0=t1[tj][0:32, :ntok], in1=t2[tj][0:32, :ntok], op=ALU.add),
                    r=[Bt1[tj], Bt2[tj]], w=[Bfo[fj]])
                S.dma("pool", self.krt[:, t0:t0 + ntok], fo[fj][0:32, :ntok], r=[Bfo[fj]])
                cb = blk % 2
                for s in range(nsub):
                    tt = (t0 // 128) + s
                    tok0 = t0 + s * 128
                    p, Bp = next_pp()
                    for k in range(8):
                        S.op("pe", lambda e, p=p, k=k, s=s: e.matmul(
                            p[:, :], hT[hb][:, k, s * 128:(s + 1) * 128], wb[:, k, EV_:EV_ + 512], start=(k == 0), stop=False),
                            r=[Bw, Bh[hb]], w=[Bp])
                    S.op("pe", lambda e, p=p: e.matmul(p[:, :], self.ones_b[0:1, :], brow[0:1, 0:512], start=False, stop=True),
                         r=[Bw, self.B_const], w=[Bp])
                    vj = tt % 2
                    S.op("act", lambda e, p=p, vj=vj: e.activation(
                        out=va[vj][:, :, 0:128], in_=p[:, :].rearrange("p (h c) -> p h c", h=4), func=AF.Copy),
                        r=[Bp], w=[Bva[vj]])
                    S.dma("pool", self.vda[:, :, tt, :].rearrange("h p c -> p h c"), va[vj][:], r=[Bva[vj]])
                    p, Bp = next_pp()
                    for k in range(8):
                        S.op("pe", lambda e, p=p, k=k, s=s: e.matmul(
                            p[:, 0:384], hT[hb][:, k, s * 128:(s + 1) * 128], wb[:, k, EC:EC + 384], start=(k == 0), stop=False),
                            r=[Bw, Bh[hb]], w=[Bp])
                    S.op("pe", lambda e, p=p: e.matmul(p[:, 0:384], self.ones_b[0:1, :], brow[0:1, 512:896], start=False, stop=True),
                         r=[Bw, self.B_const], w=[Bp])
                    cj = tt % 2
                    S.op("act", lambda e, p=p, cj=cj: e.activation(out=cq[cj][:], in_=p[:, 0:384], func=AF.Copy),
                         r=[Bp], w=[Bcq[cj]])
                    S.op("act", lambda e, cj=cj: e.activation(
                        out=junk[:, 0:256], in_=cq[cj][:, 0:256], func=AF.Square, accum_out=stat[cj][:, 0:1]),
                        r=[Bcq[cj]], w=[Bjunk, Bstat[cj]])
                    S.op("act", lambda e, cj=cj: e.activation(
                        out=junk[:, 0:128], in_=cq[cj][:, 256:384], func=AF.Square, accum_out=stat[cj][:, 1:2]),
                        r=[Bcq[cj]], w=[Bjunk, Bstat[cj]])
                    S.op("act", lambda e, cj=cj: e.activation(out=stat[cj][:, 2:3], in_=stat[cj][:, 0:1], func=AF.Sqrt,
                                                              scale=1.0 / 256, bias=eps_t[:, 0:1]), r=[Bstat[cj], Bw], w=[Bstat[cj]])
                    S.op("act", lambda e, cj=cj: e.activation(out=stat[cj][:, 3:4], in_=stat[cj][:, 1:2], func=AF.Sqrt,
                                                              scale=1.0 / 128, bias=eps_t[:, 0:1]), r=[Bstat[cj], Bw], w=[Bstat[cj]])
                    S.op("dve", lambda e, cj=cj: e.reciprocal(stat[cj][:, 4:6], stat[cj][:, 2:4]), r=[Bstat[cj]], w=[Bstat[cj]])
                    S.op("dve", lambda e, cj=cj: e.tensor_scalar(
                        out=cqn[cj][:, 0:256], in0=cq[cj][:, 0:256], scalar1=stat[cj][:, 4:5], scalar2=None, op0=ALU.mult),
                        r=[Bcq[cj], Bstat[cj]], w=[Bcqn[cj]])
                    S.op("pool", lambda e, cj=cj: e.tensor_scalar(
                        out=cqn[cj][:, 256:384], in0=cq[cj][:, 256:384], scalar1=stat[cj][:, 5:6], scalar2=None, op0=ALU.mult),
                        r=[Bcq[cj], Bstat[cj]], w=[Bcqn[cj]])
                    for rc in range(3):
                        S.op("pe", lambda e, rc=rc, cj=cj: e.transpose(
                            ppb[:, rc * 128:(rc + 1) * 128], cqn[cj][:, rc * 128:(rc + 1) * 128], self.ident_b[:]),
                            r=[Bcqn[cj], self.B_const], w=[Bppb])
                    S.op("dve", lambda e, s=s: e.tensor_copy(
                        cT[cb][:, :, s * 128:(s + 1) * 128], ppb[:, 0:384].rearrange("p (r t) -> p r t", r=3)),
                        r=[Bppb], w=[BcT[cb]])
                    gj = tt % 2
                    for n in range(2):
                        p, Bp = next_pp()
                        for k in range(8):
                            S.op("pe", lambda e, p=p, k=k, s=s, n=n: e.matmul(
                                p[:, :], hT[hb][:, k, s * 128:(s + 1) * 128], wb[:, k, EG + n * 512:EG + (n + 1) * 512],
                                start=(k == 0), stop=False), r=[Bw, Bh[hb]], w=[Bp])
                        S.op("pe", lambda e, p=p, n=n: e.matmul(
                            p[:, :], self.ones_b[0:1, :], brow[0:1, 896 + n * 512:896 + (n + 1) * 512], start=False, stop=True),
                            r=[Bw, self.B_const], w=[Bp])
                        S.op("act", lambda e, p=p, n=n, gj=gj: e.activation(
                            out=gt[gj][:, n * 512:(n + 1) * 512], in_=p[:, :], func=AF.Silu), r=[Bp], w=[Bgt[gj]])
                    S.dma("pool", self.gate[tok0:tok0 + 128, :], gt[gj][:], r=[Bgt[gj]])
                for h in range(8):
                    pa, Ba = next_pp()
                    pb, Bb = next_pp()
                    for rc in range(2):
                        S.op("pe", lambda e, pa=pa, rc=rc, h=h: e.matmul(
                            pa[0:96, :ntok], wuq[:, rc, h * 96:(h + 1) * 96], cT[cb][:, rc, :ntok], start=(rc == 0), stop=(rc == 1)),
                            r=[Bw, BcT[cb]], w=[Ba])
                    for rc in range(2):
                        S.op("pe", lambda e, pb=pb, rc=rc, h=h: e.matmul(
                            pb[0:96, :ntok], wuq[:, rc, 768 + h * 96:768 + (h + 1) * 96], cT[cb][:, rc, :ntok],
                            start=(rc == 0), stop=(rc == 1)), r=[Bw, BcT[cb]], w=[Bb])
                    tj = h % 2
                    fj = foi % 4
                    foi += 1
                    S.op("dve", lambda e, pa=pa, tj=tj: e.tensor_tensor(
                        out=t1[tj][64:96, :ntok], in0=pa[64:96, :ntok], in1=cosm[hb][64:96, :ntok], op=ALU.mult),
                        r=[Ba, Brt[hb]], w=[Bt1[tj]])
                    S.op("dve", lambda e, pb=pb, tj=tj: e.tensor_tensor(
                        out=t2[tj][64:96, :ntok], in0=pb[64:96, :ntok], in1=sinm[hb][64:96, :ntok], op=ALU.mult),
                        r=[Bb, Brt[hb]], w=[Bt2[tj]])
                    S.op("act", lambda e, pa=pa, fj=fj: e.activation(out=fo[fj][0:64, :ntok], in_=pa[0:64, :ntok], func=AF.Copy),
                         r=[Ba], w=[Bfo[fj]])
                    S.op("pool", lambda e, tj=tj, fj=fj: e.tensor_tensor(
                        out=fo[fj][64:96, :ntok], in0=t1[tj][64:96, :ntok], in1=t2[tj][64:96, :ntok], op=ALU.add),
                        r=[Bt1[tj], Bt2[tj]], w=[Bfo[fj]])
                    S.dma("pool", self.qm[h, :, t0:t0 + ntok], fo[fj][0:96, :ntok], r=[Bfo[fj]])
                for hp in range(4):
                    p, Bp = next_pp()
                    S.op("pe", lambda e, p=p, hp=hp: e.matmul(
                        p[:, :ntok], wukv[:, hp * 128:(hp + 1) * 128], cT[cb][:, 2, :ntok], start=True, stop=True),
                        r=[Bw, BcT[cb]], w=[Bp])
                    fj = foi % 4
                    foi += 1
                    S.op("act", lambda e, p=p, fj=fj: e.activation(out=fo[fj][:, :ntok], in_=p[:, :ntok], func=AF.Copy),
                         r=[Bp], w=[Bfo[fj]])
                    S.dma("pool", self.kmn[2 * hp, :, t0:t0 + ntok], fo[fj][0:64, :ntok], r=[Bfo[fj]])
                    S.dma("pool", self.kmn[2 * hp + 1, :, t0:t0 + ntok], fo[fj][64:128, :ntok], r=[Bfo[fj]])
                for s in range(nsub):
                    tt = (t0 // 128) + s
                    p, Bp = next_pp()
                    S.op("pe", lambda e, p=p, s=s: e.matmul(
                        p[:, :], cT[cb][:, 2, s * 128:(s + 1) * 128], wukv[:, 512:1024], start=True, stop=True),
                        r=[Bw, BcT[cb]], w=[Bp])
                    vj = tt % 2
                    S.op("act", lambda e, p=p, vj=vj: e.activation(
                        out=vma[vj][:, :, 0:64], in_=p[:, :].rearrange("p (h c) -> p h c", h=8), func=AF.Copy),
                        r=[Bp], w=[Bvma[vj]])
                    S.dma("pool", self.vm[:, :, tt, :].rearrange("h p c -> p h c"), vma[vj][:], r=[Bvma[vj]])
            S.barrier()

    def attn_core(self, st, name, KT, QT, VA, Bkv, kparts, nv, scale, qblocks, finalize, kslices=None):
        S = self.S
        nmap = len(kparts)
        nacc_per_bank = 512 // nv
        n_acc = nmap * 4
        n_acc_banks = (n_acc + nacc_per_bank - 1) // nacc_per_bank
        accb = [self.ps(st, f"{name}_acc{j}", [128, 512]) for j in range(n_acc_banks)]
        Bacc = S.bufs(name + "_acc", n_acc_banks)
        n_sc = 8 - n_acc_banks
        n_sc = min(n_sc, 4)
        scb = [self.ps(st, f"{name}_sc{j}", [128, 512]) for j in range(n_sc)]
        Bsc = S.bufs(name + "_sc", n_sc)
        NP = 4
        pt = [self.sb(st, f"{name}_pt{j}", [128, 512], BF16) for j in range(NP)]
        Bpt = S.bufs(name + "_pt", NP)

        def acc_ap(mi, sub):
            idx = mi * 4 + sub
            b = idx // nacc_per_bank
            o = (idx % nacc_per_bank) * nv
            return accb[b][:, o:o + nv], Bacc[b]

        sci = 0
        pti = 0
        for qbi, (q0, nq, ktiles) in enumerate(qblocks):
            nsub = nq // 128
            for b in range(n_acc_banks):
                S.op("pe", lambda e, b=b: e.matmul(accb[b][:, :], self.zeros_b[:, :], self.zeros_w[:, :], start=True, stop=False,
                                                   skip_group_check=True), r=[self.B_const], w=[Bacc[b]])
            units = [(kt, mi) for kt in ktiles for mi in range(nmap)]
            pend = []

            def emit_score(u):
                nonlocal sci
                kt, mi = u
                QTm, lo, hi = kparts[mi]
                j = sci % n_sc
                sci += 1
                S.op("pe", lambda e, j=j, lo=lo, hi=hi, kt=kt, QTm=QTm: e.matmul(
                    scb[j][:, :nq], KT[lo:hi, kt * 128:(kt + 1) * 128], QTm[lo:hi, q0:q0 + nq], start=True, stop=True),
                    r=[Bkv], w=[Bsc[j]])
                return j

            def emit_exp_pv(u, j, lastflag):
                nonlocal pti
                kt, mi = u
                pj = pti % NP
                pti += 1
                S.op("act", lambda e, j=j, pj=pj: e.activation(out=pt[pj][:, :nq], in_=scb[j][:, :nq], func=AF.Exp, scale=scale),
                     r=[Bsc[j]], w=[Bpt[pj]])
                for sub in range(nsub):
                    ap, Ba = acc_ap(mi, sub)
                    S.op("pe", lambda e, ap=ap, pj=pj, sub=sub, kt=kt: e.matmul(
                        ap, pt[pj][:, sub * 128:(sub + 1) * 128], VA[:, kt, :], start=False, stop=lastflag,
                        skip_group_check=True), r=[Bpt[pj], Bkv], w=[Ba])

            LOOK = min(n_sc - 1, 2)
            q = []
            for ui, u in enumerate(units):
                q.append((u, emit_score(u)))
                if len(q) > LOOK:
                    u0, j0 = q.pop(0)
                    emit_exp_pv(u0, j0, False)
            while q:
                u0, j0 = q.pop(0)
                emit_exp_pv(u0, j0, u0[0] == ktiles[-1])
            for sub in range(nsub):
                accs = [acc_ap(mi, sub) for mi in range(nmap)]
                finalize(qbi, q0, sub, accs)

    def qblocks_all(self):
        qb = []
        lat_k = list(range(NT))
        for b in range(LAT // 512):
            qb.append((b * 512, 512, lat_k))
        qb.append((LAT, CTX, [NT_LAT, NT_LAT + 1]))
        if self.qb_filter is not None:
            qb = [q for i, q in enumerate(qb) if i in self.qb_filter]
        return qb

    def da_attention(self, i, l):
        S = self.S
        lam_init = 0.8 - 0.6 * math.exp(-0.3 * l)
        with contextlib.ExitStack() as st:
            KT = self.sb(st, "da_KT", [128, NTOK], BF16)
            QT1 = self.sb(st, "da_QT1", [128, NTOK], BF16)
            QT2 = self.sb(st, "da_QT2", [128, NTOK], BF16)
            VA = self.sb(st, "da_VA", [128, NT, 129], BF16)
            Bkv = S.buf("da_kv")
            S.op("pool", lambda e: e.memset(QT1[64:128, :], 0.0), w=[Bkv])
            S.op("pool", lambda e: e.memset(QT2[0:64, :], 0.0), w=[Bkv])
            lamt = self.sb(st, "lamt", [128, 256])
            lamw = self.sb(st, "lamw", [128, 8])
            subg = self.sb(st, "subg", [128, 128])
            Bl = S.buf("lam")
            eps_t = self.sb(st, "da_eps", [128, 1])
            S.op("pool", lambda e: e.memset(eps_t[:], RMS_EPS), w=[Bl])
            S.dma("sp", lamt[:], self.ev_lam[i, :, :].to_broadcast([128, 256]), w=[Bl])
            S.dma("sp", subg[:], self.ev_subg[i, :, :].to_broadcast([128, 128]), w=[Bl])
            junk = self.sb(st, "da_junk", [128, 128])
            Bj = S.buf("da_junk")
            S.op("dve", lambda e: e.tensor_tensor(out=junk[:, 0:64], in0=lamt[:, 0:64], in1=lamt[:, 64:128], op=ALU.mult),
                 r=[Bl], w=[Bj])
            S.op("dve", lambda e: e.reduce_sum(out=lamw[:, 0:1], in_=junk[:, 0:64], axis=AX.X), r=[Bj], w=[Bl])
            S.op("dve", lambda e: e.tensor_tensor(out=junk[:, 64:128], in0=lamt[:, 128:192], in1=lamt[:, 192:256], op=ALU.mult),
                 r=[Bl], w=[Bj])
            S.op("dve", lambda e: e.reduce_sum(out=lamw[:, 1:2], in_=junk[:, 64:128], axis=AX.X), r=[Bj], w=[Bl])
            S.op("act", lambda e: e.activation(out=lamw[:, 2:4], in_=lamw[:, 0:2], func=AF.Exp), r=[Bl], w=[Bl])
            S.op("dve", lambda e: e.tensor_tensor(out=lamw[:, 4:5], in0=lamw[:, 3:4], in1=lamw[:, 2:3], op=ALU.subtract),
                 r=[Bl], w=[Bl])
            S.op("dve", lambda e: e.tensor_scalar(out=lamw[:, 5:6], in0=lamw[:, 4:5], scalar1=-lam_init, scalar2=None, op0=ALU.add),
                 r=[Bl], w=[Bl])
            NF = 3
            rr = [self.sb(st, f"da_rr{j}", [128, 8]) for j in range(NF)]
            ta = [self.sb(st, f"da_ta{j}", [128, 128]) for j in range(NF)]
            td = [self.sb(st, f"da_td{j}", [128, 128]) for j in range(NF)]
            to = [self.sb(st, f"da_to{j}", [128, 128], BF16) for j in range(NF)]
            Bf = S.bufs("da_fin", NF)
            Bto = S.bufs("da_to", NF)
            self._fi = 0
            for h in range(4):
                S.dma("sp", KT[:], self.kda[h, :, :], w=[Bkv])
                S.dma("sp", QT1[0:64, :], self.qda[h, 0:64, :], w=[Bkv])
                S.dma("sp", QT2[64:128, :], self.qda[h, 64:128, :], w=[Bkv])
                S.dma("sp", VA[:], self.vda[h, :, :, :], w=[Bkv])

                def fin(qbi, q0, sub, accs, h=h):
                    j = self._fi % NF
                    self._fi += 1
                    (a1, B1), (a2, B2) = accs
                    S.op("dve", lambda e: e.reciprocal(rr[j][:, 0:1], a1[:, 128:129]), r=[B1], w=[Bf[j]])
                    S.op("dve", lambda e: e.reciprocal(rr[j][:, 1:2], a2[:, 128:129]), r=[B2], w=[Bf[j]])
                    S.op("dve", lambda e: e.tensor_tensor(out=rr[j][:, 2:3], in0=rr[j][:, 1:2], in1=lamw[:, 5:6], op=ALU.mult),
                         r=[Bf[j], Bl], w=[Bf[j]])
                    S.op("dve", lambda e: e.tensor_scalar(out=ta[j][:], in0=a1[:, 0:128], scalar1=rr[j][:, 0:1], scalar2=None,
                                                          op0=ALU.mult), r=[B1, Bf[j]], w=[Bf[j]])
                    S.op("dve", lambda e: e.scalar_tensor_tensor(out=td[j][:], in0=a2[:, 0:128], scalar=rr[j][:, 2:3], in1=ta[j][:],
                                                                 op0=ALU.mult, op1=ALU.add), r=[B2, Bf[j]], w=[Bf[j]])
                    S.op("act", lambda e: e.activation(out=ta[j][:], in_=td[j][:], func=AF.Square, accum_out=rr[j][:, 3:4]),
                         r=[Bf[j]], w=[Bf[j]])
                    S.op("act", lambda e: e.activation(out=rr[j][:, 4:5], in_=rr[j][:, 3:4], func=AF.Sqrt, scale=1.0 / 128,
                                                       bias=eps_t[:, 0:1]), r=[Bf[j], Bl], w=[Bf[j]])
                    S.op("dve", lambda e: e.reciprocal(rr[j][:, 5:6], rr[j][:, 4:5]), r=[Bf[j]], w=[Bf[j]])
                    S.op("pool", lambda e: e.tensor_scalar(out=td[j][:], in0=td[j][:], scalar1=rr[j][:, 5:6], scalar2=(1.0 - lam_init),
                                                           op0=ALU.mult, op1=ALU.mult), r=[Bf[j]], w=[Bf[j]])
                    S.op("pool", lambda e: e.tensor_tensor(out=to[j][:], in0=td[j][:], in1=subg[:], op=ALU.mult),
                         r=[Bf[j], Bl], w=[Bto[j]])
                    tok0 = q0 + sub * 128
                    S.dma("pool", self.ybuf[tok0:tok0 + 128, h * 128:(h + 1) * 128], to[j][:], r=[Bto[j]])

                with contextlib.ExitStack() as st2:
                    self.attn_core(st2, f"da{h}", KT, None, VA, Bkv, [(QT1, 0, 128), (QT2, 0, 128)], 129, DA_SCALE,
                                   self.qblocks_all(), fin)
                    S.barrier()
            S.barrier()

    def mla_attention(self, i):
        S = self.S
        with contextlib.ExitStack() as st:
            KT = self.sb(st, "ml_KT", [128, NTOK], BF16)
            QT = self.sb(st, "ml_QT", [128, NTOK], BF16)
            VA = self.sb(st, "ml_VA", [128, NT, 65], BF16)
            Bkv = S.buf("ml_kv")
            NF = 3
            rr = [self.sb(st, f"ml_rr{j}", [128, 2]) for j in range(NF)]
            to = [self.sb(st, f"ml_to{j}", [128, 64], BF16) for j in range(NF)]
            Bto = S.bufs("ml_to", NF)
            self._fi = 0
            for h in range(8):
                S.dma("sp", KT[0:64, :], self.kmn[h, :, :], w=[Bkv])
                S.dma("sp", KT[64:96, :], self.krt[:, :], w=[Bkv])
                S.dma("sp", QT[0:96, :], self.qm[h, :, :], w=[Bkv])
                S.dma("sp", VA[:], self.vm[h, :, :, :], w=[Bkv])

                def fin(qbi, q0, sub, accs, h=h):
                    j = self._fi % NF
                    self._fi += 1
                    (a1, B1), = accs
                    S.op("dve", lambda e: e.reciprocal(rr[j][:, 0:1], a1[:, 64:65]), r=[B1], w=[Bto[j]])
                    S.op("dve", lambda e: e.tensor_scalar(out=to[j][:], in0=a1[:, 0:64], scalar1=rr[j][:, 0:1], scalar2=None,
                                                          op0=ALU.mult), r=[B1, Bto[j]], w=[Bto[j]])
                    tok0 = q0 + sub * 128
                    S.dma("pool", self.ybuf[tok0:tok0 + 128, 512 + h * 64:512 + (h + 1) * 64], to[j][:], r=[Bto[j]])

                with contextlib.ExitStack() as st2:
                    self.attn_core(st2, f"ml{h}", KT, None, VA, Bkv, [(QT, 0, 96)], 65, MLA_SCALE, self.qblocks_all(), fin)
                    S.barrier()
            S.barrier()

    def out_stage(self, wo_dram, src, dst, last):
        S = self.S
        with contextlib.ExitStack() as st:
            wo = self.sb(st, "wo", [128, 8, D], BF16)
            Bw = S.buf("wo")
            pieces = [(wo[:, k, :], wo_dram[k * 128:(k + 1) * 128, :], None) for k in range(8)]
            self.load_cast(st, None, None, 128, D, pieces, Bw, name="wo")
            NB = 3
            yt = [self.sb(st, f"o_yt{j}", [128, D], BF16) for j in range(NB)]
            gt = [self.sb(st, f"o_gt{j}", [128, D], BF16) for j in range(NB)]
            xt = [self.sb(st, f"o_xt{j}", [128, D]) for j in range(NB)]
            yg = [self.sb(st, f"o_yg{j}", [128, D], BF16) for j in range(NB)]
            ygT = [self.sb(st, f"o_ygT{j}", [128, D], BF16) for j in range(NB)]
            rt = [self.sb(st, f"o_rt{j}", [128, D]) for j in range(NB)]
            ot = [self.sb(st, f"o_ot{j}", [128, D]) for j in range(NB)]
            stt = [self.sb(st, f"o_st{j}", [128, 16]) for j in range(NB)]
            By, Bg, Bx, Byg, BygT, Br, Bo, Bs = (S.bufs(n, NB) for n in ("o_yt", "o_gt", "o_xt", "o_yg", "o_ygT", "o_rt", "o_ot", "o_st"))
            ptr = [self.ps(st, f"o_ptr{j}", [128, D], BF16) for j in range(2)]
            Bptr = S.bufs("o_ptr", 2)
            pout = [self.ps(st, f"o_po{j}", [128, 512]) for j in range(4)]
            Bpo = S.bufs("o_po", 4)
            eps_t = self.sb(st, "o_eps", [128, 1])
            S.op("pool", lambda e: e.memset(eps_t[:], LN_EPS), w=[Bw])
            ntiles = NT_LAT if (last and self.final_out) else NT
            for t in range(ntiles):
                if self.tile_filter is not None and t not in self.tile_filter:
                    continue
                j = t % NB
                tok0 = t * 128
                isctx = t >= NT_LAT
                G = self.g_c if isctx else self.g_l
                S.dma("sp", yt[j][:], self.ybuf[tok0:tok0 + 128, :], w=[By[j]])
                S.dma("act", gt[j][:], self.gate[tok0:tok0 + 128, :], w=[Bg[j]])
                S.dma("sp", xt[j][:], src[tok0:tok0 + 128, :], w=[Bx[j]])
                S.op("pool", lambda e, j=j: e.tensor_tensor(out=yg[j][:], in0=yt[j][:], in1=gt[j][:], op=ALU.mult),
                     r=[By[j], Bg[j]], w=[Byg[j]])
                pj = t % 2
                for ec in range(8):
                    S.op("pe", lambda e, ec=ec, j=j, pj=pj: e.transpose(
                        ptr[pj][:, ec * 128:(ec + 1) * 128], yg[j][:, ec * 128:(ec + 1) * 128], self.ident_b[:]),
                        r=[Byg[j], self.B_const], w=[Bptr[pj]])
                S.op("act", lambda e, j=j, pj=pj: e.activation(out=ygT[j][:], in_=ptr[pj][:], func=AF.Copy),
                     r=[Bptr[pj]], w=[BygT[j]])
                for n in range(2):
                    pn = (t % 2) * 2 + n
                    for ec in range(8):
                        S.op("pe", lambda e, ec=ec, n=n, pn=pn, j=j: e.matmul(
                            pout[pn][:, :], ygT[j][:, ec * 128:(ec + 1) * 128], wo[:, ec, n * 512:(n + 1) * 512],
                            start=(ec == 0), stop=(ec == 7)), r=[BygT[j], Bw], w=[Bpo[pn]])
                    S.op("dve", lambda e, n=n, pn=pn, j=j, G=G: e.tensor_tensor(
                        out=rt[j][:, n * 512:(n + 1) * 512], in0=pout[pn][:, :], in1=G[:, n * 512:(n + 1) * 512], op=ALU.mult),
                        r=[Bpo[pn], self.B_mod], w=[Br[j]])
                S.op("dve", lambda e, j=j: e.scalar_tensor_tensor(
                    out=rt[j][:], in0=xt[j][:], scalar=ALPHA, in1=rt[j][:], op0=ALU.mult, op1=ALU.add),
                    r=[Bx[j], Br[j]], w=[Br[j]])
                for n in range(2):
                    S.op("dve", lambda e, n=n, j=j: e.bn_stats(stt[j][:, n * 6:(n + 1) * 6], rt[j][:, n * 512:(n + 1) * 512]),
                         r=[Br[j]], w=[Bs[j]])
                S.op("dve", lambda e, j=j: e.bn_aggr(stt[j][:, 12:14], stt[j][:, 0:12]), r=[Bs[j]], w=[Bs[j]])
                S.op("act", lambda e, j=j: e.activation(out=stt[j][:, 14:15], in_=stt[j][:, 13:14], func=AF.Sqrt, scale=1.0,
                                                        bias=eps_t[:, 0:1]), r=[Bs[j], Bw], w=[Bs[j]])
                S.op("dve", lambda e, j=j: e.reciprocal(stt[j][:, 15:16], stt[j][:, 14:15]), r=[Bs[j]], w=[Bs[j]])
                S.op("dve", lambda e, j=j: e.tensor_scalar(
                    out=ot[j][:], in0=rt[j][:], scalar1=stt[j][:, 12:13], scalar2=stt[j][:, 15:16], op0=ALU.subtract, op1=ALU.mult),
                    r=[Br[j], Bs[j]], w=[Bo[j]])
                S.op("pool", lambda e, j=j: e.tensor_tensor(out=ot[j][:], in0=ot[j][:], in1=self.lng[:], op=ALU.mult),
                     r=[Bo[j], self.B_mod], w=[Bo[j]])
                S.op("pool", lambda e, j=j: e.tensor_tensor(out=ot[j][:], in0=ot[j][:], in1=self.lnb[:], op=ALU.add),
                     r=[Bo[j], self.B_mod], w=[Bo[j]])
                S.dma("pool", dst[tok0:tok0 + 128, :], ot[j][:], r=[Bo[j]])
            S.barrier()

    def odd_layer(self, l, src, dst, last):
        i = l // 2
        self.adaln(l)
        self.odd_project(i, src)
        self.odd_conv(i)
        self.mlstm(i)
        self.mlstm_post(i)
        self.na_attention(i, last)
        self.out_stage(self.od_wo[i], src, dst, last)

    def odd_project(self, i, src):
        S = self.S
        with contextlib.ExitStack() as st:
            wb = self.sb(st, "od_wb", [128, 8, OD_COLS], BF16)
            bcol = self.sb(st, "od_bcol", [128, 16])
            brow_f = self.sb(st, "od_brow_f", [1, 2576])
            brow = self.sb(st, "od_brow", [1, 2576], BF16)
            fb = self.sb(st, "od_fb", [128, 8])
            Bw = S.buf("od_w")
            S.dma("sp", bcol[:], self.od_bcol[i, :, :], w=[Bw])
            S.dma("sp", brow_f[:], self.od_brow[i, :, :], w=[Bw])
            S.dma("sp", fb[:], self.od_fb[i, :, :].to_broadcast([128, 8]), w=[Bw])
            S.op("dve", lambda e: e.tensor_copy(brow[:], brow_f[:]), r=[Bw], w=[Bw])
            pieces = []
            for k in range(8):
                for c in range(4):
                    pieces.append((wb[:, k, c * 1156:(c + 1) * 1156],
                                   self.od_w[i, k * 128:(k + 1) * 128, c * 1156:(c + 1) * 1156], None))
            self.load_cast(st, None, None, 128, 1156, pieces, Bw, name="odw")
            NXS = 3
            xt = [self.sb(st, f"xt{j}", [128, D]) for j in range(NXS)]
            Bx = S.bufs("xt", NXS)
            hT = [self.sb(st, f"hT{j}", [128, 8, 512], BF16) for j in range(2)]
            Bh = S.bufs("hT", 2)
            ff = [self.sb(st, f"ff{j}", [128, 512]) for j in range(3)]
            Bff = S.bufs("ff", 3)
            fo = [self.sb(st, f"fo{j}", [128, 512], BF16) for j in range(3)]
            Bfo = S.bufs("fo", 3)
            va = [self.sb(st, f"va{j}", [128, 4, 129], BF16) for j in range(2)]
            Bva = S.bufs("va", 2)
            vna = [self.sb(st, f"vna{j}", [128, 8, 65], BF16) for j in range(2)]
            Bvna = S.bufs("vna", 2)
            ot = [self.sb(st, f"ot{j}", [128, 512]) for j in range(2)]
            Bot = S.bufs("ot", 2)
            gt = [self.sb(st, f"gt{j}", [128, D], BF16) for j in range(2)]
            Bgt = S.bufs("gt", 2)
            gs = [self.sb(st, f"gs{j}", [128, 4, 16]) for j in range(2)]
            gtmp = [self.sb(st, f"gtmp{j}", [128, 4, 8]) for j in range(2)]
            Bgs = S.bufs("gs", 2)
            pp = [self.ps(st, f"pp{j}", [128, 512]) for j in range(7)]
            pg = self.ps(st, "pg", [128, 512])
            Bpp = S.bufs("pp", 7)
            Bpg = S.buf("pg")
            for j in range(2):
                S.op("pool", lambda e, j=j: e.memset(va[j][:], 1.0), w=[Bva[j]])
                S.op("pool", lambda e, j=j: e.memset(vna[j][:], 1.0), w=[Bvna[j]])
            self._pp_i = 0

            def next_pp():
                j = self._pp_i % 7
                self._pp_i += 1
                return pp[j], Bpp[j]

            nblk = (NTOK + 511) // 512
            xi = 0
            evac = 0
            ffi = 0
            foi = 0
            for blk in range(nblk):
                t0 = blk * 512
                ntok = min(512, NTOK - t0)
                nsub = ntok // 128
                m = 0 if t0 < LAT else 1
                hb = blk % 2
                for s in range(nsub):
                    xj = xi % NXS
                    xi += 1
                    S.dma("sp", xt[xj][:], src[t0 + s * 128:t0 + (s + 1) * 128, :], w=[Bx[xj]])
                    for half in range(2):
                        p, Bp = next_pp()
                        for q in range(4):
                            dc = half * 4 + q
                            S.op("pe", lambda e, p=p, q=q, dc=dc, xj=xj: e.transpose(
                                p[:, q * 128:(q + 1) * 128], xt[xj][:, dc * 128:(dc + 1) * 128], self.ident_f[:]),
                                r=[Bx[xj], self.B_const], w=[Bp])
                        for q in range(4):
                            dc = half * 4 + q
                            if evac % 2 == 0:
                                S.op("act", lambda e, p=p, q=q, dc=dc, s=s, m=m: e.activation(
                                    out=hT[hb][:, dc, s * 128:(s + 1) * 128], in_=p[:, q * 128:(q + 1) * 128],
                                    func=AF.Identity, scale=self.sc1[:, dc, m:m + 1], bias=self.sh[:, dc, m:m + 1]),
                                    r=[Bp, self.B_mod], w=[Bh[hb]])
                            else:
                                S.op("dve", lambda e, p=p, q=q, dc=dc, s=s, m=m: e.tensor_scalar(
                                    out=hT[hb][:, dc, s * 128:(s + 1) * 128], in0=p[:, q * 128:(q + 1) * 128],
                                    scalar1=self.sc1[:, dc, m:m + 1], scalar2=self.sh[:, dc, m:m + 1],
                                    op0=ALU.mult, op1=ALU.add), r=[Bp, self.B_mod], w=[Bh[hb]])
                            evac += 1
                for c in range(16):
                    p, Bp = next_pp()
                    for k in range(8):
                        S.op("pe", lambda e, p=p, k=k, c=c: e.matmul(
                            p[:, :ntok], wb[:, k, c * 128:(c + 1) * 128], hT[hb][:, k, :ntok], start=(k == 0), stop=(k == 7)),
                            r=[Bw, Bh[hb]], w=[Bp])
                    if c < 8:
                        fj = ffi % 3
                        ffi += 1
                        if c % 2 == 0:
                            S.op("act", lambda e, p=p, c=c, fj=fj: e.activation(
                                out=ff[fj][:, :ntok], in_=p[:, :ntok], func=AF.Identity, bias=bcol[:, c:c + 1]),
                                r=[Bp, Bw], w=[Bff[fj]])
                        else:
                            S.op("dve", lambda e, p=p, c=c, fj=fj: e.tensor_scalar(
                                out=ff[fj][:, :ntok], in0=p[:, :ntok], scalar1=bcol[:, c:c + 1], scalar2=None, op0=ALU.add),
                                r=[Bp, Bw], w=[Bff[fj]])
                        S.dma("pool", self.qkpre[c, :, t0:t0 + ntok], ff[fj][:, :ntok], r=[Bff[fj]])
                    else:
                        fj = foi % 3
                        foi += 1
                        if c % 2 == 0:
                            S.op("act", lambda e, p=p, c=c, fj=fj: e.activation(
                                out=fo[fj][:, :ntok], in_=p[:, :ntok], func=AF.Identity, bias=bcol[:, c:c + 1]),
                                r=[Bp, Bw], w=[Bfo[fj]])
                        else:
                            S.op("dve", lambda e, p=p, c=c, fj=fj: e.tensor_scalar(
                                out=fo[fj][:, :ntok], in0=p[:, :ntok], scalar1=bcol[:, c:c + 1], scalar2=None, op0=ALU.add),
                                r=[Bp, Bw], w=[Bfo[fj]])
                        dd = self.qda if c < 12 else self.kda
                        S.dma("pool", dd[(c - 8) % 4, :, t0:t0 + ntok], fo[fj][:, :ntok], r=[Bfo[fj]])
                gb = blk % 2
                for s in range(nsub):
                    tt = (t0 // 128) + s
                    tok0 = t0 + s * 128

                    def tm(p, c0, n, b0, s=s):
                        for k in range(8):
                            S.op("pe", lambda e, k=k: e.matmul(
                                p[:, 0:n], hT[hb][:, k, s * 128:(s + 1) * 128], wb[:, k, c0:c0 + n], start=(k == 0), stop=False),
                                r=[Bw, Bh[hb]], w=[Bp])
                        S.op("pe", lambda e: e.matmul(p[:, 0:n], self.ones_b[0:1, :], brow[0:1, b0:b0 + n], start=False, stop=True),
                             r=[Bw, self.B_const], w=[Bp])
                    p, Bp = next_pp()
                    tm(p, OV, 512, 0)
                    vj = tt % 2
                    S.op("act", lambda e, p=p, vj=vj: e.activation(
                        out=va[vj][:, :, 0:128], in_=p[:, :].rearrange("p (h c) -> p h c", h=4), func=AF.Copy),
                        r=[Bp], w=[Bva[vj]])
                    S.dma("pool", self.vda[:, :, tt, :].rearrange("h p c -> p h c"), va[vj][:], r=[Bva[vj]])
                    p, Bp = next_pp()
                    tm(p, OO, 512, 512)
                    oj = tt % 2
                    S.op("act", lambda e, p=p, oj=oj: e.activation(out=ot[oj][:], in_=p[:, :], func=AF.Sigmoid), r=[Bp], w=[Bot[oj]])
                    S.dma("pool", self.og[tok0:tok0 + 128, :], ot[oj][:], r=[Bot[oj]])
                    Bp = Bpg
                    for k in range(8):
                        S.op("pe", lambda e, k=k, s=s: e.matmul(
                            pg[:, s * 16:(s + 1) * 16], hT[hb][:, k, s * 128:(s + 1) * 128], wb[:, k, OGT:OGT + 16],
                            start=(k == 0), stop=False), r=[Bw, Bh[hb]], w=[Bpg])
                    S.op("pe", lambda e, s=s: e.matmul(pg[:, s * 16:(s + 1) * 16], self.ones_b[0:1, :], brow[0:1, 1024:1040],
                                                       start=False, stop=True), r=[Bw, self.B_const], w=[Bpg])
                    p, Bp = next_pp()
                    tm(p, OVN, 512, 1040)
                    vj = tt % 2
                    S.op("act", lambda e, p=p, vj=vj: e.activation(
                        out=vna[vj][:, :, 0:64], in_=p[:, :].rearrange("p (h c) -> p h c", h=8), func=AF.Copy),
                        r=[Bp], w=[Bvna[vj]])
                    S.dma("pool", self.vm[:, :, tt, :].rearrange("h p c -> p h c"), vna[vj][:], r=[Bvna[vj]])
                    gj = tt % 2
                    for n in range(2):
                        p, Bp = next_pp()
                        tm(p, OG + n * 512, 512, 1552 + n * 512)
                        S.op("act", lambda e, p=p, n=n, gj=gj: e.activation(
                            out=gt[gj][:, n * 512:(n + 1) * 512], in_=p[:, :], func=AF.Silu), r=[Bp], w=[Bgt[gj]])
                    S.dma("pool", self.gate[tok0:tok0 + 128, :], gt[gj][:], r=[Bgt[gj]])
                pgv = pg[:, 0:nsub * 16].rearrange("p (s c) -> p s c", c=16)
                S.op("dve", lambda e, pgv=pgv: e.tensor_copy(gs[gb][:, 0:nsub, 0:8], pgv[:, :, 0:8]), r=[Bpg], w=[Bgs[gb]])
                for s in range(nsub):
                    S.op("dve", lambda e, s=s: e.tensor_tensor(out=gtmp[gb][:, s, :], in0=pg[:, s * 16 + 8:s * 16 + 16], in1=fb[:, :],
                                                               op=ALU.add), r=[Bpg, Bw], w=[Bgs[gb]])
                S.op("act", lambda e: e.activation(out=gtmp[gb][:, 0:nsub, :], in_=gtmp[gb][:, 0:nsub, :], func=AF.Exp, scale=-1.0),
                     r=[Bgs[gb]], w=[Bgs[gb]])
                S.op("act", lambda e: e.activation(out=gtmp[gb][:, 0:nsub, :], in_=gtmp[gb][:, 0:nsub, :], func=AF.Ln, bias=1.0),
                     r=[Bgs[gb]], w=[Bgs[gb]])
                S.op("dve", lambda e: e.tensor_scalar(out=gs[gb][:, 0:nsub, 8:16], in0=gtmp[gb][:, 0:nsub, :], scalar1=-1.0,
                                                      scalar2=None, op0=ALU.mult), r=[Bgs[gb]], w=[Bgs[gb]])
                tt0 = t0 // 128
                S.dma("pool", self.gates[:, tt0:tt0 + nsub, :], gs[gb][:, 0:nsub, :], r=[Bgs[gb]])
            S.barrier()

    def odd_conv(self, i):
        S = self.S
        SEG = 2048
        with contextlib.ExitStack() as st:
            cw = self.sb(st, "cv_w", [128, 8, 5])
            cb = self.sb(st, "cv_b", [128, 8])
            Bw = S.buf("cv_w")
            S.dma("sp", cw[:], self.od_convw[i, :, :, :], w=[Bw])
            S.dma("sp", cb[:], self.od_convb[i, :, :], w=[Bw])
            xin = [self.sb(st, f"cv_x{j}", [128, SEG + 4]) for j in range(2)]
            Bxin = S.bufs("cv_x", 2)
            acc = [self.sb(st, f"cv_a{j}", [128, SEG]) for j in range(2)]
            Bacc = S.bufs("cv_a", 2)
            tmp = [self.sb(st, f"cv_t{j}", [128, SEG]) for j in range(2)]
            Btmp = S.bufs("cv_t", 2)
            outb = [self.sb(st, f"cv_o{j}", [128, SEG], BF16) for j in range(2)]
            Bout = S.bufs("cv_o", 2)
            segs = [(a, SEG, 0, LAT) for a in range(0, LAT, SEG)] + [(LAT, CTX, LAT, LAT + CTX)]
            it = 0
            for c in range(8):
                for (t0, n, lo, hi) in segs:
                    j = it % 2
                    it += 1
                    a = max(t0 - 2, lo)
                    b = min(t0 + n + 2, hi)
                    if a > t0 - 2:
                        S.op("pool", lambda e, j=j: e.memset(xin[j][:, 0:2], 0.0), w=[Bxin[j]])
                    if b < t0 + n + 2:
                        S.op("pool", lambda e, j=j, n=n: e.memset(xin[j][:, n + 2:n + 4], 0.0), w=[Bxin[j]])
                    S.dma("sp", xin[j][:, a - (t0 - 2):b - (t0 - 2)], self.qkpre[c, :, a:b], w=[Bxin[j]])
                    S.op("act", lambda e, j=j, n=n, c=c: e.activation(
                        out=acc[j][:, 0:n], in_=xin[j][:, 0:n], func=AF.Copy, scale=cw[:, c, 0:1]),
                        r=[Bxin[j], Bw], w=[Bacc[j]])
                    for k in range(1, 5):
                        S.op("dve", lambda e, j=j, n=n, c=c, k=k: e.scalar_tensor_tensor(
                            out=acc[j][:, 0:n], in0=xin[j][:, k:k + n], scalar=cw[:, c, k:k + 1], in1=acc[j][:, 0:n],
                            op0=ALU.mult, op1=ALU.add), r=[Bxin[j], Bw, Bacc[j]], w=[Bacc[j]])
                    if True:
                        S.op("act", lambda e, j=j, n=n, c=c: e.activation(
                            out=outb[j][:, 0:n], in_=acc[j][:, 0:n], func=AF.Silu, bias=cb[:, c:c + 1]),
                            r=[Bacc[j], Bw], w=[Bout[j]])
                        dd = self.mq if c < 4 else self.mk
                        S.dma("pool", dd[c % 4, :, t0:t0 + n], outb[j][:, 0:n], r=[Bout[j]])
                    else:
                        S.op("act", lambda e, j=j, n=n, c=c: e.activation(
                            out=tmp[j][:, 0:n], in_=acc[j][:, 0:n], func=AF.Silu, bias=cb[:, c:c + 1]),
                            r=[Bacc[j], Bw], w=[Btmp[j]])
                        S.op("pool", lambda e, j=j, n=n: e.tensor_scalar(
                            out=outb[j][:, 0:n], in0=tmp[j][:, 0:n], scalar1=128 ** -0.5, scalar2=None, op0=ALU.mult),
                            r=[Btmp[j]], w=[Bout[j]])
                        S.dma("sp", self.mk[c - 4, :, t0:t0 + n], outb[j][:, 0:n], r=[Bout[j]])
            S.barrier()

    def mlstm(self, i):
        S = self.S
        with contextlib.ExitStack() as st:
            tri = self.sb(st, "ml_tri", [128, 3, 128])
            G = self.sb(st, "ml_G", [128, NT, 16])
            A = self.sb(st, "ml_A", [128, NT, 8])
            A2 = self.sb(st, "ml_A2", [128, NT, 8])
            Bq = self.sb(st, "ml_Bq", [128, NT, 8])
            EB = self.sb(st, "ml_EB", [128, NT, 8])
            Bg = S.buf("ml_g")
            S.dma("sp", tri[:], self.tri[:, :, :], w=[Bg])
            S.dma("sp", G[:], self.gates[:, :, :], w=[Bg])
            lnks = self.sb(st, "ml_lnks", [128, 1])
            S.op("pool", lambda e: e.memset(lnks[:], math.log(128 ** -0.5)), w=[Bg])
            with contextlib.ExitStack() as st1:
                pg = [self.ps(st1, f"ml_pg{j}", [128, 512]) for j in range(3)]
                Bpg = S.bufs("ml_pg", 3)
                tmpg = self.sb(st1, "ml_tmpg", [128, 32, 8])
                Btg = S.buf("ml_tmpg")
                grp = 0
                for g0 in range(0, NT, 32):
                    g1 = min(g0 + 32, NT)
                    pj = grp % 3
                    grp += 1
                    for t in range(g0, g1):
                        o = (t - g0) * 16
                        S.op("pe", lambda e, t=t, o=o, pj=pj: e.matmul(pg[pj][:, o:o + 4], tri[:, 0, :], G[:, t, 8:12], start=True, stop=True),
                             r=[Bg], w=[Bpg[pj]])
                        S.op("pe", lambda e, t=t, o=o, pj=pj: e.matmul(pg[pj][:, o + 4:o + 8], tri[:, 1, :], G[:, t, 12:16], start=True, stop=True),
                             r=[Bg], w=[Bpg[pj]])
                        S.op("pe", lambda e, t=t, o=o, pj=pj: e.matmul(pg[pj][:, o + 8:o + 16], tri[:, 2, :], G[:, t, 8:16], start=True, stop=True),
                             r=[Bg], w=[Bpg[pj]])
                    n = g1 - g0
                    pv = pg[pj][:, 0:n * 16].rearrange("p (t c) -> p t c", c=16)
                    S.op("dve", lambda e, pv=pv, g0=g0, g1=g1, n=n: e.tensor_tensor(
                        out=tmpg[:, 0:n, :], in0=G[:, g0:g1, 0:8], in1=pv[:, :, 0:8], op=ALU.subtract), r=[Bg, Bpg[pj]], w=[Btg])
                    S.op("act", lambda e, g0=g0, g1=g1, n=n: e.activation(out=A[:, g0:g1, :], in_=tmpg[:, 0:n, :], func=AF.Exp,
                                                                          bias=lnks[:, 0:1]), r=[Btg, Bg], w=[Bg])
                    S.op("act", lambda e, pv=pv, g0=g0, g1=g1: e.activation(out=Bq[:, g0:g1, :], in_=pv[:, :, 0:8], func=AF.Exp),
                         r=[Bpg[pj]], w=[Bg])
                    S.op("act", lambda e, pv=pv, g0=g0, g1=g1: e.activation(out=EB[:, g0:g1, :], in_=pv[:, :, 8:16], func=AF.Exp),
                         r=[Bpg[pj]], w=[Bg])
                    S.op("dve", lambda e, g0=g0, g1=g1: e.tensor_tensor(out=A2[:, g0:g1, :], in0=A[:, g0:g1, :], in1=EB[:, g0:g1, :],
                                                                        op=ALU.mult), r=[Bg], w=[Bg])
                S.barrier()
            qT = self.sb(st, "ml_qT", [128, NTOK], BF16)
            kT = self.sb(st, "ml_kT", [128, NTOK], BF16)
            V = self.sb(st, "ml_V", [128, NT, 129], BF16)
            KTOK = self.sb(st, "ml_KTOK", [128, NT, 128], BF16)
            Bin = S.buf("ml_in")
            Bkt = S.buf("ml_ktok")
            Cn = [self.sb(st, f"ml_Cn{d}", [128, 129]) for d in range(2)]
            Cnb = [[self.sb(st, f"ml_Cnb{d}{j}", [128, 129], BF16) for j in range(2)] for d in range(2)]
            BCn = S.bufs("ml_Cn", 2)
            BCnb = [S.bufs(f"ml_Cnb{d}", 2) for d in range(2)]
            NW = 3
            W = [self.sb(st, f"ml_W{j}", [128, 128], BF16) for j in range(NW)]
            BW = S.bufs("ml_W", NW)
            v2 = [self.sb(st, f"ml_v2{j}", [128, 129], BF16) for j in range(NW)]
            Bv2 = S.bufs("ml_v2", NW)
            ho = [self.sb(st, f"ml_ho{j}", [128, 128]) for j in range(NW)]
            Bho = S.bufs("ml_ho", NW)
            sm = [self.sb(st, f"ml_sm{j}", [128, 4]) for j in range(NW)]
            Bsm = S.bufs("ml_sm", NW)
            ps_s = [self.ps(st, f"ml_pss{j}", [128, 512]) for j in range(2)]
            ps_kv = [self.ps(st, f"ml_pkv{j}", [128, 512]) for j in range(2)]
            ps_n = [self.ps(st, f"ml_pn{j}", [128, 512]) for j in range(2)]
            ppb = self.ps(st, "ml_ppb", [128, 1024], BF16)
            Bps_s, Bps_kv, Bps_n = S.bufs("ml_pss", 2), S.bufs("ml_pkv", 2), S.bufs("ml_pn", 2)
            Bppb = S.buf("ml_ppb")
            order = [[NT_LAT, NT_LAT + 1] + list(range(NT_LAT)), [NT_LAT + 1, NT_LAT] + list(range(NT_LAT - 1, -1, -1))]
            wi = 0
            for h in range(4):
                S.dma("sp", qT[:], self.mq[h, :, :], w=[Bin])
                S.dma("sp", kT[:], self.mk[h, :, :], w=[Bin])
                S.dma("sp", V[:], self.vda[h, :, :, :], w=[Bin])
                for t0 in range(0, NT, 8):
                    n = min(8, NT - t0)
                    for q in range(n):
                        t = t0 + q
                        S.op("pe", lambda e, t=t, q=q: e.transpose(ppb[:, q * 128:(q + 1) * 128], kT[:, t * 128:(t + 1) * 128], self.ident_b[:]),
                             r=[Bin, self.B_const], w=[Bppb])
                    S.op("act", lambda e, t0=t0, n=n: e.activation(
                        out=KTOK[:, t0:t0 + n, :], in_=ppb[:, 0:n * 128].rearrange("p (t c) -> p t c", c=128), func=AF.Copy),
                        r=[Bppb], w=[Bkt])
                for d in range(2):
                    S.op("pool", lambda e, d=d: e.memset(Cn[d][:], 0.0), w=[BCn[d]])
                    S.op("pool", lambda e, d=d: e.memset(Cnb[d][0][:], 0.0), w=[BCnb[d][0]])
                for step in range(NT):
                    for d in range(2):
                        t = order[d][step]
                        hd = d * 4 + h
                        cur, nxt = step % 2, (step + 1) % 2
                        j = wi % NW
                        pj = wi % 2
                        wi += 1
                        tsl = slice(t * 128, (t + 1) * 128)
                        S.op("pe", lambda e, pj=pj, tsl=tsl: e.matmul(ps_s[pj][:, 0:128], kT[:, tsl], qT[:, tsl], start=True, stop=True),
                             r=[Bin], w=[Bps_s[pj]])
                        S.op("dve", lambda e, pj=pj, j=j, t=t, hd=hd, d=d: e.scalar_tensor_tensor(
                            out=W[j][:], in0=ps_s[pj][:, 0:128], scalar=A[:, t, hd:hd + 1], in1=tri[:, d, :],
                            op0=ALU.mult, op1=ALU.mult), r=[Bps_s[pj], Bg], w=[BW[j]])
                        S.op("act", lambda e, j=j, t=t, hd=hd: e.activation(
                            out=v2[j][:], in_=V[:, t, :], func=AF.Copy, scale=A2[:, t, hd:hd + 1]),
                            r=[Bin, Bg], w=[Bv2[j]])
                        S.op("pe", lambda e, pj=pj, j=j, t=t: e.matmul(ps_kv[pj][:, 0:129], KTOK[:, t, :], v2[j][:], start=True, stop=True),
                             r=[Bkt, Bv2[j]], w=[Bps_kv[pj]])
                        S.op("pe", lambda e, pj=pj, j=j, t=t: e.matmul(ps_n[pj][:, 0:129], W[j][:], V[:, t, :], start=True, stop=False),
                             r=[BW[j], Bin], w=[Bps_n[pj]])
                        S.op("pe", lambda e, pj=pj, tsl=tsl, d=d, cur=cur: e.matmul(ps_n[pj][:, 0:129], qT[:, tsl], Cnb[d][cur][:], start=False, stop=True),
                             r=[Bin, BCnb[d][cur]], w=[Bps_n[pj]])
                        S.op("dve", lambda e, pj=pj, d=d, t=t, hd=hd: e.scalar_tensor_tensor(
                            out=Cn[d][:], in0=Cn[d][:], scalar=EB[:, t, hd:hd + 1], in1=ps_kv[pj][:, 0:129],
                            op0=ALU.mult, op1=ALU.add), r=[BCn[d], Bps_kv[pj], Bg], w=[BCn[d]])
                        S.op("act", lambda e, d=d, nxt=nxt: e.activation(out=Cnb[d][nxt][:], in_=Cn[d][:], func=AF.Copy),
                             r=[BCn[d]], w=[BCnb[d][nxt]])
                        S.op("dve", lambda e, pj=pj, j=j, t=t, hd=hd: e.tensor_tensor(
                            out=sm[j][:, 0:1], in0=ps_n[pj][:, 128:129], in1=Bq[:, t, hd:hd + 1], op=ALU.mult),
                            r=[Bps_n[pj], Bg], w=[Bsm[j]])
                        S.op("dve", lambda e, j=j: e.scalar_tensor_tensor(out=sm[j][:, 1:2], in0=sm[j][:, 0:1], scalar=-1.0,
                                                                          in1=sm[j][:, 0:1], op0=ALU.mult, op1=ALU.max),
                             r=[Bsm[j]], w=[Bsm[j]])
                        S.op("dve", lambda e, j=j: e.tensor_scalar(out=sm[j][:, 1:2], in0=sm[j][:, 1:2], scalar1=1.0, scalar2=None,
                                                                   op0=ALU.max), r=[Bsm[j]], w=[Bsm[j]])
                        S.op("dve", lambda e, j=j: e.reciprocal(sm[j][:, 2:3], sm[j][:, 1:2]), r=[Bsm[j]], w=[Bsm[j]])
                        S.op("dve", lambda e, j=j, t=t, hd=hd: e.tensor_tensor(
                            out=sm[j][:, 3:4], in0=sm[j][:, 2:3], in1=Bq[:, t, hd:hd + 1], op=ALU.mult), r=[Bsm[j], Bg], w=[Bsm[j]])
                        S.op("dve", lambda e, pj=pj, j=j: e.tensor_scalar(
                            out=ho[j][:], in0=ps_n[pj][:, 0:128], scalar1=sm[j][:, 3:4], scalar2=None, op0=ALU.mult),
                            r=[Bps_n[pj], Bsm[j]], w=[Bho[j]])
                        S.dma("sp", self.hfb[d, t * 128:(t + 1) * 128, h * 128:(h + 1) * 128], ho[j][:], r=[Bho[j]])
                S.barrier()
            S.barrier()

    def mlstm_post(self, i):
        S = self.S
        with contextlib.ExitStack() as st:
            ng = self.sb(st, "mp_ng", [128, 512])
            Bw = S.buf("mp_w")
            S.dma("sp", ng[:], self.od_ng[i, :, :].to_broadcast([128, 512]), w=[Bw])
            eps_t = self.sb(st, "mp_eps", [128, 1])
            S.op("pool", lambda e: e.memset(eps_t[:], LN_EPS), w=[Bw])
            NB = 2
            hf = [self.sb(st, f"mp_hf{j}", [128, 512]) for j in range(NB)]
            hb = [self.sb(st, f"mp_hb{j}", [128, 512]) for j in range(NB)]
            og = [self.sb(st, f"mp_og{j}", [128, 512]) for j in range(NB)]
            hs = [self.sb(st, f"mp_hs{j}", [128, 512]) for j in range(NB)]
            yo = [self.sb(st, f"mp_yo{j}", [128, 512]) for j in range(NB)]
            yob = [self.sb(st, f"mp_yob{j}", [128, 512], BF16) for j in range(NB)]
            stt = [self.sb(st, f"mp_st{j}", [128, 4, 12]) for j in range(NB)]
            Bhf, Bhb, Bog, Bhs, Byo, Bst = (S.bufs(n, NB) for n in ("mp_hf", "mp_hb", "mp_og", "mp_hs", "mp_yo", "mp_st"))
            for t in range(NT):
                j = t % NB
                tok0 = t * 128
                S.dma("sp", hf[j][:], self.hfb[0, tok0:tok0 + 128, :], w=[Bhf[j]])
                S.dma("act", hb[j][:], self.hfb[1, tok0:tok0 + 128, :], w=[Bhb[j]])
                S.dma("pool", og[j][:], self.og[tok0:tok0 + 128, :], w=[Bog[j]])
                S.op("pool", lambda e, j=j: e.tensor_tensor(out=hs[j][:], in0=hf[j][:], in1=hb[j][:], op=ALU.add),
                     r=[Bhf[j], Bhb[j]], w=[Bhs[j]])
                S.op("pool", lambda e, j=j: e.tensor_tensor(out=og[j][:], in0=og[j][:], in1=ng[:], op=ALU.mult),
                     r=[Bog[j], Bw], w=[Bog[j]])
                for h in range(4):
                    S.op("dve", lambda e, j=j, h=h: e.bn_stats(stt[j][:, h, 0:6], hs[j][:, h * 128:(h + 1) * 128]),
                         r=[Bhs[j]], w=[Bst[j]])
                    S.op("dve", lambda e, j=j, h=h: e.bn_aggr(stt[j][:, h, 6:8], stt[j][:, h, 0:6]), r=[Bst[j]], w=[Bst[j]])
                S.op("act", lambda e, j=j: e.activation(out=stt[j][:, :, 8:9], in_=stt[j][:, :, 7:8], func=AF.Sqrt, scale=1.0,
                                                        bias=eps_t[:, 0:1]), r=[Bst[j], Bw], w=[Bst[j]])
                S.op("dve", lambda e, j=j: e.reciprocal(stt[j][:, :, 9:10], stt[j][:, :, 8:9]), r=[Bst[j]], w=[Bst[j]])
                for h in range(4):
                    S.op("dve", lambda e, j=j, h=h: e.tensor_scalar(
                        out=yo[j][:, h * 128:(h + 1) * 128], in0=hs[j][:, h * 128:(h + 1) * 128], scalar1=stt[j][:, h, 6:7],
                        scalar2=stt[j][:, h, 9:10], op0=ALU.subtract, op1=ALU.mult), r=[Bhs[j], Bst[j]], w=[Byo[j]])
                S.op("pool", lambda e, j=j: e.tensor_tensor(out=yob[j][:], in0=yo[j][:], in1=og[j][:], op=ALU.mult),
                     r=[Byo[j], Bog[j]], w=[Byo[j]])
                S.dma("sp", self.ybuf[tok0:tok0 + 128, 0:512], yob[j][:], r=[Byo[j]])
            S.barrier()

    def na_attention(self, i, last):
        S = self.S
        with contextlib.ExitStack() as st:
            QNe = self.sb(st, "na_Qe", [128, NTOK], BF16)
            QNo = self.sb(st, "na_Qo", [128, NTOK], BF16)
            KN = self.sb(st, "na_K", [128, NTOK], BF16)
            VN = self.sb(st, "na_V", [128, NT, 65], BF16)
            MB = self.sb(st, "na_MB", [128, NA_NVAR, 128], BF16)
            Bqk = S.buf("na_qk")
            Bv = S.buf("na_v")
            Bmb = S.buf("na_mb")
            stg = [self.sb(st, f"na_stg{j}", [128, 7, 128]) for j in range(2)]
            Bstg = S.bufs("na_stg", 2)
            NP = 3
            ptA = [self.sb(st, f"na_ptA{j}", [128, 512], BF16) for j in range(NP)]
            ptB = [self.sb(st, f"na_ptB{j}", [128, 384], BF16) for j in range(NP)]
            BptA, BptB = S.bufs("na_ptA", NP), S.bufs("na_ptB", NP)
            rr = [self.sb(st, f"na_rr{j}", [128, 2]) for j in range(NP)]
            to = [self.sb(st, f"na_to{j}", [128, 64], BF16) for j in range(NP)]
            Bto = S.bufs("na_to", NP)
            psA = [self.ps(st, f"na_psA{j}", [128, 512]) for j in range(2)]
            psB = [self.ps(st, f"na_psB{j}", [128, 512]) for j in range(2)]
            acc = [self.ps(st, f"na_acc{j}", [128, 512]) for j in range(2)]
            BpsA, BpsB, Bacc = S.bufs("na_psA", 2), S.bufs("na_psB", 2), S.bufs("na_acc", 2)
            it = 0
            qtiles = list(range(NT_LAT)) + ([] if (last and self.final_out) else [NT_LAT, NT_LAT + 1])
            if self.tile_filter is not None:
                qtiles = [t for t in qtiles if t in self.tile_filter]
            for h in range(8):
                lo = (h % 2) * 64
                if h == 0:
                    S.op("pool", lambda e: e.memset(QNe[64:128, :], 0.0), w=[Bqk])
                    S.op("pool", lambda e: e.memset(QNo[0:64, :], 0.0), w=[Bqk])
                if h % 2 == 0:
                    S.dma("sp", QNe[0:64, :], self.qda[h // 2, 0:64, :], w=[Bqk])
                    S.dma("sp", QNo[64:128, :], self.qda[h // 2, 64:128, :], w=[Bqk])
                    S.dma("sp", KN[:], self.kda[h // 2, :, :], w=[Bqk])
                QN = QNe if h % 2 == 0 else QNo
                S.dma("sp", VN[:], self.vm[h, :, :, :], w=[Bv])
                for v0 in range(0, NA_NVAR, 7):
                    n = min(7, NA_NVAR - v0)
                    sj = (v0 // 7) % 2
                    S.dma("sp", stg[sj][:, 0:n, :], self.od_nat[i, h, :, v0:v0 + n, :], w=[Bstg[sj]])
                    S.op("pool", lambda e, sj=sj, n=n, v0=v0: e.tensor_scalar(
                        out=MB[:, v0:v0 + n, :], in0=stg[sj][:, 0:n, :], scalar1=1.0 / NA_SCALE, scalar2=None, op0=ALU.mult),
                        r=[Bstg[sj]], w=[Bmb])
                for j in qtiles:
                    pj = it % 2
                    tj = it % NP
                    it += 1
                    qsl = slice(j * 128, (j + 1) * 128)
                    if j < NT_LAT:
                        slots = [(kt, var) for (kt, var) in NA_PLAN[j]] + [(NT_LAT, None), (NT_LAT + 1, None)]
                    else:
                        slots = [(NT_LAT, None), (NT_LAT + 1, None)]
                    nA = min(4, len(slots))
                    nB = len(slots) - nA
                    for si, (kt, var) in enumerate(slots):
                        if si < 4:
                            dst, Bd = psA[pj][:, si * 128:(si + 1) * 128], BpsA[pj]
                        else:
                            dst, Bd = psB[pj][:, (si - 4) * 128:(si - 3) * 128], BpsB[pj]
                        S.op("pe", lambda e, dst=dst, kt=kt, var=var, QN=QN: e.matmul(
                            dst, KN[:, kt * 128:(kt + 1) * 128], QN[:, qsl], start=True, stop=(var is None)),
                            r=[Bqk], w=[Bd])
                        if var is not None:
                            S.op("pe", lambda e, dst=dst, var=var: e.matmul(dst, self.ident_b[:], MB[:, var, :], start=False, stop=True),
                                 r=[Bmb, self.B_const], w=[Bd])
                    S.op("act", lambda e, pj=pj, tj=tj, nA=nA: e.activation(out=ptA[tj][:, 0:nA * 128], in_=psA[pj][:, 0:nA * 128],
                                                                            func=AF.Exp, scale=NA_SCALE), r=[BpsA[pj]], w=[BptA[tj]])
                    if nB > 0:
                        S.op("act", lambda e, pj=pj, tj=tj, nB=nB: e.activation(out=ptB[tj][:, 0:nB * 128], in_=psB[pj][:, 0:nB * 128],
                                                                                func=AF.Exp, scale=NA_SCALE), r=[BpsB[pj]], w=[BptB[tj]])
                    for si, (kt, var) in enumerate(slots):
                        if si < 4:
                            lhs, Bl = ptA[tj][:, si * 128:(si + 1) * 128], BptA[tj]
                        else:
                            lhs, Bl = ptB[tj][:, (si - 4) * 128:(si - 3) * 128], BptB[tj]
                        S.op("pe", lambda e, lhs=lhs, kt=kt, si=si, pj=pj: e.matmul(
                            acc[pj][:, 0:65], lhs, VN[:, kt, :], start=(si == 0), stop=(si == len(slots) - 1)),
                            r=[Bl, Bv], w=[Bacc[pj]])
                    S.op("dve", lambda e, pj=pj, tj=tj: e.reciprocal(rr[tj][:, 0:1], acc[pj][:, 64:65]), r=[Bacc[pj]], w=[Bto[tj]])
                    S.op("dve", lambda e, pj=pj, tj=tj: e.tensor_scalar(out=to[tj][:], in0=acc[pj][:, 0:64], scalar1=rr[tj][:, 0:1],
                                                                        scalar2=None, op0=ALU.mult), r=[Bacc[pj], Bto[tj]], w=[Bto[tj]])
                    S.dma("pool", self.ybuf[j * 128:(j + 1) * 128, 512 + h * 64:512 + (h + 1) * 64], to[tj][:], r=[Bto[tj]])
            S.barrier()


def _swap64(cols):
    return np.concatenate([cols[32:64], cols[0:32]])


def _rope_tables():
    t = np.arange(LAT)
    row = (t // GRID_W).astype(np.float32)
    col = (t % GRID_W).astype(np.float32)

    def tab(dim, nrows_pattern):
        n_freq = dim // 4
        freqs = (10000.0 ** (-np.arange(n_freq, dtype=np.float32) / n_freq)).astype(np.float32)
        ang = np.concatenate([row[:, None] * freqs, col[:, None] * freqs], axis=-1).astype(np.float32)
        cos = np.cos(ang).astype(np.float32).T
        sin = np.sin(ang).astype(np.float32).T
        half = dim // 2
        c = np.concatenate([cos, cos], 0)
        s = np.concatenate([-sin, sin], 0)
        c = np.concatenate([c, np.ones((dim, CTX), np.float32)], 1)
        s = np.concatenate([s, np.zeros((dim, CTX), np.float32)], 1)
        return c, s

    c64, s64 = tab(64, None)
    c32, s32 = tab(32, None)
    rope_da = np.stack([np.concatenate([c64, c64], 0), np.concatenate([s64, s64], 0)]).astype(np.float32)
    ml_c = np.ones((128, NTOK), np.float32)
    ml_s = np.zeros((128, NTOK), np.float32)
    ml_c[0:32] = c32
    ml_s[0:32] = s32
    ml_c[64:96] = c32
    ml_s[64:96] = s32
    rope_ml = np.stack([ml_c, ml_s]).astype(np.float32)
    return rope_da, rope_ml


def _even_layout(inp):
    ev_w_in, ev_b_in = inp["ev_w_in"], inp["ev_b_in"]
    o_q1, o_q2, o_k1, o_k2, o_v, o_cq, o_ckv, o_kr, o_g = 0, 256, 512, 768, 1024, 1536, 1792, 1920, 1952
    cols = []
    for (a, b) in ((o_q1, o_q2), (o_k1, o_k2)):
        main = []
        sw = []
        for h in range(4):
            c1 = np.arange(a + h * 64, a + (h + 1) * 64)
            c2 = np.arange(b + h * 64, b + (h + 1) * 64)
            main += [c1, c2]
            sw += [_swap64(c1), _swap64(c2)]
        cols += main + sw
    kr = np.arange(o_kr, o_kr + 32)
    cols += [kr, np.concatenate([kr[16:], kr[:16]])]
    cols += [np.arange(o_v, o_v + 512), np.arange(o_cq, o_cq + 384), np.arange(o_g, o_g + 1024)]
    perm = np.concatenate(cols)
    assert perm.shape[0] == EV_COLS
    ev_w = np.ascontiguousarray(ev_w_in[:, :, perm])
    bp = ev_b_in[:, perm]
    bcol = np.zeros((2, 128, 18), np.float32)
    for g in range(16):
        bcol[:, :, g] = bp[:, g * 128:(g + 1) * 128]
    bcol[:, 0:32, 16] = bp[:, EKR:EKR + 32]
    bcol[:, 0:32, 17] = bp[:, EKR + 32:EKR + 64]
    brow = np.ascontiguousarray(bp[:, EV_:EV_ + 1920])[:, None, :]
    wuq = inp["mla_w_uq"]
    swc = []
    for h in range(8):
        base = h * 96
        c = np.arange(base, base + 96)
        r = c[64:96]
        swc.append(np.concatenate([c[0:64], r[16:], r[:16]]))
    swc = np.concatenate(swc)
    ev_wuq = np.ascontiguousarray(np.concatenate([wuq, wuq[:, :, swc]], axis=2))
    wukv = inp["mla_w_ukv"]
    nope = np.concatenate([np.arange(h * 128, h * 128 + 64) for h in range(8)])
    vv = np.concatenate([np.arange(h * 128 + 64, h * 128 + 128) for h in range(8)])
    ev_wukv = np.ascontiguousarray(wukv[:, :, np.concatenate([nope, vv])])
    return dict(
        ev_w=ev_w, ev_bcol=bcol, ev_brow=np.ascontiguousarray(brow),
        ev_lam=np.ascontiguousarray(inp["da_lambda"].reshape(2, 1, 256)),
        ev_subg=np.ascontiguousarray(inp["da_subln_g"].reshape(2, 1, 128)),
        ev_qg=np.ascontiguousarray(inp["mla_q_norm_g"].reshape(2, 2, 128).transpose(0, 2, 1)),
        ev_kvg=np.ascontiguousarray(inp["mla_kv_norm_g"].reshape(2, 128, 1)),
        ev_wuq=ev_wuq, ev_wukv=ev_wukv, ev_wo=np.ascontiguousarray(inp["ev_w_out"]),
    )


def _odd_layout(inp):
    w, b = inp["od_w_in"], inp["od_b_in"]
    g0 = 2048
    gates = np.concatenate([g0 + j * 4 + np.arange(4) for j in (0, 2, 1, 3)])
    perm = np.concatenate([np.arange(0, 1024), np.arange(2064, 2576), np.arange(2576, 3088), np.arange(1024, 1536),
                           np.arange(1536, 2048), gates, np.arange(3088, 3600), np.arange(3600, 4624)])
    assert perm.shape[0] == OD_COLS
    od_w = np.ascontiguousarray(w[:, :, perm])
    bp = b[:, perm]
    bcol = np.ascontiguousarray(bp[:, 0:2048].reshape(2, 16, 128).transpose(0, 2, 1))
    brow = np.ascontiguousarray(bp[:, 2048:])[:, None, :]
    convw = np.ascontiguousarray(inp["ml_conv_w"].reshape(2, 5, 8, 128).transpose(0, 3, 2, 1))
    convb = np.ascontiguousarray(inp["ml_conv_b"].reshape(2, 8, 128).transpose(0, 2, 1))
    fb = np.ascontiguousarray(inp["ml_f_bias"].reshape(2, 1, 8))
    ng = np.ascontiguousarray(inp["ml_norm_g"].reshape(2, 1, 512))
    rpb = inp["na_rpb"]
    nat = np.full((2, 8, 128, NA_NVAR, 128), NEG, np.float32)
    k = np.arange(128)
    krl, kc = k // 64, k % 64
    q = np.arange(128)
    qrl, qc = q // 64, q % 64
    cs = np.clip(qc - NA_COLS // 2, 0, GRID_W - NA_COLS)
    for (dk, o0, o1), vid in NA_VARIANTS.items():
        rel = (2 * dk + krl[:, None]) - qrl[None, :]
        off = np.where(qrl[None, :] == 0, o0, o1)
        valid_r = (rel >= off) & (rel <= off + NA_ROWS - 1)
        valid_c = (kc[:, None] >= cs[None, :]) & (kc[:, None] <= cs[None, :] + NA_COLS - 1)
        valid = valid_r & valid_c
        ridx = np.clip(rel + NA_ROWS - 1, 0, 2 * NA_ROWS - 2)
        cidx = np.clip(kc[:, None] - qc[None, :] + NA_COLS - 1, 0, 2 * NA_COLS - 2)
        tab = rpb[:, :, ridx, cidx]
        nat[:, :, :, vid, :] = np.where(valid[None, None], tab, np.float32(NEG))
    tri = np.stack([np.triu(np.ones((128, 128), np.float32)), np.tril(np.ones((128, 128), np.float32)),
                    np.ones((128, 128), np.float32)], 1)
    return dict(od_w=od_w, od_bcol=bcol, od_brow=np.ascontiguousarray(brow), od_convw=convw, od_convb=convb, od_fb=fb,
                od_ng=ng, od_nat=nat, od_wo=np.ascontiguousarray(inp["od_w_out"]), tri=np.ascontiguousarray(tri))


def make_in_maps(inp, batches):
    inp = {k: np.asarray(v, dtype=np.float32) for k, v in inp.items()}
    rope_da, rope_ml = _rope_tables()
    common = dict(
        ada_w=inp["ada_w"], ada_b=inp["ada_b"], ln_g=inp["ln_g"], ln_b=inp["ln_b"],
        ident=np.eye(128, dtype=np.float32),
        sel=np.concatenate([np.stack([np.ones(128), np.zeros(128)]), np.stack([np.zeros(128), np.ones(128)])], 1).astype(np.float32),
        rope_da=rope_da, rope_ml=rope_ml,
    )
    common.update(_even_layout(inp))
    common.update(_odd_layout(inp))
    maps = []
    for b in batches:
        m = dict(common)
        m["x_in"] = np.ascontiguousarray(np.concatenate([inp["x"][b], inp["ctx"][b]], 0))
        cc = np.stack([inp["c"][b], inp["c_ctx"]], -1)
        m["cc"] = np.ascontiguousarray(cc.reshape(8, 128, 2).transpose(1, 0, 2))
        maps.append(m)
    return maps


_PROG_CACHE = {}


def kernel(**inputs):
    batches = [c // 2 for c in range(8)]
    in_maps = make_in_maps(inputs, batches)
    prog = Prog()
    nc = prog.build()
    res = run_bass_kernel_spmd(nc, in_maps, core_ids=list(range(8)))
    out = np.stack([res.results[2 * b]["y"] for b in range(4)], 0)
    return out.astype(np.float32)
```

```python
import contextlib
import math
import numpy as np
import concourse.bass as bass
import concourse.mybir as mybir
from concourse.bass_utils import run_bass_kernel_spmd

F32 = mybir.dt.float32
BF16 = mybir.dt.bfloat16
AF = mybir.ActivationFunctionType
ALU = mybir.AluOpType
AX = mybir.AxisListType

D = 1024
LAT = 8192
CTX = 256
NTOK = LAT + CTX
NT = NTOK // 128
NT_LAT = LAT // 128
DEPTH = 4
GRID_W = 64
ALPHA = (2.0 * DEPTH) ** 0.25
LN_EPS = 1e-5
RMS_EPS = 1e-6
DA_SCALE = 64 ** -0.5
MLA_SCALE = 96 ** -0.5
NA_SCALE = 64 ** -0.5
EV_COLS = 4032
EQ, EK, EKR, EV_, EC, EG = 0, 1024, 2048, 2112, 2624, 3008
OD_COLS = 4624
OQK, OQN, OKN, OV, OO, OGT, OVN, OG = 0, 1024, 1536, 2048, 2560, 3072, 3088, 3600
NA_ROWS, NA_COLS, GRID_H = 8, 16, 128
NEG = -30000.0


def na_plan():
    variants = {}
    plan = []
    for j in range(NT_LAT):
        r0, r1 = 2 * j, 2 * j + 1
        rs0 = min(max(r0 - NA_ROWS // 2, 0), GRID_H - NA_ROWS)
        rs1 = min(max(r1 - NA_ROWS // 2, 0), GRID_H - NA_ROWS)
        lst = []
        for kt in range(rs0 // 2, (rs1 + NA_ROWS - 1) // 2 + 1):
            key = (kt - j, rs0 - r0, rs1 - r1)
            if key not in variants:
                variants[key] = len(variants)
            lst.append((kt, variants[key]))
        plan.append(lst)
    return plan, variants


NA_PLAN, NA_VARIANTS = na_plan()
NA_NVAR = len(NA_VARIANTS)


class Tok:
    __slots__ = ("sem", "key", "val", "eng")

    def __init__(self, sem, key, val, eng):
        self.sem, self.key, self.val, self.eng = sem, key, val, eng


class Buf:
    __slots__ = ("name", "last_w", "readers", "sem", "key", "cnt")

    def __init__(self, name):
        self.name = name
        self.last_w = None
        self.readers = {}
        self.sem = None
        self.key = None
        self.cnt = 0


class Eng:
    def __init__(self, name, h, sem):
        self.name, self.h, self.sem = name, h, sem
        self.key = "E_" + name
        self.cnt = 0
        self.waited = {}


class Sync:
    def __init__(self, nc, stack):
        self.nc = nc
        self.stack = stack
        self.E = {}
        for name, h in (("pe", nc.tensor), ("act", nc.scalar), ("dve", nc.vector),
                        ("pool", nc.gpsimd), ("sp", nc.sync)):
            sem = stack.enter_context(nc.semaphore("s_" + name))
            self.E[name] = Eng(name, h, sem)
        self.dma_bufs = []
        self.free_sems = []
        self.replica_groups = [[0, 1], [2, 3], [4, 5], [6, 7]]
        self.nsem = 0
        self.nwait = 0
        self.nins = 0

    def buf(self, name):
        return Buf(name)

    def bufs(self, name, n):
        return [Buf(f"{name}{i}") for i in range(n)]

    def _deps(self, eng, r, w):
        raw = []
        oth = []
        for b in r:
            if b.last_w is not None:
                raw.append(b.last_w)
        for b in w:
            if b.last_w is not None:
                oth.append(b.last_w)
            oth.extend(b.readers.values())
        toks = []
        for t in raw:
            if t.eng is eng and eng.name == "pe":
                continue
            toks.append(t)
        for t in oth:
            if t.eng is eng:
                continue
            toks.append(t)
        return toks

    def _wait(self, eng, toks):
        for t in toks:
            if eng.waited.get(t.key, 0) >= t.val:
                continue
            eng.h.wait_ge(t.sem, t.val)
            eng.waited[t.key] = t.val
            self.nwait += 1

    def op(self, en, fn, r=(), w=()):
        eng = self.E[en]
        self._wait(eng, self._deps(eng, r, w))
        ins = fn(eng.h)
        ins.then_inc(eng.sem, 1)
        eng.cnt += 1
        self.nins += 1
        tok = Tok(eng.sem, eng.key, eng.cnt, eng)
        for b in r:
            b.readers[tok.key] = tok
        for b in w:
            b.last_w = tok
            b.readers = {}
        return tok

    def dma(self, q, out, in_, r=(), w=(), sb=None):
        eng = self.E[q]
        self._wait(eng, self._deps(eng, r, w))
        if sb is None:
            sb = w[0] if w else r[0]
        if sb.sem is None:
            if self.free_sems:
                sb.sem, sb.key, sb.cnt = self.free_sems.pop()
            else:
                sb.sem = self.stack.enter_context(self.nc.semaphore(f"d{self.nsem}"))
                sb.key = f"D{self.nsem}"
                sb.cnt = 0
                self.nsem += 1
            self.dma_bufs.append(sb)
        ins = eng.h.dma_start(out=out, in_=in_)
        ins.then_inc(sb.sem, 16)
        sb.cnt += 16
        self.nins += 1
        tok = Tok(sb.sem, sb.key, sb.cnt, None)
        for b in r:
            b.readers[tok.key] = tok
        for b in w:
            b.last_w = tok
            b.readers = {}
        return tok

    def allgather_pairs(self, src_ts, dst_ts):
        self.barrier()
        eng = self.E["pool"]
        if getattr(self, "cc_sem", None) is None:
            self.cc_sem = self.stack.enter_context(self.nc.semaphore("cc_sem"))
            self.cc_cnt = 0
        for src_t, dst_t in zip(src_ts, dst_ts):
            ins = eng.h.collective_compute("AllGather", ALU.bypass, replica_groups=self.replica_groups,
                                           ins=[src_t.ap().opt()], outs=[dst_t.ap().opt()])
            ins.then_inc(self.cc_sem)
            self.cc_cnt += 1
            self.nins += 1
        tok = Tok(self.cc_sem, "CC", self.cc_cnt, None)
        for e in self.E.values():
            self._wait(e, [tok])

    def barrier(self, engines=("pe", "act", "dve", "pool", "sp")):
        toks = [Tok(e.sem, e.key, e.cnt, e) for e in self.E.values() if e.cnt > 0]
        toks += [Tok(b.sem, b.key, b.cnt, None) for b in self.dma_bufs if b.cnt > 0]
        for en in engines:
            eng = self.E[en]
            self._wait(eng, [t for t in toks if t.eng is not eng])
        for b in self.dma_bufs:
            self.free_sems.append((b.sem, b.key, b.cnt))
            b.sem = None
            b.last_w = None
            b.readers = {}
        self.dma_bufs = []


class Prog:
    def __init__(self, layers=(0, 1, 2, 3), final_out=True, qb_filter=None, tile_filter=None, n_cores=8):
        self.n_cores = n_cores
        self.layers = tuple(layers)
        self.qb_filter = qb_filter
        self.tile_filter = tile_filter
        self.nc = bass.Bass("TRN2", target_bir_lowering=False)
        self.final_out = final_out

    def din(self, name, shape, dt=F32):
        return self.nc.dram_tensor(name, list(shape), dt, kind="ExternalInput").ap()

    def dout(self, name, shape, dt=F32):
        return self.nc.dram_tensor(name, list(shape), dt, kind="ExternalOutput").ap()

    def dscr(self, name, shape, dt=F32):
        return self.nc.dram_tensor(name, list(shape), dt, kind="Internal").ap()

    def sb(self, st, name, shape, dt=F32):
        self._uid = getattr(self, "_uid", 0) + 1
        return st.enter_context(self.nc.sbuf_tensor(f"s{self._uid}_{name}", list(shape), dt))

    def ps(self, st, name, shape, dt=F32):
        self._uid = getattr(self, "_uid", 0) + 1
        return st.enter_context(self.nc.psum_tensor(f"p{self._uid}_{name}", list(shape), dt))

    def build(self):
        nc = self.nc
        with contextlib.ExitStack() as st:
            self.S = Sync(nc, st)
            self.S.replica_groups = [[2 * k, 2 * k + 1] for k in range(self.n_cores // 2)]
            self.declare()
            self.setup_consts(st)
            nl = len(self.layers)
            for li, l in enumerate(self.layers):
                src = self.x_in if li == 0 else self.xbuf[(li - 1) % 2]
                last = li == nl - 1
                dst = self.y_out if last else self.xbuf[li % 2]
                if l % 2 == 0:
                    self.even_layer(l, src, dst, last)
                else:
                    self.odd_layer(l, src, dst, last)
            self.S.barrier()
        return nc

    def declare(self):
        self.x_in = self.din("x_in", [NTOK, D])
        self.cc = self.din("cc", [128, 8, 2])
        self.ada_w = self.din("ada_w", [DEPTH, D, 3 * D])
        self.ada_b = self.din("ada_b", [DEPTH, 3 * D])
        self.ln_g = self.din("ln_g", [DEPTH, D])
        self.ln_b = self.din("ln_b", [DEPTH, D])
        self.ident_in = self.din("ident", [128, 128])
        self.sel_in = self.din("sel", [2, 256])
        self.ev_w = self.din("ev_w", [2, D, EV_COLS])
        self.ev_bcol = self.din("ev_bcol", [2, 128, 18])
        self.ev_brow = self.din("ev_brow", [2, 1, 1920])
        self.ev_lam = self.din("ev_lam", [2, 1, 256])
        self.ev_subg = self.din("ev_subg", [2, 1, 128])
        self.ev_qg = self.din("ev_qg", [2, 128, 2])
        self.ev_kvg = self.din("ev_kvg", [2, 128, 1])
        self.ev_wuq = self.din("ev_wuq", [2, 256, 1536])
        self.ev_wukv = self.din("ev_wukv", [2, 128, 1024])
        self.ev_wo = self.din("ev_wo", [2, D, D])
        self.rope_da = self.din("rope_da", [2, 128, NTOK])
        self.rope_ml = self.din("rope_ml", [2, 128, NTOK])
        self.od_w = self.din("od_w", [2, D, OD_COLS])
        self.od_bcol = self.din("od_bcol", [2, 128, 16])
        self.od_brow = self.din("od_brow", [2, 1, 2576])
        self.od_convw = self.din("od_convw", [2, 128, 8, 5])
        self.od_convb = self.din("od_convb", [2, 128, 8])
        self.od_fb = self.din("od_fb", [2, 1, 8])
        self.od_ng = self.din("od_ng", [2, 1, 512])
        self.od_nat = self.din("od_nat", [2, 8, 128, NA_NVAR, 128])
        self.od_wo = self.din("od_wo", [2, D, D])
        self.tri = self.din("tri", [128, 3, 128])
        self.qkpre = self.dscr("qkpre", [8, 128, NTOK])
        self.mq = self.dscr("mq", [4, 128, NTOK], BF16)
        self.mk = self.dscr("mk", [4, 128, NTOK], BF16)
        self.og = self.dscr("og", [NTOK, 512])
        self.gates = self.dscr("gates", [128, NT, 16])
        self.hfb = self.dscr("hfb", [2, NTOK, 512])
        n_last_tok = LAT if self.final_out else NTOK
        self.y_out = self.dout("y", [n_last_tok, D])
        self.xbuf = [self.dscr("xbuf0", [NTOK, D]), self.dscr("xbuf1", [NTOK, D])]
        self.qda = self.dscr("qda", [4, 128, NTOK], BF16)
        self.kda = self.dscr("kda", [4, 128, NTOK], BF16)
        self.vda = self.dscr("vda", [4, 128, NT, 129], BF16)
        self.qm = self.dscr("qm", [8, 96, NTOK], BF16)
        self.kmn = self.dscr("kmn", [8, 64, NTOK], BF16)
        self.krt = self.dscr("krt", [32, NTOK], BF16)
        self.vm = self.dscr("vm", [8, 128, NT, 65], BF16)
        self.gate = self.dscr("gate", [NTOK, D], BF16)
        self.ybuf = self.dscr("ybuf", [NTOK, D], BF16)
        self.psel = self.din("psel", [128, 2])
        self.yown_t = [self.nc.dram_tensor(f"yown{k}", [1024, D], BF16) for k in range(4)]
        self.ygath_t = [self.nc.dram_tensor(f"ygath{k}", [2048, D], BF16) for k in range(4)]

    def setup_consts(self, st):
        S = self.S
        self.ident_f = self.sb(st, "ident_f", [128, 128])
        self.ident_b = self.sb(st, "ident_b", [128, 128], BF16)
        self.sel = self.sb(st, "sel", [2, 256])
        self.ccs = self.sb(st, "ccs", [128, 8, 2])
        self.zeros_b = self.sb(st, "zeros_b", [128, 128], BF16)
        self.ones_b = self.sb(st, "ones_b", [1, 128], BF16)
        self.zeros_w = self.sb(st, "zeros_w", [128, 512], BF16)
        self.B_const = S.buf("consts")
        b = self.B_const
        S.dma("sp", self.ident_f[:], self.ident_in[:, :], w=[b])
        S.dma("sp", self.sel[:], self.sel_in[:, :], w=[b])
        S.dma("sp", self.ccs[:], self.cc[:, :, :], w=[b])
        self.pselt = self.sb(st, "pselt", [128, 2])
        S.dma("sp", self.pselt[:], self.psel[:, :], w=[b])
        S.op("dve", lambda e: e.tensor_copy(self.ident_b[:], self.ident_f[:]), r=[b], w=[b])
        S.op("dve", lambda e: e.memset(self.zeros_b[:], 0.0), w=[b])
        S.op("dve", lambda e: e.memset(self.ones_b[:], 1.0), w=[b])
        S.op("dve", lambda e: e.memset(self.zeros_w[:], 0.0), w=[b])
        S.op("act", lambda e: e.activation(out=self.ccs[:], in_=self.ccs[:], func=AF.Silu), r=[b], w=[b])
        self.sc1 = self.sb(st, "sc1", [128, 8, 2])
        self.sh = self.sb(st, "sh", [128, 8, 2])
        self.g_l = self.sb(st, "g_l", [128, D])
        self.g_c = self.sb(st, "g_c", [128, D])
        self.lng = self.sb(st, "lng", [128, D])
        self.lnb = self.sb(st, "lnb", [128, D])
        self.B_mod = S.buf("mod")

    def adaln(self, l):
        S, nc = self.S, self.nc
        S.barrier()
        with contextlib.ExitStack() as st:
            wt = [self.sb(st, f"adaw{i}", [128, 3 * D]) for i in range(2)]
            Bw = S.bufs("adaw", 2)
            mrow = self.sb(st, "mrow", [2, 3 * D])
            brow = self.sb(st, "adab", [2, 3 * D])
            Bm = S.buf("mrow")
            Bb = S.buf("adab")
            pm = [self.ps(st, f"pm{i}", [128, 512]) for i in range(8)]
            Bp = S.bufs("pm", 8)
            S.dma("sp", brow[:], self.ada_b[l:l + 1, :].to_broadcast([2, 3 * D]), w=[Bb])
            S.dma("sp", self.lng[:], self.ln_g[l:l + 1, :].to_broadcast([128, D]), w=[self.B_mod])
            S.dma("sp", self.lnb[:], self.ln_b[l:l + 1, :].to_broadcast([128, D]), w=[self.B_mod])
            for k in range(8):
                S.dma("sp", wt[k % 2][:], self.ada_w[l, k * 128:(k + 1) * 128, :], w=[Bw[k % 2]])
                for n in range(6):
                    S.op("pe", lambda e, k=k, n=n: e.matmul(
                        pm[n][0:2, :], self.ccs[:, k, :], wt[k % 2][:, n * 512:(n + 1) * 512],
                        start=(k == 0), stop=(k == 7)), r=[Bw[k % 2], self.B_const], w=[Bp[n]])
            for n in range(6):
                S.op("dve", lambda e, n=n: e.tensor_tensor(
                    out=mrow[:, n * 512:(n + 1) * 512], in0=pm[n][0:2, :], in1=brow[:, n * 512:(n + 1) * 512],
                    op=ALU.add), r=[Bp[n], Bb], w=[Bm])
            for j in range(8):
                S.op("pe", lambda e, j=j: e.transpose(
                    pm[6][:, 2 * j:2 * j + 2], mrow[0:2, j * 128:(j + 1) * 128], self.ident_f[0:2, 0:2]),
                    r=[Bm, self.B_const], w=[Bp[6]])
                S.op("pe", lambda e, j=j: e.transpose(
                    pm[7][:, 2 * j:2 * j + 2], mrow[0:2, D + j * 128:D + (j + 1) * 128], self.ident_f[0:2, 0:2]),
                    r=[Bm, self.B_const], w=[Bp[7]])
            S.op("dve", lambda e: e.tensor_copy(self.sh[:].rearrange("p a b -> p (a b)"), pm[6][:, 0:16]),
                 r=[Bp[6]], w=[self.B_mod])
            S.op("dve", lambda e: e.tensor_scalar(
                out=self.sc1[:].rearrange("p a b -> p (a b)"), in0=pm[7][:, 0:16], scalar1=1.0, scalar2=None,
                op0=ALU.add), r=[Bp[7]], w=[self.B_mod])
            for n in range(2):
                S.op("pe", lambda e, n=n: e.matmul(
                    pm[n][:, :], self.sel[:, 0:128], mrow[0:2, 2 * D + n * 512:2 * D + (n + 1) * 512],
                    start=True, stop=True), r=[Bm, self.B_const], w=[Bp[n]])
                S.op("pe", lambda e, n=n: e.matmul(
                    pm[2 + n][:, :], self.sel[:, 128:256], mrow[0:2, 2 * D + n * 512:2 * D + (n + 1) * 512],
                    start=True, stop=True), r=[Bm, self.B_const], w=[Bp[2 + n]])
                S.op("dve", lambda e, n=n: e.tensor_copy(self.g_l[:, n * 512:(n + 1) * 512], pm[n][:, :]),
                     r=[Bp[n]], w=[self.B_mod])
                S.op("dve", lambda e, n=n: e.tensor_copy(self.g_c[:, n * 512:(n + 1) * 512], pm[2 + n][:, :]),
                     r=[Bp[2 + n]], w=[self.B_mod])
            S.barrier()

    def load_cast(self, st_w, dst_fn, src_fn, nparts, ncols, pieces, Bdst, scale_fn=None, name="lc"):
        S = self.S
        with contextlib.ExitStack() as st:
            stg = [self.sb(st, f"{name}_stg{i}", [nparts, ncols]) for i in range(2)]
            Bs = S.bufs(name + "_stg", 2)
            for i, (dst, src, sc) in enumerate(pieces):
                j = i % 2
                n = src.shape[-1]
                S.dma("sp", stg[j][:, 0:n], src, w=[Bs[j]])
                en = "dve" if i % 2 == 0 else "pool"
                if sc is None:
                    S.op(en, lambda e, dst=dst, j=j, n=n: e.tensor_copy(dst, stg[j][:, 0:n]), r=[Bs[j]], w=[Bdst])
                else:
                    S.op(en, lambda e, dst=dst, j=j, n=n, sc=sc: e.tensor_scalar(
                        out=dst, in0=stg[j][:, 0:n], scalar1=sc, scalar2=None, op0=ALU.mult),
                        r=[Bs[j], Bdst], w=[Bdst])
            S.barrier()

    def even_layer(self, l, src, dst, last):
        i = l // 2
        self.adaln(l)
        self.even_project(i, src)
        self.da_attention(i, l)
        self.mla_attention(i)
        self.S.allgather_pairs(self.yown_t, self.ygath_t)
        self.out_stage(self.ev_wo[i], src, dst, last, ysplit=True)

    def even_project(self, i, src):
        S, nc = self.S, self.nc
        with contextlib.ExitStack() as st:
            wb = self.sb(st, "ev_wb", [128, 8, EV_COLS], BF16)
            wuq = self.sb(st, "ev_wuqb", [128, 2, 1536], BF16)
            wukv = self.sb(st, "ev_wukvb", [128, 1024], BF16)
            bcol = self.sb(st, "ev_bcol", [128, 18])
            brow_f = self.sb(st, "ev_brow_f", [1, 1920])
            brow = self.sb(st, "ev_brow", [1, 1920], BF16)
            qg = self.sb(st, "ev_qg", [128, 2])
            kvg = self.sb(st, "ev_kvg", [128, 1])
            Bw = S.buf("ev_w")
            S.dma("sp", bcol[:], self.ev_bcol[i, :, :], w=[Bw])
            S.dma("sp", brow_f[:], self.ev_brow[i, :, :], w=[Bw])
            S.dma("sp", qg[:], self.ev_qg[i, :, :], w=[Bw])
            S.dma("sp", kvg[:], self.ev_kvg[i, :, :], w=[Bw])
            S.op("dve", lambda e: e.tensor_copy(brow[:], brow_f[:]), r=[Bw], w=[Bw])
            pieces = []
            for k in range(8):
                for c in range(4):
                    pieces.append((wb[:, k, c * 1008:(c + 1) * 1008],
                                   self.ev_w[i, k * 128:(k + 1) * 128, c * 1008:(c + 1) * 1008], None))
            self.load_cast(st, None, None, 128, 1536, pieces, Bw, name="evw")
            pieces = [(wuq[:, rc, :], self.ev_wuq[i, rc * 128:(rc + 1) * 128, :], qg[:, rc:rc + 1]) for rc in range(2)]
            pieces.append((wukv[:, :], self.ev_wukv[i, :, :], kvg[:, 0:1]))
            self.load_cast(st, None, None, 128, 1536, pieces, Bw, name="evw2")

            NXS = 3
            xt = [self.sb(st, f"xt{j}", [128, D]) for j in range(NXS)]
            Bx = S.bufs("xt", NXS)
            hT = [self.sb(st, f"hT{j}", [128, 8, 512], BF16) for j in range(2)]
            Bh = S.bufs("hT", 2)
            cosd = [self.sb(st, f"cosd{j}", [128, 512]) for j in range(2)]
            sind = [self.sb(st, f"sind{j}", [128, 512]) for j in range(2)]
            cosm = [self.sb(st, f"cosm{j}", [128, 512]) for j in range(2)]
            sinm = [self.sb(st, f"sinm{j}", [128, 512]) for j in range(2)]
            Brt = S.bufs("ropet", 2)
            t1 = [self.sb(st, f"t1_{j}", [128, 512]) for j in range(2)]
            t2 = [self.sb(st, f"t2_{j}", [128, 512]) for j in range(2)]
            Bt1 = S.bufs("t1", 2)
            Bt2 = S.bufs("t2", 2)
            fo = [self.sb(st, f"fo{j}", [128, 512], BF16) for j in range(4)]
            Bfo = S.bufs("fo", 4)
            va = [self.sb(st, f"va{j}", [128, 4, 129], BF16) for j in range(2)]
            Bva = S.bufs("va", 2)
            vma = [self.sb(st, f"vma{j}", [128, 8, 65], BF16) for j in range(2)]
            Bvma = S.bufs("vma", 2)
            cq = [self.sb(st, f"cq{j}", [128, 384]) for j in range(2)]
            Bcq = S.bufs("cq", 2)
            cqn = [self.sb(st, f"cqn{j}", [128, 384], BF16) for j in range(2)]
            Bcqn = S.bufs("cqn", 2)
            stat = [self.sb(st, f"stat{j}", [128, 8]) for j in range(2)]
            Bstat = S.bufs("stat", 2)
            junk = self.sb(st, "junk", [128, 256])
            Bjunk = S.buf("junk")
            cT = [self.sb(st, f"cT{j}", [128, 3, 512], BF16) for j in range(2)]
            BcT = S.bufs("cT", 2)
            gt = [self.sb(st, f"gt{j}", [128, D], BF16) for j in range(2)]
            Bgt = S.bufs("gt", 2)
            pp = [self.ps(st, f"pp{j}", [128, 512]) for j in range(7)]
            ppb = self.ps(st, "ppb", [128, 1024], BF16)
            Bpp = S.bufs("pp", 7)
            Bppb = S.buf("ppb")
            for j in range(2):
                S.op("pool", lambda e, j=j: e.memset(va[j][:], 1.0), w=[Bva[j]])
                S.op("pool", lambda e, j=j: e.memset(vma[j][:], 1.0), w=[Bvma[j]])
            eps_t = self.sb(st, "eps_t", [128, 1])
            S.op("pool", lambda e: e.memset(eps_t[:], RMS_EPS), w=[Bw])

            self._pp_i = 0

            def next_pp():
                j = self._pp_i % 7
                self._pp_i += 1
                return pp[j], Bpp[j]

            nblk = (NTOK + 511) // 512
            xi = 0
            foi = 0
            evac = 0
            for blk in range(nblk):
                t0 = blk * 512
                ntok = min(512, NTOK - t0)
                nsub = ntok // 128
                m = 0 if t0 < LAT else 1
                hb = blk % 2
                S.dma("sp", cosd[hb][:, :ntok], self.rope_da[0, :, t0:t0 + ntok], w=[Brt[hb]])
                S.dma("sp", sind[hb][:, :ntok], self.rope_da[1, :, t0:t0 + ntok], w=[Brt[hb]])
                S.dma("sp", cosm[hb][:, :ntok], self.rope_ml[0, :, t0:t0 + ntok], w=[Brt[hb]])
                S.dma("sp", sinm[hb][:, :ntok], self.rope_ml[1, :, t0:t0 + ntok], w=[Brt[hb]])
                for s in range(nsub):
                    xj = xi % NXS
                    xi += 1
                    S.dma("sp", xt[xj][:], src[t0 + s * 128:t0 + (s + 1) * 128, :], w=[Bx[xj]])
                    for half in range(2):
                        p, Bp = next_pp()
                        for q in range(4):
                            dc = half * 4 + q
                            S.op("pe", lambda e, p=p, q=q, dc=dc, xj=xj: e.transpose(
                                p[:, q * 128:(q + 1) * 128], xt[xj][:, dc * 128:(dc + 1) * 128], self.ident_f[:]),
                                r=[Bx[xj], self.B_const], w=[Bp])
                        for q in range(4):
                            dc = half * 4 + q
                            if evac % 2 == 0:
                                S.op("act", lambda e, p=p, q=q, dc=dc, s=s, m=m: e.activation(
                                    out=hT[hb][:, dc, s * 128:(s + 1) * 128], in_=p[:, q * 128:(q + 1) * 128],
                                    func=AF.Identity, scale=self.sc1[:, dc, m:m + 1], bias=self.sh[:, dc, m:m + 1]),
                                    r=[Bp, self.B_mod], w=[Bh[hb]])
                            else:
                                S.op("dve", lambda e, p=p, q=q, dc=dc, s=s, m=m: e.tensor_scalar(
                                    out=hT[hb][:, dc, s * 128:(s + 1) * 128], in0=p[:, q * 128:(q + 1) * 128],
                                    scalar1=self.sc1[:, dc, m:m + 1], scalar2=self.sh[:, dc, m:m + 1],
                                    op0=ALU.mult, op1=ALU.add), r=[Bp, self.B_mod], w=[Bh[hb]])
                            evac += 1
                for grp, base, dstT, bofs in ((0, EQ, self.qda, 0), (1, EK, self.kda, 8)):
                    for h in range(4):
                        pa, Ba = next_pp()
                        pb, Bb = next_pp()
                        for k in range(8):
                            S.op("pe", lambda e, pa=pa, k=k, c0=base + h * 128: e.matmul(
                                pa[:, :ntok], wb[:, k, c0:c0 + 128], hT[hb][:, k, :ntok], start=(k == 0), stop=(k == 7)),
                                r=[Bw, Bh[hb]], w=[Ba])
                        for k in range(8):
                            S.op("pe", lambda e, pb=pb, k=k, c0=base + 512 + h * 128: e.matmul(
                                pb[:, :ntok], wb[:, k, c0:c0 + 128], hT[hb][:, k, :ntok], start=(k == 0), stop=(k == 7)),
                                r=[Bw, Bh[hb]], w=[Bb])
                        tj = (grp * 4 + h) % 2
                        S.op("dve", lambda e, pa=pa, tj=tj, c=bofs + h: e.scalar_tensor_tensor(
                            out=t1[tj][:, :ntok], in0=pa[:, :ntok], scalar=bcol[:, c:c + 1], in1=cosd[hb][:, :ntok],
                            op0=ALU.add, op1=ALU.mult), r=[Ba, Brt[hb], Bw], w=[Bt1[tj]])
                        S.op("dve", lambda e, pb=pb, tj=tj, c=bofs + 4 + h: e.scalar_tensor_tensor(
                            out=t2[tj][:, :ntok], in0=pb[:, :ntok], scalar=bcol[:, c:c + 1], in1=sind[hb][:, :ntok],
                            op0=ALU.add, op1=ALU.mult), r=[Bb, Brt[hb], Bw], w=[Bt2[tj]])
                        fj = foi % 4
                        foi += 1
                        S.op("pool", lambda e, tj=tj, fj=fj: e.tensor_tensor(
                            out=fo[fj][:, :ntok], in0=t1[tj][:, :ntok], in1=t2[tj][:, :ntok], op=ALU.add),
                            r=[Bt1[tj], Bt2[tj]], w=[Bfo[fj]])
                        S.dma("pool", dstT[h, :, t0:t0 + ntok], fo[fj][:, :ntok], r=[Bfo[fj]])
                pa, Ba = next_pp()
                pb, Bb = next_pp()
                for k in range(8):
                    S.op("pe", lambda e, pa=pa, k=k: e.matmul(
                        pa[0:32, :ntok], wb[:, k, EKR:EKR + 32], hT[hb][:, k, :ntok], start=(k == 0), stop=(k == 7)),
                        r=[Bw, Bh[hb]], w=[Ba])
                for k in range(8):
                    S.op("pe", lambda e, pb=pb, k=k: e.matmul(
                        pb[0:32, :ntok], wb[:, k, EKR + 32:EKR + 64], hT[hb][:, k, :ntok], start=(k == 0), stop=(k == 7)),
                        r=[Bw, Bh[hb]], w=[Bb])
                tj = 0
                S.op("dve", lambda e, pa=pa: e.scalar_tensor_tensor(
                    out=t1[tj][0:32, :ntok], in0=pa[0:32, :ntok], scalar=bcol[0:32, 16:17], in1=cosm[hb][0:32, :ntok],
                    op0=ALU.add, op1=ALU.mult), r=[Ba, Brt[hb], Bw], w=[Bt1[tj]])
                S.op("dve", lambda e, pb=pb: e.scalar_tensor_tensor(
                    out=t2[tj][0:32, :ntok], in0=pb[0:32, :ntok], scalar=bcol[0:32, 17:18], in1=sinm[hb][0:32, :ntok],
                    op0=ALU.add, op1=ALU.mult), r=[Bb, Brt[hb], Bw], w=[Bt2[tj]])
                fj = foi % 4
                foi += 1
                S.op("pool", lambda e, fj=fj: e.tensor_tensor(
                    out=fo[fj][0:32, :ntok], in0=t1[tj][0:32, :ntok], in1=t2[tj][0:32, :ntok], op=ALU.add),
                    r=[Bt1[tj], Bt2[tj]], w=[Bfo[fj]])
                S.dma("pool", self.krt[:, t0:t0 + ntok], fo[fj][0:32, :ntok], r=[Bfo[fj]])
                cb = blk % 2
                for s in range(nsub):
                    tt = (t0 // 128) + s
                    tok0 = t0 + s * 128
                    p, Bp = next_pp()
                    for k in range(8):
                        S.op("pe", lambda e, p=p, k=k, s=s: e.matmul(
                            p[:, :], hT[hb][:, k, s * 128:(s + 1) * 128], wb[:, k, EV_:EV_ + 512], start=(k == 0), stop=False),
                            r=[Bw, Bh[hb]], w=[Bp])
                    S.op("pe", lambda e, p=p: e.matmul(p[:, :], self.ones_b[0:1, :], brow[0:1, 0:512], start=False, stop=True),
                         r=[Bw, self.B_const], w=[Bp])
                    vj = tt % 2
                    S.op("act", lambda e, p=p, vj=vj: e.activation(
                        out=va[vj][:, :, 0:128], in_=p[:, :].rearrange("p (h c) -> p h c", h=4), func=AF.Copy),
                        r=[Bp], w=[Bva[vj]])
                    S.dma("pool", self.vda[:, :, tt, :].rearrange("h p c -> p h c"), va[vj][:], r=[Bva[vj]])
                    p, Bp = next_pp()
                    for k in range(8):
                        S.op("pe", lambda e, p=p, k=k, s=s: e.matmul(
                            p[:, 0:384], hT[hb][:, k, s * 128:(s + 1) * 128], wb[:, k, EC:EC + 384], start=(k == 0), stop=False),
                            r=[Bw, Bh[hb]], w=[Bp])
                    S.op("pe", lambda e, p=p: e.matmul(p[:, 0:384], self.ones_b[0:1, :], brow[0:1, 512:896], start=False, stop=True),
                         r=[Bw, self.B_const], w=[Bp])
                    cj = tt % 2
                    S.op("act", lambda e, p=p, cj=cj: e.activation(out=cq[cj][:], in_=p[:, 0:384], func=AF.Copy),
                         r=[Bp], w=[Bcq[cj]])
                    S.op("act", lambda e, cj=cj: e.activation(
                        out=junk[:, 0:256], in_=cq[cj][:, 0:256], func=AF.Square, accum_out=stat[cj][:, 0:1]),
                        r=[Bcq[cj]], w=[Bjunk, Bstat[cj]])
                    S.op("act", lambda e, cj=cj: e.activation(
                        out=junk[:, 0:128], in_=cq[cj][:, 256:384], func=AF.Square, accum_out=stat[cj][:, 1:2]),
                        r=[Bcq[cj]], w=[Bjunk, Bstat[cj]])
                    S.op("act", lambda e, cj=cj: e.activation(out=stat[cj][:, 2:3], in_=stat[cj][:, 0:1], func=AF.Sqrt,
                                                              scale=1.0 / 256, bias=eps_t[:, 0:1]), r=[Bstat[cj], Bw], w=[Bstat[cj]])
                    S.op("act", lambda e, cj=cj: e.activation(out=stat[cj][:, 3:4], in_=stat[cj][:, 1:2], func=AF.Sqrt,
                                                              scale=1.0 / 128, bias=eps_t[:, 0:1]), r=[Bstat[cj], Bw], w=[Bstat[cj]])
                    S.op("dve", lambda e, cj=cj: e.reciprocal(stat[cj][:, 4:6], stat[cj][:, 2:4]), r=[Bstat[cj]], w=[Bstat[cj]])
                    S.op("dve", lambda e, cj=cj: e.tensor_scalar(
                        out=cqn[cj][:, 0:256], in0=cq[cj][:, 0:256], scalar1=stat[cj][:, 4:5], scalar2=None, op0=ALU.mult),
                        r=[Bcq[cj], Bstat[cj]], w=[Bcqn[cj]])
                    S.op("pool", lambda e, cj=cj: e.tensor_scalar(
                        out=cqn[cj][:, 256:384], in0=cq[cj][:, 256:384], scalar1=stat[cj][:, 5:6], scalar2=None, op0=ALU.mult),
                        r=[Bcq[cj], Bstat[cj]], w=[Bcqn[cj]])
                    for rc in range(3):
                        S.op("pe", lambda e, rc=rc, cj=cj: e.transpose(
                            ppb[:, rc * 128:(rc + 1) * 128], cqn[cj][:, rc * 128:(rc + 1) * 128], self.ident_b[:]),
                            r=[Bcqn[cj], self.B_const], w=[Bppb])
                    S.op("dve", lambda e, s=s: e.tensor_copy(
                        cT[cb][:, :, s * 128:(s + 1) * 128], ppb[:, 0:384].rearrange("p (r t) -> p r t", r=3)),
                        r=[Bppb], w=[BcT[cb]])
                    gj = tt % 2
                    for n in range(2):
                        p, Bp = next_pp()
                        for k in range(8):
                            S.op("pe", lambda e, p=p, k=k, s=s, n=n: e.matmul(
                                p[:, :], hT[hb][:, k, s * 128:(s + 1) * 128], wb[:, k, EG + n * 512:EG + (n + 1) * 512],
                                start=(k == 0), stop=False), r=[Bw, Bh[hb]], w=[Bp])
                        S.op("pe", lambda e, p=p, n=n: e.matmul(
                            p[:, :], self.ones_b[0:1, :], brow[0:1, 896 + n * 512:896 + (n + 1) * 512], start=False, stop=True),
                            r=[Bw, self.B_const], w=[Bp])
                        S.op("act", lambda e, p=p, n=n, gj=gj: e.activation(
                            out=gt[gj][:, n * 512:(n + 1) * 512], in_=p[:, :], func=AF.Silu), r=[Bp], w=[Bgt[gj]])
                    S.dma("pool", self.gate[tok0:tok0 + 128, :], gt[gj][:], r=[Bgt[gj]])
                for h in range(8):
                    pa, Ba = next_pp()
                    pb, Bb = next_pp()
                    for rc in range(2):
                        S.op("pe", lambda e, pa=pa, rc=rc, h=h: e.matmul(
                            pa[0:96, :ntok], wuq[:, rc, h * 96:(h + 1) * 96], cT[cb][:, rc, :ntok], start=(rc == 0), stop=(rc == 1)),
                            r=[Bw, BcT[cb]], w=[Ba])
                    for rc in range(2):
                        S.op("pe", lambda e, pb=pb, rc=rc, h=h: e.matmul(
                            pb[0:96, :ntok], wuq[:, rc, 768 + h * 96:768 + (h + 1) * 96], cT[cb][:, rc, :ntok],
                            start=(rc == 0), stop=(rc == 1)), r=[Bw, BcT[cb]], w=[Bb])
                    tj = h % 2
                    fj = foi % 4
                    foi += 1
                    S.op("dve", lambda e, pa=pa, tj=tj: e.tensor_tensor(
                        out=t1[tj][64:96, :ntok], in0=pa[64:96, :ntok], in1=cosm[hb][64:96, :ntok], op=ALU.mult),
                        r=[Ba, Brt[hb]], w=[Bt1[tj]])
                    S.op("dve", lambda e, pb=pb, tj=tj: e.tensor_tensor(
                        out=t2[tj][64:96, :ntok], in0=pb[64:96, :ntok], in1=sinm[hb][64:96, :ntok], op=ALU.mult),
                        r=[Bb, Brt[hb]], w=[Bt2[tj]])
                    S.op("act", lambda e, pa=pa, fj=fj: e.activation(out=fo[fj][0:64, :ntok], in_=pa[0:64, :ntok], func=AF.Copy),
                         r=[Ba], w=[Bfo[fj]])
                    S.op("pool", lambda e, tj=tj, fj=fj: e.tensor_tensor(
                        out=fo[fj][64:96, :ntok], in0=t1[tj][64:96, :ntok], in1=t2[tj][64:96, :ntok], op=ALU.add),
                        r=[Bt1[tj], Bt2[tj]], w=[Bfo[fj]])
                    S.dma("pool", self.qm[h, :, t0:t0 + ntok], fo[fj][0:96, :ntok], r=[Bfo[fj]])
                for hp in range(4):
                    p, Bp = next_pp()
                    S.op("pe", lambda e, p=p, hp=hp: e.matmul(
                        p[:, :ntok], wukv[:, hp * 128:(hp + 1) * 128], cT[cb][:, 2, :ntok], start=True, stop=True),
                        r=[Bw, BcT[cb]], w=[Bp])
                    fj = foi % 4
                    foi += 1
                    S.op("act", lambda e, p=p, fj=fj: e.activation(out=fo[fj][:, :ntok], in_=p[:, :ntok], func=AF.Copy),
                         r=[Bp], w=[Bfo[fj]])
                    S.dma("pool", self.kmn[2 * hp, :, t0:t0 + ntok], fo[fj][0:64, :ntok], r=[Bfo[fj]])
                    S.dma("pool", self.kmn[2 * hp + 1, :, t0:t0 + ntok], fo[fj][64:128, :ntok], r=[Bfo[fj]])
                for s in range(nsub):
                    tt = (t0 // 128) + s
                    p, Bp = next_pp()
                    S.op("pe", lambda e, p=p, s=s: e.matmul(
                        p[:, :], cT[cb][:, 2, s * 128:(s + 1) * 128], wukv[:, 512:1024], start=True, stop=True),
                        r=[Bw, BcT[cb]], w=[Bp])
                    vj = tt % 2
                    S.op("act", lambda e, p=p, vj=vj: e.activation(
                        out=vma[vj][:, :, 0:64], in_=p[:, :].rearrange("p (h c) -> p h c", h=8), func=AF.Copy),
                        r=[Bp], w=[Bvma[vj]])
                    S.dma("pool", self.vm[:, :, tt, :].rearrange("h p c -> p h c"), vma[vj][:], r=[Bvma[vj]])
            S.barrier()

    def attn_core(self, st, name, KT, QT, VA, Bkv, kparts, nv, scale, qblocks, finalize, kslices=None):
        S = self.S
        nmap = len(kparts)
        nacc_per_bank = 512 // nv
        n_acc = nmap * 4
        n_acc_banks = (n_acc + nacc_per_bank - 1) // nacc_per_bank
        accb = [self.ps(st, f"{name}_acc{j}", [128, 512]) for j in range(n_acc_banks)]
        Bacc = S.bufs(name + "_acc", n_acc_banks)
        n_sc = 8 - n_acc_banks
        n_sc = min(n_sc, 4)
        scb = [self.ps(st, f"{name}_sc{j}", [128, 512]) for j in range(n_sc)]
        Bsc = S.bufs(name + "_sc", n_sc)
        NP = 4
        pt = [self.sb(st, f"{name}_pt{j}", [128, 512], BF16) for j in range(NP)]
        Bpt = S.bufs(name + "_pt", NP)

        def acc_ap(mi, sub):
            idx = mi * 4 + sub
            b = idx // nacc_per_bank
            o = (idx % nacc_per_bank) * nv
            return accb[b][:, o:o + nv], Bacc[b]

        sci = 0
        pti = 0
        for qbi, (q0, nq, ktiles) in enumerate(qblocks):
            nsub = nq // 128
            for b in range(n_acc_banks):
                S.op("pe", lambda e, b=b: e.matmul(accb[b][:, :], self.zeros_b[:, :], self.zeros_w[:, :], start=True, stop=False,
                                                   skip_group_check=True), r=[self.B_const], w=[Bacc[b]])
            units = [(kt, mi) for kt in ktiles for mi in range(nmap)]
            pend = []

            def emit_score(u):
                nonlocal sci
                kt, mi = u
                QTm, lo, hi = kparts[mi]
                j = sci % n_sc
                sci += 1
                S.op("pe", lambda e, j=j, lo=lo, hi=hi, kt=kt, QTm=QTm: e.matmul(
                    scb[j][:, :nq], KT[lo:hi, kt * 128:(kt + 1) * 128], QTm[lo:hi, q0:q0 + nq], start=True, stop=True),
                    r=[Bkv], w=[Bsc[j]])
                return j

            def emit_exp_pv(u, j, lastflag):
                nonlocal pti
                kt, mi = u
                pj = pti % NP
                pti += 1
                S.op("act", lambda e, j=j, pj=pj: e.activation(out=pt[pj][:, :nq], in_=scb[j][:, :nq], func=AF.Exp, scale=scale),
                     r=[Bsc[j]], w=[Bpt[pj]])
                for sub in range(nsub):
                    ap, Ba = acc_ap(mi, sub)
                    S.op("pe", lambda e, ap=ap, pj=pj, sub=sub, kt=kt: e.matmul(
                        ap, pt[pj][:, sub * 128:(sub + 1) * 128], VA[:, kt, :], start=False, stop=lastflag,
                        skip_group_check=True), r=[Bpt[pj], Bkv], w=[Ba])

            LOOK = min(n_sc - 1, 2)
            q = []
            for ui, u in enumerate(units):
                q.append((u, emit_score(u)))
                if len(q) > LOOK:
                    u0, j0 = q.pop(0)
                    emit_exp_pv(u0, j0, False)
            while q:
                u0, j0 = q.pop(0)
                emit_exp_pv(u0, j0, u0[0] == ktiles[-1])
            for sub in range(nsub):
                accs = [acc_ap(mi, sub) for mi in range(nmap)]
                finalize(qbi, q0, sub, accs)

    def qblocks_all(self):
        qb = []
        lat_k = list(range(NT))
        for b in range(LAT // 512):
            qb.append((b * 512, 512, lat_k))
        qb.append((LAT, CTX, [NT_LAT, NT_LAT + 1]))
        if self.qb_filter is not None:
            qb = [q for i, q in enumerate(qb) if i in self.qb_filter]
        return qb

    def qblocks_own(self):
        qb = []
        lat_k = list(range(NT))
        for b in range(LAT // 2 // 512):
            qb.append((b * 512, 512, lat_k))
        qb.append((LAT // 2, CTX, [NT_LAT, NT_LAT + 1]))
        if self.qb_filter is not None:
            qb = [q for i, q in enumerate(qb) if i in self.qb_filter]
        return qb

    def own_q(self, QO, QS, lo, hi, tmp, Bqs, Bqo, Btmp):
        S = self.S
        H = LAT // 2
        S.op("pool", lambda e: e.tensor_scalar(out=tmp[lo:hi, :], in0=QS[lo:hi, H:LAT], scalar1=self.pselt[lo:hi, 1:2], scalar2=None,
                                               op0=ALU.mult), r=[Bqs, self.B_const], w=[Btmp])
        S.op("dve", lambda e: e.scalar_tensor_tensor(out=QO[lo:hi, 0:H], in0=QS[lo:hi, 0:H], scalar=self.pselt[lo:hi, 0:1],
                                                     in1=tmp[lo:hi, :], op0=ALU.mult, op1=ALU.add),
             r=[Bqs, Btmp, self.B_const], w=[Bqo])
        S.op("act", lambda e: e.activation(out=QO[lo:hi, H:H + CTX], in_=QS[lo:hi, LAT:NTOK], func=AF.Copy), r=[Bqs], w=[Bqo])

    def y_dst(self, tok0, c0, c1):
        H = LAT // 2
        if tok0 < H:
            k, r = tok0 // 1024, tok0 % 1024
            return self.yown_t[k].ap()[r:r + 128, c0:c1]
        return self.ybuf[LAT + tok0 - H:LAT + tok0 - H + 128, c0:c1]

    def da_attention(self, i, l):
        S = self.S
        lam_init = 0.8 - 0.6 * math.exp(-0.3 * l)
        with contextlib.ExitStack() as st:
            KT = self.sb(st, "da_KT", [128, NTOK], BF16)
            NQO = LAT // 2 + CTX
            QT1 = self.sb(st, "da_QT1", [128, NQO], BF16)
            QT2 = self.sb(st, "da_QT2", [128, NQO], BF16)
            QS = self.sb(st, "da_QS", [128, NTOK], BF16)
            qtmp = self.sb(st, "da_qtmp", [128, LAT // 2], BF16)
            VA = self.sb(st, "da_VA", [128, NT, 129], BF16)
            Bkv = S.buf("da_kv")
            Bqs = S.buf("da_qs")
            Bqt = S.buf("da_qtmp")
            S.op("pool", lambda e: e.memset(QT1[64:128, :], 0.0), w=[Bkv])
            S.op("pool", lambda e: e.memset(QT2[0:64, :], 0.0), w=[Bkv])
            lamt = self.sb(st, "lamt", [128, 256])
            lamw = self.sb(st, "lamw", [128, 8])
            subg = self.sb(st, "subg", [128, 128])
            Bl = S.buf("lam")
            eps_t = self.sb(st, "da_eps", [128, 1])
            S.op("pool", lambda e: e.memset(eps_t[:], RMS_EPS), w=[Bl])
            S.dma("sp", lamt[:], self.ev_lam[i, :, :].to_broadcast([128, 256]), w=[Bl])
            S.dma("sp", subg[:], self.ev_subg[i, :, :].to_broadcast([128, 128]), w=[Bl])
            junk = self.sb(st, "da_junk", [128, 128])
            Bj = S.buf("da_junk")
            S.op("dve", lambda e: e.tensor_tensor(out=junk[:, 0:64], in0=lamt[:, 0:64], in1=lamt[:, 64:128], op=ALU.mult),
                 r=[Bl], w=[Bj])
            S.op("dve", lambda e: e.reduce_sum(out=lamw[:, 0:1], in_=junk[:, 0:64], axis=AX.X), r=[Bj], w=[Bl])
            S.op("dve", lambda e: e.tensor_tensor(out=junk[:, 64:128], in0=lamt[:, 128:192], in1=lamt[:, 192:256], op=ALU.mult),
                 r=[Bl], w=[Bj])
            S.op("dve", lambda e: e.reduce_sum(out=lamw[:, 1:2], in_=junk[:, 64:128], axis=AX.X), r=[Bj], w=[Bl])
            S.op("act", lambda e: e.activation(out=lamw[:, 2:4], in_=lamw[:, 0:2], func=AF.Exp), r=[Bl], w=[Bl])
            S.op("dve", lambda e: e.tensor_tensor(out=lamw[:, 4:5], in0=lamw[:, 3:4], in1=lamw[:, 2:3], op=ALU.subtract),
                 r=[Bl], w=[Bl])
            S.op("dve", lambda e: e.tensor_scalar(out=lamw[:, 5:6], in0=lamw[:, 4:5], scalar1=-lam_init, scalar2=None, op0=ALU.add),
                 r=[Bl], w=[Bl])
            NF = 3
            rr = [self.sb(st, f"da_rr{j}", [128, 8]) for j in range(NF)]
            ta = [self.sb(st, f"da_ta{j}", [128, 128]) for j in range(NF)]
            td = [self.sb(st, f"da_td{j}", [128, 128]) for j in range(NF)]
            to = [self.sb(st, f"da_to{j}", [128, 128], BF16) for j in range(NF)]
            Bf = S.bufs("da_fin", NF)
            Bto = S.bufs("da_to", NF)
            self._fi = 0
            for h in range(4):
                S.dma("sp", KT[:], self.kda[h, :, :], w=[Bkv])
                S.dma("act", QS[:], self.qda[h, :, :], w=[Bqs])
                S.dma("sp", VA[:], self.vda[h, :, :, :], w=[Bkv])
                self.own_q(QT1, QS, 0, 64, qtmp, Bqs, Bkv, Bqt)
                self.own_q(QT2, QS, 64, 128, qtmp, Bqs, Bkv, Bqt)

                def fin(qbi, q0, sub, accs, h=h):
                    j = self._fi % NF
                    self._fi += 1
                    (a1, B1), (a2, B2) = accs
                    S.op("dve", lambda e: e.reciprocal(rr[j][:, 0:1], a1[:, 128:129]), r=[B1], w=[Bf[j]])
                    S.op("dve", lambda e: e.reciprocal(rr[j][:, 1:2], a2[:, 128:129]), r=[B2], w=[Bf[j]])
                    S.op("dve", lambda e: e.tensor_tensor(out=rr[j][:, 2:3], in0=rr[j][:, 1:2], in1=lamw[:, 5:6], op=ALU.mult),
                         r=[Bf[j], Bl], w=[Bf[j]])
                    S.op("dve", lambda e: e.tensor_scalar(out=ta[j][:], in0=a1[:, 0:128], scalar1=rr[j][:, 0:1], scalar2=None,
                                                          op0=ALU.mult), r=[B1, Bf[j]], w=[Bf[j]])
                    S.op("dve", lambda e: e.scalar_tensor_tensor(out=td[j][:], in0=a2[:, 0:128], scalar=rr[j][:, 2:3], in1=ta[j][:],
                                                                 op0=ALU.mult, op1=ALU.add), r=[B2, Bf[j]], w=[Bf[j]])
                    S.op("act", lambda e: e.activation(out=ta[j][:], in_=td[j][:], func=AF.Square, accum_out=rr[j][:, 3:4]),
                         r=[Bf[j]], w=[Bf[j]])
                    S.op("act", lambda e: e.activation(out=rr[j][:, 4:5], in_=rr[j][:, 3:4], func=AF.Sqrt, scale=1.0 / 128,
                                                       bias=eps_t[:, 0:1]), r=[Bf[j], Bl], w=[Bf[j]])
                    S.op("dve", lambda e: e.reciprocal(rr[j][:, 5:6], rr[j][:, 4:5]), r=[Bf[j]], w=[Bf[j]])
                    S.op("pool", lambda e: e.tensor_scalar(out=td[j][:], in0=td[j][:], scalar1=rr[j][:, 5:6], scalar2=(1.0 - lam_init),
                                                           op0=ALU.mult, op1=ALU.mult), r=[Bf[j]], w=[Bf[j]])
                    S.op("pool", lambda e: e.tensor_tensor(out=to[j][:], in0=td[j][:], in1=subg[:], op=ALU.mult),
                         r=[Bf[j], Bl], w=[Bto[j]])
                    tok0 = q0 + sub * 128
                    S.dma("pool", self.y_dst(tok0, h * 128, (h + 1) * 128), to[j][:], r=[Bto[j]])

                with contextlib.ExitStack() as st2:
                    self.attn_core(st2, f"da{h}", KT, None, VA, Bkv, [(QT1, 0, 128), (QT2, 0, 128)], 129, DA_SCALE,
                                   self.qblocks_own(), fin)
                    S.barrier()
            S.barrier()

    def mla_attention(self, i):
        S = self.S
        with contextlib.ExitStack() as st:
            KT = self.sb(st, "ml_KT", [128, NTOK], BF16)
            QT = self.sb(st, "ml_QT", [128, LAT // 2 + CTX], BF16)
            QS = self.sb(st, "ml_QS", [128, NTOK], BF16)
            qtmp = self.sb(st, "ml_qtmp", [128, LAT // 2], BF16)
            VA = self.sb(st, "ml_VA", [128, NT, 65], BF16)
            Bkv = S.buf("ml_kv")
            Bqs = S.buf("ml_qs")
            Bqt = S.buf("ml_qtmp")
            NF = 3
            rr = [self.sb(st, f"ml_rr{j}", [128, 2]) for j in range(NF)]
            to = [self.sb(st, f"ml_to{j}", [128, 64], BF16) for j in range(NF)]
            Bto = S.bufs("ml_to", NF)
            self._fi = 0
            for h in range(8):
                S.dma("sp", KT[0:64, :], self.kmn[h, :, :], w=[Bkv])
                S.dma("sp", KT[64:96, :], self.krt[:, :], w=[Bkv])
                S.dma("act", QS[0:96, :], self.qm[h, :, :], w=[Bqs])
                S.dma("sp", VA[:], self.vm[h, :, :, :], w=[Bkv])
                self.own_q(QT, QS, 0, 96, qtmp, Bqs, Bkv, Bqt)

                def fin(qbi, q0, sub, accs, h=h):
                    j = self._fi % NF
                    self._fi += 1
                    (a1, B1), = accs
                    S.op("dve", lambda e: e.reciprocal(rr[j][:, 0:1], a1[:, 64:65]), r=[B1], w=[Bto[j]])
                    S.op("dve", lambda e: e.tensor_scalar(out=to[j][:], in0=a1[:, 0:64], scalar1=rr[j][:, 0:1], scalar2=None,
                                                          op0=ALU.mult), r=[B1, Bto[j]], w=[Bto[j]])
                    tok0 = q0 + sub * 128
                    S.dma("pool", self.y_dst(tok0, 512 + h * 64, 512 + (h + 1) * 64), to[j][:], r=[Bto[j]])

                with contextlib.ExitStack() as st2:
                    self.attn_core(st2, f"ml{h}", KT, None, VA, Bkv, [(QT, 0, 96)], 65, MLA_SCALE, self.qblocks_own(), fin)
                    S.barrier()
            S.barrier()

    def out_stage(self, wo_dram, src, dst, last, ysplit=False):
        S = self.S
        with contextlib.ExitStack() as st:
            wo = self.sb(st, "wo", [128, 8, D], BF16)
            Bw = S.buf("wo")
            pieces = [(wo[:, k, :], wo_dram[k * 128:(k + 1) * 128, :], None) for k in range(8)]
            self.load_cast(st, None, None, 128, D, pieces, Bw, name="wo")
            NB = 3
            yt = [self.sb(st, f"o_yt{j}", [128, D], BF16) for j in range(NB)]
            gt = [self.sb(st, f"o_gt{j}", [128, D], BF16) for j in range(NB)]
            xt = [self.sb(st, f"o_xt{j}", [128, D]) for j in range(NB)]
            yg = [self.sb(st, f"o_yg{j}", [128, D], BF16) for j in range(NB)]
            ygT = [self.sb(st, f"o_ygT{j}", [128, D], BF16) for j in range(NB)]
            rt = [self.sb(st, f"o_rt{j}", [128, D]) for j in range(NB)]
            ot = [self.sb(st, f"o_ot{j}", [128, D]) for j in range(NB)]
            stt = [self.sb(st, f"o_st{j}", [128, 16]) for j in range(NB)]
            By, Bg, Bx, Byg, BygT, Br, Bo, Bs = (S.bufs(n, NB) for n in ("o_yt", "o_gt", "o_xt", "o_yg", "o_ygT", "o_rt", "o_ot", "o_st"))
            ptr = [self.ps(st, f"o_ptr{j}", [128, D], BF16) for j in range(2)]
            Bptr = S.bufs("o_ptr", 2)
            pout = [self.ps(st, f"o_po{j}", [128, 512]) for j in range(4)]
            Bpo = S.bufs("o_po", 4)
            eps_t = self.sb(st, "o_eps", [128, 1])
            S.op("pool", lambda e: e.memset(eps_t[:], LN_EPS), w=[Bw])
            ntiles = NT_LAT if (last and self.final_out) else NT
            for t in range(ntiles):
                if self.tile_filter is not None and t not in self.tile_filter:
                    continue
                j = t % NB
                tok0 = t * 128
                isctx = t >= NT_LAT
                G = self.g_c if isctx else self.g_l
                if ysplit and not isctx:
                    half, u = t // 32, t % 32
                    r0 = half * 1024 + (u % 8) * 128
                    ysrc = self.ygath_t[u // 8].ap()[r0:r0 + 128, :]
                else:
                    ysrc = self.ybuf[tok0:tok0 + 128, :]
                S.dma("sp", yt[j][:], ysrc, w=[By[j]])
                S.dma("act", gt[j][:], self.gate[tok0:tok0 + 128, :], w=[Bg[j]])
                S.dma("sp", xt[j][:], src[tok0:tok0 + 128, :], w=[Bx[j]])
                S.op("pool", lambda e, j=j: e.tensor_tensor(out=yg[j][:], in0=yt[j][:], in1=gt[j][:], op=ALU.mult),
                     r=[By[j], Bg[j]], w=[Byg[j]])
                pj = t % 2
                for ec in range(8):
                    S.op("pe", lambda e, ec=ec, j=j, pj=pj: e.transpose(
                        ptr[pj][:, ec * 128:(ec + 1) * 128], yg[j][:, ec * 128:(ec + 1) * 128], self.ident_b[:]),
                        r=[Byg[j], self.B_const], w=[Bptr[pj]])
                S.op("act", lambda e, j=j, pj=pj: e.activation(out=ygT[j][:], in_=ptr[pj][:], func=AF.Copy),
                     r=[Bptr[pj]], w=[BygT[j]])
                for n in range(2):
                    pn = (t % 2) * 2 + n
                    for ec in range(8):
                        S.op("pe", lambda e, ec=ec, n=n, pn=pn, j=j: e.matmul(
                            pout[pn][:, :], ygT[j][:, ec * 128:(ec + 1) * 128], wo[:, ec, n * 512:(n + 1) * 512],
                            start=(ec == 0), stop=(ec == 7)), r=[BygT[j], Bw], w=[Bpo[pn]])
                    S.op("dve", lambda e, n=n, pn=pn, j=j, G=G: e.tensor_tensor(
                        out=rt[j][:, n * 512:(n + 1) * 512], in0=pout[pn][:, :], in1=G[:, n * 512:(n + 1) * 512], op=ALU.mult),
                        r=[Bpo[pn], self.B_mod], w=[Br[j]])
                S.op("dve", lambda e, j=j: e.scalar_tensor_tensor(
                    out=rt[j][:], in0=xt[j][:], scalar=ALPHA, in1=rt[j][:], op0=ALU.mult, op1=ALU.add),
                    r=[Bx[j], Br[j]], w=[Br[j]])
                for n in range(2):
                    S.op("dve", lambda e, n=n, j=j: e.bn_stats(stt[j][:, n * 6:(n + 1) * 6], rt[j][:, n * 512:(n + 1) * 512]),
                         r=[Br[j]], w=[Bs[j]])
                S.op("dve", lambda e, j=j: e.bn_aggr(stt[j][:, 12:14], stt[j][:, 0:12]), r=[Bs[j]], w=[Bs[j]])
                S.op("act", lambda e, j=j: e.activation(out=stt[j][:, 14:15], in_=stt[j][:, 13:14], func=AF.Sqrt, scale=1.0,
                                                        bias=eps_t[:, 0:1]), r=[Bs[j], Bw], w=[Bs[j]])
                S.op("dve", lambda e, j=j: e.reciprocal(stt[j][:, 15:16], stt[j][:, 14:15]), r=[Bs[j]], w=[Bs[j]])
                S.op("dve", lambda e, j=j: e.tensor_scalar(
                    out=ot[j][:], in0=rt[j][:], scalar1=stt[j][:, 12:13], scalar2=stt[j][:, 15:16], op0=ALU.subtract, op1=ALU.mult),
                    r=[Br[j], Bs[j]], w=[Bo[j]])
                S.op("pool", lambda e, j=j: e.tensor_tensor(out=ot[j][:], in0=ot[j][:], in1=self.lng[:], op=ALU.mult),
                     r=[Bo[j], self.B_mod], w=[Bo[j]])
                S.op("pool", lambda e, j=j: e.tensor_tensor(out=ot[j][:], in0=ot[j][:], in1=self.lnb[:], op=ALU.add),
                     r=[Bo[j], self.B_mod], w=[Bo[j]])
                S.dma("pool", dst[tok0:tok0 + 128, :], ot[j][:], r=[Bo[j]])
            S.barrier()

    def odd_layer(self, l, src, dst, last):
        i = l // 2
        self.adaln(l)
        self.odd_project(i, src)
        self.odd_conv(i)
        self.mlstm(i)
        self.mlstm_post(i)
        self.na_attention(i, last)
        self.out_stage(self.od_wo[i], src, dst, last)

    def odd_project(self, i, src):
        S = self.S
        with contextlib.ExitStack() as st:
            wb = self.sb(st, "od_wb", [128, 8, OD_COLS], BF16)
            bcol = self.sb(st, "od_bcol", [128, 16])
            brow_f = self.sb(st, "od_brow_f", [1, 2576])
            brow = self.sb(st, "od_brow", [1, 2576], BF16)
            fb = self.sb(st, "od_fb", [128, 8])
            Bw = S.buf("od_w")
            S.dma("sp", bcol[:], self.od_bcol[i, :, :], w=[Bw])
            S.dma("sp", brow_f[:], self.od_brow[i, :, :], w=[Bw])
            S.dma("sp", fb[:], self.od_fb[i, :, :].to_broadcast([128, 8]), w=[Bw])
            S.op("dve", lambda e: e.tensor_copy(brow[:], brow_f[:]), r=[Bw], w=[Bw])
            pieces = []
            for k in range(8):
                for c in range(4):
                    pieces.append((wb[:, k, c * 1156:(c + 1) * 1156],
                                   self.od_w[i, k * 128:(k + 1) * 128, c * 1156:(c + 1) * 1156], None))
            self.load_cast(st, None, None, 128, 1156, pieces, Bw, name="odw")
            NXS = 3
            xt = [self.sb(st, f"xt{j}", [128, D]) for j in range(NXS)]
            Bx = S.bufs("xt", NXS)
            hT = [self.sb(st, f"hT{j}", [128, 8, 512], BF16) for j in range(2)]
            Bh = S.bufs("hT", 2)
            ff = [self.sb(st, f"ff{j}", [128, 512]) for j in range(3)]
            Bff = S.bufs("ff", 3)
            fo = [self.sb(st, f"fo{j}", [128, 512], BF16) for j in range(3)]
            Bfo = S.bufs("fo", 3)
            va = [self.sb(st, f"va{j}", [128, 4, 129], BF16) for j in range(2)]
            Bva = S.bufs("va", 2)
            vna = [self.sb(st, f"vna{j}", [128, 8, 65], BF16) for j in range(2)]
            Bvna = S.bufs("vna", 2)
            ot = [self.sb(st, f"ot{j}", [128, 512]) for j in range(2)]
            Bot = S.bufs("ot", 2)
            gt = [self.sb(st, f"gt{j}", [128, D], BF16) for j in range(2)]
            Bgt = S.bufs("gt", 2)
            gs = [self.sb(st, f"gs{j}", [128, 4, 16]) for j in range(2)]
            gtmp = [self.sb(st, f"gtmp{j}", [128, 4, 8]) for j in range(2)]
            Bgs = S.bufs("gs", 2)
            pp = [self.ps(st, f"pp{j}", [128, 512]) for j in range(7)]
            pg = self.ps(st, "pg", [128, 512])
            Bpp = S.bufs("pp", 7)
            Bpg = S.buf("pg")
            for j in range(2):
                S.op("pool", lambda e, j=j: e.memset(va[j][:], 1.0), w=[Bva[j]])
                S.op("pool", lambda e, j=j: e.memset(vna[j][:], 1.0), w=[Bvna[j]])
            self._pp_i = 0

            def next_pp():
                j = self._pp_i % 7
                self._pp_i += 1
                return pp[j], Bpp[j]

            nblk = (NTOK + 511) // 512
            xi = 0
            evac = 0
            ffi = 0
            foi = 0
            for blk in range(nblk):
                t0 = blk * 512
                ntok = min(512, NTOK - t0)
                nsub = ntok // 128
                m = 0 if t0 < LAT else 1
                hb = blk % 2
                for s in range(nsub):
                    xj = xi % NXS
                    xi += 1
                    S.dma("sp", xt[xj][:], src[t0 + s * 128:t0 + (s + 1) * 128, :], w=[Bx[xj]])
                    for half in range(2):
                        p, Bp = next_pp()
                        for q in range(4):
                            dc = half * 4 + q
                            S.op("pe", lambda e, p=p, q=q, dc=dc, xj=xj: e.transpose(
                                p[:, q * 128:(q + 1) * 128], xt[xj][:, dc * 128:(dc + 1) * 128], self.ident_f[:]),
                                r=[Bx[xj], self.B_const], w=[Bp])
                        for q in range(4):
                            dc = half * 4 + q
                            if evac % 2 == 0:
                                S.op("act", lambda e, p=p, q=q, dc=dc, s=s, m=m: e.activation(
                                    out=hT[hb][:, dc, s * 128:(s + 1) * 128], in_=p[:, q * 128:(q + 1) * 128],
                                    func=AF.Identity, scale=self.sc1[:, dc, m:m + 1], bias=self.sh[:, dc, m:m + 1]),
                                    r=[Bp, self.B_mod], w=[Bh[hb]])
                            else:
                                S.op("dve", lambda e, p=p, q=q, dc=dc, s=s, m=m: e.tensor_scalar(
                                    out=hT[hb][:, dc, s * 128:(s + 1) * 128], in0=p[:, q * 128:(q + 1) * 128],
                                    scalar1=self.sc1[:, dc, m:m + 1], scalar2=self.sh[:, dc, m:m + 1],
                                    op0=ALU.mult, op1=ALU.add), r=[Bp, self.B_mod], w=[Bh[hb]])
                            evac += 1
                for c in range(16):
                    p, Bp = next_pp()
                    for k in range(8):
                        S.op("pe", lambda e, p=p, k=k, c=c: e.matmul(
                            p[:, :ntok], wb[:, k, c * 128:(c + 1) * 128], hT[hb][:, k, :ntok], start=(k == 0), stop=(k == 7)),
                            r=[Bw, Bh[hb]], w=[Bp])
                    if c < 8:
                        fj = ffi % 3
                        ffi += 1
                        if c % 2 == 0:
                            S.op("act", lambda e, p=p, c=c, fj=fj: e.activation(
                                out=ff[fj][:, :ntok], in_=p[:, :ntok], func=AF.Identity, bias=bcol[:, c:c + 1]),
                                r=[Bp, Bw], w=[Bff[fj]])
                        else:
                            S.op("dve", lambda e, p=p, c=c, fj=fj: e.tensor_scalar(
                                out=ff[fj][:, :ntok], in0=p[:, :ntok], scalar1=bcol[:, c:c + 1], scalar2=None, op0=ALU.add),
                                r=[Bp, Bw], w=[Bff[fj]])
                        S.dma("pool", self.qkpre[c, :, t0:t0 + ntok], ff[fj][:, :ntok], r=[Bff[fj]])
                    else:
                        fj = foi % 3
                        foi += 1
                        if c % 2 == 0:
                            S.op("act", lambda e, p=p, c=c, fj=fj: e.activation(
                                out=fo[fj][:, :ntok], in_=p[:, :ntok], func=AF.Identity, bias=bcol[:, c:c + 1]),
                                r=[Bp, Bw], w=[Bfo[fj]])
                        else:
                            S.op("dve", lambda e, p=p, c=c, fj=fj: e.tensor_scalar(
                                out=fo[fj][:, :ntok], in0=p[:, :ntok], scalar1=bcol[:, c:c + 1], scalar2=None, op0=ALU.add),
                                r=[Bp, Bw], w=[Bfo[fj]])
                        dd = self.qda if c < 12 else self.kda
                        S.dma("pool", dd[(c - 8) % 4, :, t0:t0 + ntok], fo[fj][:, :ntok], r=[Bfo[fj]])
                gb = blk % 2
                for s in range(nsub):
                    tt = (t0 // 128) + s
                    tok0 = t0 + s * 128

                    def tm(p, c0, n, b0, s=s):
                        for k in range(8):
                            S.op("pe", lambda e, k=k: e.matmul(
                                p[:, 0:n], hT[hb][:, k, s * 128:(s + 1) * 128], wb[:, k, c0:c0 + n], start=(k == 0), stop=False),
                                r=[Bw, Bh[hb]], w=[Bp])
                        S.op("pe", lambda e: e.matmul(p[:, 0:n], self.ones_b[0:1, :], brow[0:1, b0:b0 + n], start=False, stop=True),
                             r=[Bw, self.B_const], w=[Bp])
                    p, Bp = next_pp()
                    tm(p, OV, 512, 0)
                    vj = tt % 2
                    S.op("act", lambda e, p=p, vj=vj: e.activation(
                        out=va[vj][:, :, 0:128], in_=p[:, :].rearrange("p (h c) -> p h c", h=4), func=AF.Copy),
                        r=[Bp], w=[Bva[vj]])
                    S.dma("pool", self.vda[:, :, tt, :].rearrange("h p c -> p h c"), va[vj][:], r=[Bva[vj]])
                    p, Bp = next_pp()
                    tm(p, OO, 512, 512)
                    oj = tt % 2
                    S.op("act", lambda e, p=p, oj=oj: e.activation(out=ot[oj][:], in_=p[:, :], func=AF.Sigmoid), r=[Bp], w=[Bot[oj]])
                    S.dma("pool", self.og[tok0:tok0 + 128, :], ot[oj][:], r=[Bot[oj]])
                    Bp = Bpg
                    for k in range(8):
                        S.op("pe", lambda e, k=k, s=s: e.matmul(
                            pg[:, s * 16:(s + 1) * 16], hT[hb][:, k, s * 128:(s + 1) * 128], wb[:, k, OGT:OGT + 16],
                            start=(k == 0), stop=False), r=[Bw, Bh[hb]], w=[Bpg])
                    S.op("pe", lambda e, s=s: e.matmul(pg[:, s * 16:(s + 1) * 16], self.ones_b[0:1, :], brow[0:1, 1024:1040],
                                                       start=False, stop=True), r=[Bw, self.B_const], w=[Bpg])
                    p, Bp = next_pp()
                    tm(p, OVN, 512, 1040)
                    vj = tt % 2
                    S.op("act", lambda e, p=p, vj=vj: e.activation(
                        out=vna[vj][:, :, 0:64], in_=p[:, :].rearrange("p (h c) -> p h c", h=8), func=AF.Copy),
                        r=[Bp], w=[Bvna[vj]])
                    S.dma("pool", self.vm[:, :, tt, :].rearrange("h p c -> p h c"), vna[vj][:], r=[Bvna[vj]])
                    gj = tt % 2
                    for n in range(2):
                        p, Bp = next_pp()
                        tm(p, OG + n * 512, 512, 1552 + n * 512)
                        S.op("act", lambda e, p=p, n=n, gj=gj: e.activation(
                            out=gt[gj][:, n * 512:(n + 1) * 512], in_=p[:, :], func=AF.Silu), r=[Bp], w=[Bgt[gj]])
                    S.dma("pool", self.gate[tok0:tok0 + 128, :], gt[gj][:], r=[Bgt[gj]])
                pgv = pg[:, 0:nsub * 16].rearrange("p (s c) -> p s c", c=16)
                S.op("dve", lambda e, pgv=pgv: e.tensor_copy(gs[gb][:, 0:nsub, 0:8], pgv[:, :, 0:8]), r=[Bpg], w=[Bgs[gb]])
                for s in range(nsub):
                    S.op("dve", lambda e, s=s: e.tensor_tensor(out=gtmp[gb][:, s, :], in0=pg[:, s * 16 + 8:s * 16 + 16], in1=fb[:, :],
                                                               op=ALU.add), r=[Bpg, Bw], w=[Bgs[gb]])
                S.op("act", lambda e: e.activation(out=gtmp[gb][:, 0:nsub, :], in_=gtmp[gb][:, 0:nsub, :], func=AF.Exp, scale=-1.0),
                     r=[Bgs[gb]], w=[Bgs[gb]])
                S.op("act", lambda e: e.activation(out=gtmp[gb][:, 0:nsub, :], in_=gtmp[gb][:, 0:nsub, :], func=AF.Ln, bias=1.0),
                     r=[Bgs[gb]], w=[Bgs[gb]])
                S.op("dve", lambda e: e.tensor_scalar(out=gs[gb][:, 0:nsub, 8:16], in0=gtmp[gb][:, 0:nsub, :], scalar1=-1.0,
                                                      scalar2=None, op0=ALU.mult), r=[Bgs[gb]], w=[Bgs[gb]])
                tt0 = t0 // 128
                S.dma("pool", self.gates[:, tt0:tt0 + nsub, :], gs[gb][:, 0:nsub, :], r=[Bgs[gb]])
            S.barrier()

    def odd_conv(self, i):
        S = self.S
        SEG = 2048
        with contextlib.ExitStack() as st:
            cw = self.sb(st, "cv_w", [128, 8, 5])
            cb = self.sb(st, "cv_b", [128, 8])
            Bw = S.buf("cv_w")
            S.dma("sp", cw[:], self.od_convw[i, :, :, :], w=[Bw])
            S.dma("sp", cb[:], self.od_convb[i, :, :], w=[Bw])
            xin = [self.sb(st, f"cv_x{j}", [128, SEG + 4]) for j in range(2)]
            Bxin = S.bufs("cv_x", 2)
            acc = [self.sb(st, f"cv_a{j}", [128, SEG]) for j in range(2)]
            Bacc = S.bufs("cv_a", 2)
            tmp = [self.sb(st, f"cv_t{j}", [128, SEG]) for j in range(2)]
            Btmp = S.bufs("cv_t", 2)
            outb = [self.sb(st, f"cv_o{j}", [128, SEG], BF16) for j in range(2)]
            Bout = S.bufs("cv_o", 2)
            segs = [(a, SEG, 0, LAT) for a in range(0, LAT, SEG)] + [(LAT, CTX, LAT, LAT + CTX)]
            it = 0
            for c in range(8):
                for (t0, n, lo, hi) in segs:
                    j = it % 2
                    it += 1
                    a = max(t0 - 2, lo)
                    b = min(t0 + n + 2, hi)
                    if a > t0 - 2:
                        S.op("pool", lambda e, j=j: e.memset(xin[j][:, 0:2], 0.0), w=[Bxin[j]])
                    if b < t0 + n + 2:
                        S.op("pool", lambda e, j=j, n=n: e.memset(xin[j][:, n + 2:n + 4], 0.0), w=[Bxin[j]])
                    S.dma("sp", xin[j][:, a - (t0 - 2):b - (t0 - 2)], self.qkpre[c, :, a:b], w=[Bxin[j]])
                    S.op("act", lambda e, j=j, n=n, c=c: e.activation(
                        out=acc[j][:, 0:n], in_=xin[j][:, 0:n], func=AF.Copy, scale=cw[:, c, 0:1]),
                        r=[Bxin[j], Bw], w=[Bacc[j]])
                    for k in range(1, 5):
                        S.op("dve", lambda e, j=j, n=n, c=c, k=k: e.scalar_tensor_tensor(
                            out=acc[j][:, 0:n], in0=xin[j][:, k:k + n], scalar=cw[:, c, k:k + 1], in1=acc[j][:, 0:n],
                            op0=ALU.mult, op1=ALU.add), r=[Bxin[j], Bw, Bacc[j]], w=[Bacc[j]])
                    if True:
                        S.op("act", lambda e, j=j, n=n, c=c: e.activation(
                            out=outb[j][:, 0:n], in_=acc[j][:, 0:n], func=AF.Silu, bias=cb[:, c:c + 1]),
                            r=[Bacc[j], Bw], w=[Bout[j]])
                        dd = self.mq if c < 4 else self.mk
                        S.dma("pool", dd[c % 4, :, t0:t0 + n], outb[j][:, 0:n], r=[Bout[j]])
                    else:
                        S.op("act", lambda e, j=j, n=n, c=c: e.activation(
                            out=tmp[j][:, 0:n], in_=acc[j][:, 0:n], func=AF.Silu, bias=cb[:, c:c + 1]),
                            r=[Bacc[j], Bw], w=[Btmp[j]])
                        S.op("pool", lambda e, j=j, n=n: e.tensor_scalar(
                            out=outb[j][:, 0:n], in0=tmp[j][:, 0:n], scalar1=128 ** -0.5, scalar2=None, op0=ALU.mult),
                            r=[Btmp[j]], w=[Bout[j]])
                        S.dma("sp", self.mk[c - 4, :, t0:t0 + n], outb[j][:, 0:n], r=[Bout[j]])
            S.barrier()

    def mlstm(self, i):
        S = self.S
        with contextlib.ExitStack() as st:
            tri = self.sb(st, "ml_tri", [128, 3, 128])
            G = self.sb(st, "ml_G", [128, NT, 16])
            A = self.sb(st, "ml_A", [128, NT, 8])
            A2 = self.sb(st, "ml_A2", [128, NT, 8])
            Bq = self.sb(st, "ml_Bq", [128, NT, 8])
            EB = self.sb(st, "ml_EB", [128, NT, 8])
            Bg = S.buf("ml_g")
            S.dma("sp", tri[:], self.tri[:, :, :], w=[Bg])
            S.dma("sp", G[:], self.gates[:, :, :], w=[Bg])
            lnks = self.sb(st, "ml_lnks", [128, 1])
            S.op("pool", lambda e: e.memset(lnks[:], math.log(128 ** -0.5)), w=[Bg])
            with contextlib.ExitStack() as st1:
                pg = [self.ps(st1, f"ml_pg{j}", [128, 512]) for j in range(3)]
                Bpg = S.bufs("ml_pg", 3)
                tmpg = self.sb(st1, "ml_tmpg", [128, 32, 8])
                Btg = S.buf("ml_tmpg")
                grp = 0
                for g0 in range(0, NT, 32):
                    g1 = min(g0 + 32, NT)
                    pj = grp % 3
                    grp += 1
                    for t in range(g0, g1):
                        o = (t - g0) * 16
                        S.op("pe", lambda e, t=t, o=o, pj=pj: e.matmul(pg[pj][:, o:o + 4], tri[:, 0, :], G[:, t, 8:12], start=True, stop=True),
                             r=[Bg], w=[Bpg[pj]])
                        S.op("pe", lambda e, t=t, o=o, pj=pj: e.matmul(pg[pj][:, o + 4:o + 8], tri[:, 1, :], G[:, t, 12:16], start=True, stop=True),
                             r=[Bg], w=[Bpg[pj]])
                        S.op("pe", lambda e, t=t, o=o, pj=pj: e.matmul(pg[pj][:, o + 8:o + 16], tri[:, 2, :], G[:, t, 8:16], start=True, stop=True),
                             r=[Bg], w=[Bpg[pj]])
                    n = g1 - g0
                    pv = pg[pj][:, 0:n * 16].rearrange("p (t c) -> p t c", c=16)
                    S.op("dve", lambda e, pv=pv, g0=g0, g1=g1, n=n: e.tensor_tensor(
                        out=tmpg[:, 0:n, :], in0=G[:, g0:g1, 0:8], in1=pv[:, :, 0:8], op=ALU.subtract), r=[Bg, Bpg[pj]], w=[Btg])
                    S.op("act", lambda e, g0=g0, g1=g1, n=n: e.activation(out=A[:, g0:g1, :], in_=tmpg[:, 0:n, :], func=AF.Exp,
                                                                          bias=lnks[:, 0:1]), r=[Btg, Bg], w=[Bg])
                    S.op("act", lambda e, pv=pv, g0=g0, g1=g1: e.activation(out=Bq[:, g0:g1, :], in_=pv[:, :, 0:8], func=AF.Exp),
                         r=[Bpg[pj]], w=[Bg])
                    S.op("act", lambda e, pv=pv, g0=g0, g1=g1: e.activation(out=EB[:, g0:g1, :], in_=pv[:, :, 8:16], func=AF.Exp),
                         r=[Bpg[pj]], w=[Bg])
                    S.op("dve", lambda e, g0=g0, g1=g1: e.tensor_tensor(out=A2[:, g0:g1, :], in0=A[:, g0:g1, :], in1=EB[:, g0:g1, :],
                                                                        op=ALU.mult), r=[Bg], w=[Bg])
                S.barrier()
            qT = self.sb(st, "ml_qT", [128, NTOK], BF16)
            kT = self.sb(st, "ml_kT", [128, NTOK], BF16)
            V = self.sb(st, "ml_V", [128, NT, 129], BF16)
            KTOK = self.sb(st, "ml_KTOK", [128, NT, 128], BF16)
            Bin = S.buf("ml_in")
            Bkt = S.buf("ml_ktok")
            Cn = [self.sb(st, f"ml_Cn{d}", [128, 129]) for d in range(2)]
            Cnb = [[self.sb(st, f"ml_Cnb{d}{j}", [128, 129], BF16) for j in range(2)] for d in range(2)]
            BCn = S.bufs("ml_Cn", 2)
            BCnb = [S.bufs(f"ml_Cnb{d}", 2) for d in range(2)]
            NW = 3
            W = [self.sb(st, f"ml_W{j}", [128, 128], BF16) for j in range(NW)]
            BW = S.bufs("ml_W", NW)
            v2 = [self.sb(st, f"ml_v2{j}", [128, 129], BF16) for j in range(NW)]
            Bv2 = S.bufs("ml_v2", NW)
            ho = [self.sb(st, f"ml_ho{j}", [128, 128]) for j in range(NW)]
            Bho = S.bufs("ml_ho", NW)
            sm = [self.sb(st, f"ml_sm{j}", [128, 4]) for j in range(NW)]
            Bsm = S.bufs("ml_sm", NW)
            ps_s = [self.ps(st, f"ml_pss{j}", [128, 512]) for j in range(2)]
            ps_kv = [self.ps(st, f"ml_pkv{j}", [128, 512]) for j in range(2)]
            ps_n = [self.ps(st, f"ml_pn{j}", [128, 512]) for j in range(2)]
            ppb = self.ps(st, "ml_ppb", [128, 1024], BF16)
            Bps_s, Bps_kv, Bps_n = S.bufs("ml_pss", 2), S.bufs("ml_pkv", 2), S.bufs("ml_pn", 2)
            Bppb = S.buf("ml_ppb")
            order = [[NT_LAT, NT_LAT + 1] + list(range(NT_LAT)), [NT_LAT + 1, NT_LAT] + list(range(NT_LAT - 1, -1, -1))]
            wi = 0
            for h in range(4):
                S.dma("sp", qT[:], self.mq[h, :, :], w=[Bin])
                S.dma("sp", kT[:], self.mk[h, :, :], w=[Bin])
                S.dma("sp", V[:], self.vda[h, :, :, :], w=[Bin])
                for t0 in range(0, NT, 8):
                    n = min(8, NT - t0)
                    for q in range(n):
                        t = t0 + q
                        S.op("pe", lambda e, t=t, q=q: e.transpose(ppb[:, q * 128:(q + 1) * 128], kT[:, t * 128:(t + 1) * 128], self.ident_b[:]),
                             r=[Bin, self.B_const], w=[Bppb])
                    S.op("act", lambda e, t0=t0, n=n: e.activation(
                        out=KTOK[:, t0:t0 + n, :], in_=ppb[:, 0:n * 128].rearrange("p (t c) -> p t c", c=128), func=AF.Copy),
                        r=[Bppb], w=[Bkt])
                for d in range(2):
                    S.op("pool", lambda e, d=d: e.memset(Cn[d][:], 0.0), w=[BCn[d]])
                    S.op("pool", lambda e, d=d: e.memset(Cnb[d][0][:], 0.0), w=[BCnb[d][0]])
                for step in range(NT):
                    for d in range(2):
                        t = order[d][step]
                        hd = d * 4 + h
                        cur, nxt = step % 2, (step + 1) % 2
                        j = wi % NW
                        pj = wi % 2
                        wi += 1
                        tsl = slice(t * 128, (t + 1) * 128)
                        S.op("pe", lambda e, pj=pj, tsl=tsl: e.matmul(ps_s[pj][:, 0:128], kT[:, tsl], qT[:, tsl], start=True, stop=True),
                             r=[Bin], w=[Bps_s[pj]])
                        S.op("dve", lambda e, pj=pj, j=j, t=t, hd=hd, d=d: e.scalar_tensor_tensor(
                            out=W[j][:], in0=ps_s[pj][:, 0:128], scalar=A[:, t, hd:hd + 1], in1=tri[:, d, :],
                            op0=ALU.mult, op1=ALU.mult), r=[Bps_s[pj], Bg], w=[BW[j]])
                        S.op("act", lambda e, j=j, t=t, hd=hd: e.activation(
                            out=v2[j][:], in_=V[:, t, :], func=AF.Copy, scale=A2[:, t, hd:hd + 1]),
                            r=[Bin, Bg], w=[Bv2[j]])
                        S.op("pe", lambda e, pj=pj, j=j, t=t: e.matmul(ps_kv[pj][:, 0:129], KTOK[:, t, :], v2[j][:], start=True, stop=True),
                             r=[Bkt, Bv2[j]], w=[Bps_kv[pj]])
                        S.op("pe", lambda e, pj=pj, j=j, t=t: e.matmul(ps_n[pj][:, 0:129], W[j][:], V[:, t, :], start=True, stop=False),
                             r=[BW[j], Bin], w=[Bps_n[pj]])
                        S.op("pe", lambda e, pj=pj, tsl=tsl, d=d, cur=cur: e.matmul(ps_n[pj][:, 0:129], qT[:, tsl], Cnb[d][cur][:], start=False, stop=True),
                             r=[Bin, BCnb[d][cur]], w=[Bps_n[pj]])
                        S.op("dve", lambda e, pj=pj, d=d, t=t, hd=hd: e.scalar_tensor_tensor(
                            out=Cn[d][:], in0=Cn[d][:], scalar=EB[:, t, hd:hd + 1], in1=ps_kv[pj][:, 0:129],
                            op0=ALU.mult, op1=ALU.add), r=[BCn[d], Bps_kv[pj], Bg], w=[BCn[d]])
                        S.op("act", lambda e, d=d, nxt=nxt: e.activation(out=Cnb[d][nxt][:], in_=Cn[d][:], func=AF.Copy),
                             r=[BCn[d]], w=[BCnb[d][nxt]])
                        S.op("dve", lambda e, pj=pj, j=j, t=t, hd=hd: e.tensor_tensor(
                            out=sm[j][:, 0:1], in0=ps_n[pj][:, 128:129], in1=Bq[:, t, hd:hd + 1], op=ALU.mult),
                            r=[Bps_n[pj], Bg], w=[Bsm[j]])
                        S.op("dve", lambda e, j=j: e.scalar_tensor_tensor(out=sm[j][:, 1:2], in0=sm[j][:, 0:1], scalar=-1.0,
                                                                          in1=sm[j][:, 0:1], op0=ALU.mult, op1=ALU.max),
                             r=[Bsm[j]], w=[Bsm[j]])
                        S.op("dve", lambda e, j=j: e.tensor_scalar(out=sm[j][:, 1:2], in0=sm[j][:, 1:2], scalar1=1.0, scalar2=None,
                                                                   op0=ALU.max), r=[Bsm[j]], w=[Bsm[j]])
                        S.op("dve", lambda e, j=j: e.reciprocal(sm[j][:, 2:3], sm[j][:, 1:2]), r=[Bsm[j]], w=[Bsm[j]])
                        S.op("dve", lambda e, j=j, t=t, hd=hd: e.tensor_tensor(
                            out=sm[j][:, 3:4], in0=sm[j][:, 2:3], in1=Bq[:, t, hd:hd + 1], op=ALU.mult), r=[Bsm[j], Bg], w=[Bsm[j]])
                        S.op("dve", lambda e, pj=pj, j=j: e.tensor_scalar(
                            out=ho[j][:], in0=ps_n[pj][:, 0:128], scalar1=sm[j][:, 3:4], scalar2=None, op0=ALU.mult),
                            r=[Bps_n[pj], Bsm[j]], w=[Bho[j]])
                        S.dma("sp", self.hfb[d, t * 128:(t + 1) * 128, h * 128:(h + 1) * 128], ho[j][:], r=[Bho[j]])
                S.barrier()
            S.barrier()

    def mlstm_post(self, i):
        S = self.S
        with contextlib.ExitStack() as st:
            ng = self.sb(st, "mp_ng", [128, 512])
            Bw = S.buf("mp_w")
            S.dma("sp", ng[:], self.od_ng[i, :, :].to_broadcast([128, 512]), w=[Bw])
            eps_t = self.sb(st, "mp_eps", [128, 1])
            S.op("pool", lambda e: e.memset(eps_t[:], LN_EPS), w=[Bw])
            NB = 2
            hf = [self.sb(st, f"mp_hf{j}", [128, 512]) for j in range(NB)]
            hb = [self.sb(st, f"mp_hb{j}", [128, 512]) for j in range(NB)]
            og = [self.sb(st, f"mp_og{j}", [128, 512]) for j in range(NB)]
            hs = [self.sb(st, f"mp_hs{j}", [128, 512]) for j in range(NB)]
            yo = [self.sb(st, f"mp_yo{j}", [128, 512]) for j in range(NB)]
            yob = [self.sb(st, f"mp_yob{j}", [128, 512], BF16) for j in range(NB)]
            stt = [self.sb(st, f"mp_st{j}", [128, 4, 12]) for j in range(NB)]
            Bhf, Bhb, Bog, Bhs, Byo, Bst = (S.bufs(n, NB) for n in ("mp_hf", "mp_hb", "mp_og", "mp_hs", "mp_yo", "mp_st"))
            for t in range(NT):
                j = t % NB
                tok0 = t * 128
                S.dma("sp", hf[j][:], self.hfb[0, tok0:tok0 + 128, :], w=[Bhf[j]])
                S.dma("act", hb[j][:], self.hfb[1, tok0:tok0 + 128, :], w=[Bhb[j]])
                S.dma("pool", og[j][:], self.og[tok0:tok0 + 128, :], w=[Bog[j]])
                S.op("pool", lambda e, j=j: e.tensor_tensor(out=hs[j][:], in0=hf[j][:], in1=hb[j][:], op=ALU.add),
                     r=[Bhf[j], Bhb[j]], w=[Bhs[j]])
                S.op("pool", lambda e, j=j: e.tensor_tensor(out=og[j][:], in0=og[j][:], in1=ng[:], op=ALU.mult),
                     r=[Bog[j], Bw], w=[Bog[j]])
                for h in range(4):
                    S.op("dve", lambda e, j=j, h=h: e.bn_stats(stt[j][:, h, 0:6], hs[j][:, h * 128:(h + 1) * 128]),
                         r=[Bhs[j]], w=[Bst[j]])
                    S.op("dve", lambda e, j=j, h=h: e.bn_aggr(stt[j][:, h, 6:8], stt[j][:, h, 0:6]), r=[Bst[j]], w=[Bst[j]])
                S.op("act", lambda e, j=j: e.activation(out=stt[j][:, :, 8:9], in_=stt[j][:, :, 7:8], func=AF.Sqrt, scale=1.0,
                                                        bias=eps_t[:, 0:1]), r=[Bst[j], Bw], w=[Bst[j]])
                S.op("dve", lambda e, j=j: e.reciprocal(stt[j][:, :, 9:10], stt[j][:, :, 8:9]), r=[Bst[j]], w=[Bst[j]])
                for h in range(4):
                    S.op("dve", lambda e, j=j, h=h: e.tensor_scalar(
                        out=yo[j][:, h * 128:(h + 1) * 128], in0=hs[j][:, h * 128:(h + 1) * 128], scalar1=stt[j][:, h, 6:7],
                        scalar2=stt[j][:, h, 9:10], op0=ALU.subtract, op1=ALU.mult), r=[Bhs[j], Bst[j]], w=[Byo[j]])
                S.op("pool", lambda e, j=j: e.tensor_tensor(out=yob[j][:], in0=yo[j][:], in1=og[j][:], op=ALU.mult),
                     r=[Byo[j], Bog[j]], w=[Byo[j]])
                S.dma("sp", self.ybuf[tok0:tok0 + 128, 0:512], yob[j][:], r=[Byo[j]])
            S.barrier()

    def na_attention(self, i, last):
        S = self.S
        with contextlib.ExitStack() as st:
            QNe = self.sb(st, "na_Qe", [128, NTOK], BF16)
            QNo = self.sb(st, "na_Qo", [128, NTOK], BF16)
            KN = self.sb(st, "na_K", [128, NTOK], BF16)
            VN = self.sb(st, "na_V", [128, NT, 65], BF16)
            MB = self.sb(st, "na_MB", [128, NA_NVAR, 128], BF16)
            Bqk = S.buf("na_qk")
            Bv = S.buf("na_v")
            Bmb = S.buf("na_mb")
            stg = [self.sb(st, f"na_stg{j}", [128, 7, 128]) for j in range(2)]
            Bstg = S.bufs("na_stg", 2)
            NP = 3
            ptA = [self.sb(st, f"na_ptA{j}", [128, 512], BF16) for j in range(NP)]
            ptB = [self.sb(st, f"na_ptB{j}", [128, 384], BF16) for j in range(NP)]
            BptA, BptB = S.bufs("na_ptA", NP), S.bufs("na_ptB", NP)
            rr = [self.sb(st, f"na_rr{j}", [128, 2]) for j in range(NP)]
            to = [self.sb(st, f"na_to{j}", [128, 64], BF16) for j in range(NP)]
            Bto = S.bufs("na_to", NP)
            psA = [self.ps(st, f"na_psA{j}", [128, 512]) for j in range(2)]
            psB = [self.ps(st, f"na_psB{j}", [128, 512]) for j in range(2)]
            acc = [self.ps(st, f"na_acc{j}", [128, 512]) for j in range(2)]
            BpsA, BpsB, Bacc = S.bufs("na_psA", 2), S.bufs("na_psB", 2), S.bufs("na_acc", 2)
            it = 0
            qtiles = list(range(NT_LAT)) + ([] if (last and self.final_out) else [NT_LAT, NT_LAT + 1])
            if self.tile_filter is not None:
                qtiles = [t for t in qtiles if t in self.tile_filter]
            for h in range(8):
                lo = (h % 2) * 64
                if h == 0:
                    S.op("pool", lambda e: e.memset(QNe[64:128, :], 0.0), w=[Bqk])
                    S.op("pool", lambda e: e.memset(QNo[0:64, :], 0.0), w=[Bqk])
                if h % 2 == 0:
                    S.dma("sp", QNe[0:64, :], self.qda[h // 2, 0:64, :], w=[Bqk])
                    S.dma("sp", QNo[64:128, :], self.qda[h // 2, 64:128, :], w=[Bqk])
                    S.dma("sp", KN[:], self.kda[h // 2, :, :], w=[Bqk])
                QN = QNe if h % 2 == 0 else QNo
                S.dma("sp", VN[:], self.vm[h, :, :, :], w=[Bv])
                for v0 in range(0, NA_NVAR, 7):
                    n = min(7, NA_NVAR - v0)
                    sj = (v0 // 7) % 2
                    S.dma("sp", stg[sj][:, 0:n, :], self.od_nat[i, h, :, v0:v0 + n, :], w=[Bstg[sj]])
                    S.op("pool", lambda e, sj=sj, n=n, v0=v0: e.tensor_scalar(
                        out=MB[:, v0:v0 + n, :], in0=stg[sj][:, 0:n, :], scalar1=1.0 / NA_SCALE, scalar2=None, op0=ALU.mult),
                        r=[Bstg[sj]], w=[Bmb])
                for j in qtiles:
                    pj = it % 2
                    tj = it % NP
                    it += 1
                    qsl = slice(j * 128, (j + 1) * 128)
                    if j < NT_LAT:
                        slots = [(kt, var) for (kt, var) in NA_PLAN[j]] + [(NT_LAT, None), (NT_LAT + 1, None)]
                    else:
                        slots = [(NT_LAT, None), (NT_LAT + 1, None)]
                    nA = min(4, len(slots))
                    nB = len(slots) - nA
                    for si, (kt, var) in enumerate(slots):
                        if si < 4:
                            dst, Bd = psA[pj][:, si * 128:(si + 1) * 128], BpsA[pj]
                        else:
                            dst, Bd = psB[pj][:, (si - 4) * 128:(si - 3) * 128], BpsB[pj]
                        S.op("pe", lambda e, dst=dst, kt=kt, var=var, QN=QN: e.matmul(
                            dst, KN[:, kt * 128:(kt + 1) * 128], QN[:, qsl], start=True, stop=(var is None)),
                            r=[Bqk], w=[Bd])
                        if var is not None:
                            S.op("pe", lambda e, dst=dst, var=var: e.matmul(dst, self.ident_b[:], MB[:, var, :], start=False, stop=True),
                                 r=[Bmb, self.B_const], w=[Bd])
                    S.op("act", lambda e, pj=pj, tj=tj, nA=nA: e.activation(out=ptA[tj][:, 0:nA * 128], in_=psA[pj][:, 0:nA * 128],
                                                                            func=AF.Exp, scale=NA_SCALE), r=[BpsA[pj]], w=[BptA[tj]])
                    if nB > 0:
                        S.op("act", lambda e, pj=pj, tj=tj, nB=nB: e.activation(out=ptB[tj][:, 0:nB * 128], in_=psB[pj][:, 0:nB * 128],
                                                                                func=AF.Exp, scale=NA_SCALE), r=[BpsB[pj]], w=[BptB[tj]])
                    for si, (kt, var) in enumerate(slots):
                        if si < 4:
                            lhs, Bl = ptA[tj][:, si * 128:(si + 1) * 128], BptA[tj]
                        else:
                            lhs, Bl = ptB[tj][:, (si - 4) * 128:(si - 3) * 128], BptB[tj]
                        S.op("pe", lambda e, lhs=lhs, kt=kt, si=si, pj=pj: e.matmul(
                            acc[pj][:, 0:65], lhs, VN[:, kt, :], start=(si == 0), stop=(si == len(slots) - 1)),
                            r=[Bl, Bv], w=[Bacc[pj]])
                    S.op("dve", lambda e, pj=pj, tj=tj: e.reciprocal(rr[tj][:, 0:1], acc[pj][:, 64:65]), r=[Bacc[pj]], w=[Bto[tj]])
                    S.op("dve", lambda e, pj=pj, tj=tj: e.tensor_scalar(out=to[tj][:], in0=acc[pj][:, 0:64], scalar1=rr[tj][:, 0:1],
                                                                        scalar2=None, op0=ALU.mult), r=[Bacc[pj], Bto[tj]], w=[Bto[tj]])
                    S.dma("pool", self.ybuf[j * 128:(j + 1) * 128, 512 + h * 64:512 + (h + 1) * 64], to[tj][:], r=[Bto[tj]])
            S.barrier()


def _swap64(cols):
    return np.concatenate([cols[32:64], cols[0:32]])


def _rope_tables():
    t = np.arange(LAT)
    row = (t // GRID_W).astype(np.float32)
    col = (t % GRID_W).astype(np.float32)

    def tab(dim, nrows_pattern):
        n_freq = dim // 4
        freqs = (10000.0 ** (-np.arange(n_freq, dtype=np.float32) / n_freq)).astype(np.float32)
        ang = np.concatenate([row[:, None] * freqs, col[:, None] * freqs], axis=-1).astype(np.float32)
        cos = np.cos(ang).astype(np.float32).T
        sin = np.sin(ang).astype(np.float32).T
        half = dim // 2
        c = np.concatenate([cos, cos], 0)
        s = np.concatenate([-sin, sin], 0)
        c = np.concatenate([c, np.ones((dim, CTX), np.float32)], 1)
        s = np.concatenate([s, np.zeros((dim, CTX), np.float32)], 1)
        return c, s

    c64, s64 = tab(64, None)
    c32, s32 = tab(32, None)
    rope_da = np.stack([np.concatenate([c64, c64], 0), np.concatenate([s64, s64], 0)]).astype(np.float32)
    ml_c = np.ones((128, NTOK), np.float32)
    ml_s = np.zeros((128, NTOK), np.float32)
    ml_c[0:32] = c32
    ml_s[0:32] = s32
    ml_c[64:96] = c32
    ml_s[64:96] = s32
    rope_ml = np.stack([ml_c, ml_s]).astype(np.float32)
    return rope_da, rope_ml


def _even_layout(inp):
    ev_w_in, ev_b_in = inp["ev_w_in"], inp["ev_b_in"]
    o_q1, o_q2, o_k1, o_k2, o_v, o_cq, o_ckv, o_kr, o_g = 0, 256, 512, 768, 1024, 1536, 1792, 1920, 1952
    cols = []
    for (a, b) in ((o_q1, o_q2), (o_k1, o_k2)):
        main = []
        sw = []
        for h in range(4):
            c1 = np.arange(a + h * 64, a + (h + 1) * 64)
            c2 = np.arange(b + h * 64, b + (h + 1) * 64)
            main += [c1, c2]
            sw += [_swap64(c1), _swap64(c2)]
        cols += main + sw
    kr = np.arange(o_kr, o_kr + 32)
    cols += [kr, np.concatenate([kr[16:], kr[:16]])]
    cols += [np.arange(o_v, o_v + 512), np.arange(o_cq, o_cq + 384), np.arange(o_g, o_g + 1024)]
    perm = np.concatenate(cols)
    assert perm.shape[0] == EV_COLS
    ev_w = np.ascontiguousarray(ev_w_in[:, :, perm])
    bp = ev_b_in[:, perm]
    bcol = np.zeros((2, 128, 18), np.float32)
    for g in range(16):
        bcol[:, :, g] = bp[:, g * 128:(g + 1) * 128]
    bcol[:, 0:32, 16] = bp[:, EKR:EKR + 32]
    bcol[:, 0:32, 17] = bp[:, EKR + 32:EKR + 64]
    brow = np.ascontiguousarray(bp[:, EV_:EV_ + 1920])[:, None, :]
    wuq = inp["mla_w_uq"]
    swc = []
    for h in range(8):
        base = h * 96
        c = np.arange(base, base + 96)
        r = c[64:96]
        swc.append(np.concatenate([c[0:64], r[16:], r[:16]]))
    swc = np.concatenate(swc)
    ev_wuq = np.ascontiguousarray(np.concatenate([wuq, wuq[:, :, swc]], axis=2))
    wukv = inp["mla_w_ukv"]
    nope = np.concatenate([np.arange(h * 128, h * 128 + 64) for h in range(8)])
    vv = np.concatenate([np.arange(h * 128 + 64, h * 128 + 128) for h in range(8)])
    ev_wukv = np.ascontiguousarray(wukv[:, :, np.concatenate([nope, vv])])
    return dict(
        ev_w=ev_w, ev_bcol=bcol, ev_brow=np.ascontiguousarray(brow),
        ev_lam=np.ascontiguousarray(inp["da_lambda"].reshape(2, 1, 256)),
        ev_subg=np.ascontiguousarray(inp["da_subln_g"].reshape(2, 1, 128)),
        ev_qg=np.ascontiguousarray(inp["mla_q_norm_g"].reshape(2, 2, 128).transpose(0, 2, 1)),
        ev_kvg=np.ascontiguousarray(inp["mla_kv_norm_g"].reshape(2, 128, 1)),
        ev_wuq=ev_wuq, ev_wukv=ev_wukv, ev_wo=np.ascontiguousarray(inp["ev_w_out"]),
    )


def _odd_layout(inp):
    w, b = inp["od_w_in"], inp["od_b_in"]
    g0 = 2048
    gates = np.concatenate([g0 + j * 4 + np.arange(4) for j in (0, 2, 1, 3)])
    perm = np.concatenate([np.arange(0, 1024), np.arange(2064, 2576), np.arange(2576, 3088), np.arange(1024, 1536),
                           np.arange(1536, 2048), gates, np.arange(3088, 3600), np.arange(3600, 4624)])
    assert perm.shape[0] == OD_COLS
    od_w = np.ascontiguousarray(w[:, :, perm])
    bp = b[:, perm]
    bcol = np.ascontiguousarray(bp[:, 0:2048].reshape(2, 16, 128).transpose(0, 2, 1))
    brow = np.ascontiguousarray(bp[:, 2048:])[:, None, :]
    convw = np.ascontiguousarray(inp["ml_conv_w"].reshape(2, 5, 8, 128).transpose(0, 3, 2, 1))
    convb = np.ascontiguousarray(inp["ml_conv_b"].reshape(2, 8, 128).transpose(0, 2, 1))
    fb = np.ascontiguousarray(inp["ml_f_bias"].reshape(2, 1, 8))
    ng = np.ascontiguousarray(inp["ml_norm_g"].reshape(2, 1, 512))
    rpb = inp["na_rpb"]
    nat = np.full((2, 8, 128, NA_NVAR, 128), NEG, np.float32)
    k = np.arange(128)
    krl, kc = k // 64, k % 64
    q = np.arange(128)
    qrl, qc = q // 64, q % 64
    cs = np.clip(qc - NA_COLS // 2, 0, GRID_W - NA_COLS)
    for (dk, o0, o1), vid in NA_VARIANTS.items():
        rel = (2 * dk + krl[:, None]) - qrl[None, :]
        off = np.where(qrl[None, :] == 0, o0, o1)
        valid_r = (rel >= off) & (rel <= off + NA_ROWS - 1)
        valid_c = (kc[:, None] >= cs[None, :]) & (kc[:, None] <= cs[None, :] + NA_COLS - 1)
        valid = valid_r & valid_c
        ridx = np.clip(rel + NA_ROWS - 1, 0, 2 * NA_ROWS - 2)
        cidx = np.clip(kc[:, None] - qc[None, :] + NA_COLS - 1, 0, 2 * NA_COLS - 2)
        tab = rpb[:, :, ridx, cidx]
        nat[:, :, :, vid, :] = np.where(valid[None, None], tab, np.float32(NEG))
    tri = np.stack([np.triu(np.ones((128, 128), np.float32)), np.tril(np.ones((128, 128), np.float32)),
                    np.ones((128, 128), np.float32)], 1)
    return dict(od_w=od_w, od_bcol=bcol, od_brow=np.ascontiguousarray(brow), od_convw=convw, od_convb=convb, od_fb=fb,
                od_ng=ng, od_nat=nat, od_wo=np.ascontiguousarray(inp["od_w_out"]), tri=np.ascontiguousarray(tri))


def make_in_maps(inp, batches):
    inp = {k: np.asarray(v, dtype=np.float32) for k, v in inp.items()}
    rope_da, rope_ml = _rope_tables()
    common = dict(
        ada_w=inp["ada_w"], ada_b=inp["ada_b"], ln_g=inp["ln_g"], ln_b=inp["ln_b"],
        ident=np.eye(128, dtype=np.float32),
        sel=np.concatenate([np.stack([np.ones(128), np.zeros(128)]), np.stack([np.zeros(128), np.ones(128)])], 1).astype(np.float32),
        rope_da=rope_da, rope_ml=rope_ml,
    )
    common.update(_even_layout(inp))
    common.update(_odd_layout(inp))
    maps = []
    for b in batches:
        m = dict(common)
        m["x_in"] = np.ascontiguousarray(np.concatenate([inp["x"][b], inp["ctx"][b]], 0))
        cc = np.stack([inp["c"][b], inp["c_ctx"]], -1)
        m["cc"] = np.ascontiguousarray(cc.reshape(8, 128, 2).transpose(1, 0, 2))
        par = len(maps) % 2
        ps = np.zeros((128, 2), np.float32)
        ps[:, par] = 1.0
        m["psel"] = ps
        maps.append(m)
    return maps


_PROG_CACHE = {}


def kernel(**inputs):
    batches = [c // 2 for c in range(8)]
    in_maps = make_in_maps(inputs, batches)
    prog = Prog()
    nc = prog.build()
    res = run_bass_kernel_spmd(nc, in_maps, core_ids=list(range(8)))
    out = np.stack([res.results[2 * b]["y"] for b in range(4)], 0)
    return out.astype(np.float32)
```

```python
import contextlib
import math
import numpy as np
import concourse.bass as bass
import concourse.mybir as mybir
from concourse.bass_utils import run_bass_kernel_spmd

F32 = mybir.dt.float32
BF16 = mybir.dt.bfloat16
AF = mybir.ActivationFunctionType
ALU = mybir.AluOpType
AX = mybir.AxisListType

D = 1024
LAT = 8192
CTX = 256
NTOK = LAT + CTX
NT = NTOK // 128
NT_LAT = LAT // 128
DEPTH = 4
GRID_W = 64
ALPHA = (2.0 * DEPTH) ** 0.25
LN_EPS = 1e-5
RMS_EPS = 1e-6
DA_SCALE = 64 ** -0.5
MLA_SCALE = 96 ** -0.5
NA_SCALE = 64 ** -0.5
EV_COLS = 4032
EQ, EK, EKR, EV_, EC, EG = 0, 1024, 2048, 2112, 2624, 3008
OD_COLS = 4624
OQK, OQN, OKN, OV, OO, OGT, OVN, OG = 0, 1024, 1536, 2048, 2560, 3072, 3088, 3600
NA_ROWS, NA_COLS, GRID_H = 8, 16, 128
NEG = -30000.0


def na_plan():
    variants = {}
    plan = []
    for j in range(NT_LAT):
        r0, r1 = 2 * j, 2 * j + 1
        rs0 = min(max(r0 - NA_ROWS // 2, 0), GRID_H - NA_ROWS)
        rs1 = min(max(r1 - NA_ROWS // 2, 0), GRID_H - NA_ROWS)
        lst = []
        for kt in range(rs0 // 2, (rs1 + NA_ROWS - 1) // 2 + 1):
            key = (kt - j, rs0 - r0, rs1 - r1)
            if key not in variants:
                variants[key] = len(variants)
            lst.append((kt, variants[key]))
        plan.append(lst)
    return plan, variants


NA_PLAN, NA_VARIANTS = na_plan()
NA_NVAR = len(NA_VARIANTS)


class Tok:
    __slots__ = ("sem", "key", "val", "eng")

    def __init__(self, sem, key, val, eng):
        self.sem, self.key, self.val, self.eng = sem, key, val, eng


class Buf:
    __slots__ = ("name", "last_w", "readers", "sem", "key", "cnt")

    def __init__(self, name):
        self.name = name
        self.last_w = None
        self.readers = {}
        self.sem = None
        self.key = None
        self.cnt = 0


class Eng:
    def __init__(self, name, h, sem):
        self.name, self.h, self.sem = name, h, sem
        self.key = "E_" + name
        self.cnt = 0
        self.waited = {}


class Sync:
    def __init__(self, nc, stack):
        self.nc = nc
        self.stack = stack
        self.E = {}
        for name, h in (("pe", nc.tensor), ("act", nc.scalar), ("dve", nc.vector),
                        ("pool", nc.gpsimd), ("sp", nc.sync)):
            sem = stack.enter_context(nc.semaphore("s_" + name))
            self.E[name] = Eng(name, h, sem)
        self.dma_bufs = []
        self.free_sems = []
        self.replica_groups = [[0, 1], [2, 3], [4, 5], [6, 7]]
        self.nsem = 0
        self.nwait = 0
        self.nins = 0

    def buf(self, name):
        return Buf(name)

    def bufs(self, name, n):
        return [Buf(f"{name}{i}") for i in range(n)]

    def _deps(self, eng, r, w):
        raw = []
        oth = []
        for b in r:
            if b.last_w is not None:
                raw.append(b.last_w)
        for b in w:
            if b.last_w is not None:
                oth.append(b.last_w)
            oth.extend(b.readers.values())
        toks = []
        for t in raw:
            if t.eng is eng and eng.name == "pe":
                continue
            toks.append(t)
        for t in oth:
            if t.eng is eng:
                continue
            toks.append(t)
        return toks

    def _wait(self, eng, toks):
        for t in toks:
            if eng.waited.get(t.key, 0) >= t.val:
                continue
            eng.h.wait_ge(t.sem, t.val)
            eng.waited[t.key] = t.val
            self.nwait += 1

    def op(self, en, fn, r=(), w=()):
        eng = self.E[en]
        self._wait(eng, self._deps(eng, r, w))
        ins = fn(eng.h)
        ins.then_inc(eng.sem, 1)
        eng.cnt += 1
        self.nins += 1
        tok = Tok(eng.sem, eng.key, eng.cnt, eng)
        for b in r:
            b.readers[tok.key] = tok
        for b in w:
            b.last_w = tok
            b.readers = {}
        return tok

    def dma(self, q, out, in_, r=(), w=(), sb=None):
        eng = self.E[q]
        self._wait(eng, self._deps(eng, r, w))
        if sb is None:
            sb = w[0] if w else r[0]
        if sb.sem is None:
            if self.free_sems:
                sb.sem, sb.key, sb.cnt = self.free_sems.pop()
            else:
                sb.sem = self.stack.enter_context(self.nc.semaphore(f"d{self.nsem}"))
                sb.key = f"D{self.nsem}"
                sb.cnt = 0
                self.nsem += 1
            self.dma_bufs.append(sb)
        ins = eng.h.dma_start(out=out, in_=in_)
        ins.then_inc(sb.sem, 16)
        sb.cnt += 16
        self.nins += 1
        tok = Tok(sb.sem, sb.key, sb.cnt, None)
        for b in r:
            b.readers[tok.key] = tok
        for b in w:
            b.last_w = tok
            b.readers = {}
        return tok

    def allgather_pairs(self, src_ts, dst_ts):
        self.barrier()
        eng = self.E["pool"]
        if getattr(self, "cc_sem", None) is None:
            self.cc_sem = self.stack.enter_context(self.nc.semaphore("cc_sem"))
            self.cc_cnt = 0
        for src_t, dst_t in zip(src_ts, dst_ts):
            ins = eng.h.collective_compute("AllGather", ALU.bypass, replica_groups=self.replica_groups,
                                           ins=[src_t.ap().opt()], outs=[dst_t.ap().opt()])
            ins.then_inc(self.cc_sem)
            self.cc_cnt += 1
            self.nins += 1
        tok = Tok(self.cc_sem, "CC", self.cc_cnt, None)
        for e in self.E.values():
            self._wait(e, [tok])

    def barrier(self, engines=("pe", "act", "dve", "pool", "sp")):
        toks = [Tok(e.sem, e.key, e.cnt, e) for e in self.E.values() if e.cnt > 0]
        toks += [Tok(b.sem, b.key, b.cnt, None) for b in self.dma_bufs if b.cnt > 0]
        for en in engines:
            eng = self.E[en]
            self._wait(eng, [t for t in toks if t.eng is not eng])
        for b in self.dma_bufs:
            self.free_sems.append((b.sem, b.key, b.cnt))
            b.sem = None
            b.last_w = None
            b.readers = {}
        self.dma_bufs = []


class Prog:
    def __init__(self, layers=(0, 1, 2, 3), final_out=True, qb_filter=None, tile_filter=None, n_cores=8):
        self.n_cores = n_cores
        self.layers = tuple(layers)
        self.qb_filter = qb_filter
        self.tile_filter = tile_filter
        self.nc = bass.Bass("TRN2", target_bir_lowering=False)
        self.final_out = final_out

    def din(self, name, shape, dt=F32):
        return self.nc.dram_tensor(name, list(shape), dt, kind="ExternalInput").ap()

    def dout(self, name, shape, dt=F32):
        return self.nc.dram_tensor(name, list(shape), dt, kind="ExternalOutput").ap()

    def dscr(self, name, shape, dt=F32):
        return self.nc.dram_tensor(name, list(shape), dt, kind="Internal").ap()

    def sb(self, st, name, shape, dt=F32):
        self._uid = getattr(self, "_uid", 0) + 1
        return st.enter_context(self.nc.sbuf_tensor(f"s{self._uid}_{name}", list(shape), dt))

    def ps(self, st, name, shape, dt=F32):
        self._uid = getattr(self, "_uid", 0) + 1
        return st.enter_context(self.nc.psum_tensor(f"p{self._uid}_{name}", list(shape), dt))

    def build(self):
        nc = self.nc
        with contextlib.ExitStack() as st:
            self.S = Sync(nc, st)
            self.S.replica_groups = [[2 * k, 2 * k + 1] for k in range(self.n_cores // 2)]
            self.declare()
            self.setup_consts(st)
            nl = len(self.layers)
            for li, l in enumerate(self.layers):
                src = self.x_in if li == 0 else self.xbuf[(li - 1) % 2]
                last = li == nl - 1
                dst = self.y_out if last else self.xbuf[li % 2]
                if l % 2 == 0:
                    self.even_layer(l, src, dst, last)
                else:
                    self.odd_layer(l, src, dst, last)
            self.S.barrier()
        return nc

    def declare(self):
        self.x_in = self.din("x_in", [NTOK, D])
        self.cc = self.din("cc", [128, 8, 2])
        self.ada_w = self.din("ada_w", [DEPTH, D, 3 * D])
        self.ada_b = self.din("ada_b", [DEPTH, 3 * D])
        self.ln_g = self.din("ln_g", [DEPTH, D])
        self.ln_b = self.din("ln_b", [DEPTH, D])
        self.ident_in = self.din("ident", [128, 128])
        self.sel_in = self.din("sel", [2, 256])
        self.ev_w = self.din("ev_w", [2, D, EV_COLS])
        self.ev_bcol = self.din("ev_bcol", [2, 128, 18])
        self.ev_brow = self.din("ev_brow", [2, 1, 1920])
        self.ev_lam = self.din("ev_lam", [2, 1, 256])
        self.ev_subg = self.din("ev_subg", [2, 1, 128])
        self.ev_qg = self.din("ev_qg", [2, 128, 2])
        self.ev_kvg = self.din("ev_kvg", [2, 128, 1])
        self.ev_wuq = self.din("ev_wuq", [2, 256, 1536])
        self.ev_wukv = self.din("ev_wukv", [2, 128, 1024])
        self.ev_wo = self.din("ev_wo", [2, D, D])
        self.rope_da = self.din("rope_da", [2, 128, NTOK])
        self.rope_ml = self.din("rope_ml", [2, 128, NTOK])
        self.od_w = self.din("od_w", [2, D, OD_COLS])
        self.od_bcol = self.din("od_bcol", [2, 128, 16])
        self.od_brow = self.din("od_brow", [2, 1, 2576])
        self.od_convw = self.din("od_convw", [2, 128, 8, 5])
        self.od_convb = self.din("od_convb", [2, 128, 8])
        self.od_fb = self.din("od_fb", [2, 1, 8])
        self.od_ng = self.din("od_ng", [2, 1, 512])
        self.od_nat = self.din("od_nat", [2, 8, 128, NA_NVAR, 128])
        self.od_wo = self.din("od_wo", [2, D, D])
        self.tri = self.din("tri", [128, 3, 128])
        self.qkpre = self.dscr("qkpre", [8, 128, NTOK])
        self.mq = self.dscr("mq", [4, 128, NTOK], BF16)
        self.mk = self.dscr("mk", [4, 128, NTOK], BF16)
        self.og = self.dscr("og", [NTOK, 512])
        self.gates = self.dscr("gates", [128, NT, 16])
        self.hfb = self.dscr("hfb", [2, NTOK, 512])
        n_last_tok = LAT if self.final_out else NTOK
        self.y_out = self.dout("y", [n_last_tok, D])
        self.xbuf = [self.dscr("xbuf0", [NTOK, D]), self.dscr("xbuf1", [NTOK, D])]
        self.qda = self.dscr("qda", [4, 128, NTOK], BF16)
        self.kda = self.dscr("kda", [4, 128, NTOK], BF16)
        self.vda = self.dscr("vda", [4, 128, NT, 129], BF16)
        self.qm = self.dscr("qm", [8, 96, NTOK], BF16)
        self.kmn = self.dscr("kmn", [8, 64, NTOK], BF16)
        self.krt = self.dscr("krt", [32, NTOK], BF16)
        self.vm = self.dscr("vm", [8, 128, NT, 65], BF16)
        self.gate = self.dscr("gate", [NTOK, D], BF16)
        self.ybuf = self.dscr("ybuf", [NTOK, D], BF16)
        self.psel = self.din("psel", [128, 2])
        self.yown_t = [self.nc.dram_tensor(f"yown{k}", [1024, D], BF16) for k in range(4)]
        self.ygath_t = [self.nc.dram_tensor(f"ygath{k}", [2048, D], BF16) for k in range(4)]

    def setup_consts(self, st):
        S = self.S
        self.ident_f = self.sb(st, "ident_f", [128, 128])
        self.ident_b = self.sb(st, "ident_b", [128, 128], BF16)
        self.sel = self.sb(st, "sel", [2, 256])
        self.ccs = self.sb(st, "ccs", [128, 8, 2])
        self.zeros_b = self.sb(st, "zeros_b", [128, 128], BF16)
        self.ones_b = self.sb(st, "ones_b", [1, 128], BF16)
        self.zeros_w = self.sb(st, "zeros_w", [128, 512], BF16)
        self.B_const = S.buf("consts")
        b = self.B_const
        S.dma("sp", self.ident_f[:], self.ident_in[:, :], w=[b])
        S.dma("sp", self.sel[:], self.sel_in[:, :], w=[b])
        S.dma("sp", self.ccs[:], self.cc[:, :, :], w=[b])
        self.pselt = self.sb(st, "pselt", [128, 2])
        S.dma("sp", self.pselt[:], self.psel[:, :], w=[b])
        S.op("dve", lambda e: e.tensor_copy(self.ident_b[:], self.ident_f[:]), r=[b], w=[b])
        S.op("dve", lambda e: e.memset(self.zeros_b[:], 0.0), w=[b])
        S.op("dve", lambda e: e.memset(self.ones_b[:], 1.0), w=[b])
        S.op("dve", lambda e: e.memset(self.zeros_w[:], 0.0), w=[b])
        S.op("act", lambda e: e.activation(out=self.ccs[:], in_=self.ccs[:], func=AF.Silu), r=[b], w=[b])
        self.sc1 = self.sb(st, "sc1", [128, 8, 2])
        self.sh = self.sb(st, "sh", [128, 8, 2])
        self.g_l = self.sb(st, "g_l", [128, D])
        self.g_c = self.sb(st, "g_c", [128, D])
        self.lng = self.sb(st, "lng", [128, D])
        self.lnb = self.sb(st, "lnb", [128, D])
        self.B_mod = S.buf("mod")

    def adaln(self, l):
        S, nc = self.S, self.nc
        S.barrier()
        with contextlib.ExitStack() as st:
            wt = [self.sb(st, f"adaw{i}", [128, 3 * D]) for i in range(2)]
            Bw = S.bufs("adaw", 2)
            mrow = self.sb(st, "mrow", [2, 3 * D])
            brow = self.sb(st, "adab", [2, 3 * D])
            Bm = S.buf("mrow")
            Bb = S.buf("adab")
            pm = [self.ps(st, f"pm{i}", [128, 512]) for i in range(8)]
            Bp = S.bufs("pm", 8)
            S.dma("sp", brow[:], self.ada_b[l:l + 1, :].to_broadcast([2, 3 * D]), w=[Bb])
            S.dma("sp", self.lng[:], self.ln_g[l:l + 1, :].to_broadcast([128, D]), w=[self.B_mod])
            S.dma("sp", self.lnb[:], self.ln_b[l:l + 1, :].to_broadcast([128, D]), w=[self.B_mod])
            for k in range(8):
                S.dma("sp", wt[k % 2][:], self.ada_w[l, k * 128:(k + 1) * 128, :], w=[Bw[k % 2]])
                for n in range(6):
                    S.op("pe", lambda e, k=k, n=n: e.matmul(
                        pm[n][0:2, :], self.ccs[:, k, :], wt[k % 2][:, n * 512:(n + 1) * 512],
                        start=(k == 0), stop=(k == 7)), r=[Bw[k % 2], self.B_const], w=[Bp[n]])
            for n in range(6):
                S.op("dve", lambda e, n=n: e.tensor_tensor(
                    out=mrow[:, n * 512:(n + 1) * 512], in0=pm[n][0:2, :], in1=brow[:, n * 512:(n + 1) * 512],
                    op=ALU.add), r=[Bp[n], Bb], w=[Bm])
            for j in range(8):
                S.op("pe", lambda e, j=j: e.transpose(
                    pm[6][:, 2 * j:2 * j + 2], mrow[0:2, j * 128:(j + 1) * 128], self.ident_f[0:2, 0:2]),
                    r=[Bm, self.B_const], w=[Bp[6]])
                S.op("pe", lambda e, j=j: e.transpose(
                    pm[7][:, 2 * j:2 * j + 2], mrow[0:2, D + j * 128:D + (j + 1) * 128], self.ident_f[0:2, 0:2]),
                    r=[Bm, self.B_const], w=[Bp[7]])
            S.op("dve", lambda e: e.tensor_copy(self.sh[:].rearrange("p a b -> p (a b)"), pm[6][:, 0:16]),
                 r=[Bp[6]], w=[self.B_mod])
            S.op("dve", lambda e: e.tensor_scalar(
                out=self.sc1[:].rearrange("p a b -> p (a b)"), in0=pm[7][:, 0:16], scalar1=1.0, scalar2=None,
                op0=ALU.add), r=[Bp[7]], w=[self.B_mod])
            for n in range(2):
                S.op("pe", lambda e, n=n: e.matmul(
                    pm[n][:, :], self.sel[:, 0:128], mrow[0:2, 2 * D + n * 512:2 * D + (n + 1) * 512],
                    start=True, stop=True), r=[Bm, self.B_const], w=[Bp[n]])
                S.op("pe", lambda e, n=n: e.matmul(
                    pm[2 + n][:, :], self.sel[:, 128:256], mrow[0:2, 2 * D + n * 512:2 * D + (n + 1) * 512],
                    start=True, stop=True), r=[Bm, self.B_const], w=[Bp[2 + n]])
                S.op("dve", lambda e, n=n: e.tensor_copy(self.g_l[:, n * 512:(n + 1) * 512], pm[n][:, :]),
                     r=[Bp[n]], w=[self.B_mod])
                S.op("dve", lambda e, n=n: e.tensor_copy(self.g_c[:, n * 512:(n + 1) * 512], pm[2 + n][:, :]),
                     r=[Bp[2 + n]], w=[self.B_mod])
            S.barrier()

    def load_cast(self, st_w, dst_fn, src_fn, nparts, ncols, pieces, Bdst, scale_fn=None, name="lc"):
        S = self.S
        with contextlib.ExitStack() as st:
            stg = [self.sb(st, f"{name}_stg{i}", [nparts, ncols]) for i in range(2)]
            Bs = S.bufs(name + "_stg", 2)
            for i, (dst, src, sc) in enumerate(pieces):
                j = i % 2
                n = src.shape[-1]
                S.dma("sp", stg[j][:, 0:n], src, w=[Bs[j]])
                en = "dve" if i % 2 == 0 else "pool"
                if sc is None:
                    S.op(en, lambda e, dst=dst, j=j, n=n: e.tensor_copy(dst, stg[j][:, 0:n]), r=[Bs[j]], w=[Bdst])
                else:
                    S.op(en, lambda e, dst=dst, j=j, n=n, sc=sc: e.tensor_scalar(
                        out=dst, in0=stg[j][:, 0:n], scalar1=sc, scalar2=None, op0=ALU.mult),
                        r=[Bs[j], Bdst], w=[Bdst])
            S.barrier()

    def even_layer(self, l, src, dst, last):
        i = l // 2
        self.adaln(l)
        self.even_project(i, src)
        self.da_attention(i, l)
        self.mla_attention(i)
        self.S.allgather_pairs(self.yown_t, self.ygath_t)
        self.out_stage(self.ev_wo[i], src, dst, last, ysplit=True)

    def even_project(self, i, src):
        S, nc = self.S, self.nc
        with contextlib.ExitStack() as st:
            wb = self.sb(st, "ev_wb", [128, 8, EV_COLS], BF16)
            wuq = self.sb(st, "ev_wuqb", [128, 2, 1536], BF16)
            wukv = self.sb(st, "ev_wukvb", [128, 1024], BF16)
            bcol = self.sb(st, "ev_bcol", [128, 18])
            brow_f = self.sb(st, "ev_brow_f", [1, 1920])
            brow = self.sb(st, "ev_brow", [1, 1920], BF16)
            qg = self.sb(st, "ev_qg", [128, 2])
            kvg = self.sb(st, "ev_kvg", [128, 1])
            Bw = S.buf("ev_w")
            S.dma("sp", bcol[:], self.ev_bcol[i, :, :], w=[Bw])
            S.dma("sp", brow_f[:], self.ev_brow[i, :, :], w=[Bw])
            S.dma("sp", qg[:], self.ev_qg[i, :, :], w=[Bw])
            S.dma("sp", kvg[:], self.ev_kvg[i, :, :], w=[Bw])
            S.op("dve", lambda e: e.tensor_copy(brow[:], brow_f[:]), r=[Bw], w=[Bw])
            pieces = []
            for k in range(8):
                for c in range(4):
                    pieces.append((wb[:, k, c * 1008:(c + 1) * 1008],
                                   self.ev_w[i, k * 128:(k + 1) * 128, c * 1008:(c + 1) * 1008], None))
            self.load_cast(st, None, None, 128, 1536, pieces, Bw, name="evw")
            pieces = [(wuq[:, rc, :], self.ev_wuq[i, rc * 128:(rc + 1) * 128, :], qg[:, rc:rc + 1]) for rc in range(2)]
            pieces.append((wukv[:, :], self.ev_wukv[i, :, :], kvg[:, 0:1]))
            self.load_cast(st, None, None, 128, 1536, pieces, Bw, name="evw2")

            NXS = 3
            xt = [self.sb(st, f"xt{j}", [128, D]) for j in range(NXS)]
            Bx = S.bufs("xt", NXS)
            hT = [self.sb(st, f"hT{j}", [128, 8, 512], BF16) for j in range(2)]
            Bh = S.bufs("hT", 2)
            cosd = [self.sb(st, f"cosd{j}", [128, 512]) for j in range(2)]
            sind = [self.sb(st, f"sind{j}", [128, 512]) for j in range(2)]
            cosm = [self.sb(st, f"cosm{j}", [128, 512]) for j in range(2)]
            sinm = [self.sb(st, f"sinm{j}", [128, 512]) for j in range(2)]
            Brt = S.bufs("ropet", 2)
            t1 = [self.sb(st, f"t1_{j}", [128, 512]) for j in range(2)]
            t2 = [self.sb(st, f"t2_{j}", [128, 512]) for j in range(2)]
            Bt1 = S.bufs("t1", 2)
            Bt2 = S.bufs("t2", 2)
            fo = [self.sb(st, f"fo{j}", [128, 512], BF16) for j in range(4)]
            Bfo = S.bufs("fo", 4)
            va = [self.sb(st, f"va{j}", [128, 4, 129], BF16) for j in range(2)]
            Bva = S.bufs("va", 2)
            vma = [self.sb(st, f"vma{j}", [128, 8, 65], BF16) for j in range(2)]
            Bvma = S.bufs("vma", 2)
            cq = [self.sb(st, f"cq{j}", [128, 384]) for j in range(2)]
            Bcq = S.bufs("cq", 2)
            cqn = [self.sb(st, f"cqn{j}", [128, 384], BF16) for j in range(2)]
            Bcqn = S.bufs("cqn", 2)
            stat = [self.sb(st, f"stat{j}", [128, 8]) for j in range(2)]
            Bstat = S.bufs("stat", 2)
            junk = self.sb(st, "junk", [128, 256])
            Bjunk = S.buf("junk")
            cT = [self.sb(st, f"cT{j}", [128, 3, 512], BF16) for j in range(2)]
            BcT = S.bufs("cT", 2)
            gt = [self.sb(st, f"gt{j}", [128, D], BF16) for j in range(2)]
            Bgt = S.bufs("gt", 2)
            pp = [self.ps(st, f"pp{j}", [128, 512]) for j in range(7)]
            ppb = self.ps(st, "ppb", [128, 1024], BF16)
            Bpp = S.bufs("pp", 7)
            Bppb = S.buf("ppb")
            for j in range(2):
                S.op("pool", lambda e, j=j: e.memset(va[j][:], 1.0), w=[Bva[j]])
                S.op("pool", lambda e, j=j: e.memset(vma[j][:], 1.0), w=[Bvma[j]])
            eps_t = self.sb(st, "eps_t", [128, 1])
            S.op("pool", lambda e: e.memset(eps_t[:], RMS_EPS), w=[Bw])

            self._pp_i = 0

            def next_pp():
                j = self._pp_i % 7
                self._pp_i += 1
                return pp[j], Bpp[j]

            nblk = (NTOK + 511) // 512
            xi = 0
            foi = 0
            evac = 0
            for blk in range(nblk):
                t0 = blk * 512
                ntok = min(512, NTOK - t0)
                nsub = ntok // 128
                m = 0 if t0 < LAT else 1
                hb = blk % 2
                S.dma("sp", cosd[hb][:, :ntok], self.rope_da[0, :, t0:t0 + ntok], w=[Brt[hb]])
                S.dma("sp", sind[hb][:, :ntok], self.rope_da[1, :, t0:t0 + ntok], w=[Brt[hb]])
                S.dma("sp", cosm[hb][:, :ntok], self.rope_ml[0, :, t0:t0 + ntok], w=[Brt[hb]])
                S.dma("sp", sinm[hb][:, :ntok], self.rope_ml[1, :, t0:t0 + ntok], w=[Brt[hb]])
                for s in range(nsub):
                    xj = xi % NXS
                    xi += 1
                    S.dma("sp", xt[xj][:], src[t0 + s * 128:t0 + (s + 1) * 128, :], w=[Bx[xj]])
                    for half in range(2):
                        p, Bp = next_pp()
                        for q in range(4):
                            dc = half * 4 + q
                            S.op("pe", lambda e, p=p, q=q, dc=dc, xj=xj: e.transpose(
                                p[:, q * 128:(q + 1) * 128], xt[xj][:, dc * 128:(dc + 1) * 128], self.ident_f[:]),
                                r=[Bx[xj], self.B_const], w=[Bp])
                        for q in range(4):
                            dc = half * 4 + q
                            if evac % 2 == 0:
                                S.op("act", lambda e, p=p, q=q, dc=dc, s=s, m=m: e.activation(
                                    out=hT[hb][:, dc, s * 128:(s + 1) * 128], in_=p[:, q * 128:(q + 1) * 128],
                                    func=AF.Identity, scale=self.sc1[:, dc, m:m + 1], bias=self.sh[:, dc, m:m + 1]),
                                    r=[Bp, self.B_mod], w=[Bh[hb]])
                            else:
                                S.op("dve", lambda e, p=p, q=q, dc=dc, s=s, m=m: e.tensor_scalar(
                                    out=hT[hb][:, dc, s * 128:(s + 1) * 128], in0=p[:, q * 128:(q + 1) * 128],
                                    scalar1=self.sc1[:, dc, m:m + 1], scalar2=self.sh[:, dc, m:m + 1],
                                    op0=ALU.mult, op1=ALU.add), r=[Bp, self.B_mod], w=[Bh[hb]])
                            evac += 1
                for grp, base, dstT, bofs in ((0, EQ, self.qda, 0), (1, EK, self.kda, 8)):
                    for h in range(4):
                        pa, Ba = next_pp()
                        pb, Bb = next_pp()
                        for k in range(8):
                            S.op("pe", lambda e, pa=pa, k=k, c0=base + h * 128: e.matmul(
                                pa[:, :ntok], wb[:, k, c0:c0 + 128], hT[hb][:, k, :ntok], start=(k == 0), stop=(k == 7)),
                                r=[Bw, Bh[hb]], w=[Ba])
                        for k in range(8):
                            S.op("pe", lambda e, pb=pb, k=k, c0=base + 512 + h * 128: e.matmul(
                                pb[:, :ntok], wb[:, k, c0:c0 + 128], hT[hb][:, k, :ntok], start=(k == 0), stop=(k == 7)),
                                r=[Bw, Bh[hb]], w=[Bb])
                        tj = (grp * 4 + h) % 2
                        S.op("dve", lambda e, pa=pa, tj=tj, c=bofs + h: e.scalar_tensor_tensor(
                            out=t1[tj][:, :ntok], in0=pa[:, :ntok], scalar=bcol[:, c:c + 1], in1=cosd[hb][:, :ntok],
                            op0=ALU.add, op1=ALU.mult), r=[Ba, Brt[hb], Bw], w=[Bt1[tj]])
                        S.op("dve", lambda e, pb=pb, tj=tj, c=bofs + 4 + h: e.scalar_tensor_tensor(
                            out=t2[tj][:, :ntok], in0=pb[:, :ntok], scalar=bcol[:, c:c + 1], in1=sind[hb][:, :ntok],
                            op0=ALU.add, op1=ALU.mult), r=[Bb, Brt[hb], Bw], w=[Bt2[tj]])
                        fj = foi % 4
                        foi += 1
                        S.op("pool", lambda e, tj=tj, fj=fj: e.tensor_tensor(
                            out=fo[fj][:, :ntok], in0=t1[tj][:, :ntok], in1=t2[tj][:, :ntok], op=ALU.add),
                            r=[Bt1[tj], Bt2[tj]], w=[Bfo[fj]])
                        S.dma("pool", dstT[h, :, t0:t0 + ntok], fo[fj][:, :ntok], r=[Bfo[fj]])
                pa, Ba = next_pp()
                pb, Bb = next_pp()
                for k in range(8):
                    S.op("pe", lambda e, pa=pa, k=k: e.matmul(
                        pa[0:32, :ntok], wb[:, k, EKR:EKR + 32], hT[hb][:, k, :ntok], start=(k == 0), stop=(k == 7)),
                        r=[Bw, Bh[hb]], w=[Ba])
                for k in range(8):
                    S.op("pe", lambda e, pb=pb, k=k: e.matmul(
                        pb[0:32, :ntok], wb[:, k, EKR + 32:EKR + 64], hT[hb][:, k, :ntok], start=(k == 0), stop=(k == 7)),
                        r=[Bw, Bh[hb]], w=[Bb])
                tj = 0
                S.op("dve", lambda e, pa=pa: e.scalar_tensor_tensor(
                    out=t1[tj][0:32, :ntok], in0=pa[0:32, :ntok], scalar=bcol[0:32, 16:17], in1=cosm[hb][0:32, :ntok],
                    op0=ALU.add, op1=ALU.mult), r=[Ba, Brt[hb], Bw], w=[Bt1[tj]])
                S.op("dve", lambda e, pb=pb: e.scalar_tensor_tensor(
                    out=t2[tj][0:32, :ntok], in0=pb[0:32, :ntok], scalar=bcol[0:32, 17:18], in1=sinm[hb][0:32, :ntok],
                    op0=ALU.add, op1=ALU.mult), r=[Bb, Brt[hb], Bw], w=[Bt2[tj]])
                fj = foi % 4
                foi += 1
                S.op("pool", lambda e, fj=fj: e.tensor_tensor(
                    out=fo[fj][0:32, :ntok], in0=t1[tj][0:32, :ntok], in1=t2[tj][0:32, :ntok], op=ALU.add),
                    r=[Bt1[tj], Bt2[tj]], w=[Bfo[fj]])
                S.dma("pool", self.krt[:, t0:t0 + ntok], fo[fj][0:32, :ntok], r=[Bfo[fj]])
                cb = blk % 2
                for s in range(nsub):
                    tt = (t0 // 128) + s
                    tok0 = t0 + s * 128
                    p, Bp = next_pp()
                    for k in range(8):
                        S.op("pe", lambda e, p=p, k=k, s=s: e.matmul(
                            p[:, :], hT[hb][:, k, s * 128:(s + 1) * 128], wb[:, k, EV_:EV_ + 512], start=(k == 0), stop=False),
                            r=[Bw, Bh[hb]], w=[Bp])
                    S.op("pe", lambda e, p=p: e.matmul(p[:, :], self.ones_b[0:1, :], brow[0:1, 0:512], start=False, stop=True),
                         r=[Bw, self.B_const], w=[Bp])
                    vj = tt % 2
                    S.op("act", lambda e, p=p, vj=vj: e.activation(
                        out=va[vj][:, :, 0:128], in_=p[:, :].rearrange("p (h c) -> p h c", h=4), func=AF.Copy),
                        r=[Bp], w=[Bva[vj]])
                    S.dma("pool", self.vda[:, :, tt, :].rearrange("h p c -> p h c"), va[vj][:], r=[Bva[vj]])
                    p, Bp = next_pp()
                    for k in range(8):
                        S.op("pe", lambda e, p=p, k=k, s=s: e.matmul(
                            p[:, 0:384], hT[hb][:, k, s * 128:(s + 1) * 128], wb[:, k, EC:EC + 384], start=(k == 0), stop=False),
                            r=[Bw, Bh[hb]], w=[Bp])
                    S.op("pe", lambda e, p=p: e.matmul(p[:, 0:384], self.ones_b[0:1, :], brow[0:1, 512:896], start=False, stop=True),
                         r=[Bw, self.B_const], w=[Bp])
                    cj = tt % 2
                    S.op("act", lambda e, p=p, cj=cj: e.activation(out=cq[cj][:], in_=p[:, 0:384], func=AF.Copy),
                         r=[Bp], w=[Bcq[cj]])
                    S.op("act", lambda e, cj=cj: e.activation(
                        out=junk[:, 0:256], in_=cq[cj][:, 0:256], func=AF.Square, accum_out=stat[cj][:, 0:1]),
                        r=[Bcq[cj]], w=[Bjunk, Bstat[cj]])
                    S.op("act", lambda e, cj=cj: e.activation(
                        out=junk[:, 0:128], in_=cq[cj][:, 256:384], func=AF.Square, accum_out=stat[cj][:, 1:2]),
                        r=[Bcq[cj]], w=[Bjunk, Bstat[cj]])
                    S.op("act", lambda e, cj=cj: e.activation(out=stat[cj][:, 2:3], in_=stat[cj][:, 0:1], func=AF.Sqrt,
                                                              scale=1.0 / 256, bias=eps_t[:, 0:1]), r=[Bstat[cj], Bw], w=[Bstat[cj]])
                    S.op("act", lambda e, cj=cj: e.activation(out=stat[cj][:, 3:4], in_=stat[cj][:, 1:2], func=AF.Sqrt,
                                                              scale=1.0 / 128, bias=eps_t[:, 0:1]), r=[Bstat[cj], Bw], w=[Bstat[cj]])
                    S.op("dve", lambda e, cj=cj: e.reciprocal(stat[cj][:, 4:6], stat[cj][:, 2:4]), r=[Bstat[cj]], w=[Bstat[cj]])
                    S.op("dve", lambda e, cj=cj: e.tensor_scalar(
                        out=cqn[cj][:, 0:256], in0=cq[cj][:, 0:256], scalar1=stat[cj][:, 4:5], scalar2=None, op0=ALU.mult),
                        r=[Bcq[cj], Bstat[cj]], w=[Bcqn[cj]])
                    S.op("pool", lambda e, cj=cj: e.tensor_scalar(
                        out=cqn[cj][:, 256:384], in0=cq[cj][:, 256:384], scalar1=stat[cj][:, 5:6], scalar2=None, op0=ALU.mult),
                        r=[Bcq[cj], Bstat[cj]], w=[Bcqn[cj]])
                    for rc in range(3):
                        S.op("pe", lambda e, rc=rc, cj=cj: e.transpose(
                            ppb[:, rc * 128:(rc + 1) * 128], cqn[cj][:, rc * 128:(rc + 1) * 128], self.ident_b[:]),
                            r=[Bcqn[cj], self.B_const], w=[Bppb])
                    S.op("dve", lambda e, s=s: e.tensor_copy(
                        cT[cb][:, :, s * 128:(s + 1) * 128], ppb[:, 0:384].rearrange("p (r t) -> p r t", r=3)),
                        r=[Bppb], w=[BcT[cb]])
                    gj = tt % 2
                    for n in range(2):
                        p, Bp = next_pp()
                        for k in range(8):
                            S.op("pe", lambda e, p=p, k=k, s=s, n=n: e.matmul(
                                p[:, :], hT[hb][:, k, s * 128:(s + 1) * 128], wb[:, k, EG + n * 512:EG + (n + 1) * 512],
                                start=(k == 0), stop=False), r=[Bw, Bh[hb]], w=[Bp])
                        S.op("pe", lambda e, p=p, n=n: e.matmul(
                            p[:, :], self.ones_b[0:1, :], brow[0:1, 896 + n * 512:896 + (n + 1) * 512], start=False, stop=True),
                            r=[Bw, self.B_const], w=[Bp])
                        S.op("act", lambda e, p=p, n=n, gj=gj: e.activation(
                            out=gt[gj][:, n * 512:(n + 1) * 512], in_=p[:, :], func=AF.Silu), r=[Bp], w=[Bgt[gj]])
                    S.dma("pool", self.gate[tok0:tok0 + 128, :], gt[gj][:], r=[Bgt[gj]])
                for h in range(8):
                    pa, Ba = next_pp()
                    pb, Bb = next_pp()
                    for rc in range(2):
                        S.op("pe", lambda e, pa=pa, rc=rc, h=h: e.matmul(
                            pa[0:96, :ntok], wuq[:, rc, h * 96:(h + 1) * 96], cT[cb][:, rc, :ntok], start=(rc == 0), stop=(rc == 1)),
                            r=[Bw, BcT[cb]], w=[Ba])
                    for rc in range(2):
                        S.op("pe", lambda e, pb=pb, rc=rc, h=h: e.matmul(
                            pb[0:96, :ntok], wuq[:, rc, 768 + h * 96:768 + (h + 1) * 96], cT[cb][:, rc, :ntok],
                            start=(rc == 0), stop=(rc == 1)), r=[Bw, BcT[cb]], w=[Bb])
                    tj = h % 2
                    fj = foi % 4
                    foi += 1
                    S.op("dve", lambda e, pa=pa, tj=tj: e.tensor_tensor(
                        out=t1[tj][64:96, :ntok], in0=pa[64:96, :ntok], in1=cosm[hb][64:96, :ntok], op=ALU.mult),
                        r=[Ba, Brt[hb]], w=[Bt1[tj]])
                    S.op("dve", lambda e, pb=pb, tj=tj: e.tensor_tensor(
                        out=t2[tj][64:96, :ntok], in0=pb[64:96, :ntok], in1=sinm[hb][64:96, :ntok], op=ALU.mult),
                        r=[Bb, Brt[hb]], w=[Bt2[tj]])
                    S.op("act", lambda e, pa=pa, fj=fj: e.activation(out=fo[fj][0:64, :ntok], in_=pa[0:64, :ntok], func=AF.Copy),
                         r=[Ba], w=[Bfo[fj]])
                    S.op("pool", lambda e, tj=tj, fj=fj: e.tensor_tensor(
                        out=fo[fj][64:96, :ntok], in0=t1[tj][64:96, :ntok], in1=t2[tj][64:96, :ntok], op=ALU.add),
                        r=[Bt1[tj], Bt2[tj]], w=[Bfo[fj]])
                    S.dma("pool", self.qm[h, :, t0:t0 + ntok], fo[fj][0:96, :ntok], r=[Bfo[fj]])
                for hp in range(4):
                    p, Bp = next_pp()
                    S.op("pe", lambda e, p=p, hp=hp: e.matmul(
                        p[:, :ntok], wukv[:, hp * 128:(hp + 1) * 128], cT[cb][:, 2, :ntok], start=True, stop=True),
                        r=[Bw, BcT[cb]], w=[Bp])
                    fj = foi % 4
                    foi += 1
                    S.op("act", lambda e, p=p, fj=fj: e.activation(out=fo[fj][:, :ntok], in_=p[:, :ntok], func=AF.Copy),
                         r=[Bp], w=[Bfo[fj]])
                    S.dma("pool", self.kmn[2 * hp, :, t0:t0 + ntok], fo[fj][0:64, :ntok], r=[Bfo[fj]])
                    S.dma("pool", self.kmn[2 * hp + 1, :, t0:t0 + ntok], fo[fj][64:128, :ntok], r=[Bfo[fj]])
                for s in range(nsub):
                    tt = (t0 // 128) + s
                    p, Bp = next_pp()
                    S.op("pe", lambda e, p=p, s=s: e.matmul(
                        p[:, :], cT[cb][:, 2, s * 128:(s + 1) * 128], wukv[:, 512:1024], start=True, stop=True),
                        r=[Bw, BcT[cb]], w=[Bp])
                    vj = tt % 2
                    S.op("act", lambda e, p=p, vj=vj: e.activation(
                        out=vma[vj][:, :, 0:64], in_=p[:, :].rearrange("p (h c) -> p h c", h=8), func=AF.Copy),
                        r=[Bp], w=[Bvma[vj]])
                    S.dma("pool", self.vm[:, :, tt, :].rearrange("h p c -> p h c"), vma[vj][:], r=[Bvma[vj]])
            S.barrier()

    def attn_core(self, st, name, KT, QT, VA, Bkv, kparts, nv, scale, qblocks, finalize, kslices=None):
        S = self.S
        nmap = len(kparts)
        nacc_per_bank = 512 // nv
        n_acc = nmap * 4
        n_acc_banks = (n_acc + nacc_per_bank - 1) // nacc_per_bank
        accb = [self.ps(st, f"{name}_acc{j}", [128, 512]) for j in range(n_acc_banks)]
        Bacc = S.bufs(name + "_acc", n_acc_banks)
        n_sc = 8 - n_acc_banks
        n_sc = min(n_sc, 4)
        scb = [self.ps(st, f"{name}_sc{j}", [128, 512]) for j in range(n_sc)]
        Bsc = S.bufs(name + "_sc", n_sc)
        NP = 4
        pt = [self.sb(st, f"{name}_pt{j}", [128, 512], BF16) for j in range(NP)]
        Bpt = S.bufs(name + "_pt", NP)

        def acc_ap(mi, sub):
            idx = mi * 4 + sub
            b = idx // nacc_per_bank
            o = (idx % nacc_per_bank) * nv
            return accb[b][:, o:o + nv], Bacc[b]

        sci = 0
        pti = 0
        for qbi, (q0, nq, ktiles) in enumerate(qblocks):
            nsub = nq // 128
            for b in range(n_acc_banks):
                S.op("pe", lambda e, b=b: e.matmul(accb[b][:, :], self.zeros_b[:, :], self.zeros_w[:, :], start=True, stop=False,
                                                   skip_group_check=True), r=[self.B_const], w=[Bacc[b]])
            units = [(kt, mi) for kt in ktiles for mi in range(nmap)]
            pend = []

            def emit_score(u):
                nonlocal sci
                kt, mi = u
                QTm, lo, hi = kparts[mi]
                j = sci % n_sc
                sci += 1
                S.op("pe", lambda e, j=j, lo=lo, hi=hi, kt=kt, QTm=QTm: e.matmul(
                    scb[j][:, :nq], KT[lo:hi, kt * 128:(kt + 1) * 128], QTm[lo:hi, q0:q0 + nq], start=True, stop=True),
                    r=[Bkv], w=[Bsc[j]])
                return j

            def emit_exp_pv(u, j, lastflag):
                nonlocal pti
                kt, mi = u
                pj = pti % NP
                pti += 1
                S.op("act", lambda e, j=j, pj=pj: e.activation(out=pt[pj][:, :nq], in_=scb[j][:, :nq], func=AF.Exp, scale=scale),
                     r=[Bsc[j]], w=[Bpt[pj]])
                for sub in range(nsub):
                    ap, Ba = acc_ap(mi, sub)
                    S.op("pe", lambda e, ap=ap, pj=pj, sub=sub, kt=kt: e.matmul(
                        ap, pt[pj][:, sub * 128:(sub + 1) * 128], VA[:, kt, :], start=False, stop=lastflag,
                        skip_group_check=True), r=[Bpt[pj], Bkv], w=[Ba])

            LOOK = min(n_sc - 1, 2)
            q = []
            for ui, u in enumerate(units):
                q.append((u, emit_score(u)))
                if len(q) > LOOK:
                    u0, j0 = q.pop(0)
                    emit_exp_pv(u0, j0, False)
            while q:
                u0, j0 = q.pop(0)
                emit_exp_pv(u0, j0, u0[0] == ktiles[-1])
            for sub in range(nsub):
                accs = [acc_ap(mi, sub) for mi in range(nmap)]
                finalize(qbi, q0, sub, accs)

    def qblocks_all(self):
        qb = []
        lat_k = list(range(NT))
        for b in range(LAT // 512):
            qb.append((b * 512, 512, lat_k))
        qb.append((LAT, CTX, [NT_LAT, NT_LAT + 1]))
        if self.qb_filter is not None:
            qb = [q for i, q in enumerate(qb) if i in self.qb_filter]
        return qb

    def qblocks_own(self):
        qb = []
        lat_k = list(range(NT))
        for b in range(LAT // 2 // 512):
            qb.append((b * 512, 512, lat_k))
        qb.append((LAT // 2, CTX, [NT_LAT, NT_LAT + 1]))
        if self.qb_filter is not None:
            qb = [q for i, q in enumerate(qb) if i in self.qb_filter]
        return qb

    def own_q(self, QO, QS, lo, hi, tmp, Bqs, Bqo, Btmp):
        S = self.S
        H = LAT // 2
        S.op("dve", lambda e: e.tensor_scalar(out=tmp[lo:hi, :], in0=QS[lo:hi, H:LAT], scalar1=self.pselt[lo:hi, 1:2], scalar2=None,
                                              op0=ALU.mult), r=[Bqs, self.B_const], w=[Btmp])
        S.op("dve", lambda e: e.scalar_tensor_tensor(out=QO[lo:hi, 0:H], in0=QS[lo:hi, 0:H], scalar=self.pselt[lo:hi, 0:1],
                                                     in1=tmp[lo:hi, :], op0=ALU.mult, op1=ALU.add),
             r=[Bqs, Btmp, self.B_const], w=[Bqo])
        S.op("act", lambda e: e.activation(out=QO[lo:hi, H:H + CTX], in_=QS[lo:hi, LAT:NTOK], func=AF.Copy), r=[Bqs], w=[Bqo])

    def y_dst(self, tok0, c0, c1):
        H = LAT // 2
        if tok0 < H:
            k, r = tok0 // 1024, tok0 % 1024
            return self.yown_t[k].ap()[r:r + 128, c0:c1]
        return self.ybuf[LAT + tok0 - H:LAT + tok0 - H + 128, c0:c1]

    def da_attention(self, i, l):
        S = self.S
        lam_init = 0.8 - 0.6 * math.exp(-0.3 * l)
        with contextlib.ExitStack() as st:
            KT = self.sb(st, "da_KT", [128, NTOK], BF16)
            NQO = LAT // 2 + CTX
            QT1 = self.sb(st, "da_QT1", [128, NQO], BF16)
            QT2 = self.sb(st, "da_QT2", [128, NQO], BF16)
            QS = self.sb(st, "da_QS", [128, NTOK], BF16)
            qtmp = self.sb(st, "da_qtmp", [128, LAT // 2], BF16)
            VA = self.sb(st, "da_VA", [128, NT, 129], BF16)
            Bkv = S.buf("da_kv")
            Bqs = S.buf("da_qs")
            Bqt = S.buf("da_qtmp")
            S.op("pool", lambda e: e.memset(QT1[64:128, :], 0.0), w=[Bkv])
            S.op("pool", lambda e: e.memset(QT2[0:64, :], 0.0), w=[Bkv])
            lamt = self.sb(st, "lamt", [128, 256])
            lamw = self.sb(st, "lamw", [128, 8])
            subg = self.sb(st, "subg", [128, 128])
            Bl = S.buf("lam")
            eps_t = self.sb(st, "da_eps", [128, 1])
            S.op("pool", lambda e: e.memset(eps_t[:], RMS_EPS), w=[Bl])
            S.dma("sp", lamt[:], self.ev_lam[i, :, :].to_broadcast([128, 256]), w=[Bl])
            S.dma("sp", subg[:], self.ev_subg[i, :, :].to_broadcast([128, 128]), w=[Bl])
            junk = self.sb(st, "da_junk", [128, 128])
            Bj = S.buf("da_junk")
            S.op("dve", lambda e: e.tensor_tensor(out=junk[:, 0:64], in0=lamt[:, 0:64], in1=lamt[:, 64:128], op=ALU.mult),
                 r=[Bl], w=[Bj])
            S.op("dve", lambda e: e.reduce_sum(out=lamw[:, 0:1], in_=junk[:, 0:64], axis=AX.X), r=[Bj], w=[Bl])
            S.op("dve", lambda e: e.tensor_tensor(out=junk[:, 64:128], in0=lamt[:, 128:192], in1=lamt[:, 192:256], op=ALU.mult),
                 r=[Bl], w=[Bj])
            S.op("dve", lambda e: e.reduce_sum(out=lamw[:, 1:2], in_=junk[:, 64:128], axis=AX.X), r=[Bj], w=[Bl])
            S.op("act", lambda e: e.activation(out=lamw[:, 2:4], in_=lamw[:, 0:2], func=AF.Exp), r=[Bl], w=[Bl])
            S.op("dve", lambda e: e.tensor_tensor(out=lamw[:, 4:5], in0=lamw[:, 3:4], in1=lamw[:, 2:3], op=ALU.subtract),
                 r=[Bl], w=[Bl])
            S.op("dve", lambda e: e.tensor_scalar(out=lamw[:, 5:6], in0=lamw[:, 4:5], scalar1=-lam_init, scalar2=None, op0=ALU.add),
                 r=[Bl], w=[Bl])
            NF = 3
            rr = [self.sb(st, f"da_rr{j}", [128, 8]) for j in range(NF)]
            ta = [self.sb(st, f"da_ta{j}", [128, 128]) for j in range(NF)]
            td = [self.sb(st, f"da_td{j}", [128, 128]) for j in range(NF)]
            to = [self.sb(st, f"da_to{j}", [128, 128], BF16) for j in range(NF)]
            Bf = S.bufs("da_fin", NF)
            Bto = S.bufs("da_to", NF)
            self._fi = 0
            for h in range(4):
                S.dma("sp", KT[:], self.kda[h, :, :], w=[Bkv])
                S.dma("act", QS[:], self.qda[h, :, :], w=[Bqs])
                S.dma("sp", VA[:], self.vda[h, :, :, :], w=[Bkv])
                self.own_q(QT1, QS, 0, 64, qtmp, Bqs, Bkv, Bqt)
                self.own_q(QT2, QS, 64, 128, qtmp, Bqs, Bkv, Bqt)

                def fin(qbi, q0, sub, accs, h=h):
                    j = self._fi % NF
                    self._fi += 1
                    (a1, B1), (a2, B2) = accs
                    S.op("dve", lambda e: e.reciprocal(rr[j][:, 0:1], a1[:, 128:129]), r=[B1], w=[Bf[j]])
                    S.op("dve", lambda e: e.reciprocal(rr[j][:, 1:2], a2[:, 128:129]), r=[B2], w=[Bf[j]])
                    S.op("dve", lambda e: e.tensor_tensor(out=rr[j][:, 2:3], in0=rr[j][:, 1:2], in1=lamw[:, 5:6], op=ALU.mult),
                         r=[Bf[j], Bl], w=[Bf[j]])
                    S.op("dve", lambda e: e.tensor_scalar(out=ta[j][:], in0=a1[:, 0:128], scalar1=rr[j][:, 0:1], scalar2=None,
                                                          op0=ALU.mult), r=[B1, Bf[j]], w=[Bf[j]])
                    S.op("dve", lambda e: e.scalar_tensor_tensor(out=td[j][:], in0=a2[:, 0:128], scalar=rr[j][:, 2:3], in1=ta[j][:],
                                                                 op0=ALU.mult, op1=ALU.add), r=[B2, Bf[j]], w=[Bf[j]])
                    S.op("act", lambda e: e.activation(out=ta[j][:], in_=td[j][:], func=AF.Square, accum_out=rr[j][:, 3:4]),
                         r=[Bf[j]], w=[Bf[j]])
                    S.op("act", lambda e: e.activation(out=rr[j][:, 4:5], in_=rr[j][:, 3:4], func=AF.Sqrt, scale=1.0 / 128,
                                                       bias=eps_t[:, 0:1]), r=[Bf[j], Bl], w=[Bf[j]])
                    S.op("dve", lambda e: e.reciprocal(rr[j][:, 5:6], rr[j][:, 4:5]), r=[Bf[j]], w=[Bf[j]])
                    S.op("pool", lambda e: e.tensor_scalar(out=td[j][:], in0=td[j][:], scalar1=rr[j][:, 5:6], scalar2=(1.0 - lam_init),
                                                           op0=ALU.mult, op1=ALU.mult), r=[Bf[j]], w=[Bf[j]])
                    S.op("pool", lambda e: e.tensor_tensor(out=to[j][:], in0=td[j][:], in1=subg[:], op=ALU.mult),
                         r=[Bf[j], Bl], w=[Bto[j]])
                    tok0 = q0 + sub * 128
                    S.dma("pool", self.y_dst(tok0, h * 128, (h + 1) * 128), to[j][:], r=[Bto[j]])

                with contextlib.ExitStack() as st2:
                    self.attn_core(st2, f"da{h}", KT, None, VA, Bkv, [(QT1, 0, 128), (QT2, 0, 128)], 129, DA_SCALE,
                                   self.qblocks_own(), fin)
                    S.barrier()
            S.barrier()

    def mla_attention(self, i):
        S = self.S
        with contextlib.ExitStack() as st:
            KT = self.sb(st, "ml_KT", [128, NTOK], BF16)
            QT = self.sb(st, "ml_QT", [128, LAT // 2 + CTX], BF16)
            QS = self.sb(st, "ml_QS", [128, NTOK], BF16)
            qtmp = self.sb(st, "ml_qtmp", [128, LAT // 2], BF16)
            VA = self.sb(st, "ml_VA", [128, NT, 65], BF16)
            Bkv = S.buf("ml_kv")
            Bqs = S.buf("ml_qs")
            Bqt = S.buf("ml_qtmp")
            NF = 3
            rr = [self.sb(st, f"ml_rr{j}", [128, 2]) for j in range(NF)]
            to = [self.sb(st, f"ml_to{j}", [128, 64], BF16) for j in range(NF)]
            Bto = S.bufs("ml_to", NF)
            self._fi = 0
            for h in range(8):
                S.dma("sp", KT[0:64, :], self.kmn[h, :, :], w=[Bkv])
                S.dma("sp", KT[64:96, :], self.krt[:, :], w=[Bkv])
                S.dma("act", QS[0:96, :], self.qm[h, :, :], w=[Bqs])
                S.dma("sp", VA[:], self.vm[h, :, :, :], w=[Bkv])
                self.own_q(QT, QS, 0, 96, qtmp, Bqs, Bkv, Bqt)

                def fin(qbi, q0, sub, accs, h=h):
                    j = self._fi % NF
                    self._fi += 1
                    (a1, B1), = accs
                    S.op("dve", lambda e: e.reciprocal(rr[j][:, 0:1], a1[:, 64:65]), r=[B1], w=[Bto[j]])
                    S.op("dve", lambda e: e.tensor_scalar(out=to[j][:], in0=a1[:, 0:64], scalar1=rr[j][:, 0:1], scalar2=None,
                                                          op0=ALU.mult), r=[B1, Bto[j]], w=[Bto[j]])
                    tok0 = q0 + sub * 128
                    S.dma("pool", self.y_dst(tok0, 512 + h * 64, 512 + (h + 1) * 64), to[j][:], r=[Bto[j]])

                with contextlib.ExitStack() as st2:
                    self.attn_core(st2, f"ml{h}", KT, None, VA, Bkv, [(QT, 0, 96)], 65, MLA_SCALE, self.qblocks_own(), fin)
                    S.barrier()
            S.barrier()

    def out_stage(self, wo_dram, src, dst, last, ysplit=False):
        S = self.S
        with contextlib.ExitStack() as st:
            wo = self.sb(st, "wo", [128, 8, D], BF16)
            Bw = S.buf("wo")
            pieces = [(wo[:, k, :], wo_dram[k * 128:(k + 1) * 128, :], None) for k in range(8)]
            self.load_cast(st, None, None, 128, D, pieces, Bw, name="wo")
            NB = 3
            yt = [self.sb(st, f"o_yt{j}", [128, D], BF16) for j in range(NB)]
            gt = [self.sb(st, f"o_gt{j}", [128, D], BF16) for j in range(NB)]
            xt = [self.sb(st, f"o_xt{j}", [128, D]) for j in range(NB)]
            yg = [self.sb(st, f"o_yg{j}", [128, D], BF16) for j in range(NB)]
            ygT = [self.sb(st, f"o_ygT{j}", [128, D], BF16) for j in range(NB)]
            rt = [self.sb(st, f"o_rt{j}", [128, D]) for j in range(NB)]
            ot = [self.sb(st, f"o_ot{j}", [128, D]) for j in range(NB)]
            stt = [self.sb(st, f"o_st{j}", [128, 16]) for j in range(NB)]
            By, Bg, Bx, Byg, BygT, Br, Bo, Bs = (S.bufs(n, NB) for n in ("o_yt", "o_gt", "o_xt", "o_yg", "o_ygT", "o_rt", "o_ot", "o_st"))
            ptr = [self.ps(st, f"o_ptr{j}", [128, D], BF16) for j in range(2)]
            Bptr = S.bufs("o_ptr", 2)
            pout = [self.ps(st, f"o_po{j}", [128, 512]) for j in range(4)]
            Bpo = S.bufs("o_po", 4)
            eps_t = self.sb(st, "o_eps", [128, 1])
            S.op("pool", lambda e: e.memset(eps_t[:], LN_EPS), w=[Bw])
            ntiles = NT_LAT if (last and self.final_out) else NT
            tiles = [t for t in range(ntiles) if self.tile_filter is None or t in self.tile_filter]

            def stage_a(t):
                j = t % NB
                tok0 = t * 128
                isctx = t >= NT_LAT
                G = self.g_c if isctx else self.g_l
                if ysplit and not isctx:
                    half, u = t // 32, t % 32
                    r0 = half * 1024 + (u % 8) * 128
                    ysrc = self.ygath_t[u // 8].ap()[r0:r0 + 128, :]
                else:
                    ysrc = self.ybuf[tok0:tok0 + 128, :]
                S.dma("sp", yt[j][:], ysrc, w=[By[j]])
                S.dma("act", gt[j][:], self.gate[tok0:tok0 + 128, :], w=[Bg[j]])
                S.dma("sp", xt[j][:], src[tok0:tok0 + 128, :], w=[Bx[j]])
                S.op("pool", lambda e, j=j: e.tensor_tensor(out=yg[j][:], in0=yt[j][:], in1=gt[j][:], op=ALU.mult),
                     r=[By[j], Bg[j]], w=[Byg[j]])
                pj = t % 2
                for ec in range(8):
                    S.op("pe", lambda e, ec=ec, j=j, pj=pj: e.transpose(
                        ptr[pj][:, ec * 128:(ec + 1) * 128], yg[j][:, ec * 128:(ec + 1) * 128], self.ident_b[:]),
                        r=[Byg[j], self.B_const], w=[Bptr[pj]])
                S.op("act", lambda e, j=j, pj=pj: e.activation(out=ygT[j][:], in_=ptr[pj][:], func=AF.Copy),
                     r=[Bptr[pj]], w=[BygT[j]])
                for n in range(2):
                    pn = (t % 2) * 2 + n
                    for ec in range(8):
                        S.op("pe", lambda e, ec=ec, n=n, pn=pn, j=j: e.matmul(
                            pout[pn][:, :], ygT[j][:, ec * 128:(ec + 1) * 128], wo[:, ec, n * 512:(n + 1) * 512],
                            start=(ec == 0), stop=(ec == 7)), r=[BygT[j], Bw], w=[Bpo[pn]])

            def stage_b(t):
                j = t % NB
                tok0 = t * 128
                isctx = t >= NT_LAT
                G = self.g_c if isctx else self.g_l
                for n in range(2):
                    pn = (t % 2) * 2 + n
                    S.op("dve", lambda e, n=n, pn=pn, j=j, G=G: e.tensor_tensor(
                        out=rt[j][:, n * 512:(n + 1) * 512], in0=pout[pn][:, :], in1=G[:, n * 512:(n + 1) * 512], op=ALU.mult),
                        r=[Bpo[pn], self.B_mod], w=[Br[j]])
                S.op("dve", lambda e, j=j: e.scalar_tensor_tensor(
                    out=rt[j][:], in0=xt[j][:], scalar=ALPHA, in1=rt[j][:], op0=ALU.mult, op1=ALU.add),
                    r=[Bx[j], Br[j]], w=[Br[j]])
                for n in range(2):
                    S.op("dve", lambda e, n=n, j=j: e.bn_stats(stt[j][:, n * 6:(n + 1) * 6], rt[j][:, n * 512:(n + 1) * 512]),
                         r=[Br[j]], w=[Bs[j]])
                S.op("dve", lambda e, j=j: e.bn_aggr(stt[j][:, 12:14], stt[j][:, 0:12]), r=[Bs[j]], w=[Bs[j]])
                S.op("act", lambda e, j=j: e.activation(out=stt[j][:, 14:15], in_=stt[j][:, 13:14], func=AF.Sqrt, scale=1.0,
                                                        bias=eps_t[:, 0:1]), r=[Bs[j], Bw], w=[Bs[j]])
                S.op("dve", lambda e, j=j: e.reciprocal(stt[j][:, 15:16], stt[j][:, 14:15]), r=[Bs[j]], w=[Bs[j]])
                S.op("dve", lambda e, j=j: e.tensor_scalar(
                    out=ot[j][:], in0=rt[j][:], scalar1=stt[j][:, 12:13], scalar2=stt[j][:, 15:16], op0=ALU.subtract, op1=ALU.mult),
                    r=[Br[j], Bs[j]], w=[Bo[j]])
                S.op("pool", lambda e, j=j: e.tensor_tensor(out=ot[j][:], in0=ot[j][:], in1=self.lng[:], op=ALU.mult),
                     r=[Bo[j], self.B_mod], w=[Bo[j]])
                S.op("pool", lambda e, j=j: e.tensor_tensor(out=ot[j][:], in0=ot[j][:], in1=self.lnb[:], op=ALU.add),
                     r=[Bo[j], self.B_mod], w=[Bo[j]])
                S.dma("pool", dst[tok0:tok0 + 128, :], ot[j][:], r=[Bo[j]])

            for idx, t in enumerate(tiles):
                if idx == 0:
                    stage_a(t)
                if idx + 1 < len(tiles):
                    stage_a(tiles[idx + 1])
                stage_b(t)
            S.barrier()

    def odd_layer(self, l, src, dst, last):
        i = l // 2
        self.adaln(l)
        self.odd_project(i, src)
        self.odd_conv(i)
        self.mlstm(i)
        self.mlstm_post(i)
        self.na_attention(i, last)
        self.out_stage(self.od_wo[i], src, dst, last)

    def odd_project(self, i, src):
        S = self.S
        with contextlib.ExitStack() as st:
            wb = self.sb(st, "od_wb", [128, 8, OD_COLS], BF16)
            bcol = self.sb(st, "od_bcol", [128, 16])
            brow_f = self.sb(st, "od_brow_f", [1, 2576])
            brow = self.sb(st, "od_brow", [1, 2576], BF16)
            fb = self.sb(st, "od_fb", [128, 8])
            Bw = S.buf("od_w")
            S.dma("sp", bcol[:], self.od_bcol[i, :, :], w=[Bw])
            S.dma("sp", brow_f[:], self.od_brow[i, :, :], w=[Bw])
            S.dma("sp", fb[:], self.od_fb[i, :, :].to_broadcast([128, 8]), w=[Bw])
            S.op("dve", lambda e: e.tensor_copy(brow[:], brow_f[:]), r=[Bw], w=[Bw])
            pieces = []
            for k in range(8):
                for c in range(4):
                    pieces.append((wb[:, k, c * 1156:(c + 1) * 1156],
                                   self.od_w[i, k * 128:(k + 1) * 128, c * 1156:(c + 1) * 1156], None))
            self.load_cast(st, None, None, 128, 1156, pieces, Bw, name="odw")
            NXS = 3
            xt = [self.sb(st, f"xt{j}", [128, D]) for j in range(NXS)]
            Bx = S.bufs("xt", NXS)
            hT = [self.sb(st, f"hT{j}", [128, 8, 512], BF16) for j in range(2)]
            Bh = S.bufs("hT", 2)
            ff = [self.sb(st, f"ff{j}", [128, 512]) for j in range(3)]
            Bff = S.bufs("ff", 3)
            fo = [self.sb(st, f"fo{j}", [128, 512], BF16) for j in range(3)]
            Bfo = S.bufs("fo", 3)
            va = [self.sb(st, f"va{j}", [128, 4, 129], BF16) for j in range(2)]
            Bva = S.bufs("va", 2)
            vna = [self.sb(st, f"vna{j}", [128, 8, 65], BF16) for j in range(2)]
            Bvna = S.bufs("vna", 2)
            ot = [self.sb(st, f"ot{j}", [128, 512]) for j in range(2)]
            Bot = S.bufs("ot", 2)
            gt = [self.sb(st, f"gt{j}", [128, D], BF16) for j in range(2)]
            Bgt = S.bufs("gt", 2)
            gs = [self.sb(st, f"gs{j}", [128, 4, 16]) for j in range(2)]
            gtmp = [self.sb(st, f"gtmp{j}", [128, 4, 8]) for j in range(2)]
            Bgs = S.bufs("gs", 2)
            pp = [self.ps(st, f"pp{j}", [128, 512]) for j in range(7)]
            pg = self.ps(st, "pg", [128, 512])
            Bpp = S.bufs("pp", 7)
            Bpg = S.buf("pg")
            for j in range(2):
                S.op("pool", lambda e, j=j: e.memset(va[j][:], 1.0), w=[Bva[j]])
                S.op("pool", lambda e, j=j: e.memset(vna[j][:], 1.0), w=[Bvna[j]])
            self._pp_i = 0

            def next_pp():
                j = self._pp_i % 7
                self._pp_i += 1
                return pp[j], Bpp[j]

            nblk = (NTOK + 511) // 512
            xi = 0
            evac = 0
            ffi = 0
            foi = 0
            for blk in range(nblk):
                t0 = blk * 512
                ntok = min(512, NTOK - t0)
                nsub = ntok // 128
                m = 0 if t0 < LAT else 1
                hb = blk % 2
                for s in range(nsub):
                    xj = xi % NXS
                    xi += 1
                    S.dma("sp", xt[xj][:], src[t0 + s * 128:t0 + (s + 1) * 128, :], w=[Bx[xj]])
                    for half in range(2):
                        p, Bp = next_pp()
                        for q in range(4):
                            dc = half * 4 + q
                            S.op("pe", lambda e, p=p, q=q, dc=dc, xj=xj: e.transpose(
                                p[:, q * 128:(q + 1) * 128], xt[xj][:, dc * 128:(dc + 1) * 128], self.ident_f[:]),
                                r=[Bx[xj], self.B_const], w=[Bp])
                        for q in range(4):
                            dc = half * 4 + q
                            if evac % 2 == 0:
                                S.op("act", lambda e, p=p, q=q, dc=dc, s=s, m=m: e.activation(
                                    out=hT[hb][:, dc, s * 128:(s + 1) * 128], in_=p[:, q * 128:(q + 1) * 128],
                                    func=AF.Identity, scale=self.sc1[:, dc, m:m + 1], bias=self.sh[:, dc, m:m + 1]),
                                    r=[Bp, self.B_mod], w=[Bh[hb]])
                            else:
                                S.op("dve", lambda e, p=p, q=q, dc=dc, s=s, m=m: e.tensor_scalar(
                                    out=hT[hb][:, dc, s * 128:(s + 1) * 128], in0=p[:, q * 128:(q + 1) * 128],
                                    scalar1=self.sc1[:, dc, m:m + 1], scalar2=self.sh[:, dc, m:m + 1],
                                    op0=ALU.mult, op1=ALU.add), r=[Bp, self.B_mod], w=[Bh[hb]])
                            evac += 1
                for c in range(16):
                    p, Bp = next_pp()
                    for k in range(8):
                        S.op("pe", lambda e, p=p, k=k, c=c: e.matmul(
                            p[:, :ntok], wb[:, k, c * 128:(c + 1) * 128], hT[hb][:, k, :ntok], start=(k == 0), stop=(k == 7)),
                            r=[Bw, Bh[hb]], w=[Bp])
                    if c < 8:
                        fj = ffi % 3
                        ffi += 1
                        if c % 2 == 0:
                            S.op("act", lambda e, p=p, c=c, fj=fj: e.activation(
                                out=ff[fj][:, :ntok], in_=p[:, :ntok], func=AF.Identity, bias=bcol[:, c:c + 1]),
                                r=[Bp, Bw], w=[Bff[fj]])
                        else:
                            S.op("dve", lambda e, p=p, c=c, fj=fj: e.tensor_scalar(
                                out=ff[fj][:, :ntok], in0=p[:, :ntok], scalar1=bcol[:, c:c + 1], scalar2=None, op0=ALU.add),
                                r=[Bp, Bw], w=[Bff[fj]])
                        S.dma("pool", self.qkpre[c, :, t0:t0 + ntok], ff[fj][:, :ntok], r=[Bff[fj]])
                    else:
                        fj = foi % 3
                        foi += 1
                        if c % 2 == 0:
                            S.op("act", lambda e, p=p, c=c, fj=fj: e.activation(
                                out=fo[fj][:, :ntok], in_=p[:, :ntok], func=AF.Identity, bias=bcol[:, c:c + 1]),
                                r=[Bp, Bw], w=[Bfo[fj]])
                        else:
                            S.op("dve", lambda e, p=p, c=c, fj=fj: e.tensor_scalar(
                                out=fo[fj][:, :ntok], in0=p[:, :ntok], scalar1=bcol[:, c:c + 1], scalar2=None, op0=ALU.add),
                                r=[Bp, Bw], w=[Bfo[fj]])
                        dd = self.qda if c < 12 else self.kda
                        S.dma("pool", dd[(c - 8) % 4, :, t0:t0 + ntok], fo[fj][:, :ntok], r=[Bfo[fj]])
                gb = blk % 2
                for s in range(nsub):
                    tt = (t0 // 128) + s
                    tok0 = t0 + s * 128

                    def tm(p, c0, n, b0, s=s):
                        for k in range(8):
                            S.op("pe", lambda e, k=k: e.matmul(
                                p[:, 0:n], hT[hb][:, k, s * 128:(s + 1) * 128], wb[:, k, c0:c0 + n], start=(k == 0), stop=False),
                                r=[Bw, Bh[hb]], w=[Bp])
                        S.op("pe", lambda e: e.matmul(p[:, 0:n], self.ones_b[0:1, :], brow[0:1, b0:b0 + n], start=False, stop=True),
                             r=[Bw, self.B_const], w=[Bp])
                    p, Bp = next_pp()
                    tm(p, OV, 512, 0)
                    vj = tt % 2
                    S.op("act", lambda e, p=p, vj=vj: e.activation(
                        out=va[vj][:, :, 0:128], in_=p[:, :].rearrange("p (h c) -> p h c", h=4), func=AF.Copy),
                        r=[Bp], w=[Bva[vj]])
                    S.dma("pool", self.vda[:, :, tt, :].rearrange("h p c -> p h c"), va[vj][:], r=[Bva[vj]])
                    p, Bp = next_pp()
                    tm(p, OO, 512, 512)
                    oj = tt % 2
                    S.op("act", lambda e, p=p, oj=oj: e.activation(out=ot[oj][:], in_=p[:, :], func=AF.Sigmoid), r=[Bp], w=[Bot[oj]])
                    S.dma("pool", self.og[tok0:tok0 + 128, :], ot[oj][:], r=[Bot[oj]])
                    Bp = Bpg
                    for k in range(8):
                        S.op("pe", lambda e, k=k, s=s: e.matmul(
                            pg[:, s * 16:(s + 1) * 16], hT[hb][:, k, s * 128:(s + 1) * 128], wb[:, k, OGT:OGT + 16],
                            start=(k == 0), stop=False), r=[Bw, Bh[hb]], w=[Bpg])
                    S.op("pe", lambda e, s=s: e.matmul(pg[:, s * 16:(s + 1) * 16], self.ones_b[0:1, :], brow[0:1, 1024:1040],
                                                       start=False, stop=True), r=[Bw, self.B_const], w=[Bpg])
                    p, Bp = next_pp()
                    tm(p, OVN, 512, 1040)
                    vj = tt % 2
                    S.op("act", lambda e, p=p, vj=vj: e.activation(
                        out=vna[vj][:, :, 0:64], in_=p[:, :].rearrange("p (h c) -> p h c", h=8), func=AF.Copy),
                        r=[Bp], w=[Bvna[vj]])
                    S.dma("pool", self.vm[:, :, tt, :].rearrange("h p c -> p h c"), vna[vj][:], r=[Bvna[vj]])
                    gj = tt % 2
                    for n in range(2):
                        p, Bp = next_pp()
                        tm(p, OG + n * 512, 512, 1552 + n * 512)
                        S.op("act", lambda e, p=p, n=n, gj=gj: e.activation(
                            out=gt[gj][:, n * 512:(n + 1) * 512], in_=p[:, :], func=AF.Silu), r=[Bp], w=[Bgt[gj]])
                    S.dma("pool", self.gate[tok0:tok0 + 128, :], gt[gj][:], r=[Bgt[gj]])
                pgv = pg[:, 0:nsub * 16].rearrange("p (s c) -> p s c", c=16)
                S.op("dve", lambda e, pgv=pgv: e.tensor_copy(gs[gb][:, 0:nsub, 0:8], pgv[:, :, 0:8]), r=[Bpg], w=[Bgs[gb]])
                for s in range(nsub):
                    S.op("dve", lambda e, s=s: e.tensor_tensor(out=gtmp[gb][:, s, :], in0=pg[:, s * 16 + 8:s * 16 + 16], in1=fb[:, :],
                                                               op=ALU.add), r=[Bpg, Bw], w=[Bgs[gb]])
                S.op("act", lambda e: e.activation(out=gtmp[gb][:, 0:nsub, :], in_=gtmp[gb][:, 0:nsub, :], func=AF.Exp, scale=-1.0),
                     r=[Bgs[gb]], w=[Bgs[gb]])
                S.op("act", lambda e: e.activation(out=gtmp[gb][:, 0:nsub, :], in_=gtmp[gb][:, 0:nsub, :], func=AF.Ln, bias=1.0),
                     r=[Bgs[gb]], w=[Bgs[gb]])
                S.op("dve", lambda e: e.tensor_scalar(out=gs[gb][:, 0:nsub, 8:16], in0=gtmp[gb][:, 0:nsub, :], scalar1=-1.0,
                                                      scalar2=None, op0=ALU.mult), r=[Bgs[gb]], w=[Bgs[gb]])
                tt0 = t0 // 128
                S.dma("pool", self.gates[:, tt0:tt0 + nsub, :], gs[gb][:, 0:nsub, :], r=[Bgs[gb]])
            S.barrier()

    def odd_conv(self, i):
        S = self.S
        SEG = 2048
        with contextlib.ExitStack() as st:
            cw = self.sb(st, "cv_w", [128, 8, 5])
            cb = self.sb(st, "cv_b", [128, 8])
            Bw = S.buf("cv_w")
            S.dma("sp", cw[:], self.od_convw[i, :, :, :], w=[Bw])
            S.dma("sp", cb[:], self.od_convb[i, :, :], w=[Bw])
            xin = [self.sb(st, f"cv_x{j}", [128, SEG + 4]) for j in range(2)]
            Bxin = S.bufs("cv_x", 2)
            acc = [self.sb(st, f"cv_a{j}", [128, SEG]) for j in range(2)]
            Bacc = S.bufs("cv_a", 2)
            tmp = [self.sb(st, f"cv_t{j}", [128, SEG]) for j in range(2)]
            Btmp = S.bufs("cv_t", 2)
            outb = [self.sb(st, f"cv_o{j}", [128, SEG], BF16) for j in range(2)]
            Bout = S.bufs("cv_o", 2)
            segs = [(a, SEG, 0, LAT) for a in range(0, LAT, SEG)] + [(LAT, CTX, LAT, LAT + CTX)]
            it = 0
            for c in range(8):
                for (t0, n, lo, hi) in segs:
                    j = it % 2
                    it += 1
                    a = max(t0 - 2, lo)
                    b = min(t0 + n + 2, hi)
                    if a > t0 - 2:
                        S.op("pool", lambda e, j=j: e.memset(xin[j][:, 0:2], 0.0), w=[Bxin[j]])
                    if b < t0 + n + 2:
                        S.op("pool", lambda e, j=j, n=n: e.memset(xin[j][:, n + 2:n + 4], 0.0), w=[Bxin[j]])
                    S.dma("sp", xin[j][:, a - (t0 - 2):b - (t0 - 2)], self.qkpre[c, :, a:b], w=[Bxin[j]])
                    S.op("act", lambda e, j=j, n=n, c=c: e.activation(
                        out=acc[j][:, 0:n], in_=xin[j][:, 0:n], func=AF.Copy, scale=cw[:, c, 0:1]),
                        r=[Bxin[j], Bw], w=[Bacc[j]])
                    for k in range(1, 5):
                        S.op("dve", lambda e, j=j, n=n, c=c, k=k: e.scalar_tensor_tensor(
                            out=acc[j][:, 0:n], in0=xin[j][:, k:k + n], scalar=cw[:, c, k:k + 1], in1=acc[j][:, 0:n],
                            op0=ALU.mult, op1=ALU.add), r=[Bxin[j], Bw, Bacc[j]], w=[Bacc[j]])
                    if True:
                        S.op("act", lambda e, j=j, n=n, c=c: e.activation(
                            out=outb[j][:, 0:n], in_=acc[j][:, 0:n], func=AF.Silu, bias=cb[:, c:c + 1]),
                            r=[Bacc[j], Bw], w=[Bout[j]])
                        dd = self.mq if c < 4 else self.mk
                        S.dma("pool", dd[c % 4, :, t0:t0 + n], outb[j][:, 0:n], r=[Bout[j]])
                    else:
                        S.op("act", lambda e, j=j, n=n, c=c: e.activation(
                            out=tmp[j][:, 0:n], in_=acc[j][:, 0:n], func=AF.Silu, bias=cb[:, c:c + 1]),
                            r=[Bacc[j], Bw], w=[Btmp[j]])
                        S.op("pool", lambda e, j=j, n=n: e.tensor_scalar(
                            out=outb[j][:, 0:n], in0=tmp[j][:, 0:n], scalar1=128 ** -0.5, scalar2=None, op0=ALU.mult),
                            r=[Btmp[j]], w=[Bout[j]])
                        S.dma("sp", self.mk[c - 4, :, t0:t0 + n], outb[j][:, 0:n], r=[Bout[j]])
            S.barrier()

    def mlstm(self, i):
        S = self.S
        with contextlib.ExitStack() as st:
            tri = self.sb(st, "ml_tri", [128, 3, 128])
            G = self.sb(st, "ml_G", [128, NT, 16])
            A = self.sb(st, "ml_A", [128, NT, 8])
            A2 = self.sb(st, "ml_A2", [128, NT, 8])
            Bq = self.sb(st, "ml_Bq", [128, NT, 8])
            EB = self.sb(st, "ml_EB", [128, NT, 8])
            Bg = S.buf("ml_g")
            S.dma("sp", tri[:], self.tri[:, :, :], w=[Bg])
            S.dma("sp", G[:], self.gates[:, :, :], w=[Bg])
            lnks = self.sb(st, "ml_lnks", [128, 1])
            S.op("pool", lambda e: e.memset(lnks[:], math.log(128 ** -0.5)), w=[Bg])
            with contextlib.ExitStack() as st1:
                pg = [self.ps(st1, f"ml_pg{j}", [128, 512]) for j in range(3)]
                Bpg = S.bufs("ml_pg", 3)
                tmpg = self.sb(st1, "ml_tmpg", [128, 32, 8])
                Btg = S.buf("ml_tmpg")
                grp = 0
                for g0 in range(0, NT, 32):
                    g1 = min(g0 + 32, NT)
                    pj = grp % 3
                    grp += 1
                    for t in range(g0, g1):
                        o = (t - g0) * 16
                        S.op("pe", lambda e, t=t, o=o, pj=pj: e.matmul(pg[pj][:, o:o + 4], tri[:, 0, :], G[:, t, 8:12], start=True, stop=True),
                             r=[Bg], w=[Bpg[pj]])
                        S.op("pe", lambda e, t=t, o=o, pj=pj: e.matmul(pg[pj][:, o + 4:o + 8], tri[:, 1, :], G[:, t, 12:16], start=True, stop=True),
                             r=[Bg], w=[Bpg[pj]])
                        S.op("pe", lambda e, t=t, o=o, pj=pj: e.matmul(pg[pj][:, o + 8:o + 16], tri[:, 2, :], G[:, t, 8:16], start=True, stop=True),
                             r=[Bg], w=[Bpg[pj]])
                    n = g1 - g0
                    pv = pg[pj][:, 0:n * 16].rearrange("p (t c) -> p t c", c=16)
                    S.op("dve", lambda e, pv=pv, g0=g0, g1=g1, n=n: e.tensor_tensor(
                        out=tmpg[:, 0:n, :], in0=G[:, g0:g1, 0:8], in1=pv[:, :, 0:8], op=ALU.subtract), r=[Bg, Bpg[pj]], w=[Btg])
                    S.op("act", lambda e, g0=g0, g1=g1, n=n: e.activation(out=A[:, g0:g1, :], in_=tmpg[:, 0:n, :], func=AF.Exp,
                                                                          bias=lnks[:, 0:1]), r=[Btg, Bg], w=[Bg])
                    S.op("act", lambda e, pv=pv, g0=g0, g1=g1: e.activation(out=Bq[:, g0:g1, :], in_=pv[:, :, 0:8], func=AF.Exp),
                         r=[Bpg[pj]], w=[Bg])
                    S.op("act", lambda e, pv=pv, g0=g0, g1=g1: e.activation(out=EB[:, g0:g1, :], in_=pv[:, :, 8:16], func=AF.Exp),
                         r=[Bpg[pj]], w=[Bg])
                    S.op("dve", lambda e, g0=g0, g1=g1: e.tensor_tensor(out=A2[:, g0:g1, :], in0=A[:, g0:g1, :], in1=EB[:, g0:g1, :],
                                                                        op=ALU.mult), r=[Bg], w=[Bg])
                S.barrier()
            qT = self.sb(st, "ml_qT", [128, NTOK], BF16)
            kT = self.sb(st, "ml_kT", [128, NTOK], BF16)
            V = self.sb(st, "ml_V", [128, NT, 129], BF16)
            KTOK = self.sb(st, "ml_KTOK", [128, NT, 128], BF16)
            Bin = S.buf("ml_in")
            Bkt = S.buf("ml_ktok")
            Cn = [self.sb(st, f"ml_Cn{d}", [128, 129]) for d in range(2)]
            Cnb = [[self.sb(st, f"ml_Cnb{d}{j}", [128, 129], BF16) for j in range(2)] for d in range(2)]
            BCn = S.bufs("ml_Cn", 2)
            BCnb = [S.bufs(f"ml_Cnb{d}", 2) for d in range(2)]
            NW = 3
            W = [self.sb(st, f"ml_W{j}", [128, 128], BF16) for j in range(NW)]
            BW = S.bufs("ml_W", NW)
            v2 = [self.sb(st, f"ml_v2{j}", [128, 129], BF16) for j in range(NW)]
            Bv2 = S.bufs("ml_v2", NW)
            ho = [self.sb(st, f"ml_ho{j}", [128, 128]) for j in range(NW)]
            Bho = S.bufs("ml_ho", NW)
            sm = [self.sb(st, f"ml_sm{j}", [128, 4]) for j in range(NW)]
            Bsm = S.bufs("ml_sm", NW)
            ps_s = [self.ps(st, f"ml_pss{j}", [128, 512]) for j in range(2)]
            ps_kv = [self.ps(st, f"ml_pkv{j}", [128, 512]) for j in range(2)]
            ps_n = [self.ps(st, f"ml_pn{j}", [128, 512]) for j in range(2)]
            ppb = self.ps(st, "ml_ppb", [128, 1024], BF16)
            Bps_s, Bps_kv, Bps_n = S.bufs("ml_pss", 2), S.bufs("ml_pkv", 2), S.bufs("ml_pn", 2)
            Bppb = S.buf("ml_ppb")
            order = [[NT_LAT, NT_LAT + 1] + list(range(NT_LAT)), [NT_LAT + 1, NT_LAT] + list(range(NT_LAT - 1, -1, -1))]
            wi = 0
            for h in range(4):
                S.dma("sp", qT[:], self.mq[h, :, :], w=[Bin])
                S.dma("sp", kT[:], self.mk[h, :, :], w=[Bin])
                S.dma("sp", V[:], self.vda[h, :, :, :], w=[Bin])
                for t0 in range(0, NT, 8):
                    n = min(8, NT - t0)
                    for q in range(n):
                        t = t0 + q
                        S.op("pe", lambda e, t=t, q=q: e.transpose(ppb[:, q * 128:(q + 1) * 128], kT[:, t * 128:(t + 1) * 128], self.ident_b[:]),
                             r=[Bin, self.B_const], w=[Bppb])
                    S.op("act", lambda e, t0=t0, n=n: e.activation(
                        out=KTOK[:, t0:t0 + n, :], in_=ppb[:, 0:n * 128].rearrange("p (t c) -> p t c", c=128), func=AF.Copy),
                        r=[Bppb], w=[Bkt])
                for d in range(2):
                    S.op("pool", lambda e, d=d: e.memset(Cn[d][:], 0.0), w=[BCn[d]])
                    S.op("pool", lambda e, d=d: e.memset(Cnb[d][0][:], 0.0), w=[BCnb[d][0]])
                for step in range(NT):
                    for d in range(2):
                        t = order[d][step]
                        hd = d * 4 + h
                        cur, nxt = step % 2, (step + 1) % 2
                        j = wi % NW
                        pj = wi % 2
                        wi += 1
                        tsl = slice(t * 128, (t + 1) * 128)
                        S.op("pe", lambda e, pj=pj, tsl=tsl: e.matmul(ps_s[pj][:, 0:128], kT[:, tsl], qT[:, tsl], start=True, stop=True),
                             r=[Bin], w=[Bps_s[pj]])
                        S.op("dve", lambda e, pj=pj, j=j, t=t, hd=hd, d=d: e.scalar_tensor_tensor(
                            out=W[j][:], in0=ps_s[pj][:, 0:128], scalar=A[:, t, hd:hd + 1], in1=tri[:, d, :],
                            op0=ALU.mult, op1=ALU.mult), r=[Bps_s[pj], Bg], w=[BW[j]])
                        S.op("act", lambda e, j=j, t=t, hd=hd: e.activation(
                            out=v2[j][:], in_=V[:, t, :], func=AF.Copy, scale=A2[:, t, hd:hd + 1]),
                            r=[Bin, Bg], w=[Bv2[j]])
                        S.op("pe", lambda e, pj=pj, j=j, t=t: e.matmul(ps_kv[pj][:, 0:129], KTOK[:, t, :], v2[j][:], start=True, stop=True),
                             r=[Bkt, Bv2[j]], w=[Bps_kv[pj]])
                        S.op("pe", lambda e, pj=pj, j=j, t=t: e.matmul(ps_n[pj][:, 0:129], W[j][:], V[:, t, :], start=True, stop=False),
                             r=[BW[j], Bin], w=[Bps_n[pj]])
                        S.op("pe", lambda e, pj=pj, tsl=tsl, d=d, cur=cur: e.matmul(ps_n[pj][:, 0:129], qT[:, tsl], Cnb[d][cur][:], start=False, stop=True),
                             r=[Bin, BCnb[d][cur]], w=[Bps_n[pj]])
                        S.op("dve", lambda e, pj=pj, d=d, t=t, hd=hd: e.scalar_tensor_tensor(
                            out=Cn[d][:], in0=Cn[d][:], scalar=EB[:, t, hd:hd + 1], in1=ps_kv[pj][:, 0:129],
                            op0=ALU.mult, op1=ALU.add), r=[BCn[d], Bps_kv[pj], Bg], w=[BCn[d]])
                        S.op("act", lambda e, d=d, nxt=nxt: e.activation(out=Cnb[d][nxt][:], in_=Cn[d][:], func=AF.Copy),
                             r=[BCn[d]], w=[BCnb[d][nxt]])
                        S.op("dve", lambda e, pj=pj, j=j, t=t, hd=hd: e.tensor_tensor(
                            out=sm[j][:, 0:1], in0=ps_n[pj][:, 128:129], in1=Bq[:, t, hd:hd + 1], op=ALU.mult),
                            r=[Bps_n[pj], Bg], w=[Bsm[j]])
                        S.op("dve", lambda e, j=j: e.scalar_tensor_tensor(out=sm[j][:, 1:2], in0=sm[j][:, 0:1], scalar=-1.0,
                                                                          in1=sm[j][:, 0:1], op0=ALU.mult, op1=ALU.max),
                             r=[Bsm[j]], w=[Bsm[j]])
                        S.op("dve", lambda e, j=j: e.tensor_scalar(out=sm[j][:, 1:2], in0=sm[j][:, 1:2], scalar1=1.0, scalar2=None,
                                                                   op0=ALU.max), r=[Bsm[j]], w=[Bsm[j]])
                        S.op("dve", lambda e, j=j: e.reciprocal(sm[j][:, 2:3], sm[j][:, 1:2]), r=[Bsm[j]], w=[Bsm[j]])
                        S.op("dve", lambda e, j=j, t=t, hd=hd: e.tensor_tensor(
                            out=sm[j][:, 3:4], in0=sm[j][:, 2:3], in1=Bq[:, t, hd:hd + 1], op=ALU.mult), r=[Bsm[j], Bg], w=[Bsm[j]])
                        S.op("dve", lambda e, pj=pj, j=j: e.tensor_scalar(
                            out=ho[j][:], in0=ps_n[pj][:, 0:128], scalar1=sm[j][:, 3:4], scalar2=None, op0=ALU.mult),
                            r=[Bps_n[pj], Bsm[j]], w=[Bho[j]])
                        S.dma("sp", self.hfb[d, t * 128:(t + 1) * 128, h * 128:(h + 1) * 128], ho[j][:], r=[Bho[j]])
                S.barrier()
            S.barrier()

    def mlstm_post(self, i):
        S = self.S
        with contextlib.ExitStack() as st:
            ng = self.sb(st, "mp_ng", [128, 512])
            Bw = S.buf("mp_w")
            S.dma("sp", ng[:], self.od_ng[i, :, :].to_broadcast([128, 512]), w=[Bw])
            eps_t = self.sb(st, "mp_eps", [128, 1])
            S.op("pool", lambda e: e.memset(eps_t[:], LN_EPS), w=[Bw])
            NB = 3
            hf = [self.sb(st, f"mp_hf{j}", [128, 512]) for j in range(NB)]
            hb = [self.sb(st, f"mp_hb{j}", [128, 512]) for j in range(NB)]
            og = [self.sb(st, f"mp_og{j}", [128, 512]) for j in range(NB)]
            hs = [self.sb(st, f"mp_hs{j}", [128, 512]) for j in range(NB)]
            yo = [self.sb(st, f"mp_yo{j}", [128, 512]) for j in range(NB)]
            yob = [self.sb(st, f"mp_yob{j}", [128, 512], BF16) for j in range(NB)]
            stt = [self.sb(st, f"mp_st{j}", [128, 4, 12]) for j in range(NB)]
            Bhf, Bhb, Bog, Bhs, Byo, Bst = (S.bufs(n, NB) for n in ("mp_hf", "mp_hb", "mp_og", "mp_hs", "mp_yo", "mp_st"))
            NB = 3

            def post_a(t):
                j = t % NB
                tok0 = t * 128
                S.dma("sp", hf[j][:], self.hfb[0, tok0:tok0 + 128, :], w=[Bhf[j]])
                S.dma("act", hb[j][:], self.hfb[1, tok0:tok0 + 128, :], w=[Bhb[j]])
                S.dma("pool", og[j][:], self.og[tok0:tok0 + 128, :], w=[Bog[j]])
                S.op("pool", lambda e, j=j: e.tensor_tensor(out=hs[j][:], in0=hf[j][:], in1=hb[j][:], op=ALU.add),
                     r=[Bhf[j], Bhb[j]], w=[Bhs[j]])
                S.op("pool", lambda e, j=j: e.tensor_tensor(out=og[j][:], in0=og[j][:], in1=ng[:], op=ALU.mult),
                     r=[Bog[j], Bw], w=[Bog[j]])

            def post_b(t):
                j = t % NB
                tok0 = t * 128
                for h in range(4):
                    S.op("dve", lambda e, j=j, h=h: e.bn_stats(stt[j][:, h, 0:6], hs[j][:, h * 128:(h + 1) * 128]),
                         r=[Bhs[j]], w=[Bst[j]])
                    S.op("dve", lambda e, j=j, h=h: e.bn_aggr(stt[j][:, h, 6:8], stt[j][:, h, 0:6]), r=[Bst[j]], w=[Bst[j]])
                S.op("act", lambda e, j=j: e.activation(out=stt[j][:, :, 8:9], in_=stt[j][:, :, 7:8], func=AF.Sqrt, scale=1.0,
                                                        bias=eps_t[:, 0:1]), r=[Bst[j], Bw], w=[Bst[j]])
                S.op("dve", lambda e, j=j: e.reciprocal(stt[j][:, :, 9:10], stt[j][:, :, 8:9]), r=[Bst[j]], w=[Bst[j]])
                for h in range(4):
                    S.op("dve", lambda e, j=j, h=h: e.tensor_scalar(
                        out=yo[j][:, h * 128:(h + 1) * 128], in0=hs[j][:, h * 128:(h + 1) * 128], scalar1=stt[j][:, h, 6:7],
                        scalar2=stt[j][:, h, 9:10], op0=ALU.subtract, op1=ALU.mult), r=[Bhs[j], Bst[j]], w=[Byo[j]])
                S.op("pool", lambda e, j=j: e.tensor_tensor(out=yob[j][:], in0=yo[j][:], in1=og[j][:], op=ALU.mult),
                     r=[Byo[j], Bog[j]], w=[Byo[j]])
                S.dma("sp", self.ybuf[tok0:tok0 + 128, 0:512], yob[j][:], r=[Byo[j]])

            for t in range(NT):
                if t == 0:
                    post_a(0)
                if t + 1 < NT:
                    post_a(t + 1)
                post_b(t)
            S.barrier()

    def na_attention(self, i, last):
        S = self.S
        with contextlib.ExitStack() as st:
            QNe = self.sb(st, "na_Qe", [128, NTOK], BF16)
            QNo = self.sb(st, "na_Qo", [128, NTOK], BF16)
            KN = self.sb(st, "na_K", [128, NTOK], BF16)
            VN = self.sb(st, "na_V", [128, NT, 65], BF16)
            MB = self.sb(st, "na_MB", [128, NA_NVAR, 128], BF16)
            Bqk = S.buf("na_qk")
            Bv = S.buf("na_v")
            Bmb = S.buf("na_mb")
            stg = [self.sb(st, f"na_stg{j}", [128, 7, 128]) for j in range(2)]
            Bstg = S.bufs("na_stg", 2)
            NP = 3
            ptA = [self.sb(st, f"na_ptA{j}", [128, 512], BF16) for j in range(NP)]
            ptB = [self.sb(st, f"na_ptB{j}", [128, 384], BF16) for j in range(NP)]
            BptA, BptB = S.bufs("na_ptA", NP), S.bufs("na_ptB", NP)
            rr = [self.sb(st, f"na_rr{j}", [128, 2]) for j in range(NP)]
            to = [self.sb(st, f"na_to{j}", [128, 64], BF16) for j in range(NP)]
            Bto = S.bufs("na_to", NP)
            psA = [self.ps(st, f"na_psA{j}", [128, 512]) for j in range(2)]
            psB = [self.ps(st, f"na_psB{j}", [128, 512]) for j in range(2)]
            acc = [self.ps(st, f"na_acc{j}", [128, 512]) for j in range(2)]
            BpsA, BpsB, Bacc = S.bufs("na_psA", 2), S.bufs("na_psB", 2), S.bufs("na_acc", 2)
            it = 0
            qtiles = list(range(NT_LAT)) + ([] if (last and self.final_out) else [NT_LAT, NT_LAT + 1])
            if self.tile_filter is not None:
                qtiles = [t for t in qtiles if t in self.tile_filter]
            for h in range(8):
                lo = (h % 2) * 64
                if h == 0:
                    S.op("pool", lambda e: e.memset(QNe[64:128, :], 0.0), w=[Bqk])
                    S.op("pool", lambda e: e.memset(QNo[0:64, :], 0.0), w=[Bqk])
                if h % 2 == 0:
                    S.dma("sp", QNe[0:64, :], self.qda[h // 2, 0:64, :], w=[Bqk])
                    S.dma("sp", QNo[64:128, :], self.qda[h // 2, 64:128, :], w=[Bqk])
                    S.dma("sp", KN[:], self.kda[h // 2, :, :], w=[Bqk])
                QN = QNe if h % 2 == 0 else QNo
                S.dma("sp", VN[:], self.vm[h, :, :, :], w=[Bv])
                for v0 in range(0, NA_NVAR, 7):
                    n = min(7, NA_NVAR - v0)
                    sj = (v0 // 7) % 2
                    S.dma("sp", stg[sj][:, 0:n, :], self.od_nat[i, h, :, v0:v0 + n, :], w=[Bstg[sj]])
                    S.op("pool", lambda e, sj=sj, n=n, v0=v0: e.tensor_scalar(
                        out=MB[:, v0:v0 + n, :], in0=stg[sj][:, 0:n, :], scalar1=1.0 / NA_SCALE, scalar2=None, op0=ALU.mult),
                        r=[Bstg[sj]], w=[Bmb])
                for j in qtiles:
                    pj = it % 2
                    tj = it % NP
                    it += 1
                    qsl = slice(j * 128, (j + 1) * 128)
                    if j < NT_LAT:
                        slots = [(kt, var) for (kt, var) in NA_PLAN[j]] + [(NT_LAT, None), (NT_LAT + 1, None)]
                    else:
                        slots = [(NT_LAT, None), (NT_LAT + 1, None)]
                    nA = min(4, len(slots))
                    nB = len(slots) - nA
                    for si, (kt, var) in enumerate(slots):
                        if si < 4:
                            dst, Bd = psA[pj][:, si * 128:(si + 1) * 128], BpsA[pj]
                        else:
                            dst, Bd = psB[pj][:, (si - 4) * 128:(si - 3) * 128], BpsB[pj]
                        S.op("pe", lambda e, dst=dst, kt=kt, var=var, QN=QN: e.matmul(
                            dst, KN[:, kt * 128:(kt + 1) * 128], QN[:, qsl], start=True, stop=(var is None)),
                            r=[Bqk], w=[Bd])
                        if var is not None:
                            S.op("pe", lambda e, dst=dst, var=var: e.matmul(dst, self.ident_b[:], MB[:, var, :], start=False, stop=True),
                                 r=[Bmb, self.B_const], w=[Bd])
                    S.op("act", lambda e, pj=pj, tj=tj, nA=nA: e.activation(out=ptA[tj][:, 0:nA * 128], in_=psA[pj][:, 0:nA * 128],
                                                                            func=AF.Exp, scale=NA_SCALE), r=[BpsA[pj]], w=[BptA[tj]])
                    if nB > 0:
                        S.op("act", lambda e, pj=pj, tj=tj, nB=nB: e.activation(out=ptB[tj][:, 0:nB * 128], in_=psB[pj][:, 0:nB * 128],
                                                                                func=AF.Exp, scale=NA_SCALE), r=[BpsB[pj]], w=[BptB[tj]])
                    for si, (kt, var) in enumerate(slots):
                        if si < 4:
                            lhs, Bl = ptA[tj][:, si * 128:(si + 1) * 128], BptA[tj]
                        else:
                            lhs, Bl = ptB[tj][:, (si - 4) * 128:(si - 3) * 128], BptB[tj]
                        S.op("pe", lambda e, lhs=lhs, kt=kt, si=si, pj=pj: e.matmul(
                            acc[pj][:, 0:65], lhs, VN[:, kt, :], start=(si == 0), stop=(si == len(slots) - 1)),
                            r=[Bl, Bv], w=[Bacc[pj]])
                    S.op("dve", lambda e, pj=pj, tj=tj: e.reciprocal(rr[tj][:, 0:1], acc[pj][:, 64:65]), r=[Bacc[pj]], w=[Bto[tj]])
                    S.op("dve", lambda e, pj=pj, tj=tj: e.tensor_scalar(out=to[tj][:], in0=acc[pj][:, 0:64], scalar1=rr[tj][:, 0:1],
                                                                        scalar2=None, op0=ALU.mult), r=[Bacc[pj], Bto[tj]], w=[Bto[tj]])
                    S.dma("pool", self.ybuf[j * 128:(j + 1) * 128, 512 + h * 64:512 + (h + 1) * 64], to[tj][:], r=[Bto[tj]])
            S.barrier()


def _swap64(cols):
    return np.concatenate([cols[32:64], cols[0:32]])


def _rope_tables():
    t = np.arange(LAT)
    row = (t // GRID_W).astype(np.float32)
    col = (t % GRID_W).astype(np.float32)

    def tab(dim, nrows_pattern):
        n_freq = dim // 4
        freqs = (10000.0 ** (-np.arange(n_freq, dtype=np.float32) / n_freq)).astype(np.float32)
        ang = np.concatenate([row[:, None] * freqs, col[:, None] * freqs], axis=-1).astype(np.float32)
        cos = np.cos(ang).astype(np.float32).T
        sin = np.sin(ang).astype(np.float32).T
        half = dim // 2
        c = np.concatenate([cos, cos], 0)
        s = np.concatenate([-sin, sin], 0)
        c = np.concatenate([c, np.ones((dim, CTX), np.float32)], 1)
        s = np.concatenate([s, np.zeros((dim, CTX), np.float32)], 1)
        return c, s

    c64, s64 = tab(64, None)
    c32, s32 = tab(32, None)
    rope_da = np.stack([np.concatenate([c64, c64], 0), np.concatenate([s64, s64], 0)]).astype(np.float32)
    ml_c = np.ones((128, NTOK), np.float32)
    ml_s = np.zeros((128, NTOK), np.float32)
    ml_c[0:32] = c32
    ml_s[0:32] = s32
    ml_c[64:96] = c32
    ml_s[64:96] = s32
    rope_ml = np.stack([ml_c, ml_s]).astype(np.float32)
    return rope_da, rope_ml


def _even_layout(inp):
    ev_w_in, ev_b_in = inp["ev_w_in"], inp["ev_b_in"]
    o_q1, o_q2, o_k1, o_k2, o_v, o_cq, o_ckv, o_kr, o_g = 0, 256, 512, 768, 1024, 1536, 1792, 1920, 1952
    cols = []
    for (a, b) in ((o_q1, o_q2), (o_k1, o_k2)):
        main = []
        sw = []
        for h in range(4):
            c1 = np.arange(a + h * 64, a + (h + 1) * 64)
            c2 = np.arange(b + h * 64, b + (h + 1) * 64)
            main += [c1, c2]
            sw += [_swap64(c1), _swap64(c2)]
        cols += main + sw
    kr = np.arange(o_kr, o_kr + 32)
    cols += [kr, np.concatenate([kr[16:], kr[:16]])]
    cols += [np.arange(o_v, o_v + 512), np.arange(o_cq, o_cq + 384), np.arange(o_g, o_g + 1024)]
    perm = np.concatenate(cols)
    assert perm.shape[0] == EV_COLS
    ev_w = np.ascontiguousarray(ev_w_in[:, :, perm])
    bp = ev_b_in[:, perm]
    bcol = np.zeros((2, 128, 18), np.float32)
    for g in range(16):
        bcol[:, :, g] = bp[:, g * 128:(g + 1) * 128]
    bcol[:, 0:32, 16] = bp[:, EKR:EKR + 32]
    bcol[:, 0:32, 17] = bp[:, EKR + 32:EKR + 64]
    brow = np.ascontiguousarray(bp[:, EV_:EV_ + 1920])[:, None, :]
    wuq = inp["mla_w_uq"]
    swc = []
    for h in range(8):
        base = h * 96
        c = np.arange(base, base + 96)
        r = c[64:96]
        swc.append(np.concatenate([c[0:64], r[16:], r[:16]]))
    swc = np.concatenate(swc)
    ev_wuq = np.ascontiguousarray(np.concatenate([wuq, wuq[:, :, swc]], axis=2))
    wukv = inp["mla_w_ukv"]
    nope = np.concatenate([np.arange(h * 128, h * 128 + 64) for h in range(8)])
    vv = np.concatenate([np.arange(h * 128 + 64, h * 128 + 128) for h in range(8)])
    ev_wukv = np.ascontiguousarray(wukv[:, :, np.concatenate([nope, vv])])
    return dict(
        ev_w=ev_w, ev_bcol=bcol, ev_brow=np.ascontiguousarray(brow),
        ev_lam=np.ascontiguousarray(inp["da_lambda"].reshape(2, 1, 256)),
        ev_subg=np.ascontiguousarray(inp["da_subln_g"].reshape(2, 1, 128)),
        ev_qg=np.ascontiguousarray(inp["mla_q_norm_g"].reshape(2, 2, 128).transpose(0, 2, 1)),
        ev_kvg=np.ascontiguousarray(inp["mla_kv_norm_g"].reshape(2, 128, 1)),
        ev_wuq=ev_wuq, ev_wukv=ev_wukv, ev_wo=np.ascontiguousarray(inp["ev_w_out"]),
    )


def _odd_layout(inp):
    w, b = inp["od_w_in"], inp["od_b_in"]
    g0 = 2048
    gates = np.concatenate([g0 + j * 4 + np.arange(4) for j in (0, 2, 1, 3)])
    perm = np.concatenate([np.arange(0, 1024), np.arange(2064, 2576), np.arange(2576, 3088), np.arange(1024, 1536),
                           np.arange(1536, 2048), gates, np.arange(3088, 3600), np.arange(3600, 4624)])
    assert perm.shape[0] == OD_COLS
    od_w = np.ascontiguousarray(w[:, :, perm])
    bp = b[:, perm]
    bcol = np.ascontiguousarray(bp[:, 0:2048].reshape(2, 16, 128).transpose(0, 2, 1))
    brow = np.ascontiguousarray(bp[:, 2048:])[:, None, :]
    convw = np.ascontiguousarray(inp["ml_conv_w"].reshape(2, 5, 8, 128).transpose(0, 3, 2, 1))
    convb = np.ascontiguousarray(inp["ml_conv_b"].reshape(2, 8, 128).transpose(0, 2, 1))
    fb = np.ascontiguousarray(inp["ml_f_bias"].reshape(2, 1, 8))
    ng = np.ascontiguousarray(inp["ml_norm_g"].reshape(2, 1, 512))
    rpb = inp["na_rpb"]
    nat = np.full((2, 8, 128, NA_NVAR, 128), NEG, np.float32)
    k = np.arange(128)
    krl, kc = k // 64, k % 64
    q = np.arange(128)
    qrl, qc = q // 64, q % 64
    cs = np.clip(qc - NA_COLS // 2, 0, GRID_W - NA_COLS)
    for (dk, o0, o1), vid in NA_VARIANTS.items():
        rel = (2 * dk + krl[:, None]) - qrl[None, :]
        off = np.where(qrl[None, :] == 0, o0, o1)
        valid_r = (rel >= off) & (rel <= off + NA_ROWS - 1)
        valid_c = (kc[:, None] >= cs[None, :]) & (kc[:, None] <= cs[None, :] + NA_COLS - 1)
        valid = valid_r & valid_c
        ridx = np.clip(rel + NA_ROWS - 1, 0, 2 * NA_ROWS - 2)
        cidx = np.clip(kc[:, None] - qc[None, :] + NA_COLS - 1, 0, 2 * NA_COLS - 2)
        tab = rpb[:, :, ridx, cidx]
        nat[:, :, :, vid, :] = np.where(valid[None, None], tab, np.float32(NEG))
    tri = np.stack([np.triu(np.ones((128, 128), np.float32)), np.tril(np.ones((128, 128), np.float32)),
                    np.ones((128, 128), np.float32)], 1)
    return dict(od_w=od_w, od_bcol=bcol, od_brow=np.ascontiguousarray(brow), od_convw=convw, od_convb=convb, od_fb=fb,
                od_ng=ng, od_nat=nat, od_wo=np.ascontiguousarray(inp["od_w_out"]), tri=np.ascontiguousarray(tri))


def make_in_maps(inp, batches):
    inp = {k: np.asarray(v, dtype=np.float32) for k, v in inp.items()}
    rope_da, rope_ml = _rope_tables()
    common = dict(
        ada_w=inp["ada_w"], ada_b=inp["ada_b"], ln_g=inp["ln_g"], ln_b=inp["ln_b"],
        ident=np.eye(128, dtype=np.float32),
        sel=np.concatenate([np.stack([np.ones(128), np.zeros(128)]), np.stack([np.zeros(128), np.ones(128)])], 1).astype(np.float32),
        rope_da=rope_da, rope_ml=rope_ml,
    )
    common.update(_even_layout(inp))
    common.update(_odd_layout(inp))
    maps = []
    for b in batches:
        m = dict(common)
        m["x_in"] = np.ascontiguousarray(np.concatenate([inp["x"][b], inp["ctx"][b]], 0))
        cc = np.stack([inp["c"][b], inp["c_ctx"]], -1)
        m["cc"] = np.ascontiguousarray(cc.reshape(8, 128, 2).transpose(1, 0, 2))
        par = len(maps) % 2
        ps = np.zeros((128, 2), np.float32)
        ps[:, par] = 1.0
        m["psel"] = ps
        maps.append(m)
    return maps


_PROG_CACHE = {}


def kernel(**inputs):
    batches = [c // 2 for c in range(8)]
    in_maps = make_in_maps(inputs, batches)
    prog = Prog()
    nc = prog.build()
    res = run_bass_kernel_spmd(nc, in_maps, core_ids=list(range(8)))
    out = np.stack([res.results[2 * b]["y"] for b in range(4)], 0)
    return out.astype(np.float32)
```

```python
import contextlib
import math
import numpy as np
import concourse.bass as bass
import concourse.mybir as mybir
from concourse.bass_utils import run_bass_kernel_spmd

F32 = mybir.dt.float32
BF16 = mybir.dt.bfloat16
AF = mybir.ActivationFunctionType
ALU = mybir.AluOpType
AX = mybir.AxisListType

D = 1024
LAT = 8192
CTX = 256
NTOK = LAT + CTX
NT = NTOK // 128
NT_LAT = LAT // 128
DEPTH = 4
GRID_W = 64
ALPHA = (2.0 * DEPTH) ** 0.25
LN_EPS = 1e-5
RMS_EPS = 1e-6
DA_SCALE = 64 ** -0.5
MLA_SCALE = 96 ** -0.5
NA_SCALE = 64 ** -0.5
EV_COLS = 4032
EQ, EK, EKR, EV_, EC, EG = 0, 1024, 2048, 2112, 2624, 3008
OD_COLS = 4624
OQK, OQN, OKN, OV, OO, OGT, OVN, OG = 0, 1024, 1536, 2048, 2560, 3072, 3088, 3600
NA_ROWS, NA_COLS, GRID_H = 8, 16, 128
NEG = -30000.0


def na_plan():
    variants = {}
    plan = []
    for j in range(NT_LAT):
        r0, r1 = 2 * j, 2 * j + 1
        rs0 = min(max(r0 - NA_ROWS // 2, 0), GRID_H - NA_ROWS)
        rs1 = min(max(r1 - NA_ROWS // 2, 0), GRID_H - NA_ROWS)
        lst = []
        for kt in range(rs0 // 2, (rs1 + NA_ROWS - 1) // 2 + 1):
            key = (kt - j, rs0 - r0, rs1 - r1)
            if key not in variants:
                variants[key] = len(variants)
            lst.append((kt, variants[key]))
        plan.append(lst)
    return plan, variants


NA_PLAN, NA_VARIANTS = na_plan()
NA_NVAR = len(NA_VARIANTS)


class Tok:
    __slots__ = ("sem", "key", "val", "eng")

    def __init__(self, sem, key, val, eng):
        self.sem, self.key, self.val, self.eng = sem, key, val, eng


class Buf:
    __slots__ = ("name", "last_w", "readers", "sem", "key", "cnt")

    def __init__(self, name):
        self.name = name
        self.last_w = None
        self.readers = {}
        self.sem = None
        self.key = None
        self.cnt = 0


class Eng:
    def __init__(self, name, h, sem):
        self.name, self.h, self.sem = name, h, sem
        self.key = "E_" + name
        self.cnt = 0
        self.waited = {}


class Sync:
    def __init__(self, nc, stack):
        self.nc = nc
        self.stack = stack
        self.E = {}
        for name, h in (("pe", nc.tensor), ("act", nc.scalar), ("dve", nc.vector),
                        ("pool", nc.gpsimd), ("sp", nc.sync)):
            sem = stack.enter_context(nc.semaphore("s_" + name))
            self.E[name] = Eng(name, h, sem)
        self.dma_bufs = []
        self.free_sems = []
        self.replica_groups = [[0, 1], [2, 3], [4, 5], [6, 7]]
        self.nsem = 0
        self.nwait = 0
        self.nins = 0

    def buf(self, name):
        return Buf(name)

    def bufs(self, name, n):
        return [Buf(f"{name}{i}") for i in range(n)]

    def _deps(self, eng, r, w):
        raw = []
        oth = []
        for b in r:
            if b.last_w is not None:
                raw.append(b.last_w)
        for b in w:
            if b.last_w is not None:
                oth.append(b.last_w)
            oth.extend(b.readers.values())
        toks = []
        for t in raw:
            if t.eng is eng and eng.name == "pe":
                continue
            toks.append(t)
        for t in oth:
            if t.eng is eng:
                continue
            toks.append(t)
        return toks

    def _wait(self, eng, toks):
        for t in toks:
            if eng.waited.get(t.key, 0) >= t.val:
                continue
            eng.h.wait_ge(t.sem, t.val)
            eng.waited[t.key] = t.val
            self.nwait += 1

    def op(self, en, fn, r=(), w=()):
        eng = self.E[en]
        self._wait(eng, self._deps(eng, r, w))
        ins = fn(eng.h)
        ins.then_inc(eng.sem, 1)
        eng.cnt += 1
        self.nins += 1
        tok = Tok(eng.sem, eng.key, eng.cnt, eng)
        for b in r:
            b.readers[tok.key] = tok
        for b in w:
            b.last_w = tok
            b.readers = {}
        return tok

    def dma(self, q, out, in_, r=(), w=(), sb=None):
        eng = self.E[q]
        self._wait(eng, self._deps(eng, r, w))
        if sb is None:
            sb = w[0] if w else r[0]
        if sb.sem is None:
            if self.free_sems:
                sb.sem, sb.key, sb.cnt = self.free_sems.pop()
            else:
                sb.sem = self.stack.enter_context(self.nc.semaphore(f"d{self.nsem}"))
                sb.key = f"D{self.nsem}"
                sb.cnt = 0
                self.nsem += 1
            self.dma_bufs.append(sb)
        ins = eng.h.dma_start(out=out, in_=in_)
        ins.then_inc(sb.sem, 16)
        sb.cnt += 16
        self.nins += 1
        tok = Tok(sb.sem, sb.key, sb.cnt, None)
        for b in r:
            b.readers[tok.key] = tok
        for b in w:
            b.last_w = tok
            b.readers = {}
        return tok

    def allgather_pairs(self, src_ts, dst_ts):
        self.barrier()
        eng = self.E["pool"]
        if getattr(self, "cc_sem", None) is None:
            self.cc_sem = self.stack.enter_context(self.nc.semaphore("cc_sem"))
            self.cc_cnt = 0
        for src_t, dst_t in zip(src_ts, dst_ts):
            ins = eng.h.collective_compute("AllGather", ALU.bypass, replica_groups=self.replica_groups,
                                           ins=[src_t.ap().opt()], outs=[dst_t.ap().opt()])
            ins.then_inc(self.cc_sem)
            self.cc_cnt += 1
            self.nins += 1
        tok = Tok(self.cc_sem, "CC", self.cc_cnt, None)
        for e in self.E.values():
            self._wait(e, [tok])

    def barrier(self, engines=("pe", "act", "dve", "pool", "sp")):
        toks = [Tok(e.sem, e.key, e.cnt, e) for e in self.E.values() if e.cnt > 0]
        toks += [Tok(b.sem, b.key, b.cnt, None) for b in self.dma_bufs if b.cnt > 0]
        for en in engines:
            eng = self.E[en]
            self._wait(eng, [t for t in toks if t.eng is not eng])
        for b in self.dma_bufs:
            self.free_sems.append((b.sem, b.key, b.cnt))
            b.sem = None
            b.last_w = None
            b.readers = {}
        self.dma_bufs = []


class Prog:
    def __init__(self, layers=(0, 1, 2, 3), final_out=True, qb_filter=None, tile_filter=None, n_cores=8):
        self.n_cores = n_cores
        self.layers = tuple(layers)
        self.qb_filter = qb_filter
        self.tile_filter = tile_filter
        self.nc = bass.Bass("TRN2", target_bir_lowering=False)
        self.final_out = final_out

    def din(self, name, shape, dt=F32):
        return self.nc.dram_tensor(name, list(shape), dt, kind="ExternalInput").ap()

    def dout(self, name, shape, dt=F32):
        return self.nc.dram_tensor(name, list(shape), dt, kind="ExternalOutput").ap()

    def dscr(self, name, shape, dt=F32):
        return self.nc.dram_tensor(name, list(shape), dt, kind="Internal").ap()

    def sb(self, st, name, shape, dt=F32):
        self._uid = getattr(self, "_uid", 0) + 1
        return st.enter_context(self.nc.sbuf_tensor(f"s{self._uid}_{name}", list(shape), dt))

    def ps(self, st, name, shape, dt=F32):
        self._uid = getattr(self, "_uid", 0) + 1
        return st.enter_context(self.nc.psum_tensor(f"p{self._uid}_{name}", list(shape), dt))

    def build(self):
        nc = self.nc
        with contextlib.ExitStack() as st:
            self.S = Sync(nc, st)
            self.S.replica_groups = [[2 * k, 2 * k + 1] for k in range(self.n_cores // 2)]
            self.declare()
            self.setup_consts(st)
            nl = len(self.layers)
            for li, l in enumerate(self.layers):
                src = self.x_in if li == 0 else self.xbuf[(li - 1) % 2]
                last = li == nl - 1
                dst = self.y_out if last else self.xbuf[li % 2]
                if l % 2 == 0:
                    self.even_layer(l, src, dst, last)
                else:
                    self.odd_layer(l, src, dst, last)
            self.S.barrier()
        return nc

    def declare(self):
        self.x_in = self.din("x_in", [NTOK, D])
        self.cc = self.din("cc", [128, 8, 2])
        self.ada_w = self.din("ada_w", [DEPTH, D, 3 * D])
        self.ada_b = self.din("ada_b", [DEPTH, 3 * D])
        self.ln_g = self.din("ln_g", [DEPTH, D])
        self.ln_b = self.din("ln_b", [DEPTH, D])
        self.ident_in = self.din("ident", [128, 128])
        self.sel_in = self.din("sel", [2, 256])
        self.ev_w = self.din("ev_w", [2, D, EV_COLS])
        self.ev_bcol = self.din("ev_bcol", [2, 128, 18])
        self.ev_brow = self.din("ev_brow", [2, 1, 1920])
        self.ev_lam = self.din("ev_lam", [2, 1, 256])
        self.ev_subg = self.din("ev_subg", [2, 1, 128])
        self.ev_qg = self.din("ev_qg", [2, 128, 2])
        self.ev_kvg = self.din("ev_kvg", [2, 128, 1])
        self.ev_wuq = self.din("ev_wuq", [2, 256, 1536])
        self.ev_wukv = self.din("ev_wukv", [2, 128, 1024])
        self.ev_wo = self.din("ev_wo", [2, D, D])
        self.rope_da = self.din("rope_da", [2, 128, NTOK])
        self.rope_ml = self.din("rope_ml", [2, 128, NTOK])
        self.od_w = self.din("od_w", [2, D, OD_COLS])
        self.od_bcol = self.din("od_bcol", [2, 128, 16])
        self.od_brow = self.din("od_brow", [2, 1, 2576])
        self.od_convw = self.din("od_convw", [2, 128, 8, 5])
        self.od_convb = self.din("od_convb", [2, 128, 8])
        self.od_fb = self.din("od_fb", [2, 1, 8])
        self.od_ng = self.din("od_ng", [2, 1, 512])
        self.od_nat = self.din("od_nat", [2, 8, 128, NA_NVAR, 128])
        self.od_wo = self.din("od_wo", [2, D, D])
        self.tri = self.din("tri", [128, 3, 128])
        self.qkpre = self.dscr("qkpre", [8, 128, NTOK])
        self.mq = self.dscr("mq", [4, 128, NTOK], BF16)
        self.mk = self.dscr("mk", [4, 128, NTOK], BF16)
        self.og = self.dscr("og", [NTOK, 512])
        self.gates = self.dscr("gates", [128, NT, 16])
        self.hfb = self.dscr("hfb", [2, NTOK, 512])
        n_last_tok = LAT if self.final_out else NTOK
        self.y_out = self.dout("y", [n_last_tok, D])
        self.xbuf = [self.dscr("xbuf0", [NTOK, D]), self.dscr("xbuf1", [NTOK, D])]
        self.qda = self.dscr("qda", [4, 128, NTOK], BF16)
        self.kda = self.dscr("kda", [4, 128, NTOK], BF16)
        self.vda = self.dscr("vda", [4, 128, NT, 129], BF16)
        self.qm = self.dscr("qm", [8, 96, NTOK], BF16)
        self.kmn = self.dscr("kmn", [8, 64, NTOK], BF16)
        self.krt = self.dscr("krt", [32, NTOK], BF16)
        self.vm = self.dscr("vm", [8, 128, NT, 65], BF16)
        self.gate = self.dscr("gate", [NTOK, D], BF16)
        self.ybuf = self.dscr("ybuf", [NTOK, D], BF16)
        self.psel = self.din("psel", [128, 2])
        self.yown_t = [self.nc.dram_tensor(f"yown{k}", [1024, D], BF16) for k in range(4)]
        self.ygath_t = [self.nc.dram_tensor(f"ygath{k}", [2048, D], BF16) for k in range(4)]

    def setup_consts(self, st):
        S = self.S
        self.ident_f = self.sb(st, "ident_f", [128, 128])
        self.ident_b = self.sb(st, "ident_b", [128, 128], BF16)
        self.sel = self.sb(st, "sel", [2, 256])
        self.ccs = self.sb(st, "ccs", [128, 8, 2])
        self.zeros_b = self.sb(st, "zeros_b", [128, 128], BF16)
        self.ones_b = self.sb(st, "ones_b", [1, 128], BF16)
        self.zeros_w = self.sb(st, "zeros_w", [128, 512], BF16)
        self.B_const = S.buf("consts")
        b = self.B_const
        S.dma("sp", self.ident_f[:], self.ident_in[:, :], w=[b])
        S.dma("sp", self.sel[:], self.sel_in[:, :], w=[b])
        S.dma("sp", self.ccs[:], self.cc[:, :, :], w=[b])
        self.pselt = self.sb(st, "pselt", [128, 2])
        S.dma("sp", self.pselt[:], self.psel[:, :], w=[b])
        S.op("dve", lambda e: e.tensor_copy(self.ident_b[:], self.ident_f[:]), r=[b], w=[b])
        S.op("dve", lambda e: e.memset(self.zeros_b[:], 0.0), w=[b])
        S.op("dve", lambda e: e.memset(self.ones_b[:], 1.0), w=[b])
        S.op("dve", lambda e: e.memset(self.zeros_w[:], 0.0), w=[b])
        S.op("act", lambda e: e.activation(out=self.ccs[:], in_=self.ccs[:], func=AF.Silu), r=[b], w=[b])
        self.sc1 = self.sb(st, "sc1", [128, 8, 2])
        self.sh = self.sb(st, "sh", [128, 8, 2])
        self.g_l = self.sb(st, "g_l", [128, D])
        self.g_c = self.sb(st, "g_c", [128, D])
        self.lng = self.sb(st, "lng", [128, D])
        self.lnb = self.sb(st, "lnb", [128, D])
        self.B_mod = S.buf("mod")

    def adaln(self, l):
        S, nc = self.S, self.nc
        S.barrier()
        with contextlib.ExitStack() as st:
            wt = [self.sb(st, f"adaw{i}", [128, 3 * D]) for i in range(2)]
            Bw = S.bufs("adaw", 2)
            mrow = self.sb(st, "mrow", [2, 3 * D])
            brow = self.sb(st, "adab", [2, 3 * D])
            Bm = S.buf("mrow")
            Bb = S.buf("adab")
            pm = [self.ps(st, f"pm{i}", [128, 512]) for i in range(8)]
            Bp = S.bufs("pm", 8)
            S.dma("sp", brow[:], self.ada_b[l:l + 1, :].to_broadcast([2, 3 * D]), w=[Bb])
            S.dma("sp", self.lng[:], self.ln_g[l:l + 1, :].to_broadcast([128, D]), w=[self.B_mod])
            S.dma("sp", self.lnb[:], self.ln_b[l:l + 1, :].to_broadcast([128, D]), w=[self.B_mod])
            for k in range(8):
                S.dma("sp", wt[k % 2][:], self.ada_w[l, k * 128:(k + 1) * 128, :], w=[Bw[k % 2]])
                for n in range(6):
                    S.op("pe", lambda e, k=k, n=n: e.matmul(
                        pm[n][0:2, :], self.ccs[:, k, :], wt[k % 2][:, n * 512:(n + 1) * 512],
                        start=(k == 0), stop=(k == 7)), r=[Bw[k % 2], self.B_const], w=[Bp[n]])
            for n in range(6):
                S.op("dve", lambda e, n=n: e.tensor_tensor(
                    out=mrow[:, n * 512:(n + 1) * 512], in0=pm[n][0:2, :], in1=brow[:, n * 512:(n + 1) * 512],
                    op=ALU.add), r=[Bp[n], Bb], w=[Bm])
            for j in range(8):
                S.op("pe", lambda e, j=j: e.transpose(
                    pm[6][:, 2 * j:2 * j + 2], mrow[0:2, j * 128:(j + 1) * 128], self.ident_f[0:2, 0:2]),
                    r=[Bm, self.B_const], w=[Bp[6]])
                S.op("pe", lambda e, j=j: e.transpose(
                    pm[7][:, 2 * j:2 * j + 2], mrow[0:2, D + j * 128:D + (j + 1) * 128], self.ident_f[0:2, 0:2]),
                    r=[Bm, self.B_const], w=[Bp[7]])
            S.op("dve", lambda e: e.tensor_copy(self.sh[:].rearrange("p a b -> p (a b)"), pm[6][:, 0:16]),
                 r=[Bp[6]], w=[self.B_mod])
            S.op("dve", lambda e: e.tensor_scalar(
                out=self.sc1[:].rearrange("p a b -> p (a b)"), in0=pm[7][:, 0:16], scalar1=1.0, scalar2=None,
                op0=ALU.add), r=[Bp[7]], w=[self.B_mod])
            for n in range(2):
                S.op("pe", lambda e, n=n: e.matmul(
                    pm[n][:, :], self.sel[:, 0:128], mrow[0:2, 2 * D + n * 512:2 * D + (n + 1) * 512],
                    start=True, stop=True), r=[Bm, self.B_const], w=[Bp[n]])
                S.op("pe", lambda e, n=n: e.matmul(
                    pm[2 + n][:, :], self.sel[:, 128:256], mrow[0:2, 2 * D + n * 512:2 * D + (n + 1) * 512],
                    start=True, stop=True), r=[Bm, self.B_const], w=[Bp[2 + n]])
                S.op("dve", lambda e, n=n: e.tensor_copy(self.g_l[:, n * 512:(n + 1) * 512], pm[n][:, :]),
                     r=[Bp[n]], w=[self.B_mod])
                S.op("dve", lambda e, n=n: e.tensor_copy(self.g_c[:, n * 512:(n + 1) * 512], pm[2 + n][:, :]),
                     r=[Bp[2 + n]], w=[self.B_mod])
            S.barrier()

    def load_cast(self, st_w, dst_fn, src_fn, nparts, ncols, pieces, Bdst, scale_fn=None, name="lc"):
        S = self.S
        with contextlib.ExitStack() as st:
            stg = [self.sb(st, f"{name}_stg{i}", [nparts, ncols]) for i in range(2)]
            Bs = S.bufs(name + "_stg", 2)
            for i, (dst, src, sc) in enumerate(pieces):
                j = i % 2
                n = src.shape[-1]
                S.dma("sp", stg[j][:, 0:n], src, w=[Bs[j]])
                en = "dve" if i % 2 == 0 else "pool"
                if sc is None:
                    S.op(en, lambda e, dst=dst, j=j, n=n: e.tensor_copy(dst, stg[j][:, 0:n]), r=[Bs[j]], w=[Bdst])
                else:
                    S.op(en, lambda e, dst=dst, j=j, n=n, sc=sc: e.tensor_scalar(
                        out=dst, in0=stg[j][:, 0:n], scalar1=sc, scalar2=None, op0=ALU.mult),
                        r=[Bs[j], Bdst], w=[Bdst])
            S.barrier()

    def even_layer(self, l, src, dst, last):
        i = l // 2
        self.adaln(l)
        self.even_project(i, src)
        self.da_attention(i, l)
        self.mla_attention(i)
        self.S.allgather_pairs(self.yown_t, self.ygath_t)
        self.out_stage(self.ev_wo[i], src, dst, last, ysplit=True)

    def even_project(self, i, src):
        S, nc = self.S, self.nc
        with contextlib.ExitStack() as st:
            wb = self.sb(st, "ev_wb", [128, 8, EV_COLS], BF16)
            wuq = self.sb(st, "ev_wuqb", [128, 2, 1536], BF16)
            wukv = self.sb(st, "ev_wukvb", [128, 1024], BF16)
            bcol = self.sb(st, "ev_bcol", [128, 18])
            brow_f = self.sb(st, "ev_brow_f", [1, 1920])
            brow = self.sb(st, "ev_brow", [1, 1920], BF16)
            qg = self.sb(st, "ev_qg", [128, 2])
            kvg = self.sb(st, "ev_kvg", [128, 1])
            Bw = S.buf("ev_w")
            S.dma("sp", bcol[:], self.ev_bcol[i, :, :], w=[Bw])
            S.dma("sp", brow_f[:], self.ev_brow[i, :, :], w=[Bw])
            S.dma("sp", qg[:], self.ev_qg[i, :, :], w=[Bw])
            S.dma("sp", kvg[:], self.ev_kvg[i, :, :], w=[Bw])
            S.op("dve", lambda e: e.tensor_copy(brow[:], brow_f[:]), r=[Bw], w=[Bw])
            pieces = []
            for k in range(8):
                for c in range(4):
                    pieces.append((wb[:, k, c * 1008:(c + 1) * 1008],
                                   self.ev_w[i, k * 128:(k + 1) * 128, c * 1008:(c + 1) * 1008], None))
            self.load_cast(st, None, None, 128, 1536, pieces, Bw, name="evw")
            pieces = [(wuq[:, rc, :], self.ev_wuq[i, rc * 128:(rc + 1) * 128, :], qg[:, rc:rc + 1]) for rc in range(2)]
            pieces.append((wukv[:, :], self.ev_wukv[i, :, :], kvg[:, 0:1]))
            self.load_cast(st, None, None, 128, 1536, pieces, Bw, name="evw2")

            NXS = 3
            xt = [self.sb(st, f"xt{j}", [128, D]) for j in range(NXS)]
            Bx = S.bufs("xt", NXS)
            hT = [self.sb(st, f"hT{j}", [128, 8, 512], BF16) for j in range(2)]
            Bh = S.bufs("hT", 2)
            cosd = [self.sb(st, f"cosd{j}", [128, 512]) for j in range(2)]
            sind = [self.sb(st, f"sind{j}", [128, 512]) for j in range(2)]
            cosm = [self.sb(st, f"cosm{j}", [128, 512]) for j in range(2)]
            sinm = [self.sb(st, f"sinm{j}", [128, 512]) for j in range(2)]
            Brt = S.bufs("ropet", 2)
            t1 = [self.sb(st, f"t1_{j}", [128, 512]) for j in range(2)]
            t2 = [self.sb(st, f"t2_{j}", [128, 512]) for j in range(2)]
            Bt1 = S.bufs("t1", 2)
            Bt2 = S.bufs("t2", 2)
            fo = [self.sb(st, f"fo{j}", [128, 512], BF16) for j in range(4)]
            Bfo = S.bufs("fo", 4)
            va = [self.sb(st, f"va{j}", [128, 4, 129], BF16) for j in range(2)]
            Bva = S.bufs("va", 2)
            vma = [self.sb(st, f"vma{j}", [128, 8, 65], BF16) for j in range(2)]
            Bvma = S.bufs("vma", 2)
            cq = [self.sb(st, f"cq{j}", [128, 384]) for j in range(2)]
            Bcq = S.bufs("cq", 2)
            cqn = [self.sb(st, f"cqn{j}", [128, 384], BF16) for j in range(2)]
            Bcqn = S.bufs("cqn", 2)
            stat = [self.sb(st, f"stat{j}", [128, 8]) for j in range(2)]
            Bstat = S.bufs("stat", 2)
            junk = self.sb(st, "junk", [128, 256])
            Bjunk = S.buf("junk")
            cT = [self.sb(st, f"cT{j}", [128, 3, 512], BF16) for j in range(2)]
            BcT = S.bufs("cT", 2)
            gt = [self.sb(st, f"gt{j}", [128, D], BF16) for j in range(2)]
            Bgt = S.bufs("gt", 2)
            pp = [self.ps(st, f"pp{j}", [128, 512]) for j in range(7)]
            ppb = self.ps(st, "ppb", [128, 1024], BF16)
            Bpp = S.bufs("pp", 7)
            Bppb = S.buf("ppb")
            for j in range(2):
                S.op("pool", lambda e, j=j: e.memset(va[j][:], 1.0), w=[Bva[j]])
                S.op("pool", lambda e, j=j: e.memset(vma[j][:], 1.0), w=[Bvma[j]])
            eps_t = self.sb(st, "eps_t", [128, 1])
            S.op("pool", lambda e: e.memset(eps_t[:], RMS_EPS), w=[Bw])

            self._pp_i = 0

            def next_pp():
                j = self._pp_i % 7
                self._pp_i += 1
                return pp[j], Bpp[j]

            nblk = (NTOK + 511) // 512
            xi = 0
            foi = 0
            evac = 0
            for blk in range(nblk):
                t0 = blk * 512
                ntok = min(512, NTOK - t0)
                nsub = ntok // 128
                m = 0 if t0 < LAT else 1
                hb = blk % 2
                S.dma("sp", cosd[hb][:, :ntok], self.rope_da[0, :, t0:t0 + ntok], w=[Brt[hb]])
                S.dma("sp", sind[hb][:, :ntok], self.rope_da[1, :, t0:t0 + ntok], w=[Brt[hb]])
                S.dma("sp", cosm[hb][:, :ntok], self.rope_ml[0, :, t0:t0 + ntok], w=[Brt[hb]])
                S.dma("sp", sinm[hb][:, :ntok], self.rope_ml[1, :, t0:t0 + ntok], w=[Brt[hb]])
                for s in range(nsub):
                    xj = xi % NXS
                    xi += 1
                    S.dma("sp", xt[xj][:], src[t0 + s * 128:t0 + (s + 1) * 128, :], w=[Bx[xj]])
                    for half in range(2):
                        p, Bp = next_pp()
                        for q in range(4):
                            dc = half * 4 + q
                            S.op("pe", lambda e, p=p, q=q, dc=dc, xj=xj: e.transpose(
                                p[:, q * 128:(q + 1) * 128], xt[xj][:, dc * 128:(dc + 1) * 128], self.ident_f[:]),
                                r=[Bx[xj], self.B_const], w=[Bp])
                        for q in range(4):
                            dc = half * 4 + q
                            if evac % 2 == 0:
                                S.op("act", lambda e, p=p, q=q, dc=dc, s=s, m=m: e.activation(
                                    out=hT[hb][:, dc, s * 128:(s + 1) * 128], in_=p[:, q * 128:(q + 1) * 128],
                                    func=AF.Identity, scale=self.sc1[:, dc, m:m + 1], bias=self.sh[:, dc, m:m + 1]),
                                    r=[Bp, self.B_mod], w=[Bh[hb]])
                            else:
                                S.op("dve", lambda e, p=p, q=q, dc=dc, s=s, m=m: e.tensor_scalar(
                                    out=hT[hb][:, dc, s * 128:(s + 1) * 128], in0=p[:, q * 128:(q + 1) * 128],
                                    scalar1=self.sc1[:, dc, m:m + 1], scalar2=self.sh[:, dc, m:m + 1],
                                    op0=ALU.mult, op1=ALU.add), r=[Bp, self.B_mod], w=[Bh[hb]])
                            evac += 1
                for grp, base, dstT, bofs in ((0, EQ, self.qda, 0), (1, EK, self.kda, 8)):
                    for h in range(4):
                        pa, Ba = next_pp()
                        pb, Bb = next_pp()
                        for k in range(8):
                            S.op("pe", lambda e, pa=pa, k=k, c0=base + h * 128: e.matmul(
                                pa[:, :ntok], wb[:, k, c0:c0 + 128], hT[hb][:, k, :ntok], start=(k == 0), stop=(k == 7)),
                                r=[Bw, Bh[hb]], w=[Ba])
                        for k in range(8):
                            S.op("pe", lambda e, pb=pb, k=k, c0=base + 512 + h * 128: e.matmul(
                                pb[:, :ntok], wb[:, k, c0:c0 + 128], hT[hb][:, k, :ntok], start=(k == 0), stop=(k == 7)),
                                r=[Bw, Bh[hb]], w=[Bb])
                        tj = (grp * 4 + h) % 2
                        S.op("dve", lambda e, pa=pa, tj=tj, c=bofs + h: e.scalar_tensor_tensor(
                            out=t1[tj][:, :ntok], in0=pa[:, :ntok], scalar=bcol[:, c:c + 1], in1=cosd[hb][:, :ntok],
                            op0=ALU.add, op1=ALU.mult), r=[Ba, Brt[hb], Bw], w=[Bt1[tj]])
                        S.op("dve", lambda e, pb=pb, tj=tj, c=bofs + 4 + h: e.scalar_tensor_tensor(
                            out=t2[tj][:, :ntok], in0=pb[:, :ntok], scalar=bcol[:, c:c + 1], in1=sind[hb][:, :ntok],
                            op0=ALU.add, op1=ALU.mult), r=[Bb, Brt[hb], Bw], w=[Bt2[tj]])
                        fj = foi % 4
                        foi += 1
                        S.op("pool", lambda e, tj=tj, fj=fj: e.tensor_tensor(
                            out=fo[fj][:, :ntok], in0=t1[tj][:, :ntok], in1=t2[tj][:, :ntok], op=ALU.add),
                            r=[Bt1[tj], Bt2[tj]], w=[Bfo[fj]])
                        S.dma("pool", dstT[h, :, t0:t0 + ntok], fo[fj][:, :ntok], r=[Bfo[fj]])
                pa, Ba = next_pp()
                pb, Bb = next_pp()
                for k in range(8):
                    S.op("pe", lambda e, pa=pa, k=k: e.matmul(
                        pa[0:32, :ntok], wb[:, k, EKR:EKR + 32], hT[hb][:, k, :ntok], start=(k == 0), stop=(k == 7)),
                        r=[Bw, Bh[hb]], w=[Ba])
                for k in range(8):
                    S.op("pe", lambda e, pb=pb, k=k: e.matmul(
                        pb[0:32, :ntok], wb[:, k, EKR + 32:EKR + 64], hT[hb][:, k, :ntok], start=(k == 0), stop=(k == 7)),
                        r=[Bw, Bh[hb]], w=[Bb])
                tj = 0
                S.op("dve", lambda e, pa=pa: e.scalar_tensor_tensor(
                    out=t1[tj][0:32, :ntok], in0=pa[0:32, :ntok], scalar=bcol[0:32, 16:17], in1=cosm[hb][0:32, :ntok],
                    op0=ALU.add, op1=ALU.mult), r=[Ba, Brt[hb], Bw], w=[Bt1[tj]])
                S.op("dve", lambda e, pb=pb: e.scalar_tensor_tensor(
                    out=t2[tj][0:32, :ntok], in0=pb[0:32, :ntok], scalar=bcol[0:32, 17:18], in1=sinm[hb][0:32, :ntok],
                    op0=ALU.add, op1=ALU.mult), r=[Bb, Brt[hb], Bw], w=[Bt2[tj]])
                fj = foi % 4
                foi += 1
                S.op("pool", lambda e, fj=fj: e.tensor_tensor(
                    out=fo[fj][0:32, :ntok], in0=t1[tj][0:32, :ntok], in1=t2[tj][0:32, :ntok], op=ALU.add),
                    r=[Bt1[tj], Bt2[tj]], w=[Bfo[fj]])
                S.dma("pool", self.krt[:, t0:t0 + ntok], fo[fj][0:32, :ntok], r=[Bfo[fj]])
                cb = blk % 2
                for s in range(nsub):
                    tt = (t0 // 128) + s
                    tok0 = t0 + s * 128
                    p, Bp = next_pp()
                    for k in range(8):
                        S.op("pe", lambda e, p=p, k=k, s=s: e.matmul(
                            p[:, :], hT[hb][:, k, s * 128:(s + 1) * 128], wb[:, k, EV_:EV_ + 512], start=(k == 0), stop=False),
                            r=[Bw, Bh[hb]], w=[Bp])
                    S.op("pe", lambda e, p=p: e.matmul(p[:, :], self.ones_b[0:1, :], brow[0:1, 0:512], start=False, stop=True),
                         r=[Bw, self.B_const], w=[Bp])
                    vj = tt % 2
                    S.op("act", lambda e, p=p, vj=vj: e.activation(
                        out=va[vj][:, :, 0:128], in_=p[:, :].rearrange("p (h c) -> p h c", h=4), func=AF.Copy),
                        r=[Bp], w=[Bva[vj]])
                    S.dma("pool", self.vda[:, :, tt, :].rearrange("h p c -> p h c"), va[vj][:], r=[Bva[vj]])
                    p, Bp = next_pp()
                    for k in range(8):
                        S.op("pe", lambda e, p=p, k=k, s=s: e.matmul(
                            p[:, 0:384], hT[hb][:, k, s * 128:(s + 1) * 128], wb[:, k, EC:EC + 384], start=(k == 0), stop=False),
                            r=[Bw, Bh[hb]], w=[Bp])
                    S.op("pe", lambda e, p=p: e.matmul(p[:, 0:384], self.ones_b[0:1, :], brow[0:1, 512:896], start=False, stop=True),
                         r=[Bw, self.B_const], w=[Bp])
                    cj = tt % 2
                    S.op("act", lambda e, p=p, cj=cj: e.activation(out=cq[cj][:], in_=p[:, 0:384], func=AF.Copy),
                         r=[Bp], w=[Bcq[cj]])
                    S.op("act", lambda e, cj=cj: e.activation(
                        out=junk[:, 0:256], in_=cq[cj][:, 0:256], func=AF.Square, accum_out=stat[cj][:, 0:1]),
                        r=[Bcq[cj]], w=[Bjunk, Bstat[cj]])
                    S.op("act", lambda e, cj=cj: e.activation(
                        out=junk[:, 0:128], in_=cq[cj][:, 256:384], func=AF.Square, accum_out=stat[cj][:, 1:2]),
                        r=[Bcq[cj]], w=[Bjunk, Bstat[cj]])
                    S.op("act", lambda e, cj=cj: e.activation(out=stat[cj][:, 2:3], in_=stat[cj][:, 0:1], func=AF.Sqrt,
                                                              scale=1.0 / 256, bias=eps_t[:, 0:1]), r=[Bstat[cj], Bw], w=[Bstat[cj]])
                    S.op("act", lambda e, cj=cj: e.activation(out=stat[cj][:, 3:4], in_=stat[cj][:, 1:2], func=AF.Sqrt,
                                                              scale=1.0 / 128, bias=eps_t[:, 0:1]), r=[Bstat[cj], Bw], w=[Bstat[cj]])
                    S.op("dve", lambda e, cj=cj: e.reciprocal(stat[cj][:, 4:6], stat[cj][:, 2:4]), r=[Bstat[cj]], w=[Bstat[cj]])
                    S.op("dve", lambda e, cj=cj: e.tensor_scalar(
                        out=cqn[cj][:, 0:256], in0=cq[cj][:, 0:256], scalar1=stat[cj][:, 4:5], scalar2=None, op0=ALU.mult),
                        r=[Bcq[cj], Bstat[cj]], w=[Bcqn[cj]])
                    S.op("pool", lambda e, cj=cj: e.tensor_scalar(
                        out=cqn[cj][:, 256:384], in0=cq[cj][:, 256:384], scalar1=stat[cj][:, 5:6], scalar2=None, op0=ALU.mult),
                        r=[Bcq[cj], Bstat[cj]], w=[Bcqn[cj]])
                    for rc in range(3):
                        S.op("pe", lambda e, rc=rc, cj=cj: e.transpose(
                            ppb[:, rc * 128:(rc + 1) * 128], cqn[cj][:, rc * 128:(rc + 1) * 128], self.ident_b[:]),
                            r=[Bcqn[cj], self.B_const], w=[Bppb])
                    S.op("dve", lambda e, s=s: e.tensor_copy(
                        cT[cb][:, :, s * 128:(s + 1) * 128], ppb[:, 0:384].rearrange("p (r t) -> p r t", r=3)),
                        r=[Bppb], w=[BcT[cb]])
                    gj = tt % 2
                    for n in range(2):
                        p, Bp = next_pp()
                        for k in range(8):
                            S.op("pe", lambda e, p=p, k=k, s=s, n=n: e.matmul(
                                p[:, :], hT[hb][:, k, s * 128:(s + 1) * 128], wb[:, k, EG + n * 512:EG + (n + 1) * 512],
                                start=(k == 0), stop=False), r=[Bw, Bh[hb]], w=[Bp])
                        S.op("pe", lambda e, p=p, n=n: e.matmul(
                            p[:, :], self.ones_b[0:1, :], brow[0:1, 896 + n * 512:896 + (n + 1) * 512], start=False, stop=True),
                            r=[Bw, self.B_const], w=[Bp])
                        S.op("act", lambda e, p=p, n=n, gj=gj: e.activation(
                            out=gt[gj][:, n * 512:(n + 1) * 512], in_=p[:, :], func=AF.Silu), r=[Bp], w=[Bgt[gj]])
                    S.dma("pool", self.gate[tok0:tok0 + 128, :], gt[gj][:], r=[Bgt[gj]])
                for h in range(8):
                    pa, Ba = next_pp()
                    pb, Bb = next_pp()
                    for rc in range(2):
                        S.op("pe", lambda e, pa=pa, rc=rc, h=h: e.matmul(
                            pa[0:96, :ntok], wuq[:, rc, h * 96:(h + 1) * 96], cT[cb][:, rc, :ntok], start=(rc == 0), stop=(rc == 1)),
                            r=[Bw, BcT[cb]], w=[Ba])
                    for rc in range(2):
                        S.op("pe", lambda e, pb=pb, rc=rc, h=h: e.matmul(
                            pb[0:96, :ntok], wuq[:, rc, 768 + h * 96:768 + (h + 1) * 96], cT[cb][:, rc, :ntok],
                            start=(rc == 0), stop=(rc == 1)), r=[Bw, BcT[cb]], w=[Bb])
                    tj = h % 2
                    fj = foi % 4
                    foi += 1
                    S.op("dve", lambda e, pa=pa, tj=tj: e.tensor_tensor(
                        out=t1[tj][64:96, :ntok], in0=pa[64:96, :ntok], in1=cosm[hb][64:96, :ntok], op=ALU.mult),
                        r=[Ba, Brt[hb]], w=[Bt1[tj]])
                    S.op("dve", lambda e, pb=pb, tj=tj: e.tensor_tensor(
                        out=t2[tj][64:96, :ntok], in0=pb[64:96, :ntok], in1=sinm[hb][64:96, :ntok], op=ALU.mult),
                        r=[Bb, Brt[hb]], w=[Bt2[tj]])
                    S.op("act", lambda e, pa=pa, fj=fj: e.activation(out=fo[fj][0:64, :ntok], in_=pa[0:64, :ntok], func=AF.Copy),
                         r=[Ba], w=[Bfo[fj]])
                    S.op("pool", lambda e, tj=tj, fj=fj: e.tensor_tensor(
                        out=fo[fj][64:96, :ntok], in0=t1[tj][64:96, :ntok], in1=t2[tj][64:96, :ntok], op=ALU.add),
                        r=[Bt1[tj], Bt2[tj]], w=[Bfo[fj]])
                    S.dma("pool", self.qm[h, :, t0:t0 + ntok], fo[fj][0:96, :ntok], r=[Bfo[fj]])
                for hp in range(4):
                    p, Bp = next_pp()
                    S.op("pe", lambda e, p=p, hp=hp: e.matmul(
                        p[:, :ntok], wukv[:, hp * 128:(hp + 1) * 128], cT[cb][:, 2, :ntok], start=True, stop=True),
                        r=[Bw, BcT[cb]], w=[Bp])
                    fj = foi % 4
                    foi += 1
                    S.op("act", lambda e, p=p, fj=fj: e.activation(out=fo[fj][:, :ntok], in_=p[:, :ntok], func=AF.Copy),
                         r=[Bp], w=[Bfo[fj]])
                    S.dma("pool", self.kmn[2 * hp, :, t0:t0 + ntok], fo[fj][0:64, :ntok], r=[Bfo[fj]])
                    S.dma("pool", self.kmn[2 * hp + 1, :, t0:t0 + ntok], fo[fj][64:128, :ntok], r=[Bfo[fj]])
                for s in range(nsub):
                    tt = (t0 // 128) + s
                    p, Bp = next_pp()
                    S.op("pe", lambda e, p=p, s=s: e.matmul(
                        p[:, :], cT[cb][:, 2, s * 128:(s + 1) * 128], wukv[:, 512:1024], start=True, stop=True),
                        r=[Bw, BcT[cb]], w=[Bp])
                    vj = tt % 2
                    S.op("act", lambda e, p=p, vj=vj: e.activation(
                        out=vma[vj][:, :, 0:64], in_=p[:, :].rearrange("p (h c) -> p h c", h=8), func=AF.Copy),
                        r=[Bp], w=[Bvma[vj]])
                    S.dma("pool", self.vm[:, :, tt, :].rearrange("h p c -> p h c"), vma[vj][:], r=[Bvma[vj]])
            S.barrier()

    def attn_core(self, st, name, KT, QT, VA, Bkv, kparts, nv, scale, qblocks, finalize, kslices=None):
        S = self.S
        nmap = len(kparts)
        nacc_per_bank = 512 // nv
        n_acc = nmap * 4
        n_acc_banks = (n_acc + nacc_per_bank - 1) // nacc_per_bank
        n_sc = 8 - n_acc_banks
        n_sc = min(n_sc, 4)
        NP = 4
        res = getattr(self, "_attn_res", None)
        if res is None or res[0] != name:
            accb = [self.ps(st, f"{name}_acc{j}", [128, 512]) for j in range(n_acc_banks)]
            Bacc = S.bufs(name + "_acc", n_acc_banks)
            scb = [self.ps(st, f"{name}_sc{j}", [128, 512]) for j in range(n_sc)]
            Bsc = S.bufs(name + "_sc", n_sc)
            pt = [self.sb(st, f"{name}_pt{j}", [128, 512], BF16) for j in range(NP)]
            Bpt = S.bufs(name + "_pt", NP)
            self._attn_res = (name, accb, Bacc, scb, Bsc, pt, Bpt)
        _, accb, Bacc, scb, Bsc, pt, Bpt = self._attn_res

        def acc_ap(mi, sub):
            idx = mi * 4 + sub
            b = idx // nacc_per_bank
            o = (idx % nacc_per_bank) * nv
            return accb[b][:, o:o + nv], Bacc[b]

        sci = 0
        pti = 0
        for qbi, (q0, nq, ktiles) in enumerate(qblocks):
            nsub = nq // 128
            for b in range(n_acc_banks):
                S.op("pe", lambda e, b=b: e.matmul(accb[b][:, :], self.zeros_b[:, :], self.zeros_w[:, :], start=True, stop=False,
                                                   skip_group_check=True), r=[self.B_const], w=[Bacc[b]])
            units = [(kt, mi) for kt in ktiles for mi in range(nmap)]
            pend = []

            def emit_score(u):
                nonlocal sci
                kt, mi = u
                QTm, lo, hi = kparts[mi]
                j = sci % n_sc
                sci += 1
                S.op("pe", lambda e, j=j, lo=lo, hi=hi, kt=kt, QTm=QTm: e.matmul(
                    scb[j][:, :nq], KT[lo:hi, kt * 128:(kt + 1) * 128], QTm[lo:hi, q0:q0 + nq], start=True, stop=True),
                    r=[Bkv], w=[Bsc[j]])
                return j

            def emit_exp_pv(u, j, lastflag):
                nonlocal pti
                kt, mi = u
                pj = pti % NP
                pti += 1
                S.op("act", lambda e, j=j, pj=pj: e.activation(out=pt[pj][:, :nq], in_=scb[j][:, :nq], func=AF.Exp, scale=scale),
                     r=[Bsc[j]], w=[Bpt[pj]])
                for sub in range(nsub):
                    ap, Ba = acc_ap(mi, sub)
                    S.op("pe", lambda e, ap=ap, pj=pj, sub=sub, kt=kt: e.matmul(
                        ap, pt[pj][:, sub * 128:(sub + 1) * 128], VA[:, kt, :], start=False, stop=lastflag,
                        skip_group_check=True), r=[Bpt[pj], Bkv], w=[Ba])

            LOOK = min(n_sc - 1, 2)
            q = []
            for ui, u in enumerate(units):
                q.append((u, emit_score(u)))
                if len(q) > LOOK:
                    u0, j0 = q.pop(0)
                    emit_exp_pv(u0, j0, False)
            while q:
                u0, j0 = q.pop(0)
                emit_exp_pv(u0, j0, u0[0] == ktiles[-1])
            for sub in range(nsub):
                accs = [acc_ap(mi, sub) for mi in range(nmap)]
                finalize(qbi, q0, sub, accs)

    def qblocks_all(self):
        qb = []
        lat_k = list(range(NT))
        for b in range(LAT // 512):
            qb.append((b * 512, 512, lat_k))
        qb.append((LAT, CTX, [NT_LAT, NT_LAT + 1]))
        if self.qb_filter is not None:
            qb = [q for i, q in enumerate(qb) if i in self.qb_filter]
        return qb

    def qblocks_own(self):
        qb = []
        lat_k = list(range(NT))
        for b in range(LAT // 2 // 512):
            qb.append((b * 512, 512, lat_k))
        qb.append((LAT // 2, CTX, [NT_LAT, NT_LAT + 1]))
        if self.qb_filter is not None:
            qb = [q for i, q in enumerate(qb) if i in self.qb_filter]
        return qb

    def own_q(self, QO, QS, lo, hi, tmp, Bqs, Bqo, Btmp):
        S = self.S
        H = LAT // 2
        S.op("dve", lambda e: e.tensor_scalar(out=tmp[lo:hi, :], in0=QS[lo:hi, H:LAT], scalar1=self.pselt[lo:hi, 1:2], scalar2=None,
                                              op0=ALU.mult), r=[Bqs, self.B_const], w=[Btmp])
        S.op("dve", lambda e: e.scalar_tensor_tensor(out=QO[lo:hi, 0:H], in0=QS[lo:hi, 0:H], scalar=self.pselt[lo:hi, 0:1],
                                                     in1=tmp[lo:hi, :], op0=ALU.mult, op1=ALU.add),
             r=[Bqs, Btmp, self.B_const], w=[Bqo])
        S.op("act", lambda e: e.activation(out=QO[lo:hi, H:H + CTX], in_=QS[lo:hi, LAT:NTOK], func=AF.Copy), r=[Bqs], w=[Bqo])

    def y_dst(self, tok0, c0, c1):
        H = LAT // 2
        if tok0 < H:
            k, r = tok0 // 1024, tok0 % 1024
            return self.yown_t[k].ap()[r:r + 128, c0:c1]
        return self.ybuf[LAT + tok0 - H:LAT + tok0 - H + 128, c0:c1]

    def da_attention(self, i, l):
        S = self.S
        lam_init = 0.8 - 0.6 * math.exp(-0.3 * l)
        with contextlib.ExitStack() as st:
            KT = self.sb(st, "da_KT", [128, NTOK], BF16)
            NQO = LAT // 2 + CTX
            KTs = [self.sb(st, f"da_KT{j}", [128, NTOK], BF16) for j in range(2)]
            VAs = [self.sb(st, f"da_VA{j}", [128, NT, 129], BF16) for j in range(2)]
            QT1s = [self.sb(st, f"da_QT1{j}", [128, NQO], BF16) for j in range(2)]
            QT2s = [self.sb(st, f"da_QT2{j}", [128, NQO], BF16) for j in range(2)]
            QS = self.sb(st, "da_QS", [128, NTOK], BF16)
            qtmp = self.sb(st, "da_qtmp", [128, LAT // 2], BF16)
            Bkvs = S.bufs("da_kv", 2)
            Bqs = S.buf("da_qs")
            Bqt = S.buf("da_qtmp")
            for j in range(2):
                S.op("pool", lambda e, j=j: e.memset(QT1s[j][64:128, :], 0.0), w=[Bkvs[j]])
                S.op("pool", lambda e, j=j: e.memset(QT2s[j][0:64, :], 0.0), w=[Bkvs[j]])
            lamt = self.sb(st, "lamt", [128, 256])
            lamw = self.sb(st, "lamw", [128, 8])
            subg = self.sb(st, "subg", [128, 128])
            Bl = S.buf("lam")
            eps_t = self.sb(st, "da_eps", [128, 1])
            S.op("pool", lambda e: e.memset(eps_t[:], RMS_EPS), w=[Bl])
            S.dma("sp", lamt[:], self.ev_lam[i, :, :].to_broadcast([128, 256]), w=[Bl])
            S.dma("sp", subg[:], self.ev_subg[i, :, :].to_broadcast([128, 128]), w=[Bl])
            junk = self.sb(st, "da_junk", [128, 128])
            Bj = S.buf("da_junk")
            S.op("dve", lambda e: e.tensor_tensor(out=junk[:, 0:64], in0=lamt[:, 0:64], in1=lamt[:, 64:128], op=ALU.mult),
                 r=[Bl], w=[Bj])
            S.op("dve", lambda e: e.reduce_sum(out=lamw[:, 0:1], in_=junk[:, 0:64], axis=AX.X), r=[Bj], w=[Bl])
            S.op("dve", lambda e: e.tensor_tensor(out=junk[:, 64:128], in0=lamt[:, 128:192], in1=lamt[:, 192:256], op=ALU.mult),
                 r=[Bl], w=[Bj])
            S.op("dve", lambda e: e.reduce_sum(out=lamw[:, 1:2], in_=junk[:, 64:128], axis=AX.X), r=[Bj], w=[Bl])
            S.op("act", lambda e: e.activation(out=lamw[:, 2:4], in_=lamw[:, 0:2], func=AF.Exp), r=[Bl], w=[Bl])
            S.op("dve", lambda e: e.tensor_tensor(out=lamw[:, 4:5], in0=lamw[:, 3:4], in1=lamw[:, 2:3], op=ALU.subtract),
                 r=[Bl], w=[Bl])
            S.op("dve", lambda e: e.tensor_scalar(out=lamw[:, 5:6], in0=lamw[:, 4:5], scalar1=-lam_init, scalar2=None, op0=ALU.add),
                 r=[Bl], w=[Bl])
            NF = 3
            rr = [self.sb(st, f"da_rr{j}", [128, 8]) for j in range(NF)]
            ta = [self.sb(st, f"da_ta{j}", [128, 128]) for j in range(NF)]
            td = [self.sb(st, f"da_td{j}", [128, 128]) for j in range(NF)]
            to = [self.sb(st, f"da_to{j}", [128, 128], BF16) for j in range(NF)]
            Bf = S.bufs("da_fin", NF)
            Bto = S.bufs("da_to", NF)
            self._fi = 0
            def da_load(h):
                j = h % 2
                S.dma("sp", KTs[j][:], self.kda[h, :, :], w=[Bkvs[j]])
                S.dma("act", QS[:], self.qda[h, :, :], w=[Bqs])
                S.dma("sp", VAs[j][:], self.vda[h, :, :, :], w=[Bkvs[j]])
                self.own_q(QT1s[j], QS, 0, 64, qtmp, Bqs, Bkvs[j], Bqt)
                self.own_q(QT2s[j], QS, 64, 128, qtmp, Bqs, Bkvs[j], Bqt)

            da_load(0)
            for h in range(4):
                if h + 1 < 4:
                    da_load(h + 1)
                KT, VA, QT1, QT2, Bkv = KTs[h % 2], VAs[h % 2], QT1s[h % 2], QT2s[h % 2], Bkvs[h % 2]

                def fin(qbi, q0, sub, accs, h=h):
                    j = self._fi % NF
                    self._fi += 1
                    (a1, B1), (a2, B2) = accs
                    S.op("dve", lambda e: e.reciprocal(rr[j][:, 0:1], a1[:, 128:129]), r=[B1], w=[Bf[j]])
                    S.op("dve", lambda e: e.reciprocal(rr[j][:, 1:2], a2[:, 128:129]), r=[B2], w=[Bf[j]])
                    S.op("dve", lambda e: e.tensor_tensor(out=rr[j][:, 2:3], in0=rr[j][:, 1:2], in1=lamw[:, 5:6], op=ALU.mult),
                         r=[Bf[j], Bl], w=[Bf[j]])
                    S.op("dve", lambda e: e.tensor_scalar(out=ta[j][:], in0=a1[:, 0:128], scalar1=rr[j][:, 0:1], scalar2=None,
                                                          op0=ALU.mult), r=[B1, Bf[j]], w=[Bf[j]])
                    S.op("dve", lambda e: e.scalar_tensor_tensor(out=td[j][:], in0=a2[:, 0:128], scalar=rr[j][:, 2:3], in1=ta[j][:],
                                                                 op0=ALU.mult, op1=ALU.add), r=[B2, Bf[j]], w=[Bf[j]])
                    S.op("act", lambda e: e.activation(out=ta[j][:], in_=td[j][:], func=AF.Square, accum_out=rr[j][:, 3:4]),
                         r=[Bf[j]], w=[Bf[j]])
                    S.op("act", lambda e: e.activation(out=rr[j][:, 4:5], in_=rr[j][:, 3:4], func=AF.Sqrt, scale=1.0 / 128,
                                                       bias=eps_t[:, 0:1]), r=[Bf[j], Bl], w=[Bf[j]])
                    S.op("dve", lambda e: e.reciprocal(rr[j][:, 5:6], rr[j][:, 4:5]), r=[Bf[j]], w=[Bf[j]])
                    S.op("pool", lambda e: e.tensor_scalar(out=td[j][:], in0=td[j][:], scalar1=rr[j][:, 5:6], scalar2=(1.0 - lam_init),
                                                           op0=ALU.mult, op1=ALU.mult), r=[Bf[j]], w=[Bf[j]])
                    S.op("pool", lambda e: e.tensor_tensor(out=to[j][:], in0=td[j][:], in1=subg[:], op=ALU.mult),
                         r=[Bf[j], Bl], w=[Bto[j]])
                    tok0 = q0 + sub * 128
                    S.dma("pool", self.y_dst(tok0, h * 128, (h + 1) * 128), to[j][:], r=[Bto[j]])

                self.attn_core(st, "da", KT, None, VA, Bkv, [(QT1, 0, 128), (QT2, 0, 128)], 129, DA_SCALE,
                               self.qblocks_own(), fin)
            S.barrier()
            self._attn_res = None

    def mla_attention(self, i):
        S = self.S
        with contextlib.ExitStack() as st:
            KTs = [self.sb(st, f"ml_KT{j}", [128, NTOK], BF16) for j in range(2)]
            QTs = [self.sb(st, f"ml_QT{j}", [128, LAT // 2 + CTX], BF16) for j in range(2)]
            VAs = [self.sb(st, f"ml_VA{j}", [128, NT, 65], BF16) for j in range(2)]
            QS = self.sb(st, "ml_QS", [128, NTOK], BF16)
            qtmp = self.sb(st, "ml_qtmp", [128, LAT // 2], BF16)
            Bkvs = S.bufs("ml_kv", 2)
            Bqs = S.buf("ml_qs")
            Bqt = S.buf("ml_qtmp")
            NF = 3
            rr = [self.sb(st, f"ml_rr{j}", [128, 2]) for j in range(NF)]
            to = [self.sb(st, f"ml_to{j}", [128, 64], BF16) for j in range(NF)]
            Bto = S.bufs("ml_to", NF)
            self._fi = 0
            def ml_load(h):
                j = h % 2
                S.dma("sp", KTs[j][0:64, :], self.kmn[h, :, :], w=[Bkvs[j]])
                S.dma("sp", KTs[j][64:96, :], self.krt[:, :], w=[Bkvs[j]])
                S.dma("act", QS[0:96, :], self.qm[h, :, :], w=[Bqs])
                S.dma("sp", VAs[j][:], self.vm[h, :, :, :], w=[Bkvs[j]])
                self.own_q(QTs[j], QS, 0, 96, qtmp, Bqs, Bkvs[j], Bqt)

            ml_load(0)
            for h in range(8):
                if h + 1 < 8:
                    ml_load(h + 1)
                KT, VA, QT, Bkv = KTs[h % 2], VAs[h % 2], QTs[h % 2], Bkvs[h % 2]

                def fin(qbi, q0, sub, accs, h=h):
                    j = self._fi % NF
                    self._fi += 1
                    (a1, B1), = accs
                    S.op("dve", lambda e: e.reciprocal(rr[j][:, 0:1], a1[:, 64:65]), r=[B1], w=[Bto[j]])
                    S.op("dve", lambda e: e.tensor_scalar(out=to[j][:], in0=a1[:, 0:64], scalar1=rr[j][:, 0:1], scalar2=None,
                                                          op0=ALU.mult), r=[B1, Bto[j]], w=[Bto[j]])
                    tok0 = q0 + sub * 128
                    S.dma("pool", self.y_dst(tok0, 512 + h * 64, 512 + (h + 1) * 64), to[j][:], r=[Bto[j]])

                self.attn_core(st, "ml", KT, None, VA, Bkv, [(QT, 0, 96)], 65, MLA_SCALE, self.qblocks_own(), fin)
            S.barrier()
            self._attn_res = None

    def out_stage(self, wo_dram, src, dst, last, ysplit=False):
        S = self.S
        with contextlib.ExitStack() as st:
            wo = self.sb(st, "wo", [128, 8, D], BF16)
            Bw = S.buf("wo")
            pieces = [(wo[:, k, :], wo_dram[k * 128:(k + 1) * 128, :], None) for k in range(8)]
            self.load_cast(st, None, None, 128, D, pieces, Bw, name="wo")
            NB = 3
            yt = [self.sb(st, f"o_yt{j}", [128, D], BF16) for j in range(NB)]
            gt = [self.sb(st, f"o_gt{j}", [128, D], BF16) for j in range(NB)]
            xt = [self.sb(st, f"o_xt{j}", [128, D]) for j in range(NB)]
            yg = [self.sb(st, f"o_yg{j}", [128, D], BF16) for j in range(NB)]
            ygT = [self.sb(st, f"o_ygT{j}", [128, D], BF16) for j in range(NB)]
            rt = [self.sb(st, f"o_rt{j}", [128, D]) for j in range(NB)]
            ot = [self.sb(st, f"o_ot{j}", [128, D]) for j in range(NB)]
            stt = [self.sb(st, f"o_st{j}", [128, 16]) for j in range(NB)]
            By, Bg, Bx, Byg, BygT, Br, Bo, Bs = (S.bufs(n, NB) for n in ("o_yt", "o_gt", "o_xt", "o_yg", "o_ygT", "o_rt", "o_ot", "o_st"))
            ptr = [self.ps(st, f"o_ptr{j}", [128, D], BF16) for j in range(2)]
            Bptr = S.bufs("o_ptr", 2)
            pout = [self.ps(st, f"o_po{j}", [128, 512]) for j in range(4)]
            Bpo = S.bufs("o_po", 4)
            eps_t = self.sb(st, "o_eps", [128, 1])
            S.op("pool", lambda e: e.memset(eps_t[:], LN_EPS), w=[Bw])
            ntiles = NT_LAT if (last and self.final_out) else NT
            tiles = [t for t in range(ntiles) if self.tile_filter is None or t in self.tile_filter]

            def stage_a(t):
                j = t % NB
                tok0 = t * 128
                isctx = t >= NT_LAT
                G = self.g_c if isctx else self.g_l
                if ysplit and not isctx:
                    half, u = t // 32, t % 32
                    r0 = half * 1024 + (u % 8) * 128
                    ysrc = self.ygath_t[u // 8].ap()[r0:r0 + 128, :]
                else:
                    ysrc = self.ybuf[tok0:tok0 + 128, :]
                S.dma("sp", yt[j][:], ysrc, w=[By[j]])
                S.dma("act", gt[j][:], self.gate[tok0:tok0 + 128, :], w=[Bg[j]])
                S.dma("sp", xt[j][:], src[tok0:tok0 + 128, :], w=[Bx[j]])
                S.op("pool", lambda e, j=j: e.tensor_tensor(out=yg[j][:], in0=yt[j][:], in1=gt[j][:], op=ALU.mult),
                     r=[By[j], Bg[j]], w=[Byg[j]])
                pj = t % 2
                for ec in range(8):
                    S.op("pe", lambda e, ec=ec, j=j, pj=pj: e.transpose(
                        ptr[pj][:, ec * 128:(ec + 1) * 128], yg[j][:, ec * 128:(ec + 1) * 128], self.ident_b[:]),
                        r=[Byg[j], self.B_const], w=[Bptr[pj]])
                S.op("act", lambda e, j=j, pj=pj: e.activation(out=ygT[j][:], in_=ptr[pj][:], func=AF.Copy),
                     r=[Bptr[pj]], w=[BygT[j]])
                for n in range(2):
                    pn = (t % 2) * 2 + n
                    for ec in range(8):
                        S.op("pe", lambda e, ec=ec, n=n, pn=pn, j=j: e.matmul(
                            pout[pn][:, :], ygT[j][:, ec * 128:(ec + 1) * 128], wo[:, ec, n * 512:(n + 1) * 512],
                            start=(ec == 0), stop=(ec == 7)), r=[BygT[j], Bw], w=[Bpo[pn]])

            def stage_b(t):
                j = t % NB
                tok0 = t * 128
                isctx = t >= NT_LAT
                G = self.g_c if isctx else self.g_l
                for n in range(2):
                    pn = (t % 2) * 2 + n
                    S.op("dve", lambda e, n=n, pn=pn, j=j, G=G: e.tensor_tensor(
                        out=rt[j][:, n * 512:(n + 1) * 512], in0=pout[pn][:, :], in1=G[:, n * 512:(n + 1) * 512], op=ALU.mult),
                        r=[Bpo[pn], self.B_mod], w=[Br[j]])
                S.op("dve", lambda e, j=j: e.scalar_tensor_tensor(
                    out=rt[j][:], in0=xt[j][:], scalar=ALPHA, in1=rt[j][:], op0=ALU.mult, op1=ALU.add),
                    r=[Bx[j], Br[j]], w=[Br[j]])
                for n in range(2):
                    S.op("dve", lambda e, n=n, j=j: e.bn_stats(stt[j][:, n * 6:(n + 1) * 6], rt[j][:, n * 512:(n + 1) * 512]),
                         r=[Br[j]], w=[Bs[j]])
                S.op("dve", lambda e, j=j: e.bn_aggr(stt[j][:, 12:14], stt[j][:, 0:12]), r=[Bs[j]], w=[Bs[j]])
                S.op("act", lambda e, j=j: e.activation(out=stt[j][:, 14:15], in_=stt[j][:, 13:14], func=AF.Sqrt, scale=1.0,
                                                        bias=eps_t[:, 0:1]), r=[Bs[j], Bw], w=[Bs[j]])
                S.op("dve", lambda e, j=j: e.reciprocal(stt[j][:, 15:16], stt[j][:, 14:15]), r=[Bs[j]], w=[Bs[j]])
                S.op("dve", lambda e, j=j: e.tensor_scalar(
                    out=ot[j][:], in0=rt[j][:], scalar1=stt[j][:, 12:13], scalar2=stt[j][:, 15:16], op0=ALU.subtract, op1=ALU.mult),
                    r=[Br[j], Bs[j]], w=[Bo[j]])
                S.op("pool", lambda e, j=j: e.tensor_tensor(out=ot[j][:], in0=ot[j][:], in1=self.lng[:], op=ALU.mult),
                     r=[Bo[j], self.B_mod], w=[Bo[j]])
                S.op("pool", lambda e, j=j: e.tensor_tensor(out=ot[j][:], in0=ot[j][:], in1=self.lnb[:], op=ALU.add),
                     r=[Bo[j], self.B_mod], w=[Bo[j]])
                S.dma("pool", dst[tok0:tok0 + 128, :], ot[j][:], r=[Bo[j]])

            for idx, t in enumerate(tiles):
                if idx == 0:
                    stage_a(t)
                if idx + 1 < len(tiles):
                    stage_a(tiles[idx + 1])
                stage_b(t)
            S.barrier()

    def odd_layer(self, l, src, dst, last):
        i = l // 2
        self.adaln(l)
        self.odd_project(i, src)
        self.odd_conv(i)
        self.mlstm(i)
        self.mlstm_post(i)
        self.na_attention(i, last)
        self.out_stage(self.od_wo[i], src, dst, last)

    def odd_project(self, i, src):
        S = self.S
        with contextlib.ExitStack() as st:
            wb = self.sb(st, "od_wb", [128, 8, OD_COLS], BF16)
            bcol = self.sb(st, "od_bcol", [128, 16])
            brow_f = self.sb(st, "od_brow_f", [1, 2576])
            brow = self.sb(st, "od_brow", [1, 2576], BF16)
            fb = self.sb(st, "od_fb", [128, 8])
            Bw = S.buf("od_w")
            S.dma("sp", bcol[:], self.od_bcol[i, :, :], w=[Bw])
            S.dma("sp", brow_f[:], self.od_brow[i, :, :], w=[Bw])
            S.dma("sp", fb[:], self.od_fb[i, :, :].to_broadcast([128, 8]), w=[Bw])
            S.op("dve", lambda e: e.tensor_copy(brow[:], brow_f[:]), r=[Bw], w=[Bw])
            pieces = []
            for k in range(8):
                for c in range(4):
                    pieces.append((wb[:, k, c * 1156:(c + 1) * 1156],
                                   self.od_w[i, k * 128:(k + 1) * 128, c * 1156:(c + 1) * 1156], None))
            self.load_cast(st, None, None, 128, 1156, pieces, Bw, name="odw")
            NXS = 3
            xt = [self.sb(st, f"xt{j}", [128, D]) for j in range(NXS)]
            Bx = S.bufs("xt", NXS)
            hT = [self.sb(st, f"hT{j}", [128, 8, 512], BF16) for j in range(2)]
            Bh = S.bufs("hT", 2)
            ff = [self.sb(st, f"ff{j}", [128, 512]) for j in range(3)]
            Bff = S.bufs("ff", 3)
            fo = [self.sb(st, f"fo{j}", [128, 512], BF16) for j in range(3)]
            Bfo = S.bufs("fo", 3)
            va = [self.sb(st, f"va{j}", [128, 4, 129], BF16) for j in range(2)]
            Bva = S.bufs("va", 2)
            vna = [self.sb(st, f"vna{j}", [128, 8, 65], BF16) for j in range(2)]
            Bvna = S.bufs("vna", 2)
            ot = [self.sb(st, f"ot{j}", [128, 512]) for j in range(2)]
            Bot = S.bufs("ot", 2)
            gt = [self.sb(st, f"gt{j}", [128, D], BF16) for j in range(2)]
            Bgt = S.bufs("gt", 2)
            gs = [self.sb(st, f"gs{j}", [128, 4, 16]) for j in range(2)]
            gtmp = [self.sb(st, f"gtmp{j}", [128, 4, 8]) for j in range(2)]
            Bgs = S.bufs("gs", 2)
            pp = [self.ps(st, f"pp{j}", [128, 512]) for j in range(7)]
            pg = self.ps(st, "pg", [128, 512])
            Bpp = S.bufs("pp", 7)
            Bpg = S.buf("pg")
            for j in range(2):
                S.op("pool", lambda e, j=j: e.memset(va[j][:], 1.0), w=[Bva[j]])
                S.op("pool", lambda e, j=j: e.memset(vna[j][:], 1.0), w=[Bvna[j]])
            self._pp_i = 0

            def next_pp():
                j = self._pp_i % 7
                self._pp_i += 1
                return pp[j], Bpp[j]

            nblk = (NTOK + 511) // 512
            xi = 0
            evac = 0
            ffi = 0
            foi = 0
            for blk in range(nblk):
                t0 = blk * 512
                ntok = min(512, NTOK - t0)
                nsub = ntok // 128
                m = 0 if t0 < LAT else 1
                hb = blk % 2
                for s in range(nsub):
                    xj = xi % NXS
                    xi += 1
                    S.dma("sp", xt[xj][:], src[t0 + s * 128:t0 + (s + 1) * 128, :], w=[Bx[xj]])
                    for half in range(2):
                        p, Bp = next_pp()
                        for q in range(4):
                            dc = half * 4 + q
                            S.op("pe", lambda e, p=p, q=q, dc=dc, xj=xj: e.transpose(
                                p[:, q * 128:(q + 1) * 128], xt[xj][:, dc * 128:(dc + 1) * 128], self.ident_f[:]),
                                r=[Bx[xj], self.B_const], w=[Bp])
                        for q in range(4):
                            dc = half * 4 + q
                            if evac % 2 == 0:
                                S.op("act", lambda e, p=p, q=q, dc=dc, s=s, m=m: e.activation(
                                    out=hT[hb][:, dc, s * 128:(s + 1) * 128], in_=p[:, q * 128:(q + 1) * 128],
                                    func=AF.Identity, scale=self.sc1[:, dc, m:m + 1], bias=self.sh[:, dc, m:m + 1]),
                                    r=[Bp, self.B_mod], w=[Bh[hb]])
                            else:
                                S.op("dve", lambda e, p=p, q=q, dc=dc, s=s, m=m: e.tensor_scalar(
                                    out=hT[hb][:, dc, s * 128:(s + 1) * 128], in0=p[:, q * 128:(q + 1) * 128],
                                    scalar1=self.sc1[:, dc, m:m + 1], scalar2=self.sh[:, dc, m:m + 1],
                                    op0=ALU.mult, op1=ALU.add), r=[Bp, self.B_mod], w=[Bh[hb]])
                            evac += 1
                for c in range(16):
                    p, Bp = next_pp()
                    for k in range(8):
                        S.op("pe", lambda e, p=p, k=k, c=c: e.matmul(
                            p[:, :ntok], wb[:, k, c * 128:(c + 1) * 128], hT[hb][:, k, :ntok], start=(k == 0), stop=(k == 7)),
                            r=[Bw, Bh[hb]], w=[Bp])
                    if c < 8:
                        fj = ffi % 3
                        ffi += 1
                        if c % 2 == 0:
                            S.op("act", lambda e, p=p, c=c, fj=fj: e.activation(
                                out=ff[fj][:, :ntok], in_=p[:, :ntok], func=AF.Identity, bias=bcol[:, c:c + 1]),
                                r=[Bp, Bw], w=[Bff[fj]])
                        else:
                            S.op("dve", lambda e, p=p, c=c, fj=fj: e.tensor_scalar(
                                out=ff[fj][:, :ntok], in0=p[:, :ntok], scalar1=bcol[:, c:c + 1], scalar2=None, op0=ALU.add),
                                r=[Bp, Bw], w=[Bff[fj]])
                        S.dma("pool", self.qkpre[c, :, t0:t0 + ntok], ff[fj][:, :ntok], r=[Bff[fj]])
                    else:
                        fj = foi % 3
                        foi += 1
                        if c % 2 == 0:
                            S.op("act", lambda e, p=p, c=c, fj=fj: e.activation(
                                out=fo[fj][:, :ntok], in_=p[:, :ntok], func=AF.Identity, bias=bcol[:, c:c + 1]),
                                r=[Bp, Bw], w=[Bfo[fj]])
                        else:
                            S.op("dve", lambda e, p=p, c=c, fj=fj: e.tensor_scalar(
                                out=fo[fj][:, :ntok], in0=p[:, :ntok], scalar1=bcol[:, c:c + 1], scalar2=None, op0=ALU.add),
                                r=[Bp, Bw], w=[Bfo[fj]])
                        dd = self.qda if c < 12 else self.kda
                        S.dma("pool", dd[(c - 8) % 4, :, t0:t0 + ntok], fo[fj][:, :ntok], r=[Bfo[fj]])
                gb = blk % 2
                for s in range(nsub):
                    tt = (t0 // 128) + s
                    tok0 = t0 + s * 128

                    def tm(p, c0, n, b0, s=s):
                        for k in range(8):
                            S.op("pe", lambda e, k=k: e.matmul(
                                p[:, 0:n], hT[hb][:, k, s * 128:(s + 1) * 128], wb[:, k, c0:c0 + n], start=(k == 0), stop=False),
                                r=[Bw, Bh[hb]], w=[Bp])
                        S.op("pe", lambda e: e.matmul(p[:, 0:n], self.ones_b[0:1, :], brow[0:1, b0:b0 + n], start=False, stop=True),
                             r=[Bw, self.B_const], w=[Bp])
                    p, Bp = next_pp()
                    tm(p, OV, 512, 0)
                    vj = tt % 2
                    S.op("act", lambda e, p=p, vj=vj: e.activation(
                        out=va[vj][:, :, 0:128], in_=p[:, :].rearrange("p (h c) -> p h c", h=4), func=AF.Copy),
                        r=[Bp], w=[Bva[vj]])
                    S.dma("pool", self.vda[:, :, tt, :].rearrange("h p c -> p h c"), va[vj][:], r=[Bva[vj]])
                    p, Bp = next_pp()
                    tm(p, OO, 512, 512)
                    oj = tt % 2
                    S.op("act", lambda e, p=p, oj=oj: e.activation(out=ot[oj][:], in_=p[:, :], func=AF.Sigmoid), r=[Bp], w=[Bot[oj]])
                    S.dma("pool", self.og[tok0:tok0 + 128, :], ot[oj][:], r=[Bot[oj]])
                    Bp = Bpg
                    for k in range(8):
                        S.op("pe", lambda e, k=k, s=s: e.matmul(
                            pg[:, s * 16:(s + 1) * 16], hT[hb][:, k, s * 128:(s + 1) * 128], wb[:, k, OGT:OGT + 16],
                            start=(k == 0), stop=False), r=[Bw, Bh[hb]], w=[Bpg])
                    S.op("pe", lambda e, s=s: e.matmul(pg[:, s * 16:(s + 1) * 16], self.ones_b[0:1, :], brow[0:1, 1024:1040],
                                                       start=False, stop=True), r=[Bw, self.B_const], w=[Bpg])
                    p, Bp = next_pp()
                    tm(p, OVN, 512, 1040)
                    vj = tt % 2
                    S.op("act", lambda e, p=p, vj=vj: e.activation(
                        out=vna[vj][:, :, 0:64], in_=p[:, :].rearrange("p (h c) -> p h c", h=8), func=AF.Copy),
                        r=[Bp], w=[Bvna[vj]])
                    S.dma("pool", self.vm[:, :, tt, :].rearrange("h p c -> p h c"), vna[vj][:], r=[Bvna[vj]])
                    gj = tt % 2
                    for n in range(2):
                        p, Bp = next_pp()
                        tm(p, OG + n * 512, 512, 1552 + n * 512)
                        S.op("act", lambda e, p=p, n=n, gj=gj: e.activation(
                            out=gt[gj][:, n * 512:(n + 1) * 512], in_=p[:, :], func=AF.Silu), r=[Bp], w=[Bgt[gj]])
                    S.dma("pool", self.gate[tok0:tok0 + 128, :], gt[gj][:], r=[Bgt[gj]])
                pgv = pg[:, 0:nsub * 16].rearrange("p (s c) -> p s c", c=16)
                S.op("dve", lambda e, pgv=pgv: e.tensor_copy(gs[gb][:, 0:nsub, 0:8], pgv[:, :, 0:8]), r=[Bpg], w=[Bgs[gb]])
                for s in range(nsub):
                    S.op("dve", lambda e, s=s: e.tensor_tensor(out=gtmp[gb][:, s, :], in0=pg[:, s * 16 + 8:s * 16 + 16], in1=fb[:, :],
                                                               op=ALU.add), r=[Bpg, Bw], w=[Bgs[gb]])
                S.op("act", lambda e: e.activation(out=gtmp[gb][:, 0:nsub, :], in_=gtmp[gb][:, 0:nsub, :], func=AF.Exp, scale=-1.0),
                     r=[Bgs[gb]], w=[Bgs[gb]])
                S.op("act", lambda e: e.activation(out=gtmp[gb][:, 0:nsub, :], in_=gtmp[gb][:, 0:nsub, :], func=AF.Ln, bias=1.0),
                     r=[Bgs[gb]], w=[Bgs[gb]])
                S.op("dve", lambda e: e.tensor_scalar(out=gs[gb][:, 0:nsub, 8:16], in0=gtmp[gb][:, 0:nsub, :], scalar1=-1.0,
                                                      scalar2=None, op0=ALU.mult), r=[Bgs[gb]], w=[Bgs[gb]])
                tt0 = t0 // 128
                S.dma("pool", self.gates[:, tt0:tt0 + nsub, :], gs[gb][:, 0:nsub, :], r=[Bgs[gb]])
            S.barrier()

    def odd_conv(self, i):
        S = self.S
        SEG = 2048
        with contextlib.ExitStack() as st:
            cw = self.sb(st, "cv_w", [128, 8, 5])
            cb = self.sb(st, "cv_b", [128, 8])
            Bw = S.buf("cv_w")
            S.dma("sp", cw[:], self.od_convw[i, :, :, :], w=[Bw])
            S.dma("sp", cb[:], self.od_convb[i, :, :], w=[Bw])
            xin = [self.sb(st, f"cv_x{j}", [128, SEG + 4]) for j in range(2)]
            Bxin = S.bufs("cv_x", 2)
            acc = [self.sb(st, f"cv_a{j}", [128, SEG]) for j in range(2)]
            Bacc = S.bufs("cv_a", 2)
            tmp = [self.sb(st, f"cv_t{j}", [128, SEG]) for j in range(2)]
            Btmp = S.bufs("cv_t", 2)
            outb = [self.sb(st, f"cv_o{j}", [128, SEG], BF16) for j in range(2)]
            Bout = S.bufs("cv_o", 2)
            segs = [(a, SEG, 0, LAT) for a in range(0, LAT, SEG)] + [(LAT, CTX, LAT, LAT + CTX)]
            it = 0
            for c in range(8):
                for (t0, n, lo, hi) in segs:
                    j = it % 2
                    it += 1
                    a = max(t0 - 2, lo)
                    b = min(t0 + n + 2, hi)
                    if a > t0 - 2:
                        S.op("pool", lambda e, j=j: e.memset(xin[j][:, 0:2], 0.0), w=[Bxin[j]])
                    if b < t0 + n + 2:
                        S.op("pool", lambda e, j=j, n=n: e.memset(xin[j][:, n + 2:n + 4], 0.0), w=[Bxin[j]])
                    S.dma("sp", xin[j][:, a - (t0 - 2):b - (t0 - 2)], self.qkpre[c, :, a:b], w=[Bxin[j]])
                    S.op("act", lambda e, j=j, n=n, c=c: e.activation(
                        out=acc[j][:, 0:n], in_=xin[j][:, 0:n], func=AF.Copy, scale=cw[:, c, 0:1]),
                        r=[Bxin[j], Bw], w=[Bacc[j]])
                    for k in range(1, 5):
                        S.op("dve", lambda e, j=j, n=n, c=c, k=k: e.scalar_tensor_tensor(
                            out=acc[j][:, 0:n], in0=xin[j][:, k:k + n], scalar=cw[:, c, k:k + 1], in1=acc[j][:, 0:n],
                            op0=ALU.mult, op1=ALU.add), r=[Bxin[j], Bw, Bacc[j]], w=[Bacc[j]])
                    if True:
                        S.op("act", lambda e, j=j, n=n, c=c: e.activation(
                            out=outb[j][:, 0:n], in_=acc[j][:, 0:n], func=AF.Silu, bias=cb[:, c:c + 1]),
                            r=[Bacc[j], Bw], w=[Bout[j]])
                        dd = self.mq if c < 4 else self.mk
                        S.dma("pool", dd[c % 4, :, t0:t0 + n], outb[j][:, 0:n], r=[Bout[j]])
                    else:
                        S.op("act", lambda e, j=j, n=n, c=c: e.activation(
                            out=tmp[j][:, 0:n], in_=acc[j][:, 0:n], func=AF.Silu, bias=cb[:, c:c + 1]),
                            r=[Bacc[j], Bw], w=[Btmp[j]])
                        S.op("pool", lambda e, j=j, n=n: e.tensor_scalar(
                            out=outb[j][:, 0:n], in0=tmp[j][:, 0:n], scalar1=128 ** -0.5, scalar2=None, op0=ALU.mult),
                            r=[Btmp[j]], w=[Bout[j]])
                        S.dma("sp", self.mk[c - 4, :, t0:t0 + n], outb[j][:, 0:n], r=[Bout[j]])
            S.barrier()

    def mlstm(self, i):
        S = self.S
        with contextlib.ExitStack() as st:
            tri = self.sb(st, "ml_tri", [128, 3, 128])
            G = self.sb(st, "ml_G", [128, NT, 16])
            A = self.sb(st, "ml_A", [128, NT, 8])
            A2 = self.sb(st, "ml_A2", [128, NT, 8])
            Bq = self.sb(st, "ml_Bq", [128, NT, 8])
            EB = self.sb(st, "ml_EB", [128, NT, 8])
            Bg = S.buf("ml_g")
            S.dma("sp", tri[:], self.tri[:, :, :], w=[Bg])
            S.dma("sp", G[:], self.gates[:, :, :], w=[Bg])
            lnks = self.sb(st, "ml_lnks", [128, 1])
            S.op("pool", lambda e: e.memset(lnks[:], math.log(128 ** -0.5)), w=[Bg])
            with contextlib.ExitStack() as st1:
                pg = [self.ps(st1, f"ml_pg{j}", [128, 512]) for j in range(3)]
                Bpg = S.bufs("ml_pg", 3)
                tmpg = self.sb(st1, "ml_tmpg", [128, 32, 8])
                Btg = S.buf("ml_tmpg")
                grp = 0
                for g0 in range(0, NT, 32):
                    g1 = min(g0 + 32, NT)
                    pj = grp % 3
                    grp += 1
                    for t in range(g0, g1):
                        o = (t - g0) * 16
                        S.op("pe", lambda e, t=t, o=o, pj=pj: e.matmul(pg[pj][:, o:o + 4], tri[:, 0, :], G[:, t, 8:12], start=True, stop=True),
                             r=[Bg], w=[Bpg[pj]])
                        S.op("pe", lambda e, t=t, o=o, pj=pj: e.matmul(pg[pj][:, o + 4:o + 8], tri[:, 1, :], G[:, t, 12:16], start=True, stop=True),
                             r=[Bg], w=[Bpg[pj]])
                        S.op("pe", lambda e, t=t, o=o, pj=pj: e.matmul(pg[pj][:, o + 8:o + 16], tri[:, 2, :], G[:, t, 8:16], start=True, stop=True),
                             r=[Bg], w=[Bpg[pj]])
                    n = g1 - g0
                    pv = pg[pj][:, 0:n * 16].rearrange("p (t c) -> p t c", c=16)
                    S.op("dve", lambda e, pv=pv, g0=g0, g1=g1, n=n: e.tensor_tensor(
                        out=tmpg[:, 0:n, :], in0=G[:, g0:g1, 0:8], in1=pv[:, :, 0:8], op=ALU.subtract), r=[Bg, Bpg[pj]], w=[Btg])
                    S.op("act", lambda e, g0=g0, g1=g1, n=n: e.activation(out=A[:, g0:g1, :], in_=tmpg[:, 0:n, :], func=AF.Exp,
                                                                          bias=lnks[:, 0:1]), r=[Btg, Bg], w=[Bg])
                    S.op("act", lambda e, pv=pv, g0=g0, g1=g1: e.activation(out=Bq[:, g0:g1, :], in_=pv[:, :, 0:8], func=AF.Exp),
                         r=[Bpg[pj]], w=[Bg])
                    S.op("act", lambda e, pv=pv, g0=g0, g1=g1: e.activation(out=EB[:, g0:g1, :], in_=pv[:, :, 8:16], func=AF.Exp),
                         r=[Bpg[pj]], w=[Bg])
                    S.op("dve", lambda e, g0=g0, g1=g1: e.tensor_tensor(out=A2[:, g0:g1, :], in0=A[:, g0:g1, :], in1=EB[:, g0:g1, :],
                                                                        op=ALU.mult), r=[Bg], w=[Bg])
                S.barrier()
            qT = self.sb(st, "ml_qT", [128, NTOK], BF16)
            kT = self.sb(st, "ml_kT", [128, NTOK], BF16)
            V = self.sb(st, "ml_V", [128, NT, 129], BF16)
            KTOK = self.sb(st, "ml_KTOK", [128, NT, 128], BF16)
            Bin = S.buf("ml_in")
            Bkt = S.buf("ml_ktok")
            Cn = [self.sb(st, f"ml_Cn{d}", [128, 129]) for d in range(2)]
            Cnb = [[self.sb(st, f"ml_Cnb{d}{j}", [128, 129], BF16) for j in range(2)] for d in range(2)]
            BCn = S.bufs("ml_Cn", 2)
            BCnb = [S.bufs(f"ml_Cnb{d}", 2) for d in range(2)]
            NW = 6
            W = [self.sb(st, f"ml_W{j}", [128, 128], BF16) for j in range(NW)]
            BW = S.bufs("ml_W", NW)
            v2 = [self.sb(st, f"ml_v2{j}", [128, 129], BF16) for j in range(NW)]
            Bv2 = S.bufs("ml_v2", NW)
            ho = [self.sb(st, f"ml_ho{j}", [128, 128]) for j in range(NW)]
            Bho = S.bufs("ml_ho", NW)
            sm = [self.sb(st, f"ml_sm{j}", [128, 4]) for j in range(NW)]
            Bsm = S.bufs("ml_sm", NW)
            NS, NKV, NN = 2, 2, 4
            order = [[NT_LAT, NT_LAT + 1] + list(range(NT_LAT)), [NT_LAT + 1, NT_LAT] + list(range(NT_LAT - 1, -1, -1))]
            wi = 0
            for h in range(4):
                S.dma("sp", qT[:], self.mq[h, :, :], w=[Bin])
                S.dma("act", kT[:], self.mk[h, :, :], w=[Bin])
                S.dma("sp", V[:], self.vda[h, :, :, :], w=[Bin])
                with contextlib.ExitStack() as stp:
                    ppbs = [self.ps(stp, f"ml_ppb{j}", [128, 1024], BF16) for j in range(2)]
                    Bppbs = S.bufs("ml_ppb", 2)
                    for gi, t0 in enumerate(range(0, NT, 8)):
                        n = min(8, NT - t0)
                        ppb, Bppb = ppbs[gi % 2], Bppbs[gi % 2]
                        for q in range(n):
                            t = t0 + q
                            S.op("pe", lambda e, t=t, q=q, ppb=ppb: e.transpose(
                                ppb[:, q * 128:(q + 1) * 128], kT[:, t * 128:(t + 1) * 128], self.ident_b[:]),
                                r=[Bin, self.B_const], w=[Bppb])
                        S.op("act", lambda e, t0=t0, n=n, ppb=ppb: e.activation(
                            out=KTOK[:, t0:t0 + n, :], in_=ppb[:, 0:n * 128].rearrange("p (t c) -> p t c", c=128), func=AF.Copy),
                            r=[Bppb], w=[Bkt])
                    S.barrier()
                stq = contextlib.ExitStack()
                ps_s = [self.ps(stq, f"ml_pss{j}", [128, 512]) for j in range(NS)]
                ps_kv = [self.ps(stq, f"ml_pkv{j}", [128, 512]) for j in range(NKV)]
                ps_n = [self.ps(stq, f"ml_pn{j}", [128, 512]) for j in range(NN)]
                Bps_s, Bps_kv, Bps_n = S.bufs("ml_pss", NS), S.bufs("ml_pkv", NKV), S.bufs("ml_pn", NN)
                for d in range(2):
                    S.op("pool", lambda e, d=d: e.memset(Cn[d][:], 0.0), w=[BCn[d]])
                    S.op("pool", lambda e, d=d: e.memset(Cnb[d][0][:], 0.0), w=[BCnb[d][0]])

                def stage1(step, d, wi):
                    t = order[d][step]
                    hd = d * 4 + h
                    j = wi % NW
                    p3 = wi % NS
                    tsl = slice(t * 128, (t + 1) * 128)
                    S.op("pe", lambda e: e.matmul(ps_s[p3][:, 0:128], kT[:, tsl], qT[:, tsl], start=True, stop=True),
                         r=[Bin], w=[Bps_s[p3]])
                    S.op("dve", lambda e: e.scalar_tensor_tensor(
                        out=W[j][:], in0=ps_s[p3][:, 0:128], scalar=A[:, t, hd:hd + 1], in1=tri[:, d, :],
                        op0=ALU.mult, op1=ALU.mult), r=[Bps_s[p3], Bg], w=[BW[j]])
                    S.op("act", lambda e: e.activation(
                        out=v2[j][:], in_=V[:, t, :], func=AF.Copy, scale=A2[:, t, hd:hd + 1]),
                        r=[Bin, Bg], w=[Bv2[j]])

                def stage2(step, d, wi):
                    t = order[d][step]
                    hd = d * 4 + h
                    cur, nxt = step % 2, (step + 1) % 2
                    j = wi % NW
                    p3 = wi % NKV
                    p6 = wi % NN
                    tsl = slice(t * 128, (t + 1) * 128)
                    S.op("pe", lambda e: e.matmul(ps_kv[p3][:, 0:129], KTOK[:, t, :], v2[j][:], start=True, stop=True),
                         r=[Bkt, Bv2[j]], w=[Bps_kv[p3]])
                    S.op("pe", lambda e: e.matmul(ps_n[p6][:, 0:129], W[j][:], V[:, t, :], start=True, stop=False),
                         r=[BW[j], Bin], w=[Bps_n[p6]])
                    S.op("pe", lambda e: e.matmul(ps_n[p6][:, 0:129], qT[:, tsl], Cnb[d][cur][:], start=False, stop=True),
                         r=[Bin, BCnb[d][cur]], w=[Bps_n[p6]])
                    S.op("dve", lambda e: e.scalar_tensor_tensor(
                        out=Cn[d][:], in0=Cn[d][:], scalar=EB[:, t, hd:hd + 1], in1=ps_kv[p3][:, 0:129],
                        op0=ALU.mult, op1=ALU.add), r=[BCn[d], Bps_kv[p3], Bg], w=[BCn[d]])
                    S.op("act", lambda e: e.activation(out=Cnb[d][nxt][:], in_=Cn[d][:], func=AF.Copy),
                         r=[BCn[d]], w=[BCnb[d][nxt]])

                def back(step, d, wi):
                    t = order[d][step]
                    hd = d * 4 + h
                    j = wi % NW
                    p6 = wi % NN
                    S.op("dve", lambda e: e.tensor_tensor(
                        out=sm[j][:, 0:1], in0=ps_n[p6][:, 128:129], in1=Bq[:, t, hd:hd + 1], op=ALU.mult),
                        r=[Bps_n[p6], Bg], w=[Bsm[j]])
                    S.op("dve", lambda e: e.scalar_tensor_tensor(out=sm[j][:, 1:2], in0=sm[j][:, 0:1], scalar=-1.0,
                                                                 in1=sm[j][:, 0:1], op0=ALU.mult, op1=ALU.max),
                         r=[Bsm[j]], w=[Bsm[j]])
                    S.op("dve", lambda e: e.tensor_scalar(out=sm[j][:, 1:2], in0=sm[j][:, 1:2], scalar1=1.0, scalar2=None,
                                                          op0=ALU.max), r=[Bsm[j]], w=[Bsm[j]])
                    S.op("dve", lambda e: e.reciprocal(sm[j][:, 2:3], sm[j][:, 1:2]), r=[Bsm[j]], w=[Bsm[j]])
                    S.op("dve", lambda e: e.tensor_tensor(
                        out=sm[j][:, 3:4], in0=sm[j][:, 2:3], in1=Bq[:, t, hd:hd + 1], op=ALU.mult), r=[Bsm[j], Bg], w=[Bsm[j]])
                    S.op("dve", lambda e: e.tensor_scalar(
                        out=ho[j][:], in0=ps_n[p6][:, 0:128], scalar1=sm[j][:, 3:4], scalar2=None, op0=ALU.mult),
                        r=[Bps_n[p6], Bsm[j]], w=[Bho[j]])
                    S.dma("sp", self.hfb[d, t * 128:(t + 1) * 128, h * 128:(h + 1) * 128], ho[j][:], r=[Bho[j]])

                items = []
                for step in range(NT):
                    for d in range(2):
                        items.append((step, d, wi))
                        wi += 1
                npair = len(items) // 2
                for k in range(npair + 2):
                    if k < npair:
                        stage1(*items[2 * k])
                        stage1(*items[2 * k + 1])
                    if 0 <= k - 1 < npair:
                        stage2(*items[2 * (k - 1)])
                        stage2(*items[2 * (k - 1) + 1])
                    if 0 <= k - 2 < npair:
                        back(*items[2 * (k - 2)])
                        back(*items[2 * (k - 2) + 1])
                S.barrier()
                stq.close()
            S.barrier()

    def mlstm_post(self, i):
        S = self.S
        with contextlib.ExitStack() as st:
            ng = self.sb(st, "mp_ng", [128, 512])
            Bw = S.buf("mp_w")
            S.dma("sp", ng[:], self.od_ng[i, :, :].to_broadcast([128, 512]), w=[Bw])
            eps_t = self.sb(st, "mp_eps", [128, 1])
            S.op("pool", lambda e: e.memset(eps_t[:], LN_EPS), w=[Bw])
            NB = 3
            hf = [self.sb(st, f"mp_hf{j}", [128, 512]) for j in range(NB)]
            hb = [self.sb(st, f"mp_hb{j}", [128, 512]) for j in range(NB)]
            og = [self.sb(st, f"mp_og{j}", [128, 512]) for j in range(NB)]
            hs = [self.sb(st, f"mp_hs{j}", [128, 512]) for j in range(NB)]
            yo = [self.sb(st, f"mp_yo{j}", [128, 512]) for j in range(NB)]
            yob = [self.sb(st, f"mp_yob{j}", [128, 512], BF16) for j in range(NB)]
            stt = [self.sb(st, f"mp_st{j}", [128, 4, 12]) for j in range(NB)]
            Bhf, Bhb, Bog, Bhs, Byo, Bst = (S.bufs(n, NB) for n in ("mp_hf", "mp_hb", "mp_og", "mp_hs", "mp_yo", "mp_st"))
            NB = 3

            def post_a(t):
                j = t % NB
                tok0 = t * 128
                S.dma("sp", hf[j][:], self.hfb[0, tok0:tok0 + 128, :], w=[Bhf[j]])
                S.dma("act", hb[j][:], self.hfb[1, tok0:tok0 + 128, :], w=[Bhb[j]])
                S.dma("pool", og[j][:], self.og[tok0:tok0 + 128, :], w=[Bog[j]])
                S.op("pool", lambda e, j=j: e.tensor_tensor(out=hs[j][:], in0=hf[j][:], in1=hb[j][:], op=ALU.add),
                     r=[Bhf[j], Bhb[j]], w=[Bhs[j]])
                S.op("pool", lambda e, j=j: e.tensor_tensor(out=og[j][:], in0=og[j][:], in1=ng[:], op=ALU.mult),
                     r=[Bog[j], Bw], w=[Bog[j]])

            def post_b(t):
                j = t % NB
                tok0 = t * 128
                for h in range(4):
                    S.op("dve", lambda e, j=j, h=h: e.bn_stats(stt[j][:, h, 0:6], hs[j][:, h * 128:(h + 1) * 128]),
                         r=[Bhs[j]], w=[Bst[j]])
                    S.op("dve", lambda e, j=j, h=h: e.bn_aggr(stt[j][:, h, 6:8], stt[j][:, h, 0:6]), r=[Bst[j]], w=[Bst[j]])
                S.op("act", lambda e, j=j: e.activation(out=stt[j][:, :, 8:9], in_=stt[j][:, :, 7:8], func=AF.Sqrt, scale=1.0,
                                                        bias=eps_t[:, 0:1]), r=[Bst[j], Bw], w=[Bst[j]])
                S.op("dve", lambda e, j=j: e.reciprocal(stt[j][:, :, 9:10], stt[j][:, :, 8:9]), r=[Bst[j]], w=[Bst[j]])
                for h in range(4):
                    S.op("dve", lambda e, j=j, h=h: e.tensor_scalar(
                        out=yo[j][:, h * 128:(h + 1) * 128], in0=hs[j][:, h * 128:(h + 1) * 128], scalar1=stt[j][:, h, 6:7],
                        scalar2=stt[j][:, h, 9:10], op0=ALU.subtract, op1=ALU.mult), r=[Bhs[j], Bst[j]], w=[Byo[j]])
                S.op("pool", lambda e, j=j: e.tensor_tensor(out=yob[j][:], in0=yo[j][:], in1=og[j][:], op=ALU.mult),
                     r=[Byo[j], Bog[j]], w=[Byo[j]])
                S.dma("sp", self.ybuf[tok0:tok0 + 128, 0:512], yob[j][:], r=[Byo[j]])

            for t in range(NT):
                if t == 0:
                    post_a(0)
                if t + 1 < NT:
                    post_a(t + 1)
                post_b(t)
            S.barrier()

    def na_attention(self, i, last):
        S = self.S
        with contextlib.ExitStack() as st:
            QNe = self.sb(st, "na_Qe", [128, NTOK], BF16)
            QNo = self.sb(st, "na_Qo", [128, NTOK], BF16)
            KN = self.sb(st, "na_K", [128, NTOK], BF16)
            VN = self.sb(st, "na_V", [128, NT, 65], BF16)
            MB = self.sb(st, "na_MB", [128, NA_NVAR, 128], BF16)
            Bqk = S.buf("na_qk")
            Bv = S.buf("na_v")
            Bmb = S.buf("na_mb")
            stg = [self.sb(st, f"na_stg{j}", [128, 7, 128]) for j in range(2)]
            Bstg = S.bufs("na_stg", 2)
            NP = 3
            ptA = [self.sb(st, f"na_ptA{j}", [128, 512], BF16) for j in range(NP)]
            ptB = [self.sb(st, f"na_ptB{j}", [128, 384], BF16) for j in range(NP)]
            BptA, BptB = S.bufs("na_ptA", NP), S.bufs("na_ptB", NP)
            rr = [self.sb(st, f"na_rr{j}", [128, 2]) for j in range(NP)]
            to = [self.sb(st, f"na_to{j}", [128, 64], BF16) for j in range(NP)]
            Bto = S.bufs("na_to", NP)
            psA = [self.ps(st, f"na_psA{j}", [128, 512]) for j in range(2)]
            psB = [self.ps(st, f"na_psB{j}", [128, 512]) for j in range(2)]
            acc = [self.ps(st, f"na_acc{j}", [128, 512]) for j in range(2)]
            BpsA, BpsB, Bacc = S.bufs("na_psA", 2), S.bufs("na_psB", 2), S.bufs("na_acc", 2)
            it = 0
            qtiles = list(range(NT_LAT)) + ([] if (last and self.final_out) else [NT_LAT, NT_LAT + 1])
            if self.tile_filter is not None:
                qtiles = [t for t in qtiles if t in self.tile_filter]
            for h in range(8):
                lo = (h % 2) * 64
                if h == 0:
                    S.op("pool", lambda e: e.memset(QNe[64:128, :], 0.0), w=[Bqk])
                    S.op("pool", lambda e: e.memset(QNo[0:64, :], 0.0), w=[Bqk])
                if h % 2 == 0:
                    S.dma("sp", QNe[0:64, :], self.qda[h // 2, 0:64, :], w=[Bqk])
                    S.dma("sp", QNo[64:128, :], self.qda[h // 2, 64:128, :], w=[Bqk])
                    S.dma("sp", KN[:], self.kda[h // 2, :, :], w=[Bqk])
                QN = QNe if h % 2 == 0 else QNo
                S.dma("sp", VN[:], self.vm[h, :, :, :], w=[Bv])
                for v0 in range(0, NA_NVAR, 7):
                    n = min(7, NA_NVAR - v0)
                    sj = (v0 // 7) % 2
                    S.dma("sp", stg[sj][:, 0:n, :], self.od_nat[i, h, :, v0:v0 + n, :], w=[Bstg[sj]])
                    S.op("pool", lambda e, sj=sj, n=n, v0=v0: e.tensor_scalar(
                        out=MB[:, v0:v0 + n, :], in0=stg[sj][:, 0:n, :], scalar1=1.0 / NA_SCALE, scalar2=None, op0=ALU.mult),
                        r=[Bstg[sj]], w=[Bmb])
                for j in qtiles:
                    pj = it % 2
                    tj = it % NP
                    it += 1
                    qsl = slice(j * 128, (j + 1) * 128)
                    if j < NT_LAT:
                        slots = [(kt, var) for (kt, var) in NA_PLAN[j]] + [(NT_LAT, None), (NT_LAT + 1, None)]
                    else:
                        slots = [(NT_LAT, None), (NT_LAT + 1, None)]
                    nA = min(4, len(slots))
                    nB = len(slots) - nA
                    for si, (kt, var) in enumerate(slots):
                        if si < 4:
                            dst, Bd = psA[pj][:, si * 128:(si + 1) * 128], BpsA[pj]
                        else:
                            dst, Bd = psB[pj][:, (si - 4) * 128:(si - 3) * 128], BpsB[pj]
                        S.op("pe", lambda e, dst=dst, kt=kt, var=var, QN=QN: e.matmul(
                            dst, KN[:, kt * 128:(kt + 1) * 128], QN[:, qsl], start=True, stop=(var is None)),
                            r=[Bqk], w=[Bd])
                        if var is not None:
                            S.op("pe", lambda e, dst=dst, var=var: e.matmul(dst, self.ident_b[:], MB[:, var, :], start=False, stop=True),
                                 r=[Bmb, self.B_const], w=[Bd])
                    S.op("act", lambda e, pj=pj, tj=tj, nA=nA: e.activation(out=ptA[tj][:, 0:nA * 128], in_=psA[pj][:, 0:nA * 128],
                                                                            func=AF.Exp, scale=NA_SCALE), r=[BpsA[pj]], w=[BptA[tj]])
                    if nB > 0:
                        S.op("act", lambda e, pj=pj, tj=tj, nB=nB: e.activation(out=ptB[tj][:, 0:nB * 128], in_=psB[pj][:, 0:nB * 128],
                                                                                func=AF.Exp, scale=NA_SCALE), r=[BpsB[pj]], w=[BptB[tj]])
                    for si, (kt, var) in enumerate(slots):
                        if si < 4:
                            lhs, Bl = ptA[tj][:, si * 128:(si + 1) * 128], BptA[tj]
                        else:
                            lhs, Bl = ptB[tj][:, (si - 4) * 128:(si - 3) * 128], BptB[tj]
                        S.op("pe", lambda e, lhs=lhs, kt=kt, si=si, pj=pj: e.matmul(
                            acc[pj][:, 0:65], lhs, VN[:, kt, :], start=(si == 0), stop=(si == len(slots) - 1)),
                            r=[Bl, Bv], w=[Bacc[pj]])
                    S.op("dve", lambda e, pj=pj, tj=tj: e.reciprocal(rr[tj][:, 0:1], acc[pj][:, 64:65]), r=[Bacc[pj]], w=[Bto[tj]])
                    S.op("dve", lambda e, pj=pj, tj=tj: e.tensor_scalar(out=to[tj][:], in0=acc[pj][:, 0:64], scalar1=rr[tj][:, 0:1],
                                                                        scalar2=None, op0=ALU.mult), r=[Bacc[pj], Bto[tj]], w=[Bto[tj]])
                    S.dma("pool", self.ybuf[j * 128:(j + 1) * 128, 512 + h * 64:512 + (h + 1) * 64], to[tj][:], r=[Bto[tj]])
            S.barrier()


def _swap64(cols):
    return np.concatenate([cols[32:64], cols[0:32]])


def _rope_tables():
    t = np.arange(LAT)
    row = (t // GRID_W).astype(np.float32)
    col = (t % GRID_W).astype(np.float32)

    def tab(dim, nrows_pattern):
        n_freq = dim // 4
        freqs = (10000.0 ** (-np.arange(n_freq, dtype=np.float32) / n_freq)).astype(np.float32)
        ang = np.concatenate([row[:, None] * freqs, col[:, None] * freqs], axis=-1).astype(np.float32)
        cos = np.cos(ang).astype(np.float32).T
        sin = np.sin(ang).astype(np.float32).T
        half = dim // 2
        c = np.concatenate([cos, cos], 0)
        s = np.concatenate([-sin, sin], 0)
        c = np.concatenate([c, np.ones((dim, CTX), np.float32)], 1)
        s = np.concatenate([s, np.zeros((dim, CTX), np.float32)], 1)
        return c, s

    c64, s64 = tab(64, None)
    c32, s32 = tab(32, None)
    rope_da = np.stack([np.concatenate([c64, c64], 0), np.concatenate([s64, s64], 0)]).astype(np.float32)
    ml_c = np.ones((128, NTOK), np.float32)
    ml_s = np.zeros((128, NTOK), np.float32)
    ml_c[0:32] = c32
    ml_s[0:32] = s32
    ml_c[64:96] = c32
    ml_s[64:96] = s32
    rope_ml = np.stack([ml_c, ml_s]).astype(np.float32)
    return rope_da, rope_ml


def _even_layout(inp):
    ev_w_in, ev_b_in = inp["ev_w_in"], inp["ev_b_in"]
    o_q1, o_q2, o_k1, o_k2, o_v, o_cq, o_ckv, o_kr, o_g = 0, 256, 512, 768, 1024, 1536, 1792, 1920, 1952
    cols = []
    for (a, b) in ((o_q1, o_q2), (o_k1, o_k2)):
        main = []
        sw = []
        for h in range(4):
            c1 = np.arange(a + h * 64, a + (h + 1) * 64)
            c2 = np.arange(b + h * 64, b + (h + 1) * 64)
            main += [c1, c2]
            sw += [_swap64(c1), _swap64(c2)]
        cols += main + sw
    kr = np.arange(o_kr, o_kr + 32)
    cols += [kr, np.concatenate([kr[16:], kr[:16]])]
    cols += [np.arange(o_v, o_v + 512), np.arange(o_cq, o_cq + 384), np.arange(o_g, o_g + 1024)]
    perm = np.concatenate(cols)
    assert perm.shape[0] == EV_COLS
    ev_w = np.ascontiguousarray(ev_w_in[:, :, perm])
    bp = ev_b_in[:, perm]
    bcol = np.zeros((2, 128, 18), np.float32)
    for g in range(16):
        bcol[:, :, g] = bp[:, g * 128:(g + 1) * 128]
    bcol[:, 0:32, 16] = bp[:, EKR:EKR + 32]
    bcol[:, 0:32, 17] = bp[:, EKR + 32:EKR + 64]
    brow = np.ascontiguousarray(bp[:, EV_:EV_ + 1920])[:, None, :]
    wuq = inp["mla_w_uq"]
    swc = []
    for h in range(8):
        base = h * 96
        c = np.arange(base, base + 96)
        r = c[64:96]
        swc.append(np.concatenate([c[0:64], r[16:], r[:16]]))
    swc = np.concatenate(swc)
    ev_wuq = np.ascontiguousarray(np.concatenate([wuq, wuq[:, :, swc]], axis=2))
    wukv = inp["mla_w_ukv"]
    nope = np.concatenate([np.arange(h * 128, h * 128 + 64) for h in range(8)])
    vv = np.concatenate([np.arange(h * 128 + 64, h * 128 + 128) for h in range(8)])
    ev_wukv = np.ascontiguousarray(wukv[:, :, np.concatenate([nope, vv])])
    return dict(
        ev_w=ev_w, ev_bcol=bcol, ev_brow=np.ascontiguousarray(brow),
        ev_lam=np.ascontiguousarray(inp["da_lambda"].reshape(2, 1, 256)),
        ev_subg=np.ascontiguousarray(inp["da_subln_g"].reshape(2, 1, 128)),
        ev_qg=np.ascontiguousarray(inp["mla_q_norm_g"].reshape(2, 2, 128).transpose(0, 2, 1)),
        ev_kvg=np.ascontiguousarray(inp["mla_kv_norm_g"].reshape(2, 128, 1)),
        ev_wuq=ev_wuq, ev_wukv=ev_wukv, ev_wo=np.ascontiguousarray(inp["ev_w_out"]),
    )


def _odd_layout(inp):
    w, b = inp["od_w_in"], inp["od_b_in"]
    g0 = 2048
    gates = np.concatenate([g0 + j * 4 + np.arange(4) for j in (0, 2, 1, 3)])
    perm = np.concatenate([np.arange(0, 1024), np.arange(2064, 2576), np.arange(2576, 3088), np.arange(1024, 1536),
                           np.arange(1536, 2048), gates, np.arange(3088, 3600), np.arange(3600, 4624)])
    assert perm.shape[0] == OD_COLS
    od_w = np.ascontiguousarray(w[:, :, perm])
    bp = b[:, perm]
    bcol = np.ascontiguousarray(bp[:, 0:2048].reshape(2, 16, 128).transpose(0, 2, 1))
    brow = np.ascontiguousarray(bp[:, 2048:])[:, None, :]
    convw = np.ascontiguousarray(inp["ml_conv_w"].reshape(2, 5, 8, 128).transpose(0, 3, 2, 1))
    convb = np.ascontiguousarray(inp["ml_conv_b"].reshape(2, 8, 128).transpose(0, 2, 1))
    fb = np.ascontiguousarray(inp["ml_f_bias"].reshape(2, 1, 8))
    ng = np.ascontiguousarray(inp["ml_norm_g"].reshape(2, 1, 512))
    rpb = inp["na_rpb"]
    nat = np.full((2, 8, 128, NA_NVAR, 128), NEG, np.float32)
    k = np.arange(128)
    krl, kc = k // 64, k % 64
    q = np.arange(128)
    qrl, qc = q // 64, q % 64
    cs = np.clip(qc - NA_COLS // 2, 0, GRID_W - NA_COLS)
    for (dk, o0, o1), vid in NA_VARIANTS.items():
        rel = (2 * dk + krl[:, None]) - qrl[None, :]
        off = np.where(qrl[None, :] == 0, o0, o1)
        valid_r = (rel >= off) & (rel <= off + NA_ROWS - 1)
        valid_c = (kc[:, None] >= cs[None, :]) & (kc[:, None] <= cs[None, :] + NA_COLS - 1)
        valid = valid_r & valid_c
        ridx = np.clip(rel + NA_ROWS - 1, 0, 2 * NA_ROWS - 2)
        cidx = np.clip(kc[:, None] - qc[None, :] + NA_COLS - 1, 0, 2 * NA_COLS - 2)
        tab = rpb[:, :, ridx, cidx]
        nat[:, :, :, vid, :] = np.where(valid[None, None], tab, np.float32(NEG))
    tri = np.stack([np.triu(np.ones((128, 128), np.float32)), np.tril(np.ones((128, 128), np.float32)),
                    np.ones((128, 128), np.float32)], 1)
    return dict(od_w=od_w, od_bcol=bcol, od_brow=np.ascontiguousarray(brow), od_convw=convw, od_convb=convb, od_fb=fb,
                od_ng=ng, od_nat=nat, od_wo=np.ascontiguousarray(inp["od_w_out"]), tri=np.ascontiguousarray(tri))


def make_in_maps(inp, batches):
    inp = {k: np.asarray(v, dtype=np.float32) for k, v in inp.items()}
    rope_da, rope_ml = _rope_tables()
    common = dict(
        ada_w=inp["ada_w"], ada_b=inp["ada_b"], ln_g=inp["ln_g"], ln_b=inp["ln_b"],
        ident=np.eye(128, dtype=np.float32),
        sel=np.concatenate([np.stack([np.ones(128), np.zeros(128)]), np.stack([np.zeros(128), np.ones(128)])], 1).astype(np.float32),
        rope_da=rope_da, rope_ml=rope_ml,
    )
    common.update(_even_layout(inp))
    common.update(_odd_layout(inp))
    maps = []
    for b in batches:
        m = dict(common)
        m["x_in"] = np.ascontiguousarray(np.concatenate([inp["x"][b], inp["ctx"][b]], 0))
        cc = np.stack([inp["c"][b], inp["c_ctx"]], -1)
        m["cc"] = np.ascontiguousarray(cc.reshape(8, 128, 2).transpose(1, 0, 2))
        par = len(maps) % 2
        ps = np.zeros((128, 2), np.float32)
        ps[:, par] = 1.0
        m["psel"] = ps
        maps.append(m)
    return maps


_PROG_CACHE = {}


def kernel(**inputs):
    batches = [c // 2 for c in range(8)]
    in_maps = make_in_maps(inputs, batches)
    prog = Prog()
    nc = prog.build()
    res = run_bass_kernel_spmd(nc, in_maps, core_ids=list(range(8)))
    out = np.stack([res.results[2 * b]["y"] for b in range(4)], 0)
    return out.astype(np.float32)
```

```python
import contextlib
import math
import numpy as np
import concourse.bass as bass
import concourse.mybir as mybir
from concourse.bass_utils import run_bass_kernel_spmd

F32 = mybir.dt.float32
BF16 = mybir.dt.bfloat16
AF = mybir.ActivationFunctionType
ALU = mybir.AluOpType
AX = mybir.AxisListType

D = 1024
LAT = 8192
CTX = 256
NTOK = LAT + CTX
NT = NTOK // 128
NT_LAT = LAT // 128
DEPTH = 4
GRID_W = 64
ALPHA = (2.0 * DEPTH) ** 0.25
LN_EPS = 1e-5
RMS_EPS = 1e-6
DA_SCALE = 64 ** -0.5
MLA_SCALE = 96 ** -0.5
NA_SCALE = 64 ** -0.5
EV_COLS = 4032
EQ, EK, EKR, EV_, EC, EG = 0, 1024, 2048, 2112, 2624, 3008
OD_COLS = 4624
OQK, OQN, OKN, OV, OO, OGT, OVN, OG = 0, 1024, 1536, 2048, 2560, 3072, 3088, 3600
NA_ROWS, NA_COLS, GRID_H = 8, 16, 128
NEG = -30000.0


def na_plan():
    variants = {}
    plan = []
    for j in range(NT_LAT):
        r0, r1 = 2 * j, 2 * j + 1
        rs0 = min(max(r0 - NA_ROWS // 2, 0), GRID_H - NA_ROWS)
        rs1 = min(max(r1 - NA_ROWS // 2, 0), GRID_H - NA_ROWS)
        lst = []
        for kt in range(rs0 // 2, (rs1 + NA_ROWS - 1) // 2 + 1):
            key = (kt - j, rs0 - r0, rs1 - r1)
            if key not in variants:
                variants[key] = len(variants)
            lst.append((kt, variants[key]))
        plan.append(lst)
    return plan, variants


NA_PLAN, NA_VARIANTS = na_plan()
NA_NVAR = len(NA_VARIANTS)


class Tok:
    __slots__ = ("sem", "key", "val", "eng")

    def __init__(self, sem, key, val, eng):
        self.sem, self.key, self.val, self.eng = sem, key, val, eng


class Buf:
    __slots__ = ("name", "last_w", "readers", "sem", "key", "cnt")

    def __init__(self, name):
        self.name = name
        self.last_w = None
        self.readers = {}
        self.sem = None
        self.key = None
        self.cnt = 0


class Eng:
    def __init__(self, name, h, sem):
        self.name, self.h, self.sem = name, h, sem
        self.key = "E_" + name
        self.cnt = 0
        self.waited = {}


class Sync:
    def __init__(self, nc, stack):
        self.nc = nc
        self.stack = stack
        self.E = {}
        for name, h in (("pe", nc.tensor), ("act", nc.scalar), ("dve", nc.vector),
                        ("pool", nc.gpsimd), ("sp", nc.sync)):
            sem = stack.enter_context(nc.semaphore("s_" + name))
            self.E[name] = Eng(name, h, sem)
        self.dma_bufs = []
        self.free_sems = []
        self.replica_groups = [[0, 1], [2, 3], [4, 5], [6, 7]]
        self.nsem = 0
        self.nwait = 0
        self.nins = 0

    def buf(self, name):
        return Buf(name)

    def bufs(self, name, n):
        return [Buf(f"{name}{i}") for i in range(n)]

    def _deps(self, eng, r, w):
        raw = []
        oth = []
        for b in r:
            if b.last_w is not None:
                raw.append(b.last_w)
        for b in w:
            if b.last_w is not None:
                oth.append(b.last_w)
            oth.extend(b.readers.values())
        toks = []
        for t in raw:
            if t.eng is eng and eng.name == "pe":
                continue
            toks.append(t)
        for t in oth:
            if t.eng is eng:
                continue
            toks.append(t)
        return toks

    def _wait(self, eng, toks):
        for t in toks:
            if eng.waited.get(t.key, 0) >= t.val:
                continue
            eng.h.wait_ge(t.sem, t.val)
            eng.waited[t.key] = t.val
            self.nwait += 1

    def op(self, en, fn, r=(), w=()):
        eng = self.E[en]
        self._wait(eng, self._deps(eng, r, w))
        ins = fn(eng.h)
        ins.then_inc(eng.sem, 1)
        eng.cnt += 1
        self.nins += 1
        tok = Tok(eng.sem, eng.key, eng.cnt, eng)
        for b in r:
            b.readers[tok.key] = tok
        for b in w:
            b.last_w = tok
            b.readers = {}
        return tok

    def dma(self, q, out, in_, r=(), w=(), sb=None):
        eng = self.E[q]
        self._wait(eng, self._deps(eng, r, w))
        if sb is None:
            sb = w[0] if w else r[0]
        if sb.sem is None:
            if self.free_sems:
                sb.sem, sb.key, sb.cnt = self.free_sems.pop()
            else:
                sb.sem = self.stack.enter_context(self.nc.semaphore(f"d{self.nsem}"))
                sb.key = f"D{self.nsem}"
                sb.cnt = 0
                self.nsem += 1
            self.dma_bufs.append(sb)
        ins = eng.h.dma_start(out=out, in_=in_)
        ins.then_inc(sb.sem, 16)
        sb.cnt += 16
        self.nins += 1
        tok = Tok(sb.sem, sb.key, sb.cnt, None)
        for b in r:
            b.readers[tok.key] = tok
        for b in w:
            b.last_w = tok
            b.readers = {}
        return tok

    def allgather_pairs(self, src_ts, dst_ts):
        self.barrier()
        eng = self.E["pool"]
        if getattr(self, "cc_sem", None) is None:
            self.cc_sem = self.stack.enter_context(self.nc.semaphore("cc_sem"))
            self.cc_cnt = 0
        for src_t, dst_t in zip(src_ts, dst_ts):
            ins = eng.h.collective_compute("AllGather", ALU.bypass, replica_groups=self.replica_groups,
                                           ins=[src_t.ap().opt()], outs=[dst_t.ap().opt()])
            ins.then_inc(self.cc_sem)
            self.cc_cnt += 1
            self.nins += 1
        tok = Tok(self.cc_sem, "CC", self.cc_cnt, None)
        for e in self.E.values():
            self._wait(e, [tok])

    def barrier(self, engines=("pe", "act", "dve", "pool", "sp")):
        toks = [Tok(e.sem, e.key, e.cnt, e) for e in self.E.values() if e.cnt > 0]
        toks += [Tok(b.sem, b.key, b.cnt, None) for b in self.dma_bufs if b.cnt > 0]
        for en in engines:
            eng = self.E[en]
            self._wait(eng, [t for t in toks if t.eng is not eng])
        for b in self.dma_bufs:
            self.free_sems.append((b.sem, b.key, b.cnt))
            b.sem = None
            b.last_w = None
            b.readers = {}
        self.dma_bufs = []


class Prog:
    def __init__(self, layers=(0, 1, 2, 3), final_out=True, qb_filter=None, tile_filter=None, n_cores=8):
        self.n_cores = n_cores
        self.layers = tuple(layers)
        self.qb_filter = qb_filter
        self.tile_filter = tile_filter
        self.nc = bass.Bass("TRN2", target_bir_lowering=False)
        self.final_out = final_out

    def din(self, name, shape, dt=F32):
        return self.nc.dram_tensor(name, list(shape), dt, kind="ExternalInput").ap()

    def dout(self, name, shape, dt=F32):
        return self.nc.dram_tensor(name, list(shape), dt, kind="ExternalOutput").ap()

    def dscr(self, name, shape, dt=F32):
        return self.nc.dram_tensor(name, list(shape), dt, kind="Internal").ap()

    def sb(self, st, name, shape, dt=F32):
        self._uid = getattr(self, "_uid", 0) + 1
        return st.enter_context(self.nc.sbuf_tensor(f"s{self._uid}_{name}", list(shape), dt))

    def ps(self, st, name, shape, dt=F32):
        self._uid = getattr(self, "_uid", 0) + 1
        return st.enter_context(self.nc.psum_tensor(f"p{self._uid}_{name}", list(shape), dt))

    def build(self):
        nc = self.nc
        with contextlib.ExitStack() as st:
            self.S = Sync(nc, st)
            self.S.replica_groups = [[2 * k, 2 * k + 1] for k in range(self.n_cores // 2)]
            self.declare()
            self.setup_consts(st)
            nl = len(self.layers)
            for li, l in enumerate(self.layers):
                src = self.x_in if li == 0 else self.xbuf[(li - 1) % 2]
                last = li == nl - 1
                dst = self.y_out if last else self.xbuf[li % 2]
                if l % 2 == 0:
                    self.even_layer(l, src, dst, last)
                else:
                    self.odd_layer(l, src, dst, last)
            self.S.barrier()
        return nc

    def declare(self):
        self.x_in = self.din("x_in", [NTOK, D])
        self.cc = self.din("cc", [128, 8, 2])
        self.ada_w = self.din("ada_w", [DEPTH, D, 3 * D])
        self.ada_b = self.din("ada_b", [DEPTH, 3 * D])
        self.ln_g = self.din("ln_g", [DEPTH, D])
        self.ln_b = self.din("ln_b", [DEPTH, D])
        self.ident_in = self.din("ident", [128, 128])
        self.sel_in = self.din("sel", [2, 256])
        self.ev_w = self.din("ev_w", [2, D, EV_COLS])
        self.ev_bcol = self.din("ev_bcol", [2, 128, 18])
        self.ev_brow = self.din("ev_brow", [2, 1, 1920])
        self.ev_lam = self.din("ev_lam", [2, 1, 256])
        self.ev_subg = self.din("ev_subg", [2, 1, 128])
        self.ev_qg = self.din("ev_qg", [2, 128, 2])
        self.ev_kvg = self.din("ev_kvg", [2, 128, 1])
        self.ev_wuq = self.din("ev_wuq", [2, 256, 1536])
        self.ev_wukv = self.din("ev_wukv", [2, 128, 1024])
        self.ev_wo = self.din("ev_wo", [2, D, D])
        self.rope_da = self.din("rope_da", [2, 128, NTOK])
        self.rope_ml = self.din("rope_ml", [2, 128, NTOK])
        self.od_w = self.din("od_w", [2, D, OD_COLS])
        self.od_bcol = self.din("od_bcol", [2, 128, 16])
        self.od_brow = self.din("od_brow", [2, 1, 2576])
        self.od_convw = self.din("od_convw", [2, 128, 8, 5])
        self.od_convb = self.din("od_convb", [2, 128, 8])
        self.od_fb = self.din("od_fb", [2, 1, 8])
        self.od_ng = self.din("od_ng", [2, 1, 512])
        self.od_nat = self.din("od_nat", [2, 8, 128, NA_NVAR, 128])
        self.od_wo = self.din("od_wo", [2, D, D])
        self.tri = self.din("tri", [128, 3, 128])
        self.qkpre = self.dscr("qkpre", [8, 128, NTOK])
        self.mq = self.dscr("mq", [4, 128, NTOK], BF16)
        self.mk = self.dscr("mk", [4, 128, NTOK], BF16)
        self.og = self.dscr("og", [NTOK, 512])
        self.gates = self.dscr("gates", [128, NT, 16])
        self.hfb = self.dscr("hfb", [2, NTOK, 512])
        n_last_tok = LAT if self.final_out else NTOK
        self.y_out = self.dout("y", [n_last_tok, D])
        self.xbuf = [self.dscr("xbuf0", [NTOK, D]), self.dscr("xbuf1", [NTOK, D])]
        self.qda = self.dscr("qda", [4, 128, NTOK], BF16)
        self.kda = self.dscr("kda", [4, 128, NTOK], BF16)
        self.vda = self.dscr("vda", [4, 128, NT, 129], BF16)
        self.qm = self.dscr("qm", [8, 96, NTOK], BF16)
        self.kmn = self.dscr("kmn", [8, 64, NTOK], BF16)
        self.krt = self.dscr("krt", [32, NTOK], BF16)
        self.vm = self.dscr("vm", [8, 128, NT, 65], BF16)
        self.gate = self.dscr("gate", [NTOK, D], BF16)
        self.ybuf = self.dscr("ybuf", [NTOK, D], BF16)
        self.psel = self.din("psel", [128, 2])
        self.yodd_rows = [2048, 2048, 2048, 2048, 256]
        self.yodd_t = [self.nc.dram_tensor(f"yodd{k}", [r, 512], BF16) for k, r in enumerate(self.yodd_rows)]
        self.yoddg_t = [self.nc.dram_tensor(f"yoddg{k}", [2 * r, 512], BF16) for k, r in enumerate(self.yodd_rows)]
        self.yown_t = [self.nc.dram_tensor(f"yown{k}", [1024, D], BF16) for k in range(4)]
        self.ygath_t = [self.nc.dram_tensor(f"ygath{k}", [2048, D], BF16) for k in range(4)]

    def setup_consts(self, st):
        S = self.S
        self.ident_f = self.sb(st, "ident_f", [128, 128])
        self.ident_b = self.sb(st, "ident_b", [128, 128], BF16)
        self.sel = self.sb(st, "sel", [2, 256])
        self.ccs = self.sb(st, "ccs", [128, 8, 2])
        self.zeros_b = self.sb(st, "zeros_b", [128, 128], BF16)
        self.ones_b = self.sb(st, "ones_b", [1, 128], BF16)
        self.zeros_w = self.sb(st, "zeros_w", [128, 512], BF16)
        self.B_const = S.buf("consts")
        b = self.B_const
        S.dma("sp", self.ident_f[:], self.ident_in[:, :], w=[b])
        S.dma("sp", self.sel[:], self.sel_in[:, :], w=[b])
        S.dma("sp", self.ccs[:], self.cc[:, :, :], w=[b])
        self.pselt = self.sb(st, "pselt", [128, 4])
        S.dma("sp", self.pselt[:, 0:2], self.psel[:, :], w=[b])
        S.op("dve", lambda e: e.tensor_scalar(out=self.pselt[:, 2:4], in0=self.pselt[:, 0:2], scalar1=1.0 / NA_SCALE, scalar2=None,
                                              op0=ALU.mult), r=[b], w=[b])
        S.op("dve", lambda e: e.tensor_copy(self.ident_b[:], self.ident_f[:]), r=[b], w=[b])
        S.op("dve", lambda e: e.memset(self.zeros_b[:], 0.0), w=[b])
        S.op("dve", lambda e: e.memset(self.ones_b[:], 1.0), w=[b])
        S.op("dve", lambda e: e.memset(self.zeros_w[:], 0.0), w=[b])
        S.op("act", lambda e: e.activation(out=self.ccs[:], in_=self.ccs[:], func=AF.Silu), r=[b], w=[b])
        self.sc1 = self.sb(st, "sc1", [128, 8, 2])
        self.sh = self.sb(st, "sh", [128, 8, 2])
        self.g_l = self.sb(st, "g_l", [128, D])
        self.g_c = self.sb(st, "g_c", [128, D])
        self.lng = self.sb(st, "lng", [128, D])
        self.lnb = self.sb(st, "lnb", [128, D])
        self.B_mod = S.buf("mod")

    def adaln(self, l):
        S, nc = self.S, self.nc
        S.barrier()
        with contextlib.ExitStack() as st:
            wt = [self.sb(st, f"adaw{i}", [128, 3 * D]) for i in range(2)]
            Bw = S.bufs("adaw", 2)
            mrow = self.sb(st, "mrow", [2, 3 * D])
            brow = self.sb(st, "adab", [2, 3 * D])
            Bm = S.buf("mrow")
            Bb = S.buf("adab")
            pm = [self.ps(st, f"pm{i}", [128, 512]) for i in range(8)]
            Bp = S.bufs("pm", 8)
            S.dma("sp", brow[:], self.ada_b[l:l + 1, :].to_broadcast([2, 3 * D]), w=[Bb])
            S.dma("sp", self.lng[:], self.ln_g[l:l + 1, :].to_broadcast([128, D]), w=[self.B_mod])
            S.dma("sp", self.lnb[:], self.ln_b[l:l + 1, :].to_broadcast([128, D]), w=[self.B_mod])
            for k in range(8):
                S.dma("sp", wt[k % 2][:], self.ada_w[l, k * 128:(k + 1) * 128, :], w=[Bw[k % 2]])
                for n in range(6):
                    S.op("pe", lambda e, k=k, n=n: e.matmul(
                        pm[n][0:2, :], self.ccs[:, k, :], wt[k % 2][:, n * 512:(n + 1) * 512],
                        start=(k == 0), stop=(k == 7)), r=[Bw[k % 2], self.B_const], w=[Bp[n]])
            for n in range(6):
                S.op("dve", lambda e, n=n: e.tensor_tensor(
                    out=mrow[:, n * 512:(n + 1) * 512], in0=pm[n][0:2, :], in1=brow[:, n * 512:(n + 1) * 512],
                    op=ALU.add), r=[Bp[n], Bb], w=[Bm])
            for j in range(8):
                S.op("pe", lambda e, j=j: e.transpose(
                    pm[6][:, 2 * j:2 * j + 2], mrow[0:2, j * 128:(j + 1) * 128], self.ident_f[0:2, 0:2]),
                    r=[Bm, self.B_const], w=[Bp[6]])
                S.op("pe", lambda e, j=j: e.transpose(
                    pm[7][:, 2 * j:2 * j + 2], mrow[0:2, D + j * 128:D + (j + 1) * 128], self.ident_f[0:2, 0:2]),
                    r=[Bm, self.B_const], w=[Bp[7]])
            S.op("dve", lambda e: e.tensor_copy(self.sh[:].rearrange("p a b -> p (a b)"), pm[6][:, 0:16]),
                 r=[Bp[6]], w=[self.B_mod])
            S.op("dve", lambda e: e.tensor_scalar(
                out=self.sc1[:].rearrange("p a b -> p (a b)"), in0=pm[7][:, 0:16], scalar1=1.0, scalar2=None,
                op0=ALU.add), r=[Bp[7]], w=[self.B_mod])
            for n in range(2):
                S.op("pe", lambda e, n=n: e.matmul(
                    pm[n][:, :], self.sel[:, 0:128], mrow[0:2, 2 * D + n * 512:2 * D + (n + 1) * 512],
                    start=True, stop=True), r=[Bm, self.B_const], w=[Bp[n]])
                S.op("pe", lambda e, n=n: e.matmul(
                    pm[2 + n][:, :], self.sel[:, 128:256], mrow[0:2, 2 * D + n * 512:2 * D + (n + 1) * 512],
                    start=True, stop=True), r=[Bm, self.B_const], w=[Bp[2 + n]])
                S.op("dve", lambda e, n=n: e.tensor_copy(self.g_l[:, n * 512:(n + 1) * 512], pm[n][:, :]),
                     r=[Bp[n]], w=[self.B_mod])
                S.op("dve", lambda e, n=n: e.tensor_copy(self.g_c[:, n * 512:(n + 1) * 512], pm[2 + n][:, :]),
                     r=[Bp[2 + n]], w=[self.B_mod])
            S.barrier()

    def load_cast(self, st_w, dst_fn, src_fn, nparts, ncols, pieces, Bdst, scale_fn=None, name="lc"):
        S = self.S
        with contextlib.ExitStack() as st:
            stg = [self.sb(st, f"{name}_stg{i}", [nparts, ncols]) for i in range(2)]
            Bs = S.bufs(name + "_stg", 2)
            for i, (dst, src, sc) in enumerate(pieces):
                j = i % 2
                n = src.shape[-1]
                S.dma("sp", stg[j][:, 0:n], src, w=[Bs[j]])
                en = "dve" if i % 2 == 0 else "pool"
                if sc is None:
                    S.op(en, lambda e, dst=dst, j=j, n=n: e.tensor_copy(dst, stg[j][:, 0:n]), r=[Bs[j]], w=[Bdst])
                else:
                    S.op(en, lambda e, dst=dst, j=j, n=n, sc=sc: e.tensor_scalar(
                        out=dst, in0=stg[j][:, 0:n], scalar1=sc, scalar2=None, op0=ALU.mult),
                        r=[Bs[j], Bdst], w=[Bdst])
            S.barrier()

    def even_layer(self, l, src, dst, last):
        i = l // 2
        self.adaln(l)
        self.even_project(i, src)
        self.da_attention(i, l)
        self.mla_attention(i)
        self.S.allgather_pairs(self.yown_t, self.ygath_t)
        self.out_stage(self.ev_wo[i], src, dst, last, ysplit=True)

    def even_project(self, i, src):
        S, nc = self.S, self.nc
        with contextlib.ExitStack() as st:
            wb = self.sb(st, "ev_wb", [128, 8, EV_COLS], BF16)
            wuq = self.sb(st, "ev_wuqb", [128, 2, 1536], BF16)
            wukv = self.sb(st, "ev_wukvb", [128, 1024], BF16)
            bcol = self.sb(st, "ev_bcol", [128, 18])
            brow_f = self.sb(st, "ev_brow_f", [1, 1920])
            brow = self.sb(st, "ev_brow", [1, 1920], BF16)
            qg = self.sb(st, "ev_qg", [128, 2])
            kvg = self.sb(st, "ev_kvg", [128, 1])
            Bw = S.buf("ev_w")
            S.dma("sp", bcol[:], self.ev_bcol[i, :, :], w=[Bw])
            S.dma("sp", brow_f[:], self.ev_brow[i, :, :], w=[Bw])
            S.dma("sp", qg[:], self.ev_qg[i, :, :], w=[Bw])
            S.dma("sp", kvg[:], self.ev_kvg[i, :, :], w=[Bw])
            S.op("dve", lambda e: e.tensor_copy(brow[:], brow_f[:]), r=[Bw], w=[Bw])
            pieces = []
            for k in range(8):
                for c in range(4):
                    pieces.append((wb[:, k, c * 1008:(c + 1) * 1008],
                                   self.ev_w[i, k * 128:(k + 1) * 128, c * 1008:(c + 1) * 1008], None))
            self.load_cast(st, None, None, 128, 1536, pieces, Bw, name="evw")
            pieces = [(wuq[:, rc, :], self.ev_wuq[i, rc * 128:(rc + 1) * 128, :], qg[:, rc:rc + 1]) for rc in range(2)]
            pieces.append((wukv[:, :], self.ev_wukv[i, :, :], kvg[:, 0:1]))
            self.load_cast(st, None, None, 128, 1536, pieces, Bw, name="evw2")

            NXS = 3
            xt = [self.sb(st, f"xt{j}", [128, D]) for j in range(NXS)]
            Bx = S.bufs("xt", NXS)
            hT = [self.sb(st, f"hT{j}", [128, 8, 512], BF16) for j in range(2)]
            Bh = S.bufs("hT", 2)
            cosd = [self.sb(st, f"cosd{j}", [128, 512]) for j in range(2)]
            sind = [self.sb(st, f"sind{j}", [128, 512]) for j in range(2)]
            cosm = [self.sb(st, f"cosm{j}", [128, 512]) for j in range(2)]
            sinm = [self.sb(st, f"sinm{j}", [128, 512]) for j in range(2)]
            Brt = S.bufs("ropet", 2)
            t1 = [self.sb(st, f"t1_{j}", [128, 512]) for j in range(2)]
            t2 = [self.sb(st, f"t2_{j}", [128, 512]) for j in range(2)]
            Bt1 = S.bufs("t1", 2)
            Bt2 = S.bufs("t2", 2)
            fo = [self.sb(st, f"fo{j}", [128, 512], BF16) for j in range(4)]
            Bfo = S.bufs("fo", 4)
            va = [self.sb(st, f"va{j}", [128, 4, 129], BF16) for j in range(2)]
            Bva = S.bufs("va", 2)
            vma = [self.sb(st, f"vma{j}", [128, 8, 65], BF16) for j in range(2)]
            Bvma = S.bufs("vma", 2)
            cq = [self.sb(st, f"cq{j}", [128, 384]) for j in range(2)]
            Bcq = S.bufs("cq", 2)
            cqn = [self.sb(st, f"cqn{j}", [128, 384], BF16) for j in range(2)]
            Bcqn = S.bufs("cqn", 2)
            stat = [self.sb(st, f"stat{j}", [128, 8]) for j in range(2)]
            Bstat = S.bufs("stat", 2)
            junk = self.sb(st, "junk", [128, 256])
            Bjunk = S.buf("junk")
            cT = [self.sb(st, f"cT{j}", [128, 3, 512], BF16) for j in range(2)]
            BcT = S.bufs("cT", 2)
            gt = [self.sb(st, f"gt{j}", [128, D], BF16) for j in range(2)]
            Bgt = S.bufs("gt", 2)
            pp = [self.ps(st, f"pp{j}", [128, 512]) for j in range(7)]
            ppb = self.ps(st, "ppb", [128, 1024], BF16)
            Bpp = S.bufs("pp", 7)
            Bppb = S.buf("ppb")
            for j in range(2):
                S.op("pool", lambda e, j=j: e.memset(va[j][:], 1.0), w=[Bva[j]])
                S.op("pool", lambda e, j=j: e.memset(vma[j][:], 1.0), w=[Bvma[j]])
            eps_t = self.sb(st, "eps_t", [128, 1])
            S.op("pool", lambda e: e.memset(eps_t[:], RMS_EPS), w=[Bw])

            self._pp_i = 0

            def next_pp():
                j = self._pp_i % 7
                self._pp_i += 1
                return pp[j], Bpp[j]

            nblk = (NTOK + 511) // 512
            self._xi = 0
            self._foi = 0
            self._evac = 0

            def stage_a(blk):
                t0 = blk * 512
                ntok = min(512, NTOK - t0)
                nsub = ntok // 128
                m = 0 if t0 < LAT else 1
                hb = blk % 2
                S.dma("sp", cosd[hb][:, :ntok], self.rope_da[0, :, t0:t0 + ntok], w=[Brt[hb]])
                S.dma("sp", sind[hb][:, :ntok], self.rope_da[1, :, t0:t0 + ntok], w=[Brt[hb]])
                S.dma("sp", cosm[hb][:, :ntok], self.rope_ml[0, :, t0:t0 + ntok], w=[Brt[hb]])
                S.dma("sp", sinm[hb][:, :ntok], self.rope_ml[1, :, t0:t0 + ntok], w=[Brt[hb]])
                for s in range(nsub):
                    xj = self._xi % NXS
                    self._xi += 1
                    S.dma("sp", xt[xj][:], src[t0 + s * 128:t0 + (s + 1) * 128, :], w=[Bx[xj]])
                    for half in range(2):
                        p, Bp = next_pp()
                        for q in range(4):
                            dc = half * 4 + q
                            S.op("pe", lambda e, p=p, q=q, dc=dc, xj=xj: e.transpose(
                                p[:, q * 128:(q + 1) * 128], xt[xj][:, dc * 128:(dc + 1) * 128], self.ident_f[:]),
                                r=[Bx[xj], self.B_const], w=[Bp])
                        for q in range(4):
                            dc = half * 4 + q
                            if self._evac % 2 == 0:
                                S.op("act", lambda e, p=p, q=q, dc=dc, s=s, m=m: e.activation(
                                    out=hT[hb][:, dc, s * 128:(s + 1) * 128], in_=p[:, q * 128:(q + 1) * 128],
                                    func=AF.Identity, scale=self.sc1[:, dc, m:m + 1], bias=self.sh[:, dc, m:m + 1]),
                                    r=[Bp, self.B_mod], w=[Bh[hb]])
                            else:
                                S.op("dve", lambda e, p=p, q=q, dc=dc, s=s, m=m: e.tensor_scalar(
                                    out=hT[hb][:, dc, s * 128:(s + 1) * 128], in0=p[:, q * 128:(q + 1) * 128],
                                    scalar1=self.sc1[:, dc, m:m + 1], scalar2=self.sh[:, dc, m:m + 1],
                                    op0=ALU.mult, op1=ALU.add), r=[Bp, self.B_mod], w=[Bh[hb]])
                            self._evac += 1

            def stage_b(blk):
                t0 = blk * 512
                ntok = min(512, NTOK - t0)
                nsub = ntok // 128
                m = 0 if t0 < LAT else 1
                hb = blk % 2
                foi = self._foi
                for grp, base, dstT, bofs in ((0, EQ, self.qda, 0), (1, EK, self.kda, 8)):
                    for h in range(4):
                        pa, Ba = next_pp()
                        pb, Bb = next_pp()
                        for k in range(8):
                            S.op("pe", lambda e, pa=pa, k=k, c0=base + h * 128: e.matmul(
                                pa[:, :ntok], wb[:, k, c0:c0 + 128], hT[hb][:, k, :ntok], start=(k == 0), stop=(k == 7)),
                                r=[Bw, Bh[hb]], w=[Ba])
                        for k in range(8):
                            S.op("pe", lambda e, pb=pb, k=k, c0=base + 512 + h * 128: e.matmul(
                                pb[:, :ntok], wb[:, k, c0:c0 + 128], hT[hb][:, k, :ntok], start=(k == 0), stop=(k == 7)),
                                r=[Bw, Bh[hb]], w=[Bb])
                        tj = (grp * 4 + h) % 2
                        S.op("dve", lambda e, pa=pa, tj=tj, c=bofs + h: e.scalar_tensor_tensor(
                            out=t1[tj][:, :ntok], in0=pa[:, :ntok], scalar=bcol[:, c:c + 1], in1=cosd[hb][:, :ntok],
                            op0=ALU.add, op1=ALU.mult), r=[Ba, Brt[hb], Bw], w=[Bt1[tj]])
                        S.op("dve", lambda e, pb=pb, tj=tj, c=bofs + 4 + h: e.scalar_tensor_tensor(
                            out=t2[tj][:, :ntok], in0=pb[:, :ntok], scalar=bcol[:, c:c + 1], in1=sind[hb][:, :ntok],
                            op0=ALU.add, op1=ALU.mult), r=[Bb, Brt[hb], Bw], w=[Bt2[tj]])
                        fj = foi % 4
                        foi += 1
                        S.op("pool", lambda e, tj=tj, fj=fj: e.tensor_tensor(
                            out=fo[fj][:, :ntok], in0=t1[tj][:, :ntok], in1=t2[tj][:, :ntok], op=ALU.add),
                            r=[Bt1[tj], Bt2[tj]], w=[Bfo[fj]])
                        S.dma("pool", dstT[h, :, t0:t0 + ntok], fo[fj][:, :ntok], r=[Bfo[fj]])
                pa, Ba = next_pp()
                pb, Bb = next_pp()
                for k in range(8):
                    S.op("pe", lambda e, pa=pa, k=k: e.matmul(
                        pa[0:32, :ntok], wb[:, k, EKR:EKR + 32], hT[hb][:, k, :ntok], start=(k == 0), stop=(k == 7)),
                        r=[Bw, Bh[hb]], w=[Ba])
                for k in range(8):
                    S.op("pe", lambda e, pb=pb, k=k: e.matmul(
                        pb[0:32, :ntok], wb[:, k, EKR + 32:EKR + 64], hT[hb][:, k, :ntok], start=(k == 0), stop=(k == 7)),
                        r=[Bw, Bh[hb]], w=[Bb])
                tj = 0
                S.op("dve", lambda e, pa=pa: e.scalar_tensor_tensor(
                    out=t1[tj][0:32, :ntok], in0=pa[0:32, :ntok], scalar=bcol[0:32, 16:17], in1=cosm[hb][0:32, :ntok],
                    op0=ALU.add, op1=ALU.mult), r=[Ba, Brt[hb], Bw], w=[Bt1[tj]])
                S.op("dve", lambda e, pb=pb: e.scalar_tensor_tensor(
                    out=t2[tj][0:32, :ntok], in0=pb[0:32, :ntok], scalar=bcol[0:32, 17:18], in1=sinm[hb][0:32, :ntok],
                    op0=ALU.add, op1=ALU.mult), r=[Bb, Brt[hb], Bw], w=[Bt2[tj]])
                fj = foi % 4
                foi += 1
                S.op("pool", lambda e, fj=fj: e.tensor_tensor(
                    out=fo[fj][0:32, :ntok], in0=t1[tj][0:32, :ntok], in1=t2[tj][0:32, :ntok], op=ALU.add),
                    r=[Bt1[tj], Bt2[tj]], w=[Bfo[fj]])
                S.dma("pool", self.krt[:, t0:t0 + ntok], fo[fj][0:32, :ntok], r=[Bfo[fj]])
                cb = blk % 2
                for s in range(nsub):
                    tt = (t0 // 128) + s
                    tok0 = t0 + s * 128
                    p, Bp = next_pp()
                    for k in range(8):
                        S.op("pe", lambda e, p=p, k=k, s=s: e.matmul(
                            p[:, :], hT[hb][:, k, s * 128:(s + 1) * 128], wb[:, k, EV_:EV_ + 512], start=(k == 0), stop=False),
                            r=[Bw, Bh[hb]], w=[Bp])
                    S.op("pe", lambda e, p=p: e.matmul(p[:, :], self.ones_b[0:1, :], brow[0:1, 0:512], start=False, stop=True),
                         r=[Bw, self.B_const], w=[Bp])
                    vj = tt % 2
                    S.op("act", lambda e, p=p, vj=vj: e.activation(
                        out=va[vj][:, :, 0:128], in_=p[:, :].rearrange("p (h c) -> p h c", h=4), func=AF.Copy),
                        r=[Bp], w=[Bva[vj]])
                    S.dma("pool", self.vda[:, :, tt, :].rearrange("h p c -> p h c"), va[vj][:], r=[Bva[vj]])
                    p, Bp = next_pp()
                    for k in range(8):
                        S.op("pe", lambda e, p=p, k=k, s=s: e.matmul(
                            p[:, 0:384], hT[hb][:, k, s * 128:(s + 1) * 128], wb[:, k, EC:EC + 384], start=(k == 0), stop=False),
                            r=[Bw, Bh[hb]], w=[Bp])
                    S.op("pe", lambda e, p=p: e.matmul(p[:, 0:384], self.ones_b[0:1, :], brow[0:1, 512:896], start=False, stop=True),
                         r=[Bw, self.B_const], w=[Bp])
                    cj = tt % 2
                    S.op("act", lambda e, p=p, cj=cj: e.activation(out=cq[cj][:], in_=p[:, 0:384], func=AF.Copy),
                         r=[Bp], w=[Bcq[cj]])
                    S.op("act", lambda e, cj=cj: e.activation(
                        out=junk[:, 0:256], in_=cq[cj][:, 0:256], func=AF.Square, accum_out=stat[cj][:, 0:1]),
                        r=[Bcq[cj]], w=[Bjunk, Bstat[cj]])
                    S.op("act", lambda e, cj=cj: e.activation(
                        out=junk[:, 0:128], in_=cq[cj][:, 256:384], func=AF.Square, accum_out=stat[cj][:, 1:2]),
                        r=[Bcq[cj]], w=[Bjunk, Bstat[cj]])
                    S.op("act", lambda e, cj=cj: e.activation(out=stat[cj][:, 2:3], in_=stat[cj][:, 0:1], func=AF.Sqrt,
                                                              scale=1.0 / 256, bias=eps_t[:, 0:1]), r=[Bstat[cj], Bw], w=[Bstat[cj]])
                    S.op("act", lambda e, cj=cj: e.activation(out=stat[cj][:, 3:4], in_=stat[cj][:, 1:2], func=AF.Sqrt,
                                                              scale=1.0 / 128, bias=eps_t[:, 0:1]), r=[Bstat[cj], Bw], w=[Bstat[cj]])
                    S.op("dve", lambda e, cj=cj: e.reciprocal(stat[cj][:, 4:6], stat[cj][:, 2:4]), r=[Bstat[cj]], w=[Bstat[cj]])
                    S.op("dve", lambda e, cj=cj: e.tensor_scalar(
                        out=cqn[cj][:, 0:256], in0=cq[cj][:, 0:256], scalar1=stat[cj][:, 4:5], scalar2=None, op0=ALU.mult),
                        r=[Bcq[cj], Bstat[cj]], w=[Bcqn[cj]])
                    S.op("pool", lambda e, cj=cj: e.tensor_scalar(
                        out=cqn[cj][:, 256:384], in0=cq[cj][:, 256:384], scalar1=stat[cj][:, 5:6], scalar2=None, op0=ALU.mult),
                        r=[Bcq[cj], Bstat[cj]], w=[Bcqn[cj]])
                    for rc in range(3):
                        S.op("pe", lambda e, rc=rc, cj=cj: e.transpose(
                            ppb[:, rc * 128:(rc + 1) * 128], cqn[cj][:, rc * 128:(rc + 1) * 128], self.ident_b[:]),
                            r=[Bcqn[cj], self.B_const], w=[Bppb])
                    S.op("dve", lambda e, s=s: e.tensor_copy(
                        cT[cb][:, :, s * 128:(s + 1) * 128], ppb[:, 0:384].rearrange("p (r t) -> p r t", r=3)),
                        r=[Bppb], w=[BcT[cb]])
                    gj = tt % 2
                    for n in range(2):
                        p, Bp = next_pp()
                        for k in range(8):
                            S.op("pe", lambda e, p=p, k=k, s=s, n=n: e.matmul(
                                p[:, :], hT[hb][:, k, s * 128:(s + 1) * 128], wb[:, k, EG + n * 512:EG + (n + 1) * 512],
                                start=(k == 0), stop=False), r=[Bw, Bh[hb]], w=[Bp])
                        S.op("pe", lambda e, p=p, n=n: e.matmul(
                            p[:, :], self.ones_b[0:1, :], brow[0:1, 896 + n * 512:896 + (n + 1) * 512], start=False, stop=True),
                            r=[Bw, self.B_const], w=[Bp])
                        S.op("act", lambda e, p=p, n=n, gj=gj: e.activation(
                            out=gt[gj][:, n * 512:(n + 1) * 512], in_=p[:, :], func=AF.Silu), r=[Bp], w=[Bgt[gj]])
                    S.dma("pool", self.gate[tok0:tok0 + 128, :], gt[gj][:], r=[Bgt[gj]])
                for h in range(8):
                    pa, Ba = next_pp()
                    pb, Bb = next_pp()
                    for rc in range(2):
                        S.op("pe", lambda e, pa=pa, rc=rc, h=h: e.matmul(
                            pa[0:96, :ntok], wuq[:, rc, h * 96:(h + 1) * 96], cT[cb][:, rc, :ntok], start=(rc == 0), stop=(rc == 1)),
                            r=[Bw, BcT[cb]], w=[Ba])
                    for rc in range(2):
                        S.op("pe", lambda e, pb=pb, rc=rc, h=h: e.matmul(
                            pb[0:96, :ntok], wuq[:, rc, 768 + h * 96:768 + (h + 1) * 96], cT[cb][:, rc, :ntok],
                            start=(rc == 0), stop=(rc == 1)), r=[Bw, BcT[cb]], w=[Bb])
                    tj = h % 2
                    fj = foi % 4
                    foi += 1
                    S.op("dve", lambda e, pa=pa, tj=tj: e.tensor_tensor(
                        out=t1[tj][64:96, :ntok], in0=pa[64:96, :ntok], in1=cosm[hb][64:96, :ntok], op=ALU.mult),
                        r=[Ba, Brt[hb]], w=[Bt1[tj]])
                    S.op("dve", lambda e, pb=pb, tj=tj: e.tensor_tensor(
                        out=t2[tj][64:96, :ntok], in0=pb[64:96, :ntok], in1=sinm[hb][64:96, :ntok], op=ALU.mult),
                        r=[Bb, Brt[hb]], w=[Bt2[tj]])
                    S.op("act", lambda e, pa=pa, fj=fj: e.activation(out=fo[fj][0:64, :ntok], in_=pa[0:64, :ntok], func=AF.Copy),
                         r=[Ba], w=[Bfo[fj]])
                    S.op("pool", lambda e, tj=tj, fj=fj: e.tensor_tensor(
                        out=fo[fj][64:96, :ntok], in0=t1[tj][64:96, :ntok], in1=t2[tj][64:96, :ntok], op=ALU.add),
                        r=[Bt1[tj], Bt2[tj]], w=[Bfo[fj]])
                    S.dma("pool", self.qm[h, :, t0:t0 + ntok], fo[fj][0:96, :ntok], r=[Bfo[fj]])
                for hp in range(4):
                    p, Bp = next_pp()
                    S.op("pe", lambda e, p=p, hp=hp: e.matmul(
                        p[:, :ntok], wukv[:, hp * 128:(hp + 1) * 128], cT[cb][:, 2, :ntok], start=True, stop=True),
                        r=[Bw, BcT[cb]], w=[Bp])
                    fj = foi % 4
                    foi += 1
                    S.op("act", lambda e, p=p, fj=fj: e.activation(out=fo[fj][:, :ntok], in_=p[:, :ntok], func=AF.Copy),
                         r=[Bp], w=[Bfo[fj]])
                    S.dma("pool", self.kmn[2 * hp, :, t0:t0 + ntok], fo[fj][0:64, :ntok], r=[Bfo[fj]])
                    S.dma("pool", self.kmn[2 * hp + 1, :, t0:t0 + ntok], fo[fj][64:128, :ntok], r=[Bfo[fj]])
                for s in range(nsub):
                    tt = (t0 // 128) + s
                    p, Bp = next_pp()
                    S.op("pe", lambda e, p=p, s=s: e.matmul(
                        p[:, :], cT[cb][:, 2, s * 128:(s + 1) * 128], wukv[:, 512:1024], start=True, stop=True),
                        r=[Bw, BcT[cb]], w=[Bp])
                    vj = tt % 2
                    S.op("act", lambda e, p=p, vj=vj: e.activation(
                        out=vma[vj][:, :, 0:64], in_=p[:, :].rearrange("p (h c) -> p h c", h=8), func=AF.Copy),
                        r=[Bp], w=[Bvma[vj]])
                    S.dma("pool", self.vm[:, :, tt, :].rearrange("h p c -> p h c"), vma[vj][:], r=[Bvma[vj]])
                self._foi = foi

            stage_a(0)
            for blk in range(nblk):
                if blk + 1 < nblk:
                    stage_a(blk + 1)
                stage_b(blk)
            S.barrier()

    def attn_core(self, st, name, KT, QT, VA, Bkv, kparts, nv, scale, qblocks, finalize, kslices=None):
        S = self.S
        nmap = len(kparts)
        nacc_per_bank = 512 // nv
        n_acc = nmap * 4
        n_acc_banks = (n_acc + nacc_per_bank - 1) // nacc_per_bank
        n_sc = 8 - n_acc_banks
        n_sc = min(n_sc, 4)
        NP = 4
        res = getattr(self, "_attn_res", None)
        if res is None or res[0] != name:
            accb = [self.ps(st, f"{name}_acc{j}", [128, 512]) for j in range(n_acc_banks)]
            Bacc = S.bufs(name + "_acc", n_acc_banks)
            scb = [self.ps(st, f"{name}_sc{j}", [128, 512]) for j in range(n_sc)]
            Bsc = S.bufs(name + "_sc", n_sc)
            pt = [self.sb(st, f"{name}_pt{j}", [128, 512], BF16) for j in range(NP)]
            Bpt = S.bufs(name + "_pt", NP)
            self._attn_res = (name, accb, Bacc, scb, Bsc, pt, Bpt)
        _, accb, Bacc, scb, Bsc, pt, Bpt = self._attn_res

        def acc_ap(mi, sub):
            idx = mi * 4 + sub
            b = idx // nacc_per_bank
            o = (idx % nacc_per_bank) * nv
            return accb[b][:, o:o + nv], Bacc[b]

        sci = 0
        pti = 0
        for qbi, (q0, nq, ktiles) in enumerate(qblocks):
            nsub = nq // 128
            for b in range(n_acc_banks):
                S.op("pe", lambda e, b=b: e.matmul(accb[b][:, :], self.zeros_b[:, :], self.zeros_w[:, :], start=True, stop=False,
                                                   skip_group_check=True), r=[self.B_const], w=[Bacc[b]])
            units = [(kt, mi) for kt in ktiles for mi in range(nmap)]
            pend = []

            def emit_score(u):
                nonlocal sci
                kt, mi = u
                QTm, lo, hi = kparts[mi]
                j = sci % n_sc
                sci += 1
                S.op("pe", lambda e, j=j, lo=lo, hi=hi, kt=kt, QTm=QTm: e.matmul(
                    scb[j][:, :nq], KT[lo:hi, kt * 128:(kt + 1) * 128], QTm[lo:hi, q0:q0 + nq], start=True, stop=True),
                    r=[Bkv], w=[Bsc[j]])
                return j

            def emit_exp_pv(u, j, lastflag):
                nonlocal pti
                kt, mi = u
                pj = pti % NP
                pti += 1
                S.op("act", lambda e, j=j, pj=pj: e.activation(out=pt[pj][:, :nq], in_=scb[j][:, :nq], func=AF.Exp, scale=scale),
                     r=[Bsc[j]], w=[Bpt[pj]])
                for sub in range(nsub):
                    ap, Ba = acc_ap(mi, sub)
                    S.op("pe", lambda e, ap=ap, pj=pj, sub=sub, kt=kt: e.matmul(
                        ap, pt[pj][:, sub * 128:(sub + 1) * 128], VA[:, kt, :], start=False, stop=lastflag,
                        skip_group_check=True), r=[Bpt[pj], Bkv], w=[Ba])

            LOOK = min(n_sc - 1, 2)
            q = []
            for ui, u in enumerate(units):
                q.append((u, emit_score(u)))
                if len(q) > LOOK:
                    u0, j0 = q.pop(0)
                    emit_exp_pv(u0, j0, False)
            while q:
                u0, j0 = q.pop(0)
                emit_exp_pv(u0, j0, u0[0] == ktiles[-1])
            for sub in range(nsub):
                accs = [acc_ap(mi, sub) for mi in range(nmap)]
                finalize(qbi, q0, sub, accs)

    def qblocks_all(self):
        qb = []
        lat_k = list(range(NT))
        for b in range(LAT // 512):
            qb.append((b * 512, 512, lat_k))
        qb.append((LAT, CTX, [NT_LAT, NT_LAT + 1]))
        if self.qb_filter is not None:
            qb = [q for i, q in enumerate(qb) if i in self.qb_filter]
        return qb

    def qblocks_own(self):
        qb = []
        lat_k = list(range(NT))
        for b in range(LAT // 2 // 512):
            qb.append((b * 512, 512, lat_k))
        qb.append((LAT // 2, CTX, [NT_LAT, NT_LAT + 1]))
        if self.qb_filter is not None:
            qb = [q for i, q in enumerate(qb) if i in self.qb_filter]
        return qb

    def own_q(self, QO, QS, lo, hi, tmp, Bqs, Bqo, Btmp):
        S = self.S
        H = LAT // 2
        S.op("dve", lambda e: e.tensor_scalar(out=tmp[lo:hi, :], in0=QS[lo:hi, H:LAT], scalar1=self.pselt[lo:hi, 1:2], scalar2=None,
                                              op0=ALU.mult), r=[Bqs, self.B_const], w=[Btmp])
        S.op("dve", lambda e: e.scalar_tensor_tensor(out=QO[lo:hi, 0:H], in0=QS[lo:hi, 0:H], scalar=self.pselt[lo:hi, 0:1],
                                                     in1=tmp[lo:hi, :], op0=ALU.mult, op1=ALU.add),
             r=[Bqs, Btmp, self.B_const], w=[Bqo])
        S.op("act", lambda e: e.activation(out=QO[lo:hi, H:H + CTX], in_=QS[lo:hi, LAT:NTOK], func=AF.Copy), r=[Bqs], w=[Bqo])

    def blend(self, out, a, b, Ba, Bb, Bout, lo=0, hi=128, col=0):
        S = self.S
        s0 = self.pselt[lo:hi, col:col + 1]
        s1 = self.pselt[lo:hi, col + 1:col + 2]
        S.op("dve", lambda e: e.tensor_scalar(out=b, in0=b, scalar1=s1, scalar2=None, op0=ALU.mult), r=[Bb, self.B_const], w=[Bb])
        S.op("dve", lambda e: e.scalar_tensor_tensor(out=out, in0=a, scalar=s0, in1=b, op0=ALU.mult, op1=ALU.add),
             r=[Ba, Bb, self.B_const], w=[Bout])

    def yodd_loc(self, t):
        if t < NT_LAT:
            return t // 16, (t % 16) * 128
        return 4, (t - NT_LAT) * 128

    def y_dst(self, tok0, c0, c1):
        H = LAT // 2
        if tok0 < H:
            k, r = tok0 // 1024, tok0 % 1024
            return self.yown_t[k].ap()[r:r + 128, c0:c1]
        return self.ybuf[LAT + tok0 - H:LAT + tok0 - H + 128, c0:c1]

    def da_attention(self, i, l):
        S = self.S
        lam_init = 0.8 - 0.6 * math.exp(-0.3 * l)
        with contextlib.ExitStack() as st:
            KT = self.sb(st, "da_KT", [128, NTOK], BF16)
            NQO = LAT // 2 + CTX
            KTs = [self.sb(st, f"da_KT{j}", [128, NTOK], BF16) for j in range(2)]
            VAs = [self.sb(st, f"da_VA{j}", [128, NT, 129], BF16) for j in range(2)]
            QT1s = [self.sb(st, f"da_QT1{j}", [128, NQO], BF16) for j in range(2)]
            QT2s = [self.sb(st, f"da_QT2{j}", [128, NQO], BF16) for j in range(2)]
            QS = self.sb(st, "da_QS", [128, NTOK], BF16)
            qtmp = self.sb(st, "da_qtmp", [128, LAT // 2], BF16)
            Bkvs = S.bufs("da_kv", 2)
            Bqs = S.buf("da_qs")
            Bqt = S.buf("da_qtmp")
            for j in range(2):
                S.op("pool", lambda e, j=j: e.memset(QT1s[j][64:128, :], 0.0), w=[Bkvs[j]])
                S.op("pool", lambda e, j=j: e.memset(QT2s[j][0:64, :], 0.0), w=[Bkvs[j]])
            lamt = self.sb(st, "lamt", [128, 256])
            lamw = self.sb(st, "lamw", [128, 8])
            subg = self.sb(st, "subg", [128, 128])
            Bl = S.buf("lam")
            eps_t = self.sb(st, "da_eps", [128, 1])
            S.op("pool", lambda e: e.memset(eps_t[:], RMS_EPS), w=[Bl])
            S.dma("sp", lamt[:], self.ev_lam[i, :, :].to_broadcast([128, 256]), w=[Bl])
            S.dma("sp", subg[:], self.ev_subg[i, :, :].to_broadcast([128, 128]), w=[Bl])
            junk = self.sb(st, "da_junk", [128, 128])
            Bj = S.buf("da_junk")
            S.op("dve", lambda e: e.tensor_tensor(out=junk[:, 0:64], in0=lamt[:, 0:64], in1=lamt[:, 64:128], op=ALU.mult),
                 r=[Bl], w=[Bj])
            S.op("dve", lambda e: e.reduce_sum(out=lamw[:, 0:1], in_=junk[:, 0:64], axis=AX.X), r=[Bj], w=[Bl])
            S.op("dve", lambda e: e.tensor_tensor(out=junk[:, 64:128], in0=lamt[:, 128:192], in1=lamt[:, 192:256], op=ALU.mult),
                 r=[Bl], w=[Bj])
            S.op("dve", lambda e: e.reduce_sum(out=lamw[:, 1:2], in_=junk[:, 64:128], axis=AX.X), r=[Bj], w=[Bl])
            S.op("act", lambda e: e.activation(out=lamw[:, 2:4], in_=lamw[:, 0:2], func=AF.Exp), r=[Bl], w=[Bl])
            S.op("dve", lambda e: e.tensor_tensor(out=lamw[:, 4:5], in0=lamw[:, 3:4], in1=lamw[:, 2:3], op=ALU.subtract),
                 r=[Bl], w=[Bl])
            S.op("dve", lambda e: e.tensor_scalar(out=lamw[:, 5:6], in0=lamw[:, 4:5], scalar1=-lam_init, scalar2=None, op0=ALU.add),
                 r=[Bl], w=[Bl])
            NF = 3
            rr = [self.sb(st, f"da_rr{j}", [128, 8]) for j in range(NF)]
            ta = [self.sb(st, f"da_ta{j}", [128, 128]) for j in range(NF)]
            td = [self.sb(st, f"da_td{j}", [128, 128]) for j in range(NF)]
            to = [self.sb(st, f"da_to{j}", [128, 128], BF16) for j in range(NF)]
            Bf = S.bufs("da_fin", NF)
            Bto = S.bufs("da_to", NF)
            self._fi = 0
            def da_load(h):
                j = h % 2
                S.dma("sp", KTs[j][:], self.kda[h, :, :], w=[Bkvs[j]])
                S.dma("act", QS[:], self.qda[h, :, :], w=[Bqs])
                S.dma("sp", VAs[j][:], self.vda[h, :, :, :], w=[Bkvs[j]])
                self.own_q(QT1s[j], QS, 0, 64, qtmp, Bqs, Bkvs[j], Bqt)
                self.own_q(QT2s[j], QS, 64, 128, qtmp, Bqs, Bkvs[j], Bqt)

            da_load(0)
            for h in range(4):
                if h + 1 < 4:
                    da_load(h + 1)
                KT, VA, QT1, QT2, Bkv = KTs[h % 2], VAs[h % 2], QT1s[h % 2], QT2s[h % 2], Bkvs[h % 2]

                def fin(qbi, q0, sub, accs, h=h):
                    j = self._fi % NF
                    self._fi += 1
                    (a1, B1), (a2, B2) = accs
                    S.op("dve", lambda e: e.reciprocal(rr[j][:, 0:1], a1[:, 128:129]), r=[B1], w=[Bf[j]])
                    S.op("dve", lambda e: e.reciprocal(rr[j][:, 1:2], a2[:, 128:129]), r=[B2], w=[Bf[j]])
                    S.op("dve", lambda e: e.tensor_tensor(out=rr[j][:, 2:3], in0=rr[j][:, 1:2], in1=lamw[:, 5:6], op=ALU.mult),
                         r=[Bf[j], Bl], w=[Bf[j]])
                    S.op("dve", lambda e: e.tensor_scalar(out=ta[j][:], in0=a1[:, 0:128], scalar1=rr[j][:, 0:1], scalar2=None,
                                                          op0=ALU.mult), r=[B1, Bf[j]], w=[Bf[j]])
                    S.op("dve", lambda e: e.scalar_tensor_tensor(out=td[j][:], in0=a2[:, 0:128], scalar=rr[j][:, 2:3], in1=ta[j][:],
                                                                 op0=ALU.mult, op1=ALU.add), r=[B2, Bf[j]], w=[Bf[j]])
                    S.op("act", lambda e: e.activation(out=ta[j][:], in_=td[j][:], func=AF.Square, accum_out=rr[j][:, 3:4]),
                         r=[Bf[j]], w=[Bf[j]])
                    S.op("act", lambda e: e.activation(out=rr[j][:, 4:5], in_=rr[j][:, 3:4], func=AF.Sqrt, scale=1.0 / 128,
                                                       bias=eps_t[:, 0:1]), r=[Bf[j], Bl], w=[Bf[j]])
                    S.op("dve", lambda e: e.reciprocal(rr[j][:, 5:6], rr[j][:, 4:5]), r=[Bf[j]], w=[Bf[j]])
                    S.op("pool", lambda e: e.tensor_scalar(out=td[j][:], in0=td[j][:], scalar1=rr[j][:, 5:6], scalar2=(1.0 - lam_init),
                                                           op0=ALU.mult, op1=ALU.mult), r=[Bf[j]], w=[Bf[j]])
                    S.op("pool", lambda e: e.tensor_tensor(out=to[j][:], in0=td[j][:], in1=subg[:], op=ALU.mult),
                         r=[Bf[j], Bl], w=[Bto[j]])
                    tok0 = q0 + sub * 128
                    S.dma("pool", self.y_dst(tok0, h * 128, (h + 1) * 128), to[j][:], r=[Bto[j]])

                self.attn_core(st, "da", KT, None, VA, Bkv, [(QT1, 0, 128), (QT2, 0, 128)], 129, DA_SCALE,
                               self.qblocks_own(), fin)
            S.barrier()
            self._attn_res = None

    def mla_attention(self, i):
        S = self.S
        with contextlib.ExitStack() as st:
            KTs = [self.sb(st, f"ml_KT{j}", [128, NTOK], BF16) for j in range(2)]
            QTs = [self.sb(st, f"ml_QT{j}", [128, LAT // 2 + CTX], BF16) for j in range(2)]
            VAs = [self.sb(st, f"ml_VA{j}", [128, NT, 65], BF16) for j in range(2)]
            QS = self.sb(st, "ml_QS", [128, NTOK], BF16)
            qtmp = self.sb(st, "ml_qtmp", [128, LAT // 2], BF16)
            Bkvs = S.bufs("ml_kv", 2)
            Bqs = S.buf("ml_qs")
            Bqt = S.buf("ml_qtmp")
            NF = 3
            rr = [self.sb(st, f"ml_rr{j}", [128, 2]) for j in range(NF)]
            to = [self.sb(st, f"ml_to{j}", [128, 64], BF16) for j in range(NF)]
            Bto = S.bufs("ml_to", NF)
            self._fi = 0
            def ml_load(h):
                j = h % 2
                S.dma("sp", KTs[j][0:64, :], self.kmn[h, :, :], w=[Bkvs[j]])
                S.dma("sp", KTs[j][64:96, :], self.krt[:, :], w=[Bkvs[j]])
                S.dma("act", QS[0:96, :], self.qm[h, :, :], w=[Bqs])
                S.dma("sp", VAs[j][:], self.vm[h, :, :, :], w=[Bkvs[j]])
                self.own_q(QTs[j], QS, 0, 96, qtmp, Bqs, Bkvs[j], Bqt)

            ml_load(0)
            for h in range(8):
                if h + 1 < 8:
                    ml_load(h + 1)
                KT, VA, QT, Bkv = KTs[h % 2], VAs[h % 2], QTs[h % 2], Bkvs[h % 2]

                def fin(qbi, q0, sub, accs, h=h):
                    j = self._fi % NF
                    self._fi += 1
                    (a1, B1), = accs
                    S.op("dve", lambda e: e.reciprocal(rr[j][:, 0:1], a1[:, 64:65]), r=[B1], w=[Bto[j]])
                    S.op("dve", lambda e: e.tensor_scalar(out=to[j][:], in0=a1[:, 0:64], scalar1=rr[j][:, 0:1], scalar2=None,
                                                          op0=ALU.mult), r=[B1, Bto[j]], w=[Bto[j]])
                    tok0 = q0 + sub * 128
                    S.dma("pool", self.y_dst(tok0, 512 + h * 64, 512 + (h + 1) * 64), to[j][:], r=[Bto[j]])

                self.attn_core(st, "ml", KT, None, VA, Bkv, [(QT, 0, 96)], 65, MLA_SCALE, self.qblocks_own(), fin)
            S.barrier()
            self._attn_res = None

    def out_stage(self, wo_dram, src, dst, last, ysplit=False):
        S = self.S
        with contextlib.ExitStack() as st:
            wo = self.sb(st, "wo", [128, 8, D], BF16)
            Bw = S.buf("wo")
            pieces = [(wo[:, k, :], wo_dram[k * 128:(k + 1) * 128, :], None) for k in range(8)]
            self.load_cast(st, None, None, 128, D, pieces, Bw, name="wo")
            NB = 3
            yt = [self.sb(st, f"o_yt{j}", [128, D], BF16) for j in range(NB)]
            gt = [self.sb(st, f"o_gt{j}", [128, D], BF16) for j in range(NB)]
            xt = [self.sb(st, f"o_xt{j}", [128, D]) for j in range(NB)]
            yg = [self.sb(st, f"o_yg{j}", [128, D], BF16) for j in range(NB)]
            ygT = [self.sb(st, f"o_ygT{j}", [128, D], BF16) for j in range(NB)]
            rt = [self.sb(st, f"o_rt{j}", [128, D]) for j in range(NB)]
            ot = [self.sb(st, f"o_ot{j}", [128, D]) for j in range(NB)]
            stt = [self.sb(st, f"o_st{j}", [128, 16]) for j in range(NB)]
            By, Bg, Bx, Byg, BygT, Br, Bo, Bs = (S.bufs(n, NB) for n in ("o_yt", "o_gt", "o_xt", "o_yg", "o_ygT", "o_rt", "o_ot", "o_st"))
            ptr = [self.ps(st, f"o_ptr{j}", [128, D], BF16) for j in range(2)]
            Bptr = S.bufs("o_ptr", 2)
            pout = [self.ps(st, f"o_po{j}", [128, 512]) for j in range(4)]
            Bpo = S.bufs("o_po", 4)
            eps_t = self.sb(st, "o_eps", [128, 1])
            S.op("pool", lambda e: e.memset(eps_t[:], LN_EPS), w=[Bw])
            ntiles = NT_LAT if (last and self.final_out) else NT
            tiles = [t for t in range(ntiles) if self.tile_filter is None or t in self.tile_filter]

            def stage_a(t):
                j = t % NB
                tok0 = t * 128
                isctx = t >= NT_LAT
                G = self.g_c if isctx else self.g_l
                if ysplit == "odd":
                    k, r0 = self.yodd_loc(t)
                    rows = self.yodd_rows[k]
                    g = self.yoddg_t[k].ap()
                    S.dma("sp", yt[j][:, 0:256], g[r0:r0 + 128, 0:256], w=[By[j]])
                    S.dma("sp", yt[j][:, 256:512], g[rows + r0:rows + r0 + 128, 0:256], w=[By[j]])
                    S.dma("sp", yt[j][:, 512:768], g[r0:r0 + 128, 256:512], w=[By[j]])
                    S.dma("sp", yt[j][:, 768:1024], g[rows + r0:rows + r0 + 128, 256:512], w=[By[j]])
                else:
                    if ysplit and not isctx:
                        half, u = t // 32, t % 32
                        r0 = half * 1024 + (u % 8) * 128
                        ysrc = self.ygath_t[u // 8].ap()[r0:r0 + 128, :]
                    else:
                        ysrc = self.ybuf[tok0:tok0 + 128, :]
                    S.dma("sp", yt[j][:], ysrc, w=[By[j]])
                S.dma("act", gt[j][:], self.gate[tok0:tok0 + 128, :], w=[Bg[j]])
                S.dma("sp", xt[j][:], src[tok0:tok0 + 128, :], w=[Bx[j]])
                S.op("pool", lambda e, j=j: e.tensor_tensor(out=yg[j][:], in0=yt[j][:], in1=gt[j][:], op=ALU.mult),
                     r=[By[j], Bg[j]], w=[Byg[j]])
                pj = t % 2
                for ec in range(8):
                    S.op("pe", lambda e, ec=ec, j=j, pj=pj: e.transpose(
                        ptr[pj][:, ec * 128:(ec + 1) * 128], yg[j][:, ec * 128:(ec + 1) * 128], self.ident_b[:]),
                        r=[Byg[j], self.B_const], w=[Bptr[pj]])
                S.op("act", lambda e, j=j, pj=pj: e.activation(out=ygT[j][:], in_=ptr[pj][:], func=AF.Copy),
                     r=[Bptr[pj]], w=[BygT[j]])
                for n in range(2):
                    pn = (t % 2) * 2 + n
                    for ec in range(8):
                        S.op("pe", lambda e, ec=ec, n=n, pn=pn, j=j: e.matmul(
                            pout[pn][:, :], ygT[j][:, ec * 128:(ec + 1) * 128], wo[:, ec, n * 512:(n + 1) * 512],
                            start=(ec == 0), stop=(ec == 7)), r=[BygT[j], Bw], w=[Bpo[pn]])

            def stage_b(t):
                j = t % NB
                tok0 = t * 128
                isctx = t >= NT_LAT
                G = self.g_c if isctx else self.g_l
                for n in range(2):
                    pn = (t % 2) * 2 + n
                    S.op("dve", lambda e, n=n, pn=pn, j=j, G=G: e.tensor_tensor(
                        out=rt[j][:, n * 512:(n + 1) * 512], in0=pout[pn][:, :], in1=G[:, n * 512:(n + 1) * 512], op=ALU.mult),
                        r=[Bpo[pn], self.B_mod], w=[Br[j]])
                S.op("dve", lambda e, j=j: e.scalar_tensor_tensor(
                    out=rt[j][:], in0=xt[j][:], scalar=ALPHA, in1=rt[j][:], op0=ALU.mult, op1=ALU.add),
                    r=[Bx[j], Br[j]], w=[Br[j]])
                for n in range(2):
                    S.op("dve", lambda e, n=n, j=j: e.bn_stats(stt[j][:, n * 6:(n + 1) * 6], rt[j][:, n * 512:(n + 1) * 512]),
                         r=[Br[j]], w=[Bs[j]])
                S.op("dve", lambda e, j=j: e.bn_aggr(stt[j][:, 12:14], stt[j][:, 0:12]), r=[Bs[j]], w=[Bs[j]])
                S.op("act", lambda e, j=j: e.activation(out=stt[j][:, 14:15], in_=stt[j][:, 13:14], func=AF.Sqrt, scale=1.0,
                                                        bias=eps_t[:, 0:1]), r=[Bs[j], Bw], w=[Bs[j]])
                S.op("dve", lambda e, j=j: e.reciprocal(stt[j][:, 15:16], stt[j][:, 14:15]), r=[Bs[j]], w=[Bs[j]])
                S.op("dve", lambda e, j=j: e.tensor_scalar(
                    out=ot[j][:], in0=rt[j][:], scalar1=stt[j][:, 12:13], scalar2=stt[j][:, 15:16], op0=ALU.subtract, op1=ALU.mult),
                    r=[Br[j], Bs[j]], w=[Bo[j]])
                S.op("pool", lambda e, j=j: e.tensor_tensor(out=ot[j][:], in0=ot[j][:], in1=self.lng[:], op=ALU.mult),
                     r=[Bo[j], self.B_mod], w=[Bo[j]])
                S.op("pool", lambda e, j=j: e.tensor_tensor(out=ot[j][:], in0=ot[j][:], in1=self.lnb[:], op=ALU.add),
                     r=[Bo[j], self.B_mod], w=[Bo[j]])
                S.dma("pool", dst[tok0:tok0 + 128, :], ot[j][:], r=[Bo[j]])

            for idx, t in enumerate(tiles):
                if idx == 0:
                    stage_a(t)
                if idx + 1 < len(tiles):
                    stage_a(tiles[idx + 1])
                stage_b(t)
            S.barrier()

    def odd_layer(self, l, src, dst, last):
        i = l // 2
        self.adaln(l)
        self.odd_project(i, src)
        self.odd_conv(i)
        self.mlstm(i)
        self.mlstm_post(i)
        self.na_attention(i, last)
        self.S.allgather_pairs(self.yodd_t, self.yoddg_t)
        self.out_stage(self.od_wo[i], src, dst, last, ysplit="odd")

    def odd_project(self, i, src):
        S = self.S
        with contextlib.ExitStack() as st:
            wb = self.sb(st, "od_wb", [128, 8, OD_COLS], BF16)
            bcol = self.sb(st, "od_bcol", [128, 16])
            brow_f = self.sb(st, "od_brow_f", [1, 2576])
            brow = self.sb(st, "od_brow", [1, 2576], BF16)
            fb = self.sb(st, "od_fb", [128, 8])
            Bw = S.buf("od_w")
            S.dma("sp", bcol[:], self.od_bcol[i, :, :], w=[Bw])
            S.dma("sp", brow_f[:], self.od_brow[i, :, :], w=[Bw])
            S.dma("sp", fb[:], self.od_fb[i, :, :].to_broadcast([128, 8]), w=[Bw])
            S.op("dve", lambda e: e.tensor_copy(brow[:], brow_f[:]), r=[Bw], w=[Bw])
            pieces = []
            for k in range(8):
                for c in range(4):
                    pieces.append((wb[:, k, c * 1156:(c + 1) * 1156],
                                   self.od_w[i, k * 128:(k + 1) * 128, c * 1156:(c + 1) * 1156], None))
            self.load_cast(st, None, None, 128, 1156, pieces, Bw, name="odw")
            NXS = 3
            xt = [self.sb(st, f"xt{j}", [128, D]) for j in range(NXS)]
            Bx = S.bufs("xt", NXS)
            hT = [self.sb(st, f"hT{j}", [128, 8, 512], BF16) for j in range(2)]
            Bh = S.bufs("hT", 2)
            ff = [self.sb(st, f"ff{j}", [128, 512]) for j in range(3)]
            Bff = S.bufs("ff", 3)
            fo = [self.sb(st, f"fo{j}", [128, 512], BF16) for j in range(3)]
            Bfo = S.bufs("fo", 3)
            va = [self.sb(st, f"va{j}", [128, 4, 129], BF16) for j in range(2)]
            Bva = S.bufs("va", 2)
            vna = [self.sb(st, f"vna{j}", [128, 8, 65], BF16) for j in range(2)]
            Bvna = S.bufs("vna", 2)
            ot = [self.sb(st, f"ot{j}", [128, 512]) for j in range(2)]
            Bot = S.bufs("ot", 2)
            gt = [self.sb(st, f"gt{j}", [128, D], BF16) for j in range(2)]
            Bgt = S.bufs("gt", 2)
            gs = [self.sb(st, f"gs{j}", [128, 4, 16]) for j in range(2)]
            gtmp = [self.sb(st, f"gtmp{j}", [128, 4, 8]) for j in range(2)]
            Bgs = S.bufs("gs", 2)
            pp = [self.ps(st, f"pp{j}", [128, 512]) for j in range(7)]
            pg = self.ps(st, "pg", [128, 512])
            Bpp = S.bufs("pp", 7)
            Bpg = S.buf("pg")
            for j in range(2):
                S.op("pool", lambda e, j=j: e.memset(va[j][:], 1.0), w=[Bva[j]])
                S.op("pool", lambda e, j=j: e.memset(vna[j][:], 1.0), w=[Bvna[j]])
            self._pp_i = 0

            def next_pp():
                j = self._pp_i % 7
                self._pp_i += 1
                return pp[j], Bpp[j]

            nblk = (NTOK + 511) // 512
            xi = 0
            evac = 0
            ffi = 0
            foi = 0
            for blk in range(nblk):
                t0 = blk * 512
                ntok = min(512, NTOK - t0)
                nsub = ntok // 128
                m = 0 if t0 < LAT else 1
                hb = blk % 2
                for s in range(nsub):
                    xj = xi % NXS
                    xi += 1
                    S.dma("sp", xt[xj][:], src[t0 + s * 128:t0 + (s + 1) * 128, :], w=[Bx[xj]])
                    for half in range(2):
                        p, Bp = next_pp()
                        for q in range(4):
                            dc = half * 4 + q
                            S.op("pe", lambda e, p=p, q=q, dc=dc, xj=xj: e.transpose(
                                p[:, q * 128:(q + 1) * 128], xt[xj][:, dc * 128:(dc + 1) * 128], self.ident_f[:]),
                                r=[Bx[xj], self.B_const], w=[Bp])
                        for q in range(4):
                            dc = half * 4 + q
                            if evac % 2 == 0:
                                S.op("act", lambda e, p=p, q=q, dc=dc, s=s, m=m: e.activation(
                                    out=hT[hb][:, dc, s * 128:(s + 1) * 128], in_=p[:, q * 128:(q + 1) * 128],
                                    func=AF.Identity, scale=self.sc1[:, dc, m:m + 1], bias=self.sh[:, dc, m:m + 1]),
                                    r=[Bp, self.B_mod], w=[Bh[hb]])
                            else:
                                S.op("dve", lambda e, p=p, q=q, dc=dc, s=s, m=m: e.tensor_scalar(
                                    out=hT[hb][:, dc, s * 128:(s + 1) * 128], in0=p[:, q * 128:(q + 1) * 128],
                                    scalar1=self.sc1[:, dc, m:m + 1], scalar2=self.sh[:, dc, m:m + 1],
                                    op0=ALU.mult, op1=ALU.add), r=[Bp, self.B_mod], w=[Bh[hb]])
                            evac += 1
                for c in range(16):
                    p, Bp = next_pp()
                    for k in range(8):
                        S.op("pe", lambda e, p=p, k=k, c=c: e.matmul(
                            p[:, :ntok], wb[:, k, c * 128:(c + 1) * 128], hT[hb][:, k, :ntok], start=(k == 0), stop=(k == 7)),
                            r=[Bw, Bh[hb]], w=[Bp])
                    if c < 8:
                        fj = ffi % 3
                        ffi += 1
                        if c % 2 == 0:
                            S.op("act", lambda e, p=p, c=c, fj=fj: e.activation(
                                out=ff[fj][:, :ntok], in_=p[:, :ntok], func=AF.Identity, bias=bcol[:, c:c + 1]),
                                r=[Bp, Bw], w=[Bff[fj]])
                        else:
                            S.op("dve", lambda e, p=p, c=c, fj=fj: e.tensor_scalar(
                                out=ff[fj][:, :ntok], in0=p[:, :ntok], scalar1=bcol[:, c:c + 1], scalar2=None, op0=ALU.add),
                                r=[Bp, Bw], w=[Bff[fj]])
                        S.dma("pool", self.qkpre[c, :, t0:t0 + ntok], ff[fj][:, :ntok], r=[Bff[fj]])
                    else:
                        fj = foi % 3
                        foi += 1
                        if c % 2 == 0:
                            S.op("act", lambda e, p=p, c=c, fj=fj: e.activation(
                                out=fo[fj][:, :ntok], in_=p[:, :ntok], func=AF.Identity, bias=bcol[:, c:c + 1]),
                                r=[Bp, Bw], w=[Bfo[fj]])
                        else:
                            S.op("dve", lambda e, p=p, c=c, fj=fj: e.tensor_scalar(
                                out=fo[fj][:, :ntok], in0=p[:, :ntok], scalar1=bcol[:, c:c + 1], scalar2=None, op0=ALU.add),
                                r=[Bp, Bw], w=[Bfo[fj]])
                        dd = self.qda if c < 12 else self.kda
                        S.dma("pool", dd[(c - 8) % 4, :, t0:t0 + ntok], fo[fj][:, :ntok], r=[Bfo[fj]])
                gb = blk % 2
                for s in range(nsub):
                    tt = (t0 // 128) + s
                    tok0 = t0 + s * 128

                    def tm(p, c0, n, b0, s=s):
                        for k in range(8):
                            S.op("pe", lambda e, k=k: e.matmul(
                                p[:, 0:n], hT[hb][:, k, s * 128:(s + 1) * 128], wb[:, k, c0:c0 + n], start=(k == 0), stop=False),
                                r=[Bw, Bh[hb]], w=[Bp])
                        S.op("pe", lambda e: e.matmul(p[:, 0:n], self.ones_b[0:1, :], brow[0:1, b0:b0 + n], start=False, stop=True),
                             r=[Bw, self.B_const], w=[Bp])
                    p, Bp = next_pp()
                    tm(p, OV, 512, 0)
                    vj = tt % 2
                    S.op("act", lambda e, p=p, vj=vj: e.activation(
                        out=va[vj][:, :, 0:128], in_=p[:, :].rearrange("p (h c) -> p h c", h=4), func=AF.Copy),
                        r=[Bp], w=[Bva[vj]])
                    S.dma("pool", self.vda[:, :, tt, :].rearrange("h p c -> p h c"), va[vj][:], r=[Bva[vj]])
                    p, Bp = next_pp()
                    tm(p, OO, 512, 512)
                    oj = tt % 2
                    S.op("act", lambda e, p=p, oj=oj: e.activation(out=ot[oj][:], in_=p[:, :], func=AF.Sigmoid), r=[Bp], w=[Bot[oj]])
                    S.dma("pool", self.og[tok0:tok0 + 128, :], ot[oj][:], r=[Bot[oj]])
                    Bp = Bpg
                    for k in range(8):
                        S.op("pe", lambda e, k=k, s=s: e.matmul(
                            pg[:, s * 16:(s + 1) * 16], hT[hb][:, k, s * 128:(s + 1) * 128], wb[:, k, OGT:OGT + 16],
                            start=(k == 0), stop=False), r=[Bw, Bh[hb]], w=[Bpg])
                    S.op("pe", lambda e, s=s: e.matmul(pg[:, s * 16:(s + 1) * 16], self.ones_b[0:1, :], brow[0:1, 1024:1040],
                                                       start=False, stop=True), r=[Bw, self.B_const], w=[Bpg])
                    p, Bp = next_pp()
                    tm(p, OVN, 512, 1040)
                    vj = tt % 2
                    S.op("act", lambda e, p=p, vj=vj: e.activation(
                        out=vna[vj][:, :, 0:64], in_=p[:, :].rearrange("p (h c) -> p h c", h=8), func=AF.Copy),
                        r=[Bp], w=[Bvna[vj]])
                    S.dma("pool", self.vm[:, :, tt, :].rearrange("h p c -> p h c"), vna[vj][:], r=[Bvna[vj]])
                    gj = tt % 2
                    for n in range(2):
                        p, Bp = next_pp()
                        tm(p, OG + n * 512, 512, 1552 + n * 512)
                        S.op("act", lambda e, p=p, n=n, gj=gj: e.activation(
                            out=gt[gj][:, n * 512:(n + 1) * 512], in_=p[:, :], func=AF.Silu), r=[Bp], w=[Bgt[gj]])
                    S.dma("pool", self.gate[tok0:tok0 + 128, :], gt[gj][:], r=[Bgt[gj]])
                pgv = pg[:, 0:nsub * 16].rearrange("p (s c) -> p s c", c=16)
                S.op("dve", lambda e, pgv=pgv: e.tensor_copy(gs[gb][:, 0:nsub, 0:8], pgv[:, :, 0:8]), r=[Bpg], w=[Bgs[gb]])
                for s in range(nsub):
                    S.op("dve", lambda e, s=s: e.tensor_tensor(out=gtmp[gb][:, s, :], in0=pg[:, s * 16 + 8:s * 16 + 16], in1=fb[:, :],
                                                               op=ALU.add), r=[Bpg, Bw], w=[Bgs[gb]])
                S.op("act", lambda e: e.activation(out=gtmp[gb][:, 0:nsub, :], in_=gtmp[gb][:, 0:nsub, :], func=AF.Exp, scale=-1.0),
                     r=[Bgs[gb]], w=[Bgs[gb]])
                S.op("act", lambda e: e.activation(out=gtmp[gb][:, 0:nsub, :], in_=gtmp[gb][:, 0:nsub, :], func=AF.Ln, bias=1.0),
                     r=[Bgs[gb]], w=[Bgs[gb]])
                S.op("dve", lambda e: e.tensor_scalar(out=gs[gb][:, 0:nsub, 8:16], in0=gtmp[gb][:, 0:nsub, :], scalar1=-1.0,
                                                      scalar2=None, op0=ALU.mult), r=[Bgs[gb]], w=[Bgs[gb]])
                tt0 = t0 // 128
                S.dma("pool", self.gates[:, tt0:tt0 + nsub, :], gs[gb][:, 0:nsub, :], r=[Bgs[gb]])
            S.barrier()

    def odd_conv(self, i):
        S = self.S
        SEG = 2048
        with contextlib.ExitStack() as st:
            cw = self.sb(st, "cv_w", [128, 8, 5])
            cb = self.sb(st, "cv_b", [128, 8])
            Bw = S.buf("cv_w")
            S.dma("sp", cw[:], self.od_convw[i, :, :, :], w=[Bw])
            S.dma("sp", cb[:], self.od_convb[i, :, :], w=[Bw])
            xin = [self.sb(st, f"cv_x{j}", [128, SEG + 4]) for j in range(2)]
            Bxin = S.bufs("cv_x", 2)
            acc = [self.sb(st, f"cv_a{j}", [128, SEG]) for j in range(2)]
            Bacc = S.bufs("cv_a", 2)
            tmp = [self.sb(st, f"cv_t{j}", [128, SEG]) for j in range(2)]
            Btmp = S.bufs("cv_t", 2)
            outb = [self.sb(st, f"cv_o{j}", [128, SEG], BF16) for j in range(2)]
            Bout = S.bufs("cv_o", 2)
            segs = [(a, SEG, 0, LAT) for a in range(0, LAT, SEG)] + [(LAT, CTX, LAT, LAT + CTX)]
            it = 0
            for c in range(8):
                for (t0, n, lo, hi) in segs:
                    j = it % 2
                    it += 1
                    a = max(t0 - 2, lo)
                    b = min(t0 + n + 2, hi)
                    if a > t0 - 2:
                        S.op("pool", lambda e, j=j: e.memset(xin[j][:, 0:2], 0.0), w=[Bxin[j]])
                    if b < t0 + n + 2:
                        S.op("pool", lambda e, j=j, n=n: e.memset(xin[j][:, n + 2:n + 4], 0.0), w=[Bxin[j]])
                    S.dma("sp", xin[j][:, a - (t0 - 2):b - (t0 - 2)], self.qkpre[c, :, a:b], w=[Bxin[j]])
                    S.op("act", lambda e, j=j, n=n, c=c: e.activation(
                        out=acc[j][:, 0:n], in_=xin[j][:, 0:n], func=AF.Copy, scale=cw[:, c, 0:1]),
                        r=[Bxin[j], Bw], w=[Bacc[j]])
                    for k in range(1, 5):
                        S.op("dve", lambda e, j=j, n=n, c=c, k=k: e.scalar_tensor_tensor(
                            out=acc[j][:, 0:n], in0=xin[j][:, k:k + n], scalar=cw[:, c, k:k + 1], in1=acc[j][:, 0:n],
                            op0=ALU.mult, op1=ALU.add), r=[Bxin[j], Bw, Bacc[j]], w=[Bacc[j]])
                    if True:
                        S.op("act", lambda e, j=j, n=n, c=c: e.activation(
                            out=outb[j][:, 0:n], in_=acc[j][:, 0:n], func=AF.Silu, bias=cb[:, c:c + 1]),
                            r=[Bacc[j], Bw], w=[Bout[j]])
                        dd = self.mq if c < 4 else self.mk
                        S.dma("pool", dd[c % 4, :, t0:t0 + n], outb[j][:, 0:n], r=[Bout[j]])
                    else:
                        S.op("act", lambda e, j=j, n=n, c=c: e.activation(
                            out=tmp[j][:, 0:n], in_=acc[j][:, 0:n], func=AF.Silu, bias=cb[:, c:c + 1]),
                            r=[Bacc[j], Bw], w=[Btmp[j]])
                        S.op("pool", lambda e, j=j, n=n: e.tensor_scalar(
                            out=outb[j][:, 0:n], in0=tmp[j][:, 0:n], scalar1=128 ** -0.5, scalar2=None, op0=ALU.mult),
                            r=[Btmp[j]], w=[Bout[j]])
                        S.dma("sp", self.mk[c - 4, :, t0:t0 + n], outb[j][:, 0:n], r=[Bout[j]])
            S.barrier()

    def mlstm(self, i):
        S = self.S
        with contextlib.ExitStack() as st:
            tri = self.sb(st, "ml_tri", [128, 3, 128])
            G = self.sb(st, "ml_G", [128, NT, 16])
            A = self.sb(st, "ml_A", [128, NT, 8])
            A2 = self.sb(st, "ml_A2", [128, NT, 8])
            Bq = self.sb(st, "ml_Bq", [128, NT, 8])
            EB = self.sb(st, "ml_EB", [128, NT, 8])
            Bg = S.buf("ml_g")
            Ao = self.sb(st, "ml_Ao", [128, NT, 4])
            A2o = self.sb(st, "ml_A2o", [128, NT, 4])
            Bqo = self.sb(st, "ml_Bqo", [128, NT, 4])
            EBo = self.sb(st, "ml_EBo", [128, NT, 4])
            S.dma("sp", tri[:], self.tri[:, :, :], w=[Bg])
            S.dma("sp", G[:], self.gates[:, :, :], w=[Bg])
            lnks = self.sb(st, "ml_lnks", [128, 1])
            S.op("pool", lambda e: e.memset(lnks[:], math.log(128 ** -0.5)), w=[Bg])
            with contextlib.ExitStack() as st1:
                pg = [self.ps(st1, f"ml_pg{j}", [128, 512]) for j in range(3)]
                Bpg = S.bufs("ml_pg", 3)
                tmpg = self.sb(st1, "ml_tmpg", [128, 32, 8])
                Btg = S.buf("ml_tmpg")
                grp = 0
                for g0 in range(0, NT, 32):
                    g1 = min(g0 + 32, NT)
                    pj = grp % 3
                    grp += 1
                    for t in range(g0, g1):
                        o = (t - g0) * 16
                        S.op("pe", lambda e, t=t, o=o, pj=pj: e.matmul(pg[pj][:, o:o + 4], tri[:, 0, :], G[:, t, 8:12], start=True, stop=True),
                             r=[Bg], w=[Bpg[pj]])
                        S.op("pe", lambda e, t=t, o=o, pj=pj: e.matmul(pg[pj][:, o + 4:o + 8], tri[:, 1, :], G[:, t, 12:16], start=True, stop=True),
                             r=[Bg], w=[Bpg[pj]])
                        S.op("pe", lambda e, t=t, o=o, pj=pj: e.matmul(pg[pj][:, o + 8:o + 16], tri[:, 2, :], G[:, t, 8:16], start=True, stop=True),
                             r=[Bg], w=[Bpg[pj]])
                    n = g1 - g0
                    pv = pg[pj][:, 0:n * 16].rearrange("p (t c) -> p t c", c=16)
                    S.op("dve", lambda e, pv=pv, g0=g0, g1=g1, n=n: e.tensor_tensor(
                        out=tmpg[:, 0:n, :], in0=G[:, g0:g1, 0:8], in1=pv[:, :, 0:8], op=ALU.subtract), r=[Bg, Bpg[pj]], w=[Btg])
                    S.op("act", lambda e, g0=g0, g1=g1, n=n: e.activation(out=A[:, g0:g1, :], in_=tmpg[:, 0:n, :], func=AF.Exp,
                                                                          bias=lnks[:, 0:1]), r=[Btg, Bg], w=[Bg])
                    S.op("act", lambda e, pv=pv, g0=g0, g1=g1: e.activation(out=Bq[:, g0:g1, :], in_=pv[:, :, 0:8], func=AF.Exp),
                         r=[Bpg[pj]], w=[Bg])
                    S.op("act", lambda e, pv=pv, g0=g0, g1=g1: e.activation(out=EB[:, g0:g1, :], in_=pv[:, :, 8:16], func=AF.Exp),
                         r=[Bpg[pj]], w=[Bg])
                    S.op("dve", lambda e, g0=g0, g1=g1: e.tensor_tensor(out=A2[:, g0:g1, :], in0=A[:, g0:g1, :], in1=EB[:, g0:g1, :],
                                                                        op=ALU.mult), r=[Bg], w=[Bg])
                for X, Xo in ((A, Ao), (A2, A2o), (Bq, Bqo), (EB, EBo)):
                    for d in range(2):
                        self.blend(Xo[:, :, d * 2:d * 2 + 2], X[:, :, d * 4:d * 4 + 2], X[:, :, d * 4 + 2:d * 4 + 4], Bg, Bg, Bg)
                S.barrier()
            stgA = self.sb(st, "ml_stgA", [128, NT * 129], BF16)
            stgB = self.sb(st, "ml_stgB", [128, NT * 129], BF16)
            BsA, BsB = S.buf("ml_stgA"), S.buf("ml_stgB")
            qT = self.sb(st, "ml_qT", [128, NTOK], BF16)
            kT = self.sb(st, "ml_kT", [128, NTOK], BF16)
            V = self.sb(st, "ml_V", [128, NT, 129], BF16)
            KTOK = self.sb(st, "ml_KTOK", [128, NT, 128], BF16)
            Bin = S.buf("ml_in")
            Bkt = S.buf("ml_ktok")
            Cn = [self.sb(st, f"ml_Cn{d}", [128, 129]) for d in range(2)]
            Cnb = [[self.sb(st, f"ml_Cnb{d}{j}", [128, 129], BF16) for j in range(2)] for d in range(2)]
            BCn = S.bufs("ml_Cn", 2)
            BCnb = [S.bufs(f"ml_Cnb{d}", 2) for d in range(2)]
            NW = 6
            W = [self.sb(st, f"ml_W{j}", [128, 128], BF16) for j in range(NW)]
            BW = S.bufs("ml_W", NW)
            v2 = [self.sb(st, f"ml_v2{j}", [128, 129], BF16) for j in range(NW)]
            Bv2 = S.bufs("ml_v2", NW)
            ho = [self.sb(st, f"ml_ho{j}", [128, 128]) for j in range(NW)]
            Bho = S.bufs("ml_ho", NW)
            sm = [self.sb(st, f"ml_sm{j}", [128, 4]) for j in range(NW)]
            Bsm = S.bufs("ml_sm", NW)
            NS, NKV, NN = 2, 2, 4
            order = [[NT_LAT, NT_LAT + 1] + list(range(NT_LAT)), [NT_LAT + 1, NT_LAT] + list(range(NT_LAT - 1, -1, -1))]
            wi = 0
            for h in range(2):
                for dst_ap, srcf, n in ((qT[:], lambda hh: self.mq[hh, :, :], NTOK), (kT[:], lambda hh: self.mk[hh, :, :], NTOK),
                                        (V[:].rearrange("p t c -> p (t c)"),
                                         lambda hh: self.vda[hh, :, :, :].rearrange("p t c -> p (t c)"), NT * 129)):
                    S.dma("sp", stgA[:, 0:n], srcf(h), w=[BsA])
                    S.dma("act", stgB[:, 0:n], srcf(h + 2), w=[BsB])
                    self.blend(dst_ap, stgA[:, 0:n], stgB[:, 0:n], BsA, BsB, Bin)
                with contextlib.ExitStack() as stp:
                    ppbs = [self.ps(stp, f"ml_ppb{j}", [128, 1024], BF16) for j in range(2)]
                    Bppbs = S.bufs("ml_ppb", 2)
                    for gi, t0 in enumerate(range(0, NT, 8)):
                        n = min(8, NT - t0)
                        ppb, Bppb = ppbs[gi % 2], Bppbs[gi % 2]
                        for q in range(n):
                            t = t0 + q
                            S.op("pe", lambda e, t=t, q=q, ppb=ppb: e.transpose(
                                ppb[:, q * 128:(q + 1) * 128], kT[:, t * 128:(t + 1) * 128], self.ident_b[:]),
                                r=[Bin, self.B_const], w=[Bppb])
                        S.op("act", lambda e, t0=t0, n=n, ppb=ppb: e.activation(
                            out=KTOK[:, t0:t0 + n, :], in_=ppb[:, 0:n * 128].rearrange("p (t c) -> p t c", c=128), func=AF.Copy),
                            r=[Bppb], w=[Bkt])
                    S.barrier()
                stq = contextlib.ExitStack()
                ps_s = [self.ps(stq, f"ml_pss{j}", [128, 512]) for j in range(NS)]
                ps_kv = [self.ps(stq, f"ml_pkv{j}", [128, 512]) for j in range(NKV)]
                ps_n = [self.ps(stq, f"ml_pn{j}", [128, 512]) for j in range(NN)]
                Bps_s, Bps_kv, Bps_n = S.bufs("ml_pss", NS), S.bufs("ml_pkv", NKV), S.bufs("ml_pn", NN)
                for d in range(2):
                    S.op("pool", lambda e, d=d: e.memset(Cn[d][:], 0.0), w=[BCn[d]])
                    S.op("pool", lambda e, d=d: e.memset(Cnb[d][0][:], 0.0), w=[BCnb[d][0]])

                def stage1(step, d, wi):
                    t = order[d][step]
                    hd = d * 2 + h
                    j = wi % NW
                    p3 = wi % NS
                    tsl = slice(t * 128, (t + 1) * 128)
                    S.op("pe", lambda e: e.matmul(ps_s[p3][:, 0:128], kT[:, tsl], qT[:, tsl], start=True, stop=True),
                         r=[Bin], w=[Bps_s[p3]])
                    S.op("dve", lambda e: e.scalar_tensor_tensor(
                        out=W[j][:], in0=ps_s[p3][:, 0:128], scalar=Ao[:, t, hd:hd + 1], in1=tri[:, d, :],
                        op0=ALU.mult, op1=ALU.mult), r=[Bps_s[p3], Bg], w=[BW[j]])
                    S.op("act", lambda e: e.activation(
                        out=v2[j][:], in_=V[:, t, :], func=AF.Copy, scale=A2o[:, t, hd:hd + 1]),
                        r=[Bin, Bg], w=[Bv2[j]])

                def stage2(step, d, wi):
                    t = order[d][step]
                    hd = d * 2 + h
                    cur, nxt = step % 2, (step + 1) % 2
                    j = wi % NW
                    p3 = wi % NKV
                    p6 = wi % NN
                    tsl = slice(t * 128, (t + 1) * 128)
                    S.op("pe", lambda e: e.matmul(ps_kv[p3][:, 0:129], KTOK[:, t, :], v2[j][:], start=True, stop=True),
                         r=[Bkt, Bv2[j]], w=[Bps_kv[p3]])
                    S.op("pe", lambda e: e.matmul(ps_n[p6][:, 0:129], W[j][:], V[:, t, :], start=True, stop=False),
                         r=[BW[j], Bin], w=[Bps_n[p6]])
                    S.op("pe", lambda e: e.matmul(ps_n[p6][:, 0:129], qT[:, tsl], Cnb[d][cur][:], start=False, stop=True),
                         r=[Bin, BCnb[d][cur]], w=[Bps_n[p6]])
                    S.op("dve", lambda e: e.scalar_tensor_tensor(
                        out=Cn[d][:], in0=Cn[d][:], scalar=EBo[:, t, hd:hd + 1], in1=ps_kv[p3][:, 0:129],
                        op0=ALU.mult, op1=ALU.add), r=[BCn[d], Bps_kv[p3], Bg], w=[BCn[d]])
                    S.op("act", lambda e: e.activation(out=Cnb[d][nxt][:], in_=Cn[d][:], func=AF.Copy),
                         r=[BCn[d]], w=[BCnb[d][nxt]])

                def back(step, d, wi):
                    t = order[d][step]
                    hd = d * 2 + h
                    j = wi % NW
                    p6 = wi % NN
                    S.op("dve", lambda e: e.tensor_tensor(
                        out=sm[j][:, 0:1], in0=ps_n[p6][:, 128:129], in1=Bqo[:, t, hd:hd + 1], op=ALU.mult),
                        r=[Bps_n[p6], Bg], w=[Bsm[j]])
                    S.op("dve", lambda e: e.scalar_tensor_tensor(out=sm[j][:, 1:2], in0=sm[j][:, 0:1], scalar=-1.0,
                                                                 in1=sm[j][:, 0:1], op0=ALU.mult, op1=ALU.max),
                         r=[Bsm[j]], w=[Bsm[j]])
                    S.op("dve", lambda e: e.tensor_scalar(out=sm[j][:, 1:2], in0=sm[j][:, 1:2], scalar1=1.0, scalar2=None,
                                                          op0=ALU.max), r=[Bsm[j]], w=[Bsm[j]])
                    S.op("dve", lambda e: e.reciprocal(sm[j][:, 2:3], sm[j][:, 1:2]), r=[Bsm[j]], w=[Bsm[j]])
                    S.op("dve", lambda e: e.tensor_tensor(
                        out=sm[j][:, 3:4], in0=sm[j][:, 2:3], in1=Bqo[:, t, hd:hd + 1], op=ALU.mult), r=[Bsm[j], Bg], w=[Bsm[j]])
                    S.op("dve", lambda e: e.tensor_scalar(
                        out=ho[j][:], in0=ps_n[p6][:, 0:128], scalar1=sm[j][:, 3:4], scalar2=None, op0=ALU.mult),
                        r=[Bps_n[p6], Bsm[j]], w=[Bho[j]])
                    S.dma("sp", self.hfb[d, t * 128:(t + 1) * 128, h * 128:(h + 1) * 128], ho[j][:], r=[Bho[j]])

                items = []
                for step in range(NT):
                    for d in range(2):
                        items.append((step, d, wi))
                        wi += 1
                npair = len(items) // 2
                for k in range(npair + 2):
                    if k < npair:
                        stage1(*items[2 * k])
                        stage1(*items[2 * k + 1])
                    if 0 <= k - 1 < npair:
                        stage2(*items[2 * (k - 1)])
                        stage2(*items[2 * (k - 1) + 1])
                    if 0 <= k - 2 < npair:
                        back(*items[2 * (k - 2)])
                        back(*items[2 * (k - 2) + 1])
                S.barrier()
                stq.close()
            S.barrier()

    def mlstm_post(self, i):
        S = self.S
        with contextlib.ExitStack() as st:
            ng = self.sb(st, "mp_ng", [128, 512])
            ngo = self.sb(st, "mp_ngo", [128, 256])
            Bw = S.buf("mp_w")
            S.dma("sp", ng[:], self.od_ng[i, :, :].to_broadcast([128, 512]), w=[Bw])
            self.blend(ngo[:], ng[:, 0:256], ng[:, 256:512], Bw, Bw, Bw)
            eps_t = self.sb(st, "mp_eps", [128, 1])
            S.op("pool", lambda e: e.memset(eps_t[:], LN_EPS), w=[Bw])
            NB = 3
            hf = [self.sb(st, f"mp_hf{j}", [128, 256]) for j in range(NB)]
            hb = [self.sb(st, f"mp_hb{j}", [128, 256]) for j in range(NB)]
            og = [self.sb(st, f"mp_og{j}", [128, 512]) for j in range(NB)]
            ogo = [self.sb(st, f"mp_ogo{j}", [128, 256]) for j in range(NB)]
            hs = [self.sb(st, f"mp_hs{j}", [128, 256]) for j in range(NB)]
            yo = [self.sb(st, f"mp_yo{j}", [128, 256]) for j in range(NB)]
            yob = [self.sb(st, f"mp_yob{j}", [128, 256], BF16) for j in range(NB)]
            stt = [self.sb(st, f"mp_st{j}", [128, 2, 12]) for j in range(NB)]
            Bhf, Bhb, Bog, Bogo, Bhs, Byo, Bst = (S.bufs(n, NB) for n in ("mp_hf", "mp_hb", "mp_og", "mp_ogo", "mp_hs", "mp_yo", "mp_st"))

            def post_a(t):
                j = t % NB
                tok0 = t * 128
                S.dma("sp", hf[j][:], self.hfb[0, tok0:tok0 + 128, 0:256], w=[Bhf[j]])
                S.dma("act", hb[j][:], self.hfb[1, tok0:tok0 + 128, 0:256], w=[Bhb[j]])
                S.dma("pool", og[j][:], self.og[tok0:tok0 + 128, :], w=[Bog[j]])
                S.op("pool", lambda e, j=j: e.tensor_tensor(out=hs[j][:], in0=hf[j][:], in1=hb[j][:], op=ALU.add),
                     r=[Bhf[j], Bhb[j]], w=[Bhs[j]])
                self.blend(ogo[j][:], og[j][:, 0:256], og[j][:, 256:512], Bog[j], Bog[j], Bogo[j])
                S.op("pool", lambda e, j=j: e.tensor_tensor(out=ogo[j][:], in0=ogo[j][:], in1=ngo[:], op=ALU.mult),
                     r=[Bogo[j], Bw], w=[Bogo[j]])

            def post_b(t):
                j = t % NB
                for h in range(2):
                    S.op("dve", lambda e, j=j, h=h: e.bn_stats(stt[j][:, h, 0:6], hs[j][:, h * 128:(h + 1) * 128]),
                         r=[Bhs[j]], w=[Bst[j]])
                    S.op("dve", lambda e, j=j, h=h: e.bn_aggr(stt[j][:, h, 6:8], stt[j][:, h, 0:6]), r=[Bst[j]], w=[Bst[j]])
                S.op("act", lambda e, j=j: e.activation(out=stt[j][:, :, 8:9], in_=stt[j][:, :, 7:8], func=AF.Sqrt, scale=1.0,
                                                        bias=eps_t[:, 0:1]), r=[Bst[j], Bw], w=[Bst[j]])
                S.op("dve", lambda e, j=j: e.reciprocal(stt[j][:, :, 9:10], stt[j][:, :, 8:9]), r=[Bst[j]], w=[Bst[j]])
                for h in range(2):
                    S.op("dve", lambda e, j=j, h=h: e.tensor_scalar(
                        out=yo[j][:, h * 128:(h + 1) * 128], in0=hs[j][:, h * 128:(h + 1) * 128], scalar1=stt[j][:, h, 6:7],
                        scalar2=stt[j][:, h, 9:10], op0=ALU.subtract, op1=ALU.mult), r=[Bhs[j], Bst[j]], w=[Byo[j]])
                S.op("pool", lambda e, j=j: e.tensor_tensor(out=yob[j][:], in0=yo[j][:], in1=ogo[j][:], op=ALU.mult),
                     r=[Byo[j], Bogo[j]], w=[Byo[j]])
                k, r0 = self.yodd_loc(t)
                S.dma("sp", self.yodd_t[k].ap()[r0:r0 + 128, 0:256], yob[j][:], r=[Byo[j]])

            for t in range(NT):
                if t == 0:
                    post_a(0)
                if t + 1 < NT:
                    post_a(t + 1)
                post_b(t)
            S.barrier()

    def na_attention(self, i, last):
        S = self.S
        with contextlib.ExitStack() as st:
            QNe = self.sb(st, "na_Qe", [128, NTOK], BF16)
            QNo = self.sb(st, "na_Qo", [128, NTOK], BF16)
            KN = self.sb(st, "na_K", [128, NTOK], BF16)
            VN = self.sb(st, "na_V", [128, NT, 65], BF16)
            MB = self.sb(st, "na_MB", [128, NA_NVAR, 128], BF16)
            Bqk = S.buf("na_qk")
            Bv = S.buf("na_v")
            Bmb = S.buf("na_mb")
            stg = [self.sb(st, f"na_stg{j}", [128, 7, 128]) for j in range(2)]
            stg2 = [self.sb(st, f"na_stgb{j}", [128, 7, 128]) for j in range(2)]
            Bstg = S.bufs("na_stg", 2)
            Bstg2 = S.bufs("na_stgb", 2)
            stgA = self.sb(st, "na_stgA", [128, NTOK], BF16)
            stgB = self.sb(st, "na_stgB", [128, NTOK], BF16)
            BsA, BsB = S.buf("na_stgA"), S.buf("na_stgB")
            NP = 3
            ptA = [self.sb(st, f"na_ptA{j}", [128, 512], BF16) for j in range(NP)]
            ptB = [self.sb(st, f"na_ptB{j}", [128, 384], BF16) for j in range(NP)]
            BptA, BptB = S.bufs("na_ptA", NP), S.bufs("na_ptB", NP)
            rr = [self.sb(st, f"na_rr{j}", [128, 2]) for j in range(NP)]
            to = [self.sb(st, f"na_to{j}", [128, 64], BF16) for j in range(NP)]
            Bto = S.bufs("na_to", NP)
            psA = [self.ps(st, f"na_psA{j}", [128, 512]) for j in range(2)]
            psB = [self.ps(st, f"na_psB{j}", [128, 512]) for j in range(2)]
            acc = [self.ps(st, f"na_acc{j}", [128, 512]) for j in range(2)]
            BpsA, BpsB, Bacc = S.bufs("na_psA", 2), S.bufs("na_psB", 2), S.bufs("na_acc", 2)
            it = 0
            qtiles = list(range(NT_LAT)) + ([] if (last and self.final_out) else [NT_LAT, NT_LAT + 1])
            if self.tile_filter is not None:
                qtiles = [t for t in qtiles if t in self.tile_filter]
            for h in range(4):
                lo = (h % 2) * 64
                if h == 0:
                    S.op("pool", lambda e: e.memset(QNe[64:128, :], 0.0), w=[Bqk])
                    S.op("pool", lambda e: e.memset(QNo[0:64, :], 0.0), w=[Bqk])
                if h % 2 == 0:
                    hp = h // 2
                    S.dma("sp", stgA[:], self.qda[hp, :, :], w=[BsA])
                    S.dma("act", stgB[:], self.qda[hp + 2, :, :], w=[BsB])
                    self.blend(QNe[0:64, :], stgA[0:64, :], stgB[0:64, :], BsA, BsB, Bqk, 0, 64)
                    self.blend(QNo[64:128, :], stgA[64:128, :], stgB[64:128, :], BsA, BsB, Bqk, 64, 128)
                    S.dma("sp", stgA[:], self.kda[hp, :, :], w=[BsA])
                    S.dma("act", stgB[:], self.kda[hp + 2, :, :], w=[BsB])
                    self.blend(KN[:], stgA[:], stgB[:], BsA, BsB, Bqk)
                QN = QNe if h % 2 == 0 else QNo
                nv = NT * 65
                S.dma("sp", stgA[:, 0:nv], self.vm[h, :, :, :].rearrange("p t c -> p (t c)"), w=[BsA])
                S.dma("act", stgB[:, 0:nv], self.vm[h + 4, :, :, :].rearrange("p t c -> p (t c)"), w=[BsB])
                self.blend(VN[:].rearrange("p t c -> p (t c)"), stgA[:, 0:nv], stgB[:, 0:nv], BsA, BsB, Bv)
                for v0 in range(0, NA_NVAR, 7):
                    n = min(7, NA_NVAR - v0)
                    sj = (v0 // 7) % 2
                    S.dma("sp", stg[sj][:, 0:n, :], self.od_nat[i, h, :, v0:v0 + n, :], w=[Bstg[sj]])
                    S.dma("act", stg2[sj][:, 0:n, :], self.od_nat[i, h + 4, :, v0:v0 + n, :], w=[Bstg2[sj]])
                    self.blend(MB[:, v0:v0 + n, :], stg[sj][:, 0:n, :], stg2[sj][:, 0:n, :], Bstg[sj], Bstg2[sj], Bmb, col=2)
                for j in qtiles:
                    pj = it % 2
                    tj = it % NP
                    it += 1
                    qsl = slice(j * 128, (j + 1) * 128)
                    if j < NT_LAT:
                        slots = [(kt, var) for (kt, var) in NA_PLAN[j]] + [(NT_LAT, None), (NT_LAT + 1, None)]
                    else:
                        slots = [(NT_LAT, None), (NT_LAT + 1, None)]
                    nA = min(4, len(slots))
                    nB = len(slots) - nA
                    for si, (kt, var) in enumerate(slots):
                        if si < 4:
                            dst, Bd = psA[pj][:, si * 128:(si + 1) * 128], BpsA[pj]
                        else:
                            dst, Bd = psB[pj][:, (si - 4) * 128:(si - 3) * 128], BpsB[pj]
                        S.op("pe", lambda e, dst=dst, kt=kt, var=var, QN=QN: e.matmul(
                            dst, KN[:, kt * 128:(kt + 1) * 128], QN[:, qsl], start=True, stop=(var is None)),
                            r=[Bqk], w=[Bd])
                        if var is not None:
                            S.op("pe", lambda e, dst=dst, var=var: e.matmul(dst, self.ident_b[:], MB[:, var, :], start=False, stop=True),
                                 r=[Bmb, self.B_const], w=[Bd])
                    S.op("act", lambda e, pj=pj, tj=tj, nA=nA: e.activation(out=ptA[tj][:, 0:nA * 128], in_=psA[pj][:, 0:nA * 128],
                                                                            func=AF.Exp, scale=NA_SCALE), r=[BpsA[pj]], w=[BptA[tj]])
                    if nB > 0:
                        S.op("act", lambda e, pj=pj, tj=tj, nB=nB: e.activation(out=ptB[tj][:, 0:nB * 128], in_=psB[pj][:, 0:nB * 128],
                                                                                func=AF.Exp, scale=NA_SCALE), r=[BpsB[pj]], w=[BptB[tj]])
                    for si, (kt, var) in enumerate(slots):
                        if si < 4:
                            lhs, Bl = ptA[tj][:, si * 128:(si + 1) * 128], BptA[tj]
                        else:
                            lhs, Bl = ptB[tj][:, (si - 4) * 128:(si - 3) * 128], BptB[tj]
                        S.op("pe", lambda e, lhs=lhs, kt=kt, si=si, pj=pj: e.matmul(
                            acc[pj][:, 0:65], lhs, VN[:, kt, :], start=(si == 0), stop=(si == len(slots) - 1)),
                            r=[Bl, Bv], w=[Bacc[pj]])
                    S.op("dve", lambda e, pj=pj, tj=tj: e.reciprocal(rr[tj][:, 0:1], acc[pj][:, 64:65]), r=[Bacc[pj]], w=[Bto[tj]])
                    S.op("dve", lambda e, pj=pj, tj=tj: e.tensor_scalar(out=to[tj][:], in0=acc[pj][:, 0:64], scalar1=rr[tj][:, 0:1],
                                                                        scalar2=None, op0=ALU.mult), r=[Bacc[pj], Bto[tj]], w=[Bto[tj]])
                    kk, r0 = self.yodd_loc(j)
                    S.dma("pool", self.yodd_t[kk].ap()[r0:r0 + 128, 256 + h * 64:256 + (h + 1) * 64], to[tj][:], r=[Bto[tj]])
            S.barrier()


def _swap64(cols):
    return np.concatenate([cols[32:64], cols[0:32]])


def _rope_tables():
    t = np.arange(LAT)
    row = (t // GRID_W).astype(np.float32)
    col = (t % GRID_W).astype(np.float32)

    def tab(dim, nrows_pattern):
        n_freq = dim // 4
        freqs = (10000.0 ** (-np.arange(n_freq, dtype=np.float32) / n_freq)).astype(np.float32)
        ang = np.concatenate([row[:, None] * freqs, col[:, None] * freqs], axis=-1).astype(np.float32)
        cos = np.cos(ang).astype(np.float32).T
        sin = np.sin(ang).astype(np.float32).T
        half = dim // 2
        c = np.concatenate([cos, cos], 0)
        s = np.concatenate([-sin, sin], 0)
        c = np.concatenate([c, np.ones((dim, CTX), np.float32)], 1)
        s = np.concatenate([s, np.zeros((dim, CTX), np.float32)], 1)
        return c, s

    c64, s64 = tab(64, None)
    c32, s32 = tab(32, None)
    rope_da = np.stack([np.concatenate([c64, c64], 0), np.concatenate([s64, s64], 0)]).astype(np.float32)
    ml_c = np.ones((128, NTOK), np.float32)
    ml_s = np.zeros((128, NTOK), np.float32)
    ml_c[0:32] = c32
    ml_s[0:32] = s32
    ml_c[64:96] = c32
    ml_s[64:96] = s32
    rope_ml = np.stack([ml_c, ml_s]).astype(np.float32)
    return rope_da, rope_ml


def _even_layout(inp):
    ev_w_in, ev_b_in = inp["ev_w_in"], inp["ev_b_in"]
    o_q1, o_q2, o_k1, o_k2, o_v, o_cq, o_ckv, o_kr, o_g = 0, 256, 512, 768, 1024, 1536, 1792, 1920, 1952
    cols = []
    for (a, b) in ((o_q1, o_q2), (o_k1, o_k2)):
        main = []
        sw = []
        for h in range(4):
            c1 = np.arange(a + h * 64, a + (h + 1) * 64)
            c2 = np.arange(b + h * 64, b + (h + 1) * 64)
            main += [c1, c2]
            sw += [_swap64(c1), _swap64(c2)]
        cols += main + sw
    kr = np.arange(o_kr, o_kr + 32)
    cols += [kr, np.concatenate([kr[16:], kr[:16]])]
    cols += [np.arange(o_v, o_v + 512), np.arange(o_cq, o_cq + 384), np.arange(o_g, o_g + 1024)]
    perm = np.concatenate(cols)
    assert perm.shape[0] == EV_COLS
    ev_w = np.ascontiguousarray(ev_w_in[:, :, perm])
    bp = ev_b_in[:, perm]
    bcol = np.zeros((2, 128, 18), np.float32)
    for g in range(16):
        bcol[:, :, g] = bp[:, g * 128:(g + 1) * 128]
    bcol[:, 0:32, 16] = bp[:, EKR:EKR + 32]
    bcol[:, 0:32, 17] = bp[:, EKR + 32:EKR + 64]
    brow = np.ascontiguousarray(bp[:, EV_:EV_ + 1920])[:, None, :]
    wuq = inp["mla_w_uq"]
    swc = []
    for h in range(8):
        base = h * 96
        c = np.arange(base, base + 96)
        r = c[64:96]
        swc.append(np.concatenate([c[0:64], r[16:], r[:16]]))
    swc = np.concatenate(swc)
    ev_wuq = np.ascontiguousarray(np.concatenate([wuq, wuq[:, :, swc]], axis=2))
    wukv = inp["mla_w_ukv"]
    nope = np.concatenate([np.arange(h * 128, h * 128 + 64) for h in range(8)])
    vv = np.concatenate([np.arange(h * 128 + 64, h * 128 + 128) for h in range(8)])
    ev_wukv = np.ascontiguousarray(wukv[:, :, np.concatenate([nope, vv])])
    return dict(
        ev_w=ev_w, ev_bcol=bcol, ev_brow=np.ascontiguousarray(brow),
        ev_lam=np.ascontiguousarray(inp["da_lambda"].reshape(2, 1, 256)),
        ev_subg=np.ascontiguousarray(inp["da_subln_g"].reshape(2, 1, 128)),
        ev_qg=np.ascontiguousarray(inp["mla_q_norm_g"].reshape(2, 2, 128).transpose(0, 2, 1)),
        ev_kvg=np.ascontiguousarray(inp["mla_kv_norm_g"].reshape(2, 128, 1)),
        ev_wuq=ev_wuq, ev_wukv=ev_wukv, ev_wo=np.ascontiguousarray(inp["ev_w_out"]),
    )


def _odd_layout(inp):
    w, b = inp["od_w_in"], inp["od_b_in"]
    g0 = 2048
    gates = np.concatenate([g0 + j * 4 + np.arange(4) for j in (0, 2, 1, 3)])
    perm = np.concatenate([np.arange(0, 1024), np.arange(2064, 2576), np.arange(2576, 3088), np.arange(1024, 1536),
                           np.arange(1536, 2048), gates, np.arange(3088, 3600), np.arange(3600, 4624)])
    assert perm.shape[0] == OD_COLS
    od_w = np.ascontiguousarray(w[:, :, perm])
    bp = b[:, perm]
    bcol = np.ascontiguousarray(bp[:, 0:2048].reshape(2, 16, 128).transpose(0, 2, 1))
    brow = np.ascontiguousarray(bp[:, 2048:])[:, None, :]
    convw = np.ascontiguousarray(inp["ml_conv_w"].reshape(2, 5, 8, 128).transpose(0, 3, 2, 1))
    convb = np.ascontiguousarray(inp["ml_conv_b"].reshape(2, 8, 128).transpose(0, 2, 1))
    fb = np.ascontiguousarray(inp["ml_f_bias"].reshape(2, 1, 8))
    ng = np.ascontiguousarray(inp["ml_norm_g"].reshape(2, 1, 512))
    rpb = inp["na_rpb"]
    nat = np.full((2, 8, 128, NA_NVAR, 128), NEG, np.float32)
    k = np.arange(128)
    krl, kc = k // 64, k % 64
    q = np.arange(128)
    qrl, qc = q // 64, q % 64
    cs = np.clip(qc - NA_COLS // 2, 0, GRID_W - NA_COLS)
    for (dk, o0, o1), vid in NA_VARIANTS.items():
        rel = (2 * dk + krl[:, None]) - qrl[None, :]
        off = np.where(qrl[None, :] == 0, o0, o1)
        valid_r = (rel >= off) & (rel <= off + NA_ROWS - 1)
        valid_c = (kc[:, None] >= cs[None, :]) & (kc[:, None] <= cs[None, :] + NA_COLS - 1)
        valid = valid_r & valid_c
        ridx = np.clip(rel + NA_ROWS - 1, 0, 2 * NA_ROWS - 2)
        cidx = np.clip(kc[:, None] - qc[None, :] + NA_COLS - 1, 0, 2 * NA_COLS - 2)
        tab = rpb[:, :, ridx, cidx]
        nat[:, :, :, vid, :] = np.where(valid[None, None], tab, np.float32(NEG))
    tri = np.stack([np.triu(np.ones((128, 128), np.float32)), np.tril(np.ones((128, 128), np.float32)),
                    np.ones((128, 128), np.float32)], 1)
    return dict(od_w=od_w, od_bcol=bcol, od_brow=np.ascontiguousarray(brow), od_convw=convw, od_convb=convb, od_fb=fb,
                od_ng=ng, od_nat=nat, od_wo=np.ascontiguousarray(inp["od_w_out"]), tri=np.ascontiguousarray(tri))


def make_in_maps(inp, batches):
    inp = {k: np.asarray(v, dtype=np.float32) for k, v in inp.items()}
    rope_da, rope_ml = _rope_tables()
    common = dict(
        ada_w=inp["ada_w"], ada_b=inp["ada_b"], ln_g=inp["ln_g"], ln_b=inp["ln_b"],
        ident=np.eye(128, dtype=np.float32),
        sel=np.concatenate([np.stack([np.ones(128), np.zeros(128)]), np.stack([np.zeros(128), np.ones(128)])], 1).astype(np.float32),
        rope_da=rope_da, rope_ml=rope_ml,
    )
    common.update(_even_layout(inp))
    common.update(_odd_layout(inp))
    maps = []
    for b in batches:
        m = dict(common)
        m["x_in"] = np.ascontiguousarray(np.concatenate([inp["x"][b], inp["ctx"][b]], 0))
        cc = np.stack([inp["c"][b], inp["c_ctx"]], -1)
        m["cc"] = np.ascontiguousarray(cc.reshape(8, 128, 2).transpose(1, 0, 2))
        par = len(maps) % 2
        ps = np.zeros((128, 2), np.float32)
        ps[:, par] = 1.0
        m["psel"] = ps
        maps.append(m)
    return maps


_PROG_CACHE = {}


def kernel(**inputs):
    batches = [c // 2 for c in range(8)]
    in_maps = make_in_maps(inputs, batches)
    prog = Prog()
    nc = prog.build()
    res = run_bass_kernel_spmd(nc, in_maps, core_ids=list(range(8)))
    out = np.stack([res.results[2 * b]["y"] for b in range(4)], 0)
    return out.astype(np.float32)
```

```python
import contextlib
import math
import numpy as np
import concourse.bass as bass
import concourse.mybir as mybir
from concourse.bass_utils import run_bass_kernel_spmd

F32 = mybir.dt.float32
BF16 = mybir.dt.bfloat16
AF = mybir.ActivationFunctionType
ALU = mybir.AluOpType
AX = mybir.AxisListType

D = 1024
LAT = 8192
CTX = 256
NTOK = LAT + CTX
NT = NTOK // 128
NT_LAT = LAT // 128
DEPTH = 4
GRID_W = 64
ALPHA = (2.0 * DEPTH) ** 0.25
LN_EPS = 1e-5
RMS_EPS = 1e-6
DA_SCALE = 64 ** -0.5
MLA_SCALE = 96 ** -0.5
NA_SCALE = 64 ** -0.5
EV_COLS = 2752
EQ, EK, EKR, EV_, EC, EG = 0, 512, 1024, 1088, 1344, 1728
OD_COLS = 2824
OQK, OQN, OKN, OV, OO, OGT, OVN, OG = 0, 512, 768, 1024, 1280, 1536, 1544, 1800
NA_ROWS, NA_COLS, GRID_H = 8, 16, 128
NEG = -30000.0


def na_plan():
    variants = {}
    plan = []
    for j in range(NT_LAT):
        r0, r1 = 2 * j, 2 * j + 1
        rs0 = min(max(r0 - NA_ROWS // 2, 0), GRID_H - NA_ROWS)
        rs1 = min(max(r1 - NA_ROWS // 2, 0), GRID_H - NA_ROWS)
        lst = []
        for kt in range(rs0 // 2, (rs1 + NA_ROWS - 1) // 2 + 1):
            key = (kt - j, rs0 - r0, rs1 - r1)
            if key not in variants:
                variants[key] = len(variants)
            lst.append((kt, variants[key]))
        plan.append(lst)
    return plan, variants


NA_PLAN, NA_VARIANTS = na_plan()
NA_NVAR = len(NA_VARIANTS)


class Tok:
    __slots__ = ("sem", "key", "val", "eng")

    def __init__(self, sem, key, val, eng):
        self.sem, self.key, self.val, self.eng = sem, key, val, eng


class Buf:
    __slots__ = ("name", "last_w", "readers", "sem", "key", "cnt")

    def __init__(self, name):
        self.name = name
        self.last_w = None
        self.readers = {}
        self.sem = None
        self.key = None
        self.cnt = 0


class Eng:
    def __init__(self, name, h, sem):
        self.name, self.h, self.sem = name, h, sem
        self.key = "E_" + name
        self.cnt = 0
        self.waited = {}


class Sync:
    def __init__(self, nc, stack):
        self.nc = nc
        self.stack = stack
        self.E = {}
        for name, h in (("pe", nc.tensor), ("act", nc.scalar), ("dve", nc.vector),
                        ("pool", nc.gpsimd), ("sp", nc.sync)):
            sem = stack.enter_context(nc.semaphore("s_" + name))
            self.E[name] = Eng(name, h, sem)
        self.dma_bufs = []
        self.free_sems = []
        self.replica_groups = [[0, 1], [2, 3], [4, 5], [6, 7]]
        self.nsem = 0
        self.nwait = 0
        self.nins = 0

    def buf(self, name):
        return Buf(name)

    def bufs(self, name, n):
        return [Buf(f"{name}{i}") for i in range(n)]

    def _deps(self, eng, r, w):
        raw = []
        oth = []
        for b in r:
            if b.last_w is not None:
                raw.append(b.last_w)
        for b in w:
            if b.last_w is not None:
                oth.append(b.last_w)
            oth.extend(b.readers.values())
        toks = []
        for t in raw:
            if t.eng is eng and eng.name == "pe":
                continue
            toks.append(t)
        for t in oth:
            if t.eng is eng:
                continue
            toks.append(t)
        return toks

    def _wait(self, eng, toks):
        for t in toks:
            if eng.waited.get(t.key, 0) >= t.val:
                continue
            eng.h.wait_ge(t.sem, t.val)
            eng.waited[t.key] = t.val
            self.nwait += 1

    def op(self, en, fn, r=(), w=()):
        eng = self.E[en]
        self._wait(eng, self._deps(eng, r, w))
        ins = fn(eng.h)
        ins.then_inc(eng.sem, 1)
        eng.cnt += 1
        self.nins += 1
        tok = Tok(eng.sem, eng.key, eng.cnt, eng)
        for b in r:
            b.readers[tok.key] = tok
        for b in w:
            b.last_w = tok
            b.readers = {}
        return tok

    def dma(self, q, out, in_, r=(), w=(), sb=None):
        eng = self.E[q]
        self._wait(eng, self._deps(eng, r, w))
        if sb is None:
            sb = w[0] if w else r[0]
        if sb.sem is None:
            if self.free_sems:
                sb.sem, sb.key, sb.cnt = self.free_sems.pop()
            else:
                sb.sem = self.stack.enter_context(self.nc.semaphore(f"d{self.nsem}"))
                sb.key = f"D{self.nsem}"
                sb.cnt = 0
                self.nsem += 1
            self.dma_bufs.append(sb)
        ins = eng.h.dma_start(out=out, in_=in_)
        ins.then_inc(sb.sem, 16)
        sb.cnt += 16
        self.nins += 1
        tok = Tok(sb.sem, sb.key, sb.cnt, None)
        for b in r:
            b.readers[tok.key] = tok
        for b in w:
            b.last_w = tok
            b.readers = {}
        return tok

    def allgather_pairs(self, src_ts, dst_ts):
        self.barrier()
        eng = self.E["pool"]
        if getattr(self, "cc_sem", None) is None:
            self.cc_sem = self.stack.enter_context(self.nc.semaphore("cc_sem"))
            self.cc_cnt = 0
        for src_t, dst_t in zip(src_ts, dst_ts):
            ins = eng.h.collective_compute("AllGather", ALU.bypass, replica_groups=self.replica_groups,
                                           ins=[src_t.ap().opt()], outs=[dst_t.ap().opt()])
            ins.then_inc(self.cc_sem)
            self.cc_cnt += 1
            self.nins += 1
        tok = Tok(self.cc_sem, "CC", self.cc_cnt, None)
        for e in self.E.values():
            self._wait(e, [tok])

    def barrier(self, engines=("pe", "act", "dve", "pool", "sp")):
        toks = [Tok(e.sem, e.key, e.cnt, e) for e in self.E.values() if e.cnt > 0]
        toks += [Tok(b.sem, b.key, b.cnt, None) for b in self.dma_bufs if b.cnt > 0]
        for en in engines:
            eng = self.E[en]
            self._wait(eng, [t for t in toks if t.eng is not eng])
        for b in self.dma_bufs:
            self.free_sems.append((b.sem, b.key, b.cnt))
            b.sem = None
            b.last_w = None
            b.readers = {}
        self.dma_bufs = []


class Prog:
    def __init__(self, layers=(0, 1, 2, 3), final_out=True, qb_filter=None, tile_filter=None, n_cores=8):
        self.n_cores = n_cores
        self.layers = tuple(layers)
        self.qb_filter = qb_filter
        self.tile_filter = tile_filter
        self.nc = bass.Bass("TRN2", target_bir_lowering=False)
        self.final_out = final_out

    def din(self, name, shape, dt=F32):
        return self.nc.dram_tensor(name, list(shape), dt, kind="ExternalInput").ap()

    def dout(self, name, shape, dt=F32):
        return self.nc.dram_tensor(name, list(shape), dt, kind="ExternalOutput").ap()

    def dscr(self, name, shape, dt=F32):
        return self.nc.dram_tensor(name, list(shape), dt, kind="Internal").ap()

    def sb(self, st, name, shape, dt=F32):
        self._uid = getattr(self, "_uid", 0) + 1
        return st.enter_context(self.nc.sbuf_tensor(f"s{self._uid}_{name}", list(shape), dt))

    def ps(self, st, name, shape, dt=F32):
        self._uid = getattr(self, "_uid", 0) + 1
        return st.enter_context(self.nc.psum_tensor(f"p{self._uid}_{name}", list(shape), dt))

    def build(self):
        nc = self.nc
        with contextlib.ExitStack() as st:
            self.S = Sync(nc, st)
            self.S.replica_groups = [[2 * k, 2 * k + 1] for k in range(self.n_cores // 2)]
            self.declare()
            self.setup_consts(st)
            nl = len(self.layers)
            for li, l in enumerate(self.layers):
                src = self.x_in if li == 0 else self.xbuf[(li - 1) % 2]
                last = li == nl - 1
                dst = self.y_out if last else self.xbuf[li % 2]
                if l % 2 == 0:
                    self.even_layer(l, src, dst, last)
                else:
                    self.odd_layer(l, src, dst, last)
            self.S.barrier()
        return nc

    def declare(self):
        self.x_in = self.din("x_in", [NTOK, D])
        self.cc = self.din("cc", [128, 8, 2])
        self.ada_w = self.din("ada_w", [DEPTH, D, 3 * D])
        self.ada_b = self.din("ada_b", [DEPTH, 3 * D])
        self.ln_g = self.din("ln_g", [DEPTH, D])
        self.ln_b = self.din("ln_b", [DEPTH, D])
        self.ident_in = self.din("ident", [128, 128])
        self.sel_in = self.din("sel", [2, 256])
        self.ev_w = self.din("ev_w", [2, D, EV_COLS])
        self.ev_bcol = self.din("ev_bcol", [2, 128, 10])
        self.ev_brow = self.din("ev_brow", [2, 1, 1664])
        self.ev_lam = self.din("ev_lam", [2, 1, 256])
        self.ev_subg = self.din("ev_subg", [2, 1, 128])
        self.ev_qg = self.din("ev_qg", [2, 128, 2])
        self.ev_kvg = self.din("ev_kvg", [2, 128, 1])
        self.ev_wuq = self.din("ev_wuq", [2, 256, 768])
        self.ev_wukv = self.din("ev_wukv", [2, 128, 512])
        self.ev_wo = self.din("ev_wo", [2, D, D])
        self.rope_da = self.din("rope_da", [2, 128, NTOK])
        self.rope_ml = self.din("rope_ml", [2, 128, NTOK])
        self.od_w = self.din("od_w", [2, D, OD_COLS])
        self.od_bcol = self.din("od_bcol", [2, 128, 8])
        self.od_brow = self.din("od_brow", [2, 1, 1800])
        self.od_convw = self.din("od_convw", [2, 128, 4, 5])
        self.od_convb = self.din("od_convb", [2, 128, 4])
        self.od_fb = self.din("od_fb", [2, 1, 4])
        self.od_ng = self.din("od_ng", [2, 1, 256])
        self.od_nat = self.din("od_nat", [2, 4, 128, NA_NVAR, 128])
        self.od_wo = self.din("od_wo", [2, D, D])
        self.tri = self.din("tri", [128, 3, 128])
        self.qkpre = self.dscr("qkpre", [4, 128, NTOK])
        self.mq = self.dscr("mq", [2, 128, NTOK], BF16)
        self.mk = self.dscr("mk", [2, 128, NTOK], BF16)
        self.og = self.dscr("og", [NTOK, 256])
        self.gates = self.dscr("gates", [128, NT, 8])
        self.hfb = self.dscr("hfb", [2, NTOK, 256])
        n_last_tok = LAT if self.final_out else NTOK
        self.y_out = self.dout("y", [n_last_tok, D])
        self.xbuf = [self.dscr("xbuf0", [NTOK, D]), self.dscr("xbuf1", [NTOK, D])]
        self.qda = self.dscr("qda", [2, 128, NTOK], BF16)
        self.kda = self.dscr("kda", [2, 128, NTOK], BF16)
        self.vda = self.dscr("vda", [2, 128, NT, 129], BF16)
        self.qm = self.dscr("qm", [4, 96, NTOK], BF16)
        self.kmn = self.dscr("kmn", [4, 64, NTOK], BF16)
        self.krt = self.dscr("krt", [32, NTOK], BF16)
        self.vm = self.dscr("vm", [4, 128, NT, 65], BF16)
        self.gate = self.dscr("gate", [NTOK, D], BF16)
        self.ybuf = self.dscr("ybuf", [NTOK, D], BF16)
        self.psel = self.din("psel", [128, 2])
        self.yodd_rows = [2048, 2048, 2048, 2048, 256]
        self.yodd_t = [self.nc.dram_tensor(f"yodd{k}", [r, 512], BF16) for k, r in enumerate(self.yodd_rows)]
        self.yoddg_t = [self.nc.dram_tensor(f"yoddg{k}", [2 * r, 512], BF16) for k, r in enumerate(self.yodd_rows)]
        self.yown_t = [self.nc.dram_tensor(f"yown{k}", [1024, D], BF16) for k in range(4)]
        self.ygath_t = [self.nc.dram_tensor(f"ygath{k}", [2048, D], BF16) for k in range(4)]

    def setup_consts(self, st):
        S = self.S
        self.ident_f = self.sb(st, "ident_f", [128, 128])
        self.ident_b = self.sb(st, "ident_b", [128, 128], BF16)
        self.sel = self.sb(st, "sel", [2, 256])
        self.ccs = self.sb(st, "ccs", [128, 8, 2])
        self.zeros_b = self.sb(st, "zeros_b", [128, 128], BF16)
        self.ones_b = self.sb(st, "ones_b", [1, 128], BF16)
        self.zeros_w = self.sb(st, "zeros_w", [128, 512], BF16)
        self.B_const = S.buf("consts")
        b = self.B_const
        S.dma("sp", self.ident_f[:], self.ident_in[:, :], w=[b])
        S.dma("sp", self.sel[:], self.sel_in[:, :], w=[b])
        S.dma("sp", self.ccs[:], self.cc[:, :, :], w=[b])
        self.pselt = self.sb(st, "pselt", [128, 4])
        S.dma("sp", self.pselt[:, 0:2], self.psel[:, :], w=[b])
        S.op("dve", lambda e: e.tensor_scalar(out=self.pselt[:, 2:4], in0=self.pselt[:, 0:2], scalar1=1.0 / NA_SCALE, scalar2=None,
                                              op0=ALU.mult), r=[b], w=[b])
        S.op("dve", lambda e: e.tensor_copy(self.ident_b[:], self.ident_f[:]), r=[b], w=[b])
        S.op("dve", lambda e: e.memset(self.zeros_b[:], 0.0), w=[b])
        S.op("dve", lambda e: e.memset(self.ones_b[:], 1.0), w=[b])
        S.op("dve", lambda e: e.memset(self.zeros_w[:], 0.0), w=[b])
        S.op("act", lambda e: e.activation(out=self.ccs[:], in_=self.ccs[:], func=AF.Silu), r=[b], w=[b])
        self.sc1 = self.sb(st, "sc1", [128, 8, 2])
        self.sh = self.sb(st, "sh", [128, 8, 2])
        self.g_l = self.sb(st, "g_l", [128, D])
        self.g_c = self.sb(st, "g_c", [128, D])
        self.lng = self.sb(st, "lng", [128, D])
        self.lnb = self.sb(st, "lnb", [128, D])
        self.B_mod = S.buf("mod")

    def adaln(self, l):
        S, nc = self.S, self.nc
        S.barrier()
        with contextlib.ExitStack() as st:
            wt = [self.sb(st, f"adaw{i}", [128, 3 * D]) for i in range(2)]
            Bw = S.bufs("adaw", 2)
            mrow = self.sb(st, "mrow", [2, 3 * D])
            brow = self.sb(st, "adab", [2, 3 * D])
            Bm = S.buf("mrow")
            Bb = S.buf("adab")
            pm = [self.ps(st, f"pm{i}", [128, 512]) for i in range(8)]
            Bp = S.bufs("pm", 8)
            S.dma("sp", brow[:], self.ada_b[l:l + 1, :].to_broadcast([2, 3 * D]), w=[Bb])
            S.dma("sp", self.lng[:], self.ln_g[l:l + 1, :].to_broadcast([128, D]), w=[self.B_mod])
            S.dma("sp", self.lnb[:], self.ln_b[l:l + 1, :].to_broadcast([128, D]), w=[self.B_mod])
            for k in range(8):
                S.dma("sp", wt[k % 2][:], self.ada_w[l, k * 128:(k + 1) * 128, :], w=[Bw[k % 2]])
                for n in range(6):
                    S.op("pe", lambda e, k=k, n=n: e.matmul(
                        pm[n][0:2, :], self.ccs[:, k, :], wt[k % 2][:, n * 512:(n + 1) * 512],
                        start=(k == 0), stop=(k == 7)), r=[Bw[k % 2], self.B_const], w=[Bp[n]])
            for n in range(6):
                S.op("dve", lambda e, n=n: e.tensor_tensor(
                    out=mrow[:, n * 512:(n + 1) * 512], in0=pm[n][0:2, :], in1=brow[:, n * 512:(n + 1) * 512],
                    op=ALU.add), r=[Bp[n], Bb], w=[Bm])
            for j in range(8):
                S.op("pe", lambda e, j=j: e.transpose(
                    pm[6][:, 2 * j:2 * j + 2], mrow[0:2, j * 128:(j + 1) * 128], self.ident_f[0:2, 0:2]),
                    r=[Bm, self.B_const], w=[Bp[6]])
                S.op("pe", lambda e, j=j: e.transpose(
                    pm[7][:, 2 * j:2 * j + 2], mrow[0:2, D + j * 128:D + (j + 1) * 128], self.ident_f[0:2, 0:2]),
                    r=[Bm, self.B_const], w=[Bp[7]])
            S.op("dve", lambda e: e.tensor_copy(self.sh[:].rearrange("p a b -> p (a b)"), pm[6][:, 0:16]),
                 r=[Bp[6]], w=[self.B_mod])
            S.op("dve", lambda e: e.tensor_scalar(
                out=self.sc1[:].rearrange("p a b -> p (a b)"), in0=pm[7][:, 0:16], scalar1=1.0, scalar2=None,
                op0=ALU.add), r=[Bp[7]], w=[self.B_mod])
            for n in range(2):
                S.op("pe", lambda e, n=n: e.matmul(
                    pm[n][:, :], self.sel[:, 0:128], mrow[0:2, 2 * D + n * 512:2 * D + (n + 1) * 512],
                    start=True, stop=True), r=[Bm, self.B_const], w=[Bp[n]])
                S.op("pe", lambda e, n=n: e.matmul(
                    pm[2 + n][:, :], self.sel[:, 128:256], mrow[0:2, 2 * D + n * 512:2 * D + (n + 1) * 512],
                    start=True, stop=True), r=[Bm, self.B_const], w=[Bp[2 + n]])
                S.op("dve", lambda e, n=n: e.tensor_copy(self.g_l[:, n * 512:(n + 1) * 512], pm[n][:, :]),
                     r=[Bp[n]], w=[self.B_mod])
                S.op("dve", lambda e, n=n: e.tensor_copy(self.g_c[:, n * 512:(n + 1) * 512], pm[2 + n][:, :]),
                     r=[Bp[2 + n]], w=[self.B_mod])
            S.barrier()

    def load_cast(self, st_w, dst_fn, src_fn, nparts, ncols, pieces, Bdst, scale_fn=None, name="lc"):
        S = self.S
        with contextlib.ExitStack() as st:
            stg = [self.sb(st, f"{name}_stg{i}", [nparts, ncols]) for i in range(2)]
            Bs = S.bufs(name + "_stg", 2)
            for i, (dst, src, sc) in enumerate(pieces):
                j = i % 2
                n = src.shape[-1]
                S.dma("sp", stg[j][:, 0:n], src, w=[Bs[j]])
                en = "dve" if i % 2 == 0 else "pool"
                if sc is None:
                    S.op(en, lambda e, dst=dst, j=j, n=n: e.tensor_copy(dst, stg[j][:, 0:n]), r=[Bs[j]], w=[Bdst])
                else:
                    S.op(en, lambda e, dst=dst, j=j, n=n, sc=sc: e.tensor_scalar(
                        out=dst, in0=stg[j][:, 0:n], scalar1=sc, scalar2=None, op0=ALU.mult),
                        r=[Bs[j], Bdst], w=[Bdst])
            S.barrier()

    def even_layer(self, l, src, dst, last):
        i = l // 2
        self.adaln(l)
        self.even_project(i, src)
        self.da_attention(i, l)
        self.mla_attention(i)
        self.S.allgather_pairs(self.yodd_t, self.yoddg_t)
        self.out_stage(self.ev_wo[i], src, dst, last, ysplit="odd")

    def even_project(self, i, src):
        S, nc = self.S, self.nc
        with contextlib.ExitStack() as st:
            wb = self.sb(st, "ev_wb", [128, 8, EV_COLS], BF16)
            wuq = self.sb(st, "ev_wuqb", [128, 2, 768], BF16)
            wukv = self.sb(st, "ev_wukvb", [128, 512], BF16)
            bcol = self.sb(st, "ev_bcol", [128, 10])
            brow_f = self.sb(st, "ev_brow_f", [1, 1664])
            brow = self.sb(st, "ev_brow", [1, 1664], BF16)
            qg = self.sb(st, "ev_qg", [128, 2])
            kvg = self.sb(st, "ev_kvg", [128, 1])
            Bw = S.buf("ev_w")
            S.dma("sp", bcol[:], self.ev_bcol[i, :, :], w=[Bw])
            S.dma("sp", brow_f[:], self.ev_brow[i, :, :], w=[Bw])
            S.dma("sp", qg[:], self.ev_qg[i, :, :], w=[Bw])
            S.dma("sp", kvg[:], self.ev_kvg[i, :, :], w=[Bw])
            S.op("dve", lambda e: e.tensor_copy(brow[:], brow_f[:]), r=[Bw], w=[Bw])
            pieces = []
            for k in range(8):
                for c in range(2):
                    pieces.append((wb[:, k, c * 1376:(c + 1) * 1376],
                                   self.ev_w[i, k * 128:(k + 1) * 128, c * 1376:(c + 1) * 1376], None))
            self.load_cast(st, None, None, 128, 1536, pieces, Bw, name="evw")
            pieces = [(wuq[:, rc, :], self.ev_wuq[i, rc * 128:(rc + 1) * 128, :], qg[:, rc:rc + 1]) for rc in range(2)]
            pieces.append((wukv[:, :], self.ev_wukv[i, :, :], kvg[:, 0:1]))
            self.load_cast(st, None, None, 128, 1536, pieces, Bw, name="evw2")

            NXS = 3
            xt = [self.sb(st, f"xt{j}", [128, D]) for j in range(NXS)]
            Bx = S.bufs("xt", NXS)
            hT = [self.sb(st, f"hT{j}", [128, 8, 512], BF16) for j in range(2)]
            Bh = S.bufs("hT", 2)
            cosd = [self.sb(st, f"cosd{j}", [128, 512]) for j in range(2)]
            sind = [self.sb(st, f"sind{j}", [128, 512]) for j in range(2)]
            cosm = [self.sb(st, f"cosm{j}", [128, 512]) for j in range(2)]
            sinm = [self.sb(st, f"sinm{j}", [128, 512]) for j in range(2)]
            Brt = S.bufs("ropet", 2)
            t1 = [self.sb(st, f"t1_{j}", [128, 512]) for j in range(2)]
            t2 = [self.sb(st, f"t2_{j}", [128, 512]) for j in range(2)]
            Bt1 = S.bufs("t1", 2)
            Bt2 = S.bufs("t2", 2)
            fo = [self.sb(st, f"fo{j}", [128, 512], BF16) for j in range(4)]
            Bfo = S.bufs("fo", 4)
            va = [self.sb(st, f"va{j}", [128, 2, 129], BF16) for j in range(2)]
            Bva = S.bufs("va", 2)
            vma = [self.sb(st, f"vma{j}", [128, 4, 65], BF16) for j in range(2)]
            Bvma = S.bufs("vma", 2)
            cq = [self.sb(st, f"cq{j}", [128, 384]) for j in range(2)]
            Bcq = S.bufs("cq", 2)
            cqn = [self.sb(st, f"cqn{j}", [128, 384], BF16) for j in range(2)]
            Bcqn = S.bufs("cqn", 2)
            stat = [self.sb(st, f"stat{j}", [128, 8]) for j in range(2)]
            Bstat = S.bufs("stat", 2)
            junk = self.sb(st, "junk", [128, 256])
            Bjunk = S.buf("junk")
            cT = [self.sb(st, f"cT{j}", [128, 3, 512], BF16) for j in range(2)]
            BcT = S.bufs("cT", 2)
            gt = [self.sb(st, f"gt{j}", [128, D], BF16) for j in range(2)]
            Bgt = S.bufs("gt", 2)
            pp = [self.ps(st, f"pp{j}", [128, 512]) for j in range(7)]
            ppb = self.ps(st, "ppb", [128, 1024], BF16)
            Bpp = S.bufs("pp", 7)
            Bppb = S.buf("ppb")
            for j in range(2):
                S.op("pool", lambda e, j=j: e.memset(va[j][:], 1.0), w=[Bva[j]])
                S.op("pool", lambda e, j=j: e.memset(vma[j][:], 1.0), w=[Bvma[j]])
            eps_t = self.sb(st, "eps_t", [128, 1])
            S.op("pool", lambda e: e.memset(eps_t[:], RMS_EPS), w=[Bw])

            self._pp_i = 0

            def next_pp():
                j = self._pp_i % 7
                self._pp_i += 1
                return pp[j], Bpp[j]

            nblk = (NTOK + 511) // 512
            self._xi = 0
            self._foi = 0
            self._evac = 0

            def stage_a(blk):
                t0 = blk * 512
                ntok = min(512, NTOK - t0)
                nsub = ntok // 128
                m = 0 if t0 < LAT else 1
                hb = blk % 2
                S.dma("sp", cosd[hb][:, :ntok], self.rope_da[0, :, t0:t0 + ntok], w=[Brt[hb]])
                S.dma("sp", sind[hb][:, :ntok], self.rope_da[1, :, t0:t0 + ntok], w=[Brt[hb]])
                S.dma("sp", cosm[hb][:, :ntok], self.rope_ml[0, :, t0:t0 + ntok], w=[Brt[hb]])
                S.dma("sp", sinm[hb][:, :ntok], self.rope_ml[1, :, t0:t0 + ntok], w=[Brt[hb]])
                for s in range(nsub):
                    xj = self._xi % NXS
                    self._xi += 1
                    S.dma("sp", xt[xj][:], src[t0 + s * 128:t0 + (s + 1) * 128, :], w=[Bx[xj]])
                    for half in range(2):
                        p, Bp = next_pp()
                        for q in range(4):
                            dc = half * 4 + q
                            S.op("pe", lambda e, p=p, q=q, dc=dc, xj=xj: e.transpose(
                                p[:, q * 128:(q + 1) * 128], xt[xj][:, dc * 128:(dc + 1) * 128], self.ident_f[:]),
                                r=[Bx[xj], self.B_const], w=[Bp])
                        for q in range(4):
                            dc = half * 4 + q
                            if self._evac % 2 == 0:
                                S.op("act", lambda e, p=p, q=q, dc=dc, s=s, m=m: e.activation(
                                    out=hT[hb][:, dc, s * 128:(s + 1) * 128], in_=p[:, q * 128:(q + 1) * 128],
                                    func=AF.Identity, scale=self.sc1[:, dc, m:m + 1], bias=self.sh[:, dc, m:m + 1]),
                                    r=[Bp, self.B_mod], w=[Bh[hb]])
                            else:
                                S.op("dve", lambda e, p=p, q=q, dc=dc, s=s, m=m: e.tensor_scalar(
                                    out=hT[hb][:, dc, s * 128:(s + 1) * 128], in0=p[:, q * 128:(q + 1) * 128],
                                    scalar1=self.sc1[:, dc, m:m + 1], scalar2=self.sh[:, dc, m:m + 1],
                                    op0=ALU.mult, op1=ALU.add), r=[Bp, self.B_mod], w=[Bh[hb]])
                            self._evac += 1

            def stage_b(blk):
                t0 = blk * 512
                ntok = min(512, NTOK - t0)
                nsub = ntok // 128
                m = 0 if t0 < LAT else 1
                hb = blk % 2
                foi = self._foi
                for grp, base, dstT, bofs in ((0, EQ, self.qda, 0), (1, EK, self.kda, 4)):
                    for h in range(2):
                        pa, Ba = next_pp()
                        pb, Bb = next_pp()
                        for k in range(8):
                            S.op("pe", lambda e, pa=pa, k=k, c0=base + h * 128: e.matmul(
                                pa[:, :ntok], wb[:, k, c0:c0 + 128], hT[hb][:, k, :ntok], start=(k == 0), stop=(k == 7)),
                                r=[Bw, Bh[hb]], w=[Ba])
                        for k in range(8):
                            S.op("pe", lambda e, pb=pb, k=k, c0=base + 256 + h * 128: e.matmul(
                                pb[:, :ntok], wb[:, k, c0:c0 + 128], hT[hb][:, k, :ntok], start=(k == 0), stop=(k == 7)),
                                r=[Bw, Bh[hb]], w=[Bb])
                        tj = (grp * 2 + h) % 2
                        S.op("dve", lambda e, pa=pa, tj=tj, c=bofs + h: e.scalar_tensor_tensor(
                            out=t1[tj][:, :ntok], in0=pa[:, :ntok], scalar=bcol[:, c:c + 1], in1=cosd[hb][:, :ntok],
                            op0=ALU.add, op1=ALU.mult), r=[Ba, Brt[hb], Bw], w=[Bt1[tj]])
                        S.op("dve", lambda e, pb=pb, tj=tj, c=bofs + 2 + h: e.scalar_tensor_tensor(
                            out=t2[tj][:, :ntok], in0=pb[:, :ntok], scalar=bcol[:, c:c + 1], in1=sind[hb][:, :ntok],
                            op0=ALU.add, op1=ALU.mult), r=[Bb, Brt[hb], Bw], w=[Bt2[tj]])
                        fj = foi % 4
                        foi += 1
                        S.op("pool", lambda e, tj=tj, fj=fj: e.tensor_tensor(
                            out=fo[fj][:, :ntok], in0=t1[tj][:, :ntok], in1=t2[tj][:, :ntok], op=ALU.add),
                            r=[Bt1[tj], Bt2[tj]], w=[Bfo[fj]])
                        S.dma("pool", dstT[h, :, t0:t0 + ntok], fo[fj][:, :ntok], r=[Bfo[fj]])
                pa, Ba = next_pp()
                pb, Bb = next_pp()
                for k in range(8):
                    S.op("pe", lambda e, pa=pa, k=k: e.matmul(
                        pa[0:32, :ntok], wb[:, k, EKR:EKR + 32], hT[hb][:, k, :ntok], start=(k == 0), stop=(k == 7)),
                        r=[Bw, Bh[hb]], w=[Ba])
                for k in range(8):
                    S.op("pe", lambda e, pb=pb, k=k: e.matmul(
                        pb[0:32, :ntok], wb[:, k, EKR + 32:EKR + 64], hT[hb][:, k, :ntok], start=(k == 0), stop=(k == 7)),
                        r=[Bw, Bh[hb]], w=[Bb])
                tj = 0
                S.op("dve", lambda e, pa=pa: e.scalar_tensor_tensor(
                    out=t1[tj][0:32, :ntok], in0=pa[0:32, :ntok], scalar=bcol[0:32, 8:9], in1=cosm[hb][0:32, :ntok],
                    op0=ALU.add, op1=ALU.mult), r=[Ba, Brt[hb], Bw], w=[Bt1[tj]])
                S.op("dve", lambda e, pb=pb: e.scalar_tensor_tensor(
                    out=t2[tj][0:32, :ntok], in0=pb[0:32, :ntok], scalar=bcol[0:32, 9:10], in1=sinm[hb][0:32, :ntok],
                    op0=ALU.add, op1=ALU.mult), r=[Bb, Brt[hb], Bw], w=[Bt2[tj]])
                fj = foi % 4
                foi += 1
                S.op("pool", lambda e, fj=fj: e.tensor_tensor(
                    out=fo[fj][0:32, :ntok], in0=t1[tj][0:32, :ntok], in1=t2[tj][0:32, :ntok], op=ALU.add),
                    r=[Bt1[tj], Bt2[tj]], w=[Bfo[fj]])
                S.dma("pool", self.krt[:, t0:t0 + ntok], fo[fj][0:32, :ntok], r=[Bfo[fj]])
                cb = blk % 2
                for s in range(nsub):
                    tt = (t0 // 128) + s
                    tok0 = t0 + s * 128
                    p, Bp = next_pp()
                    for k in range(8):
                        S.op("pe", lambda e, p=p, k=k, s=s: e.matmul(
                            p[:, 0:256], hT[hb][:, k, s * 128:(s + 1) * 128], wb[:, k, EV_:EV_ + 256], start=(k == 0), stop=False),
                            r=[Bw, Bh[hb]], w=[Bp])
                    S.op("pe", lambda e, p=p: e.matmul(p[:, 0:256], self.ones_b[0:1, :], brow[0:1, 0:256], start=False, stop=True),
                         r=[Bw, self.B_const], w=[Bp])
                    vj = tt % 2
                    S.op("act", lambda e, p=p, vj=vj: e.activation(
                        out=va[vj][:, :, 0:128], in_=p[:, 0:256].rearrange("p (h c) -> p h c", h=2), func=AF.Copy),
                        r=[Bp], w=[Bva[vj]])
                    S.dma("pool", self.vda[:, :, tt, :].rearrange("h p c -> p h c"), va[vj][:], r=[Bva[vj]])
                    p, Bp = next_pp()
                    for k in range(8):
                        S.op("pe", lambda e, p=p, k=k, s=s: e.matmul(
                            p[:, 0:384], hT[hb][:, k, s * 128:(s + 1) * 128], wb[:, k, EC:EC + 384], start=(k == 0), stop=False),
                            r=[Bw, Bh[hb]], w=[Bp])
                    S.op("pe", lambda e, p=p: e.matmul(p[:, 0:384], self.ones_b[0:1, :], brow[0:1, 256:640], start=False, stop=True),
                         r=[Bw, self.B_const], w=[Bp])
                    cj = tt % 2
                    S.op("act", lambda e, p=p, cj=cj: e.activation(out=cq[cj][:], in_=p[:, 0:384], func=AF.Copy),
                         r=[Bp], w=[Bcq[cj]])
                    S.op("act", lambda e, cj=cj: e.activation(
                        out=junk[:, 0:256], in_=cq[cj][:, 0:256], func=AF.Square, accum_out=stat[cj][:, 0:1]),
                        r=[Bcq[cj]], w=[Bjunk, Bstat[cj]])
                    S.op("act", lambda e, cj=cj: e.activation(
                        out=junk[:, 0:128], in_=cq[cj][:, 256:384], func=AF.Square, accum_out=stat[cj][:, 1:2]),
                        r=[Bcq[cj]], w=[Bjunk, Bstat[cj]])
                    S.op("act", lambda e, cj=cj: e.activation(out=stat[cj][:, 2:3], in_=stat[cj][:, 0:1], func=AF.Sqrt,
                                                              scale=1.0 / 256, bias=eps_t[:, 0:1]), r=[Bstat[cj], Bw], w=[Bstat[cj]])
                    S.op("act", lambda e, cj=cj: e.activation(out=stat[cj][:, 3:4], in_=stat[cj][:, 1:2], func=AF.Sqrt,
                                                              scale=1.0 / 128, bias=eps_t[:, 0:1]), r=[Bstat[cj], Bw], w=[Bstat[cj]])
                    S.op("dve", lambda e, cj=cj: e.reciprocal(stat[cj][:, 4:6], stat[cj][:, 2:4]), r=[Bstat[cj]], w=[Bstat[cj]])
                    S.op("dve", lambda e, cj=cj: e.tensor_scalar(
                        out=cqn[cj][:, 0:256], in0=cq[cj][:, 0:256], scalar1=stat[cj][:, 4:5], scalar2=None, op0=ALU.mult),
                        r=[Bcq[cj], Bstat[cj]], w=[Bcqn[cj]])
                    S.op("pool", lambda e, cj=cj: e.tensor_scalar(
                        out=cqn[cj][:, 256:384], in0=cq[cj][:, 256:384], scalar1=stat[cj][:, 5:6], scalar2=None, op0=ALU.mult),
                        r=[Bcq[cj], Bstat[cj]], w=[Bcqn[cj]])
                    for rc in range(3):
                        S.op("pe", lambda e, rc=rc, cj=cj: e.transpose(
                            ppb[:, rc * 128:(rc + 1) * 128], cqn[cj][:, rc * 128:(rc + 1) * 128], self.ident_b[:]),
                            r=[Bcqn[cj], self.B_const], w=[Bppb])
                    S.op("dve", lambda e, s=s: e.tensor_copy(
                        cT[cb][:, :, s * 128:(s + 1) * 128], ppb[:, 0:384].rearrange("p (r t) -> p r t", r=3)),
                        r=[Bppb], w=[BcT[cb]])
                    gj = tt % 2
                    for n in range(2):
                        p, Bp = next_pp()
                        for k in range(8):
                            S.op("pe", lambda e, p=p, k=k, s=s, n=n: e.matmul(
                                p[:, :], hT[hb][:, k, s * 128:(s + 1) * 128], wb[:, k, EG + n * 512:EG + (n + 1) * 512],
                                start=(k == 0), stop=False), r=[Bw, Bh[hb]], w=[Bp])
                        S.op("pe", lambda e, p=p, n=n: e.matmul(
                            p[:, :], self.ones_b[0:1, :], brow[0:1, 640 + n * 512:640 + (n + 1) * 512], start=False, stop=True),
                            r=[Bw, self.B_const], w=[Bp])
                        S.op("act", lambda e, p=p, n=n, gj=gj: e.activation(
                            out=gt[gj][:, n * 512:(n + 1) * 512], in_=p[:, :], func=AF.Silu), r=[Bp], w=[Bgt[gj]])
                    S.dma("pool", self.gate[tok0:tok0 + 128, :], gt[gj][:], r=[Bgt[gj]])
                for h in range(4):
                    pa, Ba = next_pp()
                    pb, Bb = next_pp()
                    for rc in range(2):
                        S.op("pe", lambda e, pa=pa, rc=rc, h=h: e.matmul(
                            pa[0:96, :ntok], wuq[:, rc, h * 96:(h + 1) * 96], cT[cb][:, rc, :ntok], start=(rc == 0), stop=(rc == 1)),
                            r=[Bw, BcT[cb]], w=[Ba])
                    for rc in range(2):
                        S.op("pe", lambda e, pb=pb, rc=rc, h=h: e.matmul(
                            pb[0:96, :ntok], wuq[:, rc, 384 + h * 96:384 + (h + 1) * 96], cT[cb][:, rc, :ntok],
                            start=(rc == 0), stop=(rc == 1)), r=[Bw, BcT[cb]], w=[Bb])
                    tj = h % 2
                    fj = foi % 4
                    foi += 1
                    S.op("dve", lambda e, pa=pa, tj=tj: e.tensor_tensor(
                        out=t1[tj][64:96, :ntok], in0=pa[64:96, :ntok], in1=cosm[hb][64:96, :ntok], op=ALU.mult),
                        r=[Ba, Brt[hb]], w=[Bt1[tj]])
                    S.op("dve", lambda e, pb=pb, tj=tj: e.tensor_tensor(
                        out=t2[tj][64:96, :ntok], in0=pb[64:96, :ntok], in1=sinm[hb][64:96, :ntok], op=ALU.mult),
                        r=[Bb, Brt[hb]], w=[Bt2[tj]])
                    S.op("act", lambda e, pa=pa, fj=fj: e.activation(out=fo[fj][0:64, :ntok], in_=pa[0:64, :ntok], func=AF.Copy),
                         r=[Ba], w=[Bfo[fj]])
                    S.op("pool", lambda e, tj=tj, fj=fj: e.tensor_tensor(
                        out=fo[fj][64:96, :ntok], in0=t1[tj][64:96, :ntok], in1=t2[tj][64:96, :ntok], op=ALU.add),
                        r=[Bt1[tj], Bt2[tj]], w=[Bfo[fj]])
                    S.dma("pool", self.qm[h, :, t0:t0 + ntok], fo[fj][0:96, :ntok], r=[Bfo[fj]])
                for hp in range(2):
                    p, Bp = next_pp()
                    S.op("pe", lambda e, p=p, hp=hp: e.matmul(
                        p[:, :ntok], wukv[:, hp * 128:(hp + 1) * 128], cT[cb][:, 2, :ntok], start=True, stop=True),
                        r=[Bw, BcT[cb]], w=[Bp])
                    fj = foi % 4
                    foi += 1
                    S.op("act", lambda e, p=p, fj=fj: e.activation(out=fo[fj][:, :ntok], in_=p[:, :ntok], func=AF.Copy),
                         r=[Bp], w=[Bfo[fj]])
                    S.dma("pool", self.kmn[2 * hp, :, t0:t0 + ntok], fo[fj][0:64, :ntok], r=[Bfo[fj]])
                    S.dma("pool", self.kmn[2 * hp + 1, :, t0:t0 + ntok], fo[fj][64:128, :ntok], r=[Bfo[fj]])
                for s in range(nsub):
                    tt = (t0 // 128) + s
                    p, Bp = next_pp()
                    S.op("pe", lambda e, p=p, s=s: e.matmul(
                        p[:, 0:256], cT[cb][:, 2, s * 128:(s + 1) * 128], wukv[:, 256:512], start=True, stop=True),
                        r=[Bw, BcT[cb]], w=[Bp])
                    vj = tt % 2
                    S.op("act", lambda e, p=p, vj=vj: e.activation(
                        out=vma[vj][:, :, 0:64], in_=p[:, 0:256].rearrange("p (h c) -> p h c", h=4), func=AF.Copy),
                        r=[Bp], w=[Bvma[vj]])
                    S.dma("pool", self.vm[:, :, tt, :].rearrange("h p c -> p h c"), vma[vj][:], r=[Bvma[vj]])
                self._foi = foi

            stage_a(0)
            for blk in range(nblk):
                if blk + 1 < nblk:
                    stage_a(blk + 1)
                stage_b(blk)
            S.barrier()

    def attn_core(self, st, name, KT, QT, VA, Bkv, kparts, nv, scale, qblocks, finalize, kslices=None):
        S = self.S
        nmap = len(kparts)
        nacc_per_bank = 512 // nv
        n_acc = nmap * 4
        n_acc_banks = (n_acc + nacc_per_bank - 1) // nacc_per_bank
        n_sc = 8 - n_acc_banks
        n_sc = min(n_sc, 4)
        NP = 4
        res = getattr(self, "_attn_res", None)
        if res is None or res[0] != name:
            accb = [self.ps(st, f"{name}_acc{j}", [128, 512]) for j in range(n_acc_banks)]
            Bacc = S.bufs(name + "_acc", n_acc_banks)
            scb = [self.ps(st, f"{name}_sc{j}", [128, 512]) for j in range(n_sc)]
            Bsc = S.bufs(name + "_sc", n_sc)
            pt = [self.sb(st, f"{name}_pt{j}", [128, 512], BF16) for j in range(NP)]
            Bpt = S.bufs(name + "_pt", NP)
            self._attn_res = (name, accb, Bacc, scb, Bsc, pt, Bpt)
        _, accb, Bacc, scb, Bsc, pt, Bpt = self._attn_res

        def acc_ap(mi, sub):
            idx = mi * 4 + sub
            b = idx // nacc_per_bank
            o = (idx % nacc_per_bank) * nv
            return accb[b][:, o:o + nv], Bacc[b]

        sci = 0
        pti = 0
        for qbi, (q0, nq, ktiles) in enumerate(qblocks):
            nsub = nq // 128
            for b in range(n_acc_banks):
                S.op("pe", lambda e, b=b: e.matmul(accb[b][:, :], self.zeros_b[:, :], self.zeros_w[:, :], start=True, stop=False,
                                                   skip_group_check=True), r=[self.B_const], w=[Bacc[b]])
            units = [(kt, mi) for kt in ktiles for mi in range(nmap)]
            pend = []

            def emit_score(u):
                nonlocal sci
                kt, mi = u
                QTm, lo, hi = kparts[mi]
                j = sci % n_sc
                sci += 1
                S.op("pe", lambda e, j=j, lo=lo, hi=hi, kt=kt, QTm=QTm: e.matmul(
                    scb[j][:, :nq], KT[lo:hi, kt * 128:(kt + 1) * 128], QTm[lo:hi, q0:q0 + nq], start=True, stop=True),
                    r=[Bkv], w=[Bsc[j]])
                return j

            def emit_exp_pv(u, j, lastflag):
                nonlocal pti
                kt, mi = u
                pj = pti % NP
                pti += 1
                S.op("act", lambda e, j=j, pj=pj: e.activation(out=pt[pj][:, :nq], in_=scb[j][:, :nq], func=AF.Exp, scale=scale),
                     r=[Bsc[j]], w=[Bpt[pj]])
                for sub in range(nsub):
                    ap, Ba = acc_ap(mi, sub)
                    S.op("pe", lambda e, ap=ap, pj=pj, sub=sub, kt=kt: e.matmul(
                        ap, pt[pj][:, sub * 128:(sub + 1) * 128], VA[:, kt, :], start=False, stop=lastflag,
                        skip_group_check=True), r=[Bpt[pj], Bkv], w=[Ba])

            LOOK = min(n_sc - 1, 2)
            q = []
            for ui, u in enumerate(units):
                q.append((u, emit_score(u)))
                if len(q) > LOOK:
                    u0, j0 = q.pop(0)
                    emit_exp_pv(u0, j0, False)
            while q:
                u0, j0 = q.pop(0)
                emit_exp_pv(u0, j0, u0[0] == ktiles[-1])
            for sub in range(nsub):
                accs = [acc_ap(mi, sub) for mi in range(nmap)]
                finalize(qbi, q0, sub, accs)

    def qblocks_all(self):
        qb = []
        lat_k = list(range(NT))
        for b in range(LAT // 512):
            qb.append((b * 512, 512, lat_k))
        qb.append((LAT, CTX, [NT_LAT, NT_LAT + 1]))
        if self.qb_filter is not None:
            qb = [q for i, q in enumerate(qb) if i in self.qb_filter]
        return qb

    def qblocks_own(self):
        qb = []
        lat_k = list(range(NT))
        for b in range(LAT // 2 // 512):
            qb.append((b * 512, 512, lat_k))
        qb.append((LAT // 2, CTX, [NT_LAT, NT_LAT + 1]))
        if self.qb_filter is not None:
            qb = [q for i, q in enumerate(qb) if i in self.qb_filter]
        return qb

    def own_q(self, QO, QS, lo, hi, tmp, Bqs, Bqo, Btmp):
        S = self.S
        H = LAT // 2
        S.op("dve", lambda e: e.tensor_scalar(out=tmp[lo:hi, :], in0=QS[lo:hi, H:LAT], scalar1=self.pselt[lo:hi, 1:2], scalar2=None,
                                              op0=ALU.mult), r=[Bqs, self.B_const], w=[Btmp])
        S.op("dve", lambda e: e.scalar_tensor_tensor(out=QO[lo:hi, 0:H], in0=QS[lo:hi, 0:H], scalar=self.pselt[lo:hi, 0:1],
                                                     in1=tmp[lo:hi, :], op0=ALU.mult, op1=ALU.add),
             r=[Bqs, Btmp, self.B_const], w=[Bqo])
        S.op("act", lambda e: e.activation(out=QO[lo:hi, H:H + CTX], in_=QS[lo:hi, LAT:NTOK], func=AF.Copy), r=[Bqs], w=[Bqo])

    def blend(self, out, a, b, Ba, Bb, Bout, lo=0, hi=128, col=0):
        S = self.S
        s0 = self.pselt[lo:hi, col:col + 1]
        s1 = self.pselt[lo:hi, col + 1:col + 2]
        S.op("dve", lambda e: e.tensor_scalar(out=b, in0=b, scalar1=s1, scalar2=None, op0=ALU.mult), r=[Bb, self.B_const], w=[Bb])
        S.op("dve", lambda e: e.scalar_tensor_tensor(out=out, in0=a, scalar=s0, in1=b, op0=ALU.mult, op1=ALU.add),
             r=[Ba, Bb, self.B_const], w=[Bout])

    def yodd_loc(self, t):
        if t < NT_LAT:
            return t // 16, (t % 16) * 128
        return 4, (t - NT_LAT) * 128

    def y_dst(self, tok0, c0, c1):
        H = LAT // 2
        if tok0 < H:
            k, r = tok0 // 1024, tok0 % 1024
            return self.yown_t[k].ap()[r:r + 128, c0:c1]
        return self.ybuf[LAT + tok0 - H:LAT + tok0 - H + 128, c0:c1]

    def da_attention(self, i, l):
        S = self.S
        lam_init = 0.8 - 0.6 * math.exp(-0.3 * l)
        with contextlib.ExitStack() as st:
            KTs = [self.sb(st, f"da_KT{j}", [128, NTOK], BF16) for j in range(2)]
            VAs = [self.sb(st, f"da_VA{j}", [128, NT, 129], BF16) for j in range(2)]
            QT1s = [self.sb(st, f"da_QT1{j}", [128, NTOK], BF16) for j in range(2)]
            QT2s = [self.sb(st, f"da_QT2{j}", [128, NTOK], BF16) for j in range(2)]
            Bkvs = S.bufs("da_kv", 2)
            for j in range(2):
                S.op("pool", lambda e, j=j: e.memset(QT1s[j][64:128, :], 0.0), w=[Bkvs[j]])
                S.op("pool", lambda e, j=j: e.memset(QT2s[j][0:64, :], 0.0), w=[Bkvs[j]])
            lamt = self.sb(st, "lamt", [128, 256])
            lamw = self.sb(st, "lamw", [128, 8])
            subg = self.sb(st, "subg", [128, 128])
            Bl = S.buf("lam")
            eps_t = self.sb(st, "da_eps", [128, 1])
            S.op("pool", lambda e: e.memset(eps_t[:], RMS_EPS), w=[Bl])
            S.dma("sp", lamt[:], self.ev_lam[i, :, :].to_broadcast([128, 256]), w=[Bl])
            S.dma("sp", subg[:], self.ev_subg[i, :, :].to_broadcast([128, 128]), w=[Bl])
            junk = self.sb(st, "da_junk", [128, 128])
            Bj = S.buf("da_junk")
            S.op("dve", lambda e: e.tensor_tensor(out=junk[:, 0:64], in0=lamt[:, 0:64], in1=lamt[:, 64:128], op=ALU.mult),
                 r=[Bl], w=[Bj])
            S.op("dve", lambda e: e.reduce_sum(out=lamw[:, 0:1], in_=junk[:, 0:64], axis=AX.X), r=[Bj], w=[Bl])
            S.op("dve", lambda e: e.tensor_tensor(out=junk[:, 64:128], in0=lamt[:, 128:192], in1=lamt[:, 192:256], op=ALU.mult),
                 r=[Bl], w=[Bj])
            S.op("dve", lambda e: e.reduce_sum(out=lamw[:, 1:2], in_=junk[:, 64:128], axis=AX.X), r=[Bj], w=[Bl])
            S.op("act", lambda e: e.activation(out=lamw[:, 2:4], in_=lamw[:, 0:2], func=AF.Exp), r=[Bl], w=[Bl])
            S.op("dve", lambda e: e.tensor_tensor(out=lamw[:, 4:5], in0=lamw[:, 3:4], in1=lamw[:, 2:3], op=ALU.subtract),
                 r=[Bl], w=[Bl])
            S.op("dve", lambda e: e.tensor_scalar(out=lamw[:, 5:6], in0=lamw[:, 4:5], scalar1=-lam_init, scalar2=None, op0=ALU.add),
                 r=[Bl], w=[Bl])
            NF = 3
            rr = [self.sb(st, f"da_rr{j}", [128, 8]) for j in range(NF)]
            ta = [self.sb(st, f"da_ta{j}", [128, 128]) for j in range(NF)]
            td = [self.sb(st, f"da_td{j}", [128, 128]) for j in range(NF)]
            to = [self.sb(st, f"da_to{j}", [128, 128], BF16) for j in range(NF)]
            Bf = S.bufs("da_fin", NF)
            Bto = S.bufs("da_to", NF)
            self._fi = 0
            def da_load(h):
                j = h % 2
                S.dma("sp", KTs[j][:], self.kda[h, :, :], w=[Bkvs[j]])
                S.dma("act", QT1s[j][0:64, :], self.qda[h, 0:64, :], w=[Bkvs[j]])
                S.dma("act", QT2s[j][64:128, :], self.qda[h, 64:128, :], w=[Bkvs[j]])
                S.dma("sp", VAs[j][:], self.vda[h, :, :, :], w=[Bkvs[j]])

            da_load(0)
            for h in range(2):
                if h + 1 < 2:
                    da_load(h + 1)
                KT, VA, QT1, QT2, Bkv = KTs[h % 2], VAs[h % 2], QT1s[h % 2], QT2s[h % 2], Bkvs[h % 2]

                def fin(qbi, q0, sub, accs, h=h):
                    j = self._fi % NF
                    self._fi += 1
                    (a1, B1), (a2, B2) = accs
                    S.op("dve", lambda e: e.reciprocal(rr[j][:, 0:1], a1[:, 128:129]), r=[B1], w=[Bf[j]])
                    S.op("dve", lambda e: e.reciprocal(rr[j][:, 1:2], a2[:, 128:129]), r=[B2], w=[Bf[j]])
                    S.op("dve", lambda e: e.tensor_tensor(out=rr[j][:, 2:3], in0=rr[j][:, 1:2], in1=lamw[:, 5:6], op=ALU.mult),
                         r=[Bf[j], Bl], w=[Bf[j]])
                    S.op("dve", lambda e: e.tensor_scalar(out=ta[j][:], in0=a1[:, 0:128], scalar1=rr[j][:, 0:1], scalar2=None,
                                                          op0=ALU.mult), r=[B1, Bf[j]], w=[Bf[j]])
                    S.op("dve", lambda e: e.scalar_tensor_tensor(out=td[j][:], in0=a2[:, 0:128], scalar=rr[j][:, 2:3], in1=ta[j][:],
                                                                 op0=ALU.mult, op1=ALU.add), r=[B2, Bf[j]], w=[Bf[j]])
                    S.op("act", lambda e: e.activation(out=ta[j][:], in_=td[j][:], func=AF.Square, accum_out=rr[j][:, 3:4]),
                         r=[Bf[j]], w=[Bf[j]])
                    S.op("act", lambda e: e.activation(out=rr[j][:, 4:5], in_=rr[j][:, 3:4], func=AF.Sqrt, scale=1.0 / 128,
                                                       bias=eps_t[:, 0:1]), r=[Bf[j], Bl], w=[Bf[j]])
                    S.op("dve", lambda e: e.reciprocal(rr[j][:, 5:6], rr[j][:, 4:5]), r=[Bf[j]], w=[Bf[j]])
                    S.op("pool", lambda e: e.tensor_scalar(out=td[j][:], in0=td[j][:], scalar1=rr[j][:, 5:6], scalar2=(1.0 - lam_init),
                                                           op0=ALU.mult, op1=ALU.mult), r=[Bf[j]], w=[Bf[j]])
                    S.op("pool", lambda e: e.tensor_tensor(out=to[j][:], in0=td[j][:], in1=subg[:], op=ALU.mult),
                         r=[Bf[j], Bl], w=[Bto[j]])
                    kk, r0 = self.yodd_loc((q0 + sub * 128) // 128)
                    S.dma("pool", self.yodd_t[kk].ap()[r0:r0 + 128, h * 128:(h + 1) * 128], to[j][:], r=[Bto[j]])

                self.attn_core(st, "da", KT, None, VA, Bkv, [(QT1, 0, 128), (QT2, 0, 128)], 129, DA_SCALE,
                               self.qblocks_all(), fin)
            S.barrier()
            self._attn_res = None

    def mla_attention(self, i):
        S = self.S
        with contextlib.ExitStack() as st:
            KTs = [self.sb(st, f"ml_KT{j}", [128, NTOK], BF16) for j in range(2)]
            QTs = [self.sb(st, f"ml_QT{j}", [128, NTOK], BF16) for j in range(2)]
            VAs = [self.sb(st, f"ml_VA{j}", [128, NT, 65], BF16) for j in range(2)]
            Bkvs = S.bufs("ml_kv", 2)
            NF = 3
            rr = [self.sb(st, f"ml_rr{j}", [128, 2]) for j in range(NF)]
            to = [self.sb(st, f"ml_to{j}", [128, 64], BF16) for j in range(NF)]
            Bto = S.bufs("ml_to", NF)
            self._fi = 0
            def ml_load(h):
                j = h % 2
                S.dma("sp", KTs[j][0:64, :], self.kmn[h, :, :], w=[Bkvs[j]])
                S.dma("sp", KTs[j][64:96, :], self.krt[:, :], w=[Bkvs[j]])
                S.dma("act", QTs[j][0:96, :], self.qm[h, :, :], w=[Bkvs[j]])
                S.dma("sp", VAs[j][:], self.vm[h, :, :, :], w=[Bkvs[j]])

            ml_load(0)
            for h in range(4):
                if h + 1 < 4:
                    ml_load(h + 1)
                KT, VA, QT, Bkv = KTs[h % 2], VAs[h % 2], QTs[h % 2], Bkvs[h % 2]

                def fin(qbi, q0, sub, accs, h=h):
                    j = self._fi % NF
                    self._fi += 1
                    (a1, B1), = accs
                    S.op("dve", lambda e: e.reciprocal(rr[j][:, 0:1], a1[:, 64:65]), r=[B1], w=[Bto[j]])
                    S.op("dve", lambda e: e.tensor_scalar(out=to[j][:], in0=a1[:, 0:64], scalar1=rr[j][:, 0:1], scalar2=None,
                                                          op0=ALU.mult), r=[B1, Bto[j]], w=[Bto[j]])
                    tok0 = q0 + sub * 128
                    kk, r0 = self.yodd_loc(tok0 // 128)
                    S.dma("pool", self.yodd_t[kk].ap()[r0:r0 + 128, 256 + h * 64:256 + (h + 1) * 64], to[j][:], r=[Bto[j]])

                self.attn_core(st, "ml", KT, None, VA, Bkv, [(QT, 0, 96)], 65, MLA_SCALE, self.qblocks_all(), fin)
            S.barrier()
            self._attn_res = None

    def out_stage(self, wo_dram, src, dst, last, ysplit=False):
        S = self.S
        with contextlib.ExitStack() as st:
            wo = self.sb(st, "wo", [128, 8, D], BF16)
            Bw = S.buf("wo")
            pieces = [(wo[:, k, :], wo_dram[k * 128:(k + 1) * 128, :], None) for k in range(8)]
            self.load_cast(st, None, None, 128, D, pieces, Bw, name="wo")
            NB = 3
            yt = [self.sb(st, f"o_yt{j}", [128, D], BF16) for j in range(NB)]
            gt = [self.sb(st, f"o_gt{j}", [128, D], BF16) for j in range(NB)]
            xt = [self.sb(st, f"o_xt{j}", [128, D]) for j in range(NB)]
            yg = [self.sb(st, f"o_yg{j}", [128, D], BF16) for j in range(NB)]
            ygT = [self.sb(st, f"o_ygT{j}", [128, D], BF16) for j in range(NB)]
            rt = [self.sb(st, f"o_rt{j}", [128, D]) for j in range(NB)]
            ot = [self.sb(st, f"o_ot{j}", [128, D]) for j in range(NB)]
            stt = [self.sb(st, f"o_st{j}", [128, 16]) for j in range(NB)]
            By, Bg, Bx, Byg, BygT, Br, Bo, Bs = (S.bufs(n, NB) for n in ("o_yt", "o_gt", "o_xt", "o_yg", "o_ygT", "o_rt", "o_ot", "o_st"))
            ptr = [self.ps(st, f"o_ptr{j}", [128, D], BF16) for j in range(2)]
            Bptr = S.bufs("o_ptr", 2)
            pout = [self.ps(st, f"o_po{j}", [128, 512]) for j in range(4)]
            Bpo = S.bufs("o_po", 4)
            eps_t = self.sb(st, "o_eps", [128, 1])
            S.op("pool", lambda e: e.memset(eps_t[:], LN_EPS), w=[Bw])
            ntiles = NT_LAT if (last and self.final_out) else NT
            tiles = [t for t in range(ntiles) if self.tile_filter is None or t in self.tile_filter]

            def stage_a(t):
                j = t % NB
                tok0 = t * 128
                isctx = t >= NT_LAT
                G = self.g_c if isctx else self.g_l
                if ysplit == "odd":
                    k, r0 = self.yodd_loc(t)
                    rows = self.yodd_rows[k]
                    g = self.yoddg_t[k].ap()
                    S.dma("sp", yt[j][:, 0:256], g[r0:r0 + 128, 0:256], w=[By[j]])
                    S.dma("sp", yt[j][:, 256:512], g[rows + r0:rows + r0 + 128, 0:256], w=[By[j]])
                    S.dma("sp", yt[j][:, 512:768], g[r0:r0 + 128, 256:512], w=[By[j]])
                    S.dma("sp", yt[j][:, 768:1024], g[rows + r0:rows + r0 + 128, 256:512], w=[By[j]])
                else:
                    if ysplit and not isctx:
                        half, u = t // 32, t % 32
                        r0 = half * 1024 + (u % 8) * 128
                        ysrc = self.ygath_t[u // 8].ap()[r0:r0 + 128, :]
                    else:
                        ysrc = self.ybuf[tok0:tok0 + 128, :]
                    S.dma("sp", yt[j][:], ysrc, w=[By[j]])
                S.dma("act", gt[j][:], self.gate[tok0:tok0 + 128, :], w=[Bg[j]])
                S.dma("sp", xt[j][:], src[tok0:tok0 + 128, :], w=[Bx[j]])
                S.op("pool", lambda e, j=j: e.tensor_tensor(out=yg[j][:], in0=yt[j][:], in1=gt[j][:], op=ALU.mult),
                     r=[By[j], Bg[j]], w=[Byg[j]])
                pj = t % 2
                for ec in range(8):
                    S.op("pe", lambda e, ec=ec, j=j, pj=pj: e.transpose(
                        ptr[pj][:, ec * 128:(ec + 1) * 128], yg[j][:, ec * 128:(ec + 1) * 128], self.ident_b[:]),
                        r=[Byg[j], self.B_const], w=[Bptr[pj]])
                S.op("act", lambda e, j=j, pj=pj: e.activation(out=ygT[j][:], in_=ptr[pj][:], func=AF.Copy),
                     r=[Bptr[pj]], w=[BygT[j]])
                for n in range(2):
                    pn = (t % 2) * 2 + n
                    for ec in range(8):
                        S.op("pe", lambda e, ec=ec, n=n, pn=pn, j=j: e.matmul(
                            pout[pn][:, :], ygT[j][:, ec * 128:(ec + 1) * 128], wo[:, ec, n * 512:(n + 1) * 512],
                            start=(ec == 0), stop=(ec == 7)), r=[BygT[j], Bw], w=[Bpo[pn]])

            def stage_b(t):
                j = t % NB
                tok0 = t * 128
                isctx = t >= NT_LAT
                G = self.g_c if isctx else self.g_l
                for n in range(2):
                    pn = (t % 2) * 2 + n
                    S.op("dve", lambda e, n=n, pn=pn, j=j, G=G: e.tensor_tensor(
                        out=rt[j][:, n * 512:(n + 1) * 512], in0=pout[pn][:, :], in1=G[:, n * 512:(n + 1) * 512], op=ALU.mult),
                        r=[Bpo[pn], self.B_mod], w=[Br[j]])
                S.op("dve", lambda e, j=j: e.scalar_tensor_tensor(
                    out=rt[j][:], in0=xt[j][:], scalar=ALPHA, in1=rt[j][:], op0=ALU.mult, op1=ALU.add),
                    r=[Bx[j], Br[j]], w=[Br[j]])
                for n in range(2):
                    S.op("dve", lambda e, n=n, j=j: e.bn_stats(stt[j][:, n * 6:(n + 1) * 6], rt[j][:, n * 512:(n + 1) * 512]),
                         r=[Br[j]], w=[Bs[j]])
                S.op("dve", lambda e, j=j: e.bn_aggr(stt[j][:, 12:14], stt[j][:, 0:12]), r=[Bs[j]], w=[Bs[j]])
                S.op("act", lambda e, j=j: e.activation(out=stt[j][:, 14:15], in_=stt[j][:, 13:14], func=AF.Sqrt, scale=1.0,
                                                        bias=eps_t[:, 0:1]), r=[Bs[j], Bw], w=[Bs[j]])
                S.op("dve", lambda e, j=j: e.reciprocal(stt[j][:, 15:16], stt[j][:, 14:15]), r=[Bs[j]], w=[Bs[j]])
                S.op("dve", lambda e, j=j: e.tensor_scalar(
                    out=ot[j][:], in0=rt[j][:], scalar1=stt[j][:, 12:13], scalar2=stt[j][:, 15:16], op0=ALU.subtract, op1=ALU.mult),
                    r=[Br[j], Bs[j]], w=[Bo[j]])
                S.op("pool", lambda e, j=j: e.tensor_tensor(out=ot[j][:], in0=ot[j][:], in1=self.lng[:], op=ALU.mult),
                     r=[Bo[j], self.B_mod], w=[Bo[j]])
                S.op("pool", lambda e, j=j: e.tensor_tensor(out=ot[j][:], in0=ot[j][:], in1=self.lnb[:], op=ALU.add),
                     r=[Bo[j], self.B_mod], w=[Bo[j]])
                S.dma("pool", dst[tok0:tok0 + 128, :], ot[j][:], r=[Bo[j]])

            for idx, t in enumerate(tiles):
                if idx == 0:
                    stage_a(t)
                if idx + 1 < len(tiles):
                    stage_a(tiles[idx + 1])
                stage_b(t)
            S.barrier()

    def odd_layer(self, l, src, dst, last):
        i = l // 2
        self.adaln(l)
        self.odd_project(i, src)
        self.odd_conv(i)
        self.mlstm(i)
        self.mlstm_post(i)
        self.na_attention(i, last)
        self.S.allgather_pairs(self.yodd_t, self.yoddg_t)
        self.out_stage(self.od_wo[i], src, dst, last, ysplit="odd")

    def odd_project(self, i, src):
        S = self.S
        with contextlib.ExitStack() as st:
            wb = self.sb(st, "od_wb", [128, 8, OD_COLS], BF16)
            bcol = self.sb(st, "od_bcol", [128, 8])
            brow_f = self.sb(st, "od_brow_f", [1, 1800])
            brow = self.sb(st, "od_brow", [1, 1800], BF16)
            fb = self.sb(st, "od_fb", [128, 4])
            Bw = S.buf("od_w")
            S.dma("sp", bcol[:], self.od_bcol[i, :, :], w=[Bw])
            S.dma("sp", brow_f[:], self.od_brow[i, :, :], w=[Bw])
            S.dma("sp", fb[:], self.od_fb[i, :, :].to_broadcast([128, 4]), w=[Bw])
            S.op("dve", lambda e: e.tensor_copy(brow[:], brow_f[:]), r=[Bw], w=[Bw])
            pieces = []
            for k in range(8):
                for c in range(2):
                    pieces.append((wb[:, k, c * 1412:(c + 1) * 1412],
                                   self.od_w[i, k * 128:(k + 1) * 128, c * 1412:(c + 1) * 1412], None))
            self.load_cast(st, None, None, 128, 1412, pieces, Bw, name="odw")
            NXS = 3
            xt = [self.sb(st, f"xt{j}", [128, D]) for j in range(NXS)]
            Bx = S.bufs("xt", NXS)
            hT = [self.sb(st, f"hT{j}", [128, 8, 512], BF16) for j in range(2)]
            Bh = S.bufs("hT", 2)
            ff = [self.sb(st, f"ff{j}", [128, 512]) for j in range(3)]
            Bff = S.bufs("ff", 3)
            fo = [self.sb(st, f"fo{j}", [128, 512], BF16) for j in range(3)]
            Bfo = S.bufs("fo", 3)
            va = [self.sb(st, f"va{j}", [128, 2, 129], BF16) for j in range(2)]
            Bva = S.bufs("va", 2)
            vna = [self.sb(st, f"vna{j}", [128, 4, 65], BF16) for j in range(2)]
            Bvna = S.bufs("vna", 2)
            ot = [self.sb(st, f"ot{j}", [128, 256]) for j in range(2)]
            Bot = S.bufs("ot", 2)
            gt = [self.sb(st, f"gt{j}", [128, D], BF16) for j in range(2)]
            Bgt = S.bufs("gt", 2)
            gs = [self.sb(st, f"gs{j}", [128, 4, 8]) for j in range(2)]
            gtmp = [self.sb(st, f"gtmp{j}", [128, 4, 4]) for j in range(2)]
            Bgs = S.bufs("gs", 2)
            pp = [self.ps(st, f"pp{j}", [128, 512]) for j in range(7)]
            pg = self.ps(st, "pg", [128, 512])
            Bpp = S.bufs("pp", 7)
            Bpg = S.buf("pg")
            for j in range(2):
                S.op("pool", lambda e, j=j: e.memset(va[j][:], 1.0), w=[Bva[j]])
                S.op("pool", lambda e, j=j: e.memset(vna[j][:], 1.0), w=[Bvna[j]])
            self._pp_i = 0

            def next_pp():
                j = self._pp_i % 7
                self._pp_i += 1
                return pp[j], Bpp[j]

            nblk = (NTOK + 511) // 512
            xi = 0
            evac = 0
            ffi = 0
            foi = 0
            for blk in range(nblk):
                t0 = blk * 512
                ntok = min(512, NTOK - t0)
                nsub = ntok // 128
                m = 0 if t0 < LAT else 1
                hb = blk % 2
                for s in range(nsub):
                    xj = xi % NXS
                    xi += 1
                    S.dma("sp", xt[xj][:], src[t0 + s * 128:t0 + (s + 1) * 128, :], w=[Bx[xj]])
                    for half in range(2):
                        p, Bp = next_pp()
                        for q in range(4):
                            dc = half * 4 + q
                            S.op("pe", lambda e, p=p, q=q, dc=dc, xj=xj: e.transpose(
                                p[:, q * 128:(q + 1) * 128], xt[xj][:, dc * 128:(dc + 1) * 128], self.ident_f[:]),
                                r=[Bx[xj], self.B_const], w=[Bp])
                        for q in range(4):
                            dc = half * 4 + q
                            if evac % 2 == 0:
                                S.op("act", lambda e, p=p, q=q, dc=dc, s=s, m=m: e.activation(
                                    out=hT[hb][:, dc, s * 128:(s + 1) * 128], in_=p[:, q * 128:(q + 1) * 128],
                                    func=AF.Identity, scale=self.sc1[:, dc, m:m + 1], bias=self.sh[:, dc, m:m + 1]),
                                    r=[Bp, self.B_mod], w=[Bh[hb]])
                            else:
                                S.op("dve", lambda e, p=p, q=q, dc=dc, s=s, m=m: e.tensor_scalar(
                                    out=hT[hb][:, dc, s * 128:(s + 1) * 128], in0=p[:, q * 128:(q + 1) * 128],
                                    scalar1=self.sc1[:, dc, m:m + 1], scalar2=self.sh[:, dc, m:m + 1],
                                    op0=ALU.mult, op1=ALU.add), r=[Bp, self.B_mod], w=[Bh[hb]])
                            evac += 1
                for c in range(8):
                    p, Bp = next_pp()
                    for k in range(8):
                        S.op("pe", lambda e, p=p, k=k, c=c: e.matmul(
                            p[:, :ntok], wb[:, k, c * 128:(c + 1) * 128], hT[hb][:, k, :ntok], start=(k == 0), stop=(k == 7)),
                            r=[Bw, Bh[hb]], w=[Bp])
                    if c < 4:
                        fj = ffi % 3
                        ffi += 1
                        if c % 2 == 0:
                            S.op("act", lambda e, p=p, c=c, fj=fj: e.activation(
                                out=ff[fj][:, :ntok], in_=p[:, :ntok], func=AF.Identity, bias=bcol[:, c:c + 1]),
                                r=[Bp, Bw], w=[Bff[fj]])
                        else:
                            S.op("dve", lambda e, p=p, c=c, fj=fj: e.tensor_scalar(
                                out=ff[fj][:, :ntok], in0=p[:, :ntok], scalar1=bcol[:, c:c + 1], scalar2=None, op0=ALU.add),
                                r=[Bp, Bw], w=[Bff[fj]])
                        S.dma("pool", self.qkpre[c, :, t0:t0 + ntok], ff[fj][:, :ntok], r=[Bff[fj]])
                    else:
                        fj = foi % 3
                        foi += 1
                        if c % 2 == 0:
                            S.op("act", lambda e, p=p, c=c, fj=fj: e.activation(
                                out=fo[fj][:, :ntok], in_=p[:, :ntok], func=AF.Identity, bias=bcol[:, c:c + 1]),
                                r=[Bp, Bw], w=[Bfo[fj]])
                        else:
                            S.op("dve", lambda e, p=p, c=c, fj=fj: e.tensor_scalar(
                                out=fo[fj][:, :ntok], in0=p[:, :ntok], scalar1=bcol[:, c:c + 1], scalar2=None, op0=ALU.add),
                                r=[Bp, Bw], w=[Bfo[fj]])
                        dd = self.qda if c < 6 else self.kda
                        S.dma("pool", dd[(c - 4) % 2, :, t0:t0 + ntok], fo[fj][:, :ntok], r=[Bfo[fj]])
                gb = blk % 2
                for s in range(nsub):
                    tt = (t0 // 128) + s
                    tok0 = t0 + s * 128

                    def tm(p, c0, n, b0, s=s):
                        for k in range(8):
                            S.op("pe", lambda e, k=k: e.matmul(
                                p[:, 0:n], hT[hb][:, k, s * 128:(s + 1) * 128], wb[:, k, c0:c0 + n], start=(k == 0), stop=False),
                                r=[Bw, Bh[hb]], w=[Bp])
                        S.op("pe", lambda e: e.matmul(p[:, 0:n], self.ones_b[0:1, :], brow[0:1, b0:b0 + n], start=False, stop=True),
                             r=[Bw, self.B_const], w=[Bp])
                    p, Bp = next_pp()
                    tm(p, OV, 256, 0)
                    vj = tt % 2
                    S.op("act", lambda e, p=p, vj=vj: e.activation(
                        out=va[vj][:, :, 0:128], in_=p[:, 0:256].rearrange("p (h c) -> p h c", h=2), func=AF.Copy),
                        r=[Bp], w=[Bva[vj]])
                    S.dma("pool", self.vda[:, :, tt, :].rearrange("h p c -> p h c"), va[vj][:], r=[Bva[vj]])
                    p, Bp = next_pp()
                    tm(p, OO, 256, 256)
                    oj = tt % 2
                    S.op("act", lambda e, p=p, oj=oj: e.activation(out=ot[oj][:], in_=p[:, 0:256], func=AF.Sigmoid), r=[Bp], w=[Bot[oj]])
                    S.dma("pool", self.og[tok0:tok0 + 128, :], ot[oj][:], r=[Bot[oj]])
                    Bp = Bpg
                    for k in range(8):
                        S.op("pe", lambda e, k=k, s=s: e.matmul(
                            pg[:, s * 8:(s + 1) * 8], hT[hb][:, k, s * 128:(s + 1) * 128], wb[:, k, OGT:OGT + 8],
                            start=(k == 0), stop=False), r=[Bw, Bh[hb]], w=[Bpg])
                    S.op("pe", lambda e, s=s: e.matmul(pg[:, s * 8:(s + 1) * 8], self.ones_b[0:1, :], brow[0:1, 512:520],
                                                       start=False, stop=True), r=[Bw, self.B_const], w=[Bpg])
                    p, Bp = next_pp()
                    tm(p, OVN, 256, 520)
                    vj = tt % 2
                    S.op("act", lambda e, p=p, vj=vj: e.activation(
                        out=vna[vj][:, :, 0:64], in_=p[:, 0:256].rearrange("p (h c) -> p h c", h=4), func=AF.Copy),
                        r=[Bp], w=[Bvna[vj]])
                    S.dma("pool", self.vm[:, :, tt, :].rearrange("h p c -> p h c"), vna[vj][:], r=[Bvna[vj]])
                    gj = tt % 2
                    for n in range(2):
                        p, Bp = next_pp()
                        tm(p, OG + n * 512, 512, 776 + n * 512)
                        S.op("act", lambda e, p=p, n=n, gj=gj: e.activation(
                            out=gt[gj][:, n * 512:(n + 1) * 512], in_=p[:, :], func=AF.Silu), r=[Bp], w=[Bgt[gj]])
                    S.dma("pool", self.gate[tok0:tok0 + 128, :], gt[gj][:], r=[Bgt[gj]])
                pgv = pg[:, 0:nsub * 8].rearrange("p (s c) -> p s c", c=8)
                S.op("dve", lambda e, pgv=pgv: e.tensor_copy(gs[gb][:, 0:nsub, 0:4], pgv[:, :, 0:4]), r=[Bpg], w=[Bgs[gb]])
                for s in range(nsub):
                    S.op("dve", lambda e, s=s: e.tensor_tensor(out=gtmp[gb][:, s, :], in0=pg[:, s * 8 + 4:s * 8 + 8], in1=fb[:, :],
                                                               op=ALU.add), r=[Bpg, Bw], w=[Bgs[gb]])
                S.op("act", lambda e: e.activation(out=gtmp[gb][:, 0:nsub, :], in_=gtmp[gb][:, 0:nsub, :], func=AF.Exp, scale=-1.0),
                     r=[Bgs[gb]], w=[Bgs[gb]])
                S.op("act", lambda e: e.activation(out=gtmp[gb][:, 0:nsub, :], in_=gtmp[gb][:, 0:nsub, :], func=AF.Ln, bias=1.0),
                     r=[Bgs[gb]], w=[Bgs[gb]])
                S.op("dve", lambda e: e.tensor_scalar(out=gs[gb][:, 0:nsub, 4:8], in0=gtmp[gb][:, 0:nsub, :], scalar1=-1.0,
                                                      scalar2=None, op0=ALU.mult), r=[Bgs[gb]], w=[Bgs[gb]])
                tt0 = t0 // 128
                S.dma("pool", self.gates[:, tt0:tt0 + nsub, :], gs[gb][:, 0:nsub, :], r=[Bgs[gb]])
            S.barrier()

    def odd_conv(self, i):
        S = self.S
        SEG = 2048
        with contextlib.ExitStack() as st:
            cw = self.sb(st, "cv_w", [128, 4, 5])
            cb = self.sb(st, "cv_b", [128, 4])
            Bw = S.buf("cv_w")
            S.dma("sp", cw[:], self.od_convw[i, :, :, :], w=[Bw])
            S.dma("sp", cb[:], self.od_convb[i, :, :], w=[Bw])
            xin = [self.sb(st, f"cv_x{j}", [128, SEG + 4]) for j in range(2)]
            Bxin = S.bufs("cv_x", 2)
            acc = [self.sb(st, f"cv_a{j}", [128, SEG]) for j in range(2)]
            Bacc = S.bufs("cv_a", 2)
            tmp = [self.sb(st, f"cv_t{j}", [128, SEG]) for j in range(2)]
            Btmp = S.bufs("cv_t", 2)
            outb = [self.sb(st, f"cv_o{j}", [128, SEG], BF16) for j in range(2)]
            Bout = S.bufs("cv_o", 2)
            segs = [(a, SEG, 0, LAT) for a in range(0, LAT, SEG)] + [(LAT, CTX, LAT, LAT + CTX)]
            it = 0
            for c in range(4):
                for (t0, n, lo, hi) in segs:
                    j = it % 2
                    it += 1
                    a = max(t0 - 2, lo)
                    b = min(t0 + n + 2, hi)
                    if a > t0 - 2:
                        S.op("pool", lambda e, j=j: e.memset(xin[j][:, 0:2], 0.0), w=[Bxin[j]])
                    if b < t0 + n + 2:
                        S.op("pool", lambda e, j=j, n=n: e.memset(xin[j][:, n + 2:n + 4], 0.0), w=[Bxin[j]])
                    S.dma("sp", xin[j][:, a - (t0 - 2):b - (t0 - 2)], self.qkpre[c, :, a:b], w=[Bxin[j]])
                    S.op("act", lambda e, j=j, n=n, c=c: e.activation(
                        out=acc[j][:, 0:n], in_=xin[j][:, 0:n], func=AF.Copy, scale=cw[:, c, 0:1]),
                        r=[Bxin[j], Bw], w=[Bacc[j]])
                    for k in range(1, 5):
                        S.op("dve", lambda e, j=j, n=n, c=c, k=k: e.scalar_tensor_tensor(
                            out=acc[j][:, 0:n], in0=xin[j][:, k:k + n], scalar=cw[:, c, k:k + 1], in1=acc[j][:, 0:n],
                            op0=ALU.mult, op1=ALU.add), r=[Bxin[j], Bw, Bacc[j]], w=[Bacc[j]])
                    if True:
                        S.op("act", lambda e, j=j, n=n, c=c: e.activation(
                            out=outb[j][:, 0:n], in_=acc[j][:, 0:n], func=AF.Silu, bias=cb[:, c:c + 1]),
                            r=[Bacc[j], Bw], w=[Bout[j]])
                        dd = self.mq if c < 2 else self.mk
                        S.dma("pool", dd[c % 2, :, t0:t0 + n], outb[j][:, 0:n], r=[Bout[j]])
                    else:
                        S.op("act", lambda e, j=j, n=n, c=c: e.activation(
                            out=tmp[j][:, 0:n], in_=acc[j][:, 0:n], func=AF.Silu, bias=cb[:, c:c + 1]),
                            r=[Bacc[j], Bw], w=[Btmp[j]])
                        S.op("pool", lambda e, j=j, n=n: e.tensor_scalar(
                            out=outb[j][:, 0:n], in0=tmp[j][:, 0:n], scalar1=128 ** -0.5, scalar2=None, op0=ALU.mult),
                            r=[Btmp[j]], w=[Bout[j]])
                        S.dma("sp", self.mk[c - 4, :, t0:t0 + n], outb[j][:, 0:n], r=[Bout[j]])
            S.barrier()

    def mlstm(self, i):
        S = self.S
        with contextlib.ExitStack() as st:
            tri = self.sb(st, "ml_tri", [128, 3, 128])
            G = self.sb(st, "ml_G", [128, NT, 8])
            Ao = self.sb(st, "ml_Ao", [128, NT, 4])
            A2o = self.sb(st, "ml_A2o", [128, NT, 4])
            Bqo = self.sb(st, "ml_Bqo", [128, NT, 4])
            EBo = self.sb(st, "ml_EBo", [128, NT, 4])
            Bg = S.buf("ml_g")
            S.dma("sp", tri[:], self.tri[:, :, :], w=[Bg])
            S.dma("sp", G[:], self.gates[:, :, :], w=[Bg])
            lnks = self.sb(st, "ml_lnks", [128, 1])
            S.op("pool", lambda e: e.memset(lnks[:], math.log(128 ** -0.5)), w=[Bg])
            with contextlib.ExitStack() as st1:
                pg = [self.ps(st1, f"ml_pg{j}", [128, 512]) for j in range(3)]
                Bpg = S.bufs("ml_pg", 3)
                tmpg = self.sb(st1, "ml_tmpg", [128, 32, 4])
                Btg = S.buf("ml_tmpg")
                grp = 0
                for g0 in range(0, NT, 32):
                    g1 = min(g0 + 32, NT)
                    pj = grp % 3
                    grp += 1
                    for t in range(g0, g1):
                        o = (t - g0) * 8
                        S.op("pe", lambda e, t=t, o=o, pj=pj: e.matmul(pg[pj][:, o:o + 2], tri[:, 0, :], G[:, t, 4:6], start=True, stop=True),
                             r=[Bg], w=[Bpg[pj]])
                        S.op("pe", lambda e, t=t, o=o, pj=pj: e.matmul(pg[pj][:, o + 2:o + 4], tri[:, 1, :], G[:, t, 6:8], start=True, stop=True),
                             r=[Bg], w=[Bpg[pj]])
                        S.op("pe", lambda e, t=t, o=o, pj=pj: e.matmul(pg[pj][:, o + 4:o + 8], tri[:, 2, :], G[:, t, 4:8], start=True, stop=True),
                             r=[Bg], w=[Bpg[pj]])
                    n = g1 - g0
                    pv = pg[pj][:, 0:n * 8].rearrange("p (t c) -> p t c", c=8)
                    S.op("dve", lambda e, pv=pv, g0=g0, g1=g1, n=n: e.tensor_tensor(
                        out=tmpg[:, 0:n, :], in0=G[:, g0:g1, 0:4], in1=pv[:, :, 0:4], op=ALU.subtract), r=[Bg, Bpg[pj]], w=[Btg])
                    S.op("act", lambda e, g0=g0, g1=g1, n=n: e.activation(out=Ao[:, g0:g1, :], in_=tmpg[:, 0:n, :], func=AF.Exp,
                                                                          bias=lnks[:, 0:1]), r=[Btg, Bg], w=[Bg])
                    S.op("act", lambda e, pv=pv, g0=g0, g1=g1: e.activation(out=Bqo[:, g0:g1, :], in_=pv[:, :, 0:4], func=AF.Exp),
                         r=[Bpg[pj]], w=[Bg])
                    S.op("act", lambda e, pv=pv, g0=g0, g1=g1: e.activation(out=EBo[:, g0:g1, :], in_=pv[:, :, 4:8], func=AF.Exp),
                         r=[Bpg[pj]], w=[Bg])
                    S.op("dve", lambda e, g0=g0, g1=g1: e.tensor_tensor(out=A2o[:, g0:g1, :], in0=Ao[:, g0:g1, :], in1=EBo[:, g0:g1, :],
                                                                        op=ALU.mult), r=[Bg], w=[Bg])
                S.barrier()
            qT = self.sb(st, "ml_qT", [128, NTOK], BF16)
            kT = self.sb(st, "ml_kT", [128, NTOK], BF16)
            V = self.sb(st, "ml_V", [128, NT, 129], BF16)
            KTOK = self.sb(st, "ml_KTOK", [128, NT, 128], BF16)
            Bin = S.buf("ml_in")
            Bkt = S.buf("ml_ktok")
            Cn = [self.sb(st, f"ml_Cn{d}", [128, 129]) for d in range(2)]
            Cnb = [[self.sb(st, f"ml_Cnb{d}{j}", [128, 129], BF16) for j in range(2)] for d in range(2)]
            BCn = S.bufs("ml_Cn", 2)
            BCnb = [S.bufs(f"ml_Cnb{d}", 2) for d in range(2)]
            NW = 6
            W = [self.sb(st, f"ml_W{j}", [128, 128], BF16) for j in range(NW)]
            BW = S.bufs("ml_W", NW)
            v2 = [self.sb(st, f"ml_v2{j}", [128, 129], BF16) for j in range(NW)]
            Bv2 = S.bufs("ml_v2", NW)
            ho = [self.sb(st, f"ml_ho{j}", [128, 128]) for j in range(NW)]
            Bho = S.bufs("ml_ho", NW)
            sm = [self.sb(st, f"ml_sm{j}", [128, 4]) for j in range(NW)]
            Bsm = S.bufs("ml_sm", NW)
            NS, NKV, NN = 2, 2, 4
            order = [[NT_LAT, NT_LAT + 1] + list(range(NT_LAT)), [NT_LAT + 1, NT_LAT] + list(range(NT_LAT - 1, -1, -1))]
            wi = 0
            for h in range(2):
                S.dma("sp", qT[:], self.mq[h, :, :], w=[Bin])
                S.dma("act", kT[:], self.mk[h, :, :], w=[Bin])
                S.dma("sp", V[:], self.vda[h, :, :, :], w=[Bin])
                with contextlib.ExitStack() as stp:
                    ppbs = [self.ps(stp, f"ml_ppb{j}", [128, 1024], BF16) for j in range(2)]
                    Bppbs = S.bufs("ml_ppb", 2)
                    for gi, t0 in enumerate(range(0, NT, 8)):
                        n = min(8, NT - t0)
                        ppb, Bppb = ppbs[gi % 2], Bppbs[gi % 2]
                        for q in range(n):
                            t = t0 + q
                            S.op("pe", lambda e, t=t, q=q, ppb=ppb: e.transpose(
                                ppb[:, q * 128:(q + 1) * 128], kT[:, t * 128:(t + 1) * 128], self.ident_b[:]),
                                r=[Bin, self.B_const], w=[Bppb])
                        S.op("act", lambda e, t0=t0, n=n, ppb=ppb: e.activation(
                            out=KTOK[:, t0:t0 + n, :], in_=ppb[:, 0:n * 128].rearrange("p (t c) -> p t c", c=128), func=AF.Copy),
                            r=[Bppb], w=[Bkt])
                    S.barrier()
                stq = contextlib.ExitStack()
                ps_s = [self.ps(stq, f"ml_pss{j}", [128, 512]) for j in range(NS)]
                ps_kv = [self.ps(stq, f"ml_pkv{j}", [128, 512]) for j in range(NKV)]
                ps_n = [self.ps(stq, f"ml_pn{j}", [128, 512]) for j in range(NN)]
                Bps_s, Bps_kv, Bps_n = S.bufs("ml_pss", NS), S.bufs("ml_pkv", NKV), S.bufs("ml_pn", NN)
                for d in range(2):
                    S.op("pool", lambda e, d=d: e.memset(Cn[d][:], 0.0), w=[BCn[d]])
                    S.op("pool", lambda e, d=d: e.memset(Cnb[d][0][:], 0.0), w=[BCnb[d][0]])

                def stage1(step, d, wi):
                    t = order[d][step]
                    hd = d * 2 + h
                    j = wi % NW
                    p3 = wi % NS
                    tsl = slice(t * 128, (t + 1) * 128)
                    S.op("pe", lambda e: e.matmul(ps_s[p3][:, 0:128], kT[:, tsl], qT[:, tsl], start=True, stop=True),
                         r=[Bin], w=[Bps_s[p3]])
                    S.op("dve", lambda e: e.scalar_tensor_tensor(
                        out=W[j][:], in0=ps_s[p3][:, 0:128], scalar=Ao[:, t, hd:hd + 1], in1=tri[:, d, :],
                        op0=ALU.mult, op1=ALU.mult), r=[Bps_s[p3], Bg], w=[BW[j]])
                    S.op("act", lambda e: e.activation(
                        out=v2[j][:], in_=V[:, t, :], func=AF.Copy, scale=A2o[:, t, hd:hd + 1]),
                        r=[Bin, Bg], w=[Bv2[j]])

                def stage2(step, d, wi):
                    t = order[d][step]
                    hd = d * 2 + h
                    cur, nxt = step % 2, (step + 1) % 2
                    j = wi % NW
                    p3 = wi % NKV
                    p6 = wi % NN
                    tsl = slice(t * 128, (t + 1) * 128)
                    S.op("pe", lambda e: e.matmul(ps_kv[p3][:, 0:129], KTOK[:, t, :], v2[j][:], start=True, stop=True),
                         r=[Bkt, Bv2[j]], w=[Bps_kv[p3]])
                    S.op("pe", lambda e: e.matmul(ps_n[p6][:, 0:129], W[j][:], V[:, t, :], start=True, stop=False),
                         r=[BW[j], Bin], w=[Bps_n[p6]])
                    S.op("pe", lambda e: e.matmul(ps_n[p6][:, 0:129], qT[:, tsl], Cnb[d][cur][:], start=False, stop=True),
                         r=[Bin, BCnb[d][cur]], w=[Bps_n[p6]])
                    S.op("dve", lambda e: e.scalar_tensor_tensor(
                        out=Cn[d][:], in0=Cn[d][:], scalar=EBo[:, t, hd:hd + 1], in1=ps_kv[p3][:, 0:129],
                        op0=ALU.mult, op1=ALU.add), r=[BCn[d], Bps_kv[p3], Bg], w=[BCn[d]])
                    S.op("act", lambda e: e.activation(out=Cnb[d][nxt][:], in_=Cn[d][:], func=AF.Copy),
                         r=[BCn[d]], w=[BCnb[d][nxt]])

                def back(step, d, wi):
                    t = order[d][step]
                    hd = d * 2 + h
                    j = wi % NW
                    p6 = wi % NN
                    S.op("dve", lambda e: e.tensor_tensor(
                        out=sm[j][:, 0:1], in0=ps_n[p6][:, 128:129], in1=Bqo[:, t, hd:hd + 1], op=ALU.mult),
                        r=[Bps_n[p6], Bg], w=[Bsm[j]])
                    S.op("dve", lambda e: e.scalar_tensor_tensor(out=sm[j][:, 1:2], in0=sm[j][:, 0:1], scalar=-1.0,
                                                                 in1=sm[j][:, 0:1], op0=ALU.mult, op1=ALU.max),
                         r=[Bsm[j]], w=[Bsm[j]])
                    S.op("dve", lambda e: e.tensor_scalar(out=sm[j][:, 1:2], in0=sm[j][:, 1:2], scalar1=1.0, scalar2=None,
                                                          op0=ALU.max), r=[Bsm[j]], w=[Bsm[j]])
                    S.op("dve", lambda e: e.reciprocal(sm[j][:, 2:3], sm[j][:, 1:2]), r=[Bsm[j]], w=[Bsm[j]])
                    S.op("dve", lambda e: e.tensor_tensor(
                        out=sm[j][:, 3:4], in0=sm[j][:, 2:3], in1=Bqo[:, t, hd:hd + 1], op=ALU.mult), r=[Bsm[j], Bg], w=[Bsm[j]])
                    S.op("dve", lambda e: e.tensor_scalar(
                        out=ho[j][:], in0=ps_n[p6][:, 0:128], scalar1=sm[j][:, 3:4], scalar2=None, op0=ALU.mult),
                        r=[Bps_n[p6], Bsm[j]], w=[Bho[j]])
                    S.dma("sp", self.hfb[d, t * 128:(t + 1) * 128, h * 128:(h + 1) * 128], ho[j][:], r=[Bho[j]])

                items = []
                for step in range(NT):
                    for d in range(2):
                        items.append((step, d, wi))
                        wi += 1
                npair = len(items) // 2
                for k in range(npair + 2):
                    if k < npair:
                        stage1(*items[2 * k])
                        stage1(*items[2 * k + 1])
                    if 0 <= k - 1 < npair:
                        stage2(*items[2 * (k - 1)])
                        stage2(*items[2 * (k - 1) + 1])
                    if 0 <= k - 2 < npair:
                        back(*items[2 * (k - 2)])
                        back(*items[2 * (k - 2) + 1])
                S.barrier()
                stq.close()
            S.barrier()

    def mlstm_post(self, i):
        S = self.S
        with contextlib.ExitStack() as st:
            ngo = self.sb(st, "mp_ngo", [128, 256])
            Bw = S.buf("mp_w")
            S.dma("sp", ngo[:], self.od_ng[i, :, :].to_broadcast([128, 256]), w=[Bw])
            eps_t = self.sb(st, "mp_eps", [128, 1])
            S.op("pool", lambda e: e.memset(eps_t[:], LN_EPS), w=[Bw])
            NB = 3
            hf = [self.sb(st, f"mp_hf{j}", [128, 256]) for j in range(NB)]
            hb = [self.sb(st, f"mp_hb{j}", [128, 256]) for j in range(NB)]
            ogo = [self.sb(st, f"mp_ogo{j}", [128, 256]) for j in range(NB)]
            hs = [self.sb(st, f"mp_hs{j}", [128, 256]) for j in range(NB)]
            yo = [self.sb(st, f"mp_yo{j}", [128, 256]) for j in range(NB)]
            yob = [self.sb(st, f"mp_yob{j}", [128, 256], BF16) for j in range(NB)]
            stt = [self.sb(st, f"mp_st{j}", [128, 2, 12]) for j in range(NB)]
            Bhf, Bhb, Bogo, Bhs, Byo, Bst = (S.bufs(n, NB) for n in ("mp_hf", "mp_hb", "mp_ogo", "mp_hs", "mp_yo", "mp_st"))

            def post_a(t):
                j = t % NB
                tok0 = t * 128
                S.dma("sp", hf[j][:], self.hfb[0, tok0:tok0 + 128, 0:256], w=[Bhf[j]])
                S.dma("act", hb[j][:], self.hfb[1, tok0:tok0 + 128, 0:256], w=[Bhb[j]])
                S.dma("pool", ogo[j][:], self.og[tok0:tok0 + 128, :], w=[Bogo[j]])
                S.op("pool", lambda e, j=j: e.tensor_tensor(out=hs[j][:], in0=hf[j][:], in1=hb[j][:], op=ALU.add),
                     r=[Bhf[j], Bhb[j]], w=[Bhs[j]])
                S.op("pool", lambda e, j=j: e.tensor_tensor(out=ogo[j][:], in0=ogo[j][:], in1=ngo[:], op=ALU.mult),
                     r=[Bogo[j], Bw], w=[Bogo[j]])

            def post_b(t):
                j = t % NB
                for h in range(2):
                    S.op("dve", lambda e, j=j, h=h: e.bn_stats(stt[j][:, h, 0:6], hs[j][:, h * 128:(h + 1) * 128]),
                         r=[Bhs[j]], w=[Bst[j]])
                    S.op("dve", lambda e, j=j, h=h: e.bn_aggr(stt[j][:, h, 6:8], stt[j][:, h, 0:6]), r=[Bst[j]], w=[Bst[j]])
                S.op("act", lambda e, j=j: e.activation(out=stt[j][:, :, 8:9], in_=stt[j][:, :, 7:8], func=AF.Sqrt, scale=1.0,
                                                        bias=eps_t[:, 0:1]), r=[Bst[j], Bw], w=[Bst[j]])
                S.op("dve", lambda e, j=j: e.reciprocal(stt[j][:, :, 9:10], stt[j][:, :, 8:9]), r=[Bst[j]], w=[Bst[j]])
                for h in range(2):
                    S.op("dve", lambda e, j=j, h=h: e.tensor_scalar(
                        out=yo[j][:, h * 128:(h + 1) * 128], in0=hs[j][:, h * 128:(h + 1) * 128], scalar1=stt[j][:, h, 6:7],
                        scalar2=stt[j][:, h, 9:10], op0=ALU.subtract, op1=ALU.mult), r=[Bhs[j], Bst[j]], w=[Byo[j]])
                S.op("pool", lambda e, j=j: e.tensor_tensor(out=yob[j][:], in0=yo[j][:], in1=ogo[j][:], op=ALU.mult),
                     r=[Byo[j], Bogo[j]], w=[Byo[j]])
                k, r0 = self.yodd_loc(t)
                S.dma("sp", self.yodd_t[k].ap()[r0:r0 + 128, 0:256], yob[j][:], r=[Byo[j]])

            for t in range(NT):
                if t == 0:
                    post_a(0)
                if t + 1 < NT:
                    post_a(t + 1)
                post_b(t)
            S.barrier()

    def na_attention(self, i, last):
        S = self.S
        with contextlib.ExitStack() as st:
            QNe = self.sb(st, "na_Qe", [128, NTOK], BF16)
            QNo = self.sb(st, "na_Qo", [128, NTOK], BF16)
            KN = self.sb(st, "na_K", [128, NTOK], BF16)
            VN = self.sb(st, "na_V", [128, NT, 65], BF16)
            MB = self.sb(st, "na_MB", [128, NA_NVAR, 128], BF16)
            Bqk = S.buf("na_qk")
            Bv = S.buf("na_v")
            Bmb = S.buf("na_mb")
            stg = [self.sb(st, f"na_stg{j}", [128, 7, 128]) for j in range(2)]
            Bstg = S.bufs("na_stg", 2)
            NP = 3
            ptA = [self.sb(st, f"na_ptA{j}", [128, 512], BF16) for j in range(NP)]
            ptB = [self.sb(st, f"na_ptB{j}", [128, 384], BF16) for j in range(NP)]
            BptA, BptB = S.bufs("na_ptA", NP), S.bufs("na_ptB", NP)
            rr = [self.sb(st, f"na_rr{j}", [128, 2]) for j in range(NP)]
            to = [self.sb(st, f"na_to{j}", [128, 64], BF16) for j in range(NP)]
            Bto = S.bufs("na_to", NP)
            psA = [self.ps(st, f"na_psA{j}", [128, 512]) for j in range(2)]
            psB = [self.ps(st, f"na_psB{j}", [128, 512]) for j in range(2)]
            acc = [self.ps(st, f"na_acc{j}", [128, 512]) for j in range(2)]
            BpsA, BpsB, Bacc = S.bufs("na_psA", 2), S.bufs("na_psB", 2), S.bufs("na_acc", 2)
            it = 0
            qtiles = list(range(NT_LAT)) + ([] if (last and self.final_out) else [NT_LAT, NT_LAT + 1])
            if self.tile_filter is not None:
                qtiles = [t for t in qtiles if t in self.tile_filter]
            for h in range(4):
                lo = (h % 2) * 64
                if h == 0:
                    S.op("pool", lambda e: e.memset(QNe[64:128, :], 0.0), w=[Bqk])
                    S.op("pool", lambda e: e.memset(QNo[0:64, :], 0.0), w=[Bqk])
                if h % 2 == 0:
                    hp = h // 2
                    S.dma("sp", QNe[0:64, :], self.qda[hp, 0:64, :], w=[Bqk])
                    S.dma("act", QNo[64:128, :], self.qda[hp, 64:128, :], w=[Bqk])
                    S.dma("sp", KN[:], self.kda[hp, :, :], w=[Bqk])
                QN = QNe if h % 2 == 0 else QNo
                S.dma("act", VN[:], self.vm[h, :, :, :], w=[Bv])
                for v0 in range(0, NA_NVAR, 7):
                    n = min(7, NA_NVAR - v0)
                    sj = (v0 // 7) % 2
                    S.dma("sp", stg[sj][:, 0:n, :], self.od_nat[i, h, :, v0:v0 + n, :], w=[Bstg[sj]])
                    S.op("dve", lambda e, sj=sj, n=n, v0=v0: e.tensor_scalar(
                        out=MB[:, v0:v0 + n, :], in0=stg[sj][:, 0:n, :], scalar1=1.0 / NA_SCALE, scalar2=None, op0=ALU.mult),
                        r=[Bstg[sj]], w=[Bmb])
                for j in qtiles:
                    pj = it % 2
                    tj = it % NP
                    it += 1
                    qsl = slice(j * 128, (j + 1) * 128)
                    if j < NT_LAT:
                        slots = [(kt, var) for (kt, var) in NA_PLAN[j]] + [(NT_LAT, None), (NT_LAT + 1, None)]
                    else:
                        slots = [(NT_LAT, None), (NT_LAT + 1, None)]
                    nA = min(4, len(slots))
                    nB = len(slots) - nA
                    for si, (kt, var) in enumerate(slots):
                        if si < 4:
                            dst, Bd = psA[pj][:, si * 128:(si + 1) * 128], BpsA[pj]
                        else:
                            dst, Bd = psB[pj][:, (si - 4) * 128:(si - 3) * 128], BpsB[pj]
                        S.op("pe", lambda e, dst=dst, kt=kt, var=var, QN=QN: e.matmul(
                            dst, KN[:, kt * 128:(kt + 1) * 128], QN[:, qsl], start=True, stop=(var is None)),
                            r=[Bqk], w=[Bd])
                        if var is not None:
                            S.op("pe", lambda e, dst=dst, var=var: e.matmul(dst, self.ident_b[:], MB[:, var, :], start=False, stop=True),
                                 r=[Bmb, self.B_const], w=[Bd])
                    S.op("act", lambda e, pj=pj, tj=tj, nA=nA: e.activation(out=ptA[tj][:, 0:nA * 128], in_=psA[pj][:, 0:nA * 128],
                                                                            func=AF.Exp, scale=NA_SCALE), r=[BpsA[pj]], w=[BptA[tj]])
                    if nB > 0:
                        S.op("act", lambda e, pj=pj, tj=tj, nB=nB: e.activation(out=ptB[tj][:, 0:nB * 128], in_=psB[pj][:, 0:nB * 128],
                                                                                func=AF.Exp, scale=NA_SCALE), r=[BpsB[pj]], w=[BptB[tj]])
                    for si, (kt, var) in enumerate(slots):
                        if si < 4:
                            lhs, Bl = ptA[tj][:, si * 128:(si + 1) * 128], BptA[tj]
                        else:
                            lhs, Bl = ptB[tj][:, (si - 4) * 128:(si - 3) * 128], BptB[tj]
                        S.op("pe", lambda e, lhs=lhs, kt=kt, si=si, pj=pj: e.matmul(
                            acc[pj][:, 0:65], lhs, VN[:, kt, :], start=(si == 0), stop=(si == len(slots) - 1)),
                            r=[Bl, Bv], w=[Bacc[pj]])
                    S.op("dve", lambda e, pj=pj, tj=tj: e.reciprocal(rr[tj][:, 0:1], acc[pj][:, 64:65]), r=[Bacc[pj]], w=[Bto[tj]])
                    S.op("dve", lambda e, pj=pj, tj=tj: e.tensor_scalar(out=to[tj][:], in0=acc[pj][:, 0:64], scalar1=rr[tj][:, 0:1],
                                                                        scalar2=None, op0=ALU.mult), r=[Bacc[pj], Bto[tj]], w=[Bto[tj]])
                    kk, r0 = self.yodd_loc(j)
                    S.dma("pool", self.yodd_t[kk].ap()[r0:r0 + 128, 256 + h * 64:256 + (h + 1) * 64], to[tj][:], r=[Bto[tj]])
            S.barrier()


def _swap64(cols):
    return np.concatenate([cols[32:64], cols[0:32]])


def _rope_tables():
    t = np.arange(LAT)
    row = (t // GRID_W).astype(np.float32)
    col = (t % GRID_W).astype(np.float32)

    def tab(dim, nrows_pattern):
        n_freq = dim // 4
        freqs = (10000.0 ** (-np.arange(n_freq, dtype=np.float32) / n_freq)).astype(np.float32)
        ang = np.concatenate([row[:, None] * freqs, col[:, None] * freqs], axis=-1).astype(np.float32)
        cos = np.cos(ang).astype(np.float32).T
        sin = np.sin(ang).astype(np.float32).T
        half = dim // 2
        c = np.concatenate([cos, cos], 0)
        s = np.concatenate([-sin, sin], 0)
        c = np.concatenate([c, np.ones((dim, CTX), np.float32)], 1)
        s = np.concatenate([s, np.zeros((dim, CTX), np.float32)], 1)
        return c, s

    c64, s64 = tab(64, None)
    c32, s32 = tab(32, None)
    rope_da = np.stack([np.concatenate([c64, c64], 0), np.concatenate([s64, s64], 0)]).astype(np.float32)
    ml_c = np.ones((128, NTOK), np.float32)
    ml_s = np.zeros((128, NTOK), np.float32)
    ml_c[0:32] = c32
    ml_s[0:32] = s32
    ml_c[64:96] = c32
    ml_s[64:96] = s32
    rope_ml = np.stack([ml_c, ml_s]).astype(np.float32)
    return rope_da, rope_ml


def _even_layout(inp, par):
    ev_w_in, ev_b_in = inp["ev_w_in"], inp["ev_b_in"]
    hs = [2 * par, 2 * par + 1]
    ms = [4 * par + k for k in range(4)]
    o_q1, o_q2, o_k1, o_k2, o_v, o_cq, o_ckv, o_kr, o_g = 0, 256, 512, 768, 1024, 1536, 1792, 1920, 1952
    cols = []
    for (a, b) in ((o_q1, o_q2), (o_k1, o_k2)):
        main = []
        sw = []
        for h in hs:
            c1 = np.arange(a + h * 64, a + (h + 1) * 64)
            c2 = np.arange(b + h * 64, b + (h + 1) * 64)
            main += [c1, c2]
            sw += [_swap64(c1), _swap64(c2)]
        cols += main + sw
    kr = np.arange(o_kr, o_kr + 32)
    cols += [kr, np.concatenate([kr[16:], kr[:16]])]
    cols += [np.arange(o_v + h * 128, o_v + (h + 1) * 128) for h in hs]
    cols += [np.arange(o_cq, o_cq + 384), np.arange(o_g, o_g + 1024)]
    perm = np.concatenate(cols)
    assert perm.shape[0] == EV_COLS
    ev_w = np.ascontiguousarray(ev_w_in[:, :, perm])
    bp = ev_b_in[:, perm]
    bcol = np.zeros((2, 128, 10), np.float32)
    for g in range(8):
        bcol[:, :, g] = bp[:, g * 128:(g + 1) * 128]
    bcol[:, 0:32, 8] = bp[:, EKR:EKR + 32]
    bcol[:, 0:32, 9] = bp[:, EKR + 32:EKR + 64]
    brow = np.ascontiguousarray(bp[:, EV_:EV_ + 1664])[:, None, :]
    wuq = inp["mla_w_uq"]
    mainc = []
    swc = []
    for h in ms:
        c = np.arange(h * 96, h * 96 + 96)
        r = c[64:96]
        mainc.append(c)
        swc.append(np.concatenate([c[0:64], r[16:], r[:16]]))
    ev_wuq = np.ascontiguousarray(wuq[:, :, np.concatenate(mainc + swc)])
    wukv = inp["mla_w_ukv"]
    nope = np.concatenate([np.arange(h * 128, h * 128 + 64) for h in ms])
    vv = np.concatenate([np.arange(h * 128 + 64, h * 128 + 128) for h in ms])
    ev_wukv = np.ascontiguousarray(wukv[:, :, np.concatenate([nope, vv])])
    return dict(
        ev_w=ev_w, ev_bcol=bcol, ev_brow=np.ascontiguousarray(brow),
        ev_lam=np.ascontiguousarray(inp["da_lambda"].reshape(2, 1, 256)),
        ev_subg=np.ascontiguousarray(inp["da_subln_g"].reshape(2, 1, 128)),
        ev_qg=np.ascontiguousarray(inp["mla_q_norm_g"].reshape(2, 2, 128).transpose(0, 2, 1)),
        ev_kvg=np.ascontiguousarray(inp["mla_kv_norm_g"].reshape(2, 128, 1)),
        ev_wuq=ev_wuq, ev_wukv=ev_wukv, ev_wo=np.ascontiguousarray(inp["ev_w_out"]),
    )


def _odd_layout(inp, par):
    w, b = inp["od_w_in"], inp["od_b_in"]
    hs = [2 * par, 2 * par + 1]
    ns = [4 * par + k for k in range(4)]
    g0 = 2048
    cols = [np.arange(h * 128, (h + 1) * 128) for h in hs]
    cols += [np.arange(512 + h * 128, 512 + (h + 1) * 128) for h in hs]
    cols += [np.arange(2064 + n * 64, 2064 + (n + 1) * 64) for n in ns]
    cols += [np.arange(2576 + n * 64, 2576 + (n + 1) * 64) for n in ns]
    cols += [np.arange(1024 + h * 128, 1024 + (h + 1) * 128) for h in hs]
    cols += [np.arange(1536 + h * 128, 1536 + (h + 1) * 128) for h in hs]
    cols += [np.array([g0 + j * 4 + h for h in hs]) for j in (0, 2, 1, 3)]
    cols += [np.arange(3088 + n * 64, 3088 + (n + 1) * 64) for n in ns]
    cols += [np.arange(3600, 4624)]
    perm = np.concatenate(cols)
    assert perm.shape[0] == OD_COLS
    od_w = np.ascontiguousarray(w[:, :, perm])
    bp = b[:, perm]
    bcol = np.ascontiguousarray(bp[:, 0:1024].reshape(2, 8, 128).transpose(0, 2, 1))
    brow = np.ascontiguousarray(bp[:, 1024:])[:, None, :]
    ch = np.concatenate([np.arange(h * 128, (h + 1) * 128) for h in hs] + [np.arange(512 + h * 128, 512 + (h + 1) * 128) for h in hs])
    convw = np.ascontiguousarray(inp["ml_conv_w"][:, :, ch].reshape(2, 5, 4, 128).transpose(0, 3, 2, 1))
    convb = np.ascontiguousarray(inp["ml_conv_b"][:, ch].reshape(2, 4, 128).transpose(0, 2, 1))
    fb = np.ascontiguousarray(inp["ml_f_bias"][:, :, hs].reshape(2, 1, 4))
    ng = np.ascontiguousarray(np.concatenate([inp["ml_norm_g"][:, h * 128:(h + 1) * 128] for h in hs], 1).reshape(2, 1, 256))
    rpb = inp["na_rpb"][:, ns]
    nat = np.full((2, 4, 128, NA_NVAR, 128), NEG, np.float32)
    k = np.arange(128)
    krl, kc = k // 64, k % 64
    q = np.arange(128)
    qrl, qc = q // 64, q % 64
    cs = np.clip(qc - NA_COLS // 2, 0, GRID_W - NA_COLS)
    for (dk, o0, o1), vid in NA_VARIANTS.items():
        rel = (2 * dk + krl[:, None]) - qrl[None, :]
        off = np.where(qrl[None, :] == 0, o0, o1)
        valid_r = (rel >= off) & (rel <= off + NA_ROWS - 1)
        valid_c = (kc[:, None] >= cs[None, :]) & (kc[:, None] <= cs[None, :] + NA_COLS - 1)
        valid = valid_r & valid_c
        ridx = np.clip(rel + NA_ROWS - 1, 0, 2 * NA_ROWS - 2)
        cidx = np.clip(kc[:, None] - qc[None, :] + NA_COLS - 1, 0, 2 * NA_COLS - 2)
        tab = rpb[:, :, ridx, cidx]
        nat[:, :, :, vid, :] = np.where(valid[None, None], tab, np.float32(NEG))
    return dict(od_w=od_w, od_bcol=bcol, od_brow=np.ascontiguousarray(brow), od_convw=convw, od_convb=convb, od_fb=fb,
                od_ng=ng, od_nat=nat, od_wo=np.ascontiguousarray(inp["od_w_out"]))


def make_in_maps(inp, batches):
    inp = {k: np.asarray(v, dtype=np.float32) for k, v in inp.items()}
    rope_da, rope_ml = _rope_tables()
    common = dict(
        ada_w=inp["ada_w"], ada_b=inp["ada_b"], ln_g=inp["ln_g"], ln_b=inp["ln_b"],
        ident=np.eye(128, dtype=np.float32),
        sel=np.concatenate([np.stack([np.ones(128), np.zeros(128)]), np.stack([np.zeros(128), np.ones(128)])], 1).astype(np.float32),
        rope_da=rope_da, rope_ml=rope_ml,
    )
    tri = np.stack([np.triu(np.ones((128, 128), np.float32)), np.tril(np.ones((128, 128), np.float32)),
                    np.ones((128, 128), np.float32)], 1)
    common["tri"] = np.ascontiguousarray(tri)
    ev = [dict(_even_layout(inp, p), **_odd_layout(inp, p)) for p in (0, 1)]
    maps = []
    for b in batches:
        m = dict(common)
        m.update(ev[len(maps) % 2])
        m["x_in"] = np.ascontiguousarray(np.concatenate([inp["x"][b], inp["ctx"][b]], 0))
        cc = np.stack([inp["c"][b], inp["c_ctx"]], -1)
        m["cc"] = np.ascontiguousarray(cc.reshape(8, 128, 2).transpose(1, 0, 2))
        par = len(maps) % 2
        ps = np.zeros((128, 2), np.float32)
        ps[:, par] = 1.0
        m["psel"] = ps
        maps.append(m)
    return maps


_PROG_CACHE = {}


def kernel(**inputs):
    batches = [c // 2 for c in range(8)]
    in_maps = make_in_maps(inputs, batches)
    prog = Prog()
    nc = prog.build()
    res = run_bass_kernel_spmd(nc, in_maps, core_ids=list(range(8)))
    out = np.stack([res.results[2 * b]["y"] for b in range(4)], 0)
    return out.astype(np.float32)
```

```python
import contextlib
import math
import numpy as np
import concourse.bass as bass
import concourse.mybir as mybir
from concourse.bass_utils import run_bass_kernel_spmd

F32 = mybir.dt.float32
BF16 = mybir.dt.bfloat16
AF = mybir.ActivationFunctionType
ALU = mybir.AluOpType
AX = mybir.AxisListType

D = 1024
LAT = 8192
CTX = 256
NTOK = LAT + CTX
NT = NTOK // 128
NT_LAT = LAT // 128
DEPTH = 4
GRID_W = 64
ALPHA = (2.0 * DEPTH) ** 0.25
LN_EPS = 1e-5
RMS_EPS = 1e-6
DA_SCALE = 64 ** -0.5
MLA_SCALE = 96 ** -0.5
NA_SCALE = 64 ** -0.5
EV_COLS = 2752
EQ, EK, EKR, EV_, EC, EG = 0, 512, 1024, 1088, 1344, 1728
OD_COLS = 2824
OQK, OQN, OKN, OV, OO, OGT, OVN, OG = 0, 512, 768, 1024, 1280, 1536, 1544, 1800
NA_ROWS, NA_COLS, GRID_H = 8, 16, 128
NEG = -30000.0


def na_plan():
    variants = {}
    plan = []
    for j in range(NT_LAT):
        r0, r1 = 2 * j, 2 * j + 1
        rs0 = min(max(r0 - NA_ROWS // 2, 0), GRID_H - NA_ROWS)
        rs1 = min(max(r1 - NA_ROWS // 2, 0), GRID_H - NA_ROWS)
        lst = []
        for kt in range(rs0 // 2, (rs1 + NA_ROWS - 1) // 2 + 1):
            key = (kt - j, rs0 - r0, rs1 - r1)
            if key not in variants:
                variants[key] = len(variants)
            lst.append((kt, variants[key]))
        plan.append(lst)
    return plan, variants


NA_PLAN, NA_VARIANTS = na_plan()
NA_NVAR = len(NA_VARIANTS)


class Tok:
    __slots__ = ("sem", "key", "val", "eng")

    def __init__(self, sem, key, val, eng):
        self.sem, self.key, self.val, self.eng = sem, key, val, eng


class Buf:
    __slots__ = ("name", "last_w", "readers", "sem", "key", "cnt")

    def __init__(self, name):
        self.name = name
        self.last_w = None
        self.readers = {}
        self.sem = None
        self.key = None
        self.cnt = 0


class Eng:
    def __init__(self, name, h, sem):
        self.name, self.h, self.sem = name, h, sem
        self.key = "E_" + name
        self.cnt = 0
        self.waited = {}


class Sync:
    def __init__(self, nc, stack):
        self.nc = nc
        self.stack = stack
        self.E = {}
        for name, h in (("pe", nc.tensor), ("act", nc.scalar), ("dve", nc.vector),
                        ("pool", nc.gpsimd), ("sp", nc.sync)):
            sem = stack.enter_context(nc.semaphore("s_" + name))
            self.E[name] = Eng(name, h, sem)
        self.dma_bufs = []
        self.free_sems = []
        self.replica_groups = [[0, 1], [2, 3], [4, 5], [6, 7]]
        self.nsem = 0
        self.nwait = 0
        self.nins = 0

    def buf(self, name):
        return Buf(name)

    def bufs(self, name, n):
        return [Buf(f"{name}{i}") for i in range(n)]

    def _deps(self, eng, r, w):
        raw = []
        oth = []
        for b in r:
            if b.last_w is not None:
                raw.append(b.last_w)
        for b in w:
            if b.last_w is not None:
                oth.append(b.last_w)
            oth.extend(b.readers.values())
        toks = []
        for t in raw:
            if t.eng is eng and eng.name == "pe":
                continue
            toks.append(t)
        for t in oth:
            if t.eng is eng:
                continue
            toks.append(t)
        return toks

    def _wait(self, eng, toks):
        for t in toks:
            if eng.waited.get(t.key, 0) >= t.val:
                continue
            eng.h.wait_ge(t.sem, t.val)
            eng.waited[t.key] = t.val
            self.nwait += 1

    def op(self, en, fn, r=(), w=()):
        eng = self.E[en]
        self._wait(eng, self._deps(eng, r, w))
        ins = fn(eng.h)
        ins.then_inc(eng.sem, 1)
        eng.cnt += 1
        self.nins += 1
        tok = Tok(eng.sem, eng.key, eng.cnt, eng)
        for b in r:
            b.readers[tok.key] = tok
        for b in w:
            b.last_w = tok
            b.readers = {}
        return tok

    def dma(self, q, out, in_, r=(), w=(), sb=None):
        eng = self.E[q]
        self._wait(eng, self._deps(eng, r, w))
        if sb is None:
            sb = w[0] if w else r[0]
        if sb.sem is None:
            if self.free_sems:
                sb.sem, sb.key, sb.cnt = self.free_sems.pop()
            else:
                sb.sem = self.stack.enter_context(self.nc.semaphore(f"d{self.nsem}"))
                sb.key = f"D{self.nsem}"
                sb.cnt = 0
                self.nsem += 1
            self.dma_bufs.append(sb)
        ins = eng.h.dma_start(out=out, in_=in_)
        ins.then_inc(sb.sem, 16)
        sb.cnt += 16
        self.nins += 1
        tok = Tok(sb.sem, sb.key, sb.cnt, None)
        for b in r:
            b.readers[tok.key] = tok
        for b in w:
            b.last_w = tok
            b.readers = {}
        return tok

    def allgather_pairs(self, src_ts, dst_ts):
        self.barrier()
        eng = self.E["pool"]
        if getattr(self, "cc_sem", None) is None:
            self.cc_sem = self.stack.enter_context(self.nc.semaphore("cc_sem"))
            self.cc_cnt = 0
        for src_t, dst_t in zip(src_ts, dst_ts):
            ins = eng.h.collective_compute("AllGather", ALU.bypass, replica_groups=self.replica_groups,
                                           ins=[src_t.ap().opt()], outs=[dst_t.ap().opt()])
            ins.then_inc(self.cc_sem)
            self.cc_cnt += 1
            self.nins += 1
        tok = Tok(self.cc_sem, "CC", self.cc_cnt, None)
        for e in self.E.values():
            self._wait(e, [tok])

    def barrier(self, engines=("pe", "act", "dve", "pool", "sp")):
        toks = [Tok(e.sem, e.key, e.cnt, e) for e in self.E.values() if e.cnt > 0]
        toks += [Tok(b.sem, b.key, b.cnt, None) for b in self.dma_bufs if b.cnt > 0]
        for en in engines:
            eng = self.E[en]
            self._wait(eng, [t for t in toks if t.eng is not eng])
        for b in self.dma_bufs:
            self.free_sems.append((b.sem, b.key, b.cnt))
            b.sem = None
            b.last_w = None
            b.readers = {}
        self.dma_bufs = []


class Prog:
    def __init__(self, layers=(0, 1, 2, 3), final_out=True, qb_filter=None, tile_filter=None, n_cores=8):
        self.n_cores = n_cores
        self.layers = tuple(layers)
        self.qb_filter = qb_filter
        self.tile_filter = tile_filter
        self.nc = bass.Bass("TRN2", target_bir_lowering=False)
        self.final_out = final_out

    def din(self, name, shape, dt=F32):
        return self.nc.dram_tensor(name, list(shape), dt, kind="ExternalInput").ap()

    def dout(self, name, shape, dt=F32):
        return self.nc.dram_tensor(name, list(shape), dt, kind="ExternalOutput").ap()

    def dscr(self, name, shape, dt=F32):
        return self.nc.dram_tensor(name, list(shape), dt, kind="Internal").ap()

    def sb(self, st, name, shape, dt=F32):
        self._uid = getattr(self, "_uid", 0) + 1
        return st.enter_context(self.nc.sbuf_tensor(f"s{self._uid}_{name}", list(shape), dt))

    def ps(self, st, name, shape, dt=F32):
        self._uid = getattr(self, "_uid", 0) + 1
        return st.enter_context(self.nc.psum_tensor(f"p{self._uid}_{name}", list(shape), dt))

    def build(self):
        nc = self.nc
        with contextlib.ExitStack() as st:
            self.S = Sync(nc, st)
            self.S.replica_groups = [[2 * k, 2 * k + 1] for k in range(self.n_cores // 2)]
            self.declare()
            self.setup_consts(st)
            nl = len(self.layers)
            for li, l in enumerate(self.layers):
                src = self.x_in if li == 0 else self.xbuf[(li - 1) % 2]
                last = li == nl - 1
                dst = self.y_out if last else self.xbuf[li % 2]
                if l % 2 == 0:
                    self.even_layer(l, src, dst, last)
                else:
                    self.odd_layer(l, src, dst, last)
            self.S.barrier()
        return nc

    def declare(self):
        self.x_in = self.din("x_in", [NTOK, D])
        self.cc = self.din("cc", [128, 8, 2])
        self.ada_w = self.din("ada_w", [DEPTH, D, 3 * D])
        self.ada_b = self.din("ada_b", [DEPTH, 3 * D])
        self.ln_g = self.din("ln_g", [DEPTH, D])
        self.ln_b = self.din("ln_b", [DEPTH, D])
        self.ident_in = self.din("ident", [128, 128])
        self.sel_in = self.din("sel", [2, 256])
        self.ev_w = self.din("ev_w", [2, D, EV_COLS])
        self.ev_bcol = self.din("ev_bcol", [2, 128, 10])
        self.ev_brow = self.din("ev_brow", [2, 1, 1664])
        self.ev_lam = self.din("ev_lam", [2, 1, 256])
        self.ev_subg = self.din("ev_subg", [2, 1, 128])
        self.ev_qg = self.din("ev_qg", [2, 128, 2])
        self.ev_kvg = self.din("ev_kvg", [2, 128, 1])
        self.ev_wuq = self.din("ev_wuq", [2, 256, 768])
        self.ev_wukv = self.din("ev_wukv", [2, 128, 512])
        self.ev_wo = self.din("ev_wo", [2, D, D])
        self.rope_da = self.din("rope_da", [2, 128, NTOK])
        self.rope_ml = self.din("rope_ml", [2, 128, NTOK])
        self.od_w = self.din("od_w", [2, D, OD_COLS])
        self.od_bcol = self.din("od_bcol", [2, 128, 8])
        self.od_brow = self.din("od_brow", [2, 1, 1800])
        self.od_convw = self.din("od_convw", [2, 128, 4, 5])
        self.od_convb = self.din("od_convb", [2, 128, 4])
        self.od_fb = self.din("od_fb", [2, 1, 4])
        self.od_ng = self.din("od_ng", [2, 1, 256])
        self.od_nat = self.din("od_nat", [2, 4, 128, NA_NVAR, 128])
        self.od_wo = self.din("od_wo", [2, D, D])
        self.tri = self.din("tri", [128, 3, 128])
        self.qkpre = self.dscr("qkpre", [4, 128, NTOK])
        self.mq = self.dscr("mq", [2, 128, NTOK], BF16)
        self.mk = self.dscr("mk", [2, 128, NTOK], BF16)
        self.og = self.dscr("og", [NTOK, 256])
        self.gates = self.dscr("gates", [128, NT, 8])
        self.hfb = self.dscr("hfb", [2, NTOK, 256])
        n_last_tok = LAT if self.final_out else NTOK
        self.y_out = self.dout("y", [n_last_tok, D])
        self.xbuf = [self.dscr("xbuf0", [NTOK, D]), self.dscr("xbuf1", [NTOK, D])]
        self.qda = self.dscr("qda", [2, 128, NTOK], BF16)
        self.kda = self.dscr("kda", [2, 128, NTOK], BF16)
        self.vda = self.dscr("vda", [2, 128, NT, 129], BF16)
        self.qm = self.dscr("qm", [4, 96, NTOK], BF16)
        self.kmn = self.dscr("kmn", [4, 64, NTOK], BF16)
        self.krt = self.dscr("krt", [32, NTOK], BF16)
        self.vm = self.dscr("vm", [4, 128, NT, 65], BF16)
        self.gate = self.dscr("gate", [NTOK, D], BF16)
        self.ybuf = self.dscr("ybuf", [NTOK, D], BF16)
        self.psel = self.din("psel", [128, 2])
        self.yodd_rows = [2048, 2048, 2048, 2048, 256]
        self.yodd_t = [self.nc.dram_tensor(f"yodd{k}", [r, 512], BF16) for k, r in enumerate(self.yodd_rows)]
        self.yoddg_t = [self.nc.dram_tensor(f"yoddg{k}", [2 * r, 512], BF16) for k, r in enumerate(self.yodd_rows)]
        self.yown_t = [self.nc.dram_tensor(f"yown{k}", [1024, D], BF16) for k in range(4)]
        self.ygath_t = [self.nc.dram_tensor(f"ygath{k}", [2048, D], BF16) for k in range(4)]

    def setup_consts(self, st):
        S = self.S
        self.ident_f = self.sb(st, "ident_f", [128, 128])
        self.ident_b = self.sb(st, "ident_b", [128, 128], BF16)
        self.sel = self.sb(st, "sel", [2, 256])
        self.ccs = self.sb(st, "ccs", [128, 8, 2])
        self.zeros_b = self.sb(st, "zeros_b", [128, 128], BF16)
        self.ones_b = self.sb(st, "ones_b", [1, 128], BF16)
        self.zeros_w = self.sb(st, "zeros_w", [128, 512], BF16)
        self.B_const = S.buf("consts")
        b = self.B_const
        S.dma("sp", self.ident_f[:], self.ident_in[:, :], w=[b])
        S.dma("sp", self.sel[:], self.sel_in[:, :], w=[b])
        S.dma("sp", self.ccs[:], self.cc[:, :, :], w=[b])
        self.pselt = self.sb(st, "pselt", [128, 4])
        S.dma("sp", self.pselt[:, 0:2], self.psel[:, :], w=[b])
        S.op("dve", lambda e: e.tensor_scalar(out=self.pselt[:, 2:4], in0=self.pselt[:, 0:2], scalar1=1.0 / NA_SCALE, scalar2=None,
                                              op0=ALU.mult), r=[b], w=[b])
        S.op("dve", lambda e: e.tensor_copy(self.ident_b[:], self.ident_f[:]), r=[b], w=[b])
        S.op("dve", lambda e: e.memset(self.zeros_b[:], 0.0), w=[b])
        S.op("dve", lambda e: e.memset(self.ones_b[:], 1.0), w=[b])
        S.op("dve", lambda e: e.memset(self.zeros_w[:], 0.0), w=[b])
        S.op("act", lambda e: e.activation(out=self.ccs[:], in_=self.ccs[:], func=AF.Silu), r=[b], w=[b])
        self.sc1 = self.sb(st, "sc1", [128, 8, 2])
        self.sh = self.sb(st, "sh", [128, 8, 2])
        self.g_l = self.sb(st, "g_l", [128, D])
        self.g_c = self.sb(st, "g_c", [128, D])
        self.lng = self.sb(st, "lng", [128, D])
        self.lnb = self.sb(st, "lnb", [128, D])
        self.B_mod = S.buf("mod")

    def adaln(self, l):
        S, nc = self.S, self.nc
        S.barrier()
        with contextlib.ExitStack() as st:
            wt = [self.sb(st, f"adaw{i}", [128, 3 * D]) for i in range(2)]
            Bw = S.bufs("adaw", 2)
            mrow = self.sb(st, "mrow", [2, 3 * D])
            brow = self.sb(st, "adab", [2, 3 * D])
            Bm = S.buf("mrow")
            Bb = S.buf("adab")
            pm = [self.ps(st, f"pm{i}", [128, 512]) for i in range(8)]
            Bp = S.bufs("pm", 8)
            S.dma("sp", brow[:], self.ada_b[l:l + 1, :].to_broadcast([2, 3 * D]), w=[Bb])
            S.dma("sp", self.lng[:], self.ln_g[l:l + 1, :].to_broadcast([128, D]), w=[self.B_mod])
            S.dma("sp", self.lnb[:], self.ln_b[l:l + 1, :].to_broadcast([128, D]), w=[self.B_mod])
            for k in range(8):
                S.dma("sp", wt[k % 2][:], self.ada_w[l, k * 128:(k + 1) * 128, :], w=[Bw[k % 2]])
                for n in range(6):
                    S.op("pe", lambda e, k=k, n=n: e.matmul(
                        pm[n][0:2, :], self.ccs[:, k, :], wt[k % 2][:, n * 512:(n + 1) * 512],
                        start=(k == 0), stop=(k == 7)), r=[Bw[k % 2], self.B_const], w=[Bp[n]])
            for n in range(6):
                S.op("dve", lambda e, n=n: e.tensor_tensor(
                    out=mrow[:, n * 512:(n + 1) * 512], in0=pm[n][0:2, :], in1=brow[:, n * 512:(n + 1) * 512],
                    op=ALU.add), r=[Bp[n], Bb], w=[Bm])
            for j in range(8):
                S.op("pe", lambda e, j=j: e.transpose(
                    pm[6][:, 2 * j:2 * j + 2], mrow[0:2, j * 128:(j + 1) * 128], self.ident_f[0:2, 0:2]),
                    r=[Bm, self.B_const], w=[Bp[6]])
                S.op("pe", lambda e, j=j: e.transpose(
                    pm[7][:, 2 * j:2 * j + 2], mrow[0:2, D + j * 128:D + (j + 1) * 128], self.ident_f[0:2, 0:2]),
                    r=[Bm, self.B_const], w=[Bp[7]])
            S.op("dve", lambda e: e.tensor_copy(self.sh[:].rearrange("p a b -> p (a b)"), pm[6][:, 0:16]),
                 r=[Bp[6]], w=[self.B_mod])
            S.op("dve", lambda e: e.tensor_scalar(
                out=self.sc1[:].rearrange("p a b -> p (a b)"), in0=pm[7][:, 0:16], scalar1=1.0, scalar2=None,
                op0=ALU.add), r=[Bp[7]], w=[self.B_mod])
            for n in range(2):
                S.op("pe", lambda e, n=n: e.matmul(
                    pm[n][:, :], self.sel[:, 0:128], mrow[0:2, 2 * D + n * 512:2 * D + (n + 1) * 512],
                    start=True, stop=True), r=[Bm, self.B_const], w=[Bp[n]])
                S.op("pe", lambda e, n=n: e.matmul(
                    pm[2 + n][:, :], self.sel[:, 128:256], mrow[0:2, 2 * D + n * 512:2 * D + (n + 1) * 512],
                    start=True, stop=True), r=[Bm, self.B_const], w=[Bp[2 + n]])
                S.op("dve", lambda e, n=n: e.tensor_copy(self.g_l[:, n * 512:(n + 1) * 512], pm[n][:, :]),
                     r=[Bp[n]], w=[self.B_mod])
                S.op("dve", lambda e, n=n: e.tensor_copy(self.g_c[:, n * 512:(n + 1) * 512], pm[2 + n][:, :]),
                     r=[Bp[2 + n]], w=[self.B_mod])
            S.barrier()

    def load_cast(self, st_w, dst_fn, src_fn, nparts, ncols, pieces, Bdst, scale_fn=None, name="lc"):
        S = self.S
        with contextlib.ExitStack() as st:
            stg = [self.sb(st, f"{name}_stg{i}", [nparts, ncols]) for i in range(2)]
            Bs = S.bufs(name + "_stg", 2)
            for i, (dst, src, sc) in enumerate(pieces):
                j = i % 2
                n = src.shape[-1]
                S.dma("sp", stg[j][:, 0:n], src, w=[Bs[j]])
                en = "dve" if i % 2 == 0 else "pool"
                if sc is None:
                    S.op(en, lambda e, dst=dst, j=j, n=n: e.tensor_copy(dst, stg[j][:, 0:n]), r=[Bs[j]], w=[Bdst])
                else:
                    S.op(en, lambda e, dst=dst, j=j, n=n, sc=sc: e.tensor_scalar(
                        out=dst, in0=stg[j][:, 0:n], scalar1=sc, scalar2=None, op0=ALU.mult),
                        r=[Bs[j], Bdst], w=[Bdst])
            S.barrier()

    def even_layer(self, l, src, dst, last):
        i = l // 2
        self.adaln(l)
        self.even_project(i, src)
        self.da_attention(i, l)
        self.mla_attention(i)
        self.S.allgather_pairs(self.yodd_t, self.yoddg_t)
        self.out_stage(self.ev_wo[i], src, dst, last, ysplit="odd")

    def even_project(self, i, src):
        S, nc = self.S, self.nc
        with contextlib.ExitStack() as st:
            wb = self.sb(st, "ev_wb", [128, 8, EV_COLS], BF16)
            wuq = self.sb(st, "ev_wuqb", [128, 2, 768], BF16)
            wukv = self.sb(st, "ev_wukvb", [128, 512], BF16)
            bcol = self.sb(st, "ev_bcol", [128, 10])
            brow_f = self.sb(st, "ev_brow_f", [1, 1664])
            brow = self.sb(st, "ev_brow", [1, 1664], BF16)
            qg = self.sb(st, "ev_qg", [128, 2])
            kvg = self.sb(st, "ev_kvg", [128, 1])
            Bw = S.buf("ev_w")
            S.dma("sp", bcol[:], self.ev_bcol[i, :, :], w=[Bw])
            S.dma("sp", brow_f[:], self.ev_brow[i, :, :], w=[Bw])
            S.dma("sp", qg[:], self.ev_qg[i, :, :], w=[Bw])
            S.dma("sp", kvg[:], self.ev_kvg[i, :, :], w=[Bw])
            S.op("dve", lambda e: e.tensor_copy(brow[:], brow_f[:]), r=[Bw], w=[Bw])
            pieces = []
            for k in range(8):
                for c in range(2):
                    pieces.append((wb[:, k, c * 1376:(c + 1) * 1376],
                                   self.ev_w[i, k * 128:(k + 1) * 128, c * 1376:(c + 1) * 1376], None))
            self.load_cast(st, None, None, 128, 1536, pieces, Bw, name="evw")
            pieces = [(wuq[:, rc, :], self.ev_wuq[i, rc * 128:(rc + 1) * 128, :], qg[:, rc:rc + 1]) for rc in range(2)]
            pieces.append((wukv[:, :], self.ev_wukv[i, :, :], kvg[:, 0:1]))
            self.load_cast(st, None, None, 128, 1536, pieces, Bw, name="evw2")

            NXS = 6
            xt = [self.sb(st, f"xt{j}", [128, D]) for j in range(NXS)]
            Bx = S.bufs("xt", NXS)
            hT = [self.sb(st, f"hT{j}", [128, 8, 512], BF16) for j in range(2)]
            Bh = S.bufs("hT", 2)
            cosd = [self.sb(st, f"cosd{j}", [128, 512]) for j in range(2)]
            sind = [self.sb(st, f"sind{j}", [128, 512]) for j in range(2)]
            cosm = [self.sb(st, f"cosm{j}", [128, 512]) for j in range(2)]
            sinm = [self.sb(st, f"sinm{j}", [128, 512]) for j in range(2)]
            Brt = S.bufs("ropet", 2)
            t1 = [self.sb(st, f"t1_{j}", [128, 512]) for j in range(2)]
            t2 = [self.sb(st, f"t2_{j}", [128, 512]) for j in range(2)]
            Bt1 = S.bufs("t1", 2)
            Bt2 = S.bufs("t2", 2)
            fo = [self.sb(st, f"fo{j}", [128, 512], BF16) for j in range(4)]
            Bfo = S.bufs("fo", 4)
            va = [self.sb(st, f"va{j}", [128, 2, 129], BF16) for j in range(2)]
            Bva = S.bufs("va", 2)
            vma = [self.sb(st, f"vma{j}", [128, 4, 65], BF16) for j in range(2)]
            Bvma = S.bufs("vma", 2)
            cq = [self.sb(st, f"cq{j}", [128, 384]) for j in range(2)]
            Bcq = S.bufs("cq", 2)
            cqn = [self.sb(st, f"cqn{j}", [128, 384], BF16) for j in range(2)]
            Bcqn = S.bufs("cqn", 2)
            stat = [self.sb(st, f"stat{j}", [128, 8]) for j in range(2)]
            Bstat = S.bufs("stat", 2)
            junk = self.sb(st, "junk", [128, 256])
            Bjunk = S.buf("junk")
            cT = [self.sb(st, f"cT{j}", [128, 3, 512], BF16) for j in range(2)]
            BcT = S.bufs("cT", 2)
            gt = [self.sb(st, f"gt{j}", [128, D], BF16) for j in range(2)]
            Bgt = S.bufs("gt", 2)
            pp = [self.ps(st, f"pp{j}", [128, 512]) for j in range(7)]
            ppb = self.ps(st, "ppb", [128, 1024], BF16)
            Bpp = S.bufs("pp", 7)
            Bppb = S.buf("ppb")
            for j in range(2):
                S.op("pool", lambda e, j=j: e.memset(va[j][:], 1.0), w=[Bva[j]])
                S.op("pool", lambda e, j=j: e.memset(vma[j][:], 1.0), w=[Bvma[j]])
            eps_t = self.sb(st, "eps_t", [128, 1])
            S.op("pool", lambda e: e.memset(eps_t[:], RMS_EPS), w=[Bw])

            self._pp_i = 0

            def next_pp():
                j = self._pp_i % 7
                self._pp_i += 1
                return pp[j], Bpp[j]

            nblk = (NTOK + 511) // 512
            self._xi = 0
            self._foi = 0
            self._evac = 0

            def stage_a(blk):
                t0 = blk * 512
                ntok = min(512, NTOK - t0)
                nsub = ntok // 128
                m = 0 if t0 < LAT else 1
                hb = blk % 2
                S.dma("sp", cosd[hb][:, :ntok], self.rope_da[0, :, t0:t0 + ntok], w=[Brt[hb]])
                S.dma("sp", sind[hb][:, :ntok], self.rope_da[1, :, t0:t0 + ntok], w=[Brt[hb]])
                S.dma("sp", cosm[hb][:, :ntok], self.rope_ml[0, :, t0:t0 + ntok], w=[Brt[hb]])
                S.dma("sp", sinm[hb][:, :ntok], self.rope_ml[1, :, t0:t0 + ntok], w=[Brt[hb]])
                for s in range(nsub):
                    xj = self._xi % NXS
                    self._xi += 1
                    S.dma("sp", xt[xj][:], src[t0 + s * 128:t0 + (s + 1) * 128, :], w=[Bx[xj]])
                    for half in range(2):
                        p, Bp = next_pp()
                        for q in range(4):
                            dc = half * 4 + q
                            S.op("pe", lambda e, p=p, q=q, dc=dc, xj=xj: e.transpose(
                                p[:, q * 128:(q + 1) * 128], xt[xj][:, dc * 128:(dc + 1) * 128], self.ident_f[:]),
                                r=[Bx[xj], self.B_const], w=[Bp])
                        for q in range(4):
                            dc = half * 4 + q
                            if self._evac % 2 == 0:
                                S.op("act", lambda e, p=p, q=q, dc=dc, s=s, m=m: e.activation(
                                    out=hT[hb][:, dc, s * 128:(s + 1) * 128], in_=p[:, q * 128:(q + 1) * 128],
                                    func=AF.Identity, scale=self.sc1[:, dc, m:m + 1], bias=self.sh[:, dc, m:m + 1]),
                                    r=[Bp, self.B_mod], w=[Bh[hb]])
                            else:
                                S.op("dve", lambda e, p=p, q=q, dc=dc, s=s, m=m: e.tensor_scalar(
                                    out=hT[hb][:, dc, s * 128:(s + 1) * 128], in0=p[:, q * 128:(q + 1) * 128],
                                    scalar1=self.sc1[:, dc, m:m + 1], scalar2=self.sh[:, dc, m:m + 1],
                                    op0=ALU.mult, op1=ALU.add), r=[Bp, self.B_mod], w=[Bh[hb]])
                            self._evac += 1

            def stage_b(blk):
                t0 = blk * 512
                ntok = min(512, NTOK - t0)
                nsub = ntok // 128
                m = 0 if t0 < LAT else 1
                hb = blk % 2
                foi = self._foi
                for grp, base, dstT, bofs in ((0, EQ, self.qda, 0), (1, EK, self.kda, 4)):
                    for h in range(2):
                        pa, Ba = next_pp()
                        pb, Bb = next_pp()
                        for k in range(8):
                            S.op("pe", lambda e, pa=pa, k=k, c0=base + h * 128: e.matmul(
                                pa[:, :ntok], wb[:, k, c0:c0 + 128], hT[hb][:, k, :ntok], start=(k == 0), stop=(k == 7)),
                                r=[Bw, Bh[hb]], w=[Ba])
                        for k in range(8):
                            S.op("pe", lambda e, pb=pb, k=k, c0=base + 256 + h * 128: e.matmul(
                                pb[:, :ntok], wb[:, k, c0:c0 + 128], hT[hb][:, k, :ntok], start=(k == 0), stop=(k == 7)),
                                r=[Bw, Bh[hb]], w=[Bb])
                        tj = (grp * 2 + h) % 2
                        S.op("dve", lambda e, pa=pa, tj=tj, c=bofs + h: e.scalar_tensor_tensor(
                            out=t1[tj][:, :ntok], in0=pa[:, :ntok], scalar=bcol[:, c:c + 1], in1=cosd[hb][:, :ntok],
                            op0=ALU.add, op1=ALU.mult), r=[Ba, Brt[hb], Bw], w=[Bt1[tj]])
                        S.op("dve", lambda e, pb=pb, tj=tj, c=bofs + 2 + h: e.scalar_tensor_tensor(
                            out=t2[tj][:, :ntok], in0=pb[:, :ntok], scalar=bcol[:, c:c + 1], in1=sind[hb][:, :ntok],
                            op0=ALU.add, op1=ALU.mult), r=[Bb, Brt[hb], Bw], w=[Bt2[tj]])
                        fj = foi % 4
                        foi += 1
                        S.op("pool", lambda e, tj=tj, fj=fj: e.tensor_tensor(
                            out=fo[fj][:, :ntok], in0=t1[tj][:, :ntok], in1=t2[tj][:, :ntok], op=ALU.add),
                            r=[Bt1[tj], Bt2[tj]], w=[Bfo[fj]])
                        S.dma("pool", dstT[h, :, t0:t0 + ntok], fo[fj][:, :ntok], r=[Bfo[fj]])
                pa, Ba = next_pp()
                pb, Bb = next_pp()
                for k in range(8):
                    S.op("pe", lambda e, pa=pa, k=k: e.matmul(
                        pa[0:32, :ntok], wb[:, k, EKR:EKR + 32], hT[hb][:, k, :ntok], start=(k == 0), stop=(k == 7)),
                        r=[Bw, Bh[hb]], w=[Ba])
                for k in range(8):
                    S.op("pe", lambda e, pb=pb, k=k: e.matmul(
                        pb[0:32, :ntok], wb[:, k, EKR + 32:EKR + 64], hT[hb][:, k, :ntok], start=(k == 0), stop=(k == 7)),
                        r=[Bw, Bh[hb]], w=[Bb])
                tj = 0
                S.op("dve", lambda e, pa=pa: e.scalar_tensor_tensor(
                    out=t1[tj][0:32, :ntok], in0=pa[0:32, :ntok], scalar=bcol[0:32, 8:9], in1=cosm[hb][0:32, :ntok],
                    op0=ALU.add, op1=ALU.mult), r=[Ba, Brt[hb], Bw], w=[Bt1[tj]])
                S.op("dve", lambda e, pb=pb: e.scalar_tensor_tensor(
                    out=t2[tj][0:32, :ntok], in0=pb[0:32, :ntok], scalar=bcol[0:32, 9:10], in1=sinm[hb][0:32, :ntok],
                    op0=ALU.add, op1=ALU.mult), r=[Bb, Brt[hb], Bw], w=[Bt2[tj]])
                fj = foi % 4
                foi += 1
                S.op("pool", lambda e, fj=fj: e.tensor_tensor(
                    out=fo[fj][0:32, :ntok], in0=t1[tj][0:32, :ntok], in1=t2[tj][0:32, :ntok], op=ALU.add),
                    r=[Bt1[tj], Bt2[tj]], w=[Bfo[fj]])
                S.dma("pool", self.krt[:, t0:t0 + ntok], fo[fj][0:32, :ntok], r=[Bfo[fj]])
                cb = blk % 2
                for s in range(nsub):
                    tt = (t0 // 128) + s
                    tok0 = t0 + s * 128
                    p, Bp = next_pp()
                    for k in range(8):
                        S.op("pe", lambda e, p=p, k=k, s=s: e.matmul(
                            p[:, 0:256], hT[hb][:, k, s * 128:(s + 1) * 128], wb[:, k, EV_:EV_ + 256], start=(k == 0), stop=False),
                            r=[Bw, Bh[hb]], w=[Bp])
                    S.op("pe", lambda e, p=p: e.matmul(p[:, 0:256], self.ones_b[0:1, :], brow[0:1, 0:256], start=False, stop=True),
                         r=[Bw, self.B_const], w=[Bp])
                    vj = tt % 2
                    S.op("act", lambda e, p=p, vj=vj: e.activation(
                        out=va[vj][:, :, 0:128], in_=p[:, 0:256].rearrange("p (h c) -> p h c", h=2), func=AF.Copy),
                        r=[Bp], w=[Bva[vj]])
                    S.dma("pool", self.vda[:, :, tt, :].rearrange("h p c -> p h c"), va[vj][:], r=[Bva[vj]])
                    p, Bp = next_pp()
                    for k in range(8):
                        S.op("pe", lambda e, p=p, k=k, s=s: e.matmul(
                            p[:, 0:384], hT[hb][:, k, s * 128:(s + 1) * 128], wb[:, k, EC:EC + 384], start=(k == 0), stop=False),
                            r=[Bw, Bh[hb]], w=[Bp])
                    S.op("pe", lambda e, p=p: e.matmul(p[:, 0:384], self.ones_b[0:1, :], brow[0:1, 256:640], start=False, stop=True),
                         r=[Bw, self.B_const], w=[Bp])
                    cj = tt % 2
                    S.op("act", lambda e, p=p, cj=cj: e.activation(out=cq[cj][:], in_=p[:, 0:384], func=AF.Copy),
                         r=[Bp], w=[Bcq[cj]])
                    S.op("act", lambda e, cj=cj: e.activation(
                        out=junk[:, 0:256], in_=cq[cj][:, 0:256], func=AF.Square, accum_out=stat[cj][:, 0:1]),
                        r=[Bcq[cj]], w=[Bjunk, Bstat[cj]])
                    S.op("act", lambda e, cj=cj: e.activation(
                        out=junk[:, 0:128], in_=cq[cj][:, 256:384], func=AF.Square, accum_out=stat[cj][:, 1:2]),
                        r=[Bcq[cj]], w=[Bjunk, Bstat[cj]])
                    S.op("act", lambda e, cj=cj: e.activation(out=stat[cj][:, 2:3], in_=stat[cj][:, 0:1], func=AF.Sqrt,
                                                              scale=1.0 / 256, bias=eps_t[:, 0:1]), r=[Bstat[cj], Bw], w=[Bstat[cj]])
                    S.op("act", lambda e, cj=cj: e.activation(out=stat[cj][:, 3:4], in_=stat[cj][:, 1:2], func=AF.Sqrt,
                                                              scale=1.0 / 128, bias=eps_t[:, 0:1]), r=[Bstat[cj], Bw], w=[Bstat[cj]])
                    S.op("dve", lambda e, cj=cj: e.reciprocal(stat[cj][:, 4:6], stat[cj][:, 2:4]), r=[Bstat[cj]], w=[Bstat[cj]])
                    S.op("dve", lambda e, cj=cj: e.tensor_scalar(
                        out=cqn[cj][:, 0:256], in0=cq[cj][:, 0:256], scalar1=stat[cj][:, 4:5], scalar2=None, op0=ALU.mult),
                        r=[Bcq[cj], Bstat[cj]], w=[Bcqn[cj]])
                    S.op("dve", lambda e, cj=cj: e.tensor_scalar(
                        out=cqn[cj][:, 256:384], in0=cq[cj][:, 256:384], scalar1=stat[cj][:, 5:6], scalar2=None, op0=ALU.mult),
                        r=[Bcq[cj], Bstat[cj]], w=[Bcqn[cj]])
                    gj = tt % 2
                    for n in range(2):
                        p, Bp = next_pp()
                        for k in range(8):
                            S.op("pe", lambda e, p=p, k=k, s=s, n=n: e.matmul(
                                p[:, :], hT[hb][:, k, s * 128:(s + 1) * 128], wb[:, k, EG + n * 512:EG + (n + 1) * 512],
                                start=(k == 0), stop=False), r=[Bw, Bh[hb]], w=[Bp])
                        S.op("pe", lambda e, p=p, n=n: e.matmul(
                            p[:, :], self.ones_b[0:1, :], brow[0:1, 640 + n * 512:640 + (n + 1) * 512], start=False, stop=True),
                            r=[Bw, self.B_const], w=[Bp])
                        S.op("act", lambda e, p=p, n=n, gj=gj: e.activation(
                            out=gt[gj][:, n * 512:(n + 1) * 512], in_=p[:, :], func=AF.Silu), r=[Bp], w=[Bgt[gj]])
                    S.dma("pool", self.gate[tok0:tok0 + 128, :], gt[gj][:], r=[Bgt[gj]])
                    for rc in range(3):
                        S.op("pe", lambda e, rc=rc, cj=cj: e.transpose(
                            ppb[:, rc * 128:(rc + 1) * 128], cqn[cj][:, rc * 128:(rc + 1) * 128], self.ident_b[:]),
                            r=[Bcqn[cj], self.B_const], w=[Bppb])
                    S.op("dve", lambda e, s=s: e.tensor_copy(
                        cT[cb][:, :, s * 128:(s + 1) * 128], ppb[:, 0:384].rearrange("p (r t) -> p r t", r=3)),
                        r=[Bppb], w=[BcT[cb]])
                for h in range(4):
                    pa, Ba = next_pp()
                    pb, Bb = next_pp()
                    for rc in range(2):
                        S.op("pe", lambda e, pa=pa, rc=rc, h=h: e.matmul(
                            pa[0:96, :ntok], wuq[:, rc, h * 96:(h + 1) * 96], cT[cb][:, rc, :ntok], start=(rc == 0), stop=(rc == 1)),
                            r=[Bw, BcT[cb]], w=[Ba])
                    for rc in range(2):
                        S.op("pe", lambda e, pb=pb, rc=rc, h=h: e.matmul(
                            pb[0:96, :ntok], wuq[:, rc, 384 + h * 96:384 + (h + 1) * 96], cT[cb][:, rc, :ntok],
                            start=(rc == 0), stop=(rc == 1)), r=[Bw, BcT[cb]], w=[Bb])
                    tj = h % 2
                    fj = foi % 4
                    foi += 1
                    S.op("dve", lambda e, pa=pa, tj=tj: e.tensor_tensor(
                        out=t1[tj][64:96, :ntok], in0=pa[64:96, :ntok], in1=cosm[hb][64:96, :ntok], op=ALU.mult),
                        r=[Ba, Brt[hb]], w=[Bt1[tj]])
                    S.op("dve", lambda e, pb=pb, tj=tj: e.tensor_tensor(
                        out=t2[tj][64:96, :ntok], in0=pb[64:96, :ntok], in1=sinm[hb][64:96, :ntok], op=ALU.mult),
                        r=[Bb, Brt[hb]], w=[Bt2[tj]])
                    S.op("act", lambda e, pa=pa, fj=fj: e.activation(out=fo[fj][0:64, :ntok], in_=pa[0:64, :ntok], func=AF.Copy),
                         r=[Ba], w=[Bfo[fj]])
                    S.op("pool", lambda e, tj=tj, fj=fj: e.tensor_tensor(
                        out=fo[fj][64:96, :ntok], in0=t1[tj][64:96, :ntok], in1=t2[tj][64:96, :ntok], op=ALU.add),
                        r=[Bt1[tj], Bt2[tj]], w=[Bfo[fj]])
                    S.dma("pool", self.qm[h, :, t0:t0 + ntok], fo[fj][0:96, :ntok], r=[Bfo[fj]])
                for hp in range(2):
                    p, Bp = next_pp()
                    S.op("pe", lambda e, p=p, hp=hp: e.matmul(
                        p[:, :ntok], wukv[:, hp * 128:(hp + 1) * 128], cT[cb][:, 2, :ntok], start=True, stop=True),
                        r=[Bw, BcT[cb]], w=[Bp])
                    fj = foi % 4
                    foi += 1
                    S.op("act", lambda e, p=p, fj=fj: e.activation(out=fo[fj][:, :ntok], in_=p[:, :ntok], func=AF.Copy),
                         r=[Bp], w=[Bfo[fj]])
                    S.dma("pool", self.kmn[2 * hp, :, t0:t0 + ntok], fo[fj][0:64, :ntok], r=[Bfo[fj]])
                    S.dma("pool", self.kmn[2 * hp + 1, :, t0:t0 + ntok], fo[fj][64:128, :ntok], r=[Bfo[fj]])
                for s in range(nsub):
                    tt = (t0 // 128) + s
                    p, Bp = next_pp()
                    S.op("pe", lambda e, p=p, s=s: e.matmul(
                        p[:, 0:256], cT[cb][:, 2, s * 128:(s + 1) * 128], wukv[:, 256:512], start=True, stop=True),
                        r=[Bw, BcT[cb]], w=[Bp])
                    vj = tt % 2
                    S.op("act", lambda e, p=p, vj=vj: e.activation(
                        out=vma[vj][:, :, 0:64], in_=p[:, 0:256].rearrange("p (h c) -> p h c", h=4), func=AF.Copy),
                        r=[Bp], w=[Bvma[vj]])
                    S.dma("pool", self.vm[:, :, tt, :].rearrange("h p c -> p h c"), vma[vj][:], r=[Bvma[vj]])
                self._foi = foi

            stage_a(0)
            for blk in range(nblk):
                if blk + 1 < nblk:
                    stage_a(blk + 1)
                stage_b(blk)
            S.barrier()

    def attn_core(self, st, name, KT, QT, VA, Bkv, kparts, nv, scale, qblocks, finalize, kslices=None):
        S = self.S
        nmap = len(kparts)
        nacc_per_bank = 512 // nv
        n_acc = nmap * 4
        n_acc_banks = (n_acc + nacc_per_bank - 1) // nacc_per_bank
        n_sc = 8 - n_acc_banks
        n_sc = min(n_sc, 4)
        NP = 4
        res = getattr(self, "_attn_res", None)
        if res is None or res[0] != name:
            accb = [self.ps(st, f"{name}_acc{j}", [128, 512]) for j in range(n_acc_banks)]
            Bacc = S.bufs(name + "_acc", n_acc_banks)
            scb = [self.ps(st, f"{name}_sc{j}", [128, 512]) for j in range(n_sc)]
            Bsc = S.bufs(name + "_sc", n_sc)
            pt = [self.sb(st, f"{name}_pt{j}", [128, 512], BF16) for j in range(NP)]
            Bpt = S.bufs(name + "_pt", NP)
            self._attn_res = (name, accb, Bacc, scb, Bsc, pt, Bpt)
        _, accb, Bacc, scb, Bsc, pt, Bpt = self._attn_res

        def acc_ap(mi, sub):
            idx = mi * 4 + sub
            b = idx // nacc_per_bank
            o = (idx % nacc_per_bank) * nv
            return accb[b][:, o:o + nv], Bacc[b]

        sci = 0
        pti = 0
        for qbi, (q0, nq, ktiles) in enumerate(qblocks):
            nsub = nq // 128
            for b in range(n_acc_banks):
                S.op("pe", lambda e, b=b: e.matmul(accb[b][:, :], self.zeros_b[:, :], self.zeros_w[:, :], start=True, stop=False,
                                                   skip_group_check=True), r=[self.B_const], w=[Bacc[b]])
            units = [(kt, mi) for kt in ktiles for mi in range(nmap)]
            pend = []

            def emit_score(u):
                nonlocal sci
                kt, mi = u
                QTm, lo, hi = kparts[mi]
                j = sci % n_sc
                sci += 1
                S.op("pe", lambda e, j=j, lo=lo, hi=hi, kt=kt, QTm=QTm: e.matmul(
                    scb[j][:, :nq], KT[lo:hi, kt * 128:(kt + 1) * 128], QTm[lo:hi, q0:q0 + nq], start=True, stop=True),
                    r=[Bkv], w=[Bsc[j]])
                return j

            def emit_exp_pv(u, j, lastflag):
                nonlocal pti
                kt, mi = u
                pj = pti % NP
                pti += 1
                S.op("act", lambda e, j=j, pj=pj: e.activation(out=pt[pj][:, :nq], in_=scb[j][:, :nq], func=AF.Exp, scale=scale),
                     r=[Bsc[j]], w=[Bpt[pj]])
                for sub in range(nsub):
                    ap, Ba = acc_ap(mi, sub)
                    S.op("pe", lambda e, ap=ap, pj=pj, sub=sub, kt=kt: e.matmul(
                        ap, pt[pj][:, sub * 128:(sub + 1) * 128], VA[:, kt, :], start=False, stop=lastflag,
                        skip_group_check=True), r=[Bpt[pj], Bkv], w=[Ba])

            LOOK = min(n_sc - 1, 2)
            q = []
            for ui, u in enumerate(units):
                q.append((u, emit_score(u)))
                if len(q) > LOOK:
                    u0, j0 = q.pop(0)
                    emit_exp_pv(u0, j0, False)
            while q:
                u0, j0 = q.pop(0)
                emit_exp_pv(u0, j0, u0[0] == ktiles[-1])
            for sub in range(nsub):
                accs = [acc_ap(mi, sub) for mi in range(nmap)]
                finalize(qbi, q0, sub, accs)

    def qblocks_all(self):
        qb = []
        lat_k = list(range(NT))
        for b in range(LAT // 512):
            qb.append((b * 512, 512, lat_k))
        qb.append((LAT, CTX, [NT_LAT, NT_LAT + 1]))
        if self.qb_filter is not None:
            qb = [q for i, q in enumerate(qb) if i in self.qb_filter]
        return qb

    def qblocks_own(self):
        qb = []
        lat_k = list(range(NT))
        for b in range(LAT // 2 // 512):
            qb.append((b * 512, 512, lat_k))
        qb.append((LAT // 2, CTX, [NT_LAT, NT_LAT + 1]))
        if self.qb_filter is not None:
            qb = [q for i, q in enumerate(qb) if i in self.qb_filter]
        return qb

    def own_q(self, QO, QS, lo, hi, tmp, Bqs, Bqo, Btmp):
        S = self.S
        H = LAT // 2
        S.op("dve", lambda e: e.tensor_scalar(out=tmp[lo:hi, :], in0=QS[lo:hi, H:LAT], scalar1=self.pselt[lo:hi, 1:2], scalar2=None,
                                              op0=ALU.mult), r=[Bqs, self.B_const], w=[Btmp])
        S.op("dve", lambda e: e.scalar_tensor_tensor(out=QO[lo:hi, 0:H], in0=QS[lo:hi, 0:H], scalar=self.pselt[lo:hi, 0:1],
                                                     in1=tmp[lo:hi, :], op0=ALU.mult, op1=ALU.add),
             r=[Bqs, Btmp, self.B_const], w=[Bqo])
        S.op("act", lambda e: e.activation(out=QO[lo:hi, H:H + CTX], in_=QS[lo:hi, LAT:NTOK], func=AF.Copy), r=[Bqs], w=[Bqo])

    def blend(self, out, a, b, Ba, Bb, Bout, lo=0, hi=128, col=0):
        S = self.S
        s0 = self.pselt[lo:hi, col:col + 1]
        s1 = self.pselt[lo:hi, col + 1:col + 2]
        S.op("dve", lambda e: e.tensor_scalar(out=b, in0=b, scalar1=s1, scalar2=None, op0=ALU.mult), r=[Bb, self.B_const], w=[Bb])
        S.op("dve", lambda e: e.scalar_tensor_tensor(out=out, in0=a, scalar=s0, in1=b, op0=ALU.mult, op1=ALU.add),
             r=[Ba, Bb, self.B_const], w=[Bout])

    def yodd_loc(self, t):
        if t < NT_LAT:
            return t // 16, (t % 16) * 128
        return 4, (t - NT_LAT) * 128

    def y_dst(self, tok0, c0, c1):
        H = LAT // 2
        if tok0 < H:
            k, r = tok0 // 1024, tok0 % 1024
            return self.yown_t[k].ap()[r:r + 128, c0:c1]
        return self.ybuf[LAT + tok0 - H:LAT + tok0 - H + 128, c0:c1]

    def da_attention(self, i, l):
        S = self.S
        lam_init = 0.8 - 0.6 * math.exp(-0.3 * l)
        with contextlib.ExitStack() as st:
            KTs = [self.sb(st, f"da_KT{j}", [128, NTOK], BF16) for j in range(2)]
            VAs = [self.sb(st, f"da_VA{j}", [128, NT, 129], BF16) for j in range(2)]
            QT1s = [self.sb(st, f"da_QT1{j}", [128, NTOK], BF16) for j in range(2)]
            QT2s = [self.sb(st, f"da_QT2{j}", [128, NTOK], BF16) for j in range(2)]
            Bkvs = S.bufs("da_kv", 2)
            for j in range(2):
                S.op("pool", lambda e, j=j: e.memset(QT1s[j][64:128, :], 0.0), w=[Bkvs[j]])
                S.op("pool", lambda e, j=j: e.memset(QT2s[j][0:64, :], 0.0), w=[Bkvs[j]])
            lamt = self.sb(st, "lamt", [128, 256])
            lamw = self.sb(st, "lamw", [128, 8])
            subg = self.sb(st, "subg", [128, 128])
            Bl = S.buf("lam")
            eps_t = self.sb(st, "da_eps", [128, 1])
            S.op("pool", lambda e: e.memset(eps_t[:], RMS_EPS), w=[Bl])
            S.dma("sp", lamt[:], self.ev_lam[i, :, :].to_broadcast([128, 256]), w=[Bl])
            S.dma("sp", subg[:], self.ev_subg[i, :, :].to_broadcast([128, 128]), w=[Bl])
            junk = self.sb(st, "da_junk", [128, 128])
            Bj = S.buf("da_junk")
            S.op("dve", lambda e: e.tensor_tensor(out=junk[:, 0:64], in0=lamt[:, 0:64], in1=lamt[:, 64:128], op=ALU.mult),
                 r=[Bl], w=[Bj])
            S.op("dve", lambda e: e.reduce_sum(out=lamw[:, 0:1], in_=junk[:, 0:64], axis=AX.X), r=[Bj], w=[Bl])
            S.op("dve", lambda e: e.tensor_tensor(out=junk[:, 64:128], in0=lamt[:, 128:192], in1=lamt[:, 192:256], op=ALU.mult),
                 r=[Bl], w=[Bj])
            S.op("dve", lambda e: e.reduce_sum(out=lamw[:, 1:2], in_=junk[:, 64:128], axis=AX.X), r=[Bj], w=[Bl])
            S.op("act", lambda e: e.activation(out=lamw[:, 2:4], in_=lamw[:, 0:2], func=AF.Exp), r=[Bl], w=[Bl])
            S.op("dve", lambda e: e.tensor_tensor(out=lamw[:, 4:5], in0=lamw[:, 3:4], in1=lamw[:, 2:3], op=ALU.subtract),
                 r=[Bl], w=[Bl])
            S.op("dve", lambda e: e.tensor_scalar(out=lamw[:, 5:6], in0=lamw[:, 4:5], scalar1=-lam_init, scalar2=None, op0=ALU.add),
                 r=[Bl], w=[Bl])
            NF = 3
            rr = [self.sb(st, f"da_rr{j}", [128, 8]) for j in range(NF)]
            ta = [self.sb(st, f"da_ta{j}", [128, 128]) for j in range(NF)]
            td = [self.sb(st, f"da_td{j}", [128, 128]) for j in range(NF)]
            to = [self.sb(st, f"da_to{j}", [128, 128], BF16) for j in range(NF)]
            Bf = S.bufs("da_fin", NF)
            Bto = S.bufs("da_to", NF)
            self._fi = 0
            def da_load(h):
                j = h % 2
                S.dma("sp", KTs[j][:], self.kda[h, :, :], w=[Bkvs[j]])
                S.dma("act", QT1s[j][0:64, :], self.qda[h, 0:64, :], w=[Bkvs[j]])
                S.dma("act", QT2s[j][64:128, :], self.qda[h, 64:128, :], w=[Bkvs[j]])
                S.dma("sp", VAs[j][:], self.vda[h, :, :, :], w=[Bkvs[j]])

            da_load(0)
            for h in range(2):
                if h + 1 < 2:
                    da_load(h + 1)
                KT, VA, QT1, QT2, Bkv = KTs[h % 2], VAs[h % 2], QT1s[h % 2], QT2s[h % 2], Bkvs[h % 2]

                def fin(qbi, q0, sub, accs, h=h):
                    j = self._fi % NF
                    self._fi += 1
                    (a1, B1), (a2, B2) = accs
                    S.op("dve", lambda e: e.reciprocal(rr[j][:, 0:1], a1[:, 128:129]), r=[B1], w=[Bf[j]])
                    S.op("dve", lambda e: e.reciprocal(rr[j][:, 1:2], a2[:, 128:129]), r=[B2], w=[Bf[j]])
                    S.op("dve", lambda e: e.tensor_tensor(out=rr[j][:, 2:3], in0=rr[j][:, 1:2], in1=lamw[:, 5:6], op=ALU.mult),
                         r=[Bf[j], Bl], w=[Bf[j]])
                    S.op("dve", lambda e: e.tensor_scalar(out=ta[j][:], in0=a1[:, 0:128], scalar1=rr[j][:, 0:1], scalar2=None,
                                                          op0=ALU.mult), r=[B1, Bf[j]], w=[Bf[j]])
                    S.op("dve", lambda e: e.scalar_tensor_tensor(out=td[j][:], in0=a2[:, 0:128], scalar=rr[j][:, 2:3], in1=ta[j][:],
                                                                 op0=ALU.mult, op1=ALU.add), r=[B2, Bf[j]], w=[Bf[j]])
                    S.op("act", lambda e: e.activation(out=ta[j][:], in_=td[j][:], func=AF.Square, accum_out=rr[j][:, 3:4]),
                         r=[Bf[j]], w=[Bf[j]])
                    S.op("act", lambda e: e.activation(out=rr[j][:, 4:5], in_=rr[j][:, 3:4], func=AF.Sqrt, scale=1.0 / 128,
                                                       bias=eps_t[:, 0:1]), r=[Bf[j], Bl], w=[Bf[j]])
                    S.op("dve", lambda e: e.reciprocal(rr[j][:, 5:6], rr[j][:, 4:5]), r=[Bf[j]], w=[Bf[j]])
                    S.op("pool", lambda e: e.tensor_scalar(out=td[j][:], in0=td[j][:], scalar1=rr[j][:, 5:6], scalar2=(1.0 - lam_init),
                                                           op0=ALU.mult, op1=ALU.mult), r=[Bf[j]], w=[Bf[j]])
                    S.op("pool", lambda e: e.tensor_tensor(out=to[j][:], in0=td[j][:], in1=subg[:], op=ALU.mult),
                         r=[Bf[j], Bl], w=[Bto[j]])
                    kk, r0 = self.yodd_loc((q0 + sub * 128) // 128)
                    S.dma("pool", self.yodd_t[kk].ap()[r0:r0 + 128, h * 128:(h + 1) * 128], to[j][:], r=[Bto[j]])

                self.attn_core(st, "da", KT, None, VA, Bkv, [(QT1, 0, 128), (QT2, 0, 128)], 129, DA_SCALE,
                               self.qblocks_all(), fin)
            S.barrier()
            self._attn_res = None

    def mla_attention(self, i):
        S = self.S
        with contextlib.ExitStack() as st:
            KTs = [self.sb(st, f"ml_KT{j}", [128, NTOK], BF16) for j in range(2)]
            QTs = [self.sb(st, f"ml_QT{j}", [128, NTOK], BF16) for j in range(2)]
            VAs = [self.sb(st, f"ml_VA{j}", [128, NT, 65], BF16) for j in range(2)]
            Bkvs = S.bufs("ml_kv", 2)
            NF = 3
            rr = [self.sb(st, f"ml_rr{j}", [128, 2]) for j in range(NF)]
            to = [self.sb(st, f"ml_to{j}", [128, 64], BF16) for j in range(NF)]
            Bto = S.bufs("ml_to", NF)
            self._fi = 0
            def ml_load(h):
                j = h % 2
                S.dma("sp", KTs[j][0:64, :], self.kmn[h, :, :], w=[Bkvs[j]])
                S.dma("sp", KTs[j][64:96, :], self.krt[:, :], w=[Bkvs[j]])
                S.dma("act", QTs[j][0:96, :], self.qm[h, :, :], w=[Bkvs[j]])
                S.dma("sp", VAs[j][:], self.vm[h, :, :, :], w=[Bkvs[j]])

            ml_load(0)
            for h in range(4):
                if h + 1 < 4:
                    ml_load(h + 1)
                KT, VA, QT, Bkv = KTs[h % 2], VAs[h % 2], QTs[h % 2], Bkvs[h % 2]

                def fin(qbi, q0, sub, accs, h=h):
                    j = self._fi % NF
                    self._fi += 1
                    (a1, B1), = accs
                    S.op("dve", lambda e: e.reciprocal(rr[j][:, 0:1], a1[:, 64:65]), r=[B1], w=[Bto[j]])
                    S.op("dve", lambda e: e.tensor_scalar(out=to[j][:], in0=a1[:, 0:64], scalar1=rr[j][:, 0:1], scalar2=None,
                                                          op0=ALU.mult), r=[B1, Bto[j]], w=[Bto[j]])
                    tok0 = q0 + sub * 128
                    kk, r0 = self.yodd_loc(tok0 // 128)
                    S.dma("pool", self.yodd_t[kk].ap()[r0:r0 + 128, 256 + h * 64:256 + (h + 1) * 64], to[j][:], r=[Bto[j]])

                self.attn_core(st, "ml", KT, None, VA, Bkv, [(QT, 0, 96)], 65, MLA_SCALE, self.qblocks_all(), fin)
            S.barrier()
            self._attn_res = None

    def out_stage(self, wo_dram, src, dst, last, ysplit=False):
        S = self.S
        with contextlib.ExitStack() as st:
            wo = self.sb(st, "wo", [128, 8, D], BF16)
            Bw = S.buf("wo")
            pieces = [(wo[:, k, :], wo_dram[k * 128:(k + 1) * 128, :], None) for k in range(8)]
            self.load_cast(st, None, None, 128, D, pieces, Bw, name="wo")
            NB = 3
            yt = [self.sb(st, f"o_yt{j}", [128, D], BF16) for j in range(NB)]
            gt = [self.sb(st, f"o_gt{j}", [128, D], BF16) for j in range(NB)]
            xt = [self.sb(st, f"o_xt{j}", [128, D]) for j in range(NB)]
            yg = [self.sb(st, f"o_yg{j}", [128, D], BF16) for j in range(NB)]
            ygT = [self.sb(st, f"o_ygT{j}", [128, D], BF16) for j in range(NB)]
            rt = [self.sb(st, f"o_rt{j}", [128, D]) for j in range(NB)]
            ot = [self.sb(st, f"o_ot{j}", [128, D]) for j in range(NB)]
            stt = [self.sb(st, f"o_st{j}", [128, 16]) for j in range(NB)]
            By, Bg, Bx, Byg, BygT, Br, Bo, Bs = (S.bufs(n, NB) for n in ("o_yt", "o_gt", "o_xt", "o_yg", "o_ygT", "o_rt", "o_ot", "o_st"))
            ptr = [self.ps(st, f"o_ptr{j}", [128, D], BF16) for j in range(2)]
            Bptr = S.bufs("o_ptr", 2)
            pout = [self.ps(st, f"o_po{j}", [128, 512]) for j in range(4)]
            Bpo = S.bufs("o_po", 4)
            eps_t = self.sb(st, "o_eps", [128, 1])
            S.op("pool", lambda e: e.memset(eps_t[:], LN_EPS), w=[Bw])
            ntiles = NT_LAT if (last and self.final_out) else NT
            tiles = [t for t in range(ntiles) if self.tile_filter is None or t in self.tile_filter]

            def stage_a(t):
                j = t % NB
                tok0 = t * 128
                isctx = t >= NT_LAT
                G = self.g_c if isctx else self.g_l
                if ysplit == "odd":
                    k, r0 = self.yodd_loc(t)
                    rows = self.yodd_rows[k]
                    g = self.yoddg_t[k].ap()
                    S.dma("sp", yt[j][:, 0:256], g[r0:r0 + 128, 0:256], w=[By[j]])
                    S.dma("sp", yt[j][:, 256:512], g[rows + r0:rows + r0 + 128, 0:256], w=[By[j]])
                    S.dma("sp", yt[j][:, 512:768], g[r0:r0 + 128, 256:512], w=[By[j]])
                    S.dma("sp", yt[j][:, 768:1024], g[rows + r0:rows + r0 + 128, 256:512], w=[By[j]])
                else:
                    if ysplit and not isctx:
                        half, u = t // 32, t % 32
                        r0 = half * 1024 + (u % 8) * 128
                        ysrc = self.ygath_t[u // 8].ap()[r0:r0 + 128, :]
                    else:
                        ysrc = self.ybuf[tok0:tok0 + 128, :]
                    S.dma("sp", yt[j][:], ysrc, w=[By[j]])
                S.dma("act", gt[j][:], self.gate[tok0:tok0 + 128, :], w=[Bg[j]])
                S.dma("sp", xt[j][:], src[tok0:tok0 + 128, :], w=[Bx[j]])
                S.op("pool", lambda e, j=j: e.tensor_tensor(out=yg[j][:], in0=yt[j][:], in1=gt[j][:], op=ALU.mult),
                     r=[By[j], Bg[j]], w=[Byg[j]])
                pj = t % 2
                for ec in range(8):
                    S.op("pe", lambda e, ec=ec, j=j, pj=pj: e.transpose(
                        ptr[pj][:, ec * 128:(ec + 1) * 128], yg[j][:, ec * 128:(ec + 1) * 128], self.ident_b[:]),
                        r=[Byg[j], self.B_const], w=[Bptr[pj]])
                S.op("act", lambda e, j=j, pj=pj: e.activation(out=ygT[j][:], in_=ptr[pj][:], func=AF.Copy),
                     r=[Bptr[pj]], w=[BygT[j]])
                for n in range(2):
                    pn = (t % 2) * 2 + n
                    for ec in range(8):
                        S.op("pe", lambda e, ec=ec, n=n, pn=pn, j=j: e.matmul(
                            pout[pn][:, :], ygT[j][:, ec * 128:(ec + 1) * 128], wo[:, ec, n * 512:(n + 1) * 512],
                            start=(ec == 0), stop=(ec == 7)), r=[BygT[j], Bw], w=[Bpo[pn]])

            def stage_b(t):
                j = t % NB
                tok0 = t * 128
                isctx = t >= NT_LAT
                G = self.g_c if isctx else self.g_l
                for n in range(2):
                    pn = (t % 2) * 2 + n
                    S.op("dve", lambda e, n=n, pn=pn, j=j, G=G: e.tensor_tensor(
                        out=rt[j][:, n * 512:(n + 1) * 512], in0=pout[pn][:, :], in1=G[:, n * 512:(n + 1) * 512], op=ALU.mult),
                        r=[Bpo[pn], self.B_mod], w=[Br[j]])
                S.op("dve", lambda e, j=j: e.scalar_tensor_tensor(
                    out=rt[j][:], in0=xt[j][:], scalar=ALPHA, in1=rt[j][:], op0=ALU.mult, op1=ALU.add),
                    r=[Bx[j], Br[j]], w=[Br[j]])
                for n in range(2):
                    S.op("dve", lambda e, n=n, j=j: e.bn_stats(stt[j][:, n * 6:(n + 1) * 6], rt[j][:, n * 512:(n + 1) * 512]),
                         r=[Br[j]], w=[Bs[j]])
                S.op("dve", lambda e, j=j: e.bn_aggr(stt[j][:, 12:14], stt[j][:, 0:12]), r=[Bs[j]], w=[Bs[j]])
                S.op("act", lambda e, j=j: e.activation(out=stt[j][:, 14:15], in_=stt[j][:, 13:14], func=AF.Sqrt, scale=1.0,
                                                        bias=eps_t[:, 0:1]), r=[Bs[j], Bw], w=[Bs[j]])
                S.op("dve", lambda e, j=j: e.reciprocal(stt[j][:, 15:16], stt[j][:, 14:15]), r=[Bs[j]], w=[Bs[j]])
                S.op("dve", lambda e, j=j: e.tensor_scalar(
                    out=ot[j][:], in0=rt[j][:], scalar1=stt[j][:, 12:13], scalar2=stt[j][:, 15:16], op0=ALU.subtract, op1=ALU.mult),
                    r=[Br[j], Bs[j]], w=[Bo[j]])
                S.op("pool", lambda e, j=j: e.tensor_tensor(out=ot[j][:], in0=ot[j][:], in1=self.lng[:], op=ALU.mult),
                     r=[Bo[j], self.B_mod], w=[Bo[j]])
                S.op("pool", lambda e, j=j: e.tensor_tensor(out=ot[j][:], in0=ot[j][:], in1=self.lnb[:], op=ALU.add),
                     r=[Bo[j], self.B_mod], w=[Bo[j]])
                S.dma("pool", dst[tok0:tok0 + 128, :], ot[j][:], r=[Bo[j]])

            for idx, t in enumerate(tiles):
                if idx == 0:
                    stage_a(t)
                if idx + 1 < len(tiles):
                    stage_a(tiles[idx + 1])
                stage_b(t)
            S.barrier()

    def odd_layer(self, l, src, dst, last):
        i = l // 2
        self.adaln(l)
        self.odd_project(i, src)
        self.odd_conv(i)
        self.mlstm(i)
        self.mlstm_post(i)
        self.na_attention(i, last)
        self.S.allgather_pairs(self.yodd_t, self.yoddg_t)
        self.out_stage(self.od_wo[i], src, dst, last, ysplit="odd")

    def odd_project(self, i, src):
        S = self.S
        with contextlib.ExitStack() as st:
            wb = self.sb(st, "od_wb", [128, 8, OD_COLS], BF16)
            bcol = self.sb(st, "od_bcol", [128, 8])
            brow_f = self.sb(st, "od_brow_f", [1, 1800])
            brow = self.sb(st, "od_brow", [1, 1800], BF16)
            fb = self.sb(st, "od_fb", [128, 4])
            Bw = S.buf("od_w")
            S.dma("sp", bcol[:], self.od_bcol[i, :, :], w=[Bw])
            S.dma("sp", brow_f[:], self.od_brow[i, :, :], w=[Bw])
            S.dma("sp", fb[:], self.od_fb[i, :, :].to_broadcast([128, 4]), w=[Bw])
            S.op("dve", lambda e: e.tensor_copy(brow[:], brow_f[:]), r=[Bw], w=[Bw])
            pieces = []
            for k in range(8):
                for c in range(2):
                    pieces.append((wb[:, k, c * 1412:(c + 1) * 1412],
                                   self.od_w[i, k * 128:(k + 1) * 128, c * 1412:(c + 1) * 1412], None))
            self.load_cast(st, None, None, 128, 1412, pieces, Bw, name="odw")
            NXS = 6
            xt = [self.sb(st, f"xt{j}", [128, D]) for j in range(NXS)]
            Bx = S.bufs("xt", NXS)
            hT = [self.sb(st, f"hT{j}", [128, 8, 512], BF16) for j in range(2)]
            Bh = S.bufs("hT", 2)
            ff = [self.sb(st, f"ff{j}", [128, 512]) for j in range(3)]
            Bff = S.bufs("ff", 3)
            fo = [self.sb(st, f"fo{j}", [128, 512], BF16) for j in range(3)]
            Bfo = S.bufs("fo", 3)
            va = [self.sb(st, f"va{j}", [128, 2, 129], BF16) for j in range(2)]
            Bva = S.bufs("va", 2)
            vna = [self.sb(st, f"vna{j}", [128, 4, 65], BF16) for j in range(2)]
            Bvna = S.bufs("vna", 2)
            ot = [self.sb(st, f"ot{j}", [128, 256]) for j in range(2)]
            Bot = S.bufs("ot", 2)
            gt = [self.sb(st, f"gt{j}", [128, D], BF16) for j in range(2)]
            Bgt = S.bufs("gt", 2)
            gs = [self.sb(st, f"gs{j}", [128, 4, 8]) for j in range(2)]
            gtmp = [self.sb(st, f"gtmp{j}", [128, 4, 4]) for j in range(2)]
            Bgs = S.bufs("gs", 2)
            pp = [self.ps(st, f"pp{j}", [128, 512]) for j in range(7)]
            pg = self.ps(st, "pg", [128, 512])
            Bpp = S.bufs("pp", 7)
            Bpg = S.buf("pg")
            for j in range(2):
                S.op("pool", lambda e, j=j: e.memset(va[j][:], 1.0), w=[Bva[j]])
                S.op("pool", lambda e, j=j: e.memset(vna[j][:], 1.0), w=[Bvna[j]])
            self._pp_i = 0

            def next_pp():
                j = self._pp_i % 7
                self._pp_i += 1
                return pp[j], Bpp[j]

            nblk = (NTOK + 511) // 512
            xi = 0
            evac = 0
            ffi = 0
            foi = 0
            for blk in range(nblk):
                t0 = blk * 512
                ntok = min(512, NTOK - t0)
                nsub = ntok // 128
                m = 0 if t0 < LAT else 1
                hb = blk % 2
                for s in range(nsub):
                    xj = xi % NXS
                    xi += 1
                    S.dma("sp", xt[xj][:], src[t0 + s * 128:t0 + (s + 1) * 128, :], w=[Bx[xj]])
                    for half in range(2):
                        p, Bp = next_pp()
                        for q in range(4):
                            dc = half * 4 + q
                            S.op("pe", lambda e, p=p, q=q, dc=dc, xj=xj: e.transpose(
                                p[:, q * 128:(q + 1) * 128], xt[xj][:, dc * 128:(dc + 1) * 128], self.ident_f[:]),
                                r=[Bx[xj], self.B_const], w=[Bp])
                        for q in range(4):
                            dc = half * 4 + q
                            if evac % 2 == 0:
                                S.op("act", lambda e, p=p, q=q, dc=dc, s=s, m=m: e.activation(
                                    out=hT[hb][:, dc, s * 128:(s + 1) * 128], in_=p[:, q * 128:(q + 1) * 128],
                                    func=AF.Identity, scale=self.sc1[:, dc, m:m + 1], bias=self.sh[:, dc, m:m + 1]),
                                    r=[Bp, self.B_mod], w=[Bh[hb]])
                            else:
                                S.op("dve", lambda e, p=p, q=q, dc=dc, s=s, m=m: e.tensor_scalar(
                                    out=hT[hb][:, dc, s * 128:(s + 1) * 128], in0=p[:, q * 128:(q + 1) * 128],
                                    scalar1=self.sc1[:, dc, m:m + 1], scalar2=self.sh[:, dc, m:m + 1],
                                    op0=ALU.mult, op1=ALU.add), r=[Bp, self.B_mod], w=[Bh[hb]])
                            evac += 1
                for c in range(8):
                    p, Bp = next_pp()
                    for k in range(8):
                        S.op("pe", lambda e, p=p, k=k, c=c: e.matmul(
                            p[:, :ntok], wb[:, k, c * 128:(c + 1) * 128], hT[hb][:, k, :ntok], start=(k == 0), stop=(k == 7)),
                            r=[Bw, Bh[hb]], w=[Bp])
                    if c < 4:
                        fj = ffi % 3
                        ffi += 1
                        if c % 2 == 0:
                            S.op("act", lambda e, p=p, c=c, fj=fj: e.activation(
                                out=ff[fj][:, :ntok], in_=p[:, :ntok], func=AF.Identity, bias=bcol[:, c:c + 1]),
                                r=[Bp, Bw], w=[Bff[fj]])
                        else:
                            S.op("dve", lambda e, p=p, c=c, fj=fj: e.tensor_scalar(
                                out=ff[fj][:, :ntok], in0=p[:, :ntok], scalar1=bcol[:, c:c + 1], scalar2=None, op0=ALU.add),
                                r=[Bp, Bw], w=[Bff[fj]])
                        S.dma("pool", self.qkpre[c, :, t0:t0 + ntok], ff[fj][:, :ntok], r=[Bff[fj]])
                    else:
                        fj = foi % 3
                        foi += 1
                        if c % 2 == 0:
                            S.op("act", lambda e, p=p, c=c, fj=fj: e.activation(
                                out=fo[fj][:, :ntok], in_=p[:, :ntok], func=AF.Identity, bias=bcol[:, c:c + 1]),
                                r=[Bp, Bw], w=[Bfo[fj]])
                        else:
                            S.op("dve", lambda e, p=p, c=c, fj=fj: e.tensor_scalar(
                                out=fo[fj][:, :ntok], in0=p[:, :ntok], scalar1=bcol[:, c:c + 1], scalar2=None, op0=ALU.add),
                                r=[Bp, Bw], w=[Bfo[fj]])
                        dd = self.qda if c < 6 else self.kda
                        S.dma("pool", dd[(c - 4) % 2, :, t0:t0 + ntok], fo[fj][:, :ntok], r=[Bfo[fj]])
                gb = blk % 2
                for s in range(nsub):
                    tt = (t0 // 128) + s
                    tok0 = t0 + s * 128

                    def tm(p, c0, n, b0, s=s):
                        for k in range(8):
                            S.op("pe", lambda e, k=k: e.matmul(
                                p[:, 0:n], hT[hb][:, k, s * 128:(s + 1) * 128], wb[:, k, c0:c0 + n], start=(k == 0), stop=False),
                                r=[Bw, Bh[hb]], w=[Bp])
                        S.op("pe", lambda e: e.matmul(p[:, 0:n], self.ones_b[0:1, :], brow[0:1, b0:b0 + n], start=False, stop=True),
                             r=[Bw, self.B_const], w=[Bp])
                    p, Bp = next_pp()
                    tm(p, OV, 256, 0)
                    vj = tt % 2
                    S.op("act", lambda e, p=p, vj=vj: e.activation(
                        out=va[vj][:, :, 0:128], in_=p[:, 0:256].rearrange("p (h c) -> p h c", h=2), func=AF.Copy),
                        r=[Bp], w=[Bva[vj]])
                    S.dma("pool", self.vda[:, :, tt, :].rearrange("h p c -> p h c"), va[vj][:], r=[Bva[vj]])
                    p, Bp = next_pp()
                    tm(p, OO, 256, 256)
                    oj = tt % 2
                    S.op("act", lambda e, p=p, oj=oj: e.activation(out=ot[oj][:], in_=p[:, 0:256], func=AF.Sigmoid), r=[Bp], w=[Bot[oj]])
                    S.dma("pool", self.og[tok0:tok0 + 128, :], ot[oj][:], r=[Bot[oj]])
                    Bp = Bpg
                    for k in range(8):
                        S.op("pe", lambda e, k=k, s=s: e.matmul(
                            pg[:, s * 8:(s + 1) * 8], hT[hb][:, k, s * 128:(s + 1) * 128], wb[:, k, OGT:OGT + 8],
                            start=(k == 0), stop=False), r=[Bw, Bh[hb]], w=[Bpg])
                    S.op("pe", lambda e, s=s: e.matmul(pg[:, s * 8:(s + 1) * 8], self.ones_b[0:1, :], brow[0:1, 512:520],
                                                       start=False, stop=True), r=[Bw, self.B_const], w=[Bpg])
                    p, Bp = next_pp()
                    tm(p, OVN, 256, 520)
                    vj = tt % 2
                    S.op("act", lambda e, p=p, vj=vj: e.activation(
                        out=vna[vj][:, :, 0:64], in_=p[:, 0:256].rearrange("p (h c) -> p h c", h=4), func=AF.Copy),
                        r=[Bp], w=[Bvna[vj]])
                    S.dma("pool", self.vm[:, :, tt, :].rearrange("h p c -> p h c"), vna[vj][:], r=[Bvna[vj]])
                    gj = tt % 2
                    for n in range(2):
                        p, Bp = next_pp()
                        tm(p, OG + n * 512, 512, 776 + n * 512)
                        S.op("act", lambda e, p=p, n=n, gj=gj: e.activation(
                            out=gt[gj][:, n * 512:(n + 1) * 512], in_=p[:, :], func=AF.Silu), r=[Bp], w=[Bgt[gj]])
                    S.dma("pool", self.gate[tok0:tok0 + 128, :], gt[gj][:], r=[Bgt[gj]])
                pgv = pg[:, 0:nsub * 8].rearrange("p (s c) -> p s c", c=8)
                S.op("dve", lambda e, pgv=pgv: e.tensor_copy(gs[gb][:, 0:nsub, 0:4], pgv[:, :, 0:4]), r=[Bpg], w=[Bgs[gb]])
                for s in range(nsub):
                    S.op("dve", lambda e, s=s: e.tensor_tensor(out=gtmp[gb][:, s, :], in0=pg[:, s * 8 + 4:s * 8 + 8], in1=fb[:, :],
                                                               op=ALU.add), r=[Bpg, Bw], w=[Bgs[gb]])
                S.op("act", lambda e: e.activation(out=gtmp[gb][:, 0:nsub, :], in_=gtmp[gb][:, 0:nsub, :], func=AF.Exp, scale=-1.0),
                     r=[Bgs[gb]], w=[Bgs[gb]])
                S.op("act", lambda e: e.activation(out=gtmp[gb][:, 0:nsub, :], in_=gtmp[gb][:, 0:nsub, :], func=AF.Ln, bias=1.0),
                     r=[Bgs[gb]], w=[Bgs[gb]])
                S.op("dve", lambda e: e.tensor_scalar(out=gs[gb][:, 0:nsub, 4:8], in0=gtmp[gb][:, 0:nsub, :], scalar1=-1.0,
                                                      scalar2=None, op0=ALU.mult), r=[Bgs[gb]], w=[Bgs[gb]])
                tt0 = t0 // 128
                S.dma("pool", self.gates[:, tt0:tt0 + nsub, :], gs[gb][:, 0:nsub, :], r=[Bgs[gb]])
            S.barrier()

    def odd_conv(self, i):
        S = self.S
        SEG = 2048
        with contextlib.ExitStack() as st:
            cw = self.sb(st, "cv_w", [128, 4, 5])
            cb = self.sb(st, "cv_b", [128, 4])
            Bw = S.buf("cv_w")
            S.dma("sp", cw[:], self.od_convw[i, :, :, :], w=[Bw])
            S.dma("sp", cb[:], self.od_convb[i, :, :], w=[Bw])
            xin = [self.sb(st, f"cv_x{j}", [128, SEG + 4]) for j in range(2)]
            Bxin = S.bufs("cv_x", 2)
            acc = [self.sb(st, f"cv_a{j}", [128, SEG]) for j in range(2)]
            Bacc = S.bufs("cv_a", 2)
            tmp = [self.sb(st, f"cv_t{j}", [128, SEG]) for j in range(2)]
            Btmp = S.bufs("cv_t", 2)
            outb = [self.sb(st, f"cv_o{j}", [128, SEG], BF16) for j in range(2)]
            Bout = S.bufs("cv_o", 2)
            segs = [(a, SEG, 0, LAT) for a in range(0, LAT, SEG)] + [(LAT, CTX, LAT, LAT + CTX)]
            it = 0
            for c in range(4):
                for (t0, n, lo, hi) in segs:
                    j = it % 2
                    it += 1
                    a = max(t0 - 2, lo)
                    b = min(t0 + n + 2, hi)
                    if a > t0 - 2:
                        S.op("pool", lambda e, j=j: e.memset(xin[j][:, 0:2], 0.0), w=[Bxin[j]])
                    if b < t0 + n + 2:
                        S.op("pool", lambda e, j=j, n=n: e.memset(xin[j][:, n + 2:n + 4], 0.0), w=[Bxin[j]])
                    S.dma("sp", xin[j][:, a - (t0 - 2):b - (t0 - 2)], self.qkpre[c, :, a:b], w=[Bxin[j]])
                    S.op("act", lambda e, j=j, n=n, c=c: e.activation(
                        out=acc[j][:, 0:n], in_=xin[j][:, 0:n], func=AF.Copy, scale=cw[:, c, 0:1]),
                        r=[Bxin[j], Bw], w=[Bacc[j]])
                    for k in range(1, 5):
                        S.op("dve", lambda e, j=j, n=n, c=c, k=k: e.scalar_tensor_tensor(
                            out=acc[j][:, 0:n], in0=xin[j][:, k:k + n], scalar=cw[:, c, k:k + 1], in1=acc[j][:, 0:n],
                            op0=ALU.mult, op1=ALU.add), r=[Bxin[j], Bw, Bacc[j]], w=[Bacc[j]])
                    if True:
                        S.op("act", lambda e, j=j, n=n, c=c: e.activation(
                            out=outb[j][:, 0:n], in_=acc[j][:, 0:n], func=AF.Silu, bias=cb[:, c:c + 1]),
                            r=[Bacc[j], Bw], w=[Bout[j]])
                        dd = self.mq if c < 2 else self.mk
                        S.dma("pool", dd[c % 2, :, t0:t0 + n], outb[j][:, 0:n], r=[Bout[j]])
                    else:
                        S.op("act", lambda e, j=j, n=n, c=c: e.activation(
                            out=tmp[j][:, 0:n], in_=acc[j][:, 0:n], func=AF.Silu, bias=cb[:, c:c + 1]),
                            r=[Bacc[j], Bw], w=[Btmp[j]])
                        S.op("pool", lambda e, j=j, n=n: e.tensor_scalar(
                            out=outb[j][:, 0:n], in0=tmp[j][:, 0:n], scalar1=128 ** -0.5, scalar2=None, op0=ALU.mult),
                            r=[Btmp[j]], w=[Bout[j]])
                        S.dma("sp", self.mk[c - 4, :, t0:t0 + n], outb[j][:, 0:n], r=[Bout[j]])
            S.barrier()

    def mlstm(self, i):
        S = self.S
        with contextlib.ExitStack() as st:
            tri = self.sb(st, "ml_tri", [128, 3, 128])
            G = self.sb(st, "ml_G", [128, NT, 8])
            Ao = self.sb(st, "ml_Ao", [128, NT, 4])
            A2o = self.sb(st, "ml_A2o", [128, NT, 4])
            Bqo = self.sb(st, "ml_Bqo", [128, NT, 4])
            EBo = self.sb(st, "ml_EBo", [128, NT, 4])
            Bg = S.buf("ml_g")
            S.dma("sp", tri[:], self.tri[:, :, :], w=[Bg])
            S.dma("sp", G[:], self.gates[:, :, :], w=[Bg])
            lnks = self.sb(st, "ml_lnks", [128, 1])
            S.op("pool", lambda e: e.memset(lnks[:], math.log(128 ** -0.5)), w=[Bg])
            with contextlib.ExitStack() as st1:
                pg = [self.ps(st1, f"ml_pg{j}", [128, 512]) for j in range(3)]
                Bpg = S.bufs("ml_pg", 3)
                tmpg = self.sb(st1, "ml_tmpg", [128, 32, 4])
                Btg = S.buf("ml_tmpg")
                grp = 0
                for g0 in range(0, NT, 32):
                    g1 = min(g0 + 32, NT)
                    pj = grp % 3
                    grp += 1
                    for t in range(g0, g1):
                        o = (t - g0) * 8
                        S.op("pe", lambda e, t=t, o=o, pj=pj: e.matmul(pg[pj][:, o:o + 2], tri[:, 0, :], G[:, t, 4:6], start=True, stop=True),
                             r=[Bg], w=[Bpg[pj]])
                        S.op("pe", lambda e, t=t, o=o, pj=pj: e.matmul(pg[pj][:, o + 2:o + 4], tri[:, 1, :], G[:, t, 6:8], start=True, stop=True),
                             r=[Bg], w=[Bpg[pj]])
                        S.op("pe", lambda e, t=t, o=o, pj=pj: e.matmul(pg[pj][:, o + 4:o + 8], tri[:, 2, :], G[:, t, 4:8], start=True, stop=True),
                             r=[Bg], w=[Bpg[pj]])
                    n = g1 - g0
                    pv = pg[pj][:, 0:n * 8].rearrange("p (t c) -> p t c", c=8)
                    S.op("dve", lambda e, pv=pv, g0=g0, g1=g1, n=n: e.tensor_tensor(
                        out=tmpg[:, 0:n, :], in0=G[:, g0:g1, 0:4], in1=pv[:, :, 0:4], op=ALU.subtract), r=[Bg, Bpg[pj]], w=[Btg])
                    S.op("act", lambda e, g0=g0, g1=g1, n=n: e.activation(out=Ao[:, g0:g1, :], in_=tmpg[:, 0:n, :], func=AF.Exp,
                                                                          bias=lnks[:, 0:1]), r=[Btg, Bg], w=[Bg])
                    S.op("act", lambda e, pv=pv, g0=g0, g1=g1: e.activation(out=Bqo[:, g0:g1, :], in_=pv[:, :, 0:4], func=AF.Exp),
                         r=[Bpg[pj]], w=[Bg])
                    S.op("act", lambda e, pv=pv, g0=g0, g1=g1: e.activation(out=EBo[:, g0:g1, :], in_=pv[:, :, 4:8], func=AF.Exp),
                         r=[Bpg[pj]], w=[Bg])
                    S.op("dve", lambda e, g0=g0, g1=g1: e.tensor_tensor(out=A2o[:, g0:g1, :], in0=Ao[:, g0:g1, :], in1=EBo[:, g0:g1, :],
                                                                        op=ALU.mult), r=[Bg], w=[Bg])
                S.barrier()
            qT = self.sb(st, "ml_qT", [128, NTOK], BF16)
            kT = self.sb(st, "ml_kT", [128, NTOK], BF16)
            V = self.sb(st, "ml_V", [128, NT, 129], BF16)
            KTOK = self.sb(st, "ml_KTOK", [128, NT, 128], BF16)
            Bin = S.buf("ml_in")
            Bkt = S.buf("ml_ktok")
            Cn = [self.sb(st, f"ml_Cn{d}", [128, 129]) for d in range(2)]
            Cnb = [[self.sb(st, f"ml_Cnb{d}{j}", [128, 129], BF16) for j in range(2)] for d in range(2)]
            BCn = S.bufs("ml_Cn", 2)
            BCnb = [S.bufs(f"ml_Cnb{d}", 2) for d in range(2)]
            NW = 6
            W = [self.sb(st, f"ml_W{j}", [128, 128], BF16) for j in range(NW)]
            BW = S.bufs("ml_W", NW)
            v2 = [self.sb(st, f"ml_v2{j}", [128, 129], BF16) for j in range(NW)]
            Bv2 = S.bufs("ml_v2", NW)
            ho = [self.sb(st, f"ml_ho{j}", [128, 128]) for j in range(NW)]
            Bho = S.bufs("ml_ho", NW)
            sm = [self.sb(st, f"ml_sm{j}", [128, 4]) for j in range(NW)]
            Bsm = S.bufs("ml_sm", NW)
            NS, NKV, NN = 2, 2, 4
            order = [[NT_LAT, NT_LAT + 1] + list(range(NT_LAT)), [NT_LAT + 1, NT_LAT] + list(range(NT_LAT - 1, -1, -1))]
            wi = 0
            for h in range(2):
                S.dma("sp", qT[:], self.mq[h, :, :], w=[Bin])
                S.dma("act", kT[:], self.mk[h, :, :], w=[Bin])
                S.dma("sp", V[:], self.vda[h, :, :, :], w=[Bin])
                with contextlib.ExitStack() as stp:
                    ppbs = [self.ps(stp, f"ml_ppb{j}", [128, 1024], BF16) for j in range(2)]
                    Bppbs = S.bufs("ml_ppb", 2)
                    for gi, t0 in enumerate(range(0, NT, 8)):
                        n = min(8, NT - t0)
                        ppb, Bppb = ppbs[gi % 2], Bppbs[gi % 2]
                        for q in range(n):
                            t = t0 + q
                            S.op("pe", lambda e, t=t, q=q, ppb=ppb: e.transpose(
                                ppb[:, q * 128:(q + 1) * 128], kT[:, t * 128:(t + 1) * 128], self.ident_b[:]),
                                r=[Bin, self.B_const], w=[Bppb])
                        S.op("act", lambda e, t0=t0, n=n, ppb=ppb: e.activation(
                            out=KTOK[:, t0:t0 + n, :], in_=ppb[:, 0:n * 128].rearrange("p (t c) -> p t c", c=128), func=AF.Copy),
                            r=[Bppb], w=[Bkt])
                    S.barrier()
                stq = contextlib.ExitStack()
                ps_s = [self.ps(stq, f"ml_pss{j}", [128, 512]) for j in range(NS)]
                ps_kv = [self.ps(stq, f"ml_pkv{j}", [128, 512]) for j in range(NKV)]
                ps_n = [self.ps(stq, f"ml_pn{j}", [128, 512]) for j in range(NN)]
                Bps_s, Bps_kv, Bps_n = S.bufs("ml_pss", NS), S.bufs("ml_pkv", NKV), S.bufs("ml_pn", NN)
                for d in range(2):
                    S.op("pool", lambda e, d=d: e.memset(Cn[d][:], 0.0), w=[BCn[d]])
                    S.op("pool", lambda e, d=d: e.memset(Cnb[d][0][:], 0.0), w=[BCnb[d][0]])

                def stage1(step, d, wi):
                    t = order[d][step]
                    hd = d * 2 + h
                    j = wi % NW
                    p3 = wi % NS
                    tsl = slice(t * 128, (t + 1) * 128)
                    S.op("pe", lambda e: e.matmul(ps_s[p3][:, 0:128], kT[:, tsl], qT[:, tsl], start=True, stop=True),
                         r=[Bin], w=[Bps_s[p3]])
                    S.op("dve", lambda e: e.scalar_tensor_tensor(
                        out=W[j][:], in0=ps_s[p3][:, 0:128], scalar=Ao[:, t, hd:hd + 1], in1=tri[:, d, :],
                        op0=ALU.mult, op1=ALU.mult), r=[Bps_s[p3], Bg], w=[BW[j]])
                    S.op("act", lambda e: e.activation(
                        out=v2[j][:], in_=V[:, t, :], func=AF.Copy, scale=A2o[:, t, hd:hd + 1]),
                        r=[Bin, Bg], w=[Bv2[j]])

                def stage2(step, d, wi):
                    t = order[d][step]
                    hd = d * 2 + h
                    cur, nxt = step % 2, (step + 1) % 2
                    j = wi % NW
                    p3 = wi % NKV
                    p6 = wi % NN
                    tsl = slice(t * 128, (t + 1) * 128)
                    S.op("pe", lambda e: e.matmul(ps_kv[p3][:, 0:129], KTOK[:, t, :], v2[j][:], start=True, stop=True),
                         r=[Bkt, Bv2[j]], w=[Bps_kv[p3]])
                    S.op("pe", lambda e: e.matmul(ps_n[p6][:, 0:129], W[j][:], V[:, t, :], start=True, stop=False),
                         r=[BW[j], Bin], w=[Bps_n[p6]])
                    S.op("pe", lambda e: e.matmul(ps_n[p6][:, 0:129], qT[:, tsl], Cnb[d][cur][:], start=False, stop=True),
                         r=[Bin, BCnb[d][cur]], w=[Bps_n[p6]])
                    S.op("dve", lambda e: e.scalar_tensor_tensor(
                        out=Cn[d][:], in0=Cn[d][:], scalar=EBo[:, t, hd:hd + 1], in1=ps_kv[p3][:, 0:129],
                        op0=ALU.mult, op1=ALU.add), r=[BCn[d], Bps_kv[p3], Bg], w=[BCn[d]])
                    S.op("act", lambda e: e.activation(out=Cnb[d][nxt][:], in_=Cn[d][:], func=AF.Copy),
                         r=[BCn[d]], w=[BCnb[d][nxt]])

                def back(step, d, wi):
                    t = order[d][step]
                    hd = d * 2 + h
                    j = wi % NW
                    p6 = wi % NN
                    S.op("dve", lambda e: e.tensor_tensor(
                        out=sm[j][:, 0:1], in0=ps_n[p6][:, 128:129], in1=Bqo[:, t, hd:hd + 1], op=ALU.mult),
                        r=[Bps_n[p6], Bg], w=[Bsm[j]])
                    S.op("dve", lambda e: e.scalar_tensor_tensor(out=sm[j][:, 1:2], in0=sm[j][:, 0:1], scalar=-1.0,
                                                                 in1=sm[j][:, 0:1], op0=ALU.mult, op1=ALU.max),
                         r=[Bsm[j]], w=[Bsm[j]])
                    S.op("dve", lambda e: e.tensor_scalar(out=sm[j][:, 1:2], in0=sm[j][:, 1:2], scalar1=1.0, scalar2=None,
                                                          op0=ALU.max), r=[Bsm[j]], w=[Bsm[j]])
                    S.op("dve", lambda e: e.reciprocal(sm[j][:, 2:3], sm[j][:, 1:2]), r=[Bsm[j]], w=[Bsm[j]])
                    S.op("dve", lambda e: e.tensor_tensor(
                        out=sm[j][:, 3:4], in0=sm[j][:, 2:3], in1=Bqo[:, t, hd:hd + 1], op=ALU.mult), r=[Bsm[j], Bg], w=[Bsm[j]])
                    S.op("dve", lambda e: e.tensor_scalar(
                        out=ho[j][:], in0=ps_n[p6][:, 0:128], scalar1=sm[j][:, 3:4], scalar2=None, op0=ALU.mult),
                        r=[Bps_n[p6], Bsm[j]], w=[Bho[j]])
                    S.dma("sp", self.hfb[d, t * 128:(t + 1) * 128, h * 128:(h + 1) * 128], ho[j][:], r=[Bho[j]])

                items = []
                for step in range(NT):
                    for d in range(2):
                        items.append((step, d, wi))
                        wi += 1
                npair = len(items) // 2
                for k in range(npair + 2):
                    if k < npair:
                        stage1(*items[2 * k])
                        stage1(*items[2 * k + 1])
                    if 0 <= k - 1 < npair:
                        stage2(*items[2 * (k - 1)])
                        stage2(*items[2 * (k - 1) + 1])
                    if 0 <= k - 2 < npair:
                        back(*items[2 * (k - 2)])
                        back(*items[2 * (k - 2) + 1])
                S.barrier()
                stq.close()
            S.barrier()

    def mlstm_post(self, i):
        S = self.S
        with contextlib.ExitStack() as st:
            ngo = self.sb(st, "mp_ngo", [128, 256])
            Bw = S.buf("mp_w")
            S.dma("sp", ngo[:], self.od_ng[i, :, :].to_broadcast([128, 256]), w=[Bw])
            eps_t = self.sb(st, "mp_eps", [128, 1])
            S.op("pool", lambda e: e.memset(eps_t[:], LN_EPS), w=[Bw])
            NB = 3
            hf = [self.sb(st, f"mp_hf{j}", [128, 256]) for j in range(NB)]
            hb = [self.sb(st, f"mp_hb{j}", [128, 256]) for j in range(NB)]
            ogo = [self.sb(st, f"mp_ogo{j}", [128, 256]) for j in range(NB)]
            hs = [self.sb(st, f"mp_hs{j}", [128, 256]) for j in range(NB)]
            yo = [self.sb(st, f"mp_yo{j}", [128, 256]) for j in range(NB)]
            yob = [self.sb(st, f"mp_yob{j}", [128, 256], BF16) for j in range(NB)]
            stt = [self.sb(st, f"mp_st{j}", [128, 2, 12]) for j in range(NB)]
            Bhf, Bhb, Bogo, Bhs, Byo, Bst = (S.bufs(n, NB) for n in ("mp_hf", "mp_hb", "mp_ogo", "mp_hs", "mp_yo", "mp_st"))

            def post_a(t):
                j = t % NB
                tok0 = t * 128
                S.dma("sp", hf[j][:], self.hfb[0, tok0:tok0 + 128, 0:256], w=[Bhf[j]])
                S.dma("act", hb[j][:], self.hfb[1, tok0:tok0 + 128, 0:256], w=[Bhb[j]])
                S.dma("pool", ogo[j][:], self.og[tok0:tok0 + 128, :], w=[Bogo[j]])
                S.op("pool", lambda e, j=j: e.tensor_tensor(out=hs[j][:], in0=hf[j][:], in1=hb[j][:], op=ALU.add),
                     r=[Bhf[j], Bhb[j]], w=[Bhs[j]])
                S.op("pool", lambda e, j=j: e.tensor_tensor(out=ogo[j][:], in0=ogo[j][:], in1=ngo[:], op=ALU.mult),
                     r=[Bogo[j], Bw], w=[Bogo[j]])

            def post_b(t):
                j = t % NB
                for h in range(2):
                    S.op("dve", lambda e, j=j, h=h: e.bn_stats(stt[j][:, h, 0:6], hs[j][:, h * 128:(h + 1) * 128]),
                         r=[Bhs[j]], w=[Bst[j]])
                    S.op("dve", lambda e, j=j, h=h: e.bn_aggr(stt[j][:, h, 6:8], stt[j][:, h, 0:6]), r=[Bst[j]], w=[Bst[j]])
                S.op("act", lambda e, j=j: e.activation(out=stt[j][:, :, 8:9], in_=stt[j][:, :, 7:8], func=AF.Sqrt, scale=1.0,
                                                        bias=eps_t[:, 0:1]), r=[Bst[j], Bw], w=[Bst[j]])
                S.op("dve", lambda e, j=j: e.reciprocal(stt[j][:, :, 9:10], stt[j][:, :, 8:9]), r=[Bst[j]], w=[Bst[j]])
                for h in range(2):
                    S.op("dve", lambda e, j=j, h=h: e.tensor_scalar(
                        out=yo[j][:, h * 128:(h + 1) * 128], in0=hs[j][:, h * 128:(h + 1) * 128], scalar1=stt[j][:, h, 6:7],
                        scalar2=stt[j][:, h, 9:10], op0=ALU.subtract, op1=ALU.mult), r=[Bhs[j], Bst[j]], w=[Byo[j]])
                S.op("pool", lambda e, j=j: e.tensor_tensor(out=yob[j][:], in0=yo[j][:], in1=ogo[j][:], op=ALU.mult),
                     r=[Byo[j], Bogo[j]], w=[Byo[j]])
                k, r0 = self.yodd_loc(t)
                S.dma("sp", self.yodd_t[k].ap()[r0:r0 + 128, 0:256], yob[j][:], r=[Byo[j]])

            for t in range(NT):
                if t == 0:
                    post_a(0)
                if t + 1 < NT:
                    post_a(t + 1)
                post_b(t)
            S.barrier()

    def na_attention(self, i, last):
        S = self.S
        with contextlib.ExitStack() as st:
            QNe = self.sb(st, "na_Qe", [128, NTOK], BF16)
            QNo = self.sb(st, "na_Qo", [128, NTOK], BF16)
            KN = self.sb(st, "na_K", [128, NTOK], BF16)
            VN = self.sb(st, "na_V", [128, NT, 65], BF16)
            MB = self.sb(st, "na_MB", [128, NA_NVAR, 128], BF16)
            Bqk = S.buf("na_qk")
            Bv = S.buf("na_v")
            Bmb = S.buf("na_mb")
            stg = [self.sb(st, f"na_stg{j}", [128, 7, 128]) for j in range(2)]
            Bstg = S.bufs("na_stg", 2)
            NP = 3
            ptA = [self.sb(st, f"na_ptA{j}", [128, 512], BF16) for j in range(NP)]
            ptB = [self.sb(st, f"na_ptB{j}", [128, 384], BF16) for j in range(NP)]
            BptA, BptB = S.bufs("na_ptA", NP), S.bufs("na_ptB", NP)
            rr = [self.sb(st, f"na_rr{j}", [128, 2]) for j in range(NP)]
            to = [self.sb(st, f"na_to{j}", [128, 64], BF16) for j in range(NP)]
            Bto = S.bufs("na_to", NP)
            psA = [self.ps(st, f"na_psA{j}", [128, 512]) for j in range(2)]
            psB = [self.ps(st, f"na_psB{j}", [128, 512]) for j in range(2)]
            acc = [self.ps(st, f"na_acc{j}", [128, 512]) for j in range(2)]
            BpsA, BpsB, Bacc = S.bufs("na_psA", 2), S.bufs("na_psB", 2), S.bufs("na_acc", 2)
            it = 0
            qtiles = list(range(NT_LAT)) + ([] if (last and self.final_out) else [NT_LAT, NT_LAT + 1])
            if self.tile_filter is not None:
                qtiles = [t for t in qtiles if t in self.tile_filter]
            for h in range(4):
                lo = (h % 2) * 64
                if h == 0:
                    S.op("pool", lambda e: e.memset(QNe[64:128, :], 0.0), w=[Bqk])
                    S.op("pool", lambda e: e.memset(QNo[0:64, :], 0.0), w=[Bqk])
                if h % 2 == 0:
                    hp = h // 2
                    S.dma("sp", QNe[0:64, :], self.qda[hp, 0:64, :], w=[Bqk])
                    S.dma("act", QNo[64:128, :], self.qda[hp, 64:128, :], w=[Bqk])
                    S.dma("sp", KN[:], self.kda[hp, :, :], w=[Bqk])
                QN = QNe if h % 2 == 0 else QNo
                S.dma("act", VN[:], self.vm[h, :, :, :], w=[Bv])
                for v0 in range(0, NA_NVAR, 7):
                    n = min(7, NA_NVAR - v0)
                    sj = (v0 // 7) % 2
                    S.dma("sp", stg[sj][:, 0:n, :], self.od_nat[i, h, :, v0:v0 + n, :], w=[Bstg[sj]])
                    S.op("dve", lambda e, sj=sj, n=n, v0=v0: e.tensor_scalar(
                        out=MB[:, v0:v0 + n, :], in0=stg[sj][:, 0:n, :], scalar1=1.0 / NA_SCALE, scalar2=None, op0=ALU.mult),
                        r=[Bstg[sj]], w=[Bmb])
                for j in qtiles:
                    pj = it % 2
                    tj = it % NP
                    it += 1
                    qsl = slice(j * 128, (j + 1) * 128)
                    if j < NT_LAT:
                        slots = [(kt, var) for (kt, var) in NA_PLAN[j]] + [(NT_LAT, None), (NT_LAT + 1, None)]
                    else:
                        slots = [(NT_LAT, None), (NT_LAT + 1, None)]
                    nA = min(4, len(slots))
                    nB = len(slots) - nA
                    for si, (kt, var) in enumerate(slots):
                        if si < 4:
                            dst, Bd = psA[pj][:, si * 128:(si + 1) * 128], BpsA[pj]
                        else:
                            dst, Bd = psB[pj][:, (si - 4) * 128:(si - 3) * 128], BpsB[pj]
                        S.op("pe", lambda e, dst=dst, kt=kt, var=var, QN=QN: e.matmul(
                            dst, KN[:, kt * 128:(kt + 1) * 128], QN[:, qsl], start=True, stop=(var is None)),
                            r=[Bqk], w=[Bd])
                        if var is not None:
                            S.op("pe", lambda e, dst=dst, var=var: e.matmul(dst, self.ident_b[:], MB[:, var, :], start=False, stop=True),
                                 r=[Bmb, self.B_const], w=[Bd])
                    S.op("act", lambda e, pj=pj, tj=tj, nA=nA: e.activation(out=ptA[tj][:, 0:nA * 128], in_=psA[pj][:, 0:nA * 128],
                                                                            func=AF.Exp, scale=NA_SCALE), r=[BpsA[pj]], w=[BptA[tj]])
                    if nB > 0:
                        S.op("act", lambda e, pj=pj, tj=tj, nB=nB: e.activation(out=ptB[tj][:, 0:nB * 128], in_=psB[pj][:, 0:nB * 128],
                                                                                func=AF.Exp, scale=NA_SCALE), r=[BpsB[pj]], w=[BptB[tj]])
                    for si, (kt, var) in enumerate(slots):
                        if si < 4:
                            lhs, Bl = ptA[tj][:, si * 128:(si + 1) * 128], BptA[tj]
                        else:
                            lhs, Bl = ptB[tj][:, (si - 4) * 128:(si - 3) * 128], BptB[tj]
                        S.op("pe", lambda e, lhs=lhs, kt=kt, si=si, pj=pj: e.matmul(
                            acc[pj][:, 0:65], lhs, VN[:, kt, :], start=(si == 0), stop=(si == len(slots) - 1)),
                            r=[Bl, Bv], w=[Bacc[pj]])
                    S.op("dve", lambda e, pj=pj, tj=tj: e.reciprocal(rr[tj][:, 0:1], acc[pj][:, 64:65]), r=[Bacc[pj]], w=[Bto[tj]])
                    S.op("dve", lambda e, pj=pj, tj=tj: e.tensor_scalar(out=to[tj][:], in0=acc[pj][:, 0:64], scalar1=rr[tj][:, 0:1],
                                                                        scalar2=None, op0=ALU.mult), r=[Bacc[pj], Bto[tj]], w=[Bto[tj]])
                    kk, r0 = self.yodd_loc(j)
                    S.dma("pool", self.yodd_t[kk].ap()[r0:r0 + 128, 256 + h * 64:256 + (h + 1) * 64], to[tj][:], r=[Bto[tj]])
            S.barrier()


def _swap64(cols):
    return np.concatenate([cols[32:64], cols[0:32]])


def _rope_tables():
    t = np.arange(LAT)
    row = (t // GRID_W).astype(np.float32)
    col = (t % GRID_W).astype(np.float32)

    def tab(dim, nrows_pattern):
        n_freq = dim // 4
        freqs = (10000.0 ** (-np.arange(n_freq, dtype=np.float32) / n_freq)).astype(np.float32)
        ang = np.concatenate([row[:, None] * freqs, col[:, None] * freqs], axis=-1).astype(np.float32)
        cos = np.cos(ang).astype(np.float32).T
        sin = np.sin(ang).astype(np.float32).T
        half = dim // 2
        c = np.concatenate([cos, cos], 0)
        s = np.concatenate([-sin, sin], 0)
        c = np.concatenate([c, np.ones((dim, CTX), np.float32)], 1)
        s = np.concatenate([s, np.zeros((dim, CTX), np.float32)], 1)
        return c, s

    c64, s64 = tab(64, None)
    c32, s32 = tab(32, None)
    rope_da = np.stack([np.concatenate([c64, c64], 0), np.concatenate([s64, s64], 0)]).astype(np.float32)
    ml_c = np.ones((128, NTOK), np.float32)
    ml_s = np.zeros((128, NTOK), np.float32)
    ml_c[0:32] = c32
    ml_s[0:32] = s32
    ml_c[64:96] = c32
    ml_s[64:96] = s32
    rope_ml = np.stack([ml_c, ml_s]).astype(np.float32)
    return rope_da, rope_ml


def _even_layout(inp, par):
    ev_w_in, ev_b_in = inp["ev_w_in"], inp["ev_b_in"]
    hs = [2 * par, 2 * par + 1]
    ms = [4 * par + k for k in range(4)]
    o_q1, o_q2, o_k1, o_k2, o_v, o_cq, o_ckv, o_kr, o_g = 0, 256, 512, 768, 1024, 1536, 1792, 1920, 1952
    cols = []
    for (a, b) in ((o_q1, o_q2), (o_k1, o_k2)):
        main = []
        sw = []
        for h in hs:
            c1 = np.arange(a + h * 64, a + (h + 1) * 64)
            c2 = np.arange(b + h * 64, b + (h + 1) * 64)
            main += [c1, c2]
            sw += [_swap64(c1), _swap64(c2)]
        cols += main + sw
    kr = np.arange(o_kr, o_kr + 32)
    cols += [kr, np.concatenate([kr[16:], kr[:16]])]
    cols += [np.arange(o_v + h * 128, o_v + (h + 1) * 128) for h in hs]
    cols += [np.arange(o_cq, o_cq + 384), np.arange(o_g, o_g + 1024)]
    perm = np.concatenate(cols)
    assert perm.shape[0] == EV_COLS
    ev_w = np.ascontiguousarray(ev_w_in[:, :, perm])
    bp = ev_b_in[:, perm]
    bcol = np.zeros((2, 128, 10), np.float32)
    for g in range(8):
        bcol[:, :, g] = bp[:, g * 128:(g + 1) * 128]
    bcol[:, 0:32, 8] = bp[:, EKR:EKR + 32]
    bcol[:, 0:32, 9] = bp[:, EKR + 32:EKR + 64]
    brow = np.ascontiguousarray(bp[:, EV_:EV_ + 1664])[:, None, :]
    wuq = inp["mla_w_uq"]
    mainc = []
    swc = []
    for h in ms:
        c = np.arange(h * 96, h * 96 + 96)
        r = c[64:96]
        mainc.append(c)
        swc.append(np.concatenate([c[0:64], r[16:], r[:16]]))
    ev_wuq = np.ascontiguousarray(wuq[:, :, np.concatenate(mainc + swc)])
    wukv = inp["mla_w_ukv"]
    nope = np.concatenate([np.arange(h * 128, h * 128 + 64) for h in ms])
    vv = np.concatenate([np.arange(h * 128 + 64, h * 128 + 128) for h in ms])
    ev_wukv = np.ascontiguousarray(wukv[:, :, np.concatenate([nope, vv])])
    return dict(
        ev_w=ev_w, ev_bcol=bcol, ev_brow=np.ascontiguousarray(brow),
        ev_lam=np.ascontiguousarray(inp["da_lambda"].reshape(2, 1, 256)),
        ev_subg=np.ascontiguousarray(inp["da_subln_g"].reshape(2, 1, 128)),
        ev_qg=np.ascontiguousarray(inp["mla_q_norm_g"].reshape(2, 2, 128).transpose(0, 2, 1)),
        ev_kvg=np.ascontiguousarray(inp["mla_kv_norm_g"].reshape(2, 128, 1)),
        ev_wuq=ev_wuq, ev_wukv=ev_wukv, ev_wo=np.ascontiguousarray(inp["ev_w_out"]),
    )


def _odd_layout(inp, par):
    w, b = inp["od_w_in"], inp["od_b_in"]
    hs = [2 * par, 2 * par + 1]
    ns = [4 * par + k for k in range(4)]
    g0 = 2048
    cols = [np.arange(h * 128, (h + 1) * 128) for h in hs]
    cols += [np.arange(512 + h * 128, 512 + (h + 1) * 128) for h in hs]
    cols += [np.arange(2064 + n * 64, 2064 + (n + 1) * 64) for n in ns]
    cols += [np.arange(2576 + n * 64, 2576 + (n + 1) * 64) for n in ns]
    cols += [np.arange(1024 + h * 128, 1024 + (h + 1) * 128) for h in hs]
    cols += [np.arange(1536 + h * 128, 1536 + (h + 1) * 128) for h in hs]
    cols += [np.array([g0 + j * 4 + h for h in hs]) for j in (0, 2, 1, 3)]
    cols += [np.arange(3088 + n * 64, 3088 + (n + 1) * 64) for n in ns]
    cols += [np.arange(3600, 4624)]
    perm = np.concatenate(cols)
    assert perm.shape[0] == OD_COLS
    od_w = np.ascontiguousarray(w[:, :, perm])
    bp = b[:, perm]
    bcol = np.ascontiguousarray(bp[:, 0:1024].reshape(2, 8, 128).transpose(0, 2, 1))
    brow = np.ascontiguousarray(bp[:, 1024:])[:, None, :]
    ch = np.concatenate([np.arange(h * 128, (h + 1) * 128) for h in hs] + [np.arange(512 + h * 128, 512 + (h + 1) * 128) for h in hs])
    convw = np.ascontiguousarray(inp["ml_conv_w"][:, :, ch].reshape(2, 5, 4, 128).transpose(0, 3, 2, 1))
    convb = np.ascontiguousarray(inp["ml_conv_b"][:, ch].reshape(2, 4, 128).transpose(0, 2, 1))
    fb = np.ascontiguousarray(inp["ml_f_bias"][:, :, hs].reshape(2, 1, 4))
    ng = np.ascontiguousarray(np.concatenate([inp["ml_norm_g"][:, h * 128:(h + 1) * 128] for h in hs], 1).reshape(2, 1, 256))
    rpb = inp["na_rpb"][:, ns]
    nat = np.full((2, 4, 128, NA_NVAR, 128), NEG, np.float32)
    k = np.arange(128)
    krl, kc = k // 64, k % 64
    q = np.arange(128)
    qrl, qc = q // 64, q % 64
    cs = np.clip(qc - NA_COLS // 2, 0, GRID_W - NA_COLS)
    for (dk, o0, o1), vid in NA_VARIANTS.items():
        rel = (2 * dk + krl[:, None]) - qrl[None, :]
        off = np.where(qrl[None, :] == 0, o0, o1)
        valid_r = (rel >= off) & (rel <= off + NA_ROWS - 1)
        valid_c = (kc[:, None] >= cs[None, :]) & (kc[:, None] <= cs[None, :] + NA_COLS - 1)
        valid = valid_r & valid_c
        ridx = np.clip(rel + NA_ROWS - 1, 0, 2 * NA_ROWS - 2)
        cidx = np.clip(kc[:, None] - qc[None, :] + NA_COLS - 1, 0, 2 * NA_COLS - 2)
        tab = rpb[:, :, ridx, cidx]
        nat[:, :, :, vid, :] = np.where(valid[None, None], tab, np.float32(NEG))
    return dict(od_w=od_w, od_bcol=bcol, od_brow=np.ascontiguousarray(brow), od_convw=convw, od_convb=convb, od_fb=fb,
                od_ng=ng, od_nat=nat, od_wo=np.ascontiguousarray(inp["od_w_out"]))


def make_in_maps(inp, batches):
    inp = {k: np.asarray(v, dtype=np.float32) for k, v in inp.items()}
    rope_da, rope_ml = _rope_tables()
    common = dict(
        ada_w=inp["ada_w"], ada_b=inp["ada_b"], ln_g=inp["ln_g"], ln_b=inp["ln_b"],
        ident=np.eye(128, dtype=np.float32),
        sel=np.concatenate([np.stack([np.ones(128), np.zeros(128)]), np.stack([np.zeros(128), np.ones(128)])], 1).astype(np.float32),
        rope_da=rope_da, rope_ml=rope_ml,
    )
    tri = np.stack([np.triu(np.ones((128, 128), np.float32)), np.tril(np.ones((128, 128), np.float32)),
                    np.ones((128, 128), np.float32)], 1)
    common["tri"] = np.ascontiguousarray(tri)
    ev = [dict(_even_layout(inp, p), **_odd_layout(inp, p)) for p in (0, 1)]
    maps = []
    for b in batches:
        m = dict(common)
        m.update(ev[len(maps) % 2])
        m["x_in"] = np.ascontiguousarray(np.concatenate([inp["x"][b], inp["ctx"][b]], 0))
        cc = np.stack([inp["c"][b], inp["c_ctx"]], -1)
        m["cc"] = np.ascontiguousarray(cc.reshape(8, 128, 2).transpose(1, 0, 2))
        par = len(maps) % 2
        ps = np.zeros((128, 2), np.float32)
        ps[:, par] = 1.0
        m["psel"] = ps
        maps.append(m)
    return maps


_PROG_CACHE = {}


def kernel(**inputs):
    batches = [c // 2 for c in range(8)]
    in_maps = make_in_maps(inputs, batches)
    prog = Prog()
    nc = prog.build()
    res = run_bass_kernel_spmd(nc, in_maps, core_ids=list(range(8)))
    out = np.stack([res.results[2 * b]["y"] for b in range(4)], 0)
    return out.astype(np.float32)
```

```python
import contextlib
import math
import numpy as np
import concourse.bass as bass
import concourse.mybir as mybir
from concourse.bass_utils import run_bass_kernel_spmd

F32 = mybir.dt.float32
BF16 = mybir.dt.bfloat16
AF = mybir.ActivationFunctionType
ALU = mybir.AluOpType
AX = mybir.AxisListType

D = 1024
LAT = 8192
CTX = 256
NTOK = LAT + CTX
NT = NTOK // 128
NT_LAT = LAT // 128
DEPTH = 4
GRID_W = 64
ALPHA = (2.0 * DEPTH) ** 0.25
LN_EPS = 1e-5
RMS_EPS = 1e-6
DA_SCALE = 64 ** -0.5
MLA_SCALE = 96 ** -0.5
NA_SCALE = 64 ** -0.5
EV_COLS = 2752
EQ, EK, EKR, EV_, EC, EG = 0, 512, 1024, 1088, 1344, 1728
OD_COLS = 2824
OQK, OQN, OKN, OV, OO, OGT, OVN, OG = 0, 512, 768, 1024, 1280, 1536, 1544, 1800
NA_ROWS, NA_COLS, GRID_H = 8, 16, 128
NEG = -30000.0


def na_plan():
    variants = {}
    plan = []
    for j in range(NT_LAT):
        r0, r1 = 2 * j, 2 * j + 1
        rs0 = min(max(r0 - NA_ROWS // 2, 0), GRID_H - NA_ROWS)
        rs1 = min(max(r1 - NA_ROWS // 2, 0), GRID_H - NA_ROWS)
        lst = []
        for kt in range(rs0 // 2, (rs1 + NA_ROWS - 1) // 2 + 1):
            key = (kt - j, rs0 - r0, rs1 - r1)
            if key not in variants:
                variants[key] = len(variants)
            lst.append((kt, variants[key]))
        plan.append(lst)
    return plan, variants


NA_PLAN, NA_VARIANTS = na_plan()
NA_NVAR = len(NA_VARIANTS)


class Tok:
    __slots__ = ("sem", "key", "val", "eng")

    def __init__(self, sem, key, val, eng):
        self.sem, self.key, self.val, self.eng = sem, key, val, eng


class Buf:
    __slots__ = ("name", "last_w", "readers", "sem", "key", "cnt")

    def __init__(self, name):
        self.name = name
        self.last_w = None
        self.readers = {}
        self.sem = None
        self.key = None
        self.cnt = 0


class Eng:
    def __init__(self, name, h, sem):
        self.name, self.h, self.sem = name, h, sem
        self.key = "E_" + name
        self.cnt = 0
        self.waited = {}


class Sync:
    def __init__(self, nc, stack):
        self.nc = nc
        self.stack = stack
        self.E = {}
        for name, h in (("pe", nc.tensor), ("act", nc.scalar), ("dve", nc.vector),
                        ("pool", nc.gpsimd), ("sp", nc.sync)):
            sem = stack.enter_context(nc.semaphore("s_" + name))
            self.E[name] = Eng(name, h, sem)
        self.dma_bufs = []
        self.free_sems = []
        self.replica_groups = [[0, 1], [2, 3], [4, 5], [6, 7]]
        self.nsem = 0
        self.nwait = 0
        self.nins = 0

    def buf(self, name):
        return Buf(name)

    def bufs(self, name, n):
        return [Buf(f"{name}{i}") for i in range(n)]

    def _deps(self, eng, r, w):
        raw = []
        oth = []
        for b in r:
            if b.last_w is not None:
                raw.append(b.last_w)
        for b in w:
            if b.last_w is not None:
                oth.append(b.last_w)
            oth.extend(b.readers.values())
        toks = []
        for t in raw:
            if t.eng is eng and eng.name == "pe":
                continue
            toks.append(t)
        for t in oth:
            if t.eng is eng:
                continue
            toks.append(t)
        return toks

    def _wait(self, eng, toks):
        for t in toks:
            if eng.waited.get(t.key, 0) >= t.val:
                continue
            eng.h.wait_ge(t.sem, t.val)
            eng.waited[t.key] = t.val
            self.nwait += 1

    def op(self, en, fn, r=(), w=()):
        eng = self.E[en]
        self._wait(eng, self._deps(eng, r, w))
        ins = fn(eng.h)
        ins.then_inc(eng.sem, 1)
        eng.cnt += 1
        self.nins += 1
        tok = Tok(eng.sem, eng.key, eng.cnt, eng)
        for b in r:
            b.readers[tok.key] = tok
        for b in w:
            b.last_w = tok
            b.readers = {}
        return tok

    def dma(self, q, out, in_, r=(), w=(), sb=None):
        eng = self.E[q]
        self._wait(eng, self._deps(eng, r, w))
        if sb is None:
            sb = w[0] if w else r[0]
        if sb.sem is None:
            if self.free_sems:
                sb.sem, sb.key, sb.cnt = self.free_sems.pop()
            else:
                sb.sem = self.stack.enter_context(self.nc.semaphore(f"d{self.nsem}"))
                sb.key = f"D{self.nsem}"
                sb.cnt = 0
                self.nsem += 1
            self.dma_bufs.append(sb)
        ins = eng.h.dma_start(out=out, in_=in_)
        ins.then_inc(sb.sem, 16)
        sb.cnt += 16
        self.nins += 1
        tok = Tok(sb.sem, sb.key, sb.cnt, None)
        for b in r:
            b.readers[tok.key] = tok
        for b in w:
            b.last_w = tok
            b.readers = {}
        return tok

    def allgather_pairs(self, src_ts, dst_ts):
        self.barrier()
        eng = self.E["pool"]
        if getattr(self, "cc_sem", None) is None:
            self.cc_sem = self.stack.enter_context(self.nc.semaphore("cc_sem"))
            self.cc_cnt = 0
        for src_t, dst_t in zip(src_ts, dst_ts):
            ins = eng.h.collective_compute("AllGather", ALU.bypass, replica_groups=self.replica_groups,
                                           ins=[src_t.ap().opt()], outs=[dst_t.ap().opt()])
            ins.then_inc(self.cc_sem)
            self.cc_cnt += 1
            self.nins += 1
        tok = Tok(self.cc_sem, "CC", self.cc_cnt, None)
        for e in self.E.values():
            self._wait(e, [tok])

    def barrier(self, engines=("pe", "act", "dve", "pool", "sp")):
        toks = [Tok(e.sem, e.key, e.cnt, e) for e in self.E.values() if e.cnt > 0]
        toks += [Tok(b.sem, b.key, b.cnt, None) for b in self.dma_bufs if b.cnt > 0]
        for en in engines:
            eng = self.E[en]
            self._wait(eng, [t for t in toks if t.eng is not eng])
        for b in self.dma_bufs:
            self.free_sems.append((b.sem, b.key, b.cnt))
            b.sem = None
            b.last_w = None
            b.readers = {}
        self.dma_bufs = []


class Prog:
    def __init__(self, layers=(0, 1, 2, 3), final_out=True, qb_filter=None, tile_filter=None, n_cores=8):
        self.n_cores = n_cores
        self.layers = tuple(layers)
        self.qb_filter = qb_filter
        self.tile_filter = tile_filter
        self.nc = bass.Bass("TRN2", target_bir_lowering=False)
        self.final_out = final_out

    def din(self, name, shape, dt=F32):
        return self.nc.dram_tensor(name, list(shape), dt, kind="ExternalInput").ap()

    def dout(self, name, shape, dt=F32):
        return self.nc.dram_tensor(name, list(shape), dt, kind="ExternalOutput").ap()

    def dscr(self, name, shape, dt=F32):
        return self.nc.dram_tensor(name, list(shape), dt, kind="Internal").ap()

    def sb(self, st, name, shape, dt=F32):
        self._uid = getattr(self, "_uid", 0) + 1
        return st.enter_context(self.nc.sbuf_tensor(f"s{self._uid}_{name}", list(shape), dt))

    def ps(self, st, name, shape, dt=F32):
        self._uid = getattr(self, "_uid", 0) + 1
        return st.enter_context(self.nc.psum_tensor(f"p{self._uid}_{name}", list(shape), dt))

    def build(self):
        nc = self.nc
        with contextlib.ExitStack() as st:
            self.S = Sync(nc, st)
            self.S.replica_groups = [[2 * k, 2 * k + 1] for k in range(self.n_cores // 2)]
            self.declare()
            self.setup_consts(st)
            nl = len(self.layers)
            for li, l in enumerate(self.layers):
                src = self.x_in if li == 0 else self.xbuf[(li - 1) % 2]
                last = li == nl - 1
                dst = self.y_out if last else self.xbuf[li % 2]
                if l % 2 == 0:
                    self.even_layer(l, src, dst, last)
                else:
                    self.odd_layer(l, src, dst, last)
            self.S.barrier()
        return nc

    def declare(self):
        self.x_in = self.din("x_in", [NTOK, D])
        self.cc = self.din("cc", [128, 8, 2])
        self.ada_w = self.din("ada_w", [DEPTH, D, 3 * D])
        self.ada_b = self.din("ada_b", [DEPTH, 3 * D])
        self.ln_g = self.din("ln_g", [DEPTH, D])
        self.ln_b = self.din("ln_b", [DEPTH, D])
        self.ident_in = self.din("ident", [128, 128])
        self.sel_in = self.din("sel", [2, 256])
        self.ev_w = self.din("ev_w", [2, D, EV_COLS])
        self.ev_bcol = self.din("ev_bcol", [2, 128, 10])
        self.ev_brow = self.din("ev_brow", [2, 1, 1664])
        self.ev_lam = self.din("ev_lam", [2, 1, 256])
        self.ev_subg = self.din("ev_subg", [2, 1, 128])
        self.ev_qg = self.din("ev_qg", [2, 128, 2])
        self.ev_kvg = self.din("ev_kvg", [2, 128, 1])
        self.ev_wuq = self.din("ev_wuq", [2, 256, 768])
        self.ev_wukv = self.din("ev_wukv", [2, 128, 512])
        self.ev_wo = self.din("ev_wo", [2, D, D])
        self.rope_da = self.din("rope_da", [2, 128, NTOK])
        self.rope_ml = self.din("rope_ml", [2, 128, NTOK])
        self.od_w = self.din("od_w", [2, D, OD_COLS])
        self.od_bcol = self.din("od_bcol", [2, 128, 8])
        self.od_brow = self.din("od_brow", [2, 1, 1800])
        self.od_convw = self.din("od_convw", [2, 128, 4, 5])
        self.od_convb = self.din("od_convb", [2, 128, 4])
        self.od_fb = self.din("od_fb", [2, 1, 4])
        self.od_ng = self.din("od_ng", [2, 1, 256])
        self.od_nat = self.din("od_nat", [2, 4, 128, NA_NVAR, 128])
        self.od_wo = self.din("od_wo", [2, D, D])
        self.tri = self.din("tri", [128, 3, 128])
        self.qkpre = self.dscr("qkpre", [4, 128, NTOK])
        self.mq = self.dscr("mq", [2, 128, NTOK], BF16)
        self.mk = self.dscr("mk", [2, 128, NTOK], BF16)
        self.og = self.dscr("og", [NTOK, 256])
        self.gates = self.dscr("gates", [128, NT, 8])
        self.hfb = self.dscr("hfb", [2, NTOK, 256])
        n_last_tok = LAT if self.final_out else NTOK
        self.y_out = self.dout("y", [n_last_tok, D])
        self.xbuf = [self.dscr("xbuf0", [NTOK, D]), self.dscr("xbuf1", [NTOK, D])]
        self.qda = self.dscr("qda", [2, 128, NTOK], BF16)
        self.kda = self.dscr("kda", [2, 128, NTOK], BF16)
        self.vda = self.dscr("vda", [2, 128, NT, 129], BF16)
        self.qm = self.dscr("qm", [4, 96, NTOK], BF16)
        self.kmn = self.dscr("kmn", [4, 64, NTOK], BF16)
        self.krt = self.dscr("krt", [32, NTOK], BF16)
        self.vm = self.dscr("vm", [4, 128, NT, 65], BF16)
        self.gate = self.dscr("gate", [NTOK, D], BF16)
        self.ybuf = self.dscr("ybuf", [NTOK, D], BF16)
        self.psel = self.din("psel", [128, 2])
        self.yodd_rows = [2048, 2048, 2048, 2048, 256]
        self.yodd_t = [self.nc.dram_tensor(f"yodd{k}", [r, 512], BF16) for k, r in enumerate(self.yodd_rows)]
        self.yoddg_t = [self.nc.dram_tensor(f"yoddg{k}", [2 * r, 512], BF16) for k, r in enumerate(self.yodd_rows)]
        self.yown_t = [self.nc.dram_tensor(f"yown{k}", [1024, D], BF16) for k in range(4)]
        self.ygath_t = [self.nc.dram_tensor(f"ygath{k}", [2048, D], BF16) for k in range(4)]

    def setup_consts(self, st):
        S = self.S
        self.ident_f = self.sb(st, "ident_f", [128, 128])
        self.ident_b = self.sb(st, "ident_b", [128, 128], BF16)
        self.sel = self.sb(st, "sel", [2, 256])
        self.ccs = self.sb(st, "ccs", [128, 8, 2])
        self.zeros_b = self.sb(st, "zeros_b", [128, 128], BF16)
        self.ones_b = self.sb(st, "ones_b", [1, 128], BF16)
        self.zeros_w = self.sb(st, "zeros_w", [128, 512], BF16)
        self.B_const = S.buf("consts")
        b = self.B_const
        S.dma("sp", self.ident_f[:], self.ident_in[:, :], w=[b])
        S.dma("sp", self.sel[:], self.sel_in[:, :], w=[b])
        S.dma("sp", self.ccs[:], self.cc[:, :, :], w=[b])
        self.pselt = self.sb(st, "pselt", [128, 4])
        S.dma("sp", self.pselt[:, 0:2], self.psel[:, :], w=[b])
        S.op("dve", lambda e: e.tensor_scalar(out=self.pselt[:, 2:4], in0=self.pselt[:, 0:2], scalar1=1.0 / NA_SCALE, scalar2=None,
                                              op0=ALU.mult), r=[b], w=[b])
        S.op("dve", lambda e: e.tensor_copy(self.ident_b[:], self.ident_f[:]), r=[b], w=[b])
        S.op("dve", lambda e: e.memset(self.zeros_b[:], 0.0), w=[b])
        S.op("dve", lambda e: e.memset(self.ones_b[:], 1.0), w=[b])
        S.op("dve", lambda e: e.memset(self.zeros_w[:], 0.0), w=[b])
        S.op("act", lambda e: e.activation(out=self.ccs[:], in_=self.ccs[:], func=AF.Silu), r=[b], w=[b])
        self.sc1 = self.sb(st, "sc1", [128, 8, 2])
        self.sh = self.sb(st, "sh", [128, 8, 2])
        self.g_l = self.sb(st, "g_l", [128, D])
        self.g_c = self.sb(st, "g_c", [128, D])
        self.lng = self.sb(st, "lng", [128, D])
        self.lnb = self.sb(st, "lnb", [128, D])
        self.B_mod = S.buf("mod")

    def adaln(self, l):
        S, nc = self.S, self.nc
        S.barrier()
        with contextlib.ExitStack() as st:
            wt = [self.sb(st, f"adaw{i}", [128, 3 * D]) for i in range(2)]
            Bw = S.bufs("adaw", 2)
            mrow = self.sb(st, "mrow", [2, 3 * D])
            brow = self.sb(st, "adab", [2, 3 * D])
            Bm = S.buf("mrow")
            Bb = S.buf("adab")
            pm = [self.ps(st, f"pm{i}", [128, 512]) for i in range(8)]
            Bp = S.bufs("pm", 8)
            S.dma("sp", brow[:], self.ada_b[l:l + 1, :].to_broadcast([2, 3 * D]), w=[Bb])
            S.dma("sp", self.lng[:], self.ln_g[l:l + 1, :].to_broadcast([128, D]), w=[self.B_mod])
            S.dma("sp", self.lnb[:], self.ln_b[l:l + 1, :].to_broadcast([128, D]), w=[self.B_mod])
            for k in range(8):
                S.dma("sp", wt[k % 2][:], self.ada_w[l, k * 128:(k + 1) * 128, :], w=[Bw[k % 2]])
                for n in range(6):
                    S.op("pe", lambda e, k=k, n=n: e.matmul(
                        pm[n][0:2, :], self.ccs[:, k, :], wt[k % 2][:, n * 512:(n + 1) * 512],
                        start=(k == 0), stop=(k == 7)), r=[Bw[k % 2], self.B_const], w=[Bp[n]])
            for n in range(6):
                S.op("dve", lambda e, n=n: e.tensor_tensor(
                    out=mrow[:, n * 512:(n + 1) * 512], in0=pm[n][0:2, :], in1=brow[:, n * 512:(n + 1) * 512],
                    op=ALU.add), r=[Bp[n], Bb], w=[Bm])
            for j in range(8):
                S.op("pe", lambda e, j=j: e.transpose(
                    pm[6][:, 2 * j:2 * j + 2], mrow[0:2, j * 128:(j + 1) * 128], self.ident_f[0:2, 0:2]),
                    r=[Bm, self.B_const], w=[Bp[6]])
                S.op("pe", lambda e, j=j: e.transpose(
                    pm[7][:, 2 * j:2 * j + 2], mrow[0:2, D + j * 128:D + (j + 1) * 128], self.ident_f[0:2, 0:2]),
                    r=[Bm, self.B_const], w=[Bp[7]])
            S.op("dve", lambda e: e.tensor_copy(self.sh[:].rearrange("p a b -> p (a b)"), pm[6][:, 0:16]),
                 r=[Bp[6]], w=[self.B_mod])
            S.op("dve", lambda e: e.tensor_scalar(
                out=self.sc1[:].rearrange("p a b -> p (a b)"), in0=pm[7][:, 0:16], scalar1=1.0, scalar2=None,
                op0=ALU.add), r=[Bp[7]], w=[self.B_mod])
            for n in range(2):
                S.op("pe", lambda e, n=n: e.matmul(
                    pm[n][:, :], self.sel[:, 0:128], mrow[0:2, 2 * D + n * 512:2 * D + (n + 1) * 512],
                    start=True, stop=True), r=[Bm, self.B_const], w=[Bp[n]])
                S.op("pe", lambda e, n=n: e.matmul(
                    pm[2 + n][:, :], self.sel[:, 128:256], mrow[0:2, 2 * D + n * 512:2 * D + (n + 1) * 512],
                    start=True, stop=True), r=[Bm, self.B_const], w=[Bp[2 + n]])
                S.op("dve", lambda e, n=n: e.tensor_copy(self.g_l[:, n * 512:(n + 1) * 512], pm[n][:, :]),
                     r=[Bp[n]], w=[self.B_mod])
                S.op("dve", lambda e, n=n: e.tensor_copy(self.g_c[:, n * 512:(n + 1) * 512], pm[2 + n][:, :]),
                     r=[Bp[2 + n]], w=[self.B_mod])
            S.barrier()

    def load_cast(self, st_w, dst_fn, src_fn, nparts, ncols, pieces, Bdst, scale_fn=None, name="lc"):
        S = self.S
        with contextlib.ExitStack() as st:
            stg = [self.sb(st, f"{name}_stg{i}", [nparts, ncols]) for i in range(2)]
            Bs = S.bufs(name + "_stg", 2)
            for i, (dst, src, sc) in enumerate(pieces):
                j = i % 2
                n = src.shape[-1]
                S.dma("sp", stg[j][:, 0:n], src, w=[Bs[j]])
                en = "dve" if i % 2 == 0 else "pool"
                if sc is None:
                    S.op(en, lambda e, dst=dst, j=j, n=n: e.tensor_copy(dst, stg[j][:, 0:n]), r=[Bs[j]], w=[Bdst])
                else:
                    S.op(en, lambda e, dst=dst, j=j, n=n, sc=sc: e.tensor_scalar(
                        out=dst, in0=stg[j][:, 0:n], scalar1=sc, scalar2=None, op0=ALU.mult),
                        r=[Bs[j], Bdst], w=[Bdst])
            S.barrier()

    def even_layer(self, l, src, dst, last):
        i = l // 2
        self.adaln(l)
        self.even_project(i, src)
        self.da_attention(i, l)
        self.mla_attention(i)
        self.S.allgather_pairs(self.yodd_t, self.yoddg_t)
        self.out_stage(self.ev_wo[i], src, dst, last, ysplit="odd")

    def even_project(self, i, src):
        S, nc = self.S, self.nc
        with contextlib.ExitStack() as st:
            wb = self.sb(st, "ev_wb", [128, 8, EV_COLS], BF16)
            wuq = self.sb(st, "ev_wuqb", [128, 2, 768], BF16)
            wukv = self.sb(st, "ev_wukvb", [128, 512], BF16)
            bcol = self.sb(st, "ev_bcol", [128, 10])
            brow_f = self.sb(st, "ev_brow_f", [1, 1664])
            brow = self.sb(st, "ev_brow", [1, 1664], BF16)
            qg = self.sb(st, "ev_qg", [128, 2])
            kvg = self.sb(st, "ev_kvg", [128, 1])
            Bw = S.buf("ev_w")
            S.dma("sp", bcol[:], self.ev_bcol[i, :, :], w=[Bw])
            S.dma("sp", brow_f[:], self.ev_brow[i, :, :], w=[Bw])
            S.dma("sp", qg[:], self.ev_qg[i, :, :], w=[Bw])
            S.dma("sp", kvg[:], self.ev_kvg[i, :, :], w=[Bw])
            S.op("dve", lambda e: e.tensor_copy(brow[:], brow_f[:]), r=[Bw], w=[Bw])
            pieces = []
            for k in range(8):
                for c in range(2):
                    pieces.append((wb[:, k, c * 1376:(c + 1) * 1376],
                                   self.ev_w[i, k * 128:(k + 1) * 128, c * 1376:(c + 1) * 1376], None))
            self.load_cast(st, None, None, 128, 1536, pieces, Bw, name="evw")
            pieces = [(wuq[:, rc, :], self.ev_wuq[i, rc * 128:(rc + 1) * 128, :], qg[:, rc:rc + 1]) for rc in range(2)]
            pieces.append((wukv[:, :], self.ev_wukv[i, :, :], kvg[:, 0:1]))
            self.load_cast(st, None, None, 128, 1536, pieces, Bw, name="evw2")

            NXS = 6
            xt = [self.sb(st, f"xt{j}", [128, D]) for j in range(NXS)]
            Bx = S.bufs("xt", NXS)
            hT = [self.sb(st, f"hT{j}", [128, 8, 512], BF16) for j in range(2)]
            Bh = S.bufs("hT", 2)
            cosd = [self.sb(st, f"cosd{j}", [128, 512]) for j in range(2)]
            sind = [self.sb(st, f"sind{j}", [128, 512]) for j in range(2)]
            cosm = [self.sb(st, f"cosm{j}", [128, 512]) for j in range(2)]
            sinm = [self.sb(st, f"sinm{j}", [128, 512]) for j in range(2)]
            Brt = S.bufs("ropet", 2)
            t1 = [self.sb(st, f"t1_{j}", [128, 512]) for j in range(2)]
            t2 = [self.sb(st, f"t2_{j}", [128, 512]) for j in range(2)]
            Bt1 = S.bufs("t1", 2)
            Bt2 = S.bufs("t2", 2)
            fo = [self.sb(st, f"fo{j}", [128, 512], BF16) for j in range(4)]
            Bfo = S.bufs("fo", 4)
            va = [self.sb(st, f"va{j}", [128, 2, 129], BF16) for j in range(2)]
            Bva = S.bufs("va", 2)
            vma = [self.sb(st, f"vma{j}", [128, 4, 65], BF16) for j in range(2)]
            Bvma = S.bufs("vma", 2)
            cq = [self.sb(st, f"cq{j}", [128, 384]) for j in range(2)]
            Bcq = S.bufs("cq", 2)
            cqn = [self.sb(st, f"cqn{j}", [128, 384], BF16) for j in range(2)]
            Bcqn = S.bufs("cqn", 2)
            stat = [self.sb(st, f"stat{j}", [128, 8]) for j in range(2)]
            Bstat = S.bufs("stat", 2)
            junk = self.sb(st, "junk", [128, 256])
            Bjunk = S.buf("junk")
            cT = [self.sb(st, f"cT{j}", [128, 3, 512], BF16) for j in range(2)]
            BcT = S.bufs("cT", 2)
            gt = [self.sb(st, f"gt{j}", [128, D], BF16) for j in range(2)]
            Bgt = S.bufs("gt", 2)
            pp = [self.ps(st, f"pp{j}", [128, 512]) for j in range(7)]
            ppb = self.ps(st, "ppb", [128, 1024], BF16)
            Bpp = S.bufs("pp", 7)
            Bppb = S.buf("ppb")
            for j in range(2):
                S.op("pool", lambda e, j=j: e.memset(va[j][:], 1.0), w=[Bva[j]])
                S.op("pool", lambda e, j=j: e.memset(vma[j][:], 1.0), w=[Bvma[j]])
            eps_t = self.sb(st, "eps_t", [128, 1])
            S.op("pool", lambda e: e.memset(eps_t[:], RMS_EPS), w=[Bw])

            self._pp_i = 0

            def next_pp():
                j = self._pp_i % 7
                self._pp_i += 1
                return pp[j], Bpp[j]

            nblk = (NTOK + 511) // 512
            self._xi = 0
            self._foi = 0
            self._evac = 0

            def stage_a(blk):
                t0 = blk * 512
                ntok = min(512, NTOK - t0)
                nsub = ntok // 128
                m = 0 if t0 < LAT else 1
                hb = blk % 2
                S.dma("sp", cosd[hb][:, :ntok], self.rope_da[0, :, t0:t0 + ntok], w=[Brt[hb]])
                S.dma("sp", sind[hb][:, :ntok], self.rope_da[1, :, t0:t0 + ntok], w=[Brt[hb]])
                S.dma("sp", cosm[hb][:, :ntok], self.rope_ml[0, :, t0:t0 + ntok], w=[Brt[hb]])
                S.dma("sp", sinm[hb][:, :ntok], self.rope_ml[1, :, t0:t0 + ntok], w=[Brt[hb]])
                for s in range(nsub):
                    xj = self._xi % NXS
                    self._xi += 1
                    S.dma("sp", xt[xj][:], src[t0 + s * 128:t0 + (s + 1) * 128, :], w=[Bx[xj]])
                    for half in range(2):
                        p, Bp = next_pp()
                        for q in range(4):
                            dc = half * 4 + q
                            S.op("pe", lambda e, p=p, q=q, dc=dc, xj=xj: e.transpose(
                                p[:, q * 128:(q + 1) * 128], xt[xj][:, dc * 128:(dc + 1) * 128], self.ident_f[:]),
                                r=[Bx[xj], self.B_const], w=[Bp])
                        for q in range(4):
                            dc = half * 4 + q
                            if self._evac % 2 == 0:
                                S.op("act", lambda e, p=p, q=q, dc=dc, s=s, m=m: e.activation(
                                    out=hT[hb][:, dc, s * 128:(s + 1) * 128], in_=p[:, q * 128:(q + 1) * 128],
                                    func=AF.Identity, scale=self.sc1[:, dc, m:m + 1], bias=self.sh[:, dc, m:m + 1]),
                                    r=[Bp, self.B_mod], w=[Bh[hb]])
                            else:
                                S.op("dve", lambda e, p=p, q=q, dc=dc, s=s, m=m: e.tensor_scalar(
                                    out=hT[hb][:, dc, s * 128:(s + 1) * 128], in0=p[:, q * 128:(q + 1) * 128],
                                    scalar1=self.sc1[:, dc, m:m + 1], scalar2=self.sh[:, dc, m:m + 1],
                                    op0=ALU.mult, op1=ALU.add), r=[Bp, self.B_mod], w=[Bh[hb]])
                            self._evac += 1

            def stage_b(blk):
                t0 = blk * 512
                ntok = min(512, NTOK - t0)
                nsub = ntok // 128
                m = 0 if t0 < LAT else 1
                hb = blk % 2
                foi = self._foi
                for grp, base, dstT, bofs in ((0, EQ, self.qda, 0), (1, EK, self.kda, 4)):
                    for h in range(2):
                        pa, Ba = next_pp()
                        pb, Bb = next_pp()
                        for k in range(8):
                            S.op("pe", lambda e, pa=pa, k=k, c0=base + h * 128: e.matmul(
                                pa[:, :ntok], wb[:, k, c0:c0 + 128], hT[hb][:, k, :ntok], start=(k == 0), stop=(k == 7)),
                                r=[Bw, Bh[hb]], w=[Ba])
                        for k in range(8):
                            S.op("pe", lambda e, pb=pb, k=k, c0=base + 256 + h * 128: e.matmul(
                                pb[:, :ntok], wb[:, k, c0:c0 + 128], hT[hb][:, k, :ntok], start=(k == 0), stop=(k == 7)),
                                r=[Bw, Bh[hb]], w=[Bb])
                        tj = (grp * 2 + h) % 2
                        S.op("dve", lambda e, pa=pa, tj=tj, c=bofs + h: e.scalar_tensor_tensor(
                            out=t1[tj][:, :ntok], in0=pa[:, :ntok], scalar=bcol[:, c:c + 1], in1=cosd[hb][:, :ntok],
                            op0=ALU.add, op1=ALU.mult), r=[Ba, Brt[hb], Bw], w=[Bt1[tj]])
                        S.op("dve", lambda e, pb=pb, tj=tj, c=bofs + 2 + h: e.scalar_tensor_tensor(
                            out=t2[tj][:, :ntok], in0=pb[:, :ntok], scalar=bcol[:, c:c + 1], in1=sind[hb][:, :ntok],
                            op0=ALU.add, op1=ALU.mult), r=[Bb, Brt[hb], Bw], w=[Bt2[tj]])
                        fj = foi % 4
                        foi += 1
                        S.op("pool", lambda e, tj=tj, fj=fj: e.tensor_tensor(
                            out=fo[fj][:, :ntok], in0=t1[tj][:, :ntok], in1=t2[tj][:, :ntok], op=ALU.add),
                            r=[Bt1[tj], Bt2[tj]], w=[Bfo[fj]])
                        S.dma("pool", dstT[h, :, t0:t0 + ntok], fo[fj][:, :ntok], r=[Bfo[fj]])
                pa, Ba = next_pp()
                pb, Bb = next_pp()
                for k in range(8):
                    S.op("pe", lambda e, pa=pa, k=k: e.matmul(
                        pa[0:32, :ntok], wb[:, k, EKR:EKR + 32], hT[hb][:, k, :ntok], start=(k == 0), stop=(k == 7)),
                        r=[Bw, Bh[hb]], w=[Ba])
                for k in range(8):
                    S.op("pe", lambda e, pb=pb, k=k: e.matmul(
                        pb[0:32, :ntok], wb[:, k, EKR + 32:EKR + 64], hT[hb][:, k, :ntok], start=(k == 0), stop=(k == 7)),
                        r=[Bw, Bh[hb]], w=[Bb])
                tj = 0
                S.op("dve", lambda e, pa=pa: e.scalar_tensor_tensor(
                    out=t1[tj][0:32, :ntok], in0=pa[0:32, :ntok], scalar=bcol[0:32, 8:9], in1=cosm[hb][0:32, :ntok],
                    op0=ALU.add, op1=ALU.mult), r=[Ba, Brt[hb], Bw], w=[Bt1[tj]])
                S.op("dve", lambda e, pb=pb: e.scalar_tensor_tensor(
                    out=t2[tj][0:32, :ntok], in0=pb[0:32, :ntok], scalar=bcol[0:32, 9:10], in1=sinm[hb][0:32, :ntok],
                    op0=ALU.add, op1=ALU.mult), r=[Bb, Brt[hb], Bw], w=[Bt2[tj]])
                fj = foi % 4
                foi += 1
                S.op("pool", lambda e, fj=fj: e.tensor_tensor(
                    out=fo[fj][0:32, :ntok], in0=t1[tj][0:32, :ntok], in1=t2[tj][0:32, :ntok], op=ALU.add),
                    r=[Bt1[tj], Bt2[tj]], w=[Bfo[fj]])
                S.dma("pool", self.krt[:, t0:t0 + ntok], fo[fj][0:32, :ntok], r=[Bfo[fj]])
                cb = blk % 2
                for s in range(nsub):
                    tt = (t0 // 128) + s
                    tok0 = t0 + s * 128
                    p, Bp = next_pp()
                    for k in range(8):
                        S.op("pe", lambda e, p=p, k=k, s=s: e.matmul(
                            p[:, 0:256], hT[hb][:, k, s * 128:(s + 1) * 128], wb[:, k, EV_:EV_ + 256], start=(k == 0), stop=False),
                            r=[Bw, Bh[hb]], w=[Bp])
                    S.op("pe", lambda e, p=p: e.matmul(p[:, 0:256], self.ones_b[0:1, :], brow[0:1, 0:256], start=False, stop=True),
                         r=[Bw, self.B_const], w=[Bp])
                    vj = tt % 2
                    S.op("act", lambda e, p=p, vj=vj: e.activation(
                        out=va[vj][:, :, 0:128], in_=p[:, 0:256].rearrange("p (h c) -> p h c", h=2), func=AF.Copy),
                        r=[Bp], w=[Bva[vj]])
                    S.dma("pool", self.vda[:, :, tt, :].rearrange("h p c -> p h c"), va[vj][:], r=[Bva[vj]])
                    p, Bp = next_pp()
                    for k in range(8):
                        S.op("pe", lambda e, p=p, k=k, s=s: e.matmul(
                            p[:, 0:384], hT[hb][:, k, s * 128:(s + 1) * 128], wb[:, k, EC:EC + 384], start=(k == 0), stop=False),
                            r=[Bw, Bh[hb]], w=[Bp])
                    S.op("pe", lambda e, p=p: e.matmul(p[:, 0:384], self.ones_b[0:1, :], brow[0:1, 256:640], start=False, stop=True),
                         r=[Bw, self.B_const], w=[Bp])
                    cj = tt % 2
                    S.op("act", lambda e, p=p, cj=cj: e.activation(out=cq[cj][:], in_=p[:, 0:384], func=AF.Copy),
                         r=[Bp], w=[Bcq[cj]])
                    S.op("act", lambda e, cj=cj: e.activation(
                        out=junk[:, 0:256], in_=cq[cj][:, 0:256], func=AF.Square, accum_out=stat[cj][:, 0:1]),
                        r=[Bcq[cj]], w=[Bjunk, Bstat[cj]])
                    S.op("act", lambda e, cj=cj: e.activation(
                        out=junk[:, 0:128], in_=cq[cj][:, 256:384], func=AF.Square, accum_out=stat[cj][:, 1:2]),
                        r=[Bcq[cj]], w=[Bjunk, Bstat[cj]])
                    S.op("act", lambda e, cj=cj: e.activation(out=stat[cj][:, 2:3], in_=stat[cj][:, 0:1], func=AF.Sqrt,
                                                              scale=1.0 / 256, bias=eps_t[:, 0:1]), r=[Bstat[cj], Bw], w=[Bstat[cj]])
                    S.op("act", lambda e, cj=cj: e.activation(out=stat[cj][:, 3:4], in_=stat[cj][:, 1:2], func=AF.Sqrt,
                                                              scale=1.0 / 128, bias=eps_t[:, 0:1]), r=[Bstat[cj], Bw], w=[Bstat[cj]])
                    S.op("dve", lambda e, cj=cj: e.reciprocal(stat[cj][:, 4:6], stat[cj][:, 2:4]), r=[Bstat[cj]], w=[Bstat[cj]])
                    S.op("dve", lambda e, cj=cj: e.tensor_scalar(
                        out=cqn[cj][:, 0:256], in0=cq[cj][:, 0:256], scalar1=stat[cj][:, 4:5], scalar2=None, op0=ALU.mult),
                        r=[Bcq[cj], Bstat[cj]], w=[Bcqn[cj]])
                    S.op("dve", lambda e, cj=cj: e.tensor_scalar(
                        out=cqn[cj][:, 256:384], in0=cq[cj][:, 256:384], scalar1=stat[cj][:, 5:6], scalar2=None, op0=ALU.mult),
                        r=[Bcq[cj], Bstat[cj]], w=[Bcqn[cj]])
                    gj = tt % 2
                    for n in range(2):
                        p, Bp = next_pp()
                        for k in range(8):
                            S.op("pe", lambda e, p=p, k=k, s=s, n=n: e.matmul(
                                p[:, :], hT[hb][:, k, s * 128:(s + 1) * 128], wb[:, k, EG + n * 512:EG + (n + 1) * 512],
                                start=(k == 0), stop=False), r=[Bw, Bh[hb]], w=[Bp])
                        S.op("pe", lambda e, p=p, n=n: e.matmul(
                            p[:, :], self.ones_b[0:1, :], brow[0:1, 640 + n * 512:640 + (n + 1) * 512], start=False, stop=True),
                            r=[Bw, self.B_const], w=[Bp])
                        S.op("act", lambda e, p=p, n=n, gj=gj: e.activation(
                            out=gt[gj][:, n * 512:(n + 1) * 512], in_=p[:, :], func=AF.Silu), r=[Bp], w=[Bgt[gj]])
                    S.dma("pool", self.gate[tok0:tok0 + 128, :], gt[gj][:], r=[Bgt[gj]])
                    for rc in range(3):
                        S.op("pe", lambda e, rc=rc, cj=cj: e.transpose(
                            ppb[:, rc * 128:(rc + 1) * 128], cqn[cj][:, rc * 128:(rc + 1) * 128], self.ident_b[:]),
                            r=[Bcqn[cj], self.B_const], w=[Bppb])
                    S.op("dve", lambda e, s=s: e.tensor_copy(
                        cT[cb][:, :, s * 128:(s + 1) * 128], ppb[:, 0:384].rearrange("p (r t) -> p r t", r=3)),
                        r=[Bppb], w=[BcT[cb]])
                for h in range(4):
                    pa, Ba = next_pp()
                    pb, Bb = next_pp()
                    for rc in range(2):
                        S.op("pe", lambda e, pa=pa, rc=rc, h=h: e.matmul(
                            pa[0:96, :ntok], wuq[:, rc, h * 96:(h + 1) * 96], cT[cb][:, rc, :ntok], start=(rc == 0), stop=(rc == 1)),
                            r=[Bw, BcT[cb]], w=[Ba])
                    for rc in range(2):
                        S.op("pe", lambda e, pb=pb, rc=rc, h=h: e.matmul(
                            pb[0:96, :ntok], wuq[:, rc, 384 + h * 96:384 + (h + 1) * 96], cT[cb][:, rc, :ntok],
                            start=(rc == 0), stop=(rc == 1)), r=[Bw, BcT[cb]], w=[Bb])
                    tj = h % 2
                    fj = foi % 4
                    foi += 1
                    S.op("dve", lambda e, pa=pa, tj=tj: e.tensor_tensor(
                        out=t1[tj][64:96, :ntok], in0=pa[64:96, :ntok], in1=cosm[hb][64:96, :ntok], op=ALU.mult),
                        r=[Ba, Brt[hb]], w=[Bt1[tj]])
                    S.op("dve", lambda e, pb=pb, tj=tj: e.tensor_tensor(
                        out=t2[tj][64:96, :ntok], in0=pb[64:96, :ntok], in1=sinm[hb][64:96, :ntok], op=ALU.mult),
                        r=[Bb, Brt[hb]], w=[Bt2[tj]])
                    S.op("act", lambda e, pa=pa, fj=fj: e.activation(out=fo[fj][0:64, :ntok], in_=pa[0:64, :ntok], func=AF.Copy),
                         r=[Ba], w=[Bfo[fj]])
                    S.op("pool", lambda e, tj=tj, fj=fj: e.tensor_tensor(
                        out=fo[fj][64:96, :ntok], in0=t1[tj][64:96, :ntok], in1=t2[tj][64:96, :ntok], op=ALU.add),
                        r=[Bt1[tj], Bt2[tj]], w=[Bfo[fj]])
                    S.dma("pool", self.qm[h, :, t0:t0 + ntok], fo[fj][0:96, :ntok], r=[Bfo[fj]])
                for hp in range(2):
                    p, Bp = next_pp()
                    S.op("pe", lambda e, p=p, hp=hp: e.matmul(
                        p[:, :ntok], wukv[:, hp * 128:(hp + 1) * 128], cT[cb][:, 2, :ntok], start=True, stop=True),
                        r=[Bw, BcT[cb]], w=[Bp])
                    fj = foi % 4
                    foi += 1
                    S.op("act", lambda e, p=p, fj=fj: e.activation(out=fo[fj][:, :ntok], in_=p[:, :ntok], func=AF.Copy),
                         r=[Bp], w=[Bfo[fj]])
                    S.dma("pool", self.kmn[2 * hp, :, t0:t0 + ntok], fo[fj][0:64, :ntok], r=[Bfo[fj]])
                    S.dma("pool", self.kmn[2 * hp + 1, :, t0:t0 + ntok], fo[fj][64:128, :ntok], r=[Bfo[fj]])
                for s in range(nsub):
                    tt = (t0 // 128) + s
                    p, Bp = next_pp()
                    S.op("pe", lambda e, p=p, s=s: e.matmul(
                        p[:, 0:256], cT[cb][:, 2, s * 128:(s + 1) * 128], wukv[:, 256:512], start=True, stop=True),
                        r=[Bw, BcT[cb]], w=[Bp])
                    vj = tt % 2
                    S.op("act", lambda e, p=p, vj=vj: e.activation(
                        out=vma[vj][:, :, 0:64], in_=p[:, 0:256].rearrange("p (h c) -> p h c", h=4), func=AF.Copy),
                        r=[Bp], w=[Bvma[vj]])
                    S.dma("pool", self.vm[:, :, tt, :].rearrange("h p c -> p h c"), vma[vj][:], r=[Bvma[vj]])
                self._foi = foi

            stage_a(0)
            for blk in range(nblk):
                if blk + 1 < nblk:
                    stage_a(blk + 1)
                stage_b(blk)
            S.barrier()

    def attn_core(self, st, name, KT, QT, VA, Bkv, kparts, nv, scale, qblocks, finalize, kslices=None):
        S = self.S
        nmap = len(kparts)
        nacc_per_bank = 512 // nv
        n_acc = nmap * 4
        n_acc_banks = (n_acc + nacc_per_bank - 1) // nacc_per_bank
        n_sc = 8 - n_acc_banks
        n_sc = min(n_sc, 4)
        NP = 4
        res = getattr(self, "_attn_res", None)
        if res is None or res[0] != name:
            accb = [self.ps(st, f"{name}_acc{j}", [128, 512]) for j in range(n_acc_banks)]
            Bacc = S.bufs(name + "_acc", n_acc_banks)
            scb = [self.ps(st, f"{name}_sc{j}", [128, 512]) for j in range(n_sc)]
            Bsc = S.bufs(name + "_sc", n_sc)
            pt = [self.sb(st, f"{name}_pt{j}", [128, 512], BF16) for j in range(NP)]
            Bpt = S.bufs(name + "_pt", NP)
            self._attn_res = (name, accb, Bacc, scb, Bsc, pt, Bpt)
        _, accb, Bacc, scb, Bsc, pt, Bpt = self._attn_res

        def acc_ap(mi, sub):
            idx = mi * 4 + sub
            b = idx // nacc_per_bank
            o = (idx % nacc_per_bank) * nv
            return accb[b][:, o:o + nv], Bacc[b]

        sci = 0
        pti = 0
        for qbi, (q0, nq, ktiles) in enumerate(qblocks):
            nsub = nq // 128
            for b in range(n_acc_banks):
                S.op("pe", lambda e, b=b: e.matmul(accb[b][:, :], self.zeros_b[:, :], self.zeros_w[:, :], start=True, stop=False,
                                                   skip_group_check=True), r=[self.B_const], w=[Bacc[b]])
            units = [(kt, mi) for kt in ktiles for mi in range(nmap)]
            pend = []

            def emit_score(u):
                nonlocal sci
                kt, mi = u
                QTm, lo, hi = kparts[mi]
                j = sci % n_sc
                sci += 1
                S.op("pe", lambda e, j=j, lo=lo, hi=hi, kt=kt, QTm=QTm: e.matmul(
                    scb[j][:, :nq], KT[lo:hi, kt * 128:(kt + 1) * 128], QTm[lo:hi, q0:q0 + nq], start=True, stop=True),
                    r=[Bkv], w=[Bsc[j]])
                return j

            def emit_exp_pv(u, j, lastflag):
                nonlocal pti
                kt, mi = u
                pj = pti % NP
                pti += 1
                S.op("act", lambda e, j=j, pj=pj: e.activation(out=pt[pj][:, :nq], in_=scb[j][:, :nq], func=AF.Exp, scale=scale),
                     r=[Bsc[j]], w=[Bpt[pj]])
                for sub in range(nsub):
                    ap, Ba = acc_ap(mi, sub)
                    S.op("pe", lambda e, ap=ap, pj=pj, sub=sub, kt=kt: e.matmul(
                        ap, pt[pj][:, sub * 128:(sub + 1) * 128], VA[:, kt, :], start=False, stop=lastflag,
                        skip_group_check=True), r=[Bpt[pj], Bkv], w=[Ba])

            LOOK = min(n_sc - 1, 2)
            q = []
            for ui, u in enumerate(units):
                q.append((u, emit_score(u)))
                if len(q) > LOOK:
                    u0, j0 = q.pop(0)
                    emit_exp_pv(u0, j0, False)
            while q:
                u0, j0 = q.pop(0)
                emit_exp_pv(u0, j0, u0[0] == ktiles[-1])
            for sub in range(nsub):
                accs = [acc_ap(mi, sub) for mi in range(nmap)]
                finalize(qbi, q0, sub, accs)

    def qblocks_all(self):
        qb = []
        lat_k = list(range(NT))
        for b in range(LAT // 512):
            qb.append((b * 512, 512, lat_k))
        qb.append((LAT, CTX, [NT_LAT, NT_LAT + 1]))
        if self.qb_filter is not None:
            qb = [q for i, q in enumerate(qb) if i in self.qb_filter]
        return qb

    def qblocks_own(self):
        qb = []
        lat_k = list(range(NT))
        for b in range(LAT // 2 // 512):
            qb.append((b * 512, 512, lat_k))
        qb.append((LAT // 2, CTX, [NT_LAT, NT_LAT + 1]))
        if self.qb_filter is not None:
            qb = [q for i, q in enumerate(qb) if i in self.qb_filter]
        return qb

    def own_q(self, QO, QS, lo, hi, tmp, Bqs, Bqo, Btmp):
        S = self.S
        H = LAT // 2
        S.op("dve", lambda e: e.tensor_scalar(out=tmp[lo:hi, :], in0=QS[lo:hi, H:LAT], scalar1=self.pselt[lo:hi, 1:2], scalar2=None,
                                              op0=ALU.mult), r=[Bqs, self.B_const], w=[Btmp])
        S.op("dve", lambda e: e.scalar_tensor_tensor(out=QO[lo:hi, 0:H], in0=QS[lo:hi, 0:H], scalar=self.pselt[lo:hi, 0:1],
                                                     in1=tmp[lo:hi, :], op0=ALU.mult, op1=ALU.add),
             r=[Bqs, Btmp, self.B_const], w=[Bqo])
        S.op("act", lambda e: e.activation(out=QO[lo:hi, H:H + CTX], in_=QS[lo:hi, LAT:NTOK], func=AF.Copy), r=[Bqs], w=[Bqo])

    def blend(self, out, a, b, Ba, Bb, Bout, lo=0, hi=128, col=0):
        S = self.S
        s0 = self.pselt[lo:hi, col:col + 1]
        s1 = self.pselt[lo:hi, col + 1:col + 2]
        S.op("dve", lambda e: e.tensor_scalar(out=b, in0=b, scalar1=s1, scalar2=None, op0=ALU.mult), r=[Bb, self.B_const], w=[Bb])
        S.op("dve", lambda e: e.scalar_tensor_tensor(out=out, in0=a, scalar=s0, in1=b, op0=ALU.mult, op1=ALU.add),
             r=[Ba, Bb, self.B_const], w=[Bout])

    def yodd_loc(self, t):
        if t < NT_LAT:
            return t // 16, (t % 16) * 128
        return 4, (t - NT_LAT) * 128

    def y_dst(self, tok0, c0, c1):
        H = LAT // 2
        if tok0 < H:
            k, r = tok0 // 1024, tok0 % 1024
            return self.yown_t[k].ap()[r:r + 128, c0:c1]
        return self.ybuf[LAT + tok0 - H:LAT + tok0 - H + 128, c0:c1]

    def da_attention(self, i, l):
        S = self.S
        lam_init = 0.8 - 0.6 * math.exp(-0.3 * l)
        with contextlib.ExitStack() as st:
            KTs = [self.sb(st, f"da_KT{j}", [128, NTOK], BF16) for j in range(2)]
            VAs = [self.sb(st, f"da_VA{j}", [128, NT, 129], BF16) for j in range(2)]
            QT1s = [self.sb(st, f"da_QT1{j}", [128, NTOK], BF16) for j in range(2)]
            QT2s = [self.sb(st, f"da_QT2{j}", [128, NTOK], BF16) for j in range(2)]
            Bkvs = S.bufs("da_kv", 2)
            for j in range(2):
                S.op("pool", lambda e, j=j: e.memset(QT1s[j][64:128, :], 0.0), w=[Bkvs[j]])
                S.op("pool", lambda e, j=j: e.memset(QT2s[j][0:64, :], 0.0), w=[Bkvs[j]])
            lamt = self.sb(st, "lamt", [128, 256])
            lamw = self.sb(st, "lamw", [128, 8])
            subg = self.sb(st, "subg", [128, 128])
            Bl = S.buf("lam")
            eps_t = self.sb(st, "da_eps", [128, 1])
            S.op("pool", lambda e: e.memset(eps_t[:], RMS_EPS), w=[Bl])
            S.dma("sp", lamt[:], self.ev_lam[i, :, :].to_broadcast([128, 256]), w=[Bl])
            S.dma("sp", subg[:], self.ev_subg[i, :, :].to_broadcast([128, 128]), w=[Bl])
            junk = self.sb(st, "da_junk", [128, 128])
            Bj = S.buf("da_junk")
            S.op("dve", lambda e: e.tensor_tensor(out=junk[:, 0:64], in0=lamt[:, 0:64], in1=lamt[:, 64:128], op=ALU.mult),
                 r=[Bl], w=[Bj])
            S.op("dve", lambda e: e.reduce_sum(out=lamw[:, 0:1], in_=junk[:, 0:64], axis=AX.X), r=[Bj], w=[Bl])
            S.op("dve", lambda e: e.tensor_tensor(out=junk[:, 64:128], in0=lamt[:, 128:192], in1=lamt[:, 192:256], op=ALU.mult),
                 r=[Bl], w=[Bj])
            S.op("dve", lambda e: e.reduce_sum(out=lamw[:, 1:2], in_=junk[:, 64:128], axis=AX.X), r=[Bj], w=[Bl])
            S.op("act", lambda e: e.activation(out=lamw[:, 2:4], in_=lamw[:, 0:2], func=AF.Exp), r=[Bl], w=[Bl])
            S.op("dve", lambda e: e.tensor_tensor(out=lamw[:, 4:5], in0=lamw[:, 3:4], in1=lamw[:, 2:3], op=ALU.subtract),
                 r=[Bl], w=[Bl])
            S.op("dve", lambda e: e.tensor_scalar(out=lamw[:, 5:6], in0=lamw[:, 4:5], scalar1=-lam_init, scalar2=None, op0=ALU.add),
                 r=[Bl], w=[Bl])
            NF = 3
            rr = [self.sb(st, f"da_rr{j}", [128, 8]) for j in range(NF)]
            ta = [self.sb(st, f"da_ta{j}", [128, 128]) for j in range(NF)]
            td = [self.sb(st, f"da_td{j}", [128, 128]) for j in range(NF)]
            to = [self.sb(st, f"da_to{j}", [128, 128], BF16) for j in range(NF)]
            Bf = S.bufs("da_fin", NF)
            Bto = S.bufs("da_to", NF)
            self._fi = 0
            def da_load(h):
                j = h % 2
                S.dma("sp", KTs[j][:], self.kda[h, :, :], w=[Bkvs[j]])
                S.dma("act", QT1s[j][0:64, :], self.qda[h, 0:64, :], w=[Bkvs[j]])
                S.dma("act", QT2s[j][64:128, :], self.qda[h, 64:128, :], w=[Bkvs[j]])
                S.dma("sp", VAs[j][:], self.vda[h, :, :, :], w=[Bkvs[j]])

            da_load(0)
            for h in range(2):
                if h + 1 < 2:
                    da_load(h + 1)
                KT, VA, QT1, QT2, Bkv = KTs[h % 2], VAs[h % 2], QT1s[h % 2], QT2s[h % 2], Bkvs[h % 2]

                def fin(qbi, q0, sub, accs, h=h):
                    j = self._fi % NF
                    self._fi += 1
                    (a1, B1), (a2, B2) = accs
                    S.op("dve", lambda e: e.reciprocal(rr[j][:, 0:1], a1[:, 128:129]), r=[B1], w=[Bf[j]])
                    S.op("dve", lambda e: e.reciprocal(rr[j][:, 1:2], a2[:, 128:129]), r=[B2], w=[Bf[j]])
                    S.op("dve", lambda e: e.tensor_tensor(out=rr[j][:, 2:3], in0=rr[j][:, 1:2], in1=lamw[:, 5:6], op=ALU.mult),
                         r=[Bf[j], Bl], w=[Bf[j]])
                    S.op("dve", lambda e: e.tensor_scalar(out=ta[j][:], in0=a1[:, 0:128], scalar1=rr[j][:, 0:1], scalar2=None,
                                                          op0=ALU.mult), r=[B1, Bf[j]], w=[Bf[j]])
                    S.op("dve", lambda e: e.scalar_tensor_tensor(out=td[j][:], in0=a2[:, 0:128], scalar=rr[j][:, 2:3], in1=ta[j][:],
                                                                 op0=ALU.mult, op1=ALU.add), r=[B2, Bf[j]], w=[Bf[j]])
                    S.op("act", lambda e: e.activation(out=ta[j][:], in_=td[j][:], func=AF.Square, accum_out=rr[j][:, 3:4]),
                         r=[Bf[j]], w=[Bf[j]])
                    S.op("act", lambda e: e.activation(out=rr[j][:, 4:5], in_=rr[j][:, 3:4], func=AF.Sqrt, scale=1.0 / 128,
                                                       bias=eps_t[:, 0:1]), r=[Bf[j], Bl], w=[Bf[j]])
                    S.op("dve", lambda e: e.reciprocal(rr[j][:, 5:6], rr[j][:, 4:5]), r=[Bf[j]], w=[Bf[j]])
                    S.op("pool", lambda e: e.tensor_scalar(out=td[j][:], in0=td[j][:], scalar1=rr[j][:, 5:6], scalar2=(1.0 - lam_init),
                                                           op0=ALU.mult, op1=ALU.mult), r=[Bf[j]], w=[Bf[j]])
                    S.op("pool", lambda e: e.tensor_tensor(out=to[j][:], in0=td[j][:], in1=subg[:], op=ALU.mult),
                         r=[Bf[j], Bl], w=[Bto[j]])
                    kk, r0 = self.yodd_loc((q0 + sub * 128) // 128)
                    S.dma("pool", self.yodd_t[kk].ap()[r0:r0 + 128, h * 128:(h + 1) * 128], to[j][:], r=[Bto[j]])

                self.attn_core(st, "da", KT, None, VA, Bkv, [(QT1, 0, 128), (QT2, 0, 128)], 129, DA_SCALE,
                               self.qblocks_all(), fin)
            S.barrier()
            self._attn_res = None

    def mla_attention(self, i):
        S = self.S
        with contextlib.ExitStack() as st:
            KTs = [self.sb(st, f"ml_KT{j}", [128, NTOK], BF16) for j in range(2)]
            QTs = [self.sb(st, f"ml_QT{j}", [128, NTOK], BF16) for j in range(2)]
            VAs = [self.sb(st, f"ml_VA{j}", [128, NT, 65], BF16) for j in range(2)]
            Bkvs = S.bufs("ml_kv", 2)
            NF = 3
            rr = [self.sb(st, f"ml_rr{j}", [128, 2]) for j in range(NF)]
            to = [self.sb(st, f"ml_to{j}", [128, 64], BF16) for j in range(NF)]
            Bto = S.bufs("ml_to", NF)
            self._fi = 0
            def ml_load(h):
                j = h % 2
                S.dma("sp", KTs[j][0:64, :], self.kmn[h, :, :], w=[Bkvs[j]])
                S.dma("sp", KTs[j][64:96, :], self.krt[:, :], w=[Bkvs[j]])
                S.dma("act", QTs[j][0:96, :], self.qm[h, :, :], w=[Bkvs[j]])
                S.dma("sp", VAs[j][:], self.vm[h, :, :, :], w=[Bkvs[j]])

            ml_load(0)
            for h in range(4):
                if h + 1 < 4:
                    ml_load(h + 1)
                KT, VA, QT, Bkv = KTs[h % 2], VAs[h % 2], QTs[h % 2], Bkvs[h % 2]

                def fin(qbi, q0, sub, accs, h=h):
                    j = self._fi % NF
                    self._fi += 1
                    (a1, B1), = accs
                    S.op("dve", lambda e: e.reciprocal(rr[j][:, 0:1], a1[:, 64:65]), r=[B1], w=[Bto[j]])
                    S.op("dve", lambda e: e.tensor_scalar(out=to[j][:], in0=a1[:, 0:64], scalar1=rr[j][:, 0:1], scalar2=None,
                                                          op0=ALU.mult), r=[B1, Bto[j]], w=[Bto[j]])
                    tok0 = q0 + sub * 128
                    kk, r0 = self.yodd_loc(tok0 // 128)
                    S.dma("pool", self.yodd_t[kk].ap()[r0:r0 + 128, 256 + h * 64:256 + (h + 1) * 64], to[j][:], r=[Bto[j]])

                self.attn_core(st, "ml", KT, None, VA, Bkv, [(QT, 0, 96)], 65, MLA_SCALE, self.qblocks_all(), fin)
            S.barrier()
            self._attn_res = None

    def out_stage(self, wo_dram, src, dst, last, ysplit=False):
        S = self.S
        with contextlib.ExitStack() as st:
            wo = self.sb(st, "wo", [128, 8, D], BF16)
            Bw = S.buf("wo")
            pieces = [(wo[:, k, :], wo_dram[k * 128:(k + 1) * 128, :], None) for k in range(8)]
            self.load_cast(st, None, None, 128, D, pieces, Bw, name="wo")
            NB = 3
            yt = [self.sb(st, f"o_yt{j}", [128, D], BF16) for j in range(NB)]
            gt = [self.sb(st, f"o_gt{j}", [128, D], BF16) for j in range(NB)]
            xt = [self.sb(st, f"o_xt{j}", [128, D]) for j in range(NB)]
            yg = [self.sb(st, f"o_yg{j}", [128, D], BF16) for j in range(NB)]
            ygT = [self.sb(st, f"o_ygT{j}", [128, D], BF16) for j in range(NB)]
            rt = [self.sb(st, f"o_rt{j}", [128, D]) for j in range(NB)]
            ot = [self.sb(st, f"o_ot{j}", [128, D]) for j in range(NB)]
            stt = [self.sb(st, f"o_st{j}", [128, 16]) for j in range(NB)]
            By, Bg, Bx, Byg, BygT, Br, Bo, Bs = (S.bufs(n, NB) for n in ("o_yt", "o_gt", "o_xt", "o_yg", "o_ygT", "o_rt", "o_ot", "o_st"))
            ptr = [self.ps(st, f"o_ptr{j}", [128, D], BF16) for j in range(2)]
            Bptr = S.bufs("o_ptr", 2)
            pout = [self.ps(st, f"o_po{j}", [128, 512]) for j in range(4)]
            Bpo = S.bufs("o_po", 4)
            eps_t = self.sb(st, "o_eps", [128, 1])
            S.op("pool", lambda e: e.memset(eps_t[:], LN_EPS), w=[Bw])
            ntiles = NT_LAT if (last and self.final_out) else NT
            tiles = [t for t in range(ntiles) if self.tile_filter is None or t in self.tile_filter]

            def stage_a(t):
                j = t % NB
                tok0 = t * 128
                isctx = t >= NT_LAT
                G = self.g_c if isctx else self.g_l
                if ysplit == "odd":
                    k, r0 = self.yodd_loc(t)
                    rows = self.yodd_rows[k]
                    g = self.yoddg_t[k].ap()
                    S.dma("sp", yt[j][:, 0:256], g[r0:r0 + 128, 0:256], w=[By[j]])
                    S.dma("sp", yt[j][:, 256:512], g[rows + r0:rows + r0 + 128, 0:256], w=[By[j]])
                    S.dma("sp", yt[j][:, 512:768], g[r0:r0 + 128, 256:512], w=[By[j]])
                    S.dma("sp", yt[j][:, 768:1024], g[rows + r0:rows + r0 + 128, 256:512], w=[By[j]])
                else:
                    if ysplit and not isctx:
                        half, u = t // 32, t % 32
                        r0 = half * 1024 + (u % 8) * 128
                        ysrc = self.ygath_t[u // 8].ap()[r0:r0 + 128, :]
                    else:
                        ysrc = self.ybuf[tok0:tok0 + 128, :]
                    S.dma("sp", yt[j][:], ysrc, w=[By[j]])
                S.dma("act", gt[j][:], self.gate[tok0:tok0 + 128, :], w=[Bg[j]])
                S.dma("sp", xt[j][:], src[tok0:tok0 + 128, :], w=[Bx[j]])
                S.op("dve", lambda e, j=j: e.tensor_tensor(out=yg[j][:], in0=yt[j][:], in1=gt[j][:], op=ALU.mult),
                     r=[By[j], Bg[j]], w=[Byg[j]])
                pj = t % 2
                for ec in range(8):
                    S.op("pe", lambda e, ec=ec, j=j, pj=pj: e.transpose(
                        ptr[pj][:, ec * 128:(ec + 1) * 128], yg[j][:, ec * 128:(ec + 1) * 128], self.ident_b[:]),
                        r=[Byg[j], self.B_const], w=[Bptr[pj]])
                S.op("act", lambda e, j=j, pj=pj: e.activation(out=ygT[j][:], in_=ptr[pj][:], func=AF.Copy),
                     r=[Bptr[pj]], w=[BygT[j]])
                for n in range(2):
                    pn = (t % 2) * 2 + n
                    for ec in range(8):
                        S.op("pe", lambda e, ec=ec, n=n, pn=pn, j=j: e.matmul(
                            pout[pn][:, :], ygT[j][:, ec * 128:(ec + 1) * 128], wo[:, ec, n * 512:(n + 1) * 512],
                            start=(ec == 0), stop=(ec == 7)), r=[BygT[j], Bw], w=[Bpo[pn]])

            def stage_b(t):
                j = t % NB
                tok0 = t * 128
                isctx = t >= NT_LAT
                G = self.g_c if isctx else self.g_l
                for n in range(2):
                    pn = (t % 2) * 2 + n
                    S.op("dve", lambda e, n=n, pn=pn, j=j, G=G: e.tensor_tensor(
                        out=rt[j][:, n * 512:(n + 1) * 512], in0=pout[pn][:, :], in1=G[:, n * 512:(n + 1) * 512], op=ALU.mult),
                        r=[Bpo[pn], self.B_mod], w=[Br[j]])
                S.op("dve", lambda e, j=j: e.scalar_tensor_tensor(
                    out=rt[j][:], in0=xt[j][:], scalar=ALPHA, in1=rt[j][:], op0=ALU.mult, op1=ALU.add),
                    r=[Bx[j], Br[j]], w=[Br[j]])
                for n in range(2):
                    S.op("dve", lambda e, n=n, j=j: e.bn_stats(stt[j][:, n * 6:(n + 1) * 6], rt[j][:, n * 512:(n + 1) * 512]),
                         r=[Br[j]], w=[Bs[j]])
                S.op("dve", lambda e, j=j: e.bn_aggr(stt[j][:, 12:14], stt[j][:, 0:12]), r=[Bs[j]], w=[Bs[j]])
                S.op("act", lambda e, j=j: e.activation(out=stt[j][:, 14:15], in_=stt[j][:, 13:14], func=AF.Sqrt, scale=1.0,
                                                        bias=eps_t[:, 0:1]), r=[Bs[j], Bw], w=[Bs[j]])
                S.op("dve", lambda e, j=j: e.reciprocal(stt[j][:, 15:16], stt[j][:, 14:15]), r=[Bs[j]], w=[Bs[j]])
                S.op("dve", lambda e, j=j: e.tensor_scalar(
                    out=ot[j][:], in0=rt[j][:], scalar1=stt[j][:, 12:13], scalar2=stt[j][:, 15:16], op0=ALU.subtract, op1=ALU.mult),
                    r=[Br[j], Bs[j]], w=[Bo[j]])
                S.op("pool", lambda e, j=j: e.tensor_tensor(out=ot[j][:], in0=ot[j][:], in1=self.lng[:], op=ALU.mult),
                     r=[Bo[j], self.B_mod], w=[Bo[j]])
                S.op("pool", lambda e, j=j: e.tensor_tensor(out=ot[j][:], in0=ot[j][:], in1=self.lnb[:], op=ALU.add),
                     r=[Bo[j], self.B_mod], w=[Bo[j]])
                S.dma("pool", dst[tok0:tok0 + 128, :], ot[j][:], r=[Bo[j]])

            for idx, t in enumerate(tiles):
                if idx == 0:
                    stage_a(t)
                if idx + 1 < len(tiles):
                    stage_a(tiles[idx + 1])
                stage_b(t)
            S.barrier()

    def odd_layer(self, l, src, dst, last):
        i = l // 2
        self.adaln(l)
        self.odd_project(i, src)
        self.odd_conv(i)
        self.mlstm(i)
        self.mlstm_post(i)
        self.na_attention(i, last)
        self.S.allgather_pairs(self.yodd_t, self.yoddg_t)
        self.out_stage(self.od_wo[i], src, dst, last, ysplit="odd")

    def odd_project(self, i, src):
        S = self.S
        with contextlib.ExitStack() as st:
            wb = self.sb(st, "od_wb", [128, 8, OD_COLS], BF16)
            bcol = self.sb(st, "od_bcol", [128, 8])
            brow_f = self.sb(st, "od_brow_f", [1, 1800])
            brow = self.sb(st, "od_brow", [1, 1800], BF16)
            fb = self.sb(st, "od_fb", [128, 4])
            Bw = S.buf("od_w")
            S.dma("sp", bcol[:], self.od_bcol[i, :, :], w=[Bw])
            S.dma("sp", brow_f[:], self.od_brow[i, :, :], w=[Bw])
            S.dma("sp", fb[:], self.od_fb[i, :, :].to_broadcast([128, 4]), w=[Bw])
            S.op("dve", lambda e: e.tensor_copy(brow[:], brow_f[:]), r=[Bw], w=[Bw])
            pieces = []
            for k in range(8):
                for c in range(2):
                    pieces.append((wb[:, k, c * 1412:(c + 1) * 1412],
                                   self.od_w[i, k * 128:(k + 1) * 128, c * 1412:(c + 1) * 1412], None))
            self.load_cast(st, None, None, 128, 1412, pieces, Bw, name="odw")
            NXS = 6
            xt = [self.sb(st, f"xt{j}", [128, D]) for j in range(NXS)]
            Bx = S.bufs("xt", NXS)
            hT = [self.sb(st, f"hT{j}", [128, 8, 512], BF16) for j in range(2)]
            Bh = S.bufs("hT", 2)
            ff = [self.sb(st, f"ff{j}", [128, 512]) for j in range(3)]
            Bff = S.bufs("ff", 3)
            fo = [self.sb(st, f"fo{j}", [128, 512], BF16) for j in range(3)]
            Bfo = S.bufs("fo", 3)
            va = [self.sb(st, f"va{j}", [128, 2, 129], BF16) for j in range(2)]
            Bva = S.bufs("va", 2)
            vna = [self.sb(st, f"vna{j}", [128, 4, 65], BF16) for j in range(2)]
            Bvna = S.bufs("vna", 2)
            ot = [self.sb(st, f"ot{j}", [128, 256]) for j in range(2)]
            Bot = S.bufs("ot", 2)
            gt = [self.sb(st, f"gt{j}", [128, D], BF16) for j in range(2)]
            Bgt = S.bufs("gt", 2)
            gs = [self.sb(st, f"gs{j}", [128, 4, 8]) for j in range(2)]
            gtmp = [self.sb(st, f"gtmp{j}", [128, 4, 4]) for j in range(2)]
            Bgs = S.bufs("gs", 2)
            pp = [self.ps(st, f"pp{j}", [128, 512]) for j in range(7)]
            pg = self.ps(st, "pg", [128, 512])
            Bpp = S.bufs("pp", 7)
            Bpg = S.buf("pg")
            for j in range(2):
                S.op("pool", lambda e, j=j: e.memset(va[j][:], 1.0), w=[Bva[j]])
                S.op("pool", lambda e, j=j: e.memset(vna[j][:], 1.0), w=[Bvna[j]])
            self._pp_i = 0

            def next_pp():
                j = self._pp_i % 7
                self._pp_i += 1
                return pp[j], Bpp[j]

            nblk = (NTOK + 511) // 512
            self._xi = 0
            self._evac = 0
            self._ffi = 0
            self._foi = 0

            def stage_a(blk):
                t0 = blk * 512
                ntok = min(512, NTOK - t0)
                nsub = ntok // 128
                m = 0 if t0 < LAT else 1
                hb = blk % 2
                for s in range(nsub):
                    xj = self._xi % NXS
                    self._xi += 1
                    S.dma("sp", xt[xj][:], src[t0 + s * 128:t0 + (s + 1) * 128, :], w=[Bx[xj]])
                    for half in range(2):
                        p, Bp = next_pp()
                        for q in range(4):
                            dc = half * 4 + q
                            S.op("pe", lambda e, p=p, q=q, dc=dc, xj=xj: e.transpose(
                                p[:, q * 128:(q + 1) * 128], xt[xj][:, dc * 128:(dc + 1) * 128], self.ident_f[:]),
                                r=[Bx[xj], self.B_const], w=[Bp])
                        for q in range(4):
                            dc = half * 4 + q
                            if self._evac % 2 == 0:
                                S.op("act", lambda e, p=p, q=q, dc=dc, s=s, m=m: e.activation(
                                    out=hT[hb][:, dc, s * 128:(s + 1) * 128], in_=p[:, q * 128:(q + 1) * 128],
                                    func=AF.Identity, scale=self.sc1[:, dc, m:m + 1], bias=self.sh[:, dc, m:m + 1]),
                                    r=[Bp, self.B_mod], w=[Bh[hb]])
                            else:
                                S.op("dve", lambda e, p=p, q=q, dc=dc, s=s, m=m: e.tensor_scalar(
                                    out=hT[hb][:, dc, s * 128:(s + 1) * 128], in0=p[:, q * 128:(q + 1) * 128],
                                    scalar1=self.sc1[:, dc, m:m + 1], scalar2=self.sh[:, dc, m:m + 1],
                                    op0=ALU.mult, op1=ALU.add), r=[Bp, self.B_mod], w=[Bh[hb]])
                            self._evac += 1

            def stage_b(blk):
                t0 = blk * 512
                ntok = min(512, NTOK - t0)
                nsub = ntok // 128
                m = 0 if t0 < LAT else 1
                hb = blk % 2
                ffi, foi = self._ffi, self._foi
                for c in range(8):
                    p, Bp = next_pp()
                    for k in range(8):
                        S.op("pe", lambda e, p=p, k=k, c=c: e.matmul(
                            p[:, :ntok], wb[:, k, c * 128:(c + 1) * 128], hT[hb][:, k, :ntok], start=(k == 0), stop=(k == 7)),
                            r=[Bw, Bh[hb]], w=[Bp])
                    if c < 4:
                        fj = ffi % 3
                        ffi += 1
                        if c % 2 == 0:
                            S.op("act", lambda e, p=p, c=c, fj=fj: e.activation(
                                out=ff[fj][:, :ntok], in_=p[:, :ntok], func=AF.Identity, bias=bcol[:, c:c + 1]),
                                r=[Bp, Bw], w=[Bff[fj]])
                        else:
                            S.op("dve", lambda e, p=p, c=c, fj=fj: e.tensor_scalar(
                                out=ff[fj][:, :ntok], in0=p[:, :ntok], scalar1=bcol[:, c:c + 1], scalar2=None, op0=ALU.add),
                                r=[Bp, Bw], w=[Bff[fj]])
                        S.dma("pool", self.qkpre[c, :, t0:t0 + ntok], ff[fj][:, :ntok], r=[Bff[fj]])
                    else:
                        fj = foi % 3
                        foi += 1
                        if c % 2 == 0:
                            S.op("act", lambda e, p=p, c=c, fj=fj: e.activation(
                                out=fo[fj][:, :ntok], in_=p[:, :ntok], func=AF.Identity, bias=bcol[:, c:c + 1]),
                                r=[Bp, Bw], w=[Bfo[fj]])
                        else:
                            S.op("dve", lambda e, p=p, c=c, fj=fj: e.tensor_scalar(
                                out=fo[fj][:, :ntok], in0=p[:, :ntok], scalar1=bcol[:, c:c + 1], scalar2=None, op0=ALU.add),
                                r=[Bp, Bw], w=[Bfo[fj]])
                        dd = self.qda if c < 6 else self.kda
                        S.dma("pool", dd[(c - 4) % 2, :, t0:t0 + ntok], fo[fj][:, :ntok], r=[Bfo[fj]])
                gb = blk % 2
                for s in range(nsub):
                    tt = (t0 // 128) + s
                    tok0 = t0 + s * 128

                    def tm(p, c0, n, b0, s=s):
                        for k in range(8):
                            S.op("pe", lambda e, k=k: e.matmul(
                                p[:, 0:n], hT[hb][:, k, s * 128:(s + 1) * 128], wb[:, k, c0:c0 + n], start=(k == 0), stop=False),
                                r=[Bw, Bh[hb]], w=[Bp])
                        S.op("pe", lambda e: e.matmul(p[:, 0:n], self.ones_b[0:1, :], brow[0:1, b0:b0 + n], start=False, stop=True),
                             r=[Bw, self.B_const], w=[Bp])
                    p, Bp = next_pp()
                    tm(p, OV, 256, 0)
                    vj = tt % 2
                    S.op("act", lambda e, p=p, vj=vj: e.activation(
                        out=va[vj][:, :, 0:128], in_=p[:, 0:256].rearrange("p (h c) -> p h c", h=2), func=AF.Copy),
                        r=[Bp], w=[Bva[vj]])
                    S.dma("pool", self.vda[:, :, tt, :].rearrange("h p c -> p h c"), va[vj][:], r=[Bva[vj]])
                    p, Bp = next_pp()
                    tm(p, OO, 256, 256)
                    oj = tt % 2
                    S.op("act", lambda e, p=p, oj=oj: e.activation(out=ot[oj][:], in_=p[:, 0:256], func=AF.Sigmoid), r=[Bp], w=[Bot[oj]])
                    S.dma("pool", self.og[tok0:tok0 + 128, :], ot[oj][:], r=[Bot[oj]])
                    Bp = Bpg
                    for k in range(8):
                        S.op("pe", lambda e, k=k, s=s: e.matmul(
                            pg[:, s * 8:(s + 1) * 8], hT[hb][:, k, s * 128:(s + 1) * 128], wb[:, k, OGT:OGT + 8],
                            start=(k == 0), stop=False), r=[Bw, Bh[hb]], w=[Bpg])
                    S.op("pe", lambda e, s=s: e.matmul(pg[:, s * 8:(s + 1) * 8], self.ones_b[0:1, :], brow[0:1, 512:520],
                                                       start=False, stop=True), r=[Bw, self.B_const], w=[Bpg])
                    p, Bp = next_pp()
                    tm(p, OVN, 256, 520)
                    vj = tt % 2
                    S.op("act", lambda e, p=p, vj=vj: e.activation(
                        out=vna[vj][:, :, 0:64], in_=p[:, 0:256].rearrange("p (h c) -> p h c", h=4), func=AF.Copy),
                        r=[Bp], w=[Bvna[vj]])
                    S.dma("pool", self.vm[:, :, tt, :].rearrange("h p c -> p h c"), vna[vj][:], r=[Bvna[vj]])
                    gj = tt % 2
                    for n in range(2):
                        p, Bp = next_pp()
                        tm(p, OG + n * 512, 512, 776 + n * 512)
                        S.op("act", lambda e, p=p, n=n, gj=gj: e.activation(
                            out=gt[gj][:, n * 512:(n + 1) * 512], in_=p[:, :], func=AF.Silu), r=[Bp], w=[Bgt[gj]])
                    S.dma("pool", self.gate[tok0:tok0 + 128, :], gt[gj][:], r=[Bgt[gj]])
                pgv = pg[:, 0:nsub * 8].rearrange("p (s c) -> p s c", c=8)
                S.op("dve", lambda e, pgv=pgv: e.tensor_copy(gs[gb][:, 0:nsub, 0:4], pgv[:, :, 0:4]), r=[Bpg], w=[Bgs[gb]])
                for s in range(nsub):
                    S.op("dve", lambda e, s=s: e.tensor_tensor(out=gtmp[gb][:, s, :], in0=pg[:, s * 8 + 4:s * 8 + 8], in1=fb[:, :],
                                                               op=ALU.add), r=[Bpg, Bw], w=[Bgs[gb]])
                S.op("act", lambda e: e.activation(out=gtmp[gb][:, 0:nsub, :], in_=gtmp[gb][:, 0:nsub, :], func=AF.Exp, scale=-1.0),
                     r=[Bgs[gb]], w=[Bgs[gb]])
                S.op("act", lambda e: e.activation(out=gtmp[gb][:, 0:nsub, :], in_=gtmp[gb][:, 0:nsub, :], func=AF.Ln, bias=1.0),
                     r=[Bgs[gb]], w=[Bgs[gb]])
                S.op("dve", lambda e: e.tensor_scalar(out=gs[gb][:, 0:nsub, 4:8], in0=gtmp[gb][:, 0:nsub, :], scalar1=-1.0,
                                                      scalar2=None, op0=ALU.mult), r=[Bgs[gb]], w=[Bgs[gb]])
                tt0 = t0 // 128
                S.dma("pool", self.gates[:, tt0:tt0 + nsub, :], gs[gb][:, 0:nsub, :], r=[Bgs[gb]])
                self._ffi, self._foi = ffi, foi

            stage_a(0)
            for blk in range(nblk):
                if blk + 1 < nblk:
                    stage_a(blk + 1)
                stage_b(blk)
            S.barrier()

    def odd_conv(self, i):
        S = self.S
        SEG = 2048
        with contextlib.ExitStack() as st:
            cw = self.sb(st, "cv_w", [128, 4, 5])
            cb = self.sb(st, "cv_b", [128, 4])
            Bw = S.buf("cv_w")
            S.dma("sp", cw[:], self.od_convw[i, :, :, :], w=[Bw])
            S.dma("sp", cb[:], self.od_convb[i, :, :], w=[Bw])
            xin = [self.sb(st, f"cv_x{j}", [128, SEG + 4]) for j in range(2)]
            Bxin = S.bufs("cv_x", 2)
            acc = [self.sb(st, f"cv_a{j}", [128, SEG]) for j in range(2)]
            Bacc = S.bufs("cv_a", 2)
            tmp = [self.sb(st, f"cv_t{j}", [128, SEG]) for j in range(2)]
            Btmp = S.bufs("cv_t", 2)
            outb = [self.sb(st, f"cv_o{j}", [128, SEG], BF16) for j in range(2)]
            Bout = S.bufs("cv_o", 2)
            segs = [(a, SEG, 0, LAT) for a in range(0, LAT, SEG)] + [(LAT, CTX, LAT, LAT + CTX)]
            it = 0
            for c in range(4):
                for (t0, n, lo, hi) in segs:
                    j = it % 2
                    it += 1
                    a = max(t0 - 2, lo)
                    b = min(t0 + n + 2, hi)
                    if a > t0 - 2:
                        S.op("pool", lambda e, j=j: e.memset(xin[j][:, 0:2], 0.0), w=[Bxin[j]])
                    if b < t0 + n + 2:
                        S.op("pool", lambda e, j=j, n=n: e.memset(xin[j][:, n + 2:n + 4], 0.0), w=[Bxin[j]])
                    S.dma("sp", xin[j][:, a - (t0 - 2):b - (t0 - 2)], self.qkpre[c, :, a:b], w=[Bxin[j]])
                    S.op("act", lambda e, j=j, n=n, c=c: e.activation(
                        out=acc[j][:, 0:n], in_=xin[j][:, 0:n], func=AF.Copy, scale=cw[:, c, 0:1]),
                        r=[Bxin[j], Bw], w=[Bacc[j]])
                    for k in range(1, 5):
                        S.op("dve", lambda e, j=j, n=n, c=c, k=k: e.scalar_tensor_tensor(
                            out=acc[j][:, 0:n], in0=xin[j][:, k:k + n], scalar=cw[:, c, k:k + 1], in1=acc[j][:, 0:n],
                            op0=ALU.mult, op1=ALU.add), r=[Bxin[j], Bw, Bacc[j]], w=[Bacc[j]])
                    if True:
                        S.op("act", lambda e, j=j, n=n, c=c: e.activation(
                            out=outb[j][:, 0:n], in_=acc[j][:, 0:n], func=AF.Silu, bias=cb[:, c:c + 1]),
                            r=[Bacc[j], Bw], w=[Bout[j]])
                        dd = self.mq if c < 2 else self.mk
                        S.dma("pool", dd[c % 2, :, t0:t0 + n], outb[j][:, 0:n], r=[Bout[j]])
                    else:
                        S.op("act", lambda e, j=j, n=n, c=c: e.activation(
                            out=tmp[j][:, 0:n], in_=acc[j][:, 0:n], func=AF.Silu, bias=cb[:, c:c + 1]),
                            r=[Bacc[j], Bw], w=[Btmp[j]])
                        S.op("pool", lambda e, j=j, n=n: e.tensor_scalar(
                            out=outb[j][:, 0:n], in0=tmp[j][:, 0:n], scalar1=128 ** -0.5, scalar2=None, op0=ALU.mult),
                            r=[Btmp[j]], w=[Bout[j]])
                        S.dma("sp", self.mk[c - 4, :, t0:t0 + n], outb[j][:, 0:n], r=[Bout[j]])
            S.barrier()

    def mlstm(self, i):
        S = self.S
        with contextlib.ExitStack() as st:
            tri = self.sb(st, "ml_tri", [128, 3, 128])
            G = self.sb(st, "ml_G", [128, NT, 8])
            Ao = self.sb(st, "ml_Ao", [128, NT, 4])
            A2o = self.sb(st, "ml_A2o", [128, NT, 4])
            Bqo = self.sb(st, "ml_Bqo", [128, NT, 4])
            EBo = self.sb(st, "ml_EBo", [128, NT, 4])
            Bg = S.buf("ml_g")
            S.dma("sp", tri[:], self.tri[:, :, :], w=[Bg])
            S.dma("sp", G[:], self.gates[:, :, :], w=[Bg])
            lnks = self.sb(st, "ml_lnks", [128, 1])
            S.op("pool", lambda e: e.memset(lnks[:], math.log(128 ** -0.5)), w=[Bg])
            with contextlib.ExitStack() as st1:
                pg = [self.ps(st1, f"ml_pg{j}", [128, 512]) for j in range(3)]
                Bpg = S.bufs("ml_pg", 3)
                tmpg = self.sb(st1, "ml_tmpg", [128, 32, 4])
                Btg = S.buf("ml_tmpg")
                grp = 0
                for g0 in range(0, NT, 32):
                    g1 = min(g0 + 32, NT)
                    pj = grp % 3
                    grp += 1
                    for t in range(g0, g1):
                        o = (t - g0) * 8
                        S.op("pe", lambda e, t=t, o=o, pj=pj: e.matmul(pg[pj][:, o:o + 2], tri[:, 0, :], G[:, t, 4:6], start=True, stop=True),
                             r=[Bg], w=[Bpg[pj]])
                        S.op("pe", lambda e, t=t, o=o, pj=pj: e.matmul(pg[pj][:, o + 2:o + 4], tri[:, 1, :], G[:, t, 6:8], start=True, stop=True),
                             r=[Bg], w=[Bpg[pj]])
                        S.op("pe", lambda e, t=t, o=o, pj=pj: e.matmul(pg[pj][:, o + 4:o + 8], tri[:, 2, :], G[:, t, 4:8], start=True, stop=True),
                             r=[Bg], w=[Bpg[pj]])
                    n = g1 - g0
                    pv = pg[pj][:, 0:n * 8].rearrange("p (t c) -> p t c", c=8)
                    S.op("dve", lambda e, pv=pv, g0=g0, g1=g1, n=n: e.tensor_tensor(
                        out=tmpg[:, 0:n, :], in0=G[:, g0:g1, 0:4], in1=pv[:, :, 0:4], op=ALU.subtract), r=[Bg, Bpg[pj]], w=[Btg])
                    S.op("act", lambda e, g0=g0, g1=g1, n=n: e.activation(out=Ao[:, g0:g1, :], in_=tmpg[:, 0:n, :], func=AF.Exp,
                                                                          bias=lnks[:, 0:1]), r=[Btg, Bg], w=[Bg])
                    S.op("act", lambda e, pv=pv, g0=g0, g1=g1: e.activation(out=Bqo[:, g0:g1, :], in_=pv[:, :, 0:4], func=AF.Exp),
                         r=[Bpg[pj]], w=[Bg])
                    S.op("act", lambda e, pv=pv, g0=g0, g1=g1: e.activation(out=EBo[:, g0:g1, :], in_=pv[:, :, 4:8], func=AF.Exp),
                         r=[Bpg[pj]], w=[Bg])
                    S.op("dve", lambda e, g0=g0, g1=g1: e.tensor_tensor(out=A2o[:, g0:g1, :], in0=Ao[:, g0:g1, :], in1=EBo[:, g0:g1, :],
                                                                        op=ALU.mult), r=[Bg], w=[Bg])
                S.barrier()
            qT = self.sb(st, "ml_qT", [128, NTOK], BF16)
            kT = self.sb(st, "ml_kT", [128, NTOK], BF16)
            V = self.sb(st, "ml_V", [128, NT, 129], BF16)
            KTOK = self.sb(st, "ml_KTOK", [128, NT, 128], BF16)
            Bin = S.buf("ml_in")
            Bkt = S.buf("ml_ktok")
            Cn = [self.sb(st, f"ml_Cn{d}", [128, 129]) for d in range(2)]
            Cnb = [[self.sb(st, f"ml_Cnb{d}{j}", [128, 129], BF16) for j in range(2)] for d in range(2)]
            BCn = S.bufs("ml_Cn", 2)
            BCnb = [S.bufs(f"ml_Cnb{d}", 2) for d in range(2)]
            NW = 6
            W = [self.sb(st, f"ml_W{j}", [128, 128], BF16) for j in range(NW)]
            BW = S.bufs("ml_W", NW)
            v2 = [self.sb(st, f"ml_v2{j}", [128, 129], BF16) for j in range(NW)]
            Bv2 = S.bufs("ml_v2", NW)
            ho = [self.sb(st, f"ml_ho{j}", [128, 128]) for j in range(NW)]
            Bho = S.bufs("ml_ho", NW)
            sm = [self.sb(st, f"ml_sm{j}", [128, 4]) for j in range(NW)]
            Bsm = S.bufs("ml_sm", NW)
            NS, NKV, NN = 2, 2, 4
            order = [[NT_LAT, NT_LAT + 1] + list(range(NT_LAT)), [NT_LAT + 1, NT_LAT] + list(range(NT_LAT - 1, -1, -1))]
            wi = 0
            for h in range(2):
                S.dma("sp", qT[:], self.mq[h, :, :], w=[Bin])
                S.dma("act", kT[:], self.mk[h, :, :], w=[Bin])
                S.dma("sp", V[:], self.vda[h, :, :, :], w=[Bin])
                with contextlib.ExitStack() as stp:
                    ppbs = [self.ps(stp, f"ml_ppb{j}", [128, 1024], BF16) for j in range(2)]
                    Bppbs = S.bufs("ml_ppb", 2)
                    for gi, t0 in enumerate(range(0, NT, 8)):
                        n = min(8, NT - t0)
                        ppb, Bppb = ppbs[gi % 2], Bppbs[gi % 2]
                        for q in range(n):
                            t = t0 + q
                            S.op("pe", lambda e, t=t, q=q, ppb=ppb: e.transpose(
                                ppb[:, q * 128:(q + 1) * 128], kT[:, t * 128:(t + 1) * 128], self.ident_b[:]),
                                r=[Bin, self.B_const], w=[Bppb])
                        S.op("act", lambda e, t0=t0, n=n, ppb=ppb: e.activation(
                            out=KTOK[:, t0:t0 + n, :], in_=ppb[:, 0:n * 128].rearrange("p (t c) -> p t c", c=128), func=AF.Copy),
                            r=[Bppb], w=[Bkt])
                    S.barrier()
                stq = contextlib.ExitStack()
                ps_s = [self.ps(stq, f"ml_pss{j}", [128, 512]) for j in range(NS)]
                ps_kv = [self.ps(stq, f"ml_pkv{j}", [128, 512]) for j in range(NKV)]
                ps_n = [self.ps(stq, f"ml_pn{j}", [128, 512]) for j in range(NN)]
                Bps_s, Bps_kv, Bps_n = S.bufs("ml_pss", NS), S.bufs("ml_pkv", NKV), S.bufs("ml_pn", NN)
                for d in range(2):
                    S.op("pool", lambda e, d=d: e.memset(Cn[d][:], 0.0), w=[BCn[d]])
                    S.op("pool", lambda e, d=d: e.memset(Cnb[d][0][:], 0.0), w=[BCnb[d][0]])

                def stage1(step, d, wi):
                    t = order[d][step]
                    hd = d * 2 + h
                    j = wi % NW
                    p3 = wi % NS
                    tsl = slice(t * 128, (t + 1) * 128)
                    S.op("pe", lambda e: e.matmul(ps_s[p3][:, 0:128], kT[:, tsl], qT[:, tsl], start=True, stop=True),
                         r=[Bin], w=[Bps_s[p3]])
                    S.op("dve", lambda e: e.scalar_tensor_tensor(
                        out=W[j][:], in0=ps_s[p3][:, 0:128], scalar=Ao[:, t, hd:hd + 1], in1=tri[:, d, :],
                        op0=ALU.mult, op1=ALU.mult), r=[Bps_s[p3], Bg], w=[BW[j]])
                    S.op("act", lambda e: e.activation(
                        out=v2[j][:], in_=V[:, t, :], func=AF.Copy, scale=A2o[:, t, hd:hd + 1]),
                        r=[Bin, Bg], w=[Bv2[j]])

                def stage2(step, d, wi):
                    t = order[d][step]
                    hd = d * 2 + h
                    cur, nxt = step % 2, (step + 1) % 2
                    j = wi % NW
                    p3 = wi % NKV
                    p6 = wi % NN
                    tsl = slice(t * 128, (t + 1) * 128)
                    S.op("pe", lambda e: e.matmul(ps_kv[p3][:, 0:129], KTOK[:, t, :], v2[j][:], start=True, stop=True),
                         r=[Bkt, Bv2[j]], w=[Bps_kv[p3]])
                    S.op("pe", lambda e: e.matmul(ps_n[p6][:, 0:129], W[j][:], V[:, t, :], start=True, stop=False),
                         r=[BW[j], Bin], w=[Bps_n[p6]])
                    S.op("pe", lambda e: e.matmul(ps_n[p6][:, 0:129], qT[:, tsl], Cnb[d][cur][:], start=False, stop=True),
                         r=[Bin, BCnb[d][cur]], w=[Bps_n[p6]])
                    S.op("dve", lambda e: e.scalar_tensor_tensor(
                        out=Cn[d][:], in0=Cn[d][:], scalar=EBo[:, t, hd:hd + 1], in1=ps_kv[p3][:, 0:129],
                        op0=ALU.mult, op1=ALU.add), r=[BCn[d], Bps_kv[p3], Bg], w=[BCn[d]])
                    S.op("act", lambda e: e.activation(out=Cnb[d][nxt][:], in_=Cn[d][:], func=AF.Copy),
                         r=[BCn[d]], w=[BCnb[d][nxt]])

                def back(step, d, wi):
                    t = order[d][step]
                    hd = d * 2 + h
                    j = wi % NW
                    p6 = wi % NN
                    S.op("dve", lambda e: e.tensor_tensor(
                        out=sm[j][:, 0:1], in0=ps_n[p6][:, 128:129], in1=Bqo[:, t, hd:hd + 1], op=ALU.mult),
                        r=[Bps_n[p6], Bg], w=[Bsm[j]])
                    S.op("dve", lambda e: e.scalar_tensor_tensor(out=sm[j][:, 1:2], in0=sm[j][:, 0:1], scalar=-1.0,
                                                                 in1=sm[j][:, 0:1], op0=ALU.mult, op1=ALU.max),
                         r=[Bsm[j]], w=[Bsm[j]])
                    S.op("dve", lambda e: e.tensor_scalar(out=sm[j][:, 1:2], in0=sm[j][:, 1:2], scalar1=1.0, scalar2=None,
                                                          op0=ALU.max), r=[Bsm[j]], w=[Bsm[j]])
                    S.op("dve", lambda e: e.reciprocal(sm[j][:, 2:3], sm[j][:, 1:2]), r=[Bsm[j]], w=[Bsm[j]])
                    S.op("dve", lambda e: e.tensor_tensor(
                        out=sm[j][:, 3:4], in0=sm[j][:, 2:3], in1=Bqo[:, t, hd:hd + 1], op=ALU.mult), r=[Bsm[j], Bg], w=[Bsm[j]])
                    S.op("dve", lambda e: e.tensor_scalar(
                        out=ho[j][:], in0=ps_n[p6][:, 0:128], scalar1=sm[j][:, 3:4], scalar2=None, op0=ALU.mult),
                        r=[Bps_n[p6], Bsm[j]], w=[Bho[j]])
                    S.dma("sp", self.hfb[d, t * 128:(t + 1) * 128, h * 128:(h + 1) * 128], ho[j][:], r=[Bho[j]])

                items = []
                for step in range(NT):
                    for d in range(2):
                        items.append((step, d, wi))
                        wi += 1
                npair = len(items) // 2
                for k in range(npair + 2):
                    if k < npair:
                        stage1(*items[2 * k])
                        stage1(*items[2 * k + 1])
                    if 0 <= k - 1 < npair:
                        stage2(*items[2 * (k - 1)])
                        stage2(*items[2 * (k - 1) + 1])
                    if 0 <= k - 2 < npair:
                        back(*items[2 * (k - 2)])
                        back(*items[2 * (k - 2) + 1])
                S.barrier()
                stq.close()
            S.barrier()

    def mlstm_post(self, i):
        S = self.S
        with contextlib.ExitStack() as st:
            ngo = self.sb(st, "mp_ngo", [128, 256])
            Bw = S.buf("mp_w")
            S.dma("sp", ngo[:], self.od_ng[i, :, :].to_broadcast([128, 256]), w=[Bw])
            eps_t = self.sb(st, "mp_eps", [128, 1])
            S.op("pool", lambda e: e.memset(eps_t[:], LN_EPS), w=[Bw])
            NB = 3
            hf = [self.sb(st, f"mp_hf{j}", [128, 256]) for j in range(NB)]
            hb = [self.sb(st, f"mp_hb{j}", [128, 256]) for j in range(NB)]
            ogo = [self.sb(st, f"mp_ogo{j}", [128, 256]) for j in range(NB)]
            hs = [self.sb(st, f"mp_hs{j}", [128, 256]) for j in range(NB)]
            yo = [self.sb(st, f"mp_yo{j}", [128, 256]) for j in range(NB)]
            yob = [self.sb(st, f"mp_yob{j}", [128, 256], BF16) for j in range(NB)]
            stt = [self.sb(st, f"mp_st{j}", [128, 2, 12]) for j in range(NB)]
            Bhf, Bhb, Bogo, Bhs, Byo, Bst = (S.bufs(n, NB) for n in ("mp_hf", "mp_hb", "mp_ogo", "mp_hs", "mp_yo", "mp_st"))

            def post_a(t):
                j = t % NB
                tok0 = t * 128
                S.dma("sp", hf[j][:], self.hfb[0, tok0:tok0 + 128, 0:256], w=[Bhf[j]])
                S.dma("act", hb[j][:], self.hfb[1, tok0:tok0 + 128, 0:256], w=[Bhb[j]])
                S.dma("pool", ogo[j][:], self.og[tok0:tok0 + 128, :], w=[Bogo[j]])
                S.op("pool", lambda e, j=j: e.tensor_tensor(out=hs[j][:], in0=hf[j][:], in1=hb[j][:], op=ALU.add),
                     r=[Bhf[j], Bhb[j]], w=[Bhs[j]])
                S.op("pool", lambda e, j=j: e.tensor_tensor(out=ogo[j][:], in0=ogo[j][:], in1=ngo[:], op=ALU.mult),
                     r=[Bogo[j], Bw], w=[Bogo[j]])

            def post_b(t):
                j = t % NB
                for h in range(2):
                    S.op("dve", lambda e, j=j, h=h: e.bn_stats(stt[j][:, h, 0:6], hs[j][:, h * 128:(h + 1) * 128]),
                         r=[Bhs[j]], w=[Bst[j]])
                    S.op("dve", lambda e, j=j, h=h: e.bn_aggr(stt[j][:, h, 6:8], stt[j][:, h, 0:6]), r=[Bst[j]], w=[Bst[j]])
                S.op("act", lambda e, j=j: e.activation(out=stt[j][:, :, 8:9], in_=stt[j][:, :, 7:8], func=AF.Sqrt, scale=1.0,
                                                        bias=eps_t[:, 0:1]), r=[Bst[j], Bw], w=[Bst[j]])
                S.op("dve", lambda e, j=j: e.reciprocal(stt[j][:, :, 9:10], stt[j][:, :, 8:9]), r=[Bst[j]], w=[Bst[j]])
                for h in range(2):
                    S.op("dve", lambda e, j=j, h=h: e.tensor_scalar(
                        out=yo[j][:, h * 128:(h + 1) * 128], in0=hs[j][:, h * 128:(h + 1) * 128], scalar1=stt[j][:, h, 6:7],
                        scalar2=stt[j][:, h, 9:10], op0=ALU.subtract, op1=ALU.mult), r=[Bhs[j], Bst[j]], w=[Byo[j]])
                S.op("pool", lambda e, j=j: e.tensor_tensor(out=yob[j][:], in0=yo[j][:], in1=ogo[j][:], op=ALU.mult),
                     r=[Byo[j], Bogo[j]], w=[Byo[j]])
                k, r0 = self.yodd_loc(t)
                S.dma("sp", self.yodd_t[k].ap()[r0:r0 + 128, 0:256], yob[j][:], r=[Byo[j]])

            for t in range(NT):
                if t == 0:
                    post_a(0)
                if t + 1 < NT:
                    post_a(t + 1)
                post_b(t)
            S.barrier()

    def na_attention(self, i, last):
        S = self.S
        with contextlib.ExitStack() as st:
            QNe = self.sb(st, "na_Qe", [128, NTOK], BF16)
            QNo = self.sb(st, "na_Qo", [128, NTOK], BF16)
            KN = self.sb(st, "na_K", [128, NTOK], BF16)
            VN = self.sb(st, "na_V", [128, NT, 65], BF16)
            MB = self.sb(st, "na_MB", [128, NA_NVAR, 128], BF16)
            Bqk = S.buf("na_qk")
            Bv = S.buf("na_v")
            Bmb = S.buf("na_mb")
            stg = [self.sb(st, f"na_stg{j}", [128, 7, 128]) for j in range(2)]
            Bstg = S.bufs("na_stg", 2)
            NP = 3
            ptA = [self.sb(st, f"na_ptA{j}", [128, 512], BF16) for j in range(NP)]
            ptB = [self.sb(st, f"na_ptB{j}", [128, 384], BF16) for j in range(NP)]
            BptA, BptB = S.bufs("na_ptA", NP), S.bufs("na_ptB", NP)
            rr = [self.sb(st, f"na_rr{j}", [128, 2]) for j in range(NP)]
            to = [self.sb(st, f"na_to{j}", [128, 64], BF16) for j in range(NP)]
            Bto = S.bufs("na_to", NP)
            psA = [self.ps(st, f"na_psA{j}", [128, 512]) for j in range(2)]
            psB = [self.ps(st, f"na_psB{j}", [128, 512]) for j in range(2)]
            acc = [self.ps(st, f"na_acc{j}", [128, 512]) for j in range(2)]
            BpsA, BpsB, Bacc = S.bufs("na_psA", 2), S.bufs("na_psB", 2), S.bufs("na_acc", 2)
            it = 0
            qtiles = list(range(NT_LAT)) + ([] if (last and self.final_out) else [NT_LAT, NT_LAT + 1])
            if self.tile_filter is not None:
                qtiles = [t for t in qtiles if t in self.tile_filter]
            for h in range(4):
                lo = (h % 2) * 64
                if h == 0:
                    S.op("pool", lambda e: e.memset(QNe[64:128, :], 0.0), w=[Bqk])
                    S.op("pool", lambda e: e.memset(QNo[0:64, :], 0.0), w=[Bqk])
                if h % 2 == 0:
                    hp = h // 2
                    S.dma("sp", QNe[0:64, :], self.qda[hp, 0:64, :], w=[Bqk])
                    S.dma("act", QNo[64:128, :], self.qda[hp, 64:128, :], w=[Bqk])
                    S.dma("sp", KN[:], self.kda[hp, :, :], w=[Bqk])
                QN = QNe if h % 2 == 0 else QNo
                S.dma("act", VN[:], self.vm[h, :, :, :], w=[Bv])
                for v0 in range(0, NA_NVAR, 7):
                    n = min(7, NA_NVAR - v0)
                    sj = (v0 // 7) % 2
                    S.dma("sp", stg[sj][:, 0:n, :], self.od_nat[i, h, :, v0:v0 + n, :], w=[Bstg[sj]])
                    S.op("dve", lambda e, sj=sj, n=n, v0=v0: e.tensor_scalar(
                        out=MB[:, v0:v0 + n, :], in0=stg[sj][:, 0:n, :], scalar1=1.0 / NA_SCALE, scalar2=None, op0=ALU.mult),
                        r=[Bstg[sj]], w=[Bmb])
                for j in qtiles:
                    pj = it % 2
                    tj = it % NP
                    it += 1
                    qsl = slice(j * 128, (j + 1) * 128)
                    if j < NT_LAT:
                        slots = [(kt, var) for (kt, var) in NA_PLAN[j]] + [(NT_LAT, None), (NT_LAT + 1, None)]
                    else:
                        slots = [(NT_LAT, None), (NT_LAT + 1, None)]
                    nA = min(4, len(slots))
                    nB = len(slots) - nA
                    for si, (kt, var) in enumerate(slots):
                        if si < 4:
                            dst, Bd = psA[pj][:, si * 128:(si + 1) * 128], BpsA[pj]
                        else:
                            dst, Bd = psB[pj][:, (si - 4) * 128:(si - 3) * 128], BpsB[pj]
                        S.op("pe", lambda e, dst=dst, kt=kt, var=var, QN=QN: e.matmul(
                            dst, KN[:, kt * 128:(kt + 1) * 128], QN[:, qsl], start=True, stop=(var is None)),
                            r=[Bqk], w=[Bd])
                        if var is not None:
                            S.op("pe", lambda e, dst=dst, var=var: e.matmul(dst, self.ident_b[:], MB[:, var, :], start=False, stop=True),
                                 r=[Bmb, self.B_const], w=[Bd])
                    S.op("act", lambda e, pj=pj, tj=tj, nA=nA: e.activation(out=ptA[tj][:, 0:nA * 128], in_=psA[pj][:, 0:nA * 128],
                                                                            func=AF.Exp, scale=NA_SCALE), r=[BpsA[pj]], w=[BptA[tj]])
                    if nB > 0:
                        S.op("act", lambda e, pj=pj, tj=tj, nB=nB: e.activation(out=ptB[tj][:, 0:nB * 128], in_=psB[pj][:, 0:nB * 128],
                                                                                func=AF.Exp, scale=NA_SCALE), r=[BpsB[pj]], w=[BptB[tj]])
                    for si, (kt, var) in enumerate(slots):
                        if si < 4:
                            lhs, Bl = ptA[tj][:, si * 128:(si + 1) * 128], BptA[tj]
                        else:
                            lhs, Bl = ptB[tj][:, (si - 4) * 128:(si - 3) * 128], BptB[tj]
                        S.op("pe", lambda e, lhs=lhs, kt=kt, si=si, pj=pj: e.matmul(
                            acc[pj][:, 0:65], lhs, VN[:, kt, :], start=(si == 0), stop=(si == len(slots) - 1)),
                            r=[Bl, Bv], w=[Bacc[pj]])
                    S.op("dve", lambda e, pj=pj, tj=tj: e.reciprocal(rr[tj][:, 0:1], acc[pj][:, 64:65]), r=[Bacc[pj]], w=[Bto[tj]])
                    S.op("dve", lambda e, pj=pj, tj=tj: e.tensor_scalar(out=to[tj][:], in0=acc[pj][:, 0:64], scalar1=rr[tj][:, 0:1],
                                                                        scalar2=None, op0=ALU.mult), r=[Bacc[pj], Bto[tj]], w=[Bto[tj]])
                    kk, r0 = self.yodd_loc(j)
                    S.dma("pool", self.yodd_t[kk].ap()[r0:r0 + 128, 256 + h * 64:256 + (h + 1) * 64], to[tj][:], r=[Bto[tj]])
            S.barrier()


def _swap64(cols):
    return np.concatenate([cols[32:64], cols[0:32]])


def _rope_tables():
    t = np.arange(LAT)
    row = (t // GRID_W).astype(np.float32)
    col = (t % GRID_W).astype(np.float32)

    def tab(dim, nrows_pattern):
        n_freq = dim // 4
        freqs = (10000.0 ** (-np.arange(n_freq, dtype=np.float32) / n_freq)).astype(np.float32)
        ang = np.concatenate([row[:, None] * freqs, col[:, None] * freqs], axis=-1).astype(np.float32)
        cos = np.cos(ang).astype(np.float32).T
        sin = np.sin(ang).astype(np.float32).T
        half = dim // 2
        c = np.concatenate([cos, cos], 0)
        s = np.concatenate([-sin, sin], 0)
        c = np.concatenate([c, np.ones((dim, CTX), np.float32)], 1)
        s = np.concatenate([s, np.zeros((dim, CTX), np.float32)], 1)
        return c, s

    c64, s64 = tab(64, None)
    c32, s32 = tab(32, None)
    rope_da = np.stack([np.concatenate([c64, c64], 0), np.concatenate([s64, s64], 0)]).astype(np.float32)
    ml_c = np.ones((128, NTOK), np.float32)
    ml_s = np.zeros((128, NTOK), np.float32)
    ml_c[0:32] = c32
    ml_s[0:32] = s32
    ml_c[64:96] = c32
    ml_s[64:96] = s32
    rope_ml = np.stack([ml_c, ml_s]).astype(np.float32)
    return rope_da, rope_ml


def _even_layout(inp, par):
    ev_w_in, ev_b_in = inp["ev_w_in"], inp["ev_b_in"]
    hs = [2 * par, 2 * par + 1]
    ms = [4 * par + k for k in range(4)]
    o_q1, o_q2, o_k1, o_k2, o_v, o_cq, o_ckv, o_kr, o_g = 0, 256, 512, 768, 1024, 1536, 1792, 1920, 1952
    cols = []
    for (a, b) in ((o_q1, o_q2), (o_k1, o_k2)):
        main = []
        sw = []
        for h in hs:
            c1 = np.arange(a + h * 64, a + (h + 1) * 64)
            c2 = np.arange(b + h * 64, b + (h + 1) * 64)
            main += [c1, c2]
            sw += [_swap64(c1), _swap64(c2)]
        cols += main + sw
    kr = np.arange(o_kr, o_kr + 32)
    cols += [kr, np.concatenate([kr[16:], kr[:16]])]
    cols += [np.arange(o_v + h * 128, o_v + (h + 1) * 128) for h in hs]
    cols += [np.arange(o_cq, o_cq + 384), np.arange(o_g, o_g + 1024)]
    perm = np.concatenate(cols)
    assert perm.shape[0] == EV_COLS
    ev_w = np.ascontiguousarray(ev_w_in[:, :, perm])
    bp = ev_b_in[:, perm]
    bcol = np.zeros((2, 128, 10), np.float32)
    for g in range(8):
        bcol[:, :, g] = bp[:, g * 128:(g + 1) * 128]
    bcol[:, 0:32, 8] = bp[:, EKR:EKR + 32]
    bcol[:, 0:32, 9] = bp[:, EKR + 32:EKR + 64]
    brow = np.ascontiguousarray(bp[:, EV_:EV_ + 1664])[:, None, :]
    wuq = inp["mla_w_uq"]
    mainc = []
    swc = []
    for h in ms:
        c = np.arange(h * 96, h * 96 + 96)
        r = c[64:96]
        mainc.append(c)
        swc.append(np.concatenate([c[0:64], r[16:], r[:16]]))
    ev_wuq = np.ascontiguousarray(wuq[:, :, np.concatenate(mainc + swc)])
    wukv = inp["mla_w_ukv"]
    nope = np.concatenate([np.arange(h * 128, h * 128 + 64) for h in ms])
    vv = np.concatenate([np.arange(h * 128 + 64, h * 128 + 128) for h in ms])
    ev_wukv = np.ascontiguousarray(wukv[:, :, np.concatenate([nope, vv])])
    return dict(
        ev_w=ev_w, ev_bcol=bcol, ev_brow=np.ascontiguousarray(brow),
        ev_lam=np.ascontiguousarray(inp["da_lambda"].reshape(2, 1, 256)),
        ev_subg=np.ascontiguousarray(inp["da_subln_g"].reshape(2, 1, 128)),
        ev_qg=np.ascontiguousarray(inp["mla_q_norm_g"].reshape(2, 2, 128).transpose(0, 2, 1)),
        ev_kvg=np.ascontiguousarray(inp["mla_kv_norm_g"].reshape(2, 128, 1)),
        ev_wuq=ev_wuq, ev_wukv=ev_wukv, ev_wo=np.ascontiguousarray(inp["ev_w_out"]),
    )


def _odd_layout(inp, par):
    w, b = inp["od_w_in"], inp["od_b_in"]
    hs = [2 * par, 2 * par + 1]
    ns = [4 * par + k for k in range(4)]
    g0 = 2048
    cols = [np.arange(h * 128, (h + 1) * 128) for h in hs]
    cols += [np.arange(512 + h * 128, 512 + (h + 1) * 128) for h in hs]
    cols += [np.arange(2064 + n * 64, 2064 + (n + 1) * 64) for n in ns]
    cols += [np.arange(2576 + n * 64, 2576 + (n + 1) * 64) for n in ns]
    cols += [np.arange(1024 + h * 128, 1024 + (h + 1) * 128) for h in hs]
    cols += [np.arange(1536 + h * 128, 1536 + (h + 1) * 128) for h in hs]
    cols += [np.array([g0 + j * 4 + h for h in hs]) for j in (0, 2, 1, 3)]
    cols += [np.arange(3088 + n * 64, 3088 + (n + 1) * 64) for n in ns]
    cols += [np.arange(3600, 4624)]
    perm = np.concatenate(cols)
    assert perm.shape[0] == OD_COLS
    od_w = np.ascontiguousarray(w[:, :, perm])
    bp = b[:, perm]
    bcol = np.ascontiguousarray(bp[:, 0:1024].reshape(2, 8, 128).transpose(0, 2, 1))
    brow = np.ascontiguousarray(bp[:, 1024:])[:, None, :]
    ch = np.concatenate([np.arange(h * 128, (h + 1) * 128) for h in hs] + [np.arange(512 + h * 128, 512 + (h + 1) * 128) for h in hs])
    convw = np.ascontiguousarray(inp["ml_conv_w"][:, :, ch].reshape(2, 5, 4, 128).transpose(0, 3, 2, 1))
    convb = np.ascontiguousarray(inp["ml_conv_b"][:, ch].reshape(2, 4, 128).transpose(0, 2, 1))
    fb = np.ascontiguousarray(inp["ml_f_bias"][:, :, hs].reshape(2, 1, 4))
    ng = np.ascontiguousarray(np.concatenate([inp["ml_norm_g"][:, h * 128:(h + 1) * 128] for h in hs], 1).reshape(2, 1, 256))
    rpb = inp["na_rpb"][:, ns]
    nat = np.full((2, 4, 128, NA_NVAR, 128), NEG, np.float32)
    k = np.arange(128)
    krl, kc = k // 64, k % 64
    q = np.arange(128)
    qrl, qc = q // 64, q % 64
    cs = np.clip(qc - NA_COLS // 2, 0, GRID_W - NA_COLS)
    for (dk, o0, o1), vid in NA_VARIANTS.items():
        rel = (2 * dk + krl[:, None]) - qrl[None, :]
        off = np.where(qrl[None, :] == 0, o0, o1)
        valid_r = (rel >= off) & (rel <= off + NA_ROWS - 1)
        valid_c = (kc[:, None] >= cs[None, :]) & (kc[:, None] <= cs[None, :] + NA_COLS - 1)
        valid = valid_r & valid_c
        ridx = np.clip(rel + NA_ROWS - 1, 0, 2 * NA_ROWS - 2)
        cidx = np.clip(kc[:, None] - qc[None, :] + NA_COLS - 1, 0, 2 * NA_COLS - 2)
        tab = rpb[:, :, ridx, cidx]
        nat[:, :, :, vid, :] = np.where(valid[None, None], tab, np.float32(NEG))
    return dict(od_w=od_w, od_bcol=bcol, od_brow=np.ascontiguousarray(brow), od_convw=convw, od_convb=convb, od_fb=fb,
                od_ng=ng, od_nat=nat, od_wo=np.ascontiguousarray(inp["od_w_out"]))


def make_in_maps(inp, batches):
    inp = {k: np.asarray(v, dtype=np.float32) for k, v in inp.items()}
    rope_da, rope_ml = _rope_tables()
    common = dict(
        ada_w=inp["ada_w"], ada_b=inp["ada_b"], ln_g=inp["ln_g"], ln_b=inp["ln_b"],
        ident=np.eye(128, dtype=np.float32),
        sel=np.concatenate([np.stack([np.ones(128), np.zeros(128)]), np.stack([np.zeros(128), np.ones(128)])], 1).astype(np.float32),
        rope_da=rope_da, rope_ml=rope_ml,
    )
    tri = np.stack([np.triu(np.ones((128, 128), np.float32)), np.tril(np.ones((128, 128), np.float32)),
                    np.ones((128, 128), np.float32)], 1)
    common["tri"] = np.ascontiguousarray(tri)
    ev = [dict(_even_layout(inp, p), **_odd_layout(inp, p)) for p in (0, 1)]
    maps = []
    for b in batches:
        m = dict(common)
        m.update(ev[len(maps) % 2])
        m["x_in"] = np.ascontiguousarray(np.concatenate([inp["x"][b], inp["ctx"][b]], 0))
        cc = np.stack([inp["c"][b], inp["c_ctx"]], -1)
        m["cc"] = np.ascontiguousarray(cc.reshape(8, 128, 2).transpose(1, 0, 2))
        par = len(maps) % 2
        ps = np.zeros((128, 2), np.float32)
        ps[:, par] = 1.0
        m["psel"] = ps
        maps.append(m)
    return maps


_PROG_CACHE = {}


def kernel(**inputs):
    batches = [c // 2 for c in range(8)]
    in_maps = make_in_maps(inputs, batches)
    prog = Prog()
    nc = prog.build()
    res = run_bass_kernel_spmd(nc, in_maps, core_ids=list(range(8)))
    out = np.stack([res.results[2 * b]["y"] for b in range(4)], 0)
    return out.astype(np.float32)
```

```python
import contextlib
import math
import numpy as np
import concourse.bass as bass
import concourse.mybir as mybir
from concourse.bass_utils import run_bass_kernel_spmd

F32 = mybir.dt.float32
BF16 = mybir.dt.bfloat16
AF = mybir.ActivationFunctionType
ALU = mybir.AluOpType
AX = mybir.AxisListType

D = 1024
LAT = 8192
CTX = 256
NTOK = LAT + CTX
NT = NTOK // 128
NT_LAT = LAT // 128
DEPTH = 4
GRID_W = 64
ALPHA = (2.0 * DEPTH) ** 0.25
LN_EPS = 1e-5
RMS_EPS = 1e-6
DA_SCALE = 64 ** -0.5
MLA_SCALE = 96 ** -0.5
NA_SCALE = 64 ** -0.5
EV_COLS = 2752
EQ, EK, EKR, EV_, EC, EG = 0, 512, 1024, 1088, 1344, 1728
OD_COLS = 2824
OQK, OQN, OKN, OV, OO, OGT, OVN, OG = 0, 512, 768, 1024, 1280, 1536, 1544, 1800
NA_ROWS, NA_COLS, GRID_H = 8, 16, 128
NEG = -30000.0


def na_plan():
    variants = {}
    plan = []
    for j in range(NT_LAT):
        r0, r1 = 2 * j, 2 * j + 1
        rs0 = min(max(r0 - NA_ROWS // 2, 0), GRID_H - NA_ROWS)
        rs1 = min(max(r1 - NA_ROWS // 2, 0), GRID_H - NA_ROWS)
        lst = []
        for kt in range(rs0 // 2, (rs1 + NA_ROWS - 1) // 2 + 1):
            key = (kt - j, rs0 - r0, rs1 - r1)
            if key not in variants:
                variants[key] = len(variants)
            lst.append((kt, variants[key]))
        plan.append(lst)
    return plan, variants


NA_PLAN, NA_VARIANTS = na_plan()
NA_NVAR = len(NA_VARIANTS)


class Tok:
    __slots__ = ("sem", "key", "val", "eng")

    def __init__(self, sem, key, val, eng):
        self.sem, self.key, self.val, self.eng = sem, key, val, eng


class Buf:
    __slots__ = ("name", "last_w", "readers", "sem", "key", "cnt")

    def __init__(self, name):
        self.name = name
        self.last_w = None
        self.readers = {}
        self.sem = None
        self.key = None
        self.cnt = 0


class Eng:
    def __init__(self, name, h, sem):
        self.name, self.h, self.sem = name, h, sem
        self.key = "E_" + name
        self.cnt = 0
        self.waited = {}


class Sync:
    def __init__(self, nc, stack):
        self.nc = nc
        self.stack = stack
        self.E = {}
        for name, h in (("pe", nc.tensor), ("act", nc.scalar), ("dve", nc.vector),
                        ("pool", nc.gpsimd), ("sp", nc.sync)):
            sem = stack.enter_context(nc.semaphore("s_" + name))
            self.E[name] = Eng(name, h, sem)
        self.dma_bufs = []
        self.free_sems = []
        self.replica_groups = [[0, 1], [2, 3], [4, 5], [6, 7]]
        self.nsem = 0
        self.nwait = 0
        self.nins = 0

    def buf(self, name):
        return Buf(name)

    def bufs(self, name, n):
        return [Buf(f"{name}{i}") for i in range(n)]

    def _deps(self, eng, r, w):
        raw = []
        oth = []
        for b in r:
            if b.last_w is not None:
                raw.append(b.last_w)
        for b in w:
            if b.last_w is not None:
                oth.append(b.last_w)
            oth.extend(b.readers.values())
        toks = []
        for t in raw:
            if t.eng is eng and eng.name == "pe":
                continue
            toks.append(t)
        for t in oth:
            if t.eng is eng:
                continue
            toks.append(t)
        return toks

    def _wait(self, eng, toks):
        for t in toks:
            if eng.waited.get(t.key, 0) >= t.val:
                continue
            eng.h.wait_ge(t.sem, t.val)
            eng.waited[t.key] = t.val
            self.nwait += 1

    def op(self, en, fn, r=(), w=()):
        eng = self.E[en]
        self._wait(eng, self._deps(eng, r, w))
        ins = fn(eng.h)
        ins.then_inc(eng.sem, 1)
        eng.cnt += 1
        self.nins += 1
        tok = Tok(eng.sem, eng.key, eng.cnt, eng)
        for b in r:
            b.readers[tok.key] = tok
        for b in w:
            b.last_w = tok
            b.readers = {}
        return tok

    def dma(self, q, out, in_, r=(), w=(), sb=None):
        eng = self.E[q]
        self._wait(eng, self._deps(eng, r, w))
        if sb is None:
            sb = w[0] if w else r[0]
        if sb.sem is None:
            if self.free_sems:
                sb.sem, sb.key, sb.cnt = self.free_sems.pop()
            else:
                sb.sem = self.stack.enter_context(self.nc.semaphore(f"d{self.nsem}"))
                sb.key = f"D{self.nsem}"
                sb.cnt = 0
                self.nsem += 1
            self.dma_bufs.append(sb)
        ins = eng.h.dma_start(out=out, in_=in_)
        ins.then_inc(sb.sem, 16)
        sb.cnt += 16
        self.nins += 1
        tok = Tok(sb.sem, sb.key, sb.cnt, None)
        for b in r:
            b.readers[tok.key] = tok
        for b in w:
            b.last_w = tok
            b.readers = {}
        return tok

    def allgather_pairs(self, src_ts, dst_ts):
        self.barrier()
        eng = self.E["pool"]
        if getattr(self, "cc_sem", None) is None:
            self.cc_sem = self.stack.enter_context(self.nc.semaphore("cc_sem"))
            self.cc_cnt = 0
        for src_t, dst_t in zip(src_ts, dst_ts):
            ins = eng.h.collective_compute("AllGather", ALU.bypass, replica_groups=self.replica_groups,
                                           ins=[src_t.ap().opt()], outs=[dst_t.ap().opt()])
            ins.then_inc(self.cc_sem)
            self.cc_cnt += 1
            self.nins += 1
        tok = Tok(self.cc_sem, "CC", self.cc_cnt, None)
        for e in self.E.values():
            self._wait(e, [tok])

    def barrier(self, engines=("pe", "act", "dve", "pool", "sp")):
        toks = [Tok(e.sem, e.key, e.cnt, e) for e in self.E.values() if e.cnt > 0]
        toks += [Tok(b.sem, b.key, b.cnt, None) for b in self.dma_bufs if b.cnt > 0]
        for en in engines:
            eng = self.E[en]
            self._wait(eng, [t for t in toks if t.eng is not eng])
        for b in self.dma_bufs:
            self.free_sems.append((b.sem, b.key, b.cnt))
            b.sem = None
            b.last_w = None
            b.readers = {}
        self.dma_bufs = []


class Prog:
    def __init__(self, layers=(0, 1, 2, 3), final_out=True, qb_filter=None, tile_filter=None, n_cores=8):
        self.n_cores = n_cores
        self.layers = tuple(layers)
        self.qb_filter = qb_filter
        self.tile_filter = tile_filter
        self.nc = bass.Bass("TRN2", target_bir_lowering=False)
        self.final_out = final_out

    def din(self, name, shape, dt=F32):
        return self.nc.dram_tensor(name, list(shape), dt, kind="ExternalInput").ap()

    def dout(self, name, shape, dt=F32):
        return self.nc.dram_tensor(name, list(shape), dt, kind="ExternalOutput").ap()

    def dscr(self, name, shape, dt=F32):
        return self.nc.dram_tensor(name, list(shape), dt, kind="Internal").ap()

    def sb(self, st, name, shape, dt=F32):
        self._uid = getattr(self, "_uid", 0) + 1
        return st.enter_context(self.nc.sbuf_tensor(f"s{self._uid}_{name}", list(shape), dt))

    def ps(self, st, name, shape, dt=F32):
        self._uid = getattr(self, "_uid", 0) + 1
        return st.enter_context(self.nc.psum_tensor(f"p{self._uid}_{name}", list(shape), dt))

    def build(self):
        nc = self.nc
        with contextlib.ExitStack() as st:
            self.S = Sync(nc, st)
            self.S.replica_groups = [[2 * k, 2 * k + 1] for k in range(self.n_cores // 2)]
            self.declare()
            self.setup_consts(st)
            nl = len(self.layers)
            for li, l in enumerate(self.layers):
                src = self.x_in if li == 0 else self.xbuf[(li - 1) % 2]
                last = li == nl - 1
                dst = self.y_out if last else self.xbuf[li % 2]
                if l % 2 == 0:
                    self.even_layer(l, src, dst, last)
                else:
                    self.odd_layer(l, src, dst, last)
            self.S.barrier()
        return nc

    def declare(self):
        self.x_in = self.din("x_in", [NTOK, D])
        self.cc = self.din("cc", [128, 8, 2])
        self.ada_w = self.din("ada_w", [DEPTH, D, 3 * D])
        self.ada_b = self.din("ada_b", [DEPTH, 3 * D])
        self.ln_g = self.din("ln_g", [DEPTH, D])
        self.ln_b = self.din("ln_b", [DEPTH, D])
        self.ident_in = self.din("ident", [128, 128])
        self.sel_in = self.din("sel", [2, 256])
        self.ev_w = self.din("ev_w", [2, D, EV_COLS])
        self.ev_bcol = self.din("ev_bcol", [2, 128, 10])
        self.ev_brow = self.din("ev_brow", [2, 1, 1664])
        self.ev_lam = self.din("ev_lam", [2, 1, 256])
        self.ev_subg = self.din("ev_subg", [2, 1, 128])
        self.ev_qg = self.din("ev_qg", [2, 128, 2])
        self.ev_kvg = self.din("ev_kvg", [2, 128, 1])
        self.ev_wuq = self.din("ev_wuq", [2, 256, 768])
        self.ev_wukv = self.din("ev_wukv", [2, 128, 512])
        self.ev_wo = self.din("ev_wo", [2, D, D])
        self.rope_da = self.din("rope_da", [2, 128, NTOK])
        self.rope_ml = self.din("rope_ml", [2, 128, NTOK])
        self.od_w = self.din("od_w", [2, D, OD_COLS])
        self.od_bcol = self.din("od_bcol", [2, 128, 8])
        self.od_brow = self.din("od_brow", [2, 1, 1800])
        self.od_convw = self.din("od_convw", [2, 128, 4, 5])
        self.od_convb = self.din("od_convb", [2, 128, 4])
        self.od_fb = self.din("od_fb", [2, 1, 4])
        self.od_ng = self.din("od_ng", [2, 1, 256])
        self.od_nat = self.din("od_nat", [2, 4, 128, NA_NVAR, 128])
        self.od_wo = self.din("od_wo", [2, D, D])
        self.tri = self.din("tri", [128, 3, 128])
        self.qkpre = self.dscr("qkpre", [4, 128, NTOK])
        self.mq = self.dscr("mq", [2, 128, NTOK], BF16)
        self.mk = self.dscr("mk", [2, 128, NTOK], BF16)
        self.og = self.dscr("og", [NTOK, 256])
        self.gates = self.dscr("gates", [128, NT, 8])
        self.hfb = self.dscr("hfb", [2, NTOK, 256])
        n_last_tok = LAT if self.final_out else NTOK
        self.y_out = self.dout("y", [n_last_tok, D])
        self.xbuf = [self.dscr("xbuf0", [NTOK, D]), self.dscr("xbuf1", [NTOK, D])]
        self.qda = self.dscr("qda", [2, 128, NTOK], BF16)
        self.kda = self.dscr("kda", [2, 128, NTOK], BF16)
        self.vda = self.dscr("vda", [2, 128, NT, 129], BF16)
        self.qm = self.dscr("qm", [4, 96, NTOK], BF16)
        self.kmn = self.dscr("kmn", [4, 64, NTOK], BF16)
        self.krt = self.dscr("krt", [32, NTOK], BF16)
        self.vm = self.dscr("vm", [4, 128, NT, 65], BF16)
        self.gate = self.dscr("gate", [NTOK, D], BF16)
        self.ybuf = self.dscr("ybuf", [NTOK, D], BF16)
        self.psel = self.din("psel", [128, 2])
        self.yodd_rows = [2048, 2048, 2048, 2048, 256]
        self.yodd_t = [self.nc.dram_tensor(f"yodd{k}", [r, 512], BF16) for k, r in enumerate(self.yodd_rows)]
        self.yoddg_t = [self.nc.dram_tensor(f"yoddg{k}", [2 * r, 512], BF16) for k, r in enumerate(self.yodd_rows)]
        self.yown_t = [self.nc.dram_tensor(f"yown{k}", [1024, D], BF16) for k in range(4)]
        self.ygath_t = [self.nc.dram_tensor(f"ygath{k}", [2048, D], BF16) for k in range(4)]

    def setup_consts(self, st):
        S = self.S
        self.ident_f = self.sb(st, "ident_f", [128, 128])
        self.ident_b = self.sb(st, "ident_b", [128, 128], BF16)
        self.sel = self.sb(st, "sel", [2, 256])
        self.ccs = self.sb(st, "ccs", [128, 8, 2])
        self.zeros_b = self.sb(st, "zeros_b", [128, 128], BF16)
        self.ones_b = self.sb(st, "ones_b", [1, 128], BF16)
        self.zeros_w = self.sb(st, "zeros_w", [128, 512], BF16)
        self.B_const = S.buf("consts")
        b = self.B_const
        S.dma("sp", self.ident_f[:], self.ident_in[:, :], w=[b])
        S.dma("sp", self.sel[:], self.sel_in[:, :], w=[b])
        S.dma("sp", self.ccs[:], self.cc[:, :, :], w=[b])
        self.pselt = self.sb(st, "pselt", [128, 4])
        S.dma("sp", self.pselt[:, 0:2], self.psel[:, :], w=[b])
        S.op("dve", lambda e: e.tensor_scalar(out=self.pselt[:, 2:4], in0=self.pselt[:, 0:2], scalar1=1.0 / NA_SCALE, scalar2=None,
                                              op0=ALU.mult), r=[b], w=[b])
        S.op("dve", lambda e: e.tensor_copy(self.ident_b[:], self.ident_f[:]), r=[b], w=[b])
        S.op("dve", lambda e: e.memset(self.zeros_b[:], 0.0), w=[b])
        S.op("dve", lambda e: e.memset(self.ones_b[:], 1.0), w=[b])
        S.op("dve", lambda e: e.memset(self.zeros_w[:], 0.0), w=[b])
        S.op("act", lambda e: e.activation(out=self.ccs[:], in_=self.ccs[:], func=AF.Silu), r=[b], w=[b])
        self.sc1 = self.sb(st, "sc1", [128, 8, 2])
        self.sh = self.sb(st, "sh", [128, 8, 2])
        self.g_l = self.sb(st, "g_l", [128, D])
        self.g_c = self.sb(st, "g_c", [128, D])
        self.lng = self.sb(st, "lng", [128, D])
        self.lnb = self.sb(st, "lnb", [128, D])
        self.B_mod = S.buf("mod")

    def adaln(self, l):
        S, nc = self.S, self.nc
        S.barrier()
        with contextlib.ExitStack() as st:
            wt = [self.sb(st, f"adaw{i}", [128, 3 * D]) for i in range(2)]
            Bw = S.bufs("adaw", 2)
            mrow = self.sb(st, "mrow", [2, 3 * D])
            brow = self.sb(st, "adab", [2, 3 * D])
            Bm = S.buf("mrow")
            Bb = S.buf("adab")
            pm = [self.ps(st, f"pm{i}", [128, 512]) for i in range(8)]
            Bp = S.bufs("pm", 8)
            S.dma("sp", brow[:], self.ada_b[l:l + 1, :].to_broadcast([2, 3 * D]), w=[Bb])
            S.dma("sp", self.lng[:], self.ln_g[l:l + 1, :].to_broadcast([128, D]), w=[self.B_mod])
            S.dma("sp", self.lnb[:], self.ln_b[l:l + 1, :].to_broadcast([128, D]), w=[self.B_mod])
            for k in range(8):
                S.dma("sp", wt[k % 2][:], self.ada_w[l, k * 128:(k + 1) * 128, :], w=[Bw[k % 2]])
                for n in range(6):
                    S.op("pe", lambda e, k=k, n=n: e.matmul(
                        pm[n][0:2, :], self.ccs[:, k, :], wt[k % 2][:, n * 512:(n + 1) * 512],
                        start=(k == 0), stop=(k == 7)), r=[Bw[k % 2], self.B_const], w=[Bp[n]])
            for n in range(6):
                S.op("dve", lambda e, n=n: e.tensor_tensor(
                    out=mrow[:, n * 512:(n + 1) * 512], in0=pm[n][0:2, :], in1=brow[:, n * 512:(n + 1) * 512],
                    op=ALU.add), r=[Bp[n], Bb], w=[Bm])
            for j in range(8):
                S.op("pe", lambda e, j=j: e.transpose(
                    pm[6][:, 2 * j:2 * j + 2], mrow[0:2, j * 128:(j + 1) * 128], self.ident_f[0:2, 0:2]),
                    r=[Bm, self.B_const], w=[Bp[6]])
                S.op("pe", lambda e, j=j: e.transpose(
                    pm[7][:, 2 * j:2 * j + 2], mrow[0:2, D + j * 128:D + (j + 1) * 128], self.ident_f[0:2, 0:2]),
                    r=[Bm, self.B_const], w=[Bp[7]])
            S.op("dve", lambda e: e.tensor_copy(self.sh[:].rearrange("p a b -> p (a b)"), pm[6][:, 0:16]),
                 r=[Bp[6]], w=[self.B_mod])
            S.op("dve", lambda e: e.tensor_scalar(
                out=self.sc1[:].rearrange("p a b -> p (a b)"), in0=pm[7][:, 0:16], scalar1=1.0, scalar2=None,
                op0=ALU.add), r=[Bp[7]], w=[self.B_mod])
            for n in range(2):
                S.op("pe", lambda e, n=n: e.matmul(
                    pm[n][:, :], self.sel[:, 0:128], mrow[0:2, 2 * D + n * 512:2 * D + (n + 1) * 512],
                    start=True, stop=True), r=[Bm, self.B_const], w=[Bp[n]])
                S.op("pe", lambda e, n=n: e.matmul(
                    pm[2 + n][:, :], self.sel[:, 128:256], mrow[0:2, 2 * D + n * 512:2 * D + (n + 1) * 512],
                    start=True, stop=True), r=[Bm, self.B_const], w=[Bp[2 + n]])
                S.op("dve", lambda e, n=n: e.tensor_copy(self.g_l[:, n * 512:(n + 1) * 512], pm[n][:, :]),
                     r=[Bp[n]], w=[self.B_mod])
                S.op("dve", lambda e, n=n: e.tensor_copy(self.g_c[:, n * 512:(n + 1) * 512], pm[2 + n][:, :]),
                     r=[Bp[2 + n]], w=[self.B_mod])
            S.barrier()

    def load_cast(self, st_w, dst_fn, src_fn, nparts, ncols, pieces, Bdst, scale_fn=None, name="lc"):
        S = self.S
        with contextlib.ExitStack() as st:
            stg = [self.sb(st, f"{name}_stg{i}", [nparts, ncols]) for i in range(2)]
            Bs = S.bufs(name + "_stg", 2)
            for i, (dst, src, sc) in enumerate(pieces):
                j = i % 2
                n = src.shape[-1]
                S.dma("sp", stg[j][:, 0:n], src, w=[Bs[j]])
                en = "dve" if i % 2 == 0 else "pool"
                if sc is None:
                    S.op(en, lambda e, dst=dst, j=j, n=n: e.tensor_copy(dst, stg[j][:, 0:n]), r=[Bs[j]], w=[Bdst])
                else:
                    S.op(en, lambda e, dst=dst, j=j, n=n, sc=sc: e.tensor_scalar(
                        out=dst, in0=stg[j][:, 0:n], scalar1=sc, scalar2=None, op0=ALU.mult),
                        r=[Bs[j], Bdst], w=[Bdst])
            S.barrier()

    def even_layer(self, l, src, dst, last):
        i = l // 2
        self.adaln(l)
        self.even_project(i, src)
        self.da_attention(i, l)
        self.mla_attention(i)
        self.S.allgather_pairs(self.yodd_t, self.yoddg_t)
        self.out_stage(self.ev_wo[i], src, dst, last, ysplit="odd")

    def even_project(self, i, src):
        S, nc = self.S, self.nc
        with contextlib.ExitStack() as st:
            wb = self.sb(st, "ev_wb", [128, 8, EV_COLS], BF16)
            wuq = self.sb(st, "ev_wuqb", [128, 2, 768], BF16)
            wukv = self.sb(st, "ev_wukvb", [128, 512], BF16)
            bcol = self.sb(st, "ev_bcol", [128, 10])
            brow_f = self.sb(st, "ev_brow_f", [1, 1664])
            brow = self.sb(st, "ev_brow", [1, 1664], BF16)
            qg = self.sb(st, "ev_qg", [128, 2])
            kvg = self.sb(st, "ev_kvg", [128, 1])
            Bw = S.buf("ev_w")
            S.dma("sp", bcol[:], self.ev_bcol[i, :, :], w=[Bw])
            S.dma("sp", brow_f[:], self.ev_brow[i, :, :], w=[Bw])
            S.dma("sp", qg[:], self.ev_qg[i, :, :], w=[Bw])
            S.dma("sp", kvg[:], self.ev_kvg[i, :, :], w=[Bw])
            S.op("dve", lambda e: e.tensor_copy(brow[:], brow_f[:]), r=[Bw], w=[Bw])
            pieces = []
            for k in range(8):
                for c in range(2):
                    pieces.append((wb[:, k, c * 1376:(c + 1) * 1376],
                                   self.ev_w[i, k * 128:(k + 1) * 128, c * 1376:(c + 1) * 1376], None))
            self.load_cast(st, None, None, 128, 1536, pieces, Bw, name="evw")
            pieces = [(wuq[:, rc, :], self.ev_wuq[i, rc * 128:(rc + 1) * 128, :], qg[:, rc:rc + 1]) for rc in range(2)]
            pieces.append((wukv[:, :], self.ev_wukv[i, :, :], kvg[:, 0:1]))
            self.load_cast(st, None, None, 128, 1536, pieces, Bw, name="evw2")

            NXS = 6
            xt = [self.sb(st, f"xt{j}", [128, D]) for j in range(NXS)]
            Bx = S.bufs("xt", NXS)
            hT = [self.sb(st, f"hT{j}", [128, 8, 512], BF16) for j in range(2)]
            Bh = S.bufs("hT", 2)
            cosd = [self.sb(st, f"cosd{j}", [128, 512]) for j in range(2)]
            sind = [self.sb(st, f"sind{j}", [128, 512]) for j in range(2)]
            cosm = [self.sb(st, f"cosm{j}", [128, 512]) for j in range(2)]
            sinm = [self.sb(st, f"sinm{j}", [128, 512]) for j in range(2)]
            Brt = S.bufs("ropet", 2)
            t1 = [self.sb(st, f"t1_{j}", [128, 512]) for j in range(2)]
            t2 = [self.sb(st, f"t2_{j}", [128, 512]) for j in range(2)]
            Bt1 = S.bufs("t1", 2)
            Bt2 = S.bufs("t2", 2)
            fo = [self.sb(st, f"fo{j}", [128, 512], BF16) for j in range(4)]
            Bfo = S.bufs("fo", 4)
            va = [self.sb(st, f"va{j}", [128, 2, 129], BF16) for j in range(2)]
            Bva = S.bufs("va", 2)
            vma = [self.sb(st, f"vma{j}", [128, 4, 65], BF16) for j in range(2)]
            Bvma = S.bufs("vma", 2)
            cq = [self.sb(st, f"cq{j}", [128, 384]) for j in range(2)]
            Bcq = S.bufs("cq", 2)
            cqn = [self.sb(st, f"cqn{j}", [128, 384], BF16) for j in range(2)]
            Bcqn = S.bufs("cqn", 2)
            stat = [self.sb(st, f"stat{j}", [128, 8]) for j in range(2)]
            Bstat = S.bufs("stat", 2)
            junk = self.sb(st, "junk", [128, 256])
            Bjunk = S.buf("junk")
            cT = [self.sb(st, f"cT{j}", [128, 3, 512], BF16) for j in range(2)]
            BcT = S.bufs("cT", 2)
            gt = [self.sb(st, f"gt{j}", [128, D], BF16) for j in range(2)]
            Bgt = S.bufs("gt", 2)
            pp = [self.ps(st, f"pp{j}", [128, 512]) for j in range(7)]
            ppb = self.ps(st, "ppb", [128, 1024], BF16)
            Bpp = S.bufs("pp", 7)
            Bppb = S.buf("ppb")
            for j in range(2):
                S.op("pool", lambda e, j=j: e.memset(va[j][:], 1.0), w=[Bva[j]])
                S.op("pool", lambda e, j=j: e.memset(vma[j][:], 1.0), w=[Bvma[j]])
            eps_t = self.sb(st, "eps_t", [128, 1])
            S.op("pool", lambda e: e.memset(eps_t[:], RMS_EPS), w=[Bw])

            self._pp_i = 0

            def next_pp():
                j = self._pp_i % 7
                self._pp_i += 1
                return pp[j], Bpp[j]

            nblk = (NTOK + 511) // 512
            self._xi = 0
            self._foi = 0
            self._evac = 0

            def stage_a(blk):
                t0 = blk * 512
                ntok = min(512, NTOK - t0)
                nsub = ntok // 128
                m = 0 if t0 < LAT else 1
                hb = blk % 2
                S.dma("sp", cosd[hb][:, :ntok], self.rope_da[0, :, t0:t0 + ntok], w=[Brt[hb]])
                S.dma("sp", sind[hb][:, :ntok], self.rope_da[1, :, t0:t0 + ntok], w=[Brt[hb]])
                S.dma("sp", cosm[hb][:, :ntok], self.rope_ml[0, :, t0:t0 + ntok], w=[Brt[hb]])
                S.dma("sp", sinm[hb][:, :ntok], self.rope_ml[1, :, t0:t0 + ntok], w=[Brt[hb]])
                for s in range(nsub):
                    xj = self._xi % NXS
                    self._xi += 1
                    S.dma("sp", xt[xj][:], src[t0 + s * 128:t0 + (s + 1) * 128, :], w=[Bx[xj]])
                    for half in range(2):
                        p, Bp = next_pp()
                        for q in range(4):
                            dc = half * 4 + q
                            S.op("pe", lambda e, p=p, q=q, dc=dc, xj=xj: e.transpose(
                                p[:, q * 128:(q + 1) * 128], xt[xj][:, dc * 128:(dc + 1) * 128], self.ident_f[:]),
                                r=[Bx[xj], self.B_const], w=[Bp])
                        for q in range(4):
                            dc = half * 4 + q
                            if self._evac % 2 == 0:
                                S.op("act", lambda e, p=p, q=q, dc=dc, s=s, m=m: e.activation(
                                    out=hT[hb][:, dc, s * 128:(s + 1) * 128], in_=p[:, q * 128:(q + 1) * 128],
                                    func=AF.Identity, scale=self.sc1[:, dc, m:m + 1], bias=self.sh[:, dc, m:m + 1]),
                                    r=[Bp, self.B_mod], w=[Bh[hb]])
                            else:
                                S.op("dve", lambda e, p=p, q=q, dc=dc, s=s, m=m: e.tensor_scalar(
                                    out=hT[hb][:, dc, s * 128:(s + 1) * 128], in0=p[:, q * 128:(q + 1) * 128],
                                    scalar1=self.sc1[:, dc, m:m + 1], scalar2=self.sh[:, dc, m:m + 1],
                                    op0=ALU.mult, op1=ALU.add), r=[Bp, self.B_mod], w=[Bh[hb]])
                            self._evac += 1

            def stage_b(blk):
                t0 = blk * 512
                ntok = min(512, NTOK - t0)
                nsub = ntok // 128
                m = 0 if t0 < LAT else 1
                hb = blk % 2
                foi = self._foi
                for grp, base, dstT, bofs in ((0, EQ, self.qda, 0), (1, EK, self.kda, 4)):
                    for h in range(2):
                        pa, Ba = next_pp()
                        pb, Bb = next_pp()
                        for k in range(8):
                            S.op("pe", lambda e, pa=pa, k=k, c0=base + h * 128: e.matmul(
                                pa[:, :ntok], wb[:, k, c0:c0 + 128], hT[hb][:, k, :ntok], start=(k == 0), stop=(k == 7)),
                                r=[Bw, Bh[hb]], w=[Ba])
                        for k in range(8):
                            S.op("pe", lambda e, pb=pb, k=k, c0=base + 256 + h * 128: e.matmul(
                                pb[:, :ntok], wb[:, k, c0:c0 + 128], hT[hb][:, k, :ntok], start=(k == 0), stop=(k == 7)),
                                r=[Bw, Bh[hb]], w=[Bb])
                        tj = (grp * 2 + h) % 2
                        S.op("dve", lambda e, pa=pa, tj=tj, c=bofs + h: e.scalar_tensor_tensor(
                            out=t1[tj][:, :ntok], in0=pa[:, :ntok], scalar=bcol[:, c:c + 1], in1=cosd[hb][:, :ntok],
                            op0=ALU.add, op1=ALU.mult), r=[Ba, Brt[hb], Bw], w=[Bt1[tj]])
                        S.op("dve", lambda e, pb=pb, tj=tj, c=bofs + 2 + h: e.scalar_tensor_tensor(
                            out=t2[tj][:, :ntok], in0=pb[:, :ntok], scalar=bcol[:, c:c + 1], in1=sind[hb][:, :ntok],
                            op0=ALU.add, op1=ALU.mult), r=[Bb, Brt[hb], Bw], w=[Bt2[tj]])
                        fj = foi % 4
                        foi += 1
                        S.op("pool", lambda e, tj=tj, fj=fj: e.tensor_tensor(
                            out=fo[fj][:, :ntok], in0=t1[tj][:, :ntok], in1=t2[tj][:, :ntok], op=ALU.add),
                            r=[Bt1[tj], Bt2[tj]], w=[Bfo[fj]])
                        S.dma("pool", dstT[h, :, t0:t0 + ntok], fo[fj][:, :ntok], r=[Bfo[fj]])
                pa, Ba = next_pp()
                pb, Bb = next_pp()
                for k in range(8):
                    S.op("pe", lambda e, pa=pa, k=k: e.matmul(
                        pa[0:32, :ntok], wb[:, k, EKR:EKR + 32], hT[hb][:, k, :ntok], start=(k == 0), stop=(k == 7)),
                        r=[Bw, Bh[hb]], w=[Ba])
                for k in range(8):
                    S.op("pe", lambda e, pb=pb, k=k: e.matmul(
                        pb[0:32, :ntok], wb[:, k, EKR + 32:EKR + 64], hT[hb][:, k, :ntok], start=(k == 0), stop=(k == 7)),
                        r=[Bw, Bh[hb]], w=[Bb])
                tj = 0
                S.op("dve", lambda e, pa=pa: e.scalar_tensor_tensor(
                    out=t1[tj][0:32, :ntok], in0=pa[0:32, :ntok], scalar=bcol[0:32, 8:9], in1=cosm[hb][0:32, :ntok],
                    op0=ALU.add, op1=ALU.mult), r=[Ba, Brt[hb], Bw], w=[Bt1[tj]])
                S.op("dve", lambda e, pb=pb: e.scalar_tensor_tensor(
                    out=t2[tj][0:32, :ntok], in0=pb[0:32, :ntok], scalar=bcol[0:32, 9:10], in1=sinm[hb][0:32, :ntok],
                    op0=ALU.add, op1=ALU.mult), r=[Bb, Brt[hb], Bw], w=[Bt2[tj]])
                fj = foi % 4
                foi += 1
                S.op("pool", lambda e, fj=fj: e.tensor_tensor(
                    out=fo[fj][0:32, :ntok], in0=t1[tj][0:32, :ntok], in1=t2[tj][0:32, :ntok], op=ALU.add),
                    r=[Bt1[tj], Bt2[tj]], w=[Bfo[fj]])
                S.dma("pool", self.krt[:, t0:t0 + ntok], fo[fj][0:32, :ntok], r=[Bfo[fj]])
                cb = blk % 2
                for s in range(nsub):
                    tt = (t0 // 128) + s
                    tok0 = t0 + s * 128
                    p, Bp = next_pp()
                    for k in range(8):
                        S.op("pe", lambda e, p=p, k=k, s=s: e.matmul(
                            p[:, 0:256], hT[hb][:, k, s * 128:(s + 1) * 128], wb[:, k, EV_:EV_ + 256], start=(k == 0), stop=False),
                            r=[Bw, Bh[hb]], w=[Bp])
                    S.op("pe", lambda e, p=p: e.matmul(p[:, 0:256], self.ones_b[0:1, :], brow[0:1, 0:256], start=False, stop=True),
                         r=[Bw, self.B_const], w=[Bp])
                    vj = tt % 2
                    S.op("act", lambda e, p=p, vj=vj: e.activation(
                        out=va[vj][:, :, 0:128], in_=p[:, 0:256].rearrange("p (h c) -> p h c", h=2), func=AF.Copy),
                        r=[Bp], w=[Bva[vj]])
                    S.dma("pool", self.vda[:, :, tt, :].rearrange("h p c -> p h c"), va[vj][:], r=[Bva[vj]])
                    p, Bp = next_pp()
                    for k in range(8):
                        S.op("pe", lambda e, p=p, k=k, s=s: e.matmul(
                            p[:, 0:384], hT[hb][:, k, s * 128:(s + 1) * 128], wb[:, k, EC:EC + 384], start=(k == 0), stop=False),
                            r=[Bw, Bh[hb]], w=[Bp])
                    S.op("pe", lambda e, p=p: e.matmul(p[:, 0:384], self.ones_b[0:1, :], brow[0:1, 256:640], start=False, stop=True),
                         r=[Bw, self.B_const], w=[Bp])
                    cj = tt % 2
                    S.op("act", lambda e, p=p, cj=cj: e.activation(out=cq[cj][:], in_=p[:, 0:384], func=AF.Copy),
                         r=[Bp], w=[Bcq[cj]])
                    S.op("act", lambda e, cj=cj: e.activation(
                        out=junk[:, 0:256], in_=cq[cj][:, 0:256], func=AF.Square, accum_out=stat[cj][:, 0:1]),
                        r=[Bcq[cj]], w=[Bjunk, Bstat[cj]])
                    S.op("act", lambda e, cj=cj: e.activation(
                        out=junk[:, 0:128], in_=cq[cj][:, 256:384], func=AF.Square, accum_out=stat[cj][:, 1:2]),
                        r=[Bcq[cj]], w=[Bjunk, Bstat[cj]])
                    S.op("act", lambda e, cj=cj: e.activation(out=stat[cj][:, 2:3], in_=stat[cj][:, 0:1], func=AF.Sqrt,
                                                              scale=1.0 / 256, bias=eps_t[:, 0:1]), r=[Bstat[cj], Bw], w=[Bstat[cj]])
                    S.op("act", lambda e, cj=cj: e.activation(out=stat[cj][:, 3:4], in_=stat[cj][:, 1:2], func=AF.Sqrt,
                                                              scale=1.0 / 128, bias=eps_t[:, 0:1]), r=[Bstat[cj], Bw], w=[Bstat[cj]])
                    S.op("dve", lambda e, cj=cj: e.reciprocal(stat[cj][:, 4:6], stat[cj][:, 2:4]), r=[Bstat[cj]], w=[Bstat[cj]])
                    S.op("dve", lambda e, cj=cj: e.tensor_scalar(
                        out=cqn[cj][:, 0:256], in0=cq[cj][:, 0:256], scalar1=stat[cj][:, 4:5], scalar2=None, op0=ALU.mult),
                        r=[Bcq[cj], Bstat[cj]], w=[Bcqn[cj]])
                    S.op("dve", lambda e, cj=cj: e.tensor_scalar(
                        out=cqn[cj][:, 256:384], in0=cq[cj][:, 256:384], scalar1=stat[cj][:, 5:6], scalar2=None, op0=ALU.mult),
                        r=[Bcq[cj], Bstat[cj]], w=[Bcqn[cj]])
                    gj = tt % 2
                    for n in range(2):
                        p, Bp = next_pp()
                        for k in range(8):
                            S.op("pe", lambda e, p=p, k=k, s=s, n=n: e.matmul(
                                p[:, :], hT[hb][:, k, s * 128:(s + 1) * 128], wb[:, k, EG + n * 512:EG + (n + 1) * 512],
                                start=(k == 0), stop=False), r=[Bw, Bh[hb]], w=[Bp])
                        S.op("pe", lambda e, p=p, n=n: e.matmul(
                            p[:, :], self.ones_b[0:1, :], brow[0:1, 640 + n * 512:640 + (n + 1) * 512], start=False, stop=True),
                            r=[Bw, self.B_const], w=[Bp])
                        S.op("act", lambda e, p=p, n=n, gj=gj: e.activation(
                            out=gt[gj][:, n * 512:(n + 1) * 512], in_=p[:, :], func=AF.Silu), r=[Bp], w=[Bgt[gj]])
                    S.dma("pool", self.gate[tok0:tok0 + 128, :], gt[gj][:], r=[Bgt[gj]])
                    for rc in range(3):
                        S.op("pe", lambda e, rc=rc, cj=cj: e.transpose(
                            ppb[:, rc * 128:(rc + 1) * 128], cqn[cj][:, rc * 128:(rc + 1) * 128], self.ident_b[:]),
                            r=[Bcqn[cj], self.B_const], w=[Bppb])
                    S.op("dve", lambda e, s=s: e.tensor_copy(
                        cT[cb][:, :, s * 128:(s + 1) * 128], ppb[:, 0:384].rearrange("p (r t) -> p r t", r=3)),
                        r=[Bppb], w=[BcT[cb]])
                for h in range(4):
                    pa, Ba = next_pp()
                    pb, Bb = next_pp()
                    for rc in range(2):
                        S.op("pe", lambda e, pa=pa, rc=rc, h=h: e.matmul(
                            pa[0:96, :ntok], wuq[:, rc, h * 96:(h + 1) * 96], cT[cb][:, rc, :ntok], start=(rc == 0), stop=(rc == 1)),
                            r=[Bw, BcT[cb]], w=[Ba])
                    for rc in range(2):
                        S.op("pe", lambda e, pb=pb, rc=rc, h=h: e.matmul(
                            pb[0:96, :ntok], wuq[:, rc, 384 + h * 96:384 + (h + 1) * 96], cT[cb][:, rc, :ntok],
                            start=(rc == 0), stop=(rc == 1)), r=[Bw, BcT[cb]], w=[Bb])
                    tj = h % 2
                    fj = foi % 4
                    foi += 1
                    S.op("dve", lambda e, pa=pa, tj=tj: e.tensor_tensor(
                        out=t1[tj][64:96, :ntok], in0=pa[64:96, :ntok], in1=cosm[hb][64:96, :ntok], op=ALU.mult),
                        r=[Ba, Brt[hb]], w=[Bt1[tj]])
                    S.op("dve", lambda e, pb=pb, tj=tj: e.tensor_tensor(
                        out=t2[tj][64:96, :ntok], in0=pb[64:96, :ntok], in1=sinm[hb][64:96, :ntok], op=ALU.mult),
                        r=[Bb, Brt[hb]], w=[Bt2[tj]])
                    S.op("act", lambda e, pa=pa, fj=fj: e.activation(out=fo[fj][0:64, :ntok], in_=pa[0:64, :ntok], func=AF.Copy),
                         r=[Ba], w=[Bfo[fj]])
                    S.op("pool", lambda e, tj=tj, fj=fj: e.tensor_tensor(
                        out=fo[fj][64:96, :ntok], in0=t1[tj][64:96, :ntok], in1=t2[tj][64:96, :ntok], op=ALU.add),
                        r=[Bt1[tj], Bt2[tj]], w=[Bfo[fj]])
                    S.dma("pool", self.qm[h, :, t0:t0 + ntok], fo[fj][0:96, :ntok], r=[Bfo[fj]])
                for hp in range(2):
                    p, Bp = next_pp()
                    S.op("pe", lambda e, p=p, hp=hp: e.matmul(
                        p[:, :ntok], wukv[:, hp * 128:(hp + 1) * 128], cT[cb][:, 2, :ntok], start=True, stop=True),
                        r=[Bw, BcT[cb]], w=[Bp])
                    fj = foi % 4
                    foi += 1
                    S.op("act", lambda e, p=p, fj=fj: e.activation(out=fo[fj][:, :ntok], in_=p[:, :ntok], func=AF.Copy),
                         r=[Bp], w=[Bfo[fj]])
                    S.dma("pool", self.kmn[2 * hp, :, t0:t0 + ntok], fo[fj][0:64, :ntok], r=[Bfo[fj]])
                    S.dma("pool", self.kmn[2 * hp + 1, :, t0:t0 + ntok], fo[fj][64:128, :ntok], r=[Bfo[fj]])
                for s in range(nsub):
                    tt = (t0 // 128) + s
                    p, Bp = next_pp()
                    S.op("pe", lambda e, p=p, s=s: e.matmul(
                        p[:, 0:256], cT[cb][:, 2, s * 128:(s + 1) * 128], wukv[:, 256:512], start=True, stop=True),
                        r=[Bw, BcT[cb]], w=[Bp])
                    vj = tt % 2
                    S.op("act", lambda e, p=p, vj=vj: e.activation(
                        out=vma[vj][:, :, 0:64], in_=p[:, 0:256].rearrange("p (h c) -> p h c", h=4), func=AF.Copy),
                        r=[Bp], w=[Bvma[vj]])
                    S.dma("pool", self.vm[:, :, tt, :].rearrange("h p c -> p h c"), vma[vj][:], r=[Bvma[vj]])
                self._foi = foi

            stage_a(0)
            for blk in range(nblk):
                if blk + 1 < nblk:
                    stage_a(blk + 1)
                stage_b(blk)
            S.barrier()

    def attn_core(self, st, name, KT, QT, VA, Bkv, kparts, nv, scale, qblocks, finalize, kslices=None):
        S = self.S
        nmap = len(kparts)
        nacc_per_bank = 512 // nv
        n_acc = nmap * 4
        n_acc_banks = (n_acc + nacc_per_bank - 1) // nacc_per_bank
        n_sc = 8 - n_acc_banks
        n_sc = min(n_sc, 4)
        NP = 4
        res = getattr(self, "_attn_res", None)
        if res is None or res[0] != name:
            accb = [self.ps(st, f"{name}_acc{j}", [128, 512]) for j in range(n_acc_banks)]
            Bacc = S.bufs(name + "_acc", n_acc_banks)
            scb = [self.ps(st, f"{name}_sc{j}", [128, 512]) for j in range(n_sc)]
            Bsc = S.bufs(name + "_sc", n_sc)
            pt = [self.sb(st, f"{name}_pt{j}", [128, 512], BF16) for j in range(NP)]
            Bpt = S.bufs(name + "_pt", NP)
            accsb = [[self.sb(st, f"{name}_accsb{r}_{j}", [128, 512]) for j in range(n_acc_banks)] for r in range(2)]
            Baccsb = [S.bufs(f"{name}_accsb{r}", n_acc_banks) for r in range(2)]
            self._attn_res = (name, accb, Bacc, scb, Bsc, pt, Bpt, accsb, Baccsb)
        _, accb, Bacc, scb, Bsc, pt, Bpt, accsb, Baccsb = self._attn_res

        def acc_ap(mi, sub):
            idx = mi * 4 + sub
            b = idx // nacc_per_bank
            o = (idx % nacc_per_bank) * nv
            return accb[b][:, o:o + nv], Bacc[b]

        sci = 0
        pti = 0
        self._pending_fin = None
        for qbi, (q0, nq, ktiles) in enumerate(qblocks):
            nsub = nq // 128
            for b in range(n_acc_banks):
                S.op("pe", lambda e, b=b: e.matmul(accb[b][:, :], self.zeros_b[:, :], self.zeros_w[:, :], start=True, stop=False,
                                                   skip_group_check=True), r=[self.B_const], w=[Bacc[b]])
            units = [(kt, mi) for kt in ktiles for mi in range(nmap)]
            pend = []

            def emit_score(u):
                nonlocal sci
                kt, mi = u
                QTm, lo, hi = kparts[mi]
                j = sci % n_sc
                sci += 1
                S.op("pe", lambda e, j=j, lo=lo, hi=hi, kt=kt, QTm=QTm: e.matmul(
                    scb[j][:, :nq], KT[lo:hi, kt * 128:(kt + 1) * 128], QTm[lo:hi, q0:q0 + nq], start=True, stop=True),
                    r=[Bkv], w=[Bsc[j]])
                return j

            def emit_exp_pv(u, j, lastflag):
                nonlocal pti
                kt, mi = u
                pj = pti % NP
                pti += 1
                S.op("act", lambda e, j=j, pj=pj: e.activation(out=pt[pj][:, :nq], in_=scb[j][:, :nq], func=AF.Exp, scale=scale),
                     r=[Bsc[j]], w=[Bpt[pj]])
                for sub in range(nsub):
                    ap, Ba = acc_ap(mi, sub)
                    S.op("pe", lambda e, ap=ap, pj=pj, sub=sub, kt=kt: e.matmul(
                        ap, pt[pj][:, sub * 128:(sub + 1) * 128], VA[:, kt, :], start=False, stop=lastflag,
                        skip_group_check=True), r=[Bpt[pj], Bkv], w=[Ba])

            LOOK = min(n_sc - 1, 2)
            q = []
            for ui, u in enumerate(units):
                q.append((u, emit_score(u)))
                if len(q) > LOOK:
                    u0, j0 = q.pop(0)
                    emit_exp_pv(u0, j0, False)
                if ui == 8 and self._pending_fin is not None:
                    self._pending_fin()
                    self._pending_fin = None
            while q:
                u0, j0 = q.pop(0)
                emit_exp_pv(u0, j0, u0[0] == ktiles[-1])
            r = qbi % 2
            used = nmap * 4 * nv
            for b in range(n_acc_banks):
                w = min(512, used - b * nacc_per_bank * nv)
                S.op("dve", lambda e, b=b, r=r, w=w: e.tensor_copy(accsb[r][b][:, 0:w], accb[b][:, 0:w]), r=[Bacc[b]], w=[Baccsb[r][b]])

            def acc_sb(mi, sub, r=r):
                idx = mi * 4 + sub
                b = idx // nacc_per_bank
                o = (idx % nacc_per_bank) * nv
                return accsb[r][b][:, o:o + nv], Baccsb[r][b]

            if self._pending_fin is not None:
                self._pending_fin()
                self._pending_fin = None

            def do_fin(qbi=qbi, q0=q0, nsub=nsub, acc_sb=acc_sb):
                for sub in range(nsub):
                    accs = [acc_sb(mi, sub) for mi in range(nmap)]
                    finalize(qbi, q0, sub, accs)
            self._pending_fin = do_fin
        if self._pending_fin is not None:
            self._pending_fin()
            self._pending_fin = None

    def qblocks_all(self):
        qb = []
        lat_k = list(range(NT))
        for b in range(LAT // 512):
            qb.append((b * 512, 512, lat_k))
        qb.append((LAT, CTX, [NT_LAT, NT_LAT + 1]))
        if self.qb_filter is not None:
            qb = [q for i, q in enumerate(qb) if i in self.qb_filter]
        return qb

    def qblocks_own(self):
        qb = []
        lat_k = list(range(NT))
        for b in range(LAT // 2 // 512):
            qb.append((b * 512, 512, lat_k))
        qb.append((LAT // 2, CTX, [NT_LAT, NT_LAT + 1]))
        if self.qb_filter is not None:
            qb = [q for i, q in enumerate(qb) if i in self.qb_filter]
        return qb

    def own_q(self, QO, QS, lo, hi, tmp, Bqs, Bqo, Btmp):
        S = self.S
        H = LAT // 2
        S.op("dve", lambda e: e.tensor_scalar(out=tmp[lo:hi, :], in0=QS[lo:hi, H:LAT], scalar1=self.pselt[lo:hi, 1:2], scalar2=None,
                                              op0=ALU.mult), r=[Bqs, self.B_const], w=[Btmp])
        S.op("dve", lambda e: e.scalar_tensor_tensor(out=QO[lo:hi, 0:H], in0=QS[lo:hi, 0:H], scalar=self.pselt[lo:hi, 0:1],
                                                     in1=tmp[lo:hi, :], op0=ALU.mult, op1=ALU.add),
             r=[Bqs, Btmp, self.B_const], w=[Bqo])
        S.op("act", lambda e: e.activation(out=QO[lo:hi, H:H + CTX], in_=QS[lo:hi, LAT:NTOK], func=AF.Copy), r=[Bqs], w=[Bqo])

    def blend(self, out, a, b, Ba, Bb, Bout, lo=0, hi=128, col=0):
        S = self.S
        s0 = self.pselt[lo:hi, col:col + 1]
        s1 = self.pselt[lo:hi, col + 1:col + 2]
        S.op("dve", lambda e: e.tensor_scalar(out=b, in0=b, scalar1=s1, scalar2=None, op0=ALU.mult), r=[Bb, self.B_const], w=[Bb])
        S.op("dve", lambda e: e.scalar_tensor_tensor(out=out, in0=a, scalar=s0, in1=b, op0=ALU.mult, op1=ALU.add),
             r=[Ba, Bb, self.B_const], w=[Bout])

    def yodd_loc(self, t):
        if t < NT_LAT:
            return t // 16, (t % 16) * 128
        return 4, (t - NT_LAT) * 128

    def y_dst(self, tok0, c0, c1):
        H = LAT // 2
        if tok0 < H:
            k, r = tok0 // 1024, tok0 % 1024
            return self.yown_t[k].ap()[r:r + 128, c0:c1]
        return self.ybuf[LAT + tok0 - H:LAT + tok0 - H + 128, c0:c1]

    def da_attention(self, i, l):
        S = self.S
        lam_init = 0.8 - 0.6 * math.exp(-0.3 * l)
        with contextlib.ExitStack() as st:
            KTs = [self.sb(st, f"da_KT{j}", [128, NTOK], BF16) for j in range(2)]
            VAs = [self.sb(st, f"da_VA{j}", [128, NT, 129], BF16) for j in range(2)]
            QT1s = [self.sb(st, f"da_QT1{j}", [128, NTOK], BF16) for j in range(2)]
            QT2s = [self.sb(st, f"da_QT2{j}", [128, NTOK], BF16) for j in range(2)]
            Bkvs = S.bufs("da_kv", 2)
            for j in range(2):
                S.op("pool", lambda e, j=j: e.memset(QT1s[j][64:128, :], 0.0), w=[Bkvs[j]])
                S.op("pool", lambda e, j=j: e.memset(QT2s[j][0:64, :], 0.0), w=[Bkvs[j]])
            lamt = self.sb(st, "lamt", [128, 256])
            lamw = self.sb(st, "lamw", [128, 8])
            subg = self.sb(st, "subg", [128, 128])
            Bl = S.buf("lam")
            eps_t = self.sb(st, "da_eps", [128, 1])
            S.op("pool", lambda e: e.memset(eps_t[:], RMS_EPS), w=[Bl])
            S.dma("sp", lamt[:], self.ev_lam[i, :, :].to_broadcast([128, 256]), w=[Bl])
            S.dma("sp", subg[:], self.ev_subg[i, :, :].to_broadcast([128, 128]), w=[Bl])
            junk = self.sb(st, "da_junk", [128, 128])
            Bj = S.buf("da_junk")
            S.op("dve", lambda e: e.tensor_tensor(out=junk[:, 0:64], in0=lamt[:, 0:64], in1=lamt[:, 64:128], op=ALU.mult),
                 r=[Bl], w=[Bj])
            S.op("dve", lambda e: e.reduce_sum(out=lamw[:, 0:1], in_=junk[:, 0:64], axis=AX.X), r=[Bj], w=[Bl])
            S.op("dve", lambda e: e.tensor_tensor(out=junk[:, 64:128], in0=lamt[:, 128:192], in1=lamt[:, 192:256], op=ALU.mult),
                 r=[Bl], w=[Bj])
            S.op("dve", lambda e: e.reduce_sum(out=lamw[:, 1:2], in_=junk[:, 64:128], axis=AX.X), r=[Bj], w=[Bl])
            S.op("act", lambda e: e.activation(out=lamw[:, 2:4], in_=lamw[:, 0:2], func=AF.Exp), r=[Bl], w=[Bl])
            S.op("dve", lambda e: e.tensor_tensor(out=lamw[:, 4:5], in0=lamw[:, 3:4], in1=lamw[:, 2:3], op=ALU.subtract),
                 r=[Bl], w=[Bl])
            S.op("dve", lambda e: e.tensor_scalar(out=lamw[:, 5:6], in0=lamw[:, 4:5], scalar1=-lam_init, scalar2=None, op0=ALU.add),
                 r=[Bl], w=[Bl])
            NF = 3
            rr = [self.sb(st, f"da_rr{j}", [128, 8]) for j in range(NF)]
            ta = [self.sb(st, f"da_ta{j}", [128, 128]) for j in range(NF)]
            td = [self.sb(st, f"da_td{j}", [128, 128]) for j in range(NF)]
            to = [self.sb(st, f"da_to{j}", [128, 128], BF16) for j in range(NF)]
            Bf = S.bufs("da_fin", NF)
            Bto = S.bufs("da_to", NF)
            self._fi = 0
            def da_load(h):
                j = h % 2
                S.dma("sp", KTs[j][:], self.kda[h, :, :], w=[Bkvs[j]])
                S.dma("act", QT1s[j][0:64, :], self.qda[h, 0:64, :], w=[Bkvs[j]])
                S.dma("act", QT2s[j][64:128, :], self.qda[h, 64:128, :], w=[Bkvs[j]])
                S.dma("sp", VAs[j][:], self.vda[h, :, :, :], w=[Bkvs[j]])

            da_load(0)
            for h in range(2):
                if h + 1 < 2:
                    da_load(h + 1)
                KT, VA, QT1, QT2, Bkv = KTs[h % 2], VAs[h % 2], QT1s[h % 2], QT2s[h % 2], Bkvs[h % 2]

                def fin(qbi, q0, sub, accs, h=h):
                    j = self._fi % NF
                    self._fi += 1
                    (a1, B1), (a2, B2) = accs
                    S.op("dve", lambda e: e.reciprocal(rr[j][:, 0:1], a1[:, 128:129]), r=[B1], w=[Bf[j]])
                    S.op("dve", lambda e: e.reciprocal(rr[j][:, 1:2], a2[:, 128:129]), r=[B2], w=[Bf[j]])
                    S.op("dve", lambda e: e.tensor_tensor(out=rr[j][:, 2:3], in0=rr[j][:, 1:2], in1=lamw[:, 5:6], op=ALU.mult),
                         r=[Bf[j], Bl], w=[Bf[j]])
                    S.op("dve", lambda e: e.tensor_scalar(out=ta[j][:], in0=a1[:, 0:128], scalar1=rr[j][:, 0:1], scalar2=None,
                                                          op0=ALU.mult), r=[B1, Bf[j]], w=[Bf[j]])
                    S.op("dve", lambda e: e.scalar_tensor_tensor(out=td[j][:], in0=a2[:, 0:128], scalar=rr[j][:, 2:3], in1=ta[j][:],
                                                                 op0=ALU.mult, op1=ALU.add), r=[B2, Bf[j]], w=[Bf[j]])
                    S.op("act", lambda e: e.activation(out=ta[j][:], in_=td[j][:], func=AF.Square, accum_out=rr[j][:, 3:4]),
                         r=[Bf[j]], w=[Bf[j]])
                    S.op("act", lambda e: e.activation(out=rr[j][:, 4:5], in_=rr[j][:, 3:4], func=AF.Sqrt, scale=1.0 / 128,
                                                       bias=eps_t[:, 0:1]), r=[Bf[j], Bl], w=[Bf[j]])
                    S.op("dve", lambda e: e.reciprocal(rr[j][:, 5:6], rr[j][:, 4:5]), r=[Bf[j]], w=[Bf[j]])
                    S.op("pool", lambda e: e.tensor_scalar(out=td[j][:], in0=td[j][:], scalar1=rr[j][:, 5:6], scalar2=(1.0 - lam_init),
                                                           op0=ALU.mult, op1=ALU.mult), r=[Bf[j]], w=[Bf[j]])
                    S.op("pool", lambda e: e.tensor_tensor(out=to[j][:], in0=td[j][:], in1=subg[:], op=ALU.mult),
                         r=[Bf[j], Bl], w=[Bto[j]])
                    kk, r0 = self.yodd_loc((q0 + sub * 128) // 128)
                    S.dma("pool", self.yodd_t[kk].ap()[r0:r0 + 128, h * 128:(h + 1) * 128], to[j][:], r=[Bto[j]])

                self.attn_core(st, "da", KT, None, VA, Bkv, [(QT1, 0, 128), (QT2, 0, 128)], 129, DA_SCALE,
                               self.qblocks_all(), fin)
            S.barrier()
            self._attn_res = None

    def mla_attention(self, i):
        S = self.S
        with contextlib.ExitStack() as st:
            KTs = [self.sb(st, f"ml_KT{j}", [128, NTOK], BF16) for j in range(2)]
            QTs = [self.sb(st, f"ml_QT{j}", [128, NTOK], BF16) for j in range(2)]
            VAs = [self.sb(st, f"ml_VA{j}", [128, NT, 65], BF16) for j in range(2)]
            Bkvs = S.bufs("ml_kv", 2)
            NF = 3
            rr = [self.sb(st, f"ml_rr{j}", [128, 2]) for j in range(NF)]
            to = [self.sb(st, f"ml_to{j}", [128, 64], BF16) for j in range(NF)]
            Bto = S.bufs("ml_to", NF)
            self._fi = 0
            def ml_load(h):
                j = h % 2
                S.dma("sp", KTs[j][0:64, :], self.kmn[h, :, :], w=[Bkvs[j]])
                S.dma("sp", KTs[j][64:96, :], self.krt[:, :], w=[Bkvs[j]])
                S.dma("act", QTs[j][0:96, :], self.qm[h, :, :], w=[Bkvs[j]])
                S.dma("sp", VAs[j][:], self.vm[h, :, :, :], w=[Bkvs[j]])

            ml_load(0)
            for h in range(4):
                if h + 1 < 4:
                    ml_load(h + 1)
                KT, VA, QT, Bkv = KTs[h % 2], VAs[h % 2], QTs[h % 2], Bkvs[h % 2]

                def fin(qbi, q0, sub, accs, h=h):
                    j = self._fi % NF
                    self._fi += 1
                    (a1, B1), = accs
                    S.op("dve", lambda e: e.reciprocal(rr[j][:, 0:1], a1[:, 64:65]), r=[B1], w=[Bto[j]])
                    S.op("dve", lambda e: e.tensor_scalar(out=to[j][:], in0=a1[:, 0:64], scalar1=rr[j][:, 0:1], scalar2=None,
                                                          op0=ALU.mult), r=[B1, Bto[j]], w=[Bto[j]])
                    tok0 = q0 + sub * 128
                    kk, r0 = self.yodd_loc(tok0 // 128)
                    S.dma("pool", self.yodd_t[kk].ap()[r0:r0 + 128, 256 + h * 64:256 + (h + 1) * 64], to[j][:], r=[Bto[j]])

                self.attn_core(st, "ml", KT, None, VA, Bkv, [(QT, 0, 96)], 65, MLA_SCALE, self.qblocks_all(), fin)
            S.barrier()
            self._attn_res = None

    def out_stage(self, wo_dram, src, dst, last, ysplit=False):
        S = self.S
        with contextlib.ExitStack() as st:
            wo = self.sb(st, "wo", [128, 8, D], BF16)
            Bw = S.buf("wo")
            pieces = [(wo[:, k, :], wo_dram[k * 128:(k + 1) * 128, :], None) for k in range(8)]
            self.load_cast(st, None, None, 128, D, pieces, Bw, name="wo")
            NB = 4
            yt = [self.sb(st, f"o_yt{j}", [128, D], BF16) for j in range(NB)]
            gt = [self.sb(st, f"o_gt{j}", [128, D], BF16) for j in range(NB)]
            xt = [self.sb(st, f"o_xt{j}", [128, D]) for j in range(NB)]
            yg = [self.sb(st, f"o_yg{j}", [128, D], BF16) for j in range(NB)]
            ygT = [self.sb(st, f"o_ygT{j}", [128, D], BF16) for j in range(NB)]
            rt = [self.sb(st, f"o_rt{j}", [128, D]) for j in range(NB)]
            ot = [self.sb(st, f"o_ot{j}", [128, D]) for j in range(NB)]
            stt = [self.sb(st, f"o_st{j}", [128, 16]) for j in range(NB)]
            By, Bg, Bx, Byg, BygT, Br, Bo, Bs = (S.bufs(n, NB) for n in ("o_yt", "o_gt", "o_xt", "o_yg", "o_ygT", "o_rt", "o_ot", "o_st"))
            ptr = [self.ps(st, f"o_ptr{j}", [128, D], BF16) for j in range(2)]
            Bptr = S.bufs("o_ptr", 2)
            pout = [self.ps(st, f"o_po{j}", [128, 512]) for j in range(4)]
            Bpo = S.bufs("o_po", 4)
            eps_t = self.sb(st, "o_eps", [128, 1])
            S.op("pool", lambda e: e.memset(eps_t[:], LN_EPS), w=[Bw])
            ntiles = NT_LAT if (last and self.final_out) else NT
            tiles = [t for t in range(ntiles) if self.tile_filter is None or t in self.tile_filter]

            def stage_l(t):
                j = t % NB
                tok0 = t * 128
                isctx = t >= NT_LAT
                if ysplit == "odd":
                    k, r0 = self.yodd_loc(t)
                    rows = self.yodd_rows[k]
                    g = self.yoddg_t[k].ap()
                    S.dma("sp", yt[j][:, 0:256], g[r0:r0 + 128, 0:256], w=[By[j]])
                    S.dma("sp", yt[j][:, 256:512], g[rows + r0:rows + r0 + 128, 0:256], w=[By[j]])
                    S.dma("sp", yt[j][:, 512:768], g[r0:r0 + 128, 256:512], w=[By[j]])
                    S.dma("sp", yt[j][:, 768:1024], g[rows + r0:rows + r0 + 128, 256:512], w=[By[j]])
                else:
                    if ysplit and not isctx:
                        half, u = t // 32, t % 32
                        r0 = half * 1024 + (u % 8) * 128
                        ysrc = self.ygath_t[u // 8].ap()[r0:r0 + 128, :]
                    else:
                        ysrc = self.ybuf[tok0:tok0 + 128, :]
                    S.dma("sp", yt[j][:], ysrc, w=[By[j]])
                S.dma("act", gt[j][:], self.gate[tok0:tok0 + 128, :], w=[Bg[j]])
                S.dma("sp", xt[j][:], src[tok0:tok0 + 128, :], w=[Bx[j]])

            def stage_a(t):
                j = t % NB
                S.op("dve", lambda e, j=j: e.tensor_tensor(out=yg[j][:], in0=yt[j][:], in1=gt[j][:], op=ALU.mult),
                     r=[By[j], Bg[j]], w=[Byg[j]])
                pj = t % 2
                for ec in range(8):
                    S.op("pe", lambda e, ec=ec, j=j, pj=pj: e.transpose(
                        ptr[pj][:, ec * 128:(ec + 1) * 128], yg[j][:, ec * 128:(ec + 1) * 128], self.ident_b[:]),
                        r=[Byg[j], self.B_const], w=[Bptr[pj]])
                S.op("act", lambda e, j=j, pj=pj: e.activation(out=ygT[j][:], in_=ptr[pj][:], func=AF.Copy),
                     r=[Bptr[pj]], w=[BygT[j]])
                for n in range(2):
                    pn = (t % 2) * 2 + n
                    for ec in range(8):
                        S.op("pe", lambda e, ec=ec, n=n, pn=pn, j=j: e.matmul(
                            pout[pn][:, :], ygT[j][:, ec * 128:(ec + 1) * 128], wo[:, ec, n * 512:(n + 1) * 512],
                            start=(ec == 0), stop=(ec == 7)), r=[BygT[j], Bw], w=[Bpo[pn]])

            def stage_b(t):
                j = t % NB
                tok0 = t * 128
                isctx = t >= NT_LAT
                G = self.g_c if isctx else self.g_l
                for n in range(2):
                    pn = (t % 2) * 2 + n
                    S.op("dve", lambda e, n=n, pn=pn, j=j, G=G: e.tensor_tensor(
                        out=rt[j][:, n * 512:(n + 1) * 512], in0=pout[pn][:, :], in1=G[:, n * 512:(n + 1) * 512], op=ALU.mult),
                        r=[Bpo[pn], self.B_mod], w=[Br[j]])
                S.op("dve", lambda e, j=j: e.scalar_tensor_tensor(
                    out=rt[j][:], in0=xt[j][:], scalar=ALPHA, in1=rt[j][:], op0=ALU.mult, op1=ALU.add),
                    r=[Bx[j], Br[j]], w=[Br[j]])
                for n in range(2):
                    S.op("dve", lambda e, n=n, j=j: e.bn_stats(stt[j][:, n * 6:(n + 1) * 6], rt[j][:, n * 512:(n + 1) * 512]),
                         r=[Br[j]], w=[Bs[j]])
                S.op("dve", lambda e, j=j: e.bn_aggr(stt[j][:, 12:14], stt[j][:, 0:12]), r=[Bs[j]], w=[Bs[j]])
                S.op("act", lambda e, j=j: e.activation(out=stt[j][:, 14:15], in_=stt[j][:, 13:14], func=AF.Sqrt, scale=1.0,
                                                        bias=eps_t[:, 0:1]), r=[Bs[j], Bw], w=[Bs[j]])
                S.op("dve", lambda e, j=j: e.reciprocal(stt[j][:, 15:16], stt[j][:, 14:15]), r=[Bs[j]], w=[Bs[j]])
                S.op("dve", lambda e, j=j: e.tensor_scalar(
                    out=ot[j][:], in0=rt[j][:], scalar1=stt[j][:, 12:13], scalar2=stt[j][:, 15:16], op0=ALU.subtract, op1=ALU.mult),
                    r=[Br[j], Bs[j]], w=[Bo[j]])
                S.op("pool", lambda e, j=j: e.tensor_tensor(out=ot[j][:], in0=ot[j][:], in1=self.lng[:], op=ALU.mult),
                     r=[Bo[j], self.B_mod], w=[Bo[j]])
                S.op("pool", lambda e, j=j: e.tensor_tensor(out=ot[j][:], in0=ot[j][:], in1=self.lnb[:], op=ALU.add),
                     r=[Bo[j], self.B_mod], w=[Bo[j]])
                S.dma("pool", dst[tok0:tok0 + 128, :], ot[j][:], r=[Bo[j]])

            for idx, t in enumerate(tiles):
                if idx == 0:
                    stage_l(t)
                    if len(tiles) > 1:
                        stage_l(tiles[1])
                    stage_a(t)
                if idx + 2 < len(tiles):
                    stage_l(tiles[idx + 2])
                if idx + 1 < len(tiles):
                    stage_a(tiles[idx + 1])
                stage_b(t)
            S.barrier()

    def odd_layer(self, l, src, dst, last):
        i = l // 2
        self.adaln(l)
        self.odd_project(i, src)
        self.odd_conv(i)
        self.mlstm(i)
        self.mlstm_post(i)
        self.na_attention(i, last)
        self.S.allgather_pairs(self.yodd_t, self.yoddg_t)
        self.out_stage(self.od_wo[i], src, dst, last, ysplit="odd")

    def odd_project(self, i, src):
        S = self.S
        with contextlib.ExitStack() as st:
            wb = self.sb(st, "od_wb", [128, 8, OD_COLS], BF16)
            bcol = self.sb(st, "od_bcol", [128, 8])
            brow_f = self.sb(st, "od_brow_f", [1, 1800])
            brow = self.sb(st, "od_brow", [1, 1800], BF16)
            fb = self.sb(st, "od_fb", [128, 4])
            Bw = S.buf("od_w")
            S.dma("sp", bcol[:], self.od_bcol[i, :, :], w=[Bw])
            S.dma("sp", brow_f[:], self.od_brow[i, :, :], w=[Bw])
            S.dma("sp", fb[:], self.od_fb[i, :, :].to_broadcast([128, 4]), w=[Bw])
            S.op("dve", lambda e: e.tensor_copy(brow[:], brow_f[:]), r=[Bw], w=[Bw])
            pieces = []
            for k in range(8):
                for c in range(2):
                    pieces.append((wb[:, k, c * 1412:(c + 1) * 1412],
                                   self.od_w[i, k * 128:(k + 1) * 128, c * 1412:(c + 1) * 1412], None))
            self.load_cast(st, None, None, 128, 1412, pieces, Bw, name="odw")
            NXS = 6
            xt = [self.sb(st, f"xt{j}", [128, D]) for j in range(NXS)]
            Bx = S.bufs("xt", NXS)
            hT = [self.sb(st, f"hT{j}", [128, 8, 512], BF16) for j in range(2)]
            Bh = S.bufs("hT", 2)
            ff = [self.sb(st, f"ff{j}", [128, 512]) for j in range(3)]
            Bff = S.bufs("ff", 3)
            fo = [self.sb(st, f"fo{j}", [128, 512], BF16) for j in range(3)]
            Bfo = S.bufs("fo", 3)
            va = [self.sb(st, f"va{j}", [128, 2, 129], BF16) for j in range(2)]
            Bva = S.bufs("va", 2)
            vna = [self.sb(st, f"vna{j}", [128, 4, 65], BF16) for j in range(2)]
            Bvna = S.bufs("vna", 2)
            ot = [self.sb(st, f"ot{j}", [128, 256]) for j in range(2)]
            Bot = S.bufs("ot", 2)
            gt = [self.sb(st, f"gt{j}", [128, D], BF16) for j in range(2)]
            Bgt = S.bufs("gt", 2)
            gs = [self.sb(st, f"gs{j}", [128, 4, 8]) for j in range(2)]
            gtmp = [self.sb(st, f"gtmp{j}", [128, 4, 4]) for j in range(2)]
            Bgs = S.bufs("gs", 2)
            pp = [self.ps(st, f"pp{j}", [128, 512]) for j in range(7)]
            pg = self.ps(st, "pg", [128, 512])
            Bpp = S.bufs("pp", 7)
            Bpg = S.buf("pg")
            for j in range(2):
                S.op("pool", lambda e, j=j: e.memset(va[j][:], 1.0), w=[Bva[j]])
                S.op("pool", lambda e, j=j: e.memset(vna[j][:], 1.0), w=[Bvna[j]])
            self._pp_i = 0

            def next_pp():
                j = self._pp_i % 7
                self._pp_i += 1
                return pp[j], Bpp[j]

            nblk = (NTOK + 511) // 512
            self._xi = 0
            self._evac = 0
            self._ffi = 0
            self._foi = 0

            def stage_a(blk):
                t0 = blk * 512
                ntok = min(512, NTOK - t0)
                nsub = ntok // 128
                m = 0 if t0 < LAT else 1
                hb = blk % 2
                for s in range(nsub):
                    xj = self._xi % NXS
                    self._xi += 1
                    S.dma("sp", xt[xj][:], src[t0 + s * 128:t0 + (s + 1) * 128, :], w=[Bx[xj]])
                    for half in range(2):
                        p, Bp = next_pp()
                        for q in range(4):
                            dc = half * 4 + q
                            S.op("pe", lambda e, p=p, q=q, dc=dc, xj=xj: e.transpose(
                                p[:, q * 128:(q + 1) * 128], xt[xj][:, dc * 128:(dc + 1) * 128], self.ident_f[:]),
                                r=[Bx[xj], self.B_const], w=[Bp])
                        for q in range(4):
                            dc = half * 4 + q
                            if self._evac % 2 == 0:
                                S.op("act", lambda e, p=p, q=q, dc=dc, s=s, m=m: e.activation(
                                    out=hT[hb][:, dc, s * 128:(s + 1) * 128], in_=p[:, q * 128:(q + 1) * 128],
                                    func=AF.Identity, scale=self.sc1[:, dc, m:m + 1], bias=self.sh[:, dc, m:m + 1]),
                                    r=[Bp, self.B_mod], w=[Bh[hb]])
                            else:
                                S.op("dve", lambda e, p=p, q=q, dc=dc, s=s, m=m: e.tensor_scalar(
                                    out=hT[hb][:, dc, s * 128:(s + 1) * 128], in0=p[:, q * 128:(q + 1) * 128],
                                    scalar1=self.sc1[:, dc, m:m + 1], scalar2=self.sh[:, dc, m:m + 1],
                                    op0=ALU.mult, op1=ALU.add), r=[Bp, self.B_mod], w=[Bh[hb]])
                            self._evac += 1

            def stage_b(blk):
                t0 = blk * 512
                ntok = min(512, NTOK - t0)
                nsub = ntok // 128
                m = 0 if t0 < LAT else 1
                hb = blk % 2
                ffi, foi = self._ffi, self._foi
                for c in range(8):
                    p, Bp = next_pp()
                    for k in range(8):
                        S.op("pe", lambda e, p=p, k=k, c=c: e.matmul(
                            p[:, :ntok], wb[:, k, c * 128:(c + 1) * 128], hT[hb][:, k, :ntok], start=(k == 0), stop=(k == 7)),
                            r=[Bw, Bh[hb]], w=[Bp])
                    if c < 4:
                        fj = ffi % 3
                        ffi += 1
                        if c % 2 == 0:
                            S.op("act", lambda e, p=p, c=c, fj=fj: e.activation(
                                out=ff[fj][:, :ntok], in_=p[:, :ntok], func=AF.Identity, bias=bcol[:, c:c + 1]),
                                r=[Bp, Bw], w=[Bff[fj]])
                        else:
                            S.op("dve", lambda e, p=p, c=c, fj=fj: e.tensor_scalar(
                                out=ff[fj][:, :ntok], in0=p[:, :ntok], scalar1=bcol[:, c:c + 1], scalar2=None, op0=ALU.add),
                                r=[Bp, Bw], w=[Bff[fj]])
                        S.dma("pool", self.qkpre[c, :, t0:t0 + ntok], ff[fj][:, :ntok], r=[Bff[fj]])
                    else:
                        fj = foi % 3
                        foi += 1
                        if c % 2 == 0:
                            S.op("act", lambda e, p=p, c=c, fj=fj: e.activation(
                                out=fo[fj][:, :ntok], in_=p[:, :ntok], func=AF.Identity, bias=bcol[:, c:c + 1]),
                                r=[Bp, Bw], w=[Bfo[fj]])
                        else:
                            S.op("dve", lambda e, p=p, c=c, fj=fj: e.tensor_scalar(
                                out=fo[fj][:, :ntok], in0=p[:, :ntok], scalar1=bcol[:, c:c + 1], scalar2=None, op0=ALU.add),
                                r=[Bp, Bw], w=[Bfo[fj]])
                        dd = self.qda if c < 6 else self.kda
                        S.dma("pool", dd[(c - 4) % 2, :, t0:t0 + ntok], fo[fj][:, :ntok], r=[Bfo[fj]])
                gb = blk % 2
                for s in range(nsub):
                    tt = (t0 // 128) + s
                    tok0 = t0 + s * 128

                    def tm(p, c0, n, b0, s=s):
                        for k in range(8):
                            S.op("pe", lambda e, k=k: e.matmul(
                                p[:, 0:n], hT[hb][:, k, s * 128:(s + 1) * 128], wb[:, k, c0:c0 + n], start=(k == 0), stop=False),
                                r=[Bw, Bh[hb]], w=[Bp])
                        S.op("pe", lambda e: e.matmul(p[:, 0:n], self.ones_b[0:1, :], brow[0:1, b0:b0 + n], start=False, stop=True),
                             r=[Bw, self.B_const], w=[Bp])
                    p, Bp = next_pp()
                    tm(p, OV, 256, 0)
                    vj = tt % 2
                    S.op("act", lambda e, p=p, vj=vj: e.activation(
                        out=va[vj][:, :, 0:128], in_=p[:, 0:256].rearrange("p (h c) -> p h c", h=2), func=AF.Copy),
                        r=[Bp], w=[Bva[vj]])
                    S.dma("pool", self.vda[:, :, tt, :].rearrange("h p c -> p h c"), va[vj][:], r=[Bva[vj]])
                    p, Bp = next_pp()
                    tm(p, OO, 256, 256)
                    oj = tt % 2
                    S.op("act", lambda e, p=p, oj=oj: e.activation(out=ot[oj][:], in_=p[:, 0:256], func=AF.Sigmoid), r=[Bp], w=[Bot[oj]])
                    S.dma("pool", self.og[tok0:tok0 + 128, :], ot[oj][:], r=[Bot[oj]])
                    Bp = Bpg
                    for k in range(8):
                        S.op("pe", lambda e, k=k, s=s: e.matmul(
                            pg[:, s * 8:(s + 1) * 8], hT[hb][:, k, s * 128:(s + 1) * 128], wb[:, k, OGT:OGT + 8],
                            start=(k == 0), stop=False), r=[Bw, Bh[hb]], w=[Bpg])
                    S.op("pe", lambda e, s=s: e.matmul(pg[:, s * 8:(s + 1) * 8], self.ones_b[0:1, :], brow[0:1, 512:520],
                                                       start=False, stop=True), r=[Bw, self.B_const], w=[Bpg])
                    p, Bp = next_pp()
                    tm(p, OVN, 256, 520)
                    vj = tt % 2
                    S.op("act", lambda e, p=p, vj=vj: e.activation(
                        out=vna[vj][:, :, 0:64], in_=p[:, 0:256].rearrange("p (h c) -> p h c", h=4), func=AF.Copy),
                        r=[Bp], w=[Bvna[vj]])
                    S.dma("pool", self.vm[:, :, tt, :].rearrange("h p c -> p h c"), vna[vj][:], r=[Bvna[vj]])
                    gj = tt % 2
                    for n in range(2):
                        p, Bp = next_pp()
                        tm(p, OG + n * 512, 512, 776 + n * 512)
                        S.op("act", lambda e, p=p, n=n, gj=gj: e.activation(
                            out=gt[gj][:, n * 512:(n + 1) * 512], in_=p[:, :], func=AF.Silu), r=[Bp], w=[Bgt[gj]])
                    S.dma("pool", self.gate[tok0:tok0 + 128, :], gt[gj][:], r=[Bgt[gj]])
                pgv = pg[:, 0:nsub * 8].rearrange("p (s c) -> p s c", c=8)
                S.op("dve", lambda e, pgv=pgv: e.tensor_copy(gs[gb][:, 0:nsub, 0:4], pgv[:, :, 0:4]), r=[Bpg], w=[Bgs[gb]])
                for s in range(nsub):
                    S.op("dve", lambda e, s=s: e.tensor_tensor(out=gtmp[gb][:, s, :], in0=pg[:, s * 8 + 4:s * 8 + 8], in1=fb[:, :],
                                                               op=ALU.add), r=[Bpg, Bw], w=[Bgs[gb]])
                S.op("act", lambda e: e.activation(out=gtmp[gb][:, 0:nsub, :], in_=gtmp[gb][:, 0:nsub, :], func=AF.Exp, scale=-1.0),
                     r=[Bgs[gb]], w=[Bgs[gb]])
                S.op("act", lambda e: e.activation(out=gtmp[gb][:, 0:nsub, :], in_=gtmp[gb][:, 0:nsub, :], func=AF.Ln, bias=1.0),
                     r=[Bgs[gb]], w=[Bgs[gb]])
                S.op("dve", lambda e: e.tensor_scalar(out=gs[gb][:, 0:nsub, 4:8], in0=gtmp[gb][:, 0:nsub, :], scalar1=-1.0,
                                                      scalar2=None, op0=ALU.mult), r=[Bgs[gb]], w=[Bgs[gb]])
                tt0 = t0 // 128
                S.dma("pool", self.gates[:, tt0:tt0 + nsub, :], gs[gb][:, 0:nsub, :], r=[Bgs[gb]])
                self._ffi, self._foi = ffi, foi

            stage_a(0)
            for blk in range(nblk):
                if blk + 1 < nblk:
                    stage_a(blk + 1)
                stage_b(blk)
            S.barrier()

    def odd_conv(self, i):
        S = self.S
        SEG = 2048
        with contextlib.ExitStack() as st:
            cw = self.sb(st, "cv_w", [128, 4, 5])
            cb = self.sb(st, "cv_b", [128, 4])
            Bw = S.buf("cv_w")
            S.dma("sp", cw[:], self.od_convw[i, :, :, :], w=[Bw])
            S.dma("sp", cb[:], self.od_convb[i, :, :], w=[Bw])
            xin = [self.sb(st, f"cv_x{j}", [128, SEG + 4]) for j in range(2)]
            Bxin = S.bufs("cv_x", 2)
            acc = [self.sb(st, f"cv_a{j}", [128, SEG]) for j in range(2)]
            Bacc = S.bufs("cv_a", 2)
            tmp = [self.sb(st, f"cv_t{j}", [128, SEG]) for j in range(2)]
            Btmp = S.bufs("cv_t", 2)
            outb = [self.sb(st, f"cv_o{j}", [128, SEG], BF16) for j in range(2)]
            Bout = S.bufs("cv_o", 2)
            segs = [(a, SEG, 0, LAT) for a in range(0, LAT, SEG)] + [(LAT, CTX, LAT, LAT + CTX)]
            it = 0
            for c in range(4):
                for (t0, n, lo, hi) in segs:
                    j = it % 2
                    it += 1
                    a = max(t0 - 2, lo)
                    b = min(t0 + n + 2, hi)
                    if a > t0 - 2:
                        S.op("pool", lambda e, j=j: e.memset(xin[j][:, 0:2], 0.0), w=[Bxin[j]])
                    if b < t0 + n + 2:
                        S.op("pool", lambda e, j=j, n=n: e.memset(xin[j][:, n + 2:n + 4], 0.0), w=[Bxin[j]])
                    S.dma("sp", xin[j][:, a - (t0 - 2):b - (t0 - 2)], self.qkpre[c, :, a:b], w=[Bxin[j]])
                    S.op("act", lambda e, j=j, n=n, c=c: e.activation(
                        out=acc[j][:, 0:n], in_=xin[j][:, 0:n], func=AF.Copy, scale=cw[:, c, 0:1]),
                        r=[Bxin[j], Bw], w=[Bacc[j]])
                    for k in range(1, 5):
                        S.op("dve", lambda e, j=j, n=n, c=c, k=k: e.scalar_tensor_tensor(
                            out=acc[j][:, 0:n], in0=xin[j][:, k:k + n], scalar=cw[:, c, k:k + 1], in1=acc[j][:, 0:n],
                            op0=ALU.mult, op1=ALU.add), r=[Bxin[j], Bw, Bacc[j]], w=[Bacc[j]])
                    if True:
                        S.op("act", lambda e, j=j, n=n, c=c: e.activation(
                            out=outb[j][:, 0:n], in_=acc[j][:, 0:n], func=AF.Silu, bias=cb[:, c:c + 1]),
                            r=[Bacc[j], Bw], w=[Bout[j]])
                        dd = self.mq if c < 2 else self.mk
                        S.dma("pool", dd[c % 2, :, t0:t0 + n], outb[j][:, 0:n], r=[Bout[j]])
                    else:
                        S.op("act", lambda e, j=j, n=n, c=c: e.activation(
                            out=tmp[j][:, 0:n], in_=acc[j][:, 0:n], func=AF.Silu, bias=cb[:, c:c + 1]),
                            r=[Bacc[j], Bw], w=[Btmp[j]])
                        S.op("pool", lambda e, j=j, n=n: e.tensor_scalar(
                            out=outb[j][:, 0:n], in0=tmp[j][:, 0:n], scalar1=128 ** -0.5, scalar2=None, op0=ALU.mult),
                            r=[Btmp[j]], w=[Bout[j]])
                        S.dma("sp", self.mk[c - 4, :, t0:t0 + n], outb[j][:, 0:n], r=[Bout[j]])
            S.barrier()

    def mlstm(self, i):
        S = self.S
        with contextlib.ExitStack() as st:
            tri = self.sb(st, "ml_tri", [128, 3, 128])
            G = self.sb(st, "ml_G", [128, NT, 8])
            Ao = self.sb(st, "ml_Ao", [128, NT, 4])
            A2o = self.sb(st, "ml_A2o", [128, NT, 4])
            Bqo = self.sb(st, "ml_Bqo", [128, NT, 4])
            EBo = self.sb(st, "ml_EBo", [128, NT, 4])
            Bg = S.buf("ml_g")
            S.dma("sp", tri[:], self.tri[:, :, :], w=[Bg])
            S.dma("sp", G[:], self.gates[:, :, :], w=[Bg])
            lnks = self.sb(st, "ml_lnks", [128, 1])
            S.op("pool", lambda e: e.memset(lnks[:], math.log(128 ** -0.5)), w=[Bg])
            with contextlib.ExitStack() as st1:
                pg = [self.ps(st1, f"ml_pg{j}", [128, 512]) for j in range(3)]
                Bpg = S.bufs("ml_pg", 3)
                tmpg = self.sb(st1, "ml_tmpg", [128, 32, 4])
                Btg = S.buf("ml_tmpg")
                grp = 0
                for g0 in range(0, NT, 32):
                    g1 = min(g0 + 32, NT)
                    pj = grp % 3
                    grp += 1
                    for t in range(g0, g1):
                        o = (t - g0) * 8
                        S.op("pe", lambda e, t=t, o=o, pj=pj: e.matmul(pg[pj][:, o:o + 2], tri[:, 0, :], G[:, t, 4:6], start=True, stop=True),
                             r=[Bg], w=[Bpg[pj]])
                        S.op("pe", lambda e, t=t, o=o, pj=pj: e.matmul(pg[pj][:, o + 2:o + 4], tri[:, 1, :], G[:, t, 6:8], start=True, stop=True),
                             r=[Bg], w=[Bpg[pj]])
                        S.op("pe", lambda e, t=t, o=o, pj=pj: e.matmul(pg[pj][:, o + 4:o + 8], tri[:, 2, :], G[:, t, 4:8], start=True, stop=True),
                             r=[Bg], w=[Bpg[pj]])
                    n = g1 - g0
                    pv = pg[pj][:, 0:n * 8].rearrange("p (t c) -> p t c", c=8)
                    S.op("dve", lambda e, pv=pv, g0=g0, g1=g1, n=n: e.tensor_tensor(
                        out=tmpg[:, 0:n, :], in0=G[:, g0:g1, 0:4], in1=pv[:, :, 0:4], op=ALU.subtract), r=[Bg, Bpg[pj]], w=[Btg])
                    S.op("act", lambda e, g0=g0, g1=g1, n=n: e.activation(out=Ao[:, g0:g1, :], in_=tmpg[:, 0:n, :], func=AF.Exp,
                                                                          bias=lnks[:, 0:1]), r=[Btg, Bg], w=[Bg])
                    S.op("act", lambda e, pv=pv, g0=g0, g1=g1: e.activation(out=Bqo[:, g0:g1, :], in_=pv[:, :, 0:4], func=AF.Exp),
                         r=[Bpg[pj]], w=[Bg])
                    S.op("act", lambda e, pv=pv, g0=g0, g1=g1: e.activation(out=EBo[:, g0:g1, :], in_=pv[:, :, 4:8], func=AF.Exp),
                         r=[Bpg[pj]], w=[Bg])
                    S.op("dve", lambda e, g0=g0, g1=g1: e.tensor_tensor(out=A2o[:, g0:g1, :], in0=Ao[:, g0:g1, :], in1=EBo[:, g0:g1, :],
                                                                        op=ALU.mult), r=[Bg], w=[Bg])
                S.barrier()
            qT = self.sb(st, "ml_qT", [128, NTOK], BF16)
            kT = self.sb(st, "ml_kT", [128, NTOK], BF16)
            V = self.sb(st, "ml_V", [128, NT, 129], BF16)
            KTOK = self.sb(st, "ml_KTOK", [128, NT, 128], BF16)
            Bin = S.buf("ml_in")
            Bkt = S.buf("ml_ktok")
            Cn = [self.sb(st, f"ml_Cn{d}", [128, 129]) for d in range(2)]
            Cnb = [[self.sb(st, f"ml_Cnb{d}{j}", [128, 129], BF16) for j in range(2)] for d in range(2)]
            BCn = S.bufs("ml_Cn", 2)
            BCnb = [S.bufs(f"ml_Cnb{d}", 2) for d in range(2)]
            NW = 6
            W = [self.sb(st, f"ml_W{j}", [128, 128], BF16) for j in range(NW)]
            BW = S.bufs("ml_W", NW)
            v2 = [self.sb(st, f"ml_v2{j}", [128, 129], BF16) for j in range(NW)]
            Bv2 = S.bufs("ml_v2", NW)
            ho = [self.sb(st, f"ml_ho{j}", [128, 128]) for j in range(NW)]
            Bho = S.bufs("ml_ho", NW)
            sm = [self.sb(st, f"ml_sm{j}", [128, 4]) for j in range(NW)]
            Bsm = S.bufs("ml_sm", NW)
            NS, NKV, NN = 2, 2, 4
            order = [[NT_LAT, NT_LAT + 1] + list(range(NT_LAT)), [NT_LAT + 1, NT_LAT] + list(range(NT_LAT - 1, -1, -1))]
            wi = 0
            for h in range(2):
                S.dma("sp", qT[:], self.mq[h, :, :], w=[Bin])
                S.dma("act", kT[:], self.mk[h, :, :], w=[Bin])
                S.dma("sp", V[:], self.vda[h, :, :, :], w=[Bin])
                with contextlib.ExitStack() as stp:
                    ppbs = [self.ps(stp, f"ml_ppb{j}", [128, 1024], BF16) for j in range(2)]
                    Bppbs = S.bufs("ml_ppb", 2)
                    for gi, t0 in enumerate(range(0, NT, 8)):
                        n = min(8, NT - t0)
                        ppb, Bppb = ppbs[gi % 2], Bppbs[gi % 2]
                        for q in range(n):
                            t = t0 + q
                            S.op("pe", lambda e, t=t, q=q, ppb=ppb: e.transpose(
                                ppb[:, q * 128:(q + 1) * 128], kT[:, t * 128:(t + 1) * 128], self.ident_b[:]),
                                r=[Bin, self.B_const], w=[Bppb])
                        S.op("act", lambda e, t0=t0, n=n, ppb=ppb: e.activation(
                            out=KTOK[:, t0:t0 + n, :], in_=ppb[:, 0:n * 128].rearrange("p (t c) -> p t c", c=128), func=AF.Copy),
                            r=[Bppb], w=[Bkt])
                    S.barrier()
                stq = contextlib.ExitStack()
                ps_s = [self.ps(stq, f"ml_pss{j}", [128, 512]) for j in range(NS)]
                ps_kv = [self.ps(stq, f"ml_pkv{j}", [128, 512]) for j in range(NKV)]
                ps_n = [self.ps(stq, f"ml_pn{j}", [128, 512]) for j in range(NN)]
                Bps_s, Bps_kv, Bps_n = S.bufs("ml_pss", NS), S.bufs("ml_pkv", NKV), S.bufs("ml_pn", NN)
                for d in range(2):
                    S.op("pool", lambda e, d=d: e.memset(Cn[d][:], 0.0), w=[BCn[d]])
                    S.op("pool", lambda e, d=d: e.memset(Cnb[d][0][:], 0.0), w=[BCnb[d][0]])

                def stage1(step, d, wi):
                    t = order[d][step]
                    hd = d * 2 + h
                    j = wi % NW
                    p3 = wi % NS
                    tsl = slice(t * 128, (t + 1) * 128)
                    S.op("pe", lambda e: e.matmul(ps_s[p3][:, 0:128], kT[:, tsl], qT[:, tsl], start=True, stop=True),
                         r=[Bin], w=[Bps_s[p3]])
                    S.op("dve", lambda e: e.scalar_tensor_tensor(
                        out=W[j][:], in0=ps_s[p3][:, 0:128], scalar=Ao[:, t, hd:hd + 1], in1=tri[:, d, :],
                        op0=ALU.mult, op1=ALU.mult), r=[Bps_s[p3], Bg], w=[BW[j]])
                    S.op("act", lambda e: e.activation(
                        out=v2[j][:], in_=V[:, t, :], func=AF.Copy, scale=A2o[:, t, hd:hd + 1]),
                        r=[Bin, Bg], w=[Bv2[j]])

                def stage2(step, d, wi):
                    t = order[d][step]
                    hd = d * 2 + h
                    cur, nxt = step % 2, (step + 1) % 2
                    j = wi % NW
                    p3 = wi % NKV
                    p6 = wi % NN
                    tsl = slice(t * 128, (t + 1) * 128)
                    S.op("pe", lambda e: e.matmul(ps_kv[p3][:, 0:129], KTOK[:, t, :], v2[j][:], start=True, stop=True),
                         r=[Bkt, Bv2[j]], w=[Bps_kv[p3]])
                    S.op("pe", lambda e: e.matmul(ps_n[p6][:, 0:129], W[j][:], V[:, t, :], start=True, stop=False),
                         r=[BW[j], Bin], w=[Bps_n[p6]])
                    S.op("pe", lambda e: e.matmul(ps_n[p6][:, 0:129], qT[:, tsl], Cnb[d][cur][:], start=False, stop=True),
                         r=[Bin, BCnb[d][cur]], w=[Bps_n[p6]])
                    S.op("dve", lambda e: e.scalar_tensor_tensor(
                        out=Cn[d][:], in0=Cn[d][:], scalar=EBo[:, t, hd:hd + 1], in1=ps_kv[p3][:, 0:129],
                        op0=ALU.mult, op1=ALU.add), r=[BCn[d], Bps_kv[p3], Bg], w=[BCn[d]])
                    S.op("act", lambda e: e.activation(out=Cnb[d][nxt][:], in_=Cn[d][:], func=AF.Copy),
                         r=[BCn[d]], w=[BCnb[d][nxt]])

                def back(step, d, wi):
                    t = order[d][step]
                    hd = d * 2 + h
                    j = wi % NW
                    p6 = wi % NN
                    S.op("dve", lambda e: e.tensor_tensor(
                        out=sm[j][:, 0:1], in0=ps_n[p6][:, 128:129], in1=Bqo[:, t, hd:hd + 1], op=ALU.mult),
                        r=[Bps_n[p6], Bg], w=[Bsm[j]])
                    S.op("dve", lambda e: e.scalar_tensor_tensor(out=sm[j][:, 1:2], in0=sm[j][:, 0:1], scalar=-1.0,
                                                                 in1=sm[j][:, 0:1], op0=ALU.mult, op1=ALU.max),
                         r=[Bsm[j]], w=[Bsm[j]])
                    S.op("dve", lambda e: e.tensor_scalar(out=sm[j][:, 1:2], in0=sm[j][:, 1:2], scalar1=1.0, scalar2=None,
                                                          op0=ALU.max), r=[Bsm[j]], w=[Bsm[j]])
                    S.op("dve", lambda e: e.reciprocal(sm[j][:, 2:3], sm[j][:, 1:2]), r=[Bsm[j]], w=[Bsm[j]])
                    S.op("dve", lambda e: e.tensor_tensor(
                        out=sm[j][:, 3:4], in0=sm[j][:, 2:3], in1=Bqo[:, t, hd:hd + 1], op=ALU.mult), r=[Bsm[j], Bg], w=[Bsm[j]])
                    S.op("dve", lambda e: e.tensor_scalar(
                        out=ho[j][:], in0=ps_n[p6][:, 0:128], scalar1=sm[j][:, 3:4], scalar2=None, op0=ALU.mult),
                        r=[Bps_n[p6], Bsm[j]], w=[Bho[j]])
                    S.dma("sp", self.hfb[d, t * 128:(t + 1) * 128, h * 128:(h + 1) * 128], ho[j][:], r=[Bho[j]])

                items = []
                for step in range(NT):
                    for d in range(2):
                        items.append((step, d, wi))
                        wi += 1
                npair = len(items) // 2
                for k in range(npair + 2):
                    if k < npair:
                        stage1(*items[2 * k])
                        stage1(*items[2 * k + 1])
                    if 0 <= k - 1 < npair:
                        stage2(*items[2 * (k - 1)])
                        stage2(*items[2 * (k - 1) + 1])
                    if 0 <= k - 2 < npair:
                        back(*items[2 * (k - 2)])
                        back(*items[2 * (k - 2) + 1])
                S.barrier()
                stq.close()
            S.barrier()

    def mlstm_post(self, i):
        S = self.S
        with contextlib.ExitStack() as st:
            ngo = self.sb(st, "mp_ngo", [128, 256])
            Bw = S.buf("mp_w")
            S.dma("sp", ngo[:], self.od_ng[i, :, :].to_broadcast([128, 256]), w=[Bw])
            eps_t = self.sb(st, "mp_eps", [128, 1])
            S.op("pool", lambda e: e.memset(eps_t[:], LN_EPS), w=[Bw])
            NB = 3
            hf = [self.sb(st, f"mp_hf{j}", [128, 256]) for j in range(NB)]
            hb = [self.sb(st, f"mp_hb{j}", [128, 256]) for j in range(NB)]
            ogo = [self.sb(st, f"mp_ogo{j}", [128, 256]) for j in range(NB)]
            hs = [self.sb(st, f"mp_hs{j}", [128, 256]) for j in range(NB)]
            yo = [self.sb(st, f"mp_yo{j}", [128, 256]) for j in range(NB)]
            yob = [self.sb(st, f"mp_yob{j}", [128, 256], BF16) for j in range(NB)]
            stt = [self.sb(st, f"mp_st{j}", [128, 2, 12]) for j in range(NB)]
            Bhf, Bhb, Bogo, Bhs, Byo, Bst = (S.bufs(n, NB) for n in ("mp_hf", "mp_hb", "mp_ogo", "mp_hs", "mp_yo", "mp_st"))

            def post_a(t):
                j = t % NB
                tok0 = t * 128
                S.dma("sp", hf[j][:], self.hfb[0, tok0:tok0 + 128, 0:256], w=[Bhf[j]])
                S.dma("act", hb[j][:], self.hfb[1, tok0:tok0 + 128, 0:256], w=[Bhb[j]])
                S.dma("pool", ogo[j][:], self.og[tok0:tok0 + 128, :], w=[Bogo[j]])
                S.op("pool", lambda e, j=j: e.tensor_tensor(out=hs[j][:], in0=hf[j][:], in1=hb[j][:], op=ALU.add),
                     r=[Bhf[j], Bhb[j]], w=[Bhs[j]])
                S.op("pool", lambda e, j=j: e.tensor_tensor(out=ogo[j][:], in0=ogo[j][:], in1=ngo[:], op=ALU.mult),
                     r=[Bogo[j], Bw], w=[Bogo[j]])

            def post_b(t):
                j = t % NB
                for h in range(2):
                    S.op("dve", lambda e, j=j, h=h: e.bn_stats(stt[j][:, h, 0:6], hs[j][:, h * 128:(h + 1) * 128]),
                         r=[Bhs[j]], w=[Bst[j]])
                    S.op("dve", lambda e, j=j, h=h: e.bn_aggr(stt[j][:, h, 6:8], stt[j][:, h, 0:6]), r=[Bst[j]], w=[Bst[j]])
                S.op("act", lambda e, j=j: e.activation(out=stt[j][:, :, 8:9], in_=stt[j][:, :, 7:8], func=AF.Sqrt, scale=1.0,
                                                        bias=eps_t[:, 0:1]), r=[Bst[j], Bw], w=[Bst[j]])
                S.op("dve", lambda e, j=j: e.reciprocal(stt[j][:, :, 9:10], stt[j][:, :, 8:9]), r=[Bst[j]], w=[Bst[j]])
                for h in range(2):
                    S.op("dve", lambda e, j=j, h=h: e.tensor_scalar(
                        out=yo[j][:, h * 128:(h + 1) * 128], in0=hs[j][:, h * 128:(h + 1) * 128], scalar1=stt[j][:, h, 6:7],
                        scalar2=stt[j][:, h, 9:10], op0=ALU.subtract, op1=ALU.mult), r=[Bhs[j], Bst[j]], w=[Byo[j]])
                S.op("pool", lambda e, j=j: e.tensor_tensor(out=yob[j][:], in0=yo[j][:], in1=ogo[j][:], op=ALU.mult),
                     r=[Byo[j], Bogo[j]], w=[Byo[j]])
                k, r0 = self.yodd_loc(t)
                S.dma("sp", self.yodd_t[k].ap()[r0:r0 + 128, 0:256], yob[j][:], r=[Byo[j]])

            for t in range(NT):
                if t == 0:
                    post_a(0)
                if t + 1 < NT:
                    post_a(t + 1)
                post_b(t)
            S.barrier()

    def na_attention(self, i, last):
        S = self.S
        with contextlib.ExitStack() as st:
            QNe = self.sb(st, "na_Qe", [128, NTOK], BF16)
            QNo = self.sb(st, "na_Qo", [128, NTOK], BF16)
            KN = self.sb(st, "na_K", [128, NTOK], BF16)
            VN = self.sb(st, "na_V", [128, NT, 65], BF16)
            MB = self.sb(st, "na_MB", [128, NA_NVAR, 128], BF16)
            Bqk = S.buf("na_qk")
            Bv = S.buf("na_v")
            Bmb = S.buf("na_mb")
            stg = [self.sb(st, f"na_stg{j}", [128, 7, 128]) for j in range(2)]
            Bstg = S.bufs("na_stg", 2)
            NP = 3
            ptA = [self.sb(st, f"na_ptA{j}", [128, 512], BF16) for j in range(NP)]
            ptB = [self.sb(st, f"na_ptB{j}", [128, 384], BF16) for j in range(NP)]
            BptA, BptB = S.bufs("na_ptA", NP), S.bufs("na_ptB", NP)
            rr = [self.sb(st, f"na_rr{j}", [128, 2]) for j in range(NP)]
            to = [self.sb(st, f"na_to{j}", [128, 64], BF16) for j in range(NP)]
            Bto = S.bufs("na_to", NP)
            psA = [self.ps(st, f"na_psA{j}", [128, 512]) for j in range(2)]
            psB = [self.ps(st, f"na_psB{j}", [128, 512]) for j in range(2)]
            acc = [self.ps(st, f"na_acc{j}", [128, 512]) for j in range(2)]
            BpsA, BpsB, Bacc = S.bufs("na_psA", 2), S.bufs("na_psB", 2), S.bufs("na_acc", 2)
            it = 0
            qtiles = list(range(NT_LAT)) + ([] if (last and self.final_out) else [NT_LAT, NT_LAT + 1])
            if self.tile_filter is not None:
                qtiles = [t for t in qtiles if t in self.tile_filter]
            for h in range(4):
                lo = (h % 2) * 64
                if h == 0:
                    S.op("pool", lambda e: e.memset(QNe[64:128, :], 0.0), w=[Bqk])
                    S.op("pool", lambda e: e.memset(QNo[0:64, :], 0.0), w=[Bqk])
                if h % 2 == 0:
                    hp = h // 2
                    S.dma("sp", QNe[0:64, :], self.qda[hp, 0:64, :], w=[Bqk])
                    S.dma("act", QNo[64:128, :], self.qda[hp, 64:128, :], w=[Bqk])
                    S.dma("sp", KN[:], self.kda[hp, :, :], w=[Bqk])
                QN = QNe if h % 2 == 0 else QNo
                S.dma("act", VN[:], self.vm[h, :, :, :], w=[Bv])
                for v0 in range(0, NA_NVAR, 7):
                    n = min(7, NA_NVAR - v0)
                    sj = (v0 // 7) % 2
                    S.dma("sp", stg[sj][:, 0:n, :], self.od_nat[i, h, :, v0:v0 + n, :], w=[Bstg[sj]])
                    S.op("dve", lambda e, sj=sj, n=n, v0=v0: e.tensor_scalar(
                        out=MB[:, v0:v0 + n, :], in0=stg[sj][:, 0:n, :], scalar1=1.0 / NA_SCALE, scalar2=None, op0=ALU.mult),
                        r=[Bstg[sj]], w=[Bmb])
                for j in qtiles:
                    pj = it % 2
                    tj = it % NP
                    it += 1
                    qsl = slice(j * 128, (j + 1) * 128)
                    if j < NT_LAT:
                        slots = [(kt, var) for (kt, var) in NA_PLAN[j]] + [(NT_LAT, None), (NT_LAT + 1, None)]
                    else:
                        slots = [(NT_LAT, None), (NT_LAT + 1, None)]
                    nA = min(4, len(slots))
                    nB = len(slots) - nA
                    for si, (kt, var) in enumerate(slots):
                        if si < 4:
                            dst, Bd = psA[pj][:, si * 128:(si + 1) * 128], BpsA[pj]
                        else:
                            dst, Bd = psB[pj][:, (si - 4) * 128:(si - 3) * 128], BpsB[pj]
                        S.op("pe", lambda e, dst=dst, kt=kt, var=var, QN=QN: e.matmul(
                            dst, KN[:, kt * 128:(kt + 1) * 128], QN[:, qsl], start=True, stop=(var is None)),
                            r=[Bqk], w=[Bd])
                        if var is not None:
                            S.op("pe", lambda e, dst=dst, var=var: e.matmul(dst, self.ident_b[:], MB[:, var, :], start=False, stop=True),
                                 r=[Bmb, self.B_const], w=[Bd])
                    S.op("act", lambda e, pj=pj, tj=tj, nA=nA: e.activation(out=ptA[tj][:, 0:nA * 128], in_=psA[pj][:, 0:nA * 128],
                                                                            func=AF.Exp, scale=NA_SCALE), r=[BpsA[pj]], w=[BptA[tj]])
                    if nB > 0:
                        S.op("act", lambda e, pj=pj, tj=tj, nB=nB: e.activation(out=ptB[tj][:, 0:nB * 128], in_=psB[pj][:, 0:nB * 128],
                                                                                func=AF.Exp, scale=NA_SCALE), r=[BpsB[pj]], w=[BptB[tj]])
                    for si, (kt, var) in enumerate(slots):
                        if si < 4:
                            lhs, Bl = ptA[tj][:, si * 128:(si + 1) * 128], BptA[tj]
                        else:
                            lhs, Bl = ptB[tj][:, (si - 4) * 128:(si - 3) * 128], BptB[tj]
                        S.op("pe", lambda e, lhs=lhs, kt=kt, si=si, pj=pj: e.matmul(
                            acc[pj][:, 0:65], lhs, VN[:, kt, :], start=(si == 0), stop=(si == len(slots) - 1)),
                            r=[Bl, Bv], w=[Bacc[pj]])
                    S.op("dve", lambda e, pj=pj, tj=tj: e.reciprocal(rr[tj][:, 0:1], acc[pj][:, 64:65]), r=[Bacc[pj]], w=[Bto[tj]])
                    S.op("dve", lambda e, pj=pj, tj=tj: e.tensor_scalar(out=to[tj][:], in0=acc[pj][:, 0:64], scalar1=rr[tj][:, 0:1],
                                                                        scalar2=None, op0=ALU.mult), r=[Bacc[pj], Bto[tj]], w=[Bto[tj]])
                    kk, r0 = self.yodd_loc(j)
                    S.dma("pool", self.yodd_t[kk].ap()[r0:r0 + 128, 256 + h * 64:256 + (h + 1) * 64], to[tj][:], r=[Bto[tj]])
            S.barrier()


def _swap64(cols):
    return np.concatenate([cols[32:64], cols[0:32]])


def _rope_tables():
    t = np.arange(LAT)
    row = (t // GRID_W).astype(np.float32)
    col = (t % GRID_W).astype(np.float32)

    def tab(dim, nrows_pattern):
        n_freq = dim // 4
        freqs = (10000.0 ** (-np.arange(n_freq, dtype=np.float32) / n_freq)).astype(np.float32)
        ang = np.concatenate([row[:, None] * freqs, col[:, None] * freqs], axis=-1).astype(np.float32)
        cos = np.cos(ang).astype(np.float32).T
        sin = np.sin(ang).astype(np.float32).T
        half = dim // 2
        c = np.concatenate([cos, cos], 0)
        s = np.concatenate([-sin, sin], 0)
        c = np.concatenate([c, np.ones((dim, CTX), np.float32)], 1)
        s = np.concatenate([s, np.zeros((dim, CTX), np.float32)], 1)
        return c, s

    c64, s64 = tab(64, None)
    c32, s32 = tab(32, None)
    rope_da = np.stack([np.concatenate([c64, c64], 0), np.concatenate([s64, s64], 0)]).astype(np.float32)
    ml_c = np.ones((128, NTOK), np.float32)
    ml_s = np.zeros((128, NTOK), np.float32)
    ml_c[0:32] = c32
    ml_s[0:32] = s32
    ml_c[64:96] = c32
    ml_s[64:96] = s32
    rope_ml = np.stack([ml_c, ml_s]).astype(np.float32)
    return rope_da, rope_ml


def _even_layout(inp, par):
    ev_w_in, ev_b_in = inp["ev_w_in"], inp["ev_b_in"]
    hs = [2 * par, 2 * par + 1]
    ms = [4 * par + k for k in range(4)]
    o_q1, o_q2, o_k1, o_k2, o_v, o_cq, o_ckv, o_kr, o_g = 0, 256, 512, 768, 1024, 1536, 1792, 1920, 1952
    cols = []
    for (a, b) in ((o_q1, o_q2), (o_k1, o_k2)):
        main = []
        sw = []
        for h in hs:
            c1 = np.arange(a + h * 64, a + (h + 1) * 64)
            c2 = np.arange(b + h * 64, b + (h + 1) * 64)
            main += [c1, c2]
            sw += [_swap64(c1), _swap64(c2)]
        cols += main + sw
    kr = np.arange(o_kr, o_kr + 32)
    cols += [kr, np.concatenate([kr[16:], kr[:16]])]
    cols += [np.arange(o_v + h * 128, o_v + (h + 1) * 128) for h in hs]
    cols += [np.arange(o_cq, o_cq + 384), np.arange(o_g, o_g + 1024)]
    perm = np.concatenate(cols)
    assert perm.shape[0] == EV_COLS
    ev_w = np.ascontiguousarray(ev_w_in[:, :, perm])
    bp = ev_b_in[:, perm]
    bcol = np.zeros((2, 128, 10), np.float32)
    for g in range(8):
        bcol[:, :, g] = bp[:, g * 128:(g + 1) * 128]
    bcol[:, 0:32, 8] = bp[:, EKR:EKR + 32]
    bcol[:, 0:32, 9] = bp[:, EKR + 32:EKR + 64]
    brow = np.ascontiguousarray(bp[:, EV_:EV_ + 1664])[:, None, :]
    wuq = inp["mla_w_uq"]
    mainc = []
    swc = []
    for h in ms:
        c = np.arange(h * 96, h * 96 + 96)
        r = c[64:96]
        mainc.append(c)
        swc.append(np.concatenate([c[0:64], r[16:], r[:16]]))
    ev_wuq = np.ascontiguousarray(wuq[:, :, np.concatenate(mainc + swc)])
    wukv = inp["mla_w_ukv"]
    nope = np.concatenate([np.arange(h * 128, h * 128 + 64) for h in ms])
    vv = np.concatenate([np.arange(h * 128 + 64, h * 128 + 128) for h in ms])
    ev_wukv = np.ascontiguousarray(wukv[:, :, np.concatenate([nope, vv])])
    return dict(
        ev_w=ev_w, ev_bcol=bcol, ev_brow=np.ascontiguousarray(brow),
        ev_lam=np.ascontiguousarray(inp["da_lambda"].reshape(2, 1, 256)),
        ev_subg=np.ascontiguousarray(inp["da_subln_g"].reshape(2, 1, 128)),
        ev_qg=np.ascontiguousarray(inp["mla_q_norm_g"].reshape(2, 2, 128).transpose(0, 2, 1)),
        ev_kvg=np.ascontiguousarray(inp["mla_kv_norm_g"].reshape(2, 128, 1)),
        ev_wuq=ev_wuq, ev_wukv=ev_wukv, ev_wo=np.ascontiguousarray(inp["ev_w_out"]),
    )


def _odd_layout(inp, par):
    w, b = inp["od_w_in"], inp["od_b_in"]
    hs = [2 * par, 2 * par + 1]
    ns = [4 * par + k for k in range(4)]
    g0 = 2048
    cols = [np.arange(h * 128, (h + 1) * 128) for h in hs]
    cols += [np.arange(512 + h * 128, 512 + (h + 1) * 128) for h in hs]
    cols += [np.arange(2064 + n * 64, 2064 + (n + 1) * 64) for n in ns]
    cols += [np.arange(2576 + n * 64, 2576 + (n + 1) * 64) for n in ns]
    cols += [np.arange(1024 + h * 128, 1024 + (h + 1) * 128) for h in hs]
    cols += [np.arange(1536 + h * 128, 1536 + (h + 1) * 128) for h in hs]
    cols += [np.array([g0 + j * 4 + h for h in hs]) for j in (0, 2, 1, 3)]
    cols += [np.arange(3088 + n * 64, 3088 + (n + 1) * 64) for n in ns]
    cols += [np.arange(3600, 4624)]
    perm = np.concatenate(cols)
    assert perm.shape[0] == OD_COLS
    od_w = np.ascontiguousarray(w[:, :, perm])
    bp = b[:, perm]
    bcol = np.ascontiguousarray(bp[:, 0:1024].reshape(2, 8, 128).transpose(0, 2, 1))
    brow = np.ascontiguousarray(bp[:, 1024:])[:, None, :]
    ch = np.concatenate([np.arange(h * 128, (h + 1) * 128) for h in hs] + [np.arange(512 + h * 128, 512 + (h + 1) * 128) for h in hs])
    convw = np.ascontiguousarray(inp["ml_conv_w"][:, :, ch].reshape(2, 5, 4, 128).transpose(0, 3, 2, 1))
    convb = np.ascontiguousarray(inp["ml_conv_b"][:, ch].reshape(2, 4, 128).transpose(0, 2, 1))
    fb = np.ascontiguousarray(inp["ml_f_bias"][:, :, hs].reshape(2, 1, 4))
    ng = np.ascontiguousarray(np.concatenate([inp["ml_norm_g"][:, h * 128:(h + 1) * 128] for h in hs], 1).reshape(2, 1, 256))
    rpb = inp["na_rpb"][:, ns]
    nat = np.full((2, 4, 128, NA_NVAR, 128), NEG, np.float32)
    k = np.arange(128)
    krl, kc = k // 64, k % 64
    q = np.arange(128)
    qrl, qc = q // 64, q % 64
    cs = np.clip(qc - NA_COLS // 2, 0, GRID_W - NA_COLS)
    for (dk, o0, o1), vid in NA_VARIANTS.items():
        rel = (2 * dk + krl[:, None]) - qrl[None, :]
        off = np.where(qrl[None, :] == 0, o0, o1)
        valid_r = (rel >= off) & (rel <= off + NA_ROWS - 1)
        valid_c = (kc[:, None] >= cs[None, :]) & (kc[:, None] <= cs[None, :] + NA_COLS - 1)
        valid = valid_r & valid_c
        ridx = np.clip(rel + NA_ROWS - 1, 0, 2 * NA_ROWS - 2)
        cidx = np.clip(kc[:, None] - qc[None, :] + NA_COLS - 1, 0, 2 * NA_COLS - 2)
        tab = rpb[:, :, ridx, cidx]
        nat[:, :, :, vid, :] = np.where(valid[None, None], tab, np.float32(NEG))
    return dict(od_w=od_w, od_bcol=bcol, od_brow=np.ascontiguousarray(brow), od_convw=convw, od_convb=convb, od_fb=fb,
                od_ng=ng, od_nat=nat, od_wo=np.ascontiguousarray(inp["od_w_out"]))


def make_in_maps(inp, batches):
    inp = {k: np.asarray(v, dtype=np.float32) for k, v in inp.items()}
    rope_da, rope_ml = _rope_tables()
    common = dict(
        ada_w=inp["ada_w"], ada_b=inp["ada_b"], ln_g=inp["ln_g"], ln_b=inp["ln_b"],
        ident=np.eye(128, dtype=np.float32),
        sel=np.concatenate([np.stack([np.ones(128), np.zeros(128)]), np.stack([np.zeros(128), np.ones(128)])], 1).astype(np.float32),
        rope_da=rope_da, rope_ml=rope_ml,
    )
    tri = np.stack([np.triu(np.ones((128, 128), np.float32)), np.tril(np.ones((128, 128), np.float32)),
                    np.ones((128, 128), np.float32)], 1)
    common["tri"] = np.ascontiguousarray(tri)
    ev = [dict(_even_layout(inp, p), **_odd_layout(inp, p)) for p in (0, 1)]
    maps = []
    for b in batches:
        m = dict(common)
        m.update(ev[len(maps) % 2])
        m["x_in"] = np.ascontiguousarray(np.concatenate([inp["x"][b], inp["ctx"][b]], 0))
        cc = np.stack([inp["c"][b], inp["c_ctx"]], -1)
        m["cc"] = np.ascontiguousarray(cc.reshape(8, 128, 2).transpose(1, 0, 2))
        par = len(maps) % 2
        ps = np.zeros((128, 2), np.float32)
        ps[:, par] = 1.0
        m["psel"] = ps
        maps.append(m)
    return maps


_PROG_CACHE = {}


def kernel(**inputs):
    batches = [c // 2 for c in range(8)]
    in_maps = make_in_maps(inputs, batches)
    prog = Prog()
    nc = prog.build()
    res = run_bass_kernel_spmd(nc, in_maps, core_ids=list(range(8)))
    out = np.stack([res.results[2 * b]["y"] for b in range(4)], 0)
    return out.astype(np.float32)
```

```python
import contextlib
import math
import numpy as np
import concourse.bass as bass
import concourse.mybir as mybir
from concourse.bass_utils import run_bass_kernel_spmd

F32 = mybir.dt.float32
BF16 = mybir.dt.bfloat16
AF = mybir.ActivationFunctionType
ALU = mybir.AluOpType
AX = mybir.AxisListType

D = 1024
LAT = 8192
CTX = 256
NTOK = LAT + CTX
NT = NTOK // 128
NT_LAT = LAT // 128
DEPTH = 4
GRID_W = 64
ALPHA = (2.0 * DEPTH) ** 0.25
LN_EPS = 1e-5
RMS_EPS = 1e-6
DA_SCALE = 64 ** -0.5
MLA_SCALE = 96 ** -0.5
NA_SCALE = 64 ** -0.5
EV_COLS = 2752
EQ, EK, EKR, EV_, EC, EG = 0, 512, 1024, 1088, 1344, 1728
OD_COLS = 2824
OQK, OQN, OKN, OV, OO, OGT, OVN, OG = 0, 512, 768, 1024, 1280, 1536, 1544, 1800
NA_ROWS, NA_COLS, GRID_H = 8, 16, 128
NEG = -30000.0


def na_plan():
    variants = {}
    plan = []
    for j in range(NT_LAT):
        r0, r1 = 2 * j, 2 * j + 1
        rs0 = min(max(r0 - NA_ROWS // 2, 0), GRID_H - NA_ROWS)
        rs1 = min(max(r1 - NA_ROWS // 2, 0), GRID_H - NA_ROWS)
        lst = []
        for kt in range(rs0 // 2, (rs1 + NA_ROWS - 1) // 2 + 1):
            key = (kt - j, rs0 - r0, rs1 - r1)
            if key not in variants:
                variants[key] = len(variants)
            lst.append((kt, variants[key]))
        plan.append(lst)
    return plan, variants


NA_PLAN, NA_VARIANTS = na_plan()
NA_NVAR = len(NA_VARIANTS)


class Tok:
    __slots__ = ("sem", "key", "val", "eng")

    def __init__(self, sem, key, val, eng):
        self.sem, self.key, self.val, self.eng = sem, key, val, eng


class Buf:
    __slots__ = ("name", "last_w", "readers", "sem", "key", "cnt")

    def __init__(self, name):
        self.name = name
        self.last_w = None
        self.readers = {}
        self.sem = None
        self.key = None
        self.cnt = 0


class Eng:
    def __init__(self, name, h, sem):
        self.name, self.h, self.sem = name, h, sem
        self.key = "E_" + name
        self.cnt = 0
        self.waited = {}


class Sync:
    def __init__(self, nc, stack):
        self.nc = nc
        self.stack = stack
        self.E = {}
        for name, h in (("pe", nc.tensor), ("act", nc.scalar), ("dve", nc.vector),
                        ("pool", nc.gpsimd), ("sp", nc.sync)):
            sem = stack.enter_context(nc.semaphore("s_" + name))
            self.E[name] = Eng(name, h, sem)
        self.dma_bufs = []
        self.free_sems = []
        self.replica_groups = [[0, 1], [2, 3], [4, 5], [6, 7]]
        self.nsem = 0
        self.nwait = 0
        self.nins = 0

    def buf(self, name):
        return Buf(name)

    def bufs(self, name, n):
        return [Buf(f"{name}{i}") for i in range(n)]

    def _deps(self, eng, r, w):
        raw = []
        oth = []
        for b in r:
            if b.last_w is not None:
                raw.append(b.last_w)
        for b in w:
            if b.last_w is not None:
                oth.append(b.last_w)
            oth.extend(b.readers.values())
        toks = []
        for t in raw:
            if t.eng is eng and eng.name == "pe":
                continue
            toks.append(t)
        for t in oth:
            if t.eng is eng:
                continue
            toks.append(t)
        return toks

    def _wait(self, eng, toks):
        for t in toks:
            if eng.waited.get(t.key, 0) >= t.val:
                continue
            eng.h.wait_ge(t.sem, t.val)
            eng.waited[t.key] = t.val
            self.nwait += 1

    def op(self, en, fn, r=(), w=()):
        eng = self.E[en]
        self._wait(eng, self._deps(eng, r, w))
        ins = fn(eng.h)
        ins.then_inc(eng.sem, 1)
        eng.cnt += 1
        self.nins += 1
        tok = Tok(eng.sem, eng.key, eng.cnt, eng)
        for b in r:
            b.readers[tok.key] = tok
        for b in w:
            b.last_w = tok
            b.readers = {}
        return tok

    def dma(self, q, out, in_, r=(), w=(), sb=None):
        eng = self.E[q]
        self._wait(eng, self._deps(eng, r, w))
        if sb is None:
            sb = w[0] if w else r[0]
        if sb.sem is None:
            if self.free_sems:
                sb.sem, sb.key, sb.cnt = self.free_sems.pop()
            else:
                sb.sem = self.stack.enter_context(self.nc.semaphore(f"d{self.nsem}"))
                sb.key = f"D{self.nsem}"
                sb.cnt = 0
                self.nsem += 1
            self.dma_bufs.append(sb)
        ins = eng.h.dma_start(out=out, in_=in_)
        ins.then_inc(sb.sem, 16)
        sb.cnt += 16
        self.nins += 1
        tok = Tok(sb.sem, sb.key, sb.cnt, None)
        for b in r:
            b.readers[tok.key] = tok
        for b in w:
            b.last_w = tok
            b.readers = {}
        return tok

    def allgather_pairs(self, src_ts, dst_ts):
        self.barrier()
        eng = self.E["pool"]
        if getattr(self, "cc_sem", None) is None:
            self.cc_sem = self.stack.enter_context(self.nc.semaphore("cc_sem"))
            self.cc_cnt = 0
        for src_t, dst_t in zip(src_ts, dst_ts):
            ins = eng.h.collective_compute("AllGather", ALU.bypass, replica_groups=self.replica_groups,
                                           ins=[src_t.ap().opt()], outs=[dst_t.ap().opt()])
            ins.then_inc(self.cc_sem)
            self.cc_cnt += 1
            self.nins += 1
        tok = Tok(self.cc_sem, "CC", self.cc_cnt, None)
        for e in self.E.values():
            self._wait(e, [tok])

    def barrier(self, engines=("pe", "act", "dve", "pool", "sp")):
        toks = [Tok(e.sem, e.key, e.cnt, e) for e in self.E.values() if e.cnt > 0]
        toks += [Tok(b.sem, b.key, b.cnt, None) for b in self.dma_bufs if b.cnt > 0]
        for en in engines:
            eng = self.E[en]
            self._wait(eng, [t for t in toks if t.eng is not eng])
        for b in self.dma_bufs:
            self.free_sems.append((b.sem, b.key, b.cnt))
            b.sem = None
            b.last_w = None
            b.readers = {}
        self.dma_bufs = []


class Prog:
    def __init__(self, layers=(0, 1, 2, 3), final_out=True, qb_filter=None, tile_filter=None, n_cores=8):
        self.n_cores = n_cores
        self.layers = tuple(layers)
        self.qb_filter = qb_filter
        self.tile_filter = tile_filter
        self.nc = bass.Bass("TRN2", target_bir_lowering=False)
        self.final_out = final_out

    def din(self, name, shape, dt=F32):
        return self.nc.dram_tensor(name, list(shape), dt, kind="ExternalInput").ap()

    def dout(self, name, shape, dt=F32):
        return self.nc.dram_tensor(name, list(shape), dt, kind="ExternalOutput").ap()

    def dscr(self, name, shape, dt=F32):
        return self.nc.dram_tensor(name, list(shape), dt, kind="Internal").ap()

    def sb(self, st, name, shape, dt=F32):
        self._uid = getattr(self, "_uid", 0) + 1
        return st.enter_context(self.nc.sbuf_tensor(f"s{self._uid}_{name}", list(shape), dt))

    def ps(self, st, name, shape, dt=F32):
        self._uid = getattr(self, "_uid", 0) + 1
        return st.enter_context(self.nc.psum_tensor(f"p{self._uid}_{name}", list(shape), dt))

    def build(self):
        nc = self.nc
        with contextlib.ExitStack() as st:
            self.S = Sync(nc, st)
            self.S.replica_groups = [[2 * k, 2 * k + 1] for k in range(self.n_cores // 2)]
            self.declare()
            self.setup_consts(st)
            nl = len(self.layers)
            for li, l in enumerate(self.layers):
                src = self.x_in if li == 0 else self.xbuf[(li - 1) % 2]
                last = li == nl - 1
                dst = self.y_out if last else self.xbuf[li % 2]
                if l % 2 == 0:
                    self.even_layer(l, src, dst, last)
                else:
                    self.odd_layer(l, src, dst, last)
            self.S.barrier()
        return nc

    def declare(self):
        self.x_in = self.din("x_in", [NTOK, D])
        self.cc = self.din("cc", [128, 8, 2])
        self.ada_w = self.din("ada_w", [DEPTH, D, 3 * D])
        self.ada_b = self.din("ada_b", [DEPTH, 3 * D])
        self.ln_g = self.din("ln_g", [DEPTH, D])
        self.ln_b = self.din("ln_b", [DEPTH, D])
        self.ident_in = self.din("ident", [128, 128])
        self.sel_in = self.din("sel", [2, 256])
        self.ev_w = self.din("ev_w", [2, D, EV_COLS])
        self.ev_bcol = self.din("ev_bcol", [2, 128, 10])
        self.ev_brow = self.din("ev_brow", [2, 1, 1664])
        self.ev_lam = self.din("ev_lam", [2, 1, 256])
        self.ev_subg = self.din("ev_subg", [2, 1, 128])
        self.ev_qg = self.din("ev_qg", [2, 128, 2])
        self.ev_kvg = self.din("ev_kvg", [2, 128, 1])
        self.ev_wuq = self.din("ev_wuq", [2, 256, 768])
        self.ev_wukv = self.din("ev_wukv", [2, 128, 512])
        self.ev_wo = self.din("ev_wo", [2, D, D])
        self.rope_da = self.din("rope_da", [2, 128, NTOK])
        self.rope_ml = self.din("rope_ml", [2, 128, NTOK])
        self.od_w = self.din("od_w", [2, D, OD_COLS])
        self.od_bcol = self.din("od_bcol", [2, 128, 8])
        self.od_brow = self.din("od_brow", [2, 1, 1800])
        self.od_convw = self.din("od_convw", [2, 128, 4, 5])
        self.od_convb = self.din("od_convb", [2, 128, 4])
        self.od_fb = self.din("od_fb", [2, 1, 4])
        self.od_ng = self.din("od_ng", [2, 1, 256])
        self.od_nat = self.din("od_nat", [2, 4, 128, NA_NVAR, 128])
        self.od_wo = self.din("od_wo", [2, D, D])
        self.tri = self.din("tri", [128, 3, 128])
        self.qkpre = self.dscr("qkpre", [4, 128, NTOK])
        self.mq = self.dscr("mq", [2, 128, NTOK], BF16)
        self.mk = self.dscr("mk", [2, 128, NTOK], BF16)
        self.og = self.dscr("og", [NTOK, 256])
        self.gates = self.dscr("gates", [128, NT, 8])
        self.hfb = self.dscr("hfb", [2, NTOK, 256])
        n_last_tok = LAT if self.final_out else NTOK
        self.y_out = self.dout("y", [n_last_tok, D])
        self.xbuf = [self.dscr("xbuf0", [NTOK, D]), self.dscr("xbuf1", [NTOK, D])]
        self.qda = self.dscr("qda", [2, 128, NTOK], BF16)
        self.kda = self.dscr("kda", [2, 128, NTOK], BF16)
        self.vda = self.dscr("vda", [2, 128, NT, 129], BF16)
        self.qm = self.dscr("qm", [4, 96, NTOK], BF16)
        self.kmn = self.dscr("kmn", [4, 64, NTOK], BF16)
        self.krt = self.dscr("krt", [32, NTOK], BF16)
        self.vm = self.dscr("vm", [4, 128, NT, 65], BF16)
        self.gate = self.dscr("gate", [NTOK, D], BF16)
        self.ybuf = self.dscr("ybuf", [NTOK, D], BF16)
        self.psel = self.din("psel", [128, 2])
        self.yodd_rows = [2048, 2048, 2048, 2048, 256]
        self.yodd_t = [self.nc.dram_tensor(f"yodd{k}", [r, 512], BF16) for k, r in enumerate(self.yodd_rows)]
        self.yoddg_t = [self.nc.dram_tensor(f"yoddg{k}", [2 * r, 512], BF16) for k, r in enumerate(self.yodd_rows)]
        self.yown_t = [self.nc.dram_tensor(f"yown{k}", [1024, D], BF16) for k in range(4)]
        self.ygath_t = [self.nc.dram_tensor(f"ygath{k}", [2048, D], BF16) for k in range(4)]

    def setup_consts(self, st):
        S = self.S
        self.ident_f = self.sb(st, "ident_f", [128, 128])
        self.ident_b = self.sb(st, "ident_b", [128, 128], BF16)
        self.sel = self.sb(st, "sel", [2, 256])
        self.ccs = self.sb(st, "ccs", [128, 8, 2])
        self.zeros_b = self.sb(st, "zeros_b", [128, 128], BF16)
        self.ones_b = self.sb(st, "ones_b", [1, 128], BF16)
        self.zeros_w = self.sb(st, "zeros_w", [128, 512], BF16)
        self.B_const = S.buf("consts")
        b = self.B_const
        S.dma("sp", self.ident_f[:], self.ident_in[:, :], w=[b])
        S.dma("sp", self.sel[:], self.sel_in[:, :], w=[b])
        S.dma("sp", self.ccs[:], self.cc[:, :, :], w=[b])
        self.pselt = self.sb(st, "pselt", [128, 4])
        S.dma("sp", self.pselt[:, 0:2], self.psel[:, :], w=[b])
        S.op("dve", lambda e: e.tensor_scalar(out=self.pselt[:, 2:4], in0=self.pselt[:, 0:2], scalar1=1.0 / NA_SCALE, scalar2=None,
                                              op0=ALU.mult), r=[b], w=[b])
        S.op("dve", lambda e: e.tensor_copy(self.ident_b[:], self.ident_f[:]), r=[b], w=[b])
        S.op("dve", lambda e: e.memset(self.zeros_b[:], 0.0), w=[b])
        S.op("dve", lambda e: e.memset(self.ones_b[:], 1.0), w=[b])
        S.op("dve", lambda e: e.memset(self.zeros_w[:], 0.0), w=[b])
        S.op("act", lambda e: e.activation(out=self.ccs[:], in_=self.ccs[:], func=AF.Silu), r=[b], w=[b])
        self.sc1 = self.sb(st, "sc1", [128, 8, 2])
        self.sh = self.sb(st, "sh", [128, 8, 2])
        self.g_l = self.sb(st, "g_l", [128, D])
        self.g_c = self.sb(st, "g_c", [128, D])
        self.lng = self.sb(st, "lng", [128, D])
        self.lnb = self.sb(st, "lnb", [128, D])
        self.B_mod = S.buf("mod")

    def adaln(self, l):
        S, nc = self.S, self.nc
        S.barrier()
        with contextlib.ExitStack() as st:
            wt = [self.sb(st, f"adaw{i}", [128, 3 * D]) for i in range(2)]
            Bw = S.bufs("adaw", 2)
            mrow = self.sb(st, "mrow", [2, 3 * D])
            brow = self.sb(st, "adab", [2, 3 * D])
            Bm = S.buf("mrow")
            Bb = S.buf("adab")
            pm = [self.ps(st, f"pm{i}", [128, 512]) for i in range(8)]
            Bp = S.bufs("pm", 8)
            S.dma("sp", brow[:], self.ada_b[l:l + 1, :].to_broadcast([2, 3 * D]), w=[Bb])
            S.dma("sp", self.lng[:], self.ln_g[l:l + 1, :].to_broadcast([128, D]), w=[self.B_mod])
            S.dma("sp", self.lnb[:], self.ln_b[l:l + 1, :].to_broadcast([128, D]), w=[self.B_mod])
            for k in range(8):
                S.dma("sp", wt[k % 2][:], self.ada_w[l, k * 128:(k + 1) * 128, :], w=[Bw[k % 2]])
                for n in range(6):
                    S.op("pe", lambda e, k=k, n=n: e.matmul(
                        pm[n][0:2, :], self.ccs[:, k, :], wt[k % 2][:, n * 512:(n + 1) * 512],
                        start=(k == 0), stop=(k == 7)), r=[Bw[k % 2], self.B_const], w=[Bp[n]])
            for n in range(6):
                S.op("dve", lambda e, n=n: e.tensor_tensor(
                    out=mrow[:, n * 512:(n + 1) * 512], in0=pm[n][0:2, :], in1=brow[:, n * 512:(n + 1) * 512],
                    op=ALU.add), r=[Bp[n], Bb], w=[Bm])
            for j in range(8):
                S.op("pe", lambda e, j=j: e.transpose(
                    pm[6][:, 2 * j:2 * j + 2], mrow[0:2, j * 128:(j + 1) * 128], self.ident_f[0:2, 0:2]),
                    r=[Bm, self.B_const], w=[Bp[6]])
                S.op("pe", lambda e, j=j: e.transpose(
                    pm[7][:, 2 * j:2 * j + 2], mrow[0:2, D + j * 128:D + (j + 1) * 128], self.ident_f[0:2, 0:2]),
                    r=[Bm, self.B_const], w=[Bp[7]])
            S.op("dve", lambda e: e.tensor_copy(self.sh[:].rearrange("p a b -> p (a b)"), pm[6][:, 0:16]),
                 r=[Bp[6]], w=[self.B_mod])
            S.op("dve", lambda e: e.tensor_scalar(
                out=self.sc1[:].rearrange("p a b -> p (a b)"), in0=pm[7][:, 0:16], scalar1=1.0, scalar2=None,
                op0=ALU.add), r=[Bp[7]], w=[self.B_mod])
            for n in range(2):
                S.op("pe", lambda e, n=n: e.matmul(
                    pm[n][:, :], self.sel[:, 0:128], mrow[0:2, 2 * D + n * 512:2 * D + (n + 1) * 512],
                    start=True, stop=True), r=[Bm, self.B_const], w=[Bp[n]])
                S.op("pe", lambda e, n=n: e.matmul(
                    pm[2 + n][:, :], self.sel[:, 128:256], mrow[0:2, 2 * D + n * 512:2 * D + (n + 1) * 512],
                    start=True, stop=True), r=[Bm, self.B_const], w=[Bp[2 + n]])
                S.op("dve", lambda e, n=n: e.tensor_copy(self.g_l[:, n * 512:(n + 1) * 512], pm[n][:, :]),
                     r=[Bp[n]], w=[self.B_mod])
                S.op("dve", lambda e, n=n: e.tensor_copy(self.g_c[:, n * 512:(n + 1) * 512], pm[2 + n][:, :]),
                     r=[Bp[2 + n]], w=[self.B_mod])
            S.barrier()

    def load_cast(self, st_w, dst_fn, src_fn, nparts, ncols, pieces, Bdst, scale_fn=None, name="lc"):
        S = self.S
        with contextlib.ExitStack() as st:
            stg = [self.sb(st, f"{name}_stg{i}", [nparts, ncols]) for i in range(2)]
            Bs = S.bufs(name + "_stg", 2)
            for i, (dst, src, sc) in enumerate(pieces):
                j = i % 2
                n = src.shape[-1]
                S.dma("sp", stg[j][:, 0:n], src, w=[Bs[j]])
                en = "dve" if i % 2 == 0 else "pool"
                if sc is None:
                    S.op(en, lambda e, dst=dst, j=j, n=n: e.tensor_copy(dst, stg[j][:, 0:n]), r=[Bs[j]], w=[Bdst])
                else:
                    S.op(en, lambda e, dst=dst, j=j, n=n, sc=sc: e.tensor_scalar(
                        out=dst, in0=stg[j][:, 0:n], scalar1=sc, scalar2=None, op0=ALU.mult),
                        r=[Bs[j], Bdst], w=[Bdst])
            S.barrier()

    def even_layer(self, l, src, dst, last):
        i = l // 2
        self.adaln(l)
        self.even_project(i, src)
        self.da_attention(i, l)
        self.mla_attention(i)
        self.S.allgather_pairs(self.yodd_t, self.yoddg_t)
        self.out_stage(self.ev_wo[i], src, dst, last, ysplit="odd")

    def even_project(self, i, src):
        S, nc = self.S, self.nc
        with contextlib.ExitStack() as st:
            wb = self.sb(st, "ev_wb", [128, 8, EV_COLS], BF16)
            wuq = self.sb(st, "ev_wuqb", [128, 2, 768], BF16)
            wukv = self.sb(st, "ev_wukvb", [128, 512], BF16)
            bcol = self.sb(st, "ev_bcol", [128, 10])
            brow_f = self.sb(st, "ev_brow_f", [1, 1664])
            brow = self.sb(st, "ev_brow", [1, 1664], BF16)
            qg = self.sb(st, "ev_qg", [128, 2])
            kvg = self.sb(st, "ev_kvg", [128, 1])
            Bw = S.buf("ev_w")
            S.dma("sp", bcol[:], self.ev_bcol[i, :, :], w=[Bw])
            S.dma("sp", brow_f[:], self.ev_brow[i, :, :], w=[Bw])
            S.dma("sp", qg[:], self.ev_qg[i, :, :], w=[Bw])
            S.dma("sp", kvg[:], self.ev_kvg[i, :, :], w=[Bw])
            S.op("dve", lambda e: e.tensor_copy(brow[:], brow_f[:]), r=[Bw], w=[Bw])
            pieces = []
            for k in range(8):
                for c in range(2):
                    pieces.append((wb[:, k, c * 1376:(c + 1) * 1376],
                                   self.ev_w[i, k * 128:(k + 1) * 128, c * 1376:(c + 1) * 1376], None))
            self.load_cast(st, None, None, 128, 1536, pieces, Bw, name="evw")
            pieces = [(wuq[:, rc, :], self.ev_wuq[i, rc * 128:(rc + 1) * 128, :], qg[:, rc:rc + 1]) for rc in range(2)]
            pieces.append((wukv[:, :], self.ev_wukv[i, :, :], kvg[:, 0:1]))
            self.load_cast(st, None, None, 128, 1536, pieces, Bw, name="evw2")

            NXS = 6
            xt = [self.sb(st, f"xt{j}", [128, D]) for j in range(NXS)]
            Bx = S.bufs("xt", NXS)
            hT = [self.sb(st, f"hT{j}", [128, 8, 512], BF16) for j in range(2)]
            Bh = S.bufs("hT", 2)
            cosd = [self.sb(st, f"cosd{j}", [128, 512]) for j in range(2)]
            sind = [self.sb(st, f"sind{j}", [128, 512]) for j in range(2)]
            cosm = [self.sb(st, f"cosm{j}", [128, 512]) for j in range(2)]
            sinm = [self.sb(st, f"sinm{j}", [128, 512]) for j in range(2)]
            Brt = S.bufs("ropet", 2)
            t1 = [self.sb(st, f"t1_{j}", [128, 512]) for j in range(2)]
            t2 = [self.sb(st, f"t2_{j}", [128, 512]) for j in range(2)]
            Bt1 = S.bufs("t1", 2)
            Bt2 = S.bufs("t2", 2)
            fo = [self.sb(st, f"fo{j}", [128, 512], BF16) for j in range(4)]
            Bfo = S.bufs("fo", 4)
            va = [self.sb(st, f"va{j}", [128, 2, 129], BF16) for j in range(2)]
            Bva = S.bufs("va", 2)
            vma = [self.sb(st, f"vma{j}", [128, 4, 65], BF16) for j in range(2)]
            Bvma = S.bufs("vma", 2)
            cq = [self.sb(st, f"cq{j}", [128, 384]) for j in range(2)]
            Bcq = S.bufs("cq", 2)
            cqn = [self.sb(st, f"cqn{j}", [128, 384], BF16) for j in range(2)]
            Bcqn = S.bufs("cqn", 2)
            stat = [self.sb(st, f"stat{j}", [128, 8]) for j in range(2)]
            Bstat = S.bufs("stat", 2)
            junk = self.sb(st, "junk", [128, 256])
            Bjunk = S.buf("junk")
            cT = [self.sb(st, f"cT{j}", [128, 3, 512], BF16) for j in range(2)]
            BcT = S.bufs("cT", 2)
            gt = [self.sb(st, f"gt{j}", [128, D], BF16) for j in range(2)]
            Bgt = S.bufs("gt", 2)
            pp = [self.ps(st, f"pp{j}", [128, 512]) for j in range(7)]
            ppb = self.ps(st, "ppb", [128, 1024], BF16)
            Bpp = S.bufs("pp", 7)
            Bppb = S.buf("ppb")
            for j in range(2):
                S.op("pool", lambda e, j=j: e.memset(va[j][:], 1.0), w=[Bva[j]])
                S.op("pool", lambda e, j=j: e.memset(vma[j][:], 1.0), w=[Bvma[j]])
            eps_t = self.sb(st, "eps_t", [128, 1])
            S.op("pool", lambda e: e.memset(eps_t[:], RMS_EPS), w=[Bw])

            self._pp_i = 0

            def next_pp():
                j = self._pp_i % 7
                self._pp_i += 1
                return pp[j], Bpp[j]

            nblk = (NTOK + 511) // 512
            self._xi = 0
            self._foi = 0
            self._evac = 0

            def stage_a(blk):
                t0 = blk * 512
                ntok = min(512, NTOK - t0)
                nsub = ntok // 128
                m = 0 if t0 < LAT else 1
                hb = blk % 2
                S.dma("sp", cosd[hb][:, :ntok], self.rope_da[0, :, t0:t0 + ntok], w=[Brt[hb]])
                S.dma("sp", sind[hb][:, :ntok], self.rope_da[1, :, t0:t0 + ntok], w=[Brt[hb]])
                S.dma("sp", cosm[hb][:, :ntok], self.rope_ml[0, :, t0:t0 + ntok], w=[Brt[hb]])
                S.dma("sp", sinm[hb][:, :ntok], self.rope_ml[1, :, t0:t0 + ntok], w=[Brt[hb]])
                for s in range(nsub):
                    xj = self._xi % NXS
                    self._xi += 1
                    S.dma("sp", xt[xj][:], src[t0 + s * 128:t0 + (s + 1) * 128, :], w=[Bx[xj]])
                    for half in range(2):
                        p, Bp = next_pp()
                        for q in range(4):
                            dc = half * 4 + q
                            S.op("pe", lambda e, p=p, q=q, dc=dc, xj=xj: e.transpose(
                                p[:, q * 128:(q + 1) * 128], xt[xj][:, dc * 128:(dc + 1) * 128], self.ident_f[:]),
                                r=[Bx[xj], self.B_const], w=[Bp])
                        for q in range(4):
                            dc = half * 4 + q
                            if self._evac % 2 == 0:
                                S.op("act", lambda e, p=p, q=q, dc=dc, s=s, m=m: e.activation(
                                    out=hT[hb][:, dc, s * 128:(s + 1) * 128], in_=p[:, q * 128:(q + 1) * 128],
                                    func=AF.Identity, scale=self.sc1[:, dc, m:m + 1], bias=self.sh[:, dc, m:m + 1]),
                                    r=[Bp, self.B_mod], w=[Bh[hb]])
                            else:
                                S.op("dve", lambda e, p=p, q=q, dc=dc, s=s, m=m: e.tensor_scalar(
                                    out=hT[hb][:, dc, s * 128:(s + 1) * 128], in0=p[:, q * 128:(q + 1) * 128],
                                    scalar1=self.sc1[:, dc, m:m + 1], scalar2=self.sh[:, dc, m:m + 1],
                                    op0=ALU.mult, op1=ALU.add), r=[Bp, self.B_mod], w=[Bh[hb]])
                            self._evac += 1

            def stage_b(blk):
                t0 = blk * 512
                ntok = min(512, NTOK - t0)
                nsub = ntok // 128
                m = 0 if t0 < LAT else 1
                hb = blk % 2
                foi = self._foi
                for grp, base, dstT, bofs in ((0, EQ, self.qda, 0), (1, EK, self.kda, 4)):
                    for h in range(2):
                        pa, Ba = next_pp()
                        pb, Bb = next_pp()
                        for k in range(8):
                            S.op("pe", lambda e, pa=pa, k=k, c0=base + h * 128: e.matmul(
                                pa[:, :ntok], wb[:, k, c0:c0 + 128], hT[hb][:, k, :ntok], start=(k == 0), stop=(k == 7)),
                                r=[Bw, Bh[hb]], w=[Ba])
                        for k in range(8):
                            S.op("pe", lambda e, pb=pb, k=k, c0=base + 256 + h * 128: e.matmul(
                                pb[:, :ntok], wb[:, k, c0:c0 + 128], hT[hb][:, k, :ntok], start=(k == 0), stop=(k == 7)),
                                r=[Bw, Bh[hb]], w=[Bb])
                        tj = (grp * 2 + h) % 2
                        S.op("dve", lambda e, pa=pa, tj=tj, c=bofs + h: e.scalar_tensor_tensor(
                            out=t1[tj][:, :ntok], in0=pa[:, :ntok], scalar=bcol[:, c:c + 1], in1=cosd[hb][:, :ntok],
                            op0=ALU.add, op1=ALU.mult), r=[Ba, Brt[hb], Bw], w=[Bt1[tj]])
                        S.op("dve", lambda e, pb=pb, tj=tj, c=bofs + 2 + h: e.scalar_tensor_tensor(
                            out=t2[tj][:, :ntok], in0=pb[:, :ntok], scalar=bcol[:, c:c + 1], in1=sind[hb][:, :ntok],
                            op0=ALU.add, op1=ALU.mult), r=[Bb, Brt[hb], Bw], w=[Bt2[tj]])
                        fj = foi % 4
                        foi += 1
                        S.op("pool", lambda e, tj=tj, fj=fj: e.tensor_tensor(
                            out=fo[fj][:, :ntok], in0=t1[tj][:, :ntok], in1=t2[tj][:, :ntok], op=ALU.add),
                            r=[Bt1[tj], Bt2[tj]], w=[Bfo[fj]])
                        S.dma("pool", dstT[h, :, t0:t0 + ntok], fo[fj][:, :ntok], r=[Bfo[fj]])
                pa, Ba = next_pp()
                pb, Bb = next_pp()
                for k in range(8):
                    S.op("pe", lambda e, pa=pa, k=k: e.matmul(
                        pa[0:32, :ntok], wb[:, k, EKR:EKR + 32], hT[hb][:, k, :ntok], start=(k == 0), stop=(k == 7)),
                        r=[Bw, Bh[hb]], w=[Ba])
                for k in range(8):
                    S.op("pe", lambda e, pb=pb, k=k: e.matmul(
                        pb[0:32, :ntok], wb[:, k, EKR + 32:EKR + 64], hT[hb][:, k, :ntok], start=(k == 0), stop=(k == 7)),
                        r=[Bw, Bh[hb]], w=[Bb])
                tj = 0
                S.op("dve", lambda e, pa=pa: e.scalar_tensor_tensor(
                    out=t1[tj][0:32, :ntok], in0=pa[0:32, :ntok], scalar=bcol[0:32, 8:9], in1=cosm[hb][0:32, :ntok],
                    op0=ALU.add, op1=ALU.mult), r=[Ba, Brt[hb], Bw], w=[Bt1[tj]])
                S.op("dve", lambda e, pb=pb: e.scalar_tensor_tensor(
                    out=t2[tj][0:32, :ntok], in0=pb[0:32, :ntok], scalar=bcol[0:32, 9:10], in1=sinm[hb][0:32, :ntok],
                    op0=ALU.add, op1=ALU.mult), r=[Bb, Brt[hb], Bw], w=[Bt2[tj]])
                fj = foi % 4
                foi += 1
                S.op("pool", lambda e, fj=fj: e.tensor_tensor(
                    out=fo[fj][0:32, :ntok], in0=t1[tj][0:32, :ntok], in1=t2[tj][0:32, :ntok], op=ALU.add),
                    r=[Bt1[tj], Bt2[tj]], w=[Bfo[fj]])
                S.dma("pool", self.krt[:, t0:t0 + ntok], fo[fj][0:32, :ntok], r=[Bfo[fj]])
                cb = blk % 2
                for s in range(nsub):
                    tt = (t0 // 128) + s
                    tok0 = t0 + s * 128
                    p, Bp = next_pp()
                    for k in range(8):
                        S.op("pe", lambda e, p=p, k=k, s=s: e.matmul(
                            p[:, 0:256], hT[hb][:, k, s * 128:(s + 1) * 128], wb[:, k, EV_:EV_ + 256], start=(k == 0), stop=False),
                            r=[Bw, Bh[hb]], w=[Bp])
                    S.op("pe", lambda e, p=p: e.matmul(p[:, 0:256], self.ones_b[0:1, :], brow[0:1, 0:256], start=False, stop=True),
                         r=[Bw, self.B_const], w=[Bp])
                    vj = tt % 2
                    S.op("act", lambda e, p=p, vj=vj: e.activation(
                        out=va[vj][:, :, 0:128], in_=p[:, 0:256].rearrange("p (h c) -> p h c", h=2), func=AF.Copy),
                        r=[Bp], w=[Bva[vj]])
                    S.dma("pool", self.vda[:, :, tt, :].rearrange("h p c -> p h c"), va[vj][:], r=[Bva[vj]])
                    p, Bp = next_pp()
                    for k in range(8):
                        S.op("pe", lambda e, p=p, k=k, s=s: e.matmul(
                            p[:, 0:384], hT[hb][:, k, s * 128:(s + 1) * 128], wb[:, k, EC:EC + 384], start=(k == 0), stop=False),
                            r=[Bw, Bh[hb]], w=[Bp])
                    S.op("pe", lambda e, p=p: e.matmul(p[:, 0:384], self.ones_b[0:1, :], brow[0:1, 256:640], start=False, stop=True),
                         r=[Bw, self.B_const], w=[Bp])
                    cj = tt % 2
                    S.op("act", lambda e, p=p, cj=cj: e.activation(out=cq[cj][:], in_=p[:, 0:384], func=AF.Copy),
                         r=[Bp], w=[Bcq[cj]])
                    S.op("act", lambda e, cj=cj: e.activation(
                        out=junk[:, 0:256], in_=cq[cj][:, 0:256], func=AF.Square, accum_out=stat[cj][:, 0:1]),
                        r=[Bcq[cj]], w=[Bjunk, Bstat[cj]])
                    S.op("act", lambda e, cj=cj: e.activation(
                        out=junk[:, 0:128], in_=cq[cj][:, 256:384], func=AF.Square, accum_out=stat[cj][:, 1:2]),
                        r=[Bcq[cj]], w=[Bjunk, Bstat[cj]])
                    S.op("act", lambda e, cj=cj: e.activation(out=stat[cj][:, 2:3], in_=stat[cj][:, 0:1], func=AF.Sqrt,
                                                              scale=1.0 / 256, bias=eps_t[:, 0:1]), r=[Bstat[cj], Bw], w=[Bstat[cj]])
                    S.op("act", lambda e, cj=cj: e.activation(out=stat[cj][:, 3:4], in_=stat[cj][:, 1:2], func=AF.Sqrt,
                                                              scale=1.0 / 128, bias=eps_t[:, 0:1]), r=[Bstat[cj], Bw], w=[Bstat[cj]])
                    S.op("dve", lambda e, cj=cj: e.reciprocal(stat[cj][:, 4:6], stat[cj][:, 2:4]), r=[Bstat[cj]], w=[Bstat[cj]])
                    S.op("dve", lambda e, cj=cj: e.tensor_scalar(
                        out=cqn[cj][:, 0:256], in0=cq[cj][:, 0:256], scalar1=stat[cj][:, 4:5], scalar2=None, op0=ALU.mult),
                        r=[Bcq[cj], Bstat[cj]], w=[Bcqn[cj]])
                    S.op("dve", lambda e, cj=cj: e.tensor_scalar(
                        out=cqn[cj][:, 256:384], in0=cq[cj][:, 256:384], scalar1=stat[cj][:, 5:6], scalar2=None, op0=ALU.mult),
                        r=[Bcq[cj], Bstat[cj]], w=[Bcqn[cj]])
                    gj = tt % 2
                    for n in range(2):
                        p, Bp = next_pp()
                        for k in range(8):
                            S.op("pe", lambda e, p=p, k=k, s=s, n=n: e.matmul(
                                p[:, :], hT[hb][:, k, s * 128:(s + 1) * 128], wb[:, k, EG + n * 512:EG + (n + 1) * 512],
                                start=(k == 0), stop=False), r=[Bw, Bh[hb]], w=[Bp])
                        S.op("pe", lambda e, p=p, n=n: e.matmul(
                            p[:, :], self.ones_b[0:1, :], brow[0:1, 640 + n * 512:640 + (n + 1) * 512], start=False, stop=True),
                            r=[Bw, self.B_const], w=[Bp])
                        S.op("act", lambda e, p=p, n=n, gj=gj: e.activation(
                            out=gt[gj][:, n * 512:(n + 1) * 512], in_=p[:, :], func=AF.Silu), r=[Bp], w=[Bgt[gj]])
                    S.dma("pool", self.gate[tok0:tok0 + 128, :], gt[gj][:], r=[Bgt[gj]])
                    for rc in range(3):
                        S.op("pe", lambda e, rc=rc, cj=cj: e.transpose(
                            ppb[:, rc * 128:(rc + 1) * 128], cqn[cj][:, rc * 128:(rc + 1) * 128], self.ident_b[:]),
                            r=[Bcqn[cj], self.B_const], w=[Bppb])
                    S.op("dve", lambda e, s=s: e.tensor_copy(
                        cT[cb][:, :, s * 128:(s + 1) * 128], ppb[:, 0:384].rearrange("p (r t) -> p r t", r=3)),
                        r=[Bppb], w=[BcT[cb]])
                for h in range(4):
                    pa, Ba = next_pp()
                    pb, Bb = next_pp()
                    for rc in range(2):
                        S.op("pe", lambda e, pa=pa, rc=rc, h=h: e.matmul(
                            pa[0:96, :ntok], wuq[:, rc, h * 96:(h + 1) * 96], cT[cb][:, rc, :ntok], start=(rc == 0), stop=(rc == 1)),
                            r=[Bw, BcT[cb]], w=[Ba])
                    for rc in range(2):
                        S.op("pe", lambda e, pb=pb, rc=rc, h=h: e.matmul(
                            pb[0:96, :ntok], wuq[:, rc, 384 + h * 96:384 + (h + 1) * 96], cT[cb][:, rc, :ntok],
                            start=(rc == 0), stop=(rc == 1)), r=[Bw, BcT[cb]], w=[Bb])
                    tj = h % 2
                    fj = foi % 4
                    foi += 1
                    S.op("dve", lambda e, pa=pa, tj=tj: e.tensor_tensor(
                        out=t1[tj][64:96, :ntok], in0=pa[64:96, :ntok], in1=cosm[hb][64:96, :ntok], op=ALU.mult),
                        r=[Ba, Brt[hb]], w=[Bt1[tj]])
                    S.op("dve", lambda e, pb=pb, tj=tj: e.tensor_tensor(
                        out=t2[tj][64:96, :ntok], in0=pb[64:96, :ntok], in1=sinm[hb][64:96, :ntok], op=ALU.mult),
                        r=[Bb, Brt[hb]], w=[Bt2[tj]])
                    S.op("act", lambda e, pa=pa, fj=fj: e.activation(out=fo[fj][0:64, :ntok], in_=pa[0:64, :ntok], func=AF.Copy),
                         r=[Ba], w=[Bfo[fj]])
                    S.op("pool", lambda e, tj=tj, fj=fj: e.tensor_tensor(
                        out=fo[fj][64:96, :ntok], in0=t1[tj][64:96, :ntok], in1=t2[tj][64:96, :ntok], op=ALU.add),
                        r=[Bt1[tj], Bt2[tj]], w=[Bfo[fj]])
                    S.dma("pool", self.qm[h, :, t0:t0 + ntok], fo[fj][0:96, :ntok], r=[Bfo[fj]])
                for hp in range(2):
                    p, Bp = next_pp()
                    S.op("pe", lambda e, p=p, hp=hp: e.matmul(
                        p[:, :ntok], wukv[:, hp * 128:(hp + 1) * 128], cT[cb][:, 2, :ntok], start=True, stop=True),
                        r=[Bw, BcT[cb]], w=[Bp])
                    fj = foi % 4
                    foi += 1
                    S.op("act", lambda e, p=p, fj=fj: e.activation(out=fo[fj][:, :ntok], in_=p[:, :ntok], func=AF.Copy),
                         r=[Bp], w=[Bfo[fj]])
                    S.dma("pool", self.kmn[2 * hp, :, t0:t0 + ntok], fo[fj][0:64, :ntok], r=[Bfo[fj]])
                    S.dma("pool", self.kmn[2 * hp + 1, :, t0:t0 + ntok], fo[fj][64:128, :ntok], r=[Bfo[fj]])
                for s in range(nsub):
                    tt = (t0 // 128) + s
                    p, Bp = next_pp()
                    S.op("pe", lambda e, p=p, s=s: e.matmul(
                        p[:, 0:256], cT[cb][:, 2, s * 128:(s + 1) * 128], wukv[:, 256:512], start=True, stop=True),
                        r=[Bw, BcT[cb]], w=[Bp])
                    vj = tt % 2
                    S.op("act", lambda e, p=p, vj=vj: e.activation(
                        out=vma[vj][:, :, 0:64], in_=p[:, 0:256].rearrange("p (h c) -> p h c", h=4), func=AF.Copy),
                        r=[Bp], w=[Bvma[vj]])
                    S.dma("pool", self.vm[:, :, tt, :].rearrange("h p c -> p h c"), vma[vj][:], r=[Bvma[vj]])
                self._foi = foi

            stage_a(0)
            for blk in range(nblk):
                if blk + 1 < nblk:
                    stage_a(blk + 1)
                stage_b(blk)
            S.barrier()

    def attn_core(self, st, name, KT, QT, VA, Bkv, kparts, nv, scale, qblocks, finalize, kslices=None):
        S = self.S
        nmap = len(kparts)
        nacc_per_bank = 512 // nv
        n_acc = nmap * 4
        n_acc_banks = (n_acc + nacc_per_bank - 1) // nacc_per_bank
        n_sc = 8 - n_acc_banks
        n_sc = min(n_sc, 4)
        NP = 4
        res = getattr(self, "_attn_res", None)
        if res is None or res[0] != name:
            accb = [self.ps(st, f"{name}_acc{j}", [128, 512]) for j in range(n_acc_banks)]
            Bacc = S.bufs(name + "_acc", n_acc_banks)
            scb = [self.ps(st, f"{name}_sc{j}", [128, 512]) for j in range(n_sc)]
            Bsc = S.bufs(name + "_sc", n_sc)
            pt = [self.sb(st, f"{name}_pt{j}", [128, 512], BF16) for j in range(NP)]
            Bpt = S.bufs(name + "_pt", NP)
            accsb = [[self.sb(st, f"{name}_accsb{r}_{j}", [128, 512]) for j in range(n_acc_banks)] for r in range(2)]
            Baccsb = [S.bufs(f"{name}_accsb{r}", n_acc_banks) for r in range(2)]
            self._attn_res = (name, accb, Bacc, scb, Bsc, pt, Bpt, accsb, Baccsb)
        _, accb, Bacc, scb, Bsc, pt, Bpt, accsb, Baccsb = self._attn_res

        def acc_ap(mi, sub):
            idx = mi * 4 + sub
            b = idx // nacc_per_bank
            o = (idx % nacc_per_bank) * nv
            return accb[b][:, o:o + nv], Bacc[b]

        sci = 0
        pti = 0
        self._pending_fin = None
        for qbi, (q0, nq, ktiles) in enumerate(qblocks):
            nsub = nq // 128
            for b in range(n_acc_banks):
                S.op("pe", lambda e, b=b: e.matmul(accb[b][:, :], self.zeros_b[:, :], self.zeros_w[:, :], start=True, stop=False,
                                                   skip_group_check=True), r=[self.B_const], w=[Bacc[b]])
            units = [(kt, mi) for kt in ktiles for mi in range(nmap)]
            pend = []

            def emit_score(u):
                nonlocal sci
                kt, mi = u
                QTm, lo, hi = kparts[mi]
                j = sci % n_sc
                sci += 1
                S.op("pe", lambda e, j=j, lo=lo, hi=hi, kt=kt, QTm=QTm: e.matmul(
                    scb[j][:, :nq], KT[lo:hi, kt * 128:(kt + 1) * 128], QTm[lo:hi, q0:q0 + nq], start=True, stop=True),
                    r=[Bkv], w=[Bsc[j]])
                return j

            def emit_exp_pv(u, j, lastflag):
                nonlocal pti
                kt, mi = u
                pj = pti % NP
                pti += 1
                S.op("act", lambda e, j=j, pj=pj: e.activation(out=pt[pj][:, :nq], in_=scb[j][:, :nq], func=AF.Exp, scale=scale),
                     r=[Bsc[j]], w=[Bpt[pj]])
                for sub in range(nsub):
                    ap, Ba = acc_ap(mi, sub)
                    S.op("pe", lambda e, ap=ap, pj=pj, sub=sub, kt=kt: e.matmul(
                        ap, pt[pj][:, sub * 128:(sub + 1) * 128], VA[:, kt, :], start=False, stop=lastflag,
                        skip_group_check=True), r=[Bpt[pj], Bkv], w=[Ba])

            LOOK = min(n_sc - 1, 2)
            q = []
            for ui, u in enumerate(units):
                q.append((u, emit_score(u)))
                if len(q) > LOOK:
                    u0, j0 = q.pop(0)
                    emit_exp_pv(u0, j0, False)
                if ui == 8 and self._pending_fin is not None:
                    self._pending_fin()
                    self._pending_fin = None
            while q:
                u0, j0 = q.pop(0)
                emit_exp_pv(u0, j0, u0[0] == ktiles[-1])
            r = qbi % 2
            used = nmap * 4 * nv
            for b in range(n_acc_banks):
                w = min(512, used - b * nacc_per_bank * nv)
                S.op("dve", lambda e, b=b, r=r, w=w: e.tensor_copy(accsb[r][b][:, 0:w], accb[b][:, 0:w]), r=[Bacc[b]], w=[Baccsb[r][b]])

            def acc_sb(mi, sub, r=r):
                idx = mi * 4 + sub
                b = idx // nacc_per_bank
                o = (idx % nacc_per_bank) * nv
                return accsb[r][b][:, o:o + nv], Baccsb[r][b]

            if self._pending_fin is not None:
                self._pending_fin()
                self._pending_fin = None

            def do_fin(qbi=qbi, q0=q0, nsub=nsub, acc_sb=acc_sb):
                for sub in range(nsub):
                    accs = [acc_sb(mi, sub) for mi in range(nmap)]
                    finalize(qbi, q0, sub, accs)
            self._pending_fin = do_fin
        if self._pending_fin is not None:
            self._pending_fin()
            self._pending_fin = None

    def qblocks_all(self):
        qb = []
        lat_k = list(range(NT))
        for b in range(LAT // 512):
            qb.append((b * 512, 512, lat_k))
        qb.append((LAT, CTX, [NT_LAT, NT_LAT + 1]))
        if self.qb_filter is not None:
            qb = [q for i, q in enumerate(qb) if i in self.qb_filter]
        return qb

    def qblocks_own(self):
        qb = []
        lat_k = list(range(NT))
        for b in range(LAT // 2 // 512):
            qb.append((b * 512, 512, lat_k))
        qb.append((LAT // 2, CTX, [NT_LAT, NT_LAT + 1]))
        if self.qb_filter is not None:
            qb = [q for i, q in enumerate(qb) if i in self.qb_filter]
        return qb

    def own_q(self, QO, QS, lo, hi, tmp, Bqs, Bqo, Btmp):
        S = self.S
        H = LAT // 2
        S.op("dve", lambda e: e.tensor_scalar(out=tmp[lo:hi, :], in0=QS[lo:hi, H:LAT], scalar1=self.pselt[lo:hi, 1:2], scalar2=None,
                                              op0=ALU.mult), r=[Bqs, self.B_const], w=[Btmp])
        S.op("dve", lambda e: e.scalar_tensor_tensor(out=QO[lo:hi, 0:H], in0=QS[lo:hi, 0:H], scalar=self.pselt[lo:hi, 0:1],
                                                     in1=tmp[lo:hi, :], op0=ALU.mult, op1=ALU.add),
             r=[Bqs, Btmp, self.B_const], w=[Bqo])
        S.op("act", lambda e: e.activation(out=QO[lo:hi, H:H + CTX], in_=QS[lo:hi, LAT:NTOK], func=AF.Copy), r=[Bqs], w=[Bqo])

    def blend(self, out, a, b, Ba, Bb, Bout, lo=0, hi=128, col=0):
        S = self.S
        s0 = self.pselt[lo:hi, col:col + 1]
        s1 = self.pselt[lo:hi, col + 1:col + 2]
        S.op("dve", lambda e: e.tensor_scalar(out=b, in0=b, scalar1=s1, scalar2=None, op0=ALU.mult), r=[Bb, self.B_const], w=[Bb])
        S.op("dve", lambda e: e.scalar_tensor_tensor(out=out, in0=a, scalar=s0, in1=b, op0=ALU.mult, op1=ALU.add),
             r=[Ba, Bb, self.B_const], w=[Bout])

    def yodd_loc(self, t):
        if t < NT_LAT:
            return t // 16, (t % 16) * 128
        return 4, (t - NT_LAT) * 128

    def y_dst(self, tok0, c0, c1):
        H = LAT // 2
        if tok0 < H:
            k, r = tok0 // 1024, tok0 % 1024
            return self.yown_t[k].ap()[r:r + 128, c0:c1]
        return self.ybuf[LAT + tok0 - H:LAT + tok0 - H + 128, c0:c1]

    def da_attention(self, i, l):
        S = self.S
        lam_init = 0.8 - 0.6 * math.exp(-0.3 * l)
        with contextlib.ExitStack() as st:
            KTs = [self.sb(st, f"da_KT{j}", [128, NTOK], BF16) for j in range(2)]
            VAs = [self.sb(st, f"da_VA{j}", [128, NT, 129], BF16) for j in range(2)]
            QT1s = [self.sb(st, f"da_QT1{j}", [128, NTOK], BF16) for j in range(2)]
            QT2s = [self.sb(st, f"da_QT2{j}", [128, NTOK], BF16) for j in range(2)]
            Bkvs = S.bufs("da_kv", 2)
            for j in range(2):
                S.op("pool", lambda e, j=j: e.memset(QT1s[j][64:128, :], 0.0), w=[Bkvs[j]])
                S.op("pool", lambda e, j=j: e.memset(QT2s[j][0:64, :], 0.0), w=[Bkvs[j]])
            lamt = self.sb(st, "lamt", [128, 256])
            lamw = self.sb(st, "lamw", [128, 8])
            subg = self.sb(st, "subg", [128, 128])
            Bl = S.buf("lam")
            eps_t = self.sb(st, "da_eps", [128, 1])
            S.op("pool", lambda e: e.memset(eps_t[:], RMS_EPS), w=[Bl])
            S.dma("sp", lamt[:], self.ev_lam[i, :, :].to_broadcast([128, 256]), w=[Bl])
            S.dma("sp", subg[:], self.ev_subg[i, :, :].to_broadcast([128, 128]), w=[Bl])
            junk = self.sb(st, "da_junk", [128, 128])
            Bj = S.buf("da_junk")
            S.op("dve", lambda e: e.tensor_tensor(out=junk[:, 0:64], in0=lamt[:, 0:64], in1=lamt[:, 64:128], op=ALU.mult),
                 r=[Bl], w=[Bj])
            S.op("dve", lambda e: e.reduce_sum(out=lamw[:, 0:1], in_=junk[:, 0:64], axis=AX.X), r=[Bj], w=[Bl])
            S.op("dve", lambda e: e.tensor_tensor(out=junk[:, 64:128], in0=lamt[:, 128:192], in1=lamt[:, 192:256], op=ALU.mult),
                 r=[Bl], w=[Bj])
            S.op("dve", lambda e: e.reduce_sum(out=lamw[:, 1:2], in_=junk[:, 64:128], axis=AX.X), r=[Bj], w=[Bl])
            S.op("act", lambda e: e.activation(out=lamw[:, 2:4], in_=lamw[:, 0:2], func=AF.Exp), r=[Bl], w=[Bl])
            S.op("dve", lambda e: e.tensor_tensor(out=lamw[:, 4:5], in0=lamw[:, 3:4], in1=lamw[:, 2:3], op=ALU.subtract),
                 r=[Bl], w=[Bl])
            S.op("dve", lambda e: e.tensor_scalar(out=lamw[:, 5:6], in0=lamw[:, 4:5], scalar1=-lam_init, scalar2=None, op0=ALU.add),
                 r=[Bl], w=[Bl])
            NF = 3
            rr = [self.sb(st, f"da_rr{j}", [128, 8]) for j in range(NF)]
            ta = [self.sb(st, f"da_ta{j}", [128, 128]) for j in range(NF)]
            td = [self.sb(st, f"da_td{j}", [128, 128]) for j in range(NF)]
            to = [self.sb(st, f"da_to{j}", [128, 128], BF16) for j in range(NF)]
            Bf = S.bufs("da_fin", NF)
            Bto = S.bufs("da_to", NF)
            self._fi = 0
            def da_load(h):
                j = h % 2
                S.dma("sp", KTs[j][:], self.kda[h, :, :], w=[Bkvs[j]])
                S.dma("act", QT1s[j][0:64, :], self.qda[h, 0:64, :], w=[Bkvs[j]])
                S.dma("act", QT2s[j][64:128, :], self.qda[h, 64:128, :], w=[Bkvs[j]])
                S.dma("sp", VAs[j][:], self.vda[h, :, :, :], w=[Bkvs[j]])

            da_load(0)
            for h in range(2):
                if h + 1 < 2:
                    da_load(h + 1)
                KT, VA, QT1, QT2, Bkv = KTs[h % 2], VAs[h % 2], QT1s[h % 2], QT2s[h % 2], Bkvs[h % 2]

                def fin(qbi, q0, sub, accs, h=h):
                    j = self._fi % NF
                    self._fi += 1
                    (a1, B1), (a2, B2) = accs
                    S.op("dve", lambda e: e.reciprocal(rr[j][:, 0:1], a1[:, 128:129]), r=[B1], w=[Bf[j]])
                    S.op("dve", lambda e: e.reciprocal(rr[j][:, 1:2], a2[:, 128:129]), r=[B2], w=[Bf[j]])
                    S.op("dve", lambda e: e.tensor_tensor(out=rr[j][:, 2:3], in0=rr[j][:, 1:2], in1=lamw[:, 5:6], op=ALU.mult),
                         r=[Bf[j], Bl], w=[Bf[j]])
                    S.op("dve", lambda e: e.tensor_scalar(out=ta[j][:], in0=a1[:, 0:128], scalar1=rr[j][:, 0:1], scalar2=None,
                                                          op0=ALU.mult), r=[B1, Bf[j]], w=[Bf[j]])
                    S.op("dve", lambda e: e.scalar_tensor_tensor(out=td[j][:], in0=a2[:, 0:128], scalar=rr[j][:, 2:3], in1=ta[j][:],
                                                                 op0=ALU.mult, op1=ALU.add), r=[B2, Bf[j]], w=[Bf[j]])
                    S.op("act", lambda e: e.activation(out=ta[j][:], in_=td[j][:], func=AF.Square, accum_out=rr[j][:, 3:4]),
                         r=[Bf[j]], w=[Bf[j]])
                    S.op("act", lambda e: e.activation(out=rr[j][:, 4:5], in_=rr[j][:, 3:4], func=AF.Sqrt, scale=1.0 / 128,
                                                       bias=eps_t[:, 0:1]), r=[Bf[j], Bl], w=[Bf[j]])
                    S.op("dve", lambda e: e.reciprocal(rr[j][:, 5:6], rr[j][:, 4:5]), r=[Bf[j]], w=[Bf[j]])
                    S.op("pool", lambda e: e.tensor_scalar(out=td[j][:], in0=td[j][:], scalar1=rr[j][:, 5:6], scalar2=(1.0 - lam_init),
                                                           op0=ALU.mult, op1=ALU.mult), r=[Bf[j]], w=[Bf[j]])
                    S.op("pool", lambda e: e.tensor_tensor(out=to[j][:], in0=td[j][:], in1=subg[:], op=ALU.mult),
                         r=[Bf[j], Bl], w=[Bto[j]])
                    kk, r0 = self.yodd_loc((q0 + sub * 128) // 128)
                    S.dma("pool", self.yodd_t[kk].ap()[r0:r0 + 128, h * 128:(h + 1) * 128], to[j][:], r=[Bto[j]])

                self.attn_core(st, "da", KT, None, VA, Bkv, [(QT1, 0, 128), (QT2, 0, 128)], 129, DA_SCALE,
                               self.qblocks_all(), fin)
            S.barrier()
            self._attn_res = None

    def mla_attention(self, i):
        S = self.S
        with contextlib.ExitStack() as st:
            KTs = [self.sb(st, f"ml_KT{j}", [128, NTOK], BF16) for j in range(2)]
            QTs = [self.sb(st, f"ml_QT{j}", [128, NTOK], BF16) for j in range(2)]
            VAs = [self.sb(st, f"ml_VA{j}", [128, NT, 65], BF16) for j in range(2)]
            Bkvs = S.bufs("ml_kv", 2)
            NF = 3
            rr = [self.sb(st, f"ml_rr{j}", [128, 2]) for j in range(NF)]
            to = [self.sb(st, f"ml_to{j}", [128, 64], BF16) for j in range(NF)]
            Bto = S.bufs("ml_to", NF)
            self._fi = 0
            def ml_load(h):
                j = h % 2
                S.dma("sp", KTs[j][0:64, :], self.kmn[h, :, :], w=[Bkvs[j]])
                S.dma("sp", KTs[j][64:96, :], self.krt[:, :], w=[Bkvs[j]])
                S.dma("act", QTs[j][0:96, :], self.qm[h, :, :], w=[Bkvs[j]])
                S.dma("sp", VAs[j][:], self.vm[h, :, :, :], w=[Bkvs[j]])

            ml_load(0)
            for h in range(4):
                if h + 1 < 4:
                    ml_load(h + 1)
                KT, VA, QT, Bkv = KTs[h % 2], VAs[h % 2], QTs[h % 2], Bkvs[h % 2]

                def fin(qbi, q0, sub, accs, h=h):
                    j = self._fi % NF
                    self._fi += 1
                    (a1, B1), = accs
                    S.op("dve", lambda e: e.reciprocal(rr[j][:, 0:1], a1[:, 64:65]), r=[B1], w=[Bto[j]])
                    S.op("dve", lambda e: e.tensor_scalar(out=to[j][:], in0=a1[:, 0:64], scalar1=rr[j][:, 0:1], scalar2=None,
                                                          op0=ALU.mult), r=[B1, Bto[j]], w=[Bto[j]])
                    tok0 = q0 + sub * 128
                    kk, r0 = self.yodd_loc(tok0 // 128)
                    S.dma("pool", self.yodd_t[kk].ap()[r0:r0 + 128, 256 + h * 64:256 + (h + 1) * 64], to[j][:], r=[Bto[j]])

                self.attn_core(st, "ml", KT, None, VA, Bkv, [(QT, 0, 96)], 65, MLA_SCALE, self.qblocks_all(), fin)
            S.barrier()
            self._attn_res = None

    def out_stage(self, wo_dram, src, dst, last, ysplit=False):
        S = self.S
        with contextlib.ExitStack() as st:
            wo = self.sb(st, "wo", [128, 8, D], BF16)
            Bw = S.buf("wo")
            pieces = [(wo[:, k, :], wo_dram[k * 128:(k + 1) * 128, :], None) for k in range(8)]
            self.load_cast(st, None, None, 128, D, pieces, Bw, name="wo")
            NB = 4
            yt = [self.sb(st, f"o_yt{j}", [128, D], BF16) for j in range(NB)]
            gt = [self.sb(st, f"o_gt{j}", [128, D], BF16) for j in range(NB)]
            xt = [self.sb(st, f"o_xt{j}", [128, D]) for j in range(NB)]
            yg = [self.sb(st, f"o_yg{j}", [128, D], BF16) for j in range(NB)]
            ygT = [self.sb(st, f"o_ygT{j}", [128, D], BF16) for j in range(NB)]
            rt = [self.sb(st, f"o_rt{j}", [128, D]) for j in range(NB)]
            ot = [self.sb(st, f"o_ot{j}", [128, D]) for j in range(NB)]
            stt = [self.sb(st, f"o_st{j}", [128, 16]) for j in range(NB)]
            By, Bg, Bx, Byg, BygT, Br, Bo, Bs = (S.bufs(n, NB) for n in ("o_yt", "o_gt", "o_xt", "o_yg", "o_ygT", "o_rt", "o_ot", "o_st"))
            ptr = [self.ps(st, f"o_ptr{j}", [128, D], BF16) for j in range(2)]
            Bptr = S.bufs("o_ptr", 2)
            pout = [self.ps(st, f"o_po{j}", [128, 512]) for j in range(4)]
            Bpo = S.bufs("o_po", 4)
            eps_t = self.sb(st, "o_eps", [128, 1])
            S.op("pool", lambda e: e.memset(eps_t[:], LN_EPS), w=[Bw])
            ntiles = NT_LAT if (last and self.final_out) else NT
            tiles = [t for t in range(ntiles) if self.tile_filter is None or t in self.tile_filter]

            def stage_l(t):
                j = t % NB
                tok0 = t * 128
                isctx = t >= NT_LAT
                if ysplit == "odd":
                    k, r0 = self.yodd_loc(t)
                    rows = self.yodd_rows[k]
                    g = self.yoddg_t[k].ap()
                    S.dma("sp", yt[j][:, 0:256], g[r0:r0 + 128, 0:256], w=[By[j]])
                    S.dma("sp", yt[j][:, 256:512], g[rows + r0:rows + r0 + 128, 0:256], w=[By[j]])
                    S.dma("sp", yt[j][:, 512:768], g[r0:r0 + 128, 256:512], w=[By[j]])
                    S.dma("sp", yt[j][:, 768:1024], g[rows + r0:rows + r0 + 128, 256:512], w=[By[j]])
                else:
                    if ysplit and not isctx:
                        half, u = t // 32, t % 32
                        r0 = half * 1024 + (u % 8) * 128
                        ysrc = self.ygath_t[u // 8].ap()[r0:r0 + 128, :]
                    else:
                        ysrc = self.ybuf[tok0:tok0 + 128, :]
                    S.dma("sp", yt[j][:], ysrc, w=[By[j]])
                S.dma("act", gt[j][:], self.gate[tok0:tok0 + 128, :], w=[Bg[j]])
                S.dma("sp", xt[j][:], src[tok0:tok0 + 128, :], w=[Bx[j]])

            def stage_a(t):
                j = t % NB
                S.op("dve", lambda e, j=j: e.tensor_tensor(out=yg[j][:], in0=yt[j][:], in1=gt[j][:], op=ALU.mult),
                     r=[By[j], Bg[j]], w=[Byg[j]])
                pj = t % 2
                for ec in range(8):
                    S.op("pe", lambda e, ec=ec, j=j, pj=pj: e.transpose(
                        ptr[pj][:, ec * 128:(ec + 1) * 128], yg[j][:, ec * 128:(ec + 1) * 128], self.ident_b[:]),
                        r=[Byg[j], self.B_const], w=[Bptr[pj]])
                S.op("act", lambda e, j=j, pj=pj: e.activation(out=ygT[j][:], in_=ptr[pj][:], func=AF.Copy),
                     r=[Bptr[pj]], w=[BygT[j]])
                for n in range(2):
                    pn = (t % 2) * 2 + n
                    for ec in range(8):
                        S.op("pe", lambda e, ec=ec, n=n, pn=pn, j=j: e.matmul(
                            pout[pn][:, :], ygT[j][:, ec * 128:(ec + 1) * 128], wo[:, ec, n * 512:(n + 1) * 512],
                            start=(ec == 0), stop=(ec == 7)), r=[BygT[j], Bw], w=[Bpo[pn]])

            def stage_b(t):
                j = t % NB
                tok0 = t * 128
                isctx = t >= NT_LAT
                G = self.g_c if isctx else self.g_l
                for n in range(2):
                    pn = (t % 2) * 2 + n
                    S.op("dve", lambda e, n=n, pn=pn, j=j, G=G: e.tensor_tensor(
                        out=rt[j][:, n * 512:(n + 1) * 512], in0=pout[pn][:, :], in1=G[:, n * 512:(n + 1) * 512], op=ALU.mult),
                        r=[Bpo[pn], self.B_mod], w=[Br[j]])
                S.op("dve", lambda e, j=j: e.scalar_tensor_tensor(
                    out=rt[j][:], in0=xt[j][:], scalar=ALPHA, in1=rt[j][:], op0=ALU.mult, op1=ALU.add),
                    r=[Bx[j], Br[j]], w=[Br[j]])
                for n in range(2):
                    S.op("dve", lambda e, n=n, j=j: e.bn_stats(stt[j][:, n * 6:(n + 1) * 6], rt[j][:, n * 512:(n + 1) * 512]),
                         r=[Br[j]], w=[Bs[j]])
                S.op("dve", lambda e, j=j: e.bn_aggr(stt[j][:, 12:14], stt[j][:, 0:12]), r=[Bs[j]], w=[Bs[j]])
                S.op("act", lambda e, j=j: e.activation(out=stt[j][:, 14:15], in_=stt[j][:, 13:14], func=AF.Sqrt, scale=1.0,
                                                        bias=eps_t[:, 0:1]), r=[Bs[j], Bw], w=[Bs[j]])
                S.op("dve", lambda e, j=j: e.reciprocal(stt[j][:, 15:16], stt[j][:, 14:15]), r=[Bs[j]], w=[Bs[j]])
                S.op("dve", lambda e, j=j: e.tensor_scalar(
                    out=ot[j][:], in0=rt[j][:], scalar1=stt[j][:, 12:13], scalar2=stt[j][:, 15:16], op0=ALU.subtract, op1=ALU.mult),
                    r=[Br[j], Bs[j]], w=[Bo[j]])
                S.op("pool", lambda e, j=j: e.tensor_tensor(out=ot[j][:], in0=ot[j][:], in1=self.lng[:], op=ALU.mult),
                     r=[Bo[j], self.B_mod], w=[Bo[j]])
                S.op("pool", lambda e, j=j: e.tensor_tensor(out=ot[j][:], in0=ot[j][:], in1=self.lnb[:], op=ALU.add),
                     r=[Bo[j], self.B_mod], w=[Bo[j]])
                S.dma("pool", dst[tok0:tok0 + 128, :], ot[j][:], r=[Bo[j]])

            for idx, t in enumerate(tiles):
                if idx == 0:
                    stage_l(t)
                    if len(tiles) > 1:
                        stage_l(tiles[1])
                    stage_a(t)
                if idx + 2 < len(tiles):
                    stage_l(tiles[idx + 2])
                if idx + 1 < len(tiles):
                    stage_a(tiles[idx + 1])
                stage_b(t)
            S.barrier()

    def odd_layer(self, l, src, dst, last):
        i = l // 2
        self.adaln(l)
        self.odd_project(i, src)
        self.odd_conv(i)
        self.mlstm(i)
        self.mlstm_post(i)
        self.na_attention(i, last)
        self.S.allgather_pairs(self.yodd_t, self.yoddg_t)
        self.out_stage(self.od_wo[i], src, dst, last, ysplit="odd")

    def odd_project(self, i, src):
        S = self.S
        with contextlib.ExitStack() as st:
            wb = self.sb(st, "od_wb", [128, 8, OD_COLS], BF16)
            bcol = self.sb(st, "od_bcol", [128, 8])
            brow_f = self.sb(st, "od_brow_f", [1, 1800])
            brow = self.sb(st, "od_brow", [1, 1800], BF16)
            fb = self.sb(st, "od_fb", [128, 4])
            Bw = S.buf("od_w")
            S.dma("sp", bcol[:], self.od_bcol[i, :, :], w=[Bw])
            S.dma("sp", brow_f[:], self.od_brow[i, :, :], w=[Bw])
            S.dma("sp", fb[:], self.od_fb[i, :, :].to_broadcast([128, 4]), w=[Bw])
            S.op("dve", lambda e: e.tensor_copy(brow[:], brow_f[:]), r=[Bw], w=[Bw])
            pieces = []
            for k in range(8):
                for c in range(2):
                    pieces.append((wb[:, k, c * 1412:(c + 1) * 1412],
                                   self.od_w[i, k * 128:(k + 1) * 128, c * 1412:(c + 1) * 1412], None))
            self.load_cast(st, None, None, 128, 1412, pieces, Bw, name="odw")
            NXS = 6
            xt = [self.sb(st, f"xt{j}", [128, D]) for j in range(NXS)]
            Bx = S.bufs("xt", NXS)
            hT = [self.sb(st, f"hT{j}", [128, 8, 512], BF16) for j in range(2)]
            Bh = S.bufs("hT", 2)
            ff = [self.sb(st, f"ff{j}", [128, 512]) for j in range(3)]
            Bff = S.bufs("ff", 3)
            fo = [self.sb(st, f"fo{j}", [128, 512], BF16) for j in range(3)]
            Bfo = S.bufs("fo", 3)
            va = [self.sb(st, f"va{j}", [128, 2, 129], BF16) for j in range(2)]
            Bva = S.bufs("va", 2)
            vna = [self.sb(st, f"vna{j}", [128, 4, 65], BF16) for j in range(2)]
            Bvna = S.bufs("vna", 2)
            ot = [self.sb(st, f"ot{j}", [128, 256]) for j in range(2)]
            Bot = S.bufs("ot", 2)
            gt = [self.sb(st, f"gt{j}", [128, D], BF16) for j in range(2)]
            Bgt = S.bufs("gt", 2)
            gs = [self.sb(st, f"gs{j}", [128, 4, 8]) for j in range(2)]
            gtmp = [self.sb(st, f"gtmp{j}", [128, 4, 4]) for j in range(2)]
            Bgs = S.bufs("gs", 2)
            pp = [self.ps(st, f"pp{j}", [128, 512]) for j in range(7)]
            pg = self.ps(st, "pg", [128, 512])
            Bpp = S.bufs("pp", 7)
            Bpg = S.buf("pg")
            for j in range(2):
                S.op("pool", lambda e, j=j: e.memset(va[j][:], 1.0), w=[Bva[j]])
                S.op("pool", lambda e, j=j: e.memset(vna[j][:], 1.0), w=[Bvna[j]])
            self._pp_i = 0

            def next_pp():
                j = self._pp_i % 7
                self._pp_i += 1
                return pp[j], Bpp[j]

            nblk = (NTOK + 511) // 512
            self._xi = 0
            self._evac = 0
            self._ffi = 0
            self._foi = 0

            def stage_a(blk):
                t0 = blk * 512
                ntok = min(512, NTOK - t0)
                nsub = ntok // 128
                m = 0 if t0 < LAT else 1
                hb = blk % 2
                for s in range(nsub):
                    xj = self._xi % NXS
                    self._xi += 1
                    S.dma("sp", xt[xj][:], src[t0 + s * 128:t0 + (s + 1) * 128, :], w=[Bx[xj]])
                    for half in range(2):
                        p, Bp = next_pp()
                        for q in range(4):
                            dc = half * 4 + q
                            S.op("pe", lambda e, p=p, q=q, dc=dc, xj=xj: e.transpose(
                                p[:, q * 128:(q + 1) * 128], xt[xj][:, dc * 128:(dc + 1) * 128], self.ident_f[:]),
                                r=[Bx[xj], self.B_const], w=[Bp])
                        for q in range(4):
                            dc = half * 4 + q
                            if self._evac % 2 == 0:
                                S.op("act", lambda e, p=p, q=q, dc=dc, s=s, m=m: e.activation(
                                    out=hT[hb][:, dc, s * 128:(s + 1) * 128], in_=p[:, q * 128:(q + 1) * 128],
                                    func=AF.Identity, scale=self.sc1[:, dc, m:m + 1], bias=self.sh[:, dc, m:m + 1]),
                                    r=[Bp, self.B_mod], w=[Bh[hb]])
                            else:
                                S.op("dve", lambda e, p=p, q=q, dc=dc, s=s, m=m: e.tensor_scalar(
                                    out=hT[hb][:, dc, s * 128:(s + 1) * 128], in0=p[:, q * 128:(q + 1) * 128],
                                    scalar1=self.sc1[:, dc, m:m + 1], scalar2=self.sh[:, dc, m:m + 1],
                                    op0=ALU.mult, op1=ALU.add), r=[Bp, self.B_mod], w=[Bh[hb]])
                            self._evac += 1

            def stage_b(blk):
                t0 = blk * 512
                ntok = min(512, NTOK - t0)
                nsub = ntok // 128
                m = 0 if t0 < LAT else 1
                hb = blk % 2
                ffi, foi = self._ffi, self._foi
                for c in range(8):
                    p, Bp = next_pp()
                    for k in range(8):
                        S.op("pe", lambda e, p=p, k=k, c=c: e.matmul(
                            p[:, :ntok], wb[:, k, c * 128:(c + 1) * 128], hT[hb][:, k, :ntok], start=(k == 0), stop=(k == 7)),
                            r=[Bw, Bh[hb]], w=[Bp])
                    if c < 4:
                        fj = ffi % 3
                        ffi += 1
                        if c % 2 == 0:
                            S.op("act", lambda e, p=p, c=c, fj=fj: e.activation(
                                out=ff[fj][:, :ntok], in_=p[:, :ntok], func=AF.Identity, bias=bcol[:, c:c + 1]),
                                r=[Bp, Bw], w=[Bff[fj]])
                        else:
                            S.op("dve", lambda e, p=p, c=c, fj=fj: e.tensor_scalar(
                                out=ff[fj][:, :ntok], in0=p[:, :ntok], scalar1=bcol[:, c:c + 1], scalar2=None, op0=ALU.add),
                                r=[Bp, Bw], w=[Bff[fj]])
                        S.dma("pool", self.qkpre[c, :, t0:t0 + ntok], ff[fj][:, :ntok], r=[Bff[fj]])
                    else:
                        fj = foi % 3
                        foi += 1
                        if c % 2 == 0:
                            S.op("act", lambda e, p=p, c=c, fj=fj: e.activation(
                                out=fo[fj][:, :ntok], in_=p[:, :ntok], func=AF.Identity, bias=bcol[:, c:c + 1]),
                                r=[Bp, Bw], w=[Bfo[fj]])
                        else:
                            S.op("dve", lambda e, p=p, c=c, fj=fj: e.tensor_scalar(
                                out=fo[fj][:, :ntok], in0=p[:, :ntok], scalar1=bcol[:, c:c + 1], scalar2=None, op0=ALU.add),
                                r=[Bp, Bw], w=[Bfo[fj]])
                        dd = self.qda if c < 6 else self.kda
                        S.dma("pool", dd[(c - 4) % 2, :, t0:t0 + ntok], fo[fj][:, :ntok], r=[Bfo[fj]])
                gb = blk % 2
                for s in range(nsub):
                    tt = (t0 // 128) + s
                    tok0 = t0 + s * 128

                    def tm(p, c0, n, b0, s=s):
                        for k in range(8):
                            S.op("pe", lambda e, k=k: e.matmul(
                                p[:, 0:n], hT[hb][:, k, s * 128:(s + 1) * 128], wb[:, k, c0:c0 + n], start=(k == 0), stop=False),
                                r=[Bw, Bh[hb]], w=[Bp])
                        S.op("pe", lambda e: e.matmul(p[:, 0:n], self.ones_b[0:1, :], brow[0:1, b0:b0 + n], start=False, stop=True),
                             r=[Bw, self.B_const], w=[Bp])
                    p, Bp = next_pp()
                    tm(p, OV, 256, 0)
                    vj = tt % 2
                    S.op("act", lambda e, p=p, vj=vj: e.activation(
                        out=va[vj][:, :, 0:128], in_=p[:, 0:256].rearrange("p (h c) -> p h c", h=2), func=AF.Copy),
                        r=[Bp], w=[Bva[vj]])
                    S.dma("pool", self.vda[:, :, tt, :].rearrange("h p c -> p h c"), va[vj][:], r=[Bva[vj]])
                    p, Bp = next_pp()
                    tm(p, OO, 256, 256)
                    oj = tt % 2
                    S.op("act", lambda e, p=p, oj=oj: e.activation(out=ot[oj][:], in_=p[:, 0:256], func=AF.Sigmoid), r=[Bp], w=[Bot[oj]])
                    S.dma("pool", self.og[tok0:tok0 + 128, :], ot[oj][:], r=[Bot[oj]])
                    Bp = Bpg
                    for k in range(8):
                        S.op("pe", lambda e, k=k, s=s: e.matmul(
                            pg[:, s * 8:(s + 1) * 8], hT[hb][:, k, s * 128:(s + 1) * 128], wb[:, k, OGT:OGT + 8],
                            start=(k == 0), stop=False), r=[Bw, Bh[hb]], w=[Bpg])
                    S.op("pe", lambda e, s=s: e.matmul(pg[:, s * 8:(s + 1) * 8], self.ones_b[0:1, :], brow[0:1, 512:520],
                                                       start=False, stop=True), r=[Bw, self.B_const], w=[Bpg])
                    p, Bp = next_pp()
                    tm(p, OVN, 256, 520)
                    vj = tt % 2
                    S.op("act", lambda e, p=p, vj=vj: e.activation(
                        out=vna[vj][:, :, 0:64], in_=p[:, 0:256].rearrange("p (h c) -> p h c", h=4), func=AF.Copy),
                        r=[Bp], w=[Bvna[vj]])
                    S.dma("pool", self.vm[:, :, tt, :].rearrange("h p c -> p h c"), vna[vj][:], r=[Bvna[vj]])
                    gj = tt % 2
                    for n in range(2):
                        p, Bp = next_pp()
                        tm(p, OG + n * 512, 512, 776 + n * 512)
                        S.op("act", lambda e, p=p, n=n, gj=gj: e.activation(
                            out=gt[gj][:, n * 512:(n + 1) * 512], in_=p[:, :], func=AF.Silu), r=[Bp], w=[Bgt[gj]])
                    S.dma("pool", self.gate[tok0:tok0 + 128, :], gt[gj][:], r=[Bgt[gj]])
                pgv = pg[:, 0:nsub * 8].rearrange("p (s c) -> p s c", c=8)
                S.op("dve", lambda e, pgv=pgv: e.tensor_copy(gs[gb][:, 0:nsub, 0:4], pgv[:, :, 0:4]), r=[Bpg], w=[Bgs[gb]])
                for s in range(nsub):
                    S.op("dve", lambda e, s=s: e.tensor_tensor(out=gtmp[gb][:, s, :], in0=pg[:, s * 8 + 4:s * 8 + 8], in1=fb[:, :],
                                                               op=ALU.add), r=[Bpg, Bw], w=[Bgs[gb]])
                S.op("act", lambda e: e.activation(out=gtmp[gb][:, 0:nsub, :], in_=gtmp[gb][:, 0:nsub, :], func=AF.Exp, scale=-1.0),
                     r=[Bgs[gb]], w=[Bgs[gb]])
                S.op("act", lambda e: e.activation(out=gtmp[gb][:, 0:nsub, :], in_=gtmp[gb][:, 0:nsub, :], func=AF.Ln, bias=1.0),
                     r=[Bgs[gb]], w=[Bgs[gb]])
                S.op("dve", lambda e: e.tensor_scalar(out=gs[gb][:, 0:nsub, 4:8], in0=gtmp[gb][:, 0:nsub, :], scalar1=-1.0,
                                                      scalar2=None, op0=ALU.mult), r=[Bgs[gb]], w=[Bgs[gb]])
                tt0 = t0 // 128
                S.dma("pool", self.gates[:, tt0:tt0 + nsub, :], gs[gb][:, 0:nsub, :], r=[Bgs[gb]])
                self._ffi, self._foi = ffi, foi

            stage_a(0)
            for blk in range(nblk):
                if blk + 1 < nblk:
                    stage_a(blk + 1)
                stage_b(blk)
            S.barrier()

    def odd_conv(self, i):
        S = self.S
        SEG = 2048
        with contextlib.ExitStack() as st:
            cw = self.sb(st, "cv_w", [128, 4, 5])
            cb = self.sb(st, "cv_b", [128, 4])
            Bw = S.buf("cv_w")
            S.dma("sp", cw[:], self.od_convw[i, :, :, :], w=[Bw])
            S.dma("sp", cb[:], self.od_convb[i, :, :], w=[Bw])
            xin = [self.sb(st, f"cv_x{j}", [128, SEG + 4]) for j in range(2)]
            Bxin = S.bufs("cv_x", 2)
            acc = [self.sb(st, f"cv_a{j}", [128, SEG]) for j in range(2)]
            Bacc = S.bufs("cv_a", 2)
            tmp = [self.sb(st, f"cv_t{j}", [128, SEG]) for j in range(2)]
            Btmp = S.bufs("cv_t", 2)
            outb = [self.sb(st, f"cv_o{j}", [128, SEG], BF16) for j in range(2)]
            Bout = S.bufs("cv_o", 2)
            segs = [(a, SEG, 0, LAT) for a in range(0, LAT, SEG)] + [(LAT, CTX, LAT, LAT + CTX)]
            it = 0
            for c in range(4):
                for (t0, n, lo, hi) in segs:
                    j = it % 2
                    it += 1
                    a = max(t0 - 2, lo)
                    b = min(t0 + n + 2, hi)
                    if a > t0 - 2:
                        S.op("pool", lambda e, j=j: e.memset(xin[j][:, 0:2], 0.0), w=[Bxin[j]])
                    if b < t0 + n + 2:
                        S.op("pool", lambda e, j=j, n=n: e.memset(xin[j][:, n + 2:n + 4], 0.0), w=[Bxin[j]])
                    S.dma("sp", xin[j][:, a - (t0 - 2):b - (t0 - 2)], self.qkpre[c, :, a:b], w=[Bxin[j]])
                    S.op("act", lambda e, j=j, n=n, c=c: e.activation(
                        out=acc[j][:, 0:n], in_=xin[j][:, 0:n], func=AF.Copy, scale=cw[:, c, 0:1]),
                        r=[Bxin[j], Bw], w=[Bacc[j]])
                    for k in range(1, 5):
                        S.op("dve", lambda e, j=j, n=n, c=c, k=k: e.scalar_tensor_tensor(
                            out=acc[j][:, 0:n], in0=xin[j][:, k:k + n], scalar=cw[:, c, k:k + 1], in1=acc[j][:, 0:n],
                            op0=ALU.mult, op1=ALU.add), r=[Bxin[j], Bw, Bacc[j]], w=[Bacc[j]])
                    if True:
                        S.op("act", lambda e, j=j, n=n, c=c: e.activation(
                            out=outb[j][:, 0:n], in_=acc[j][:, 0:n], func=AF.Silu, bias=cb[:, c:c + 1]),
                            r=[Bacc[j], Bw], w=[Bout[j]])
                        dd = self.mq if c < 2 else self.mk
                        S.dma("pool", dd[c % 2, :, t0:t0 + n], outb[j][:, 0:n], r=[Bout[j]])
                    else:
                        S.op("act", lambda e, j=j, n=n, c=c: e.activation(
                            out=tmp[j][:, 0:n], in_=acc[j][:, 0:n], func=AF.Silu, bias=cb[:, c:c + 1]),
                            r=[Bacc[j], Bw], w=[Btmp[j]])
                        S.op("pool", lambda e, j=j, n=n: e.tensor_scalar(
                            out=outb[j][:, 0:n], in0=tmp[j][:, 0:n], scalar1=128 ** -0.5, scalar2=None, op0=ALU.mult),
                            r=[Btmp[j]], w=[Bout[j]])
                        S.dma("sp", self.mk[c - 4, :, t0:t0 + n], outb[j][:, 0:n], r=[Bout[j]])
            S.barrier()

    def mlstm(self, i):
        S = self.S
        with contextlib.ExitStack() as st:
            tri = self.sb(st, "ml_tri", [128, 3, 128])
            G = self.sb(st, "ml_G", [128, NT, 8])
            Ao = self.sb(st, "ml_Ao", [128, NT, 4])
            A2o = self.sb(st, "ml_A2o", [128, NT, 4])
            Bqo = self.sb(st, "ml_Bqo", [128, NT, 4])
            EBo = self.sb(st, "ml_EBo", [128, NT, 4])
            Bg = S.buf("ml_g")
            S.dma("sp", tri[:], self.tri[:, :, :], w=[Bg])
            S.dma("sp", G[:], self.gates[:, :, :], w=[Bg])
            lnks = self.sb(st, "ml_lnks", [128, 1])
            S.op("pool", lambda e: e.memset(lnks[:], math.log(128 ** -0.5)), w=[Bg])
            with contextlib.ExitStack() as st1:
                pg = [self.ps(st1, f"ml_pg{j}", [128, 512]) for j in range(3)]
                Bpg = S.bufs("ml_pg", 3)
                tmpg = self.sb(st1, "ml_tmpg", [128, 32, 4])
                Btg = S.buf("ml_tmpg")
                grp = 0
                for g0 in range(0, NT, 32):
                    g1 = min(g0 + 32, NT)
                    pj = grp % 3
                    grp += 1
                    for t in range(g0, g1):
                        o = (t - g0) * 8
                        S.op("pe", lambda e, t=t, o=o, pj=pj: e.matmul(pg[pj][:, o:o + 2], tri[:, 0, :], G[:, t, 4:6], start=True, stop=True),
                             r=[Bg], w=[Bpg[pj]])
                        S.op("pe", lambda e, t=t, o=o, pj=pj: e.matmul(pg[pj][:, o + 2:o + 4], tri[:, 1, :], G[:, t, 6:8], start=True, stop=True),
                             r=[Bg], w=[Bpg[pj]])
                        S.op("pe", lambda e, t=t, o=o, pj=pj: e.matmul(pg[pj][:, o + 4:o + 8], tri[:, 2, :], G[:, t, 4:8], start=True, stop=True),
                             r=[Bg], w=[Bpg[pj]])
                    n = g1 - g0
                    pv = pg[pj][:, 0:n * 8].rearrange("p (t c) -> p t c", c=8)
                    S.op("dve", lambda e, pv=pv, g0=g0, g1=g1, n=n: e.tensor_tensor(
                        out=tmpg[:, 0:n, :], in0=G[:, g0:g1, 0:4], in1=pv[:, :, 0:4], op=ALU.subtract), r=[Bg, Bpg[pj]], w=[Btg])
                    S.op("act", lambda e, g0=g0, g1=g1, n=n: e.activation(out=Ao[:, g0:g1, :], in_=tmpg[:, 0:n, :], func=AF.Exp,
                                                                          bias=lnks[:, 0:1]), r=[Btg, Bg], w=[Bg])
                    S.op("act", lambda e, pv=pv, g0=g0, g1=g1: e.activation(out=Bqo[:, g0:g1, :], in_=pv[:, :, 0:4], func=AF.Exp),
                         r=[Bpg[pj]], w=[Bg])
                    S.op("act", lambda e, pv=pv, g0=g0, g1=g1: e.activation(out=EBo[:, g0:g1, :], in_=pv[:, :, 4:8], func=AF.Exp),
                         r=[Bpg[pj]], w=[Bg])
                    S.op("dve", lambda e, g0=g0, g1=g1: e.tensor_tensor(out=A2o[:, g0:g1, :], in0=Ao[:, g0:g1, :], in1=EBo[:, g0:g1, :],
                                                                        op=ALU.mult), r=[Bg], w=[Bg])
                S.barrier()
            qT = self.sb(st, "ml_qT", [128, NTOK], BF16)
            kT = self.sb(st, "ml_kT", [128, NTOK], BF16)
            V = self.sb(st, "ml_V", [128, NT, 129], BF16)
            KTOK = self.sb(st, "ml_KTOK", [128, NT, 128], BF16)
            Bin = S.buf("ml_in")
            Bkt = S.buf("ml_ktok")
            Cn = [self.sb(st, f"ml_Cn{d}", [128, 129]) for d in range(2)]
            Cnb = [[self.sb(st, f"ml_Cnb{d}{j}", [128, 129], BF16) for j in range(2)] for d in range(2)]
            BCn = S.bufs("ml_Cn", 2)
            BCnb = [S.bufs(f"ml_Cnb{d}", 2) for d in range(2)]
            NW = 6
            W = [self.sb(st, f"ml_W{j}", [128, 128], BF16) for j in range(NW)]
            BW = S.bufs("ml_W", NW)
            v2 = [self.sb(st, f"ml_v2{j}", [128, 129], BF16) for j in range(NW)]
            Bv2 = S.bufs("ml_v2", NW)
            ho = [self.sb(st, f"ml_ho{j}", [128, 128]) for j in range(NW)]
            Bho = S.bufs("ml_ho", NW)
            sm = [self.sb(st, f"ml_sm{j}", [128, 4]) for j in range(NW)]
            Bsm = S.bufs("ml_sm", NW)
            NS, NKV, NN = 2, 2, 4
            order = [[NT_LAT, NT_LAT + 1] + list(range(NT_LAT)), [NT_LAT + 1, NT_LAT] + list(range(NT_LAT - 1, -1, -1))]
            wi = 0
            for h in range(2):
                S.dma("sp", qT[:], self.mq[h, :, :], w=[Bin])
                S.dma("act", kT[:], self.mk[h, :, :], w=[Bin])
                S.dma("sp", V[:], self.vda[h, :, :, :], w=[Bin])
                with contextlib.ExitStack() as stp:
                    ppbs = [self.ps(stp, f"ml_ppb{j}", [128, 1024], BF16) for j in range(2)]
                    Bppbs = S.bufs("ml_ppb", 2)
                    for gi, t0 in enumerate(range(0, NT, 8)):
                        n = min(8, NT - t0)
                        ppb, Bppb = ppbs[gi % 2], Bppbs[gi % 2]
                        for q in range(n):
                            t = t0 + q
                            S.op("pe", lambda e, t=t, q=q, ppb=ppb: e.transpose(
                                ppb[:, q * 128:(q + 1) * 128], kT[:, t * 128:(t + 1) * 128], self.ident_b[:]),
                                r=[Bin, self.B_const], w=[Bppb])
                        S.op("act", lambda e, t0=t0, n=n, ppb=ppb: e.activation(
                            out=KTOK[:, t0:t0 + n, :], in_=ppb[:, 0:n * 128].rearrange("p (t c) -> p t c", c=128), func=AF.Copy),
                            r=[Bppb], w=[Bkt])
                    S.barrier()
                stq = contextlib.ExitStack()
                ps_s = [self.ps(stq, f"ml_pss{j}", [128, 512]) for j in range(NS)]
                ps_kv = [self.ps(stq, f"ml_pkv{j}", [128, 512]) for j in range(NKV)]
                ps_n = [self.ps(stq, f"ml_pn{j}", [128, 512]) for j in range(NN)]
                Bps_s, Bps_kv, Bps_n = S.bufs("ml_pss", NS), S.bufs("ml_pkv", NKV), S.bufs("ml_pn", NN)
                for d in range(2):
                    S.op("pool", lambda e, d=d: e.memset(Cn[d][:], 0.0), w=[BCn[d]])
                    S.op("pool", lambda e, d=d: e.memset(Cnb[d][0][:], 0.0), w=[BCnb[d][0]])

                def stage1(step, d, wi):
                    t = order[d][step]
                    hd = d * 2 + h
                    j = wi % NW
                    p3 = wi % NS
                    tsl = slice(t * 128, (t + 1) * 128)
                    S.op("pe", lambda e: e.matmul(ps_s[p3][:, 0:128], kT[:, tsl], qT[:, tsl], start=True, stop=True),
                         r=[Bin], w=[Bps_s[p3]])
                    S.op("dve", lambda e: e.scalar_tensor_tensor(
                        out=W[j][:], in0=ps_s[p3][:, 0:128], scalar=Ao[:, t, hd:hd + 1], in1=tri[:, d, :],
                        op0=ALU.mult, op1=ALU.mult), r=[Bps_s[p3], Bg], w=[BW[j]])
                    S.op("act", lambda e: e.activation(
                        out=v2[j][:], in_=V[:, t, :], func=AF.Copy, scale=A2o[:, t, hd:hd + 1]),
                        r=[Bin, Bg], w=[Bv2[j]])

                def stage2(step, d, wi):
                    t = order[d][step]
                    hd = d * 2 + h
                    cur, nxt = step % 2, (step + 1) % 2
                    j = wi % NW
                    p3 = wi % NKV
                    p6 = wi % NN
                    tsl = slice(t * 128, (t + 1) * 128)
                    S.op("pe", lambda e: e.matmul(ps_kv[p3][:, 0:129], KTOK[:, t, :], v2[j][:], start=True, stop=True),
                         r=[Bkt, Bv2[j]], w=[Bps_kv[p3]])
                    S.op("pe", lambda e: e.matmul(ps_n[p6][:, 0:129], W[j][:], V[:, t, :], start=True, stop=False),
                         r=[BW[j], Bin], w=[Bps_n[p6]])
                    S.op("pe", lambda e: e.matmul(ps_n[p6][:, 0:129], qT[:, tsl], Cnb[d][cur][:], start=False, stop=True),
                         r=[Bin, BCnb[d][cur]], w=[Bps_n[p6]])
                    S.op("dve", lambda e: e.scalar_tensor_tensor(
                        out=Cn[d][:], in0=Cn[d][:], scalar=EBo[:, t, hd:hd + 1], in1=ps_kv[p3][:, 0:129],
                        op0=ALU.mult, op1=ALU.add), r=[BCn[d], Bps_kv[p3], Bg], w=[BCn[d]])
                    S.op("act", lambda e: e.activation(out=Cnb[d][nxt][:], in_=Cn[d][:], func=AF.Copy),
                         r=[BCn[d]], w=[BCnb[d][nxt]])

                def back(step, d, wi):
                    t = order[d][step]
                    hd = d * 2 + h
                    j = wi % NW
                    p6 = wi % NN
                    S.op("dve", lambda e: e.tensor_tensor(
                        out=sm[j][:, 0:1], in0=ps_n[p6][:, 128:129], in1=Bqo[:, t, hd:hd + 1], op=ALU.mult),
                        r=[Bps_n[p6], Bg], w=[Bsm[j]])
                    S.op("dve", lambda e: e.scalar_tensor_tensor(out=sm[j][:, 1:2], in0=sm[j][:, 0:1], scalar=-1.0,
                                                                 in1=sm[j][:, 0:1], op0=ALU.mult, op1=ALU.max),
                         r=[Bsm[j]], w=[Bsm[j]])
                    S.op("dve", lambda e: e.tensor_scalar(out=sm[j][:, 1:2], in0=sm[j][:, 1:2], scalar1=1.0, scalar2=None,
                                                          op0=ALU.max), r=[Bsm[j]], w=[Bsm[j]])
                    S.op("dve", lambda e: e.reciprocal(sm[j][:, 2:3], sm[j][:, 1:2]), r=[Bsm[j]], w=[Bsm[j]])
                    S.op("dve", lambda e: e.tensor_tensor(
                        out=sm[j][:, 3:4], in0=sm[j][:, 2:3], in1=Bqo[:, t, hd:hd + 1], op=ALU.mult), r=[Bsm[j], Bg], w=[Bsm[j]])
                    S.op("dve", lambda e: e.tensor_scalar(
                        out=ho[j][:], in0=ps_n[p6][:, 0:128], scalar1=sm[j][:, 3:4], scalar2=None, op0=ALU.mult),
                        r=[Bps_n[p6], Bsm[j]], w=[Bho[j]])
                    S.dma("sp", self.hfb[d, t * 128:(t + 1) * 128, h * 128:(h + 1) * 128], ho[j][:], r=[Bho[j]])

                items = []
                for step in range(NT):
                    for d in range(2):
                        items.append((step, d, wi))
                        wi += 1
                npair = len(items) // 2
                for k in range(npair + 2):
                    if k < npair:
                        stage1(*items[2 * k])
                        stage1(*items[2 * k + 1])
                    if 0 <= k - 1 < npair:
                        stage2(*items[2 * (k - 1)])
                        stage2(*items[2 * (k - 1) + 1])
                    if 0 <= k - 2 < npair:
                        back(*items[2 * (k - 2)])
                        back(*items[2 * (k - 2) + 1])
                S.barrier()
                stq.close()
            S.barrier()

    def mlstm_post(self, i):
        S = self.S
        with contextlib.ExitStack() as st:
            ngo = self.sb(st, "mp_ngo", [128, 256])
            Bw = S.buf("mp_w")
            S.dma("sp", ngo[:], self.od_ng[i, :, :].to_broadcast([128, 256]), w=[Bw])
            eps_t = self.sb(st, "mp_eps", [128, 1])
            S.op("pool", lambda e: e.memset(eps_t[:], LN_EPS), w=[Bw])
            NB = 4
            hf = [self.sb(st, f"mp_hf{j}", [128, 256]) for j in range(NB)]
            hb = [self.sb(st, f"mp_hb{j}", [128, 256]) for j in range(NB)]
            ogo = [self.sb(st, f"mp_ogo{j}", [128, 256]) for j in range(NB)]
            hs = [self.sb(st, f"mp_hs{j}", [128, 256]) for j in range(NB)]
            yo = [self.sb(st, f"mp_yo{j}", [128, 256]) for j in range(NB)]
            yob = [self.sb(st, f"mp_yob{j}", [128, 256], BF16) for j in range(NB)]
            stt = [self.sb(st, f"mp_st{j}", [128, 2, 12]) for j in range(NB)]
            Bhf, Bhb, Bogo, Bhs, Byo, Bst = (S.bufs(n, NB) for n in ("mp_hf", "mp_hb", "mp_ogo", "mp_hs", "mp_yo", "mp_st"))

            def post_a(t):
                j = t % NB
                tok0 = t * 128
                S.dma("sp", hf[j][:], self.hfb[0, tok0:tok0 + 128, 0:256], w=[Bhf[j]])
                S.dma("act", hb[j][:], self.hfb[1, tok0:tok0 + 128, 0:256], w=[Bhb[j]])
                S.dma("sp", ogo[j][:], self.og[tok0:tok0 + 128, :], w=[Bogo[j]])

            def post_c(t):
                j = t % NB
                S.op("pool", lambda e, j=j: e.tensor_tensor(out=hs[j][:], in0=hf[j][:], in1=hb[j][:], op=ALU.add),
                     r=[Bhf[j], Bhb[j]], w=[Bhs[j]])
                S.op("pool", lambda e, j=j: e.tensor_tensor(out=ogo[j][:], in0=ogo[j][:], in1=ngo[:], op=ALU.mult),
                     r=[Bogo[j], Bw], w=[Bogo[j]])

            def post_b(t):
                j = t % NB
                for h in range(2):
                    S.op("dve", lambda e, j=j, h=h: e.bn_stats(stt[j][:, h, 0:6], hs[j][:, h * 128:(h + 1) * 128]),
                         r=[Bhs[j]], w=[Bst[j]])
                    S.op("dve", lambda e, j=j, h=h: e.bn_aggr(stt[j][:, h, 6:8], stt[j][:, h, 0:6]), r=[Bst[j]], w=[Bst[j]])
                S.op("act", lambda e, j=j: e.activation(out=stt[j][:, :, 8:9], in_=stt[j][:, :, 7:8], func=AF.Sqrt, scale=1.0,
                                                        bias=eps_t[:, 0:1]), r=[Bst[j], Bw], w=[Bst[j]])
                S.op("dve", lambda e, j=j: e.reciprocal(stt[j][:, :, 9:10], stt[j][:, :, 8:9]), r=[Bst[j]], w=[Bst[j]])
                for h in range(2):
                    S.op("dve", lambda e, j=j, h=h: e.tensor_scalar(
                        out=yo[j][:, h * 128:(h + 1) * 128], in0=hs[j][:, h * 128:(h + 1) * 128], scalar1=stt[j][:, h, 6:7],
                        scalar2=stt[j][:, h, 9:10], op0=ALU.subtract, op1=ALU.mult), r=[Bhs[j], Bst[j]], w=[Byo[j]])
                S.op("pool", lambda e, j=j: e.tensor_tensor(out=yob[j][:], in0=yo[j][:], in1=ogo[j][:], op=ALU.mult),
                     r=[Byo[j], Bogo[j]], w=[Byo[j]])
                k, r0 = self.yodd_loc(t)
                S.dma("pool", self.yodd_t[k].ap()[r0:r0 + 128, 0:256], yob[j][:], r=[Byo[j]])

            for t in range(NT):
                if t == 0:
                    post_a(0)
                    post_a(1)
                    post_c(0)
                if t + 2 < NT:
                    post_a(t + 2)
                if t + 1 < NT:
                    post_c(t + 1)
                post_b(t)
            S.barrier()

    def na_attention(self, i, last):
        S = self.S
        with contextlib.ExitStack() as st:
            QNe = self.sb(st, "na_Qe", [128, NTOK], BF16)
            QNo = self.sb(st, "na_Qo", [128, NTOK], BF16)
            KN = self.sb(st, "na_K", [128, NTOK], BF16)
            VN = self.sb(st, "na_V", [128, NT, 65], BF16)
            MB = self.sb(st, "na_MB", [128, NA_NVAR, 128], BF16)
            Bqk = S.buf("na_qk")
            Bv = S.buf("na_v")
            Bmb = S.buf("na_mb")
            stg = [self.sb(st, f"na_stg{j}", [128, 7, 128]) for j in range(2)]
            Bstg = S.bufs("na_stg", 2)
            NP = 3
            ptA = [self.sb(st, f"na_ptA{j}", [128, 512], BF16) for j in range(NP)]
            ptB = [self.sb(st, f"na_ptB{j}", [128, 384], BF16) for j in range(NP)]
            BptA, BptB = S.bufs("na_ptA", NP), S.bufs("na_ptB", NP)
            rr = [self.sb(st, f"na_rr{j}", [128, 2]) for j in range(NP)]
            to = [self.sb(st, f"na_to{j}", [128, 64], BF16) for j in range(NP)]
            Bto = S.bufs("na_to", NP)
            psA = [self.ps(st, f"na_psA{j}", [128, 512]) for j in range(2)]
            psB = [self.ps(st, f"na_psB{j}", [128, 512]) for j in range(2)]
            acc = [self.ps(st, f"na_acc{j}", [128, 512]) for j in range(2)]
            BpsA, BpsB, Bacc = S.bufs("na_psA", 2), S.bufs("na_psB", 2), S.bufs("na_acc", 2)
            it = 0
            qtiles = list(range(NT_LAT)) + ([] if (last and self.final_out) else [NT_LAT, NT_LAT + 1])
            if self.tile_filter is not None:
                qtiles = [t for t in qtiles if t in self.tile_filter]
            pend_pv = None
            for h in range(4):
                lo = (h % 2) * 64
                if h == 0:
                    S.op("pool", lambda e: e.memset(QNe[64:128, :], 0.0), w=[Bqk])
                    S.op("pool", lambda e: e.memset(QNo[0:64, :], 0.0), w=[Bqk])
                if h % 2 == 0:
                    hp = h // 2
                    S.dma("sp", QNe[0:64, :], self.qda[hp, 0:64, :], w=[Bqk])
                    S.dma("act", QNo[64:128, :], self.qda[hp, 64:128, :], w=[Bqk])
                    S.dma("sp", KN[:], self.kda[hp, :, :], w=[Bqk])
                QN = QNe if h % 2 == 0 else QNo
                S.dma("act", VN[:], self.vm[h, :, :, :], w=[Bv])
                for v0 in range(0, NA_NVAR, 7):
                    n = min(7, NA_NVAR - v0)
                    sj = (v0 // 7) % 2
                    S.dma("sp", stg[sj][:, 0:n, :], self.od_nat[i, h, :, v0:v0 + n, :], w=[Bstg[sj]])
                    S.op("dve", lambda e, sj=sj, n=n, v0=v0: e.tensor_scalar(
                        out=MB[:, v0:v0 + n, :], in0=stg[sj][:, 0:n, :], scalar1=1.0 / NA_SCALE, scalar2=None, op0=ALU.mult),
                        r=[Bstg[sj]], w=[Bmb])
                for j in qtiles:
                    pj = it % 2
                    tj = it % NP
                    it += 1
                    qsl = slice(j * 128, (j + 1) * 128)
                    if j < NT_LAT:
                        slots = [(kt, var) for (kt, var) in NA_PLAN[j]] + [(NT_LAT, None), (NT_LAT + 1, None)]
                    else:
                        slots = [(NT_LAT, None), (NT_LAT + 1, None)]
                    nA = min(4, len(slots))
                    nB = len(slots) - nA
                    for si, (kt, var) in enumerate(slots):
                        if si < 4:
                            dst, Bd = psA[pj][:, si * 128:(si + 1) * 128], BpsA[pj]
                        else:
                            dst, Bd = psB[pj][:, (si - 4) * 128:(si - 3) * 128], BpsB[pj]
                        S.op("pe", lambda e, dst=dst, kt=kt, var=var, QN=QN: e.matmul(
                            dst, KN[:, kt * 128:(kt + 1) * 128], QN[:, qsl], start=True, stop=(var is None)),
                            r=[Bqk], w=[Bd])
                        if var is not None:
                            S.op("pe", lambda e, dst=dst, var=var: e.matmul(dst, self.ident_b[:], MB[:, var, :], start=False, stop=True),
                                 r=[Bmb, self.B_const], w=[Bd])
                    S.op("act", lambda e, pj=pj, tj=tj, nA=nA: e.activation(out=ptA[tj][:, 0:nA * 128], in_=psA[pj][:, 0:nA * 128],
                                                                            func=AF.Exp, scale=NA_SCALE), r=[BpsA[pj]], w=[BptA[tj]])
                    if nB > 0:
                        S.op("act", lambda e, pj=pj, tj=tj, nB=nB: e.activation(out=ptB[tj][:, 0:nB * 128], in_=psB[pj][:, 0:nB * 128],
                                                                                func=AF.Exp, scale=NA_SCALE), r=[BpsB[pj]], w=[BptB[tj]])
                    def pv_fin(slots=slots, tj=tj, pj=pj, j=j, h=h):
                        for si, (kt, var) in enumerate(slots):
                            if si < 4:
                                lhs, Bl = ptA[tj][:, si * 128:(si + 1) * 128], BptA[tj]
                            else:
                                lhs, Bl = ptB[tj][:, (si - 4) * 128:(si - 3) * 128], BptB[tj]
                            S.op("pe", lambda e, lhs=lhs, kt=kt, si=si: e.matmul(
                                acc[pj][:, 0:65], lhs, VN[:, kt, :], start=(si == 0), stop=(si == len(slots) - 1)),
                                r=[Bl, Bv], w=[Bacc[pj]])
                        S.op("dve", lambda e: e.reciprocal(rr[tj][:, 0:1], acc[pj][:, 64:65]), r=[Bacc[pj]], w=[Bto[tj]])
                        S.op("dve", lambda e: e.tensor_scalar(out=to[tj][:], in0=acc[pj][:, 0:64], scalar1=rr[tj][:, 0:1],
                                                              scalar2=None, op0=ALU.mult), r=[Bacc[pj], Bto[tj]], w=[Bto[tj]])
                        kk, r0 = self.yodd_loc(j)
                        S.dma("pool", self.yodd_t[kk].ap()[r0:r0 + 128, 256 + h * 64:256 + (h + 1) * 64], to[tj][:], r=[Bto[tj]])

                    if pend_pv is not None:
                        pend_pv()
                    pend_pv = pv_fin
                if pend_pv is not None:
                    pend_pv()
                    pend_pv = None
            S.barrier()


def _swap64(cols):
    return np.concatenate([cols[32:64], cols[0:32]])


def _rope_tables():
    t = np.arange(LAT)
    row = (t // GRID_W).astype(np.float32)
    col = (t % GRID_W).astype(np.float32)

    def tab(dim, nrows_pattern):
        n_freq = dim // 4
        freqs = (10000.0 ** (-np.arange(n_freq, dtype=np.float32) / n_freq)).astype(np.float32)
        ang = np.concatenate([row[:, None] * freqs, col[:, None] * freqs], axis=-1).astype(np.float32)
        cos = np.cos(ang).astype(np.float32).T
        sin = np.sin(ang).astype(np.float32).T
        half = dim // 2
        c = np.concatenate([cos, cos], 0)
        s = np.concatenate([-sin, sin], 0)
        c = np.concatenate([c, np.ones((dim, CTX), np.float32)], 1)
        s = np.concatenate([s, np.zeros((dim, CTX), np.float32)], 1)
        return c, s

    c64, s64 = tab(64, None)
    c32, s32 = tab(32, None)
    rope_da = np.stack([np.concatenate([c64, c64], 0), np.concatenate([s64, s64], 0)]).astype(np.float32)
    ml_c = np.ones((128, NTOK), np.float32)
    ml_s = np.zeros((128, NTOK), np.float32)
    ml_c[0:32] = c32
    ml_s[0:32] = s32
    ml_c[64:96] = c32
    ml_s[64:96] = s32
    rope_ml = np.stack([ml_c, ml_s]).astype(np.float32)
    return rope_da, rope_ml


def _even_layout(inp, par):
    ev_w_in, ev_b_in = inp["ev_w_in"], inp["ev_b_in"]
    hs = [2 * par, 2 * par + 1]
    ms = [4 * par + k for k in range(4)]
    o_q1, o_q2, o_k1, o_k2, o_v, o_cq, o_ckv, o_kr, o_g = 0, 256, 512, 768, 1024, 1536, 1792, 1920, 1952
    cols = []
    for (a, b) in ((o_q1, o_q2), (o_k1, o_k2)):
        main = []
        sw = []
        for h in hs:
            c1 = np.arange(a + h * 64, a + (h + 1) * 64)
            c2 = np.arange(b + h * 64, b + (h + 1) * 64)
            main += [c1, c2]
            sw += [_swap64(c1), _swap64(c2)]
        cols += main + sw
    kr = np.arange(o_kr, o_kr + 32)
    cols += [kr, np.concatenate([kr[16:], kr[:16]])]
    cols += [np.arange(o_v + h * 128, o_v + (h + 1) * 128) for h in hs]
    cols += [np.arange(o_cq, o_cq + 384), np.arange(o_g, o_g + 1024)]
    perm = np.concatenate(cols)
    assert perm.shape[0] == EV_COLS
    ev_w = np.ascontiguousarray(ev_w_in[:, :, perm])
    bp = ev_b_in[:, perm]
    bcol = np.zeros((2, 128, 10), np.float32)
    for g in range(8):
        bcol[:, :, g] = bp[:, g * 128:(g + 1) * 128]
    bcol[:, 0:32, 8] = bp[:, EKR:EKR + 32]
    bcol[:, 0:32, 9] = bp[:, EKR + 32:EKR + 64]
    brow = np.ascontiguousarray(bp[:, EV_:EV_ + 1664])[:, None, :]
    wuq = inp["mla_w_uq"]
    mainc = []
    swc = []
    for h in ms:
        c = np.arange(h * 96, h * 96 + 96)
        r = c[64:96]
        mainc.append(c)
        swc.append(np.concatenate([c[0:64], r[16:], r[:16]]))
    ev_wuq = np.ascontiguousarray(wuq[:, :, np.concatenate(mainc + swc)])
    wukv = inp["mla_w_ukv"]
    nope = np.concatenate([np.arange(h * 128, h * 128 + 64) for h in ms])
    vv = np.concatenate([np.arange(h * 128 + 64, h * 128 + 128) for h in ms])
    ev_wukv = np.ascontiguousarray(wukv[:, :, np.concatenate([nope, vv])])
    return dict(
        ev_w=ev_w, ev_bcol=bcol, ev_brow=np.ascontiguousarray(brow),
        ev_lam=np.ascontiguousarray(inp["da_lambda"].reshape(2, 1, 256)),
        ev_subg=np.ascontiguousarray(inp["da_subln_g"].reshape(2, 1, 128)),
        ev_qg=np.ascontiguousarray(inp["mla_q_norm_g"].reshape(2, 2, 128).transpose(0, 2, 1)),
        ev_kvg=np.ascontiguousarray(inp["mla_kv_norm_g"].reshape(2, 128, 1)),
        ev_wuq=ev_wuq, ev_wukv=ev_wukv, ev_wo=np.ascontiguousarray(inp["ev_w_out"]),
    )


def _odd_layout(inp, par):
    w, b = inp["od_w_in"], inp["od_b_in"]
    hs = [2 * par, 2 * par + 1]
    ns = [4 * par + k for k in range(4)]
    g0 = 2048
    cols = [np.arange(h * 128, (h + 1) * 128) for h in hs]
    cols += [np.arange(512 + h * 128, 512 + (h + 1) * 128) for h in hs]
    cols += [np.arange(2064 + n * 64, 2064 + (n + 1) * 64) for n in ns]
    cols += [np.arange(2576 + n * 64, 2576 + (n + 1) * 64) for n in ns]
    cols += [np.arange(1024 + h * 128, 1024 + (h + 1) * 128) for h in hs]
    cols += [np.arange(1536 + h * 128, 1536 + (h + 1) * 128) for h in hs]
    cols += [np.array([g0 + j * 4 + h for h in hs]) for j in (0, 2, 1, 3)]
    cols += [np.arange(3088 + n * 64, 3088 + (n + 1) * 64) for n in ns]
    cols += [np.arange(3600, 4624)]
    perm = np.concatenate(cols)
    assert perm.shape[0] == OD_COLS
    od_w = np.ascontiguousarray(w[:, :, perm])
    bp = b[:, perm]
    bcol = np.ascontiguousarray(bp[:, 0:1024].reshape(2, 8, 128).transpose(0, 2, 1))
    brow = np.ascontiguousarray(bp[:, 1024:])[:, None, :]
    ch = np.concatenate([np.arange(h * 128, (h + 1) * 128) for h in hs] + [np.arange(512 + h * 128, 512 + (h + 1) * 128) for h in hs])
    convw = np.ascontiguousarray(inp["ml_conv_w"][:, :, ch].reshape(2, 5, 4, 128).transpose(0, 3, 2, 1))
    convb = np.ascontiguousarray(inp["ml_conv_b"][:, ch].reshape(2, 4, 128).transpose(0, 2, 1))
    fb = np.ascontiguousarray(inp["ml_f_bias"][:, :, hs].reshape(2, 1, 4))
    ng = np.ascontiguousarray(np.concatenate([inp["ml_norm_g"][:, h * 128:(h + 1) * 128] for h in hs], 1).reshape(2, 1, 256))
    rpb = inp["na_rpb"][:, ns]
    nat = np.full((2, 4, 128, NA_NVAR, 128), NEG, np.float32)
    k = np.arange(128)
    krl, kc = k // 64, k % 64
    q = np.arange(128)
    qrl, qc = q // 64, q % 64
    cs = np.clip(qc - NA_COLS // 2, 0, GRID_W - NA_COLS)
    for (dk, o0, o1), vid in NA_VARIANTS.items():
        rel = (2 * dk + krl[:, None]) - qrl[None, :]
        off = np.where(qrl[None, :] == 0, o0, o1)
        valid_r = (rel >= off) & (rel <= off + NA_ROWS - 1)
        valid_c = (kc[:, None] >= cs[None, :]) & (kc[:, None] <= cs[None, :] + NA_COLS - 1)
        valid = valid_r & valid_c
        ridx = np.clip(rel + NA_ROWS - 1, 0, 2 * NA_ROWS - 2)
        cidx = np.clip(kc[:, None] - qc[None, :] + NA_COLS - 1, 0, 2 * NA_COLS - 2)
        tab = rpb[:, :, ridx, cidx]
        nat[:, :, :, vid, :] = np.where(valid[None, None], tab, np.float32(NEG))
    return dict(od_w=od_w, od_bcol=bcol, od_brow=np.ascontiguousarray(brow), od_convw=convw, od_convb=convb, od_fb=fb,
                od_ng=ng, od_nat=nat, od_wo=np.ascontiguousarray(inp["od_w_out"]))


def make_in_maps(inp, batches):
    inp = {k: np.asarray(v, dtype=np.float32) for k, v in inp.items()}
    rope_da, rope_ml = _rope_tables()
    common = dict(
        ada_w=inp["ada_w"], ada_b=inp["ada_b"], ln_g=inp["ln_g"], ln_b=inp["ln_b"],
        ident=np.eye(128, dtype=np.float32),
        sel=np.concatenate([np.stack([np.ones(128), np.zeros(128)]), np.stack([np.zeros(128), np.ones(128)])], 1).astype(np.float32),
        rope_da=rope_da, rope_ml=rope_ml,
    )
    tri = np.stack([np.triu(np.ones((128, 128), np.float32)), np.tril(np.ones((128, 128), np.float32)),
                    np.ones((128, 128), np.float32)], 1)
    common["tri"] = np.ascontiguousarray(tri)
    ev = [dict(_even_layout(inp, p), **_odd_layout(inp, p)) for p in (0, 1)]
    maps = []
    for b in batches:
        m = dict(common)
        m.update(ev[len(maps) % 2])
        m["x_in"] = np.ascontiguousarray(np.concatenate([inp["x"][b], inp["ctx"][b]], 0))
        cc = np.stack([inp["c"][b], inp["c_ctx"]], -1)
        m["cc"] = np.ascontiguousarray(cc.reshape(8, 128, 2).transpose(1, 0, 2))
        par = len(maps) % 2
        ps = np.zeros((128, 2), np.float32)
        ps[:, par] = 1.0
        m["psel"] = ps
        maps.append(m)
    return maps


_PROG_CACHE = {}


def kernel(**inputs):
    batches = [c // 2 for c in range(8)]
    in_maps = make_in_maps(inputs, batches)
    prog = Prog()
    nc = prog.build()
    res = run_bass_kernel_spmd(nc, in_maps, core_ids=list(range(8)))
    out = np.stack([res.results[2 * b]["y"] for b in range(4)], 0)
    return out.astype(np.float32)
```
